# Optimizing a Trainium2 kernel written in Bass

```python
import jax, jax.numpy as jnp
from jax import lax
import numpy as np

D_MODEL = 1024
BATCH = 32
SEQ = 256
DEPTH = 4
DEC_BATCH = 2
DEC_SEQ = 1024
PAST_LEN = 512

GRID_W = 64
N_AB = (DEPTH + 1) // 2
N_C = DEPTH // 2
EPS = 1e-6
ROPE_THETA = 10000.0
Q_BLOCK = 128
GLA_HEADS = 4
GLA_DK = 64
GLA_DV = 128
GLA_QK = GLA_HEADS * GLA_DK
GLA_V = GLA_HEADS * GLA_DV
GLA_RANK = 16
GLA_TAU = 16.0
GLA_CHUNK = 64
MLA_HEADS = 8
MLA_Q_RANK = 384
MLA_KV_RANK = 256
MLA_NOPE = 64
MLA_ROPE = 32
MLA_QK = MLA_NOPE + MLA_ROPE
MLA_V = 64
GQA_HEADS = 16
GQA_KV_HEADS = 4
GQA_DH = 64
C_OUT = GQA_HEADS * GQA_DH
MIX_WIDTH = GLA_V + MLA_HEADS * MLA_V
AB_WIDTHS = (GLA_QK, GLA_QK, GLA_V, GLA_V, 2 * GLA_RANK, MLA_Q_RANK, MLA_KV_RANK, MLA_ROPE)
AB_IN = GLA_QK * 2 + GLA_V * 2 + 2 * GLA_RANK + MLA_Q_RANK + MLA_KV_RANK + MLA_ROPE
C_WIDTHS = (GQA_HEADS * GQA_DH, GQA_KV_HEADS * GQA_DH, GQA_KV_HEADS * GQA_DH)
C_IN = GQA_HEADS * GQA_DH + 2 * GQA_KV_HEADS * GQA_DH
FFN_HIDDEN = -(-(8 * D_MODEL) // (3 * 256)) * 256

kernel_name = 'hybrid_diffusion_prefix_trunk_step'

F32 = jnp.float32


def split_cols(x, widths):
    offs, acc = [], 0
    for w in widths[:-1]:
        acc += w
        offs.append(acc)
    return jnp.split(x, offs, axis=-1)


def rms_norm(x, g):
    xf = x.astype(F32)
    y = xf * lax.rsqrt(jnp.mean(xf * xf, axis=-1, keepdims=True) + EPS)
    return (y * g.astype(F32)).astype(x.dtype)


def modulate(x, g, shift, scale):
    return rms_norm(x, g) * (1 + scale) + shift


def adaln(cond, w, b):
    m = jnp.einsum('...d,de->...e', jax.nn.silu(cond), w) + b
    return jnp.split(m[..., None, :], 6, axis=-1)


def axial_rope_tables(n_tok, d_rot):
    t = jnp.arange(n_tok, dtype=jnp.int32)
    pos = jnp.stack([t // GRID_W, t % GRID_W], axis=-1).astype(F32)
    quarter = d_rot // 4
    inv = jnp.power(ROPE_THETA, -jnp.arange(quarter, dtype=F32) / quarter)
    ang = pos[:, :, None] * inv
    return jnp.cos(ang), jnp.sin(ang)


def apply_axial_rope(x, cos, sin):
    B, T, H, d = x.shape
    xr = x.astype(F32).reshape(B, T, H, 2, 2, d // 4)
    x1, x2 = xr[..., 0, :], xr[..., 1, :]
    c, s = cos[None, :, None], sin[None, :, None]
    out = jnp.stack([x1 * c - x2 * s, x1 * s + x2 * c], axis=-2)
    return out.reshape(B, T, H, d).astype(x.dtype)


def rope_tail(x, cos, sin, start):
    return jnp.concatenate([x[..., :start], apply_axial_rope(x[..., start:], cos, sin)], axis=-1)


def block_attention(q, k, v):
    B, T, H, dh = q.shape
    Hkv, dv = k.shape[2], v.shape[-1]
    G = H // Hkv
    nb = T // Q_BLOCK
    scale = dh ** -0.5
    qb = q.reshape(B, nb, Q_BLOCK, Hkv, G, dh).transpose(1, 0, 2, 3, 4, 5)

    def one_block(qi):
        s = jnp.einsum('bqkgd,bskd->bkgqs', qi, k, preferred_element_type=F32) * scale
        p = jax.nn.softmax(s, axis=-1)
        return jnp.einsum('bkgqs,bskv->bqkgv', p.astype(v.dtype), v)

    o = lax.map(one_block, qb)
    return o.transpose(1, 0, 2, 3, 4, 5).reshape(B, T, H, dv)


def gla_chunk_scan(q, k, v, log_a, s0):
    B, T, H, dk = q.shape
    dv = v.shape[-1]
    C = GLA_CHUNK
    nc = T // C

    def rs(x):
        return x.astype(F32).reshape(B, nc, C, H, x.shape[-1]).transpose(1, 0, 3, 2, 4)

    qc, kc, vc, ac = rs(q), rs(k), rs(v), rs(log_a)
    b = jnp.cumsum(ac, axis=3)
    b_ref = b[:, :, :, C // 2 - 1:C // 2]
    b_last = b[:, :, :, C - 1:C]
    q_loc = qc * jnp.exp(b - b_ref)
    k_loc = kc * jnp.exp(b_ref - b)
    a_intra = jnp.einsum('nbhtd,nbhsd->nbhts', q_loc, k_loc)
    causal = jnp.tril(jnp.ones((C, C), dtype=bool))
    a_intra = jnp.where(causal, a_intra, 0.0)
    o_intra = jnp.einsum('nbhts,nbhsv->nbhtv', a_intra, vc)
    q_in = qc * jnp.exp(b)
    k_st = kc * jnp.exp(b_last - b)

    def step(S, inp):
        qi, ki, vi, bl = inp
        o = jnp.einsum('bhtd,bhdv->bhtv', qi, S)
        S = S * jnp.exp(bl)[:, :, 0, :, None] + jnp.einsum('bhsd,bhsv->bhdv', ki, vi)
        return S, o

    s_fin, o_inter = lax.scan(step, s0, (q_in, k_st, vc, b_last))
    o = (o_intra + o_inter).transpose(1, 0, 3, 2, 4).reshape(B, T, H, dv)
    return o, s_fin


def gla_prepare(q, k, v, a_lo, a_w2, a_b):
    B, T, _ = q.shape
    q = q.reshape(B, T, GLA_HEADS, GLA_DK) * (GLA_DK ** -0.5)
    k = k.reshape(B, T, GLA_HEADS, GLA_DK)
    v = v.reshape(B, T, GLA_HEADS, GLA_DV)
    logit = jnp.einsum('btzr,zre->btze', a_lo.reshape(B, T, 2, GLA_RANK), a_w2) + a_b
    log_a = (jax.nn.log_sigmoid(logit.astype(F32)) / GLA_TAU).reshape(B, T, 2, GLA_HEADS, GLA_DK)
    return q, k, v, log_a[:, :, 0], log_a[:, :, 1]


def gla_bidirectional(q, k, v, la_fwd, la_bwd, s0):
    o_f, s_f = gla_chunk_scan(q, k, v, la_fwd, s0[:, 0])
    fl = lambda a: jnp.flip(a, axis=1)
    o_b, s_b = gla_chunk_scan(fl(q), fl(k), fl(v), fl(la_bwd), s0[:, 1])
    return o_f + fl(o_b), jnp.stack([s_f, s_b], axis=1)


def gla_output(o, r, out_g):
    B, T = r.shape[:2]
    o = rms_norm(o.astype(r.dtype), out_g)
    return o.reshape(B, T, GLA_V) * jax.nn.silu(r)


def mla_queries(cq, q_norm_g, w_qb, qn_g):
    B, T, _ = cq.shape
    q = jnp.einsum('btr,re->bte', rms_norm(cq, q_norm_g), w_qb).reshape(B, T, MLA_HEADS, MLA_QK)
    return rms_norm(q, qn_g)


def mla_keys_values(ckv, kpe, w_kvb, kn_g):
    B, S, _ = ckv.shape
    kv = jnp.einsum('bsr,re->bse', ckv, w_kvb).reshape(B, S, MLA_HEADS, MLA_NOPE + MLA_V)
    k_nope, v = kv[..., :MLA_NOPE], kv[..., MLA_NOPE:]
    k_pe = jnp.broadcast_to(kpe[:, :, None, :], (B, S, MLA_HEADS, MLA_ROPE))
    k = rms_norm(jnp.concatenate([k_nope, k_pe], axis=-1), kn_g)
    return k, v


def ab_mixer_context(h, lp):
    B, T, _ = h.shape
    q, k, v, r, a_lo, cq, ckv, kpe = split_cols(jnp.einsum('btd,de->bte', h, lp['w_in']), AB_WIDTHS)
    qg, kg, vg, la_f, la_b = gla_prepare(q, k, v, a_lo, lp['a_w2'], lp['a_b'])
    s0 = jnp.zeros((B, 2, GLA_HEADS, GLA_DK, GLA_DV), F32)
    o_gla, gla_state = gla_bidirectional(qg, kg, vg, la_f, la_b, s0)
    o_gla = gla_output(o_gla, r, lp['gla_out_g'])
    ckv = rms_norm(ckv, lp['kv_norm_g'])
    qm = mla_queries(cq, lp['q_norm_g'], lp['w_qb'], lp['qn_g'])
    km, vm = mla_keys_values(ckv, kpe, lp['w_kvb'], lp['kn_g'])
    o_mla = block_attention(qm, km, vm).reshape(B, T, MLA_HEADS * MLA_V)
    out = jnp.einsum('bte,ed->btd', jnp.concatenate([o_gla, o_mla], axis=-1), lp['w_out'])
    return out, ckv, kpe, gla_state.astype(h.dtype)


def ab_mixer_latent(h, lp, ckv_ctx, kpe_ctx, gla_ctx, cos, sin):
    B, T, _ = h.shape
    q, k, v, r, a_lo, cq, ckv, kpe = split_cols(jnp.einsum('btd,de->bte', h, lp['w_in']), AB_WIDTHS)
    qg, kg, vg, la_f, la_b = gla_prepare(q, k, v, a_lo, lp['a_w2'], lp['a_b'])
    o_gla, _ = gla_bidirectional(qg, kg, vg, la_f, la_b, gla_ctx.astype(F32))
    o_gla = gla_output(o_gla, r, lp['gla_out_g'])
    ckv = rms_norm(ckv, lp['kv_norm_g'])
    qm = rope_tail(mla_queries(cq, lp['q_norm_g'], lp['w_qb'], lp['qn_g']), cos, sin, MLA_NOPE)
    k_lat, v_lat = mla_keys_values(ckv, kpe, lp['w_kvb'], lp['kn_g'])
    k_lat = rope_tail(k_lat, cos, sin, MLA_NOPE)
    k_ctx, v_ctx = mla_keys_values(ckv_ctx, kpe_ctx, lp['w_kvb'], lp['kn_g'])
    km = jnp.concatenate([k_ctx, k_lat], axis=1)
    vm = jnp.concatenate([v_ctx, v_lat], axis=1)
    o_mla = block_attention(qm, km, vm).reshape(B, T, MLA_HEADS * MLA_V)
    return jnp.einsum('bte,ed->btd', jnp.concatenate([o_gla, o_mla], axis=-1), lp['w_out'])


def gqa_project(h, lp):
    B, T, _ = h.shape
    q, k, v = split_cols(jnp.einsum('btd,de->bte', h, lp['w_in']), C_WIDTHS)
    q = rms_norm(q.reshape(B, T, GQA_HEADS, GQA_DH), lp['qn_g'])
    k = rms_norm(k.reshape(B, T, GQA_KV_HEADS, GQA_DH), lp['kn_g'])
    return q, k, v.reshape(B, T, GQA_KV_HEADS, GQA_DH)


def c_mixer_context(h, lp):
    B, T, _ = h.shape
    q, k, v = gqa_project(h, lp)
    o = block_attention(q, k, v).reshape(B, T, C_OUT)
    return jnp.einsum('bte,ed->btd', o, lp['w_out']), k, v


def c_mixer_latent(h, lp, k_ctx, v_ctx, cos, sin):
    B, T, _ = h.shape
    q, k, v = gqa_project(h, lp)
    q = apply_axial_rope(q, cos, sin)
    k = apply_axial_rope(k, cos, sin)
    o = block_attention(q, jnp.concatenate([k_ctx, k], axis=1), jnp.concatenate([v_ctx, v], axis=1))
    return jnp.einsum('bte,ed->btd', o.reshape(B, T, C_OUT), lp['w_out'])


def swiglu(h, w_in, w_out):
    g, u = jnp.split(jnp.einsum('btd,de->bte', h, w_in), 2, axis=-1)
    return jnp.einsum('btf,fd->btd', jax.nn.silu(g) * u, w_out)


def setup_inputs(seed: int = 0) -> dict:
    key = jax.random.key(seed)
    ks = jax.random.split(key, 30)
    nrm = lambda k, shape, s: jax.random.normal(k, shape, F32) * s
    gain = lambda k, shape: 1.0 + 0.05 * jax.random.normal(k, shape, F32)
    D = D_MODEL
    return {
        'x_prompt': nrm(ks[0], (BATCH, SEQ, D), 1.0),
        'x_sample': nrm(ks[1], (DEC_BATCH, DEC_SEQ, D), 1.0),
        'c': nrm(ks[2], (DEC_BATCH, D), 1.0),
        'cache_mla_ckv': nrm(ks[3], (DEC_BATCH, N_AB, PAST_LEN, MLA_KV_RANK), 1.0),
        'cache_mla_kpe': nrm(ks[4], (DEC_BATCH, N_AB, PAST_LEN, MLA_ROPE), 1.0),
        'state_gla': nrm(ks[5], (DEC_BATCH, N_AB, 2, GLA_HEADS, GLA_DK, GLA_DV), 1.0),
        'cache_gqa_k': nrm(ks[6], (DEC_BATCH, N_C, PAST_LEN, GQA_KV_HEADS, GQA_DH), 1.0),
        'cache_gqa_v': nrm(ks[7], (DEC_BATCH, N_C, PAST_LEN, GQA_KV_HEADS, GQA_DH), 1.0),
        'c_ctx': nrm(ks[8], (D,), 1.0),
        'ada_w': nrm(ks[9], (DEPTH, D, 6 * D), D ** -0.5),
        'ada_b': nrm(ks[10], (DEPTH, 6 * D), 0.01),
        'norm_mix_g': gain(ks[11], (DEPTH, D)),
        'norm_ffn_g': gain(ks[12], (DEPTH, D)),
        'ffn_w_in': nrm(ks[13], (DEPTH, D, 2 * FFN_HIDDEN), D ** -0.5),
        'ffn_w_out': nrm(ks[14], (DEPTH, FFN_HIDDEN, D), FFN_HIDDEN ** -0.5),
        'ab_w_in': nrm(ks[15], (N_AB, D, AB_IN), D ** -0.5),
        'ab_w_out': nrm(ks[16], (N_AB, MIX_WIDTH, D), MIX_WIDTH ** -0.5),
        'gla_a_w2': nrm(ks[17], (N_AB, 2, GLA_RANK, GLA_QK), GLA_RANK ** -0.5),
        'gla_a_b': nrm(ks[18], (N_AB, 2, GLA_QK), 0.1),
        'gla_out_g': gain(ks[19], (N_AB, GLA_DV)),
        'mla_q_norm_g': gain(ks[20], (N_AB, MLA_Q_RANK)),
        'mla_w_qb': nrm(ks[21], (N_AB, MLA_Q_RANK, MLA_HEADS * MLA_QK), MLA_Q_RANK ** -0.5),
        'mla_kv_norm_g': gain(ks[22], (N_AB, MLA_KV_RANK)),
        'mla_w_kvb': nrm(ks[23], (N_AB, MLA_KV_RANK, MLA_HEADS * (MLA_NOPE + MLA_V)), MLA_KV_RANK ** -0.5),
        'mla_qn_g': gain(ks[24], (N_AB, MLA_QK)),
        'mla_kn_g': gain(ks[25], (N_AB, MLA_QK)),
        'gqa_w_in': nrm(ks[26], (N_C, D, C_IN), D ** -0.5),
        'gqa_w_out': nrm(ks[27], (N_C, C_OUT, D), C_OUT ** -0.5),
        'gqa_qn_g': gain(ks[28], (N_C, GQA_DH)),
        'gqa_kn_g': gain(ks[29], (N_C, GQA_DH)),
    }


def reference(x_prompt, x_sample, c, cache_mla_ckv, cache_mla_kpe, state_gla, cache_gqa_k, cache_gqa_v,
              c_ctx, ada_w, ada_b, norm_mix_g, norm_ffn_g, ffn_w_in, ffn_w_out, ab_w_in, ab_w_out,
              gla_a_w2, gla_a_b, gla_out_g, mla_q_norm_g, mla_w_qb, mla_kv_norm_g, mla_w_kvb, mla_qn_g,
              mla_kn_g, gqa_w_in, gqa_w_out, gqa_qn_g, gqa_kn_g):
    rows = x_sample.shape[1] // GRID_W
    n_lat = rows * GRID_W
    cos_mla, sin_mla = axial_rope_tables(n_lat, MLA_ROPE)
    cos_gqa, sin_gqa = axial_rope_tables(n_lat, GQA_DH)
    xp, xs = x_prompt, x_sample
    new_ckv, new_kpe, new_gla, new_k, new_v = [], [], [], [], []
    for l in range(DEPTH):
        i = l // 2
        mp = adaln(c_ctx, ada_w[l], ada_b[l])
        ms = adaln(c, ada_w[l], ada_b[l])
        hp = modulate(xp, norm_mix_g[l], mp[0], mp[1])
        hs = modulate(xs, norm_mix_g[l], ms[0], ms[1])
        if l % 2 == 0:
            lp = {'w_in': ab_w_in[i], 'w_out': ab_w_out[i], 'a_w2': gla_a_w2[i], 'a_b': gla_a_b[i],
                  'gla_out_g': gla_out_g[i], 'q_norm_g': mla_q_norm_g[i], 'w_qb': mla_w_qb[i],
                  'kv_norm_g': mla_kv_norm_g[i], 'w_kvb': mla_w_kvb[i], 'qn_g': mla_qn_g[i], 'kn_g': mla_kn_g[i]}
            op, ckv, kpe, gst = ab_mixer_context(hp, lp)
            os_ = ab_mixer_latent(hs, lp, cache_mla_ckv[:, i], cache_mla_kpe[:, i], state_gla[:, i],
                                  cos_mla, sin_mla)
            new_ckv.append(ckv)
            new_kpe.append(kpe)
            new_gla.append(gst)
        else:
            lp = {'w_in': gqa_w_in[i], 'w_out': gqa_w_out[i], 'qn_g': gqa_qn_g[i], 'kn_g': gqa_kn_g[i]}
            op, kc, vc = c_mixer_context(hp, lp)
            os_ = c_mixer_latent(hs, lp, cache_gqa_k[:, i], cache_gqa_v[:, i], cos_gqa, sin_gqa)
            new_k.append(kc)
            new_v.append(vc)
        xp = xp + mp[2] * op
        xs = xs + ms[2] * os_
        hp = modulate(xp, norm_ffn_g[l], mp[3], mp[4])
        hs = modulate(xs, norm_ffn_g[l], ms[3], ms[4])
        xp = xp + mp[5] * swiglu(hp, ffn_w_in[l], ffn_w_out[l])
        xs = xs + ms[5] * swiglu(hs, ffn_w_in[l], ffn_w_out[l])
    return (xp, xs, jnp.stack(new_ckv, axis=1), jnp.stack(new_kpe, axis=1), jnp.stack(new_gla, axis=1),
            jnp.stack(new_k, axis=1), jnp.stack(new_v, axis=1))
```

```python
import bisect
import contextlib
import numpy as np
import concourse.bass as bass
import concourse.mybir as mybir
from concourse.bass_utils import run_bass_kernel_spmd

F32 = mybir.dt.float32
BF16 = mybir.dt.bfloat16
AF = mybir.ActivationFunctionType
ALU = mybir.AluOpType
AX = mybir.AxisListType

ENGS = ("pe", "act", "dve", "pool", "sp")
RAW, WAR, WAW = 1, 2, 4
EPS = 1e-6
D = 1024
FH = 2816
ARENA_BYTES = 160 * 1024


class Prog:
    def __init__(self):
        self.ops = []
        self.last_w = {}
        self.readers = {}

    def op(self, eng, fn, r=(), w=(), dq=None):
        i = len(self.ops)
        deps = {}
        psr = [k for k in r if isinstance(k, tuple) and k and k[0] == "ps"]
        if psr:
            r = [k for k in r if not (isinstance(k, tuple) and k and k[0] == "ps")]
            w = list(w) + [k for k in psr if k not in w]
            for k in psr:
                lw = self.last_w.get(k)
                if lw is not None:
                    deps[lw] = deps.get(lw, 0) | RAW
        for k in r:
            lw = self.last_w.get(k)
            if lw is not None:
                deps[lw] = deps.get(lw, 0) | RAW
        for k in w:
            lw = self.last_w.get(k)
            if lw is not None:
                deps[lw] = deps.get(lw, 0) | WAW
            for rd in self.readers.get(k, ()):
                if rd != i:
                    deps[rd] = deps.get(rd, 0) | WAR
        for k in r:
            self.readers.setdefault(k, []).append(i)
        for k in w:
            self.last_w[k] = i
            self.readers[k] = []
        deps.pop(i, None)
        self.ops.append(dict(eng=eng, fn=fn, deps=deps, dq=dq, bar=None))
        return i

    def barrier(self):
        first = len(self.ops)
        for e in ENGS:
            self.ops.append(dict(eng=e, fn="drain", deps={}, dq=None, bar=("sig", first)))
        sig_ids = list(range(first, first + len(ENGS)))
        for e in ENGS:
            self.ops.append(dict(eng=e, fn="nop", deps={s: RAW for s in sig_ids}, dq=None, bar=("wait", first)))
        self.last_w = {}
        self.readers = {}

    def emit(self, nc, es):
        ops = self.ops
        n = len(ops)
        needed = [False] * n
        for i, o in enumerate(ops):
            kept = []
            for d, kind in o["deps"].items():
                od = ops[d]
                if od["dq"] is None and o["dq"] is None and od["eng"] == o["eng"] and o["bar"] is None:
                    if o["eng"] == "pe":
                        continue
                kept.append(d)
                needed[d] = True
            o["kdeps"] = kept
        eng_sem = {e: es.enter_context(nc.semaphore("sem_" + e)) for e in ENGS}
        dq_keys = []
        seen = set()
        for o in ops:
            if o["dq"] is not None and o["dq"] not in seen:
                seen.add(o["dq"])
                dq_keys.append(o["dq"])
        dq_sem = {k: es.enter_context(nc.semaphore("dq_%d" % j)) for j, k in enumerate(dq_keys)}
        dq_idx = {k: [] for k in dq_keys}
        eng_cnt = {e: 0 for e in ENGS}
        for i, o in enumerate(ops):
            if o["dq"] is not None:
                dq_idx[o["dq"]].append(i)
                o["sig"] = ("dq", o["dq"])
            elif needed[i] or (o["bar"] is not None and o["bar"][0] == "sig"):
                eng_cnt[o["eng"]] += 1
                o["sig"] = ("eng", o["eng"], eng_cnt[o["eng"]])
            else:
                o["sig"] = None
        per_eng = {e: [] for e in ENGS}
        for i, o in enumerate(ops):
            per_eng[o["eng"]].append(i)
        self.n_sems = len(ENGS) + len(dq_keys)
        self.counts = {e: len(per_eng[e]) for e in ENGS}

        def run(e, h):
            waited = {}
            for i in per_eng[e]:
                o = ops[i]
                waits = {}
                for d in o["kdeps"]:
                    od = ops[d]
                    if od["dq"] is not None:
                        k = od["dq"]
                        cnt = 16 * bisect.bisect_left(dq_idx[k], i)
                        key = ("dq", k)
                        waits[key] = max(waits.get(key, 0), cnt)
                    else:
                        key = ("eng", od["eng"])
                        waits[key] = max(waits.get(key, 0), od["sig"][2])
                if o["bar"] is not None and o["bar"][0] == "sig" and e == "sp":
                    for k in dq_keys:
                        cnt = 16 * bisect.bisect_left(dq_idx[k], i)
                        if cnt:
                            waits[("dq", k)] = cnt
                for key, v in waits.items():
                    if waited.get(key, 0) >= v:
                        continue
                    waited[key] = v
                    sem = dq_sem[key[1]] if key[0] == "dq" else eng_sem[key[1]]
                    h.wait_ge(sem, v)
                if o["fn"] == "drain":
                    inst = h.nop() if e == "sp" else h.drain()
                elif o["fn"] == "nop":
                    inst = None
                else:
                    inst = o["fn"](h)
                s = o["sig"]
                if s is not None:
                    if s[0] == "dq":
                        inst.then_inc(dq_sem[s[1]], 16)
                    else:
                        inst.then_inc(eng_sem[s[1]], 1)
            if e == "sp":
                for k in dq_keys:
                    h.wait_ge(dq_sem[k], 16 * len(dq_idx[k]))

        with nc.Block() as block:
            @block.tensor
            def _(h):
                run("pe", h)

            @block.scalar
            def _(h):
                run("act", h)

            @block.vector
            def _(h):
                run("dve", h)

            @block.gpsimd
            def _(h):
                run("pool", h)

            @block.sync
            def _(h):
                run("sp", h)


class Arena:
    def __init__(self, t, nbytes):
        self.t = t
        self.nbytes = nbytes
        self.off = 0
        self.peak = 0

    def reset(self, off=0):
        self.off = off

    def alloc(self, free_shape, dtype, parts=128):
        n = int(np.prod(free_shape))
        esz = 4 if dtype == F32 else 2
        sz = (n * esz + 31) // 32 * 32
        o = self.off
        assert o + sz <= self.nbytes, ("arena overflow", o, sz, self.nbytes)
        self.off = o + sz
        self.peak = max(self.peak, self.off)
        ap = self.t[0:parts, o // 2:(o + n * esz) // 2]
        if dtype == F32:
            ap = ap.bitcast(F32)
        fs = list(free_shape)
        if len(fs) == 2:
            ap = ap.rearrange("p (a b) -> p a b", a=fs[0], b=fs[1])
        elif len(fs) == 3:
            ap = ap.rearrange("p (a b c) -> p a b c", a=fs[0], b=fs[1], c=fs[2])
        elif len(fs) == 4:
            ap = ap.rearrange("p (a b c d) -> p a b c d", a=fs[0], b=fs[1], c=fs[2], d=fs[3])
        return ap


def _cst_layout():
    off = {}
    o = 0

    def add(name, n):
        nonlocal o
        off[name] = (o, n)
        o += n

    add("ident", 128)
    add("trim0", 128)
    add("trim1", 128)
    add("tris0", 128)
    add("tris1", 128)
    add("mask0", 128)
    add("mask1", 128)
    add("rm", 2)
    add("cond", 16)
    add("gmix", 32)
    add("gffn", 32)
    for i in range(2):
        add("gq%d" % i, 64)
        add("gk%d" % i, 64)
        add("gout%d" % i, 1)
        add("gqn%d" % i, 384)
        add("gkvn%d" % i, 256)
        add("gq96%d" % i, 96)
        add("gk96%d" % i, 96)
    return off, o


CST_OFF, NCST = _cst_layout()


class KB:
    def __init__(self, debug=(), stages=None):
        self.stages = set(stages) if stages is not None else {"adaln", "ffn", "mixc", "mixab", "P", "S"}
        self.debug = set(debug)
        self.dbg_outs = {}

    def mm(self, out, lhsT, rhs, start, stop, r, w, **kw):
        self.P.op("pe", lambda h: h.matmul(out, lhsT, rhs, start=start, stop=stop, **kw), r=r, w=w)

    def tr(self, out, in_, ident, r, w):
        self.P.op("pe", lambda h: h.transpose(out, in_, ident), r=r, w=w)

    def act(self, out, in_, func, r, w, **kw):
        self.P.op("act", lambda h: h.activation(out, in_, func, **kw), r=r, w=w)

    def tt(self, eng, out, a, b, op, r, w):
        self.P.op(eng, lambda h: h.tensor_tensor(out, a, b, op), r=r, w=w)

    def stt(self, out, in0, scalar, in1, op0, op1, r, w):
        self.P.op("dve", lambda h: h.scalar_tensor_tensor(out, in0, scalar, in1, op0, op1), r=r, w=w)

    def cp(self, eng, out, in_, r, w):
        if eng == "act":
            self.P.op("act", lambda h: h.copy(out, in_), r=r, w=w)
        else:
            self.P.op(eng, lambda h: h.tensor_copy(out, in_), r=r, w=w)

    def recip(self, out, in_, r, w):
        self.P.op("dve", lambda h: h.reciprocal(out, in_), r=r, w=w)

    def red(self, out, in_, r, w):
        self.P.op("dve", lambda h: h.tensor_reduce(out, in_, AX.X, ALU.add), r=r, w=w)

    def memset(self, eng, ap, val, w):
        self.P.op(eng, lambda h: h.memset(ap, val), w=w)

    def dma(self, q, out, in_, r, w, dq):
        self.P.op(q, lambda h: h.dma_start(out=out, in_=in_), r=r, w=w, dq=dq)

    def bank(self, pool):
        lst, idx = self.pools[pool]
        b = lst[idx % len(lst)]
        self.pools[pool][1] = idx + 1
        return b

    def rstd(self, out, ss, n, r, w):
        self.act(out, ss, AF.Sqrt, r=r, w=w, bias=EPS, scale=1.0 / n)
        self.recip(out, out, r=w, w=w)

    def cst(self, name, parts=128):
        o, n = CST_OFF[name]
        return self.cst_t[0:parts, o:o + n]

    def dbg(self, name, ap, r, shape):
        if name not in self.debug:
            return
        t = self.nc.dram_tensor("dbg_" + name, list(shape), ap.dtype if hasattr(ap, "dtype") else F32, kind="ExternalOutput").ap()
        self.dbg_outs[name] = shape
        self.dma("sp", t, ap, r=r, w=[], dq=("dbg", name))

    def build(self):
        nc = bass.Bass("TRN2", target_bir_lowering=False)
        self.nc = nc
        self.P = Prog()

        def din(name, shape):
            return nc.dram_tensor(name, list(shape), F32, kind="ExternalInput").ap()

        def dout(name, shape):
            return nc.dram_tensor(name, list(shape), F32, kind="ExternalOutput").ap()

        I = {}
        I["cst"] = din("cst", [128, NCST])
        I["xp"] = din("xp", [1024, D])
        I["xs"] = din("xs", [1024, D])
        I["ada_w"] = din("ada_w", [4, D, 6 * D])
        I["ada_b"] = din("ada_b", [4, 6 * D])
        I["ffn_w_in"] = din("ffn_w_in", [4, D, 2 * FH])
        I["ffn_w_out"] = din("ffn_w_out", [4, FH, D])
        I["ab_w_in"] = din("ab_w_in", [2, D, 2240])
        I["ab_w_out"] = din("ab_w_out", [2, D, D])
        I["a_w2"] = din("a_w2", [2, 2, 16, 256])
        I["a_b"] = din("a_b", [2, 2, 256])
        I["w_qb"] = din("w_qb", [2, 384, 768])
        I["w_kvb"] = din("w_kvb", [2, 256, 1024])
        I["gqa_w_in"] = din("gqa_w_in", [2, D, 1536])
        I["gqa_w_out"] = din("gqa_w_out", [2, D, D])
        I["c_ckv"] = din("c_ckv", [2, 512, 256])
        I["c_kpe"] = din("c_kpe", [2, 512, 32])
        I["c_gla"] = din("c_gla", [2, 2, 256, 128])
        I["c_gk"] = din("c_gk", [2, 512, 256])
        I["c_gv"] = din("c_gv", [2, 512, 256])
        I["ropeg"] = din("ropeg", [1024, 2, 32])
        I["ropem"] = din("ropem", [1024, 2, 16])
        O = {}
        O["yp"] = dout("yp", [1024, D])
        O["ys"] = dout("ys", [1024, D])
        O["o_ckv"] = dout("o_ckv", [4, 2, 256, 256])
        O["o_kpe"] = dout("o_kpe", [4, 2, 256, 32])
        O["o_gla"] = dout("o_gla", [4, 2, 2, 256, 128])
        O["o_gk"] = dout("o_gk", [4, 2, 256, 256])
        O["o_gv"] = dout("o_gv", [4, 2, 256, 256])
        self.I, self.O = I, O

        with contextlib.ExitStack() as es:
            self.xT = es.enter_context(nc.sbuf_tensor("xT", [128, 8, 1024], F32))
            self.cst_t = es.enter_context(nc.sbuf_tensor("cst_sb", [128, NCST], F32))
            small = es.enter_context(nc.sbuf_tensor("small", [128, 4 * 48 * 2 + 48 + 8], F32))
            cbf = es.enter_context(nc.sbuf_tensor("cbf", [128, 256 + 16], BF16))
            arena_t = es.enter_context(nc.sbuf_tensor("arena", [128, ARENA_BYTES // 2], BF16))
            self.A = Arena(arena_t, ARENA_BYTES)
            self.ps = [es.enter_context(nc.psum_tensor("ps%d" % i, [128, 512], F32)) for i in range(8)]
            self.psb = [p.bitcast(BF16) for p in self.ps]
            self.pools = {"mm": [[0, 1, 2, 3], 0], "acc": [[4, 5], 0], "tr": [[6, 7], 0]}
            self.modt = small[:, 0:384].rearrange("p (l c k) -> p l c k", l=4, c=48, k=2)
            self.lsc = small[:, 384:432].rearrange("p (a b) -> p a b", a=6, b=8)
            self.eps_c = small[:, 432:433]
            self.one_c = small[:, 433:434]
            self.ident_b = cbf[:, 0:128]
            self.ones_b = cbf[:, 128:256]
            self.sc_b = cbf[:, 256:272].rearrange("p (k c) -> p k c", k=8, c=2)
            self.ident_f = self.cst("ident")

            self.prologue()
            if "adaln" in self.stages:
                self.adaln_all()
            for pas in ("P", "S"):
                if pas in self.stages:
                    self.run_pass(pas)
            self.P.emit(nc, es)
        return nc

    def prologue(self):
        self.dma("sp", self.cst_t[:, :], self.I["cst"], r=[], w=["cst"], dq="cst")
        self.memset("dve", self.eps_c, EPS, w=["small_c"])
        self.memset("dve", self.one_c, 1.0, w=["small_c"])
        self.memset("dve", self.ones_b, 1.0, w=["ones"])
        self.cp("dve", self.ident_b, self.ident_f, r=["cst"], w=["identb"])
        cond = self.cst("cond").rearrange("p (k c) -> p k c", k=8, c=2)
        self.act(self.sc_b, cond, AF.Silu, r=["cst"], w=["scb"])

    def adaln_all(self):
        A = self.A
        A.reset()
        slots = [A.alloc([8, 512], BF16) for _ in range(3)]
        mtok = A.alloc([6144], F32)
        rm = self.cst("rm", parts=3)
        for l in range(4):
            self.dma("sp", mtok[2:3, :], self.I["ada_b"][l:l + 1, :], r=[], w=[("mtok", "b")], dq="mtokb")
            for j in range(12):
                s = (l * 12 + j) % 3
                src = self.I["ada_w"][l, :, j * 512:(j + 1) * 512].rearrange("(k p) n -> p k n", p=128)
                self.dma("pool", slots[s], src, r=[], w=[("adw", s)], dq=("adw", s))
                b = self.bank("mm")
                for kc in range(8):
                    self.mm(self.ps[b][0:2, :], self.sc_b[:, kc, :], slots[s][:, kc, :], kc == 0, kc == 7,
                            r=[("adw", s), "scb"], w=[("ps", b)])
                self.cp("act", mtok[0:2, j * 512:(j + 1) * 512], self.ps[b][0:2, :], r=[("ps", b)], w=[("mtok", j)])
            b = self.bank("mm")
            for c in range(48):
                self.mm(self.ps[b][:, 2 * c:2 * c + 2], mtok[0:3, c * 128:(c + 1) * 128], rm, True, True,
                        r=[("mtok", c // 4), ("mtok", "b"), "cst"], w=[("ps", b)])
            self.cp("dve", self.modt[:, l, :, :], self.ps[b][:, 0:96].rearrange("p (c k) -> p c k", c=48, k=2),
                    r=[("ps", b)], w=["modt"])
        self.P.barrier()

    def layer_scalars(self, l, col):
        mv = self.modt[:, l, :, col]
        gmix = self.cst("gmix").rearrange("p (l c) -> p l c", l=4, c=8)[:, l, :]
        gffn = self.cst("gffn").rearrange("p (l c) -> p l c", l=4, c=8)[:, l, :]
        L = self.lsc
        self.stt(L[:, 0, :], mv[:, 8:16], 1.0, gmix, ALU.add, ALU.mult, r=["modt", "cst"], w=["lsc"])
        self.cp("dve", L[:, 1, :], mv[:, 0:8], r=["modt"], w=["lsc"])
        self.cp("dve", L[:, 2, :], mv[:, 16:24], r=["modt"], w=["lsc"])
        self.stt(L[:, 3, :], mv[:, 32:40], 1.0, gffn, ALU.add, ALU.mult, r=["modt", "cst"], w=["lsc"])
        self.cp("dve", L[:, 4, :], mv[:, 24:32], r=["modt"], w=["lsc"])
        self.cp("dve", L[:, 5, :], mv[:, 40:48], r=["modt"], w=["lsc"])

    def alloc_norm_tmp(self):
        A = self.A
        self.nm_sq = A.alloc([8, 128], BF16)
        self.nm_rstd = A.alloc([128], F32)
        self.nm_tmp = A.alloc([8, 128], F32)

    def normmod(self, t, which, dst, dstkey):
        xv = self.xT[:, :, t * 128:(t + 1) * 128]
        xk = [("xT", t, c) for c in range(8)]
        G = self.lsc[:, 3 * which, :]
        SH = self.lsc[:, 3 * which + 1, :]
        self.act(self.nm_sq, xv, AF.Square, r=xk, w=["nm_sq"])
        b = self.bank("tr")
        for c in range(8):
            self.mm(self.ps[b][:, 0:128], self.ones_b, self.nm_sq[:, c, :], c == 0, c == 7, r=["nm_sq", "ones"], w=[("ps", b)])
        self.rstd(self.nm_rstd, self.ps[b][:, 0:128], D, r=[("ps", b), "small_c"], w=["nm_rstd"])
        self.tt("dve", self.nm_tmp, xv, self.nm_rstd.unsqueeze(1).broadcast_to([128, 8, 128]), ALU.mult,
                r=xk + ["nm_rstd"], w=["nm_tmp"])
        self.tt("pool", self.nm_tmp, self.nm_tmp, G.unsqueeze(2).broadcast_to([128, 8, 128]), ALU.mult,
                r=["nm_tmp", "lsc"], w=["nm_tmp"])
        self.tt("dve", dst, self.nm_tmp, SH.unsqueeze(2).broadcast_to([128, 8, 128]), ALU.add,
                r=["nm_tmp", "lsc"], w=[dstkey])

    def run_pass(self, pas):
        self.pas = pas
        col = 0 if pas == "P" else 1
        xin_d = self.I["xp"] if pas == "P" else self.I["xs"]
        yout_d = self.O["yp"] if pas == "P" else self.O["ys"]
        if pas == "P":
            self.seqs = [dict(tiles=[2 * s, 2 * s + 1], ctx=False, rope=False, bidx=s) for s in range(4)]
        else:
            self.seqs = [dict(tiles=list(range(8)), ctx=True, rope=True, bidx=None)]
        A = self.A
        A.reset()
        xin = [A.alloc([1024], F32) for _ in range(2)]
        for t in range(8):
            s = t % 2
            self.dma("sp", xin[s], xin_d[t * 128:(t + 1) * 128, :], r=[], w=[("xin", s)], dq=("xin", s))
            for hb in range(2):
                b = self.bank("tr")
                for cc in range(4):
                    c = hb * 4 + cc
                    self.tr(self.ps[b][:, cc * 128:(cc + 1) * 128], xin[s][:, c * 128:(c + 1) * 128], self.ident_f,
                            r=[("xin", s), "cst"], w=[("ps", b)])
                self.cp("dve" if hb else "act", self.xT[:, hb * 4:hb * 4 + 4, t * 128:(t + 1) * 128],
                        self.ps[b][:, :].rearrange("p (a b) -> p a b", a=4),
                        r=[("ps", b)], w=[("xT", t, hb * 4 + cc) for cc in range(4)])
        self.P.barrier()
        for l in range(4):
            self.layer_scalars(l, col)
            if l % 2 == 0:
                if "mixab" in self.stages:
                    self.mixer_ab(l, l // 2)
            else:
                if "mixc" in self.stages:
                    self.mixer_c(l, l // 2)
            if self.pas == "P" and l in (0, 1):
                self.dbg("x_l%d_mix" % l, self.xT[:, :, :], [("xT", t, c) for t in range(8) for c in range(8)], [128, 8, 1024])
            self.P.barrier()
            if "ffn" in self.stages:
                self.ffn(l)
            if self.pas == "P" and l in (0, 1):
                self.dbg("x_l%d_ffn" % l, self.xT[:, :, :], [("xT", t, c) for t in range(8) for c in range(8)], [128, 8, 1024])
            self.P.barrier()
        A.reset()
        yo = [A.alloc([1024], F32) for _ in range(2)]
        for t in range(8):
            s = t % 2
            for hb in range(2):
                b = self.bank("tr")
                for cc in range(4):
                    c = hb * 4 + cc
                    self.tr(self.ps[b][:, cc * 128:(cc + 1) * 128], self.xT[:, c, t * 128:(t + 1) * 128], self.ident_f,
                            r=[("xT", t, c), "cst"], w=[("ps", b)])
                self.cp("dve" if hb else "act", yo[s][:, hb * 512:(hb + 1) * 512], self.ps[b][:, :],
                        r=[("ps", b)], w=[("yo", s, hb)])
            self.dma("sp", yout_d[t * 128:(t + 1) * 128, :], yo[s], r=[("yo", s, 0), ("yo", s, 1)], w=[], dq=("yo", s))
        self.P.barrier()

    def ffn(self, l):
        A = self.A
        A.reset()
        hT = A.alloc([8, 1024], BF16)
        actT = A.alloc([22, 1024], BF16)
        wi = [A.alloc([8, 2, 256], BF16) for _ in range(3)]
        wo = [A.alloc([11, 1024], BF16) for _ in range(2)]
        sg = [A.alloc([512], F32) for _ in range(2)]
        self.alloc_norm_tmp()
        W1 = self.I["ffn_w_in"]
        W2 = self.I["ffn_w_out"]

        def load_wi(j2):
            s = j2 % 3
            for gu in range(2):
                c0 = gu * FH + j2 * 256
                src = W1[l, :, c0:c0 + 256].rearrange("(k p) n -> p k n", p=128)
                self.dma("pool", wi[s][:, :, gu, :], src, r=[], w=[("wi", s, gu)], dq=("wi", s))

        def load_wo(hf):
            src = W2[l, hf * 1408:(hf + 1) * 1408, :].rearrange("(j p) n -> p j n", p=128)
            self.dma("pool", wo[hf], src, r=[], w=[("wo", hf)], dq=("wo", hf))

        for j2 in range(3):
            load_wi(j2)
        for t in range(8):
            self.normmod(t, 1, hT[:, :, t * 128:(t + 1) * 128], ("hT", t))
        load_wo(0)
        load_wo(1)
        k = 0
        for j2 in range(11):
            s = j2 % 3
            for tb in range(2):
                hk = [("hT", tb * 4 + q) for q in range(4)]
                for hf in range(2):
                    j = j2 * 2 + hf
                    bg = self.bank("mm")
                    for kc in range(8):
                        self.mm(self.ps[bg][:, :], wi[s][:, kc, 0, hf * 128:(hf + 1) * 128], hT[:, kc, tb * 512:(tb + 1) * 512],
                                kc == 0, kc == 7, r=[("wi", s, 0)] + hk, w=[("ps", bg)])
                    bu = self.bank("mm")
                    for kc in range(8):
                        self.mm(self.ps[bu][:, :], wi[s][:, kc, 1, hf * 128:(hf + 1) * 128], hT[:, kc, tb * 512:(tb + 1) * 512],
                                kc == 0, kc == 7, r=[("wi", s, 1)] + hk, w=[("ps", bu)])
                    sgi = k % 2
                    k += 1
                    self.act(sg[sgi], self.ps[bg][:, :], AF.Silu, r=[("ps", bg)], w=[("sg", sgi)])
                    self.tt("dve", actT[:, j, tb * 512:(tb + 1) * 512], sg[sgi], self.ps[bu][:, :], ALU.mult,
                            r=[("sg", sgi), ("ps", bu)], w=[("act", j, tb)])
            if j2 + 3 < 11:
                load_wi(j2 + 3)
        gate = self.lsc[:, 5, :]
        for hf in range(2):
            for c in range(8):
                for tb in range(2):
                    b = self.bank("mm")
                    for jj in range(11):
                        self.mm(self.ps[b][:, :], wo[hf][:, jj, c * 128:(c + 1) * 128], actT[:, hf * 11 + jj, tb * 512:(tb + 1) * 512],
                                jj == 0, jj == 10, r=[("wo", hf), ("act", hf * 11 + jj, tb)], w=[("ps", b)])
                    xv = self.xT[:, c, tb * 512:(tb + 1) * 512]
                    xk = [("xT", tb * 4 + q, c) for q in range(4)]
                    self.stt(xv, self.ps[b][:, :], gate[:, c:c + 1], xv, ALU.mult, ALU.add, r=[("ps", b), "lsc"] + xk, w=xk)

    def mixer_residual(self, t, banks, tmp):
        gate = self.lsc[:, 2, :]
        for hb in range(2):
            b = banks[hb]
            pv = self.ps[b][:, :].rearrange("p (a b) -> p a b", a=4)
            tv = tmp[:, hb * 4:hb * 4 + 4, :]
            self.tt("dve", tv, pv, gate[:, hb * 4:hb * 4 + 4].unsqueeze(2).broadcast_to([128, 4, 128]), ALU.mult,
                    r=[("ps", b), "lsc"], w=[("mr_tmp", hb)])
            xv = self.xT[:, hb * 4:hb * 4 + 4, t * 128:(t + 1) * 128]
            xk = [("xT", t, hb * 4 + q) for q in range(4)]
            self.tt("pool", xv, xv, tv, ALU.add, r=[("mr_tmp", hb)] + xk, w=xk)

    def rope(self, xv, H, Q, cs, tmps, r, w, keyp):
        x1 = xv[:, :, :, 0, :]
        x2 = xv[:, :, :, 1, :]
        c = cs[:, 0, :].rearrange("p (a q) -> p a q", a=2, q=Q).unsqueeze(1).broadcast_to([128, H, 2, Q])
        s = cs[:, 1, :].rearrange("p (a q) -> p a q", a=2, q=Q).unsqueeze(1).broadcast_to([128, H, 2, Q])
        t1, t2, t3, t4 = tmps
        k1, k2, k3, k4 = [(keyp, i) for i in range(4)]
        self.tt("dve", t1, x1, c, ALU.mult, r=r, w=[k1])
        self.tt("pool", t2, x2, s, ALU.mult, r=r, w=[k2])
        self.tt("dve", t3, x1, s, ALU.mult, r=r, w=[k3])
        self.tt("pool", t4, x2, c, ALU.mult, r=r, w=[k4])
        self.tt("dve", x1, t1, t2, ALU.subtract, r=[k1, k2, k3], w=w)
        self.tt("dve", x2, t3, t4, ALU.add, r=[k3, k4], w=w)

    def attention(self, KT, kparts, kt_list, QTv, nq, VA_of, scale, Pt, out_of, kr, qr, vr, ow):
        ob = self.bank("acc")
        first, last = kt_list[0], kt_list[-1]
        for st in kt_list:
            sb = self.bank("mm")
            self.mm(self.ps[sb][:, 0:nq * 128], KT(st), QTv, True, True, r=kr + qr, w=[("ps", sb)])
            pi = self.pt_i % 2
            self.pt_i += 1
            self.act(Pt[pi][:, 0:nq, :], self.ps[sb][:, 0:nq * 128].rearrange("p (a b) -> p a b", a=nq), AF.Exp,
                     r=[("ps", sb)], w=[("Pt", pi)], scale=scale)
            for j in range(nq):
                self.mm(self.ps[ob][:, j * 65:(j + 1) * 65], Pt[pi][:, j, :], VA_of(st), (st == first and j == 0), st == last,
                        r=[("Pt", pi)] + vr, w=[("ps", ob)], skip_group_check=True)
        ov = self.ps[ob][:, 0:nq * 65].rearrange("p (a b) -> p a b", a=nq)
        rd = self.at_rden
        self.recip(rd[:, 0:nq], ov[:, :, 64], r=[("ps", ob)], w=["at_rden"])
        for j in range(nq):
            dst, dk = out_of(j)
            self.P.op("dve", lambda h, j=j, dst=dst, rd=rd, ov=ov: h.tensor_scalar(dst, ov[:, j, 0:64], rd[:, j:j + 1], None, ALU.mult),
                      r=[("ps", ob), "at_rden"], w=[dk])

    def mixer_c(self, l, i):
        A = self.A
        A.reset()
        I, O = self.I, self.O
        w_in = A.alloc([8, 1536], BF16)
        w_out = A.alloc([8, 1024], BF16)
        hT = A.alloc([8, 1024], BF16)
        KT = A.alloc([4, 1536], BF16)
        VA = A.alloc([12, 4, 65], BF16)
        self.alloc_norm_tmp()
        kvf = A.alloc([512], F32)
        sq = A.alloc([1024], F32)
        qn = A.alloc([1024], F32)
        st16 = A.alloc([16], F32)
        knb = A.alloc([256], BF16)
        qnb = A.alloc([1024], BF16)
        QT = A.alloc([16, 128], BF16)
        Pt = [A.alloc([4, 128], BF16) for _ in range(2)]
        ob = A.alloc([1024], BF16)
        OT = A.alloc([8, 128], BF16)
        mr_tmp = A.alloc([8, 128], F32)
        self.at_rden = A.alloc([4], F32)
        rtm = [A.alloc([16, 2, 16], F32) for _ in range(4)]
        ropeg = A.alloc([8, 2, 32], F32)
        ck = A.alloc([4, 256], F32)
        cv = A.alloc([4, 256], F32)
        ckb = A.alloc([4, 256], BF16)
        self.pt_i = 0
        gq = self.cst("gq%d" % i)
        gk = self.cst("gk%d" % i)
        for kh in range(2):
            self.dma("pool", w_in[:, kh * 4:(kh + 1) * 4, :],
                     I["gqa_w_in"][i, kh * 512:(kh + 1) * 512, :].rearrange("(k p) n -> p k n", p=128), r=[], w=[("w_in", kh)], dq="w_in")
        self.dma("pool", w_out, I["gqa_w_out"][i].rearrange("(k p) n -> p k n", p=128), r=[], w=["w_out"], dq="w_out")
        self.memset("pool", VA[:, :, :, 64:65], 1.0, w=["VA1"])
        wk = [("w_in", 0), ("w_in", 1)]
        if self.pas == "S":
            self.dma("sp", ropeg, I["ropeg"].rearrange("(t p) c q -> p t c q", p=128), r=[], w=["ropeg"], dq="ropeg")

        def put_keys(knb_ap, kt, rk):
            b = self.bank("tr")
            for g in range(4):
                self.tr(self.psb[b][0:64, g * 128:(g + 1) * 128], knb_ap[:, g * 64:(g + 1) * 64], self.ident_b,
                        r=rk + ["identb"], w=[("ps", b)])
            self.cp("dve", KT[0:64, :, kt * 128:(kt + 1) * 128], self.psb[b][0:64, 0:512].rearrange("p (a b) -> p a b", a=4),
                    r=[("ps", b)], w=[("KT", kt)])

        for sq_ in self.seqs:
            tiles = sq_["tiles"]
            nk0 = 4 if sq_["ctx"] else 0
            if sq_["ctx"]:
                self.dma("sp", ck, I["c_gk"][i].rearrange("(t p) n -> p t n", p=128), r=[], w=["ck"], dq="ck")
                self.dma("sp", cv, I["c_gv"][i].rearrange("(t p) n -> p t n", p=128), r=[], w=["cv"], dq="cv")
                self.cp("act", ckb, ck, r=["ck"], w=["ckb"])
                for kt in range(4):
                    put_keys(ckb[:, kt, :], kt, ["ckb"])
                    self.cp("pool", VA[:, kt, :, 0:64], cv[:, kt, :].rearrange("p (g d) -> p g d", g=4), r=["cv"], w=[("VA", kt)])
            for n, t in enumerate(tiles):
                self.normmod(t, 0, hT[:, :, t * 128:(t + 1) * 128], ("hT", t))
                b = self.bank("mm")
                for kc in range(8):
                    self.mm(self.ps[b][:, :], hT[:, kc, t * 128:(t + 1) * 128], w_in[:, kc, 1024:1536], kc == 0, kc == 7,
                            r=[("hT", t)] + wk, w=[("ps", b)])
                self.cp("act", kvf, self.ps[b][:, :], r=[("ps", b)], w=["kvf"])
                self.act(sq[:, 0:256], self.ps[b][:, 0:256], AF.Square, r=[("ps", b)], w=[("sqq", 0)])
                self.red(st16[:, 0:4], sq[:, 0:256].rearrange("p (g d) -> p g d", g=4), r=[("sqq", 0)], w=["st16"])
                self.rstd(st16[:, 0:4], st16[:, 0:4], 64, r=["st16", "small_c"], w=["st16"])
                kv3 = kvf[:, 0:256].rearrange("p (g d) -> p g d", g=4)
                self.tt("dve", kv3, kv3, st16[:, 0:4].unsqueeze(2).broadcast_to([128, 4, 64]), ALU.mult, r=["kvf", "st16"], w=["kvf"])
                self.tt("dve", kv3, kv3, gk.unsqueeze(1).broadcast_to([128, 4, 64]), ALU.mult, r=["kvf", "cst"], w=["kvf"])
                if sq_["bidx"] is not None:
                    bi = sq_["bidx"]
                    self.dma("sp", O["o_gk"][bi, i, n * 128:(n + 1) * 128, :], kvf[:, 0:256], r=["kvf"], w=[], dq="kvf_o")
                    self.dma("sp", O["o_gv"][bi, i, n * 128:(n + 1) * 128, :], kvf[:, 256:512], r=["kvf"], w=[], dq="kvf_o")
                if sq_["rope"]:
                    self.rope(kvf[:, 0:256].rearrange("p (h a b q) -> p h a b q", h=4, a=2, b=2, q=16), 4, 16, ropeg[:, t, :, :],
                              [x[:, 0:4, :, :] for x in rtm], r=["kvf", "ropeg"], w=["kvf"], keyp="rtm")
                self.cp("act", knb, kvf[:, 0:256], r=["kvf"], w=["knb"])
                put_keys(knb, nk0 + n, ["knb"])
                self.cp("pool", VA[:, nk0 + n, :, 0:64], kvf[:, 256:512].rearrange("p (g d) -> p g d", g=4), r=["kvf"], w=[("VA", nk0 + n)])
            nkt = nk0 + len(tiles)
            kt_list = list(range(nkt))
            kkeys = [("KT", kt) for kt in kt_list]
            vkeys = [("VA", kt) for kt in kt_list] + ["VA1"]
            for n, t in enumerate(tiles):
                qb = []
                for bk in range(2):
                    b = self.bank("mm")
                    qb.append(b)
                    for kc in range(8):
                        self.mm(self.ps[b][:, :], hT[:, kc, t * 128:(t + 1) * 128], w_in[:, kc, bk * 512:(bk + 1) * 512], kc == 0, kc == 7,
                                r=[("hT", t)] + wk, w=[("ps", b)])
                    self.act(sq[:, bk * 512:(bk + 1) * 512], self.ps[b][:, :], AF.Square, r=[("ps", b)], w=[("sqq", bk)])
                self.red(st16, sq.rearrange("p (g d) -> p g d", g=16), r=[("sqq", 0), ("sqq", 1)], w=["st16"])
                self.rstd(st16, st16, 64, r=["st16", "small_c"], w=["st16"])
                for bk in range(2):
                    self.tt("dve", qn[:, bk * 512:(bk + 1) * 512].rearrange("p (g d) -> p g d", g=8),
                            self.ps[qb[bk]][:, :].rearrange("p (g d) -> p g d", g=8),
                            st16[:, bk * 8:(bk + 1) * 8].unsqueeze(2).broadcast_to([128, 8, 64]), ALU.mult,
                            r=[("ps", qb[bk]), "st16"], w=["qn"])
                qn3 = qn.rearrange("p (g d) -> p g d", g=16)
                self.tt("pool", qn3, qn3, gq.unsqueeze(1).broadcast_to([128, 16, 64]), ALU.mult,
                        r=["qn", "cst"], w=["qn"])
                if sq_["rope"]:
                    self.rope(qn.rearrange("p (h a b q) -> p h a b q", h=16, a=2, b=2, q=16), 16, 16, ropeg[:, t, :, :], rtm,
                              r=["qn", "ropeg"], w=["qn"], keyp="rtm")
                self.cp("act", qnb, qn, r=["qn"], w=["qnb"])
                for hb in range(2):
                    b = self.bank("tr")
                    for hh in range(8):
                        h_ = hb * 8 + hh
                        self.tr(self.psb[b][0:64, hh * 128:(hh + 1) * 128], qnb[:, h_ * 64:(h_ + 1) * 64], self.ident_b,
                                r=["qnb", "identb"], w=[("ps", b)])
                    self.cp("dve" if hb else "act", QT[0:64, hb * 8:(hb + 1) * 8, :],
                            self.psb[b][0:64, :].rearrange("p (a b) -> p a b", a=8), r=[("ps", b)], w=[("QT", hb)])
                for g in range(4):
                    self.attention(
                        KT=lambda st, g=g: KT[0:64, g, st * 128:(st + 1) * 128], kparts=64, kt_list=kt_list,
                        QTv=QT[0:64, 4 * g:4 * g + 4, :], nq=4,
                        VA_of=lambda st, g=g: VA[:, st, g, :], scale=0.125, Pt=Pt,
                        out_of=lambda j, g=g: (ob[:, (4 * g + j) * 64:(4 * g + j + 1) * 64], ("ob", g)),
                        kr=kkeys, qr=[("QT", g // 2)], vr=vkeys, ow=None)
                b = self.bank("tr")
                for c in range(8):
                    self.tr(self.psb[b][:, c * 128:(c + 1) * 128], ob[:, c * 128:(c + 1) * 128], self.ident_b,
                            r=[("ob", c // 2), "identb"], w=[("ps", b)])
                self.cp("act", OT, self.psb[b][:, :].rearrange("p (a b) -> p a b", a=8), r=[("ps", b)], w=["OT"])
                banks = []
                for hb in range(2):
                    b = self.bank("mm")
                    banks.append(b)
                    for cc in range(4):
                        c = hb * 4 + cc
                        for kc in range(8):
                            self.mm(self.ps[b][:, cc * 128:(cc + 1) * 128], w_out[:, kc, c * 128:(c + 1) * 128], OT[:, kc, :], kc == 0, kc == 7,
                                    r=["w_out", "OT"], w=[("ps", b)])
                self.mixer_residual(t, banks, mr_tmp)

    def mixer_ab(self, l, i):
        A = self.A
        A.reset()
        I, O = self.I, self.O
        S_pass = self.pas == "S"
        w_in = A.alloc([8, 2240], BF16)
        w_out = A.alloc([8, 1024], BF16)
        OG = A.alloc([4, 1024], BF16)
        hTm = [A.alloc([8, 128], BF16) for _ in range(2)]
        self.alloc_norm_tmp()
        self.at_rden = A.alloc([4], F32)
        Pt = [A.alloc([4, 128], BF16) for _ in range(2)]
        self.pt_i = 0
        base_off = A.off
        for kh in range(2):
            for ch in range(2):
                self.dma("pool", w_in[:, kh * 4:(kh + 1) * 4, ch * 1120:(ch + 1) * 1120],
                         I["ab_w_in"][i, kh * 512:(kh + 1) * 512, ch * 1120:(ch + 1) * 1120].rearrange("(k p) n -> p k n", p=128),
                         r=[], w=[("w_in", kh, ch)], dq="w_in")
        self.dma("pool", w_out, I["ab_w_out"][i].rearrange("(k p) n -> p k n", p=128), r=[], w=["w_out"], dq="w_out")
        wk = [("w_in", 0, 0), ("w_in", 0, 1), ("w_in", 1, 0), ("w_in", 1, 1)]
        gout = self.cst("gout%d" % i)
        gqn = self.cst("gqn%d" % i)
        gkvn = self.cst("gkvn%d" % i)
        gq96 = self.cst("gq96%d" % i)
        gk96 = self.cst("gk96%d" % i)
        trim = [self.cst("trim0"), self.cst("trim1")]
        tris = [self.cst("tris0"), self.cst("tris1")]
        mask = [self.cst("mask0"), self.cst("mask1")]
        hi = 0
        for sq_ in self.seqs:
            tiles = sq_["tiles"]
            NT = len(tiles)
            T = NT * 128
            t0 = tiles[0]
            A.reset(base_off)
            qlT = [A.alloc([2, T], BF16) for _ in range(2)]
            klT = [A.alloc([2, T], BF16) for _ in range(2)]
            kst = [A.alloc([NT, 256], BF16) for _ in range(2)]
            vtk = A.alloc([NT, 512], BF16)
            rsT = A.alloc([4, T], BF16)
            oT = A.alloc([4, T], F32)
            etot = A.alloc([4, NT], F32)
            Sst = [[A.alloc([128], F32) for _ in range(2)] for _ in range(2)]
            Sbf = [[A.alloc([128], BF16) for _ in range(2)] for _ in range(2)]
            alT = A.alloc([128], F32)
            lsp = A.alloc([512], F32)
            Eb = A.alloc([4, 128], F32)
            Enb = A.alloc([4, 128], F32)
            Ed2 = A.alloc([512], F32)
            ATm = [A.alloc([128], BF16) for _ in range(2)]
            osq = A.alloc([4, 128], BF16)
            orst = A.alloc([4, 128], F32)
            otmp = A.alloc([4, 128], F32)
            self.memset("dve", alT[32:33, :], 1.0, w=["alT1"])
            aw2 = A.alloc([512], F32)
            self.memset("dve", aw2[0:33, :], 0.0, w=["aw2"])
            for z in range(2):
                self.dma("sp", aw2[16 * z:16 * z + 16, z * 256:(z + 1) * 256], I["a_w2"][i, z], r=[], w=["aw2"], dq="aw2")
            self.dma("sp", aw2[32:33, :], I["a_b"][i:i + 1].rearrange("o z n -> o (z n)"), r=[], w=["aw2"], dq="aw2")
            for n, t in enumerate(tiles):
                nc_ = slice(n * 128, (n + 1) * 128)
                h = hTm[hi % 2]
                hk = ("hTm", hi % 2)
                hi += 1
                self.normmod(t, 0, h, hk)
                bqk = self.bank("mm")
                for ch in range(4):
                    for kc in range(8):
                        self.mm(self.ps[bqk][:, ch * 128:(ch + 1) * 128], w_in[:, kc, ch * 128:(ch + 1) * 128], h[:, kc, :], kc == 0, kc == 7,
                                r=[hk] + wk, w=[("ps", bqk)])
                br = self.bank("mm")
                for ch in range(4):
                    for kc in range(8):
                        self.mm(self.ps[br][:, ch * 128:(ch + 1) * 128], w_in[:, kc, 1024 + ch * 128:1024 + (ch + 1) * 128], h[:, kc, :], kc == 0, kc == 7,
                                r=[hk] + wk, w=[("ps", br)])
                self.act(rsT[:, :, nc_], self.ps[br][:, :].rearrange("p (a b) -> p a b", a=4), AF.Silu, r=[("ps", br)], w=[("rsT", n)])
                ba = self.bank("tr")
                for kc in range(8):
                    self.mm(self.ps[ba][0:32, 0:128], w_in[:, kc, 1536:1568], h[:, kc, :], kc == 0, kc == 7, r=[hk] + wk, w=[("ps", ba)])
                self.cp("act", alT[0:32, :], self.ps[ba][0:32, 0:128], r=[("ps", ba)], w=["alT"])
                bkv = self.bank("mm")
                for kc in range(8):
                    self.mm(self.ps[bkv][:, :], h[:, kc, :], w_in[:, kc, 256:768], kc == 0, kc == 7, r=[hk] + wk, w=[("ps", bkv)])
                bv2 = self.bank("mm")
                for kc in range(8):
                    self.mm(self.ps[bv2][:, 0:256], h[:, kc, :], w_in[:, kc, 768:1024], kc == 0, kc == 7, r=[hk] + wk, w=[("ps", bv2)])
                self.cp("act", vtk[:, n, 0:256], self.ps[bkv][:, 256:512], r=[("ps", bkv)], w=[("vtk", n, 0)])
                self.cp("act", vtk[:, n, 256:512], self.ps[bv2][:, 0:256], r=[("ps", bv2)], w=[("vtk", n, 1)])
                bl = self.bank("tr")
                self.mm(self.ps[bl][:, :], alT[0:33, :], aw2[0:33, :], True, True, r=["alT", "alT1", "aw2"], w=[("ps", bl)])
                self.act(lsp, self.ps[bl][:, :], AF.Exp, r=[("ps", bl)], w=["lsp"], scale=-1.0)
                self.act(lsp, lsp, AF.Ln, r=["lsp"], w=["lsp"], bias=1.0)
                bb = self.bank("tr")
                for z in range(2):
                    for fc in range(2):
                        zf = z * 2 + fc
                        self.mm(self.ps[bb][:, zf * 128:(zf + 1) * 128], lsp[:, zf * 128:(zf + 1) * 128], trim[z], True, True,
                                r=["lsp", "cst"], w=[("ps", bb)])
                bd = self.bank("tr")
                for z in range(2):
                    self.mm(self.ps[bd][:, z * 256:(z + 1) * 256], tris[z], lsp[:, z * 256:(z + 1) * 256], True, True,
                            r=["lsp", "cst"], w=[("ps", bd)])
                pbb = self.ps[bb][:, :].rearrange("p (a b) -> p a b", a=4)
                self.act(Eb, pbb, AF.Exp, r=[("ps", bb)], w=["Eb"])
                self.act(Enb, pbb, AF.Exp, r=[("ps", bb)], w=["Enb"], scale=-1.0)
                self.act(Ed2, self.ps[bd][:, :], AF.Exp, r=[("ps", bd)], w=["Ed2"])
                self.cp("pool", etot[:, 0:2, n], Eb[:, 0:2, 127], r=["Eb"], w=[("etot", n)])
                self.cp("pool", etot[:, 2:4, n], Eb[:, 2:4, 0], r=["Eb"], w=[("etot", n)])
                pqk = self.ps[bqk][:, :].rearrange("p (a b) -> p a b", a=4)
                for z in range(2):
                    self.stt(qlT[z][:, :, nc_], pqk[:, 0:2, :], 0.125, Eb[:, 2 * z:2 * z + 2, :], ALU.mult, ALU.mult,
                             r=[("ps", bqk), "Eb"], w=[("qlT", z, n)])
                    self.tt("dve", klT[z][:, :, nc_], pqk[:, 2:4, :], Enb[:, 2 * z:2 * z + 2, :], ALU.mult,
                            r=[("ps", bqk), "Enb"], w=[("klT", z, n)])
                    self.tt("dve", kst[z][:, n, :], self.ps[bkv][:, 0:256], Ed2[:, z * 256:(z + 1) * 256], ALU.mult,
                            r=[("ps", bkv), "Ed2"], w=[("kst", z, n)])
            for z in (range(2) if "noscan" not in self.stages else ()):
                for fc in range(2):
                    if sq_["ctx"]:
                        self.dma("sp", Sst[z][fc], I["c_gla"][i, z, fc * 128:(fc + 1) * 128, :], r=[], w=[("S", z, fc)], dq=("S", z, fc))
                    else:
                        self.memset("pool", Sst[z][fc], 0.0, w=[("S", z, fc)])
                    self.cp("act", Sbf[z][fc], Sst[z][fc], r=[("S", z, fc)], w=[("Sbf", z, fc)])
                order = list(range(NT)) if z == 0 else list(range(NT - 1, -1, -1))
                ai = 0
                for n in order:
                    nc_ = slice(n * 128, (n + 1) * 128)
                    bo = self.bank("acc")
                    for hh in range(4):
                        fc, hp = hh // 2, hh % 2
                        pr = slice(hp * 64, (hp + 1) * 64)
                        bat = self.bank("mm")
                        self.mm(self.ps[bat][:, 0:128], klT[z][pr, fc, nc_], qlT[z][pr, fc, nc_], True, True,
                                r=[("klT", z, n), ("qlT", z, n)], w=[("ps", bat)])
                        am = ATm[ai % 2]
                        amk = ("ATm", ai % 2)
                        ai += 1
                        self.tt("dve", am, self.ps[bat][:, 0:128], mask[z], ALU.mult, r=[("ps", bat), "cst"], w=[amk])
                        self.mm(self.ps[bo][:, hh * 128:(hh + 1) * 128], vtk[:, n, hh * 128:(hh + 1) * 128], am, True, False,
                                r=[("vtk", n, 0), ("vtk", n, 1), amk], w=[("ps", bo)])
                        self.mm(self.ps[bo][:, hh * 128:(hh + 1) * 128], Sbf[z][fc][pr, :], qlT[z][pr, fc, nc_], False, True,
                                r=[("Sbf", z, fc), ("qlT", z, n)], w=[("ps", bo)])
                    pbo = self.ps[bo][:, :].rearrange("p (a b) -> p a b", a=4)
                    if z == 0:
                        self.cp("act", oT[:, :, nc_], pbo, r=[("ps", bo)], w=[("oT", n)])
                    else:
                        self.tt("dve", oT[:, :, nc_], oT[:, :, nc_], pbo, ALU.add, r=[("ps", bo), ("oT", n)], w=[("oT", n)])
                    for fc in range(2):
                        bu = self.bank("mm")
                        for hp in range(2):
                            hh = fc * 2 + hp
                            self.mm(self.ps[bu][hp * 64:(hp + 1) * 64, 0:128], kst[z][:, n, hh * 64:(hh + 1) * 64],
                                    vtk[:, n, hh * 128:(hh + 1) * 128], True, True,
                                    r=[("kst", z, n), ("vtk", n, 0), ("vtk", n, 1)], w=[("ps", bu)])
                        self.stt(Sst[z][fc], Sst[z][fc], etot[:, z * 2 + fc, n:n + 1], self.ps[bu][:, 0:128], ALU.mult, ALU.add,
                                 r=[("S", z, fc), ("etot", n), ("ps", bu)], w=[("S", z, fc)])
                        self.cp("act", Sbf[z][fc], Sst[z][fc], r=[("S", z, fc)], w=[("Sbf", z, fc)])
                if sq_["bidx"] is not None:
                    for fc in range(2):
                        self.dma("sp", O["o_gla"][sq_["bidx"], i, z, fc * 128:(fc + 1) * 128, :], Sst[z][fc],
                                 r=[("S", z, fc)], w=[], dq=("S", z, fc))
            for n, t in (enumerate(tiles) if "noglaout" not in self.stages else ()):
                nc_ = slice(n * 128, (n + 1) * 128)
                self.act(osq, oT[:, :, nc_], AF.Square, r=[("oT", n)], w=["osq"])
                b = self.bank("tr")
                self.mm(self.ps[b][:, :], self.ones_b, osq.rearrange("p a b -> p (a b)"), True, True, r=["osq", "ones"], w=[("ps", b)])
                self.rstd(orst, self.ps[b][:, :].rearrange("p (a b) -> p a b", a=4), 128, r=[("ps", b), "small_c"], w=["orst"])
                self.tt("dve", otmp, oT[:, :, nc_], orst, ALU.mult, r=[("oT", n), "orst"], w=["otmp"])
                self.stt(OG[:, :, t * 128:(t + 1) * 128], otmp, gout, rsT[:, :, nc_], ALU.mult, ALU.mult,
                         r=["otmp", "cst", ("rsT", n)], w=[("OG", t)])
            if self.pas == "P" and l == 0 and sq_["bidx"] == 0:
                self.dbg("oT", oT, [("oT", n) for n in range(NT)], [128, 4, T])
                self.dbg("OG", OG[:, :, 0:T], [("OG", t) for t in tiles], [128, 4, T])
            self.P.barrier()
            if "noM" in self.stages:
                continue
            A.reset(base_off)
            nk0 = 4 if sq_["ctx"] else 0
            NK = nk0 + NT
            w_qb = A.alloc([3, 768], BF16)
            w_kvb = A.alloc([2, 1024], BF16)
            self.dma("pool", w_qb, I["w_qb"][i].rearrange("(k p) n -> p k n", p=128), r=[], w=["w_qb"], dq="w_qb")
            self.dma("pool", w_kvb, I["w_kvb"][i].rearrange("(k p) n -> p k n", p=128), r=[], w=["w_kvb"], dq="w_kvb")
            KTm = A.alloc([8, NK * 128], BF16)
            VAm = A.alloc([NK, 8, 65], BF16)
            cqb = A.alloc([NT, 384], BF16)
            QG = 4
            QTm = A.alloc([8, QG * 128], BF16)
            omb = A.alloc([QG, 512], BF16)
            OM = A.alloc([4, QG * 128], BF16)
            cb = A.alloc([256], BF16)
            cT = A.alloc([2, 128], BF16)
            kc96 = A.alloc([8, 96], F32)
            sq96 = A.alloc([8, 96], F32)
            knb = A.alloc([8, 96], BF16)
            ckvf = A.alloc([256], F32)
            ckvn = A.alloc([256], F32)
            kpe = A.alloc([32], F32)
            st8 = A.alloc([8], F32)
            st1 = A.alloc([2], F32)
            cqT = A.alloc([3, 128], BF16)
            rtm = [A.alloc([8, 2, 8], F32) for _ in range(4)]
            mr_tmp = A.alloc([8, 128], F32)
            if S_pass:
                ropem = A.alloc([8, 2, 16], F32)
                cck = A.alloc([256], F32)
                ckp = A.alloc([4, 32], F32)
                self.dma("sp", ropem, I["ropem"].rearrange("(t p) c q -> p t c q", p=128), r=[], w=["ropem"], dq="ropem")
                self.dma("sp", ckp, I["c_kpe"][i].rearrange("(t p) n -> p t n", p=128), r=[], w=["ckp"], dq="ckp")
            self.memset("pool", VAm[:, :, :, 64:65], 1.0, w=["VA1"])

            def norm96(src_views, srck, dstn, gain, rope_t):
                self.act(sq96, kc96, AF.Square, r=["kc96"], w=["sq96"])
                self.red(st8, sq96, r=["sq96"], w=["st8"])
                self.rstd(st8, st8, 96, r=["st8", "small_c"], w=["st8"])
                self.tt("dve", kc96, kc96, st8.unsqueeze(2).broadcast_to([128, 8, 96]), ALU.mult, r=["kc96", "st8"], w=["kc96"])
                self.tt("pool", kc96, kc96, gain.unsqueeze(1).broadcast_to([128, 8, 96]), ALU.mult, r=["kc96", "cst"], w=["kc96"])
                if rope_t is not None:
                    self.rope(kc96[:, :, 64:96].rearrange("p h (a b q) -> p h a b q", a=2, b=2, q=8), 8, 8, ropem[:, rope_t, :, :], rtm,
                              r=["kc96", "ropem"], w=["kc96"], keyp="rtm")
                self.cp("act", knb, kc96, r=["kc96"], w=["knb"])

            import os
            LVL = int(os.environ.get("KSIDE_LVL", "99"))

            def kside(ckvn_ap, ckvn_k, kpe_ap, kpe_k, kt, rope_t):
                self.cp("act", cb, ckvn_ap, r=[ckvn_k], w=["cb"])
                if LVL < 1:
                    return
                b = self.bank("tr")
                for kc in range(2):
                    self.tr(self.psb[b][:, kc * 128:(kc + 1) * 128], cb[:, kc * 128:(kc + 1) * 128], self.ident_b, r=["cb", "identb"], w=[("ps", b)])
                self.cp("dve", cT, self.psb[b][:, 0:256].rearrange("p (a b) -> p a b", a=2), r=[("ps", b)], w=["cT"])
                if LVL < 2:
                    return
                for bk in range(2):
                    b = self.bank("mm")
                    for kc in range(2):
                        self.mm(self.ps[b][:, :], cT[:, kc, :], w_kvb[:, kc, bk * 512:(bk + 1) * 512], kc == 0, kc == 1,
                                r=["cT", "w_kvb"], w=[("ps", b)])
                    pv = self.ps[b][:, :].rearrange("p (h d) -> p h d", h=4)
                    self.cp("act", kc96[:, bk * 4:(bk + 1) * 4, 0:64], pv[:, :, 0:64], r=[("ps", b)], w=["kc96"])
                    self.cp("dve", VAm[:, kt, bk * 4:(bk + 1) * 4, 0:64], pv[:, :, 64:128], r=[("ps", b)], w=[("VAm", kt)])
                if LVL < 3:
                    return
                self.cp("pool", kc96[:, :, 64:96], kpe_ap.unsqueeze(1).broadcast_to([128, 8, 32]), r=[kpe_k], w=["kc96"])
                if LVL < 4:
                    return
                norm96(None, None, None, gk96, rope_t)
                if LVL < 5:
                    return
                b = self.bank("tr")
                for hh in range(8):
                    self.tr(self.psb[b][0:96, hh * 128:(hh + 1) * 128], knb[:, hh, :], self.ident_b, r=["knb", "identb"], w=[("ps", b)])
                self.cp("dve", KTm[0:96, :, kt * 128:(kt + 1) * 128], self.psb[b][0:96, :].rearrange("p (a b) -> p a b", a=8),
                        r=[("ps", b)], w=[("KTm", kt)])

            if sq_["ctx"]:
                for kt in range(4):
                    self.dma("sp", cck, I["c_ckv"][i, kt * 128:(kt + 1) * 128, :], r=[], w=["cck"], dq="cck")
                    kside(cck, "cck", ckp[:, kt, :], "ckp", kt, None)
            for n, t in enumerate(tiles):
                h = hTm[hi % 2]
                hk = ("hTm", hi % 2)
                hi += 1
                self.normmod(t, 0, h, hk)
                b1 = self.bank("mm")
                for kc in range(8):
                    self.mm(self.ps[b1][:, :], h[:, kc, :], w_in[:, kc, 1568:2080], kc == 0, kc == 7, r=[hk] + wk, w=[("ps", b1)])
                b2 = self.bank("mm")
                for kc in range(8):
                    self.mm(self.ps[b2][:, 0:160], h[:, kc, :], w_in[:, kc, 2080:2240], kc == 0, kc == 7, r=[hk] + wk, w=[("ps", b2)])
                self.act(sq96.rearrange("p a b -> p (a b)")[:, 0:384], self.ps[b1][:, 0:384], AF.Square, r=[("ps", b1)], w=["sq96", "st1"],
                         accum_out=st1[:, 0:1])
                self.rstd(st1[:, 0:1], st1[:, 0:1], 384, r=["st1"], w=["st1"])
                self.stt(cqb[:, n, :], self.ps[b1][:, 0:384], st1[:, 0:1], gqn, ALU.mult, ALU.mult, r=[("ps", b1), "st1", "cst"], w=[("cqb", n)])
                self.cp("act", ckvf[:, 0:128], self.ps[b1][:, 384:512], r=[("ps", b1)], w=["ckvf"])
                self.cp("act", ckvf[:, 128:256], self.ps[b2][:, 0:128], r=[("ps", b2)], w=["ckvf"])
                self.cp("dve", kpe, self.ps[b2][:, 128:160], r=[("ps", b2)], w=["kpe"])
                self.act(sq96.rearrange("p a b -> p (a b)")[:, 0:256], ckvf, AF.Square, r=["ckvf"], w=["sq96", "st1b"], accum_out=st1[:, 1:2])
                self.rstd(st1[:, 1:2], st1[:, 1:2], 256, r=["st1b"], w=["st1b"])
                self.stt(ckvn, ckvf, st1[:, 1:2], gkvn, ALU.mult, ALU.mult, r=["ckvf", "st1b", "cst"], w=["ckvn"])
                if sq_["bidx"] is not None and "noMout" not in self.stages:
                    bi = sq_["bidx"]
                    self.dma("sp", O["o_ckv"][bi, i, n * 128:(n + 1) * 128, :], ckvn, r=["ckvn"], w=[], dq="ckvn_o")
                    self.dma("sp", O["o_kpe"][bi, i, n * 128:(n + 1) * 128, :], kpe, r=["kpe"], w=[], dq="kpe_o")
                if "noMkside" not in self.stages:
                    kside(ckvn, "ckvn", kpe, "kpe", nk0 + n, t if sq_["rope"] else None)
            kt_list = list(range(NK))
            kkeys = [("KTm", kt) for kt in kt_list]
            vkeys = [("VAm", kt) for kt in kt_list] + ["VA1"]
            for g0 in (range(0, NT, QG) if "noMattn" not in self.stages else ()):
                gt = list(range(g0, min(NT, g0 + QG)))
                nq = len(gt)
                for jn, n in enumerate(gt):
                    t = tiles[n]
                    b = self.bank("tr")
                    for kc in range(3):
                        self.tr(self.psb[b][:, kc * 128:(kc + 1) * 128], cqb[:, n, kc * 128:(kc + 1) * 128], self.ident_b,
                                r=[("cqb", n), "identb"], w=[("ps", b)])
                    self.cp("act", cqT, self.psb[b][:, 0:384].rearrange("p (a b) -> p a b", a=3), r=[("ps", b)], w=["cqT"])
                    for bk in range(2):
                        b = self.bank("mm")
                        for kc in range(3):
                            self.mm(self.ps[b][:, 0:384], cqT[:, kc, :], w_qb[:, kc, bk * 384:(bk + 1) * 384], kc == 0, kc == 2,
                                    r=["cqT", "w_qb"], w=[("ps", b)])
                        self.cp("act", kc96[:, bk * 4:(bk + 1) * 4, :], self.ps[b][:, 0:384].rearrange("p (h d) -> p h d", h=4),
                                r=[("ps", b)], w=["kc96"])
                    norm96(None, None, None, gq96, t if sq_["rope"] else None)
                    b = self.bank("tr")
                    for hh in range(8):
                        self.tr(self.psb[b][0:96, hh * 128:(hh + 1) * 128], knb[:, hh, :], self.ident_b, r=["knb", "identb"], w=[("ps", b)])
                    self.cp("dve", QTm[0:96, :, jn * 128:(jn + 1) * 128], self.psb[b][0:96, :].rearrange("p (a b) -> p a b", a=8),
                            r=[("ps", b)], w=[("QTm", jn)])
                for hh in range(8):
                    self.attention(
                        KT=lambda st, hh=hh: KTm[0:96, hh, st * 128:(st + 1) * 128], kparts=96, kt_list=kt_list,
                        QTv=QTm[0:96, hh, 0:nq * 128], nq=nq,
                        VA_of=lambda st, hh=hh: VAm[:, st, hh, :], scale=float(96 ** -0.5), Pt=Pt,
                        out_of=lambda j, hh=hh: (omb[:, j, hh * 64:(hh + 1) * 64], ("omb", j)),
                        kr=kkeys, qr=[("QTm", j) for j in range(nq)], vr=vkeys, ow=None)
                if self.pas == "P" and l == 0 and sq_["bidx"] == 0:
                    self.dbg("omb", omb[:, 0:nq, :], [("omb", j) for j in range(nq)], [128, nq, 512])
                for jn, n in enumerate(gt):
                    t = tiles[n]
                    b = self.bank("tr")
                    for c in range(4):
                        self.tr(self.psb[b][:, c * 128:(c + 1) * 128], omb[:, jn, c * 128:(c + 1) * 128], self.ident_b,
                                r=[("omb", jn), "identb"], w=[("ps", b)])
                    self.cp("act", OM[:, :, jn * 128:(jn + 1) * 128], self.psb[b][:, 0:512].rearrange("p (a b) -> p a b", a=4),
                            r=[("ps", b)], w=[("OM", jn)])
                    banks = []
                    for hb in range(2):
                        b = self.bank("mm")
                        banks.append(b)
                        for cc in range(4):
                            c = hb * 4 + cc
                            for kc in range(8):
                                rhs = OG[:, kc, t * 128:(t + 1) * 128] if kc < 4 else OM[:, kc - 4, jn * 128:(jn + 1) * 128]
                                self.mm(self.ps[b][:, cc * 128:(cc + 1) * 128], w_out[:, kc, c * 128:(c + 1) * 128],
                                        rhs, kc == 0, kc == 7,
                                        r=["w_out", ("OG", t), ("OM", jn)], w=[("ps", b)])
                    self.mixer_residual(t, banks, mr_tmp)
            self.P.barrier()


def _rope_tables(n_tok, d_rot):
    t = np.arange(n_tok, dtype=np.int32)
    pos = np.stack([t // 64, t % 64], axis=-1).astype(np.float32)
    quarter = d_rot // 4
    inv = np.power(np.float32(10000.0), -np.arange(quarter, dtype=np.float32) / np.float32(quarter)).astype(np.float32)
    ang = pos[:, :, None] * inv
    cos = np.cos(ang).astype(np.float32).reshape(n_tok, 2 * quarter)
    sin = np.sin(ang).astype(np.float32).reshape(n_tok, 2 * quarter)
    return np.ascontiguousarray(np.stack([cos, sin], axis=1))


def _build_cst(inp, b):
    c = np.zeros((128, NCST), np.float32)

    def put(name, arr, parts=128):
        o, n = CST_OFF[name]
        c[0:parts, o:o + n] = np.asarray(arr, np.float32).reshape(parts, n)

    s = np.arange(128)[:, None]
    t = np.arange(128)[None, :]
    v = np.float32(-1.0 / 16.0)
    put("ident", np.eye(128))
    put("trim0", (s <= t) * v)
    put("trim1", (s >= t) * v)
    put("tris0", (s > t) * v)
    put("tris1", (s < t) * v)
    put("mask0", (s <= t) * 1.0)
    put("mask1", (s >= t) * 1.0)
    put("rm", np.array([[1, 0], [0, 1], [1, 1]], np.float32), parts=3)
    cond = np.stack([inp["c_ctx"].reshape(8, 128).T, inp["c"][b].reshape(8, 128).T], axis=-1)
    put("cond", cond)
    put("gmix", inp["norm_mix_g"].reshape(4, 8, 128).transpose(2, 0, 1))
    put("gffn", inp["norm_ffn_g"].reshape(4, 8, 128).transpose(2, 0, 1))
    for i in range(2):
        put("gq%d" % i, np.broadcast_to(inp["gqa_qn_g"][i][None, :], (128, 64)))
        put("gk%d" % i, np.broadcast_to(inp["gqa_kn_g"][i][None, :], (128, 64)))
        put("gout%d" % i, inp["gla_out_g"][i].reshape(128, 1))
        put("gqn%d" % i, np.broadcast_to(inp["mla_q_norm_g"][i][None, :], (128, 384)))
        put("gkvn%d" % i, np.broadcast_to(inp["mla_kv_norm_g"][i][None, :], (128, 256)))
        put("gq96%d" % i, np.broadcast_to(inp["mla_qn_g"][i][None, :], (128, 96)))
        put("gk96%d" % i, np.broadcast_to(inp["mla_kn_g"][i][None, :], (128, 96)))
    return c


_NC_CACHE = {}


def _get_nc(debug=(), stages=None):
    key = (tuple(sorted(debug)), None if stages is None else tuple(sorted(stages)))
    if key not in _NC_CACHE:
        kb = KB(debug, stages)
        nc = kb.build()
        _NC_CACHE[key] = (nc, kb)
    return _NC_CACHE[key]


def make_in_maps(inp):
    inp = {k: np.ascontiguousarray(np.asarray(v)) for k, v in inp.items()}
    ropeg = _rope_tables(1024, 64)
    ropem = _rope_tables(1024, 32)
    shared = dict(
        ada_w=inp["ada_w"], ada_b=inp["ada_b"], ffn_w_in=inp["ffn_w_in"], ffn_w_out=inp["ffn_w_out"],
        ab_w_in=inp["ab_w_in"], ab_w_out=inp["ab_w_out"], a_w2=inp["gla_a_w2"], a_b=inp["gla_a_b"],
        w_qb=inp["mla_w_qb"], w_kvb=inp["mla_w_kvb"], gqa_w_in=inp["gqa_w_in"], gqa_w_out=inp["gqa_w_out"],
        ropeg=ropeg, ropem=ropem)
    in_maps = []
    for core in range(8):
        b = core // 4
        m = dict(shared)
        m["cst"] = _build_cst(inp, b)
        m["xp"] = np.ascontiguousarray(inp["x_prompt"][core * 4:(core + 1) * 4].reshape(1024, D))
        m["xs"] = np.ascontiguousarray(inp["x_sample"][b])
        m["c_ckv"] = np.ascontiguousarray(inp["cache_mla_ckv"][b])
        m["c_kpe"] = np.ascontiguousarray(inp["cache_mla_kpe"][b])
        m["c_gla"] = np.ascontiguousarray(inp["state_gla"][b].reshape(2, 2, 256, 128))
        m["c_gk"] = np.ascontiguousarray(inp["cache_gqa_k"][b].reshape(2, 512, 256))
        m["c_gv"] = np.ascontiguousarray(inp["cache_gqa_v"][b].reshape(2, 512, 256))
        in_maps.append(m)
    return in_maps


def kernel(**inputs):
    nc, kb = _get_nc()
    in_maps = make_in_maps(inputs)
    res = run_bass_kernel_spmd(nc, in_maps, core_ids=list(range(8)))
    R = res.results
    y_prompt = np.concatenate([R[c]["yp"].reshape(4, 256, D) for c in range(8)], axis=0)
    y_sample = np.stack([R[0]["ys"], R[4]["ys"]], axis=0)
    new_ckv = np.concatenate([R[c]["o_ckv"] for c in range(8)], axis=0)
    new_kpe = np.concatenate([R[c]["o_kpe"] for c in range(8)], axis=0)
    new_gla = np.concatenate([R[c]["o_gla"].reshape(4, 2, 2, 4, 64, 128) for c in range(8)], axis=0)
    new_k = np.concatenate([R[c]["o_gk"].reshape(4, 2, 256, 4, 64) for c in range(8)], axis=0)
    new_v = np.concatenate([R[c]["o_gv"].reshape(4, 2, 256, 4, 64) for c in range(8)], axis=0)
    outs = (y_prompt, y_sample, new_ckv, new_kpe, new_gla, new_k, new_v)
    return tuple(np.ascontiguousarray(o, dtype=np.float32) for o in outs)
```

```python
import bisect
import contextlib
import numpy as np
import concourse.bass as bass
import concourse.mybir as mybir
from concourse.bass_utils import run_bass_kernel_spmd

F32 = mybir.dt.float32
BF16 = mybir.dt.bfloat16
AF = mybir.ActivationFunctionType
ALU = mybir.AluOpType
AX = mybir.AxisListType

ENGS = ("pe", "act", "dve", "pool", "sp")
RAW, WAR, WAW = 1, 2, 4
EPS = 1e-6
D = 1024
FH = 2816
ARENA_BYTES = 160 * 1024


class Prog:
    def __init__(self):
        self.ops = []
        self.last_w = {}
        self.readers = {}

    def op(self, eng, fn, r=(), w=(), dq=None):
        i = len(self.ops)
        deps = {}
        psr = [k for k in r if isinstance(k, tuple) and k and k[0] == "ps"]
        if psr:
            r = [k for k in r if not (isinstance(k, tuple) and k and k[0] == "ps")]
            w = list(w) + [k for k in psr if k not in w]
            for k in psr:
                lw = self.last_w.get(k)
                if lw is not None:
                    deps[lw] = deps.get(lw, 0) | RAW
        for k in r:
            lw = self.last_w.get(k)
            if lw is not None:
                deps[lw] = deps.get(lw, 0) | RAW
        for k in w:
            lw = self.last_w.get(k)
            if lw is not None:
                deps[lw] = deps.get(lw, 0) | WAW
            for rd in self.readers.get(k, ()):
                if rd != i:
                    deps[rd] = deps.get(rd, 0) | WAR
        for k in r:
            self.readers.setdefault(k, []).append(i)
        for k in w:
            self.last_w[k] = i
            self.readers[k] = []
        deps.pop(i, None)
        self.ops.append(dict(eng=eng, fn=fn, deps=deps, dq=dq, bar=None))
        return i

    def barrier(self):
        first = len(self.ops)
        for e in ENGS:
            self.ops.append(dict(eng=e, fn="drain", deps={}, dq=None, bar=("sig", first)))
        sig_ids = list(range(first, first + len(ENGS)))
        for e in ENGS:
            self.ops.append(dict(eng=e, fn="nop", deps={s: RAW for s in sig_ids}, dq=None, bar=("wait", first)))
        self.last_w = {}
        self.readers = {}

    def emit(self, nc, es):
        ops = self.ops
        n = len(ops)
        needed = [False] * n
        for i, o in enumerate(ops):
            kept = []
            for d, kind in o["deps"].items():
                od = ops[d]
                if od["dq"] is None and o["dq"] is None and od["eng"] == o["eng"] and o["bar"] is None:
                    if o["eng"] == "pe":
                        continue
                kept.append(d)
                needed[d] = True
            o["kdeps"] = kept
        eng_sem = {e: es.enter_context(nc.semaphore("sem_" + e)) for e in ENGS}
        dq_keys = []
        seen = set()
        for o in ops:
            if o["dq"] is not None and o["dq"] not in seen:
                seen.add(o["dq"])
                dq_keys.append(o["dq"])
        dq_sem = {k: es.enter_context(nc.semaphore("dq_%d" % j)) for j, k in enumerate(dq_keys)}
        dq_idx = {k: [] for k in dq_keys}
        eng_cnt = {e: 0 for e in ENGS}
        for i, o in enumerate(ops):
            if o["dq"] is not None:
                dq_idx[o["dq"]].append(i)
                o["sig"] = ("dq", o["dq"])
            elif needed[i] or (o["bar"] is not None and o["bar"][0] == "sig"):
                eng_cnt[o["eng"]] += 1
                o["sig"] = ("eng", o["eng"], eng_cnt[o["eng"]])
            else:
                o["sig"] = None
        per_eng = {e: [] for e in ENGS}
        for i, o in enumerate(ops):
            per_eng[o["eng"]].append(i)
        self.n_sems = len(ENGS) + len(dq_keys)
        self.counts = {e: len(per_eng[e]) for e in ENGS}

        def run(e, h):
            waited = {}
            for i in per_eng[e]:
                o = ops[i]
                waits = {}
                for d in o["kdeps"]:
                    od = ops[d]
                    if od["dq"] is not None:
                        k = od["dq"]
                        cnt = 16 * bisect.bisect_left(dq_idx[k], i)
                        key = ("dq", k)
                        waits[key] = max(waits.get(key, 0), cnt)
                    else:
                        key = ("eng", od["eng"])
                        waits[key] = max(waits.get(key, 0), od["sig"][2])
                if o["bar"] is not None and o["bar"][0] == "sig" and e == "sp":
                    for k in dq_keys:
                        cnt = 16 * bisect.bisect_left(dq_idx[k], i)
                        if cnt:
                            waits[("dq", k)] = cnt
                for key, v in waits.items():
                    if waited.get(key, 0) >= v:
                        continue
                    waited[key] = v
                    sem = dq_sem[key[1]] if key[0] == "dq" else eng_sem[key[1]]
                    h.wait_ge(sem, v)
                if o["fn"] == "drain":
                    inst = h.nop() if e == "sp" else h.drain()
                elif o["fn"] == "nop":
                    inst = None
                else:
                    inst = o["fn"](h)
                s = o["sig"]
                if s is not None:
                    if s[0] == "dq":
                        inst.then_inc(dq_sem[s[1]], 16)
                    else:
                        inst.then_inc(eng_sem[s[1]], 1)
            if e == "sp":
                for k in dq_keys:
                    h.wait_ge(dq_sem[k], 16 * len(dq_idx[k]))

        with nc.Block() as block:
            @block.tensor
            def _(h):
                run("pe", h)

            @block.scalar
            def _(h):
                run("act", h)

            @block.vector
            def _(h):
                run("dve", h)

            @block.gpsimd
            def _(h):
                run("pool", h)

            @block.sync
            def _(h):
                run("sp", h)


class Arena:
    def __init__(self, t, nbytes):
        self.t = t
        self.nbytes = nbytes
        self.off = 0
        self.peak = 0

    def reset(self, off=0):
        self.off = off

    def alloc(self, free_shape, dtype, parts=128):
        n = int(np.prod(free_shape))
        esz = 4 if dtype == F32 else 2
        sz = (n * esz + 31) // 32 * 32
        o = self.off
        assert o + sz <= self.nbytes, ("arena overflow", o, sz, self.nbytes)
        self.off = o + sz
        self.peak = max(self.peak, self.off)
        ap = self.t[0:parts, o // 2:(o + n * esz) // 2]
        if dtype == F32:
            ap = ap.bitcast(F32)
        fs = list(free_shape)
        if len(fs) == 2:
            ap = ap.rearrange("p (a b) -> p a b", a=fs[0], b=fs[1])
        elif len(fs) == 3:
            ap = ap.rearrange("p (a b c) -> p a b c", a=fs[0], b=fs[1], c=fs[2])
        elif len(fs) == 4:
            ap = ap.rearrange("p (a b c d) -> p a b c d", a=fs[0], b=fs[1], c=fs[2], d=fs[3])
        return ap


def _cst_layout():
    off = {}
    o = 0

    def add(name, n):
        nonlocal o
        off[name] = (o, n)
        o += n

    add("ident", 128)
    add("trim0", 128)
    add("trim1", 128)
    add("tris0", 128)
    add("tris1", 128)
    add("mask0", 128)
    add("mask1", 128)
    add("rm", 2)
    add("cond", 16)
    add("gmix", 32)
    add("gffn", 32)
    for i in range(2):
        add("gq%d" % i, 64)
        add("gk%d" % i, 64)
        add("gout%d" % i, 1)
        add("gqn%d" % i, 384)
        add("gkvn%d" % i, 256)
        add("gq96%d" % i, 96)
        add("gk96%d" % i, 96)
    return off, o


CST_OFF, NCST = _cst_layout()


class KB:
    def __init__(self, debug=(), stages=None):
        self.stages = set(stages) if stages is not None else {"adaln", "ffn", "mixc", "mixab", "P", "S"}
        self.debug = set(debug)
        self.dbg_outs = {}

    def mm(self, out, lhsT, rhs, start, stop, r, w, **kw):
        self.P.op("pe", lambda h: h.matmul(out, lhsT, rhs, start=start, stop=stop, **kw), r=r, w=w)

    def tr(self, out, in_, ident, r, w):
        self.P.op("pe", lambda h: h.transpose(out, in_, ident), r=r, w=w)

    def act(self, out, in_, func, r, w, **kw):
        self.P.op("act", lambda h: h.activation(out, in_, func, **kw), r=r, w=w)

    def tt(self, eng, out, a, b, op, r, w):
        self.P.op(eng, lambda h: h.tensor_tensor(out, a, b, op), r=r, w=w)

    def stt(self, out, in0, scalar, in1, op0, op1, r, w):
        self.P.op("dve", lambda h: h.scalar_tensor_tensor(out, in0, scalar, in1, op0, op1), r=r, w=w)

    def cp(self, eng, out, in_, r, w):
        if eng == "act":
            self.P.op("act", lambda h: h.copy(out, in_), r=r, w=w)
        else:
            self.P.op(eng, lambda h: h.tensor_copy(out, in_), r=r, w=w)

    def recip(self, out, in_, r, w):
        self.P.op("dve", lambda h: h.reciprocal(out, in_), r=r, w=w)

    def red(self, out, in_, r, w):
        self.P.op("dve", lambda h: h.tensor_reduce(out, in_, AX.X, ALU.add), r=r, w=w)

    def memset(self, eng, ap, val, w):
        self.P.op(eng, lambda h: h.memset(ap, val), w=w)

    def dma(self, q, out, in_, r, w, dq):
        self.P.op(q, lambda h: h.dma_start(out=out, in_=in_), r=r, w=w, dq=dq)

    def bank(self, pool):
        lst, idx = self.pools[pool]
        b = lst[idx % len(lst)]
        self.pools[pool][1] = idx + 1
        return b

    def rstd(self, out, ss, n, r, w):
        self.act(out, ss, AF.Sqrt, r=r, w=w, bias=EPS, scale=1.0 / n)
        self.recip(out, out, r=w, w=w)

    def cst(self, name, parts=128):
        o, n = CST_OFF[name]
        return self.cst_t[0:parts, o:o + n]

    def dbg(self, name, ap, r, shape):
        if name not in self.debug:
            return
        t = self.nc.dram_tensor("dbg_" + name, list(shape), ap.dtype if hasattr(ap, "dtype") else F32, kind="ExternalOutput").ap()
        self.dbg_outs[name] = shape
        self.dma("sp", t, ap, r=r, w=[], dq=("dbg", name))

    def build(self):
        nc = bass.Bass("TRN2", target_bir_lowering=False)
        self.nc = nc
        self.P = Prog()

        def din(name, shape):
            return nc.dram_tensor(name, list(shape), F32, kind="ExternalInput").ap()

        def dout(name, shape):
            return nc.dram_tensor(name, list(shape), F32, kind="ExternalOutput").ap()

        I = {}
        I["cst"] = din("cst", [128, NCST])
        I["xp"] = din("xp", [1024, D])
        I["xs"] = din("xs", [1024, D])
        I["ada_w"] = din("ada_w", [4, D, 6 * D])
        I["ada_b"] = din("ada_b", [4, 6 * D])
        I["ffn_w_in"] = din("ffn_w_in", [4, D, 2 * FH])
        I["ffn_w_out"] = din("ffn_w_out", [4, FH, D])
        I["ab_w_in"] = din("ab_w_in", [2, D, 2240])
        I["ab_w_out"] = din("ab_w_out", [2, D, D])
        I["a_w2"] = din("a_w2", [2, 2, 16, 256])
        I["a_b"] = din("a_b", [2, 2, 256])
        I["w_qb"] = din("w_qb", [2, 384, 768])
        I["w_kvb"] = din("w_kvb", [2, 256, 1024])
        I["gqa_w_in"] = din("gqa_w_in", [2, D, 1536])
        I["gqa_w_out"] = din("gqa_w_out", [2, D, D])
        I["c_ckv"] = din("c_ckv", [2, 512, 256])
        I["c_kpe"] = din("c_kpe", [2, 512, 32])
        I["c_gla"] = din("c_gla", [2, 2, 256, 128])
        I["c_gk"] = din("c_gk", [2, 512, 256])
        I["c_gv"] = din("c_gv", [2, 512, 256])
        I["ropeg"] = din("ropeg", [1024, 2, 32])
        I["ropem"] = din("ropem", [1024, 2, 16])
        O = {}
        O["yp"] = dout("yp", [1024, D])
        O["ys"] = dout("ys", [1024, D])
        O["o_ckv"] = dout("o_ckv", [4, 2, 256, 256])
        O["o_kpe"] = dout("o_kpe", [4, 2, 256, 32])
        O["o_gla"] = dout("o_gla", [4, 2, 2, 256, 128])
        O["o_gk"] = dout("o_gk", [4, 2, 256, 256])
        O["o_gv"] = dout("o_gv", [4, 2, 256, 256])
        self.I, self.O = I, O

        with contextlib.ExitStack() as es:
            self.xT = es.enter_context(nc.sbuf_tensor("xT", [128, 8, 1024], F32))
            self.cst_t = es.enter_context(nc.sbuf_tensor("cst_sb", [128, NCST], F32))
            small = es.enter_context(nc.sbuf_tensor("small", [128, 4 * 48 * 2 + 48 + 8], F32))
            cbf = es.enter_context(nc.sbuf_tensor("cbf", [128, 256 + 16], BF16))
            arena_t = es.enter_context(nc.sbuf_tensor("arena", [128, ARENA_BYTES // 2], BF16))
            self.A = Arena(arena_t, ARENA_BYTES)
            self.ps = [es.enter_context(nc.psum_tensor("ps%d" % i, [128, 512], F32)) for i in range(8)]
            self.psb = [p.bitcast(BF16) for p in self.ps]
            self.pools = {"mm": [[0, 1, 2, 3], 0], "acc": [[4, 5], 0], "tr": [[6, 7], 0]}
            self.modt = small[:, 0:384].rearrange("p (l c k) -> p l c k", l=4, c=48, k=2)
            self.lsc = small[:, 384:432].rearrange("p (a b) -> p a b", a=6, b=8)
            self.eps_c = small[:, 432:433]
            self.one_c = small[:, 433:434]
            self.ident_b = cbf[:, 0:128]
            self.ones_b = cbf[:, 128:256]
            self.sc_b = cbf[:, 256:272].rearrange("p (k c) -> p k c", k=8, c=2)
            self.ident_f = self.cst("ident")

            self.prologue()
            if "adaln" in self.stages:
                self.adaln_all()
            for pas in ("P", "S"):
                if pas in self.stages:
                    self.run_pass(pas)
            self.P.emit(nc, es)
        return nc

    def prologue(self):
        self.dma("sp", self.cst_t[:, :], self.I["cst"], r=[], w=["cst"], dq="cst")
        self.memset("dve", self.eps_c, EPS, w=["small_c"])
        self.memset("dve", self.one_c, 1.0, w=["small_c"])
        self.memset("dve", self.ones_b, 1.0, w=["ones"])
        self.cp("dve", self.ident_b, self.ident_f, r=["cst"], w=["identb"])
        cond = self.cst("cond").rearrange("p (k c) -> p k c", k=8, c=2)
        self.act(self.sc_b, cond, AF.Silu, r=["cst"], w=["scb"])

    def adaln_all(self):
        A = self.A
        A.reset()
        slots = [A.alloc([8, 512], BF16) for _ in range(3)]
        mtok = A.alloc([6144], F32)
        rm = self.cst("rm", parts=3)
        for l in range(4):
            self.dma("sp", mtok[2:3, :], self.I["ada_b"][l:l + 1, :], r=[], w=[("mtok", "b")], dq="mtokb")
            for j in range(12):
                s = (l * 12 + j) % 3
                src = self.I["ada_w"][l, :, j * 512:(j + 1) * 512].rearrange("(k p) n -> p k n", p=128)
                self.dma("pool", slots[s], src, r=[], w=[("adw", s)], dq=("adw", s))
                b = self.bank("mm")
                for kc in range(8):
                    self.mm(self.ps[b][0:2, :], self.sc_b[:, kc, :], slots[s][:, kc, :], kc == 0, kc == 7,
                            r=[("adw", s), "scb"], w=[("ps", b)])
                self.cp("act", mtok[0:2, j * 512:(j + 1) * 512], self.ps[b][0:2, :], r=[("ps", b)], w=[("mtok", j)])
            b = self.bank("mm")
            for c in range(48):
                self.mm(self.ps[b][:, 2 * c:2 * c + 2], mtok[0:3, c * 128:(c + 1) * 128], rm, True, True,
                        r=[("mtok", c // 4), ("mtok", "b"), "cst"], w=[("ps", b)])
            self.cp("dve", self.modt[:, l, :, :], self.ps[b][:, 0:96].rearrange("p (c k) -> p c k", c=48, k=2),
                    r=[("ps", b)], w=["modt"])
        self.P.barrier()

    def layer_scalars(self, l, col):
        mv = self.modt[:, l, :, col]
        gmix = self.cst("gmix").rearrange("p (l c) -> p l c", l=4, c=8)[:, l, :]
        gffn = self.cst("gffn").rearrange("p (l c) -> p l c", l=4, c=8)[:, l, :]
        L = self.lsc
        self.stt(L[:, 0, :], mv[:, 8:16], 1.0, gmix, ALU.add, ALU.mult, r=["modt", "cst"], w=["lsc"])
        self.cp("dve", L[:, 1, :], mv[:, 0:8], r=["modt"], w=["lsc"])
        self.cp("dve", L[:, 2, :], mv[:, 16:24], r=["modt"], w=["lsc"])
        self.stt(L[:, 3, :], mv[:, 32:40], 1.0, gffn, ALU.add, ALU.mult, r=["modt", "cst"], w=["lsc"])
        self.cp("dve", L[:, 4, :], mv[:, 24:32], r=["modt"], w=["lsc"])
        self.cp("dve", L[:, 5, :], mv[:, 40:48], r=["modt"], w=["lsc"])

    def alloc_norm_tmp(self):
        A = self.A
        self.nm_sq = A.alloc([8, 128], BF16)
        self.nm_rstd = A.alloc([128], F32)
        self.nm_tmp = A.alloc([8, 128], F32)

    def normmod(self, t, which, dst, dstkey):
        xv = self.xT[:, :, t * 128:(t + 1) * 128]
        xk = [("xT", t, c) for c in range(8)]
        G = self.lsc[:, 3 * which, :]
        SH = self.lsc[:, 3 * which + 1, :]
        self.act(self.nm_sq, xv, AF.Square, r=xk, w=["nm_sq"])
        b = self.bank("tr")
        for c in range(8):
            self.mm(self.ps[b][:, 0:128], self.ones_b, self.nm_sq[:, c, :], c == 0, c == 7, r=["nm_sq", "ones"], w=[("ps", b)])
        self.rstd(self.nm_rstd, self.ps[b][:, 0:128], D, r=[("ps", b), "small_c"], w=["nm_rstd"])
        self.tt("dve", self.nm_tmp, xv, self.nm_rstd.unsqueeze(1).broadcast_to([128, 8, 128]), ALU.mult,
                r=xk + ["nm_rstd"], w=["nm_tmp"])
        self.tt("pool", self.nm_tmp, self.nm_tmp, G.unsqueeze(2).broadcast_to([128, 8, 128]), ALU.mult,
                r=["nm_tmp", "lsc"], w=["nm_tmp"])
        self.tt("dve", dst, self.nm_tmp, SH.unsqueeze(2).broadcast_to([128, 8, 128]), ALU.add,
                r=["nm_tmp", "lsc"], w=[dstkey])

    def run_pass(self, pas):
        self.pas = pas
        col = 0 if pas == "P" else 1
        xin_d = self.I["xp"] if pas == "P" else self.I["xs"]
        yout_d = self.O["yp"] if pas == "P" else self.O["ys"]
        if pas == "P":
            self.seqs = [dict(tiles=[2 * s, 2 * s + 1], ctx=False, rope=False, bidx=s) for s in range(4)]
        else:
            self.seqs = [dict(tiles=list(range(8)), ctx=True, rope=True, bidx=None)]
        A = self.A
        A.reset()
        xin = [A.alloc([1024], F32) for _ in range(2)]
        for t in range(8):
            s = t % 2
            self.dma("sp", xin[s], xin_d[t * 128:(t + 1) * 128, :], r=[], w=[("xin", s)], dq=("xin", s))
            for hb in range(2):
                b = self.bank("tr")
                for cc in range(4):
                    c = hb * 4 + cc
                    self.tr(self.ps[b][:, cc * 128:(cc + 1) * 128], xin[s][:, c * 128:(c + 1) * 128], self.ident_f,
                            r=[("xin", s), "cst"], w=[("ps", b)])
                self.cp("dve" if hb else "act", self.xT[:, hb * 4:hb * 4 + 4, t * 128:(t + 1) * 128],
                        self.ps[b][:, :].rearrange("p (a b) -> p a b", a=4),
                        r=[("ps", b)], w=[("xT", t, hb * 4 + cc) for cc in range(4)])
        self.P.barrier()
        for l in range(4):
            self.layer_scalars(l, col)
            if l % 2 == 0:
                if "mixab" in self.stages:
                    self.mixer_ab(l, l // 2)
            else:
                if "mixc" in self.stages:
                    self.mixer_c(l, l // 2)
            if self.pas == "P" and l in (0, 1):
                self.dbg("x_l%d_mix" % l, self.xT[:, :, :], [("xT", t, c) for t in range(8) for c in range(8)], [128, 8, 1024])
            self.P.barrier()
            if "ffn" in self.stages:
                self.ffn(l)
            if self.pas == "P" and l in (0, 1):
                self.dbg("x_l%d_ffn" % l, self.xT[:, :, :], [("xT", t, c) for t in range(8) for c in range(8)], [128, 8, 1024])
            self.P.barrier()
        A.reset()
        yo = [A.alloc([1024], F32) for _ in range(2)]
        for t in range(8):
            s = t % 2
            for hb in range(2):
                b = self.bank("tr")
                for cc in range(4):
                    c = hb * 4 + cc
                    self.tr(self.ps[b][:, cc * 128:(cc + 1) * 128], self.xT[:, c, t * 128:(t + 1) * 128], self.ident_f,
                            r=[("xT", t, c), "cst"], w=[("ps", b)])
                self.cp("dve" if hb else "act", yo[s][:, hb * 512:(hb + 1) * 512], self.ps[b][:, :],
                        r=[("ps", b)], w=[("yo", s, hb)])
            self.dma("sp", yout_d[t * 128:(t + 1) * 128, :], yo[s], r=[("yo", s, 0), ("yo", s, 1)], w=[], dq=("yo", s))
        self.P.barrier()

    def ffn(self, l):
        A = self.A
        A.reset()
        hT = A.alloc([8, 1024], BF16)
        actT = A.alloc([22, 1024], BF16)
        wi = [A.alloc([8, 2, 256], BF16) for _ in range(3)]
        wo = [A.alloc([11, 1024], BF16) for _ in range(2)]
        sg = [A.alloc([512], F32) for _ in range(2)]
        self.alloc_norm_tmp()
        W1 = self.I["ffn_w_in"]
        W2 = self.I["ffn_w_out"]

        def load_wi(j2):
            s = j2 % 3
            for gu in range(2):
                c0 = gu * FH + j2 * 256
                src = W1[l, :, c0:c0 + 256].rearrange("(k p) n -> p k n", p=128)
                self.dma("pool", wi[s][:, :, gu, :], src, r=[], w=[("wi", s, gu)], dq=("wi", s))

        def load_wo(hf):
            src = W2[l, hf * 1408:(hf + 1) * 1408, :].rearrange("(j p) n -> p j n", p=128)
            self.dma("pool", wo[hf], src, r=[], w=[("wo", hf)], dq=("wo", hf))

        for j2 in range(3):
            load_wi(j2)
        for t in range(8):
            self.normmod(t, 1, hT[:, :, t * 128:(t + 1) * 128], ("hT", t))
        load_wo(0)
        load_wo(1)
        k = 0
        for j2 in range(11):
            s = j2 % 3
            for tb in range(2):
                hk = [("hT", tb * 4 + q) for q in range(4)]
                for hf in range(2):
                    j = j2 * 2 + hf
                    bg = self.bank("mm")
                    for kc in range(8):
                        self.mm(self.ps[bg][:, :], wi[s][:, kc, 0, hf * 128:(hf + 1) * 128], hT[:, kc, tb * 512:(tb + 1) * 512],
                                kc == 0, kc == 7, r=[("wi", s, 0)] + hk, w=[("ps", bg)])
                    bu = self.bank("mm")
                    for kc in range(8):
                        self.mm(self.ps[bu][:, :], wi[s][:, kc, 1, hf * 128:(hf + 1) * 128], hT[:, kc, tb * 512:(tb + 1) * 512],
                                kc == 0, kc == 7, r=[("wi", s, 1)] + hk, w=[("ps", bu)])
                    sgi = k % 2
                    k += 1
                    self.act(sg[sgi], self.ps[bg][:, :], AF.Silu, r=[("ps", bg)], w=[("sg", sgi)])
                    self.tt("dve", actT[:, j, tb * 512:(tb + 1) * 512], sg[sgi], self.ps[bu][:, :], ALU.mult,
                            r=[("sg", sgi), ("ps", bu)], w=[("act", j, tb)])
            if j2 + 3 < 11:
                load_wi(j2 + 3)
        gate = self.lsc[:, 5, :]
        for hf in range(2):
            for c in range(8):
                for tb in range(2):
                    b = self.bank("mm")
                    for jj in range(11):
                        self.mm(self.ps[b][:, :], wo[hf][:, jj, c * 128:(c + 1) * 128], actT[:, hf * 11 + jj, tb * 512:(tb + 1) * 512],
                                jj == 0, jj == 10, r=[("wo", hf), ("act", hf * 11 + jj, tb)], w=[("ps", b)])
                    xv = self.xT[:, c, tb * 512:(tb + 1) * 512]
                    xk = [("xT", tb * 4 + q, c) for q in range(4)]
                    self.stt(xv, self.ps[b][:, :], gate[:, c:c + 1], xv, ALU.mult, ALU.add, r=[("ps", b), "lsc"] + xk, w=xk)

    def mixer_residual(self, t, banks, tmp):
        gate = self.lsc[:, 2, :]
        for hb in range(2):
            b = banks[hb]
            pv = self.ps[b][:, :].rearrange("p (a b) -> p a b", a=4)
            tv = tmp[:, hb * 4:hb * 4 + 4, :]
            self.tt("dve", tv, pv, gate[:, hb * 4:hb * 4 + 4].unsqueeze(2).broadcast_to([128, 4, 128]), ALU.mult,
                    r=[("ps", b), "lsc"], w=[("mr_tmp", hb)])
            xv = self.xT[:, hb * 4:hb * 4 + 4, t * 128:(t + 1) * 128]
            xk = [("xT", t, hb * 4 + q) for q in range(4)]
            self.tt("pool", xv, xv, tv, ALU.add, r=[("mr_tmp", hb)] + xk, w=xk)

    def rope(self, xv, H, Q, cs, tmps, r, w, keyp):
        x1 = xv[:, :, :, 0, :]
        x2 = xv[:, :, :, 1, :]
        c = cs[:, 0, :].rearrange("p (a q) -> p a q", a=2, q=Q).unsqueeze(1).broadcast_to([128, H, 2, Q])
        s = cs[:, 1, :].rearrange("p (a q) -> p a q", a=2, q=Q).unsqueeze(1).broadcast_to([128, H, 2, Q])
        t1, t2, t3, t4 = tmps
        k1, k2, k3, k4 = [(keyp, i) for i in range(4)]
        self.tt("dve", t1, x1, c, ALU.mult, r=r, w=[k1])
        self.tt("pool", t2, x2, s, ALU.mult, r=r, w=[k2])
        self.tt("dve", t3, x1, s, ALU.mult, r=r, w=[k3])
        self.tt("pool", t4, x2, c, ALU.mult, r=r, w=[k4])
        self.tt("dve", x1, t1, t2, ALU.subtract, r=[k1, k2, k3], w=w)
        self.tt("dve", x2, t3, t4, ALU.add, r=[k3, k4], w=w)

    def attention(self, KT, kparts, kt_list, QTv, nq, VA_of, scale, Pt, out_of, kr, qr, vr, ow):
        ob = self.bank("acc")
        first, last = kt_list[0], kt_list[-1]
        npt = len(Pt)

        def pv(st, pi):
            for j in range(nq):
                self.mm(self.ps[ob][:, j * 65:(j + 1) * 65], Pt[pi][:, j, :], VA_of(st), (st == first and j == 0), st == last,
                        r=[("Pt", pi)] + vr, w=[("ps", ob)], skip_group_check=True)

        pend = None
        for st in kt_list:
            sb = self.bank("mm")
            self.mm(self.ps[sb][:, 0:nq * 128], KT(st), QTv, True, True, r=kr + qr, w=[("ps", sb)])
            pi = self.pt_i % npt
            self.pt_i += 1
            self.act(Pt[pi][:, 0:nq, :], self.ps[sb][:, 0:nq * 128].rearrange("p (a b) -> p a b", a=nq), AF.Exp,
                     r=[("ps", sb)], w=[("Pt", pi)], scale=scale)
            if pend is not None:
                pv(*pend)
            pend = (st, pi)
        pv(*pend)
        ov = self.ps[ob][:, 0:nq * 65].rearrange("p (a b) -> p a b", a=nq)
        rd = self.at_rden
        self.recip(rd[:, 0:nq], ov[:, :, 64], r=[("ps", ob)], w=["at_rden"])
        for j in range(nq):
            dst, dk = out_of(j)
            self.P.op("dve", lambda h, j=j, dst=dst, rd=rd, ov=ov: h.tensor_scalar(dst, ov[:, j, 0:64], rd[:, j:j + 1], None, ALU.mult),
                      r=[("ps", ob), "at_rden"], w=[dk])

    def mixer_c(self, l, i):
        A = self.A
        A.reset()
        I, O = self.I, self.O
        w_in = A.alloc([8, 1536], BF16)
        w_out = A.alloc([8, 1024], BF16)
        hT = A.alloc([8, 1024], BF16)
        KT = A.alloc([4, 1536], BF16)
        VA = A.alloc([12, 4, 65], BF16)
        self.alloc_norm_tmp()
        kvf = A.alloc([512], F32)
        sq = A.alloc([1024], F32)
        qn = A.alloc([1024], F32)
        st16 = A.alloc([16], F32)
        knb = A.alloc([256], BF16)
        qnb = A.alloc([1024], BF16)
        QT = A.alloc([16, 128], BF16)
        Pt = [A.alloc([4, 128], BF16) for _ in range(3)]
        ob = A.alloc([1024], BF16)
        OT = A.alloc([8, 128], BF16)
        mr_tmp = A.alloc([8, 128], F32)
        self.at_rden = A.alloc([4], F32)
        rtm = [A.alloc([16, 2, 16], F32) for _ in range(4)]
        ropeg = A.alloc([8, 2, 32], F32)
        ck = A.alloc([4, 256], F32)
        cv = A.alloc([4, 256], F32)
        ckb = A.alloc([4, 256], BF16)
        self.pt_i = 0
        gq = self.cst("gq%d" % i)
        gk = self.cst("gk%d" % i)
        for kh in range(2):
            self.dma("pool", w_in[:, kh * 4:(kh + 1) * 4, :],
                     I["gqa_w_in"][i, kh * 512:(kh + 1) * 512, :].rearrange("(k p) n -> p k n", p=128), r=[], w=[("w_in", kh)], dq="w_in")
        self.dma("pool", w_out, I["gqa_w_out"][i].rearrange("(k p) n -> p k n", p=128), r=[], w=["w_out"], dq="w_out")
        self.memset("pool", VA[:, :, :, 64:65], 1.0, w=["VA1"])
        wk = [("w_in", 0), ("w_in", 1)]
        if self.pas == "S":
            self.dma("sp", ropeg, I["ropeg"].rearrange("(t p) c q -> p t c q", p=128), r=[], w=["ropeg"], dq="ropeg")

        def put_keys(knb_ap, kt, rk):
            b = self.bank("tr")
            for g in range(4):
                self.tr(self.psb[b][0:64, g * 128:(g + 1) * 128], knb_ap[:, g * 64:(g + 1) * 64], self.ident_b,
                        r=rk + ["identb"], w=[("ps", b)])
            self.cp("dve", KT[0:64, :, kt * 128:(kt + 1) * 128], self.psb[b][0:64, 0:512].rearrange("p (a b) -> p a b", a=4),
                    r=[("ps", b)], w=[("KT", kt)])

        for sq_ in self.seqs:
            tiles = sq_["tiles"]
            nk0 = 4 if sq_["ctx"] else 0
            if sq_["ctx"]:
                self.dma("sp", ck, I["c_gk"][i].rearrange("(t p) n -> p t n", p=128), r=[], w=["ck"], dq="ck")
                self.dma("sp", cv, I["c_gv"][i].rearrange("(t p) n -> p t n", p=128), r=[], w=["cv"], dq="cv")
                self.cp("act", ckb, ck, r=["ck"], w=["ckb"])
                for kt in range(4):
                    put_keys(ckb[:, kt, :], kt, ["ckb"])
                    self.cp("pool", VA[:, kt, :, 0:64], cv[:, kt, :].rearrange("p (g d) -> p g d", g=4), r=["cv"], w=[("VA", kt)])
            for n, t in enumerate(tiles):
                self.normmod(t, 0, hT[:, :, t * 128:(t + 1) * 128], ("hT", t))
                b = self.bank("mm")
                for kc in range(8):
                    self.mm(self.ps[b][:, :], hT[:, kc, t * 128:(t + 1) * 128], w_in[:, kc, 1024:1536], kc == 0, kc == 7,
                            r=[("hT", t)] + wk, w=[("ps", b)])
                self.cp("act", kvf, self.ps[b][:, :], r=[("ps", b)], w=["kvf"])
                self.act(sq[:, 0:256], self.ps[b][:, 0:256], AF.Square, r=[("ps", b)], w=[("sqq", 0)])
                self.red(st16[:, 0:4], sq[:, 0:256].rearrange("p (g d) -> p g d", g=4), r=[("sqq", 0)], w=["st16"])
                self.rstd(st16[:, 0:4], st16[:, 0:4], 64, r=["st16", "small_c"], w=["st16"])
                kv3 = kvf[:, 0:256].rearrange("p (g d) -> p g d", g=4)
                self.tt("dve", kv3, kv3, st16[:, 0:4].unsqueeze(2).broadcast_to([128, 4, 64]), ALU.mult, r=["kvf", "st16"], w=["kvf"])
                self.tt("dve", kv3, kv3, gk.unsqueeze(1).broadcast_to([128, 4, 64]), ALU.mult, r=["kvf", "cst"], w=["kvf"])
                if sq_["bidx"] is not None:
                    bi = sq_["bidx"]
                    self.dma("sp", O["o_gk"][bi, i, n * 128:(n + 1) * 128, :], kvf[:, 0:256], r=["kvf"], w=[], dq="kvf_o")
                    self.dma("sp", O["o_gv"][bi, i, n * 128:(n + 1) * 128, :], kvf[:, 256:512], r=["kvf"], w=[], dq="kvf_o")
                if sq_["rope"]:
                    self.rope(kvf[:, 0:256].rearrange("p (h a b q) -> p h a b q", h=4, a=2, b=2, q=16), 4, 16, ropeg[:, t, :, :],
                              [x[:, 0:4, :, :] for x in rtm], r=["kvf", "ropeg"], w=["kvf"], keyp="rtm")
                self.cp("act", knb, kvf[:, 0:256], r=["kvf"], w=["knb"])
                put_keys(knb, nk0 + n, ["knb"])
                self.cp("pool", VA[:, nk0 + n, :, 0:64], kvf[:, 256:512].rearrange("p (g d) -> p g d", g=4), r=["kvf"], w=[("VA", nk0 + n)])
            nkt = nk0 + len(tiles)
            kt_list = list(range(nkt))
            kkeys = [("KT", kt) for kt in kt_list]
            vkeys = [("VA", kt) for kt in kt_list] + ["VA1"]
            for n, t in enumerate(tiles):
                qb = []
                for bk in range(2):
                    b = self.bank("mm")
                    qb.append(b)
                    for kc in range(8):
                        self.mm(self.ps[b][:, :], hT[:, kc, t * 128:(t + 1) * 128], w_in[:, kc, bk * 512:(bk + 1) * 512], kc == 0, kc == 7,
                                r=[("hT", t)] + wk, w=[("ps", b)])
                    self.act(sq[:, bk * 512:(bk + 1) * 512], self.ps[b][:, :], AF.Square, r=[("ps", b)], w=[("sqq", bk)])
                self.red(st16, sq.rearrange("p (g d) -> p g d", g=16), r=[("sqq", 0), ("sqq", 1)], w=["st16"])
                self.rstd(st16, st16, 64, r=["st16", "small_c"], w=["st16"])
                for bk in range(2):
                    self.tt("dve", qn[:, bk * 512:(bk + 1) * 512].rearrange("p (g d) -> p g d", g=8),
                            self.ps[qb[bk]][:, :].rearrange("p (g d) -> p g d", g=8),
                            st16[:, bk * 8:(bk + 1) * 8].unsqueeze(2).broadcast_to([128, 8, 64]), ALU.mult,
                            r=[("ps", qb[bk]), "st16"], w=["qn"])
                qn3 = qn.rearrange("p (g d) -> p g d", g=16)
                self.tt("pool", qn3, qn3, gq.unsqueeze(1).broadcast_to([128, 16, 64]), ALU.mult,
                        r=["qn", "cst"], w=["qn"])
                if sq_["rope"]:
                    self.rope(qn.rearrange("p (h a b q) -> p h a b q", h=16, a=2, b=2, q=16), 16, 16, ropeg[:, t, :, :], rtm,
                              r=["qn", "ropeg"], w=["qn"], keyp="rtm")
                self.cp("act", qnb, qn, r=["qn"], w=["qnb"])
                for hb in range(2):
                    b = self.bank("tr")
                    for hh in range(8):
                        h_ = hb * 8 + hh
                        self.tr(self.psb[b][0:64, hh * 128:(hh + 1) * 128], qnb[:, h_ * 64:(h_ + 1) * 64], self.ident_b,
                                r=["qnb", "identb"], w=[("ps", b)])
                    self.cp("dve" if hb else "act", QT[0:64, hb * 8:(hb + 1) * 8, :],
                            self.psb[b][0:64, :].rearrange("p (a b) -> p a b", a=8), r=[("ps", b)], w=[("QT", hb)])
                for g in range(4):
                    self.attention(
                        KT=lambda st, g=g: KT[0:64, g, st * 128:(st + 1) * 128], kparts=64, kt_list=kt_list,
                        QTv=QT[0:64, 4 * g:4 * g + 4, :], nq=4,
                        VA_of=lambda st, g=g: VA[:, st, g, :], scale=0.125, Pt=Pt,
                        out_of=lambda j, g=g: (ob[:, (4 * g + j) * 64:(4 * g + j + 1) * 64], ("ob", g)),
                        kr=kkeys, qr=[("QT", g // 2)], vr=vkeys, ow=None)
                b = self.bank("tr")
                for c in range(8):
                    self.tr(self.psb[b][:, c * 128:(c + 1) * 128], ob[:, c * 128:(c + 1) * 128], self.ident_b,
                            r=[("ob", c // 2), "identb"], w=[("ps", b)])
                self.cp("act", OT, self.psb[b][:, :].rearrange("p (a b) -> p a b", a=8), r=[("ps", b)], w=["OT"])
                banks = []
                for hb in range(2):
                    b = self.bank("mm")
                    banks.append(b)
                    for cc in range(4):
                        c = hb * 4 + cc
                        for kc in range(8):
                            self.mm(self.ps[b][:, cc * 128:(cc + 1) * 128], w_out[:, kc, c * 128:(c + 1) * 128], OT[:, kc, :], kc == 0, kc == 7,
                                    r=["w_out", "OT"], w=[("ps", b)])
                self.mixer_residual(t, banks, mr_tmp)

    def mixer_ab(self, l, i):
        A = self.A
        A.reset()
        I, O = self.I, self.O
        S_pass = self.pas == "S"
        w_in = A.alloc([8, 2240], BF16)
        w_out = A.alloc([8, 1024], BF16)
        OG = A.alloc([4, 1024], BF16)
        hTm = [A.alloc([8, 128], BF16) for _ in range(2)]
        self.alloc_norm_tmp()
        self.at_rden = A.alloc([4], F32)
        Pt = [A.alloc([4, 128], BF16) for _ in range(3)]
        self.pt_i = 0
        base_off = A.off
        for kh in range(2):
            for ch in range(2):
                self.dma("pool", w_in[:, kh * 4:(kh + 1) * 4, ch * 1120:(ch + 1) * 1120],
                         I["ab_w_in"][i, kh * 512:(kh + 1) * 512, ch * 1120:(ch + 1) * 1120].rearrange("(k p) n -> p k n", p=128),
                         r=[], w=[("w_in", kh, ch)], dq="w_in")
        self.dma("pool", w_out, I["ab_w_out"][i].rearrange("(k p) n -> p k n", p=128), r=[], w=["w_out"], dq="w_out")
        wk = [("w_in", 0, 0), ("w_in", 0, 1), ("w_in", 1, 0), ("w_in", 1, 1)]
        gout = self.cst("gout%d" % i)
        gqn = self.cst("gqn%d" % i)
        gkvn = self.cst("gkvn%d" % i)
        gq96 = self.cst("gq96%d" % i)
        gk96 = self.cst("gk96%d" % i)
        trim = [self.cst("trim0"), self.cst("trim1")]
        tris = [self.cst("tris0"), self.cst("tris1")]
        mask = [self.cst("mask0"), self.cst("mask1")]
        hi = 0
        for sq_ in self.seqs:
            tiles = sq_["tiles"]
            NT = len(tiles)
            T = NT * 128
            t0 = tiles[0]
            A.reset(base_off)
            qlT = [A.alloc([2, T], BF16) for _ in range(2)]
            klT = [A.alloc([2, T], BF16) for _ in range(2)]
            kst = [A.alloc([NT, 256], BF16) for _ in range(2)]
            vtk = A.alloc([NT, 512], BF16)
            rsT = A.alloc([4, T], BF16)
            oT = A.alloc([4, T], F32)
            etot = A.alloc([4, NT], F32)
            Sst = [[A.alloc([128], F32) for _ in range(2)] for _ in range(2)]
            Sbf = [[A.alloc([128], BF16) for _ in range(2)] for _ in range(2)]
            alT = A.alloc([128], F32)
            lsp = A.alloc([512], F32)
            Eb = A.alloc([4, 128], F32)
            Enb = A.alloc([4, 128], F32)
            Ed2 = A.alloc([512], F32)
            ATm = [A.alloc([128], BF16) for _ in range(2)]
            osq = A.alloc([4, 128], BF16)
            orst = A.alloc([4, 128], F32)
            otmp = A.alloc([4, 128], F32)
            self.memset("dve", alT[32:33, :], 1.0, w=["alT1"])
            aw2 = A.alloc([512], F32)
            self.memset("dve", aw2[0:33, :], 0.0, w=["aw2"])
            for z in range(2):
                self.dma("sp", aw2[16 * z:16 * z + 16, z * 256:(z + 1) * 256], I["a_w2"][i, z], r=[], w=["aw2"], dq="aw2")
            self.dma("sp", aw2[32:33, :], I["a_b"][i:i + 1].rearrange("o z n -> o (z n)"), r=[], w=["aw2"], dq="aw2")
            for n, t in enumerate(tiles):
                nc_ = slice(n * 128, (n + 1) * 128)
                h = hTm[hi % 2]
                hk = ("hTm", hi % 2)
                hi += 1
                self.normmod(t, 0, h, hk)
                bqk = self.bank("mm")
                for ch in range(4):
                    for kc in range(8):
                        self.mm(self.ps[bqk][:, ch * 128:(ch + 1) * 128], w_in[:, kc, ch * 128:(ch + 1) * 128], h[:, kc, :], kc == 0, kc == 7,
                                r=[hk] + wk, w=[("ps", bqk)])
                br = self.bank("mm")
                for ch in range(4):
                    for kc in range(8):
                        self.mm(self.ps[br][:, ch * 128:(ch + 1) * 128], w_in[:, kc, 1024 + ch * 128:1024 + (ch + 1) * 128], h[:, kc, :], kc == 0, kc == 7,
                                r=[hk] + wk, w=[("ps", br)])
                self.act(rsT[:, :, nc_], self.ps[br][:, :].rearrange("p (a b) -> p a b", a=4), AF.Silu, r=[("ps", br)], w=[("rsT", n)])
                ba = self.bank("tr")
                for kc in range(8):
                    self.mm(self.ps[ba][0:32, 0:128], w_in[:, kc, 1536:1568], h[:, kc, :], kc == 0, kc == 7, r=[hk] + wk, w=[("ps", ba)])
                self.cp("act", alT[0:32, :], self.ps[ba][0:32, 0:128], r=[("ps", ba)], w=["alT"])
                bkv = self.bank("mm")
                for kc in range(8):
                    self.mm(self.ps[bkv][:, :], h[:, kc, :], w_in[:, kc, 256:768], kc == 0, kc == 7, r=[hk] + wk, w=[("ps", bkv)])
                bv2 = self.bank("mm")
                for kc in range(8):
                    self.mm(self.ps[bv2][:, 0:256], h[:, kc, :], w_in[:, kc, 768:1024], kc == 0, kc == 7, r=[hk] + wk, w=[("ps", bv2)])
                self.cp("act", vtk[:, n, 0:256], self.ps[bkv][:, 256:512], r=[("ps", bkv)], w=[("vtk", n, 0)])
                self.cp("act", vtk[:, n, 256:512], self.ps[bv2][:, 0:256], r=[("ps", bv2)], w=[("vtk", n, 1)])
                bl = self.bank("tr")
                self.mm(self.ps[bl][:, :], alT[0:33, :], aw2[0:33, :], True, True, r=["alT", "alT1", "aw2"], w=[("ps", bl)])
                self.act(lsp, self.ps[bl][:, :], AF.Exp, r=[("ps", bl)], w=["lsp"], scale=-1.0)
                self.act(lsp, lsp, AF.Ln, r=["lsp"], w=["lsp"], bias=1.0)
                bb = self.bank("tr")
                for z in range(2):
                    for fc in range(2):
                        zf = z * 2 + fc
                        self.mm(self.ps[bb][:, zf * 128:(zf + 1) * 128], lsp[:, zf * 128:(zf + 1) * 128], trim[z], True, True,
                                r=["lsp", "cst"], w=[("ps", bb)])
                bd = self.bank("tr")
                for z in range(2):
                    self.mm(self.ps[bd][:, z * 256:(z + 1) * 256], tris[z], lsp[:, z * 256:(z + 1) * 256], True, True,
                            r=["lsp", "cst"], w=[("ps", bd)])
                pbb = self.ps[bb][:, :].rearrange("p (a b) -> p a b", a=4)
                self.act(Eb, pbb, AF.Exp, r=[("ps", bb)], w=["Eb"])
                self.act(Enb, pbb, AF.Exp, r=[("ps", bb)], w=["Enb"], scale=-1.0)
                self.act(Ed2, self.ps[bd][:, :], AF.Exp, r=[("ps", bd)], w=["Ed2"])
                self.cp("pool", etot[:, 0:2, n], Eb[:, 0:2, 127], r=["Eb"], w=[("etot", n)])
                self.cp("pool", etot[:, 2:4, n], Eb[:, 2:4, 0], r=["Eb"], w=[("etot", n)])
                pqk = self.ps[bqk][:, :].rearrange("p (a b) -> p a b", a=4)
                for z in range(2):
                    self.stt(qlT[z][:, :, nc_], pqk[:, 0:2, :], 0.125, Eb[:, 2 * z:2 * z + 2, :], ALU.mult, ALU.mult,
                             r=[("ps", bqk), "Eb"], w=[("qlT", z, n)])
                    self.tt("dve", klT[z][:, :, nc_], pqk[:, 2:4, :], Enb[:, 2 * z:2 * z + 2, :], ALU.mult,
                            r=[("ps", bqk), "Enb"], w=[("klT", z, n)])
                    self.tt("dve", kst[z][:, n, :], self.ps[bkv][:, 0:256], Ed2[:, z * 256:(z + 1) * 256], ALU.mult,
                            r=[("ps", bkv), "Ed2"], w=[("kst", z, n)])
            for z in (range(2) if "noscan" not in self.stages else ()):
                for fc in range(2):
                    if sq_["ctx"]:
                        self.dma("sp", Sst[z][fc], I["c_gla"][i, z, fc * 128:(fc + 1) * 128, :], r=[], w=[("S", z, fc)], dq=("S", z, fc))
                    else:
                        self.memset("pool", Sst[z][fc], 0.0, w=[("S", z, fc)])
                    self.cp("act", Sbf[z][fc], Sst[z][fc], r=[("S", z, fc)], w=[("Sbf", z, fc)])
                order = list(range(NT)) if z == 0 else list(range(NT - 1, -1, -1))
                ai = 0
                for n in order:
                    nc_ = slice(n * 128, (n + 1) * 128)
                    bo = self.bank("acc")
                    for hh in range(4):
                        fc, hp = hh // 2, hh % 2
                        pr = slice(hp * 64, (hp + 1) * 64)
                        bat = self.bank("mm")
                        self.mm(self.ps[bat][:, 0:128], klT[z][pr, fc, nc_], qlT[z][pr, fc, nc_], True, True,
                                r=[("klT", z, n), ("qlT", z, n)], w=[("ps", bat)])
                        am = ATm[ai % 2]
                        amk = ("ATm", ai % 2)
                        ai += 1
                        self.tt("dve", am, self.ps[bat][:, 0:128], mask[z], ALU.mult, r=[("ps", bat), "cst"], w=[amk])
                        self.mm(self.ps[bo][:, hh * 128:(hh + 1) * 128], vtk[:, n, hh * 128:(hh + 1) * 128], am, True, False,
                                r=[("vtk", n, 0), ("vtk", n, 1), amk], w=[("ps", bo)])
                        self.mm(self.ps[bo][:, hh * 128:(hh + 1) * 128], Sbf[z][fc][pr, :], qlT[z][pr, fc, nc_], False, True,
                                r=[("Sbf", z, fc), ("qlT", z, n)], w=[("ps", bo)])
                    pbo = self.ps[bo][:, :].rearrange("p (a b) -> p a b", a=4)
                    if z == 0:
                        self.cp("act", oT[:, :, nc_], pbo, r=[("ps", bo)], w=[("oT", n)])
                    else:
                        self.tt("dve", oT[:, :, nc_], oT[:, :, nc_], pbo, ALU.add, r=[("ps", bo), ("oT", n)], w=[("oT", n)])
                    for fc in range(2):
                        bu = self.bank("mm")
                        for hp in range(2):
                            hh = fc * 2 + hp
                            self.mm(self.ps[bu][hp * 64:(hp + 1) * 64, 0:128], kst[z][:, n, hh * 64:(hh + 1) * 64],
                                    vtk[:, n, hh * 128:(hh + 1) * 128], True, True,
                                    r=[("kst", z, n), ("vtk", n, 0), ("vtk", n, 1)], w=[("ps", bu)])
                        self.stt(Sst[z][fc], Sst[z][fc], etot[:, z * 2 + fc, n:n + 1], self.ps[bu][:, 0:128], ALU.mult, ALU.add,
                                 r=[("S", z, fc), ("etot", n), ("ps", bu)], w=[("S", z, fc)])
                        self.cp("act", Sbf[z][fc], Sst[z][fc], r=[("S", z, fc)], w=[("Sbf", z, fc)])
                if sq_["bidx"] is not None:
                    for fc in range(2):
                        self.dma("sp", O["o_gla"][sq_["bidx"], i, z, fc * 128:(fc + 1) * 128, :], Sst[z][fc],
                                 r=[("S", z, fc)], w=[], dq=("S", z, fc))
            for n, t in (enumerate(tiles) if "noglaout" not in self.stages else ()):
                nc_ = slice(n * 128, (n + 1) * 128)
                self.act(osq, oT[:, :, nc_], AF.Square, r=[("oT", n)], w=["osq"])
                b = self.bank("tr")
                self.mm(self.ps[b][:, :], self.ones_b, osq.rearrange("p a b -> p (a b)"), True, True, r=["osq", "ones"], w=[("ps", b)])
                self.rstd(orst, self.ps[b][:, :].rearrange("p (a b) -> p a b", a=4), 128, r=[("ps", b), "small_c"], w=["orst"])
                self.tt("dve", otmp, oT[:, :, nc_], orst, ALU.mult, r=[("oT", n), "orst"], w=["otmp"])
                self.stt(OG[:, :, t * 128:(t + 1) * 128], otmp, gout, rsT[:, :, nc_], ALU.mult, ALU.mult,
                         r=["otmp", "cst", ("rsT", n)], w=[("OG", t)])
            if self.pas == "P" and l == 0 and sq_["bidx"] == 0:
                self.dbg("oT", oT, [("oT", n) for n in range(NT)], [128, 4, T])
                self.dbg("OG", OG[:, :, 0:T], [("OG", t) for t in tiles], [128, 4, T])
            self.P.barrier()
            if "noM" in self.stages:
                continue
            A.reset(base_off)
            nk0 = 4 if sq_["ctx"] else 0
            NK = nk0 + NT
            w_qb = A.alloc([3, 768], BF16)
            w_kvb = A.alloc([2, 1024], BF16)
            self.dma("pool", w_qb, I["w_qb"][i].rearrange("(k p) n -> p k n", p=128), r=[], w=["w_qb"], dq="w_qb")
            self.dma("pool", w_kvb, I["w_kvb"][i].rearrange("(k p) n -> p k n", p=128), r=[], w=["w_kvb"], dq="w_kvb")
            KTm = A.alloc([8, NK * 128], BF16)
            VAm = A.alloc([NK, 8, 65], BF16)
            cqb = A.alloc([NT, 384], BF16)
            QG = 4
            QTm = A.alloc([8, QG * 128], BF16)
            omb = A.alloc([QG, 512], BF16)
            OM = A.alloc([4, QG * 128], BF16)
            cb = A.alloc([256], BF16)
            cT = A.alloc([2, 128], BF16)
            kc96 = A.alloc([8, 96], F32)
            sq96 = A.alloc([8, 96], F32)
            knb = A.alloc([8, 96], BF16)
            ckvf = A.alloc([256], F32)
            ckvn = A.alloc([256], F32)
            kpe = A.alloc([32], F32)
            st8 = A.alloc([8], F32)
            st1 = A.alloc([2], F32)
            cqT = A.alloc([3, 128], BF16)
            rtm = [A.alloc([8, 2, 8], F32) for _ in range(4)]
            mr_tmp = A.alloc([8, 128], F32)
            if S_pass:
                ropem = A.alloc([8, 2, 16], F32)
                cck = A.alloc([256], F32)
                ckp = A.alloc([4, 32], F32)
                self.dma("sp", ropem, I["ropem"].rearrange("(t p) c q -> p t c q", p=128), r=[], w=["ropem"], dq="ropem")
                self.dma("sp", ckp, I["c_kpe"][i].rearrange("(t p) n -> p t n", p=128), r=[], w=["ckp"], dq="ckp")
            self.memset("pool", VAm[:, :, :, 64:65], 1.0, w=["VA1"])

            def norm96(src_views, srck, dstn, gain, rope_t):
                self.act(sq96, kc96, AF.Square, r=["kc96"], w=["sq96"])
                self.red(st8, sq96, r=["sq96"], w=["st8"])
                self.rstd(st8, st8, 96, r=["st8", "small_c"], w=["st8"])
                self.tt("dve", kc96, kc96, st8.unsqueeze(2).broadcast_to([128, 8, 96]), ALU.mult, r=["kc96", "st8"], w=["kc96"])
                self.tt("pool", kc96, kc96, gain.unsqueeze(1).broadcast_to([128, 8, 96]), ALU.mult, r=["kc96", "cst"], w=["kc96"])
                if rope_t is not None:
                    self.rope(kc96[:, :, 64:96].rearrange("p h (a b q) -> p h a b q", a=2, b=2, q=8), 8, 8, ropem[:, rope_t, :, :], rtm,
                              r=["kc96", "ropem"], w=["kc96"], keyp="rtm")
                self.cp("act", knb, kc96, r=["kc96"], w=["knb"])

            import os
            LVL = int(os.environ.get("KSIDE_LVL", "99"))

            def kside(ckvn_ap, ckvn_k, kpe_ap, kpe_k, kt, rope_t):
                self.cp("act", cb, ckvn_ap, r=[ckvn_k], w=["cb"])
                if LVL < 1:
                    return
                b = self.bank("tr")
                for kc in range(2):
                    self.tr(self.psb[b][:, kc * 128:(kc + 1) * 128], cb[:, kc * 128:(kc + 1) * 128], self.ident_b, r=["cb", "identb"], w=[("ps", b)])
                self.cp("dve", cT, self.psb[b][:, 0:256].rearrange("p (a b) -> p a b", a=2), r=[("ps", b)], w=["cT"])
                if LVL < 2:
                    return
                for bk in range(2):
                    b = self.bank("mm")
                    for kc in range(2):
                        self.mm(self.ps[b][:, :], cT[:, kc, :], w_kvb[:, kc, bk * 512:(bk + 1) * 512], kc == 0, kc == 1,
                                r=["cT", "w_kvb"], w=[("ps", b)])
                    pv = self.ps[b][:, :].rearrange("p (h d) -> p h d", h=4)
                    self.cp("act", kc96[:, bk * 4:(bk + 1) * 4, 0:64], pv[:, :, 0:64], r=[("ps", b)], w=["kc96"])
                    self.cp("dve", VAm[:, kt, bk * 4:(bk + 1) * 4, 0:64], pv[:, :, 64:128], r=[("ps", b)], w=[("VAm", kt)])
                if LVL < 3:
                    return
                self.cp("pool", kc96[:, :, 64:96], kpe_ap.unsqueeze(1).broadcast_to([128, 8, 32]), r=[kpe_k], w=["kc96"])
                if LVL < 4:
                    return
                norm96(None, None, None, gk96, rope_t)
                if LVL < 5:
                    return
                b = self.bank("tr")
                for hh in range(8):
                    self.tr(self.psb[b][0:96, hh * 128:(hh + 1) * 128], knb[:, hh, :], self.ident_b, r=["knb", "identb"], w=[("ps", b)])
                self.cp("dve", KTm[0:96, :, kt * 128:(kt + 1) * 128], self.psb[b][0:96, :].rearrange("p (a b) -> p a b", a=8),
                        r=[("ps", b)], w=[("KTm", kt)])

            if sq_["ctx"]:
                for kt in range(4):
                    self.dma("sp", cck, I["c_ckv"][i, kt * 128:(kt + 1) * 128, :], r=[], w=["cck"], dq="cck")
                    kside(cck, "cck", ckp[:, kt, :], "ckp", kt, None)
            for n, t in enumerate(tiles):
                h = hTm[hi % 2]
                hk = ("hTm", hi % 2)
                hi += 1
                self.normmod(t, 0, h, hk)
                b1 = self.bank("mm")
                for kc in range(8):
                    self.mm(self.ps[b1][:, :], h[:, kc, :], w_in[:, kc, 1568:2080], kc == 0, kc == 7, r=[hk] + wk, w=[("ps", b1)])
                b2 = self.bank("mm")
                for kc in range(8):
                    self.mm(self.ps[b2][:, 0:160], h[:, kc, :], w_in[:, kc, 2080:2240], kc == 0, kc == 7, r=[hk] + wk, w=[("ps", b2)])
                self.act(sq96.rearrange("p a b -> p (a b)")[:, 0:384], self.ps[b1][:, 0:384], AF.Square, r=[("ps", b1)], w=["sq96", "st1"],
                         accum_out=st1[:, 0:1])
                self.rstd(st1[:, 0:1], st1[:, 0:1], 384, r=["st1"], w=["st1"])
                self.stt(cqb[:, n, :], self.ps[b1][:, 0:384], st1[:, 0:1], gqn, ALU.mult, ALU.mult, r=[("ps", b1), "st1", "cst"], w=[("cqb", n)])
                self.cp("act", ckvf[:, 0:128], self.ps[b1][:, 384:512], r=[("ps", b1)], w=["ckvf"])
                self.cp("act", ckvf[:, 128:256], self.ps[b2][:, 0:128], r=[("ps", b2)], w=["ckvf"])
                self.cp("dve", kpe, self.ps[b2][:, 128:160], r=[("ps", b2)], w=["kpe"])
                self.act(sq96.rearrange("p a b -> p (a b)")[:, 0:256], ckvf, AF.Square, r=["ckvf"], w=["sq96", "st1b"], accum_out=st1[:, 1:2])
                self.rstd(st1[:, 1:2], st1[:, 1:2], 256, r=["st1b"], w=["st1b"])
                self.stt(ckvn, ckvf, st1[:, 1:2], gkvn, ALU.mult, ALU.mult, r=["ckvf", "st1b", "cst"], w=["ckvn"])
                if sq_["bidx"] is not None and "noMout" not in self.stages:
                    bi = sq_["bidx"]
                    self.dma("sp", O["o_ckv"][bi, i, n * 128:(n + 1) * 128, :], ckvn, r=["ckvn"], w=[], dq="ckvn_o")
                    self.dma("sp", O["o_kpe"][bi, i, n * 128:(n + 1) * 128, :], kpe, r=["kpe"], w=[], dq="kpe_o")
                if "noMkside" not in self.stages:
                    kside(ckvn, "ckvn", kpe, "kpe", nk0 + n, t if sq_["rope"] else None)
            kt_list = list(range(NK))
            kkeys = [("KTm", kt) for kt in kt_list]
            vkeys = [("VAm", kt) for kt in kt_list] + ["VA1"]
            for g0 in (range(0, NT, QG) if "noMattn" not in self.stages else ()):
                gt = list(range(g0, min(NT, g0 + QG)))
                nq = len(gt)
                for jn, n in enumerate(gt):
                    t = tiles[n]
                    b = self.bank("tr")
                    for kc in range(3):
                        self.tr(self.psb[b][:, kc * 128:(kc + 1) * 128], cqb[:, n, kc * 128:(kc + 1) * 128], self.ident_b,
                                r=[("cqb", n), "identb"], w=[("ps", b)])
                    self.cp("act", cqT, self.psb[b][:, 0:384].rearrange("p (a b) -> p a b", a=3), r=[("ps", b)], w=["cqT"])
                    for bk in range(2):
                        b = self.bank("mm")
                        for kc in range(3):
                            self.mm(self.ps[b][:, 0:384], cqT[:, kc, :], w_qb[:, kc, bk * 384:(bk + 1) * 384], kc == 0, kc == 2,
                                    r=["cqT", "w_qb"], w=[("ps", b)])
                        self.cp("act", kc96[:, bk * 4:(bk + 1) * 4, :], self.ps[b][:, 0:384].rearrange("p (h d) -> p h d", h=4),
                                r=[("ps", b)], w=["kc96"])
                    norm96(None, None, None, gq96, t if sq_["rope"] else None)
                    b = self.bank("tr")
                    for hh in range(8):
                        self.tr(self.psb[b][0:96, hh * 128:(hh + 1) * 128], knb[:, hh, :], self.ident_b, r=["knb", "identb"], w=[("ps", b)])
                    self.cp("dve", QTm[0:96, :, jn * 128:(jn + 1) * 128], self.psb[b][0:96, :].rearrange("p (a b) -> p a b", a=8),
                            r=[("ps", b)], w=[("QTm", jn)])
                for hh in range(8):
                    self.attention(
                        KT=lambda st, hh=hh: KTm[0:96, hh, st * 128:(st + 1) * 128], kparts=96, kt_list=kt_list,
                        QTv=QTm[0:96, hh, 0:nq * 128], nq=nq,
                        VA_of=lambda st, hh=hh: VAm[:, st, hh, :], scale=float(96 ** -0.5), Pt=Pt,
                        out_of=lambda j, hh=hh: (omb[:, j, hh * 64:(hh + 1) * 64], ("omb", j)),
                        kr=kkeys, qr=[("QTm", j) for j in range(nq)], vr=vkeys, ow=None)
                if self.pas == "P" and l == 0 and sq_["bidx"] == 0:
                    self.dbg("omb", omb[:, 0:nq, :], [("omb", j) for j in range(nq)], [128, nq, 512])
                for jn, n in enumerate(gt):
                    t = tiles[n]
                    b = self.bank("tr")
                    for c in range(4):
                        self.tr(self.psb[b][:, c * 128:(c + 1) * 128], omb[:, jn, c * 128:(c + 1) * 128], self.ident_b,
                                r=[("omb", jn), "identb"], w=[("ps", b)])
                    self.cp("act", OM[:, :, jn * 128:(jn + 1) * 128], self.psb[b][:, 0:512].rearrange("p (a b) -> p a b", a=4),
                            r=[("ps", b)], w=[("OM", jn)])
                    banks = []
                    for hb in range(2):
                        b = self.bank("mm")
                        banks.append(b)
                        for cc in range(4):
                            c = hb * 4 + cc
                            for kc in range(8):
                                rhs = OG[:, kc, t * 128:(t + 1) * 128] if kc < 4 else OM[:, kc - 4, jn * 128:(jn + 1) * 128]
                                self.mm(self.ps[b][:, cc * 128:(cc + 1) * 128], w_out[:, kc, c * 128:(c + 1) * 128],
                                        rhs, kc == 0, kc == 7,
                                        r=["w_out", ("OG", t), ("OM", jn)], w=[("ps", b)])
                    self.mixer_residual(t, banks, mr_tmp)
            self.P.barrier()


def _rope_tables(n_tok, d_rot):
    t = np.arange(n_tok, dtype=np.int32)
    pos = np.stack([t // 64, t % 64], axis=-1).astype(np.float32)
    quarter = d_rot // 4
    inv = np.power(np.float32(10000.0), -np.arange(quarter, dtype=np.float32) / np.float32(quarter)).astype(np.float32)
    ang = pos[:, :, None] * inv
    cos = np.cos(ang).astype(np.float32).reshape(n_tok, 2 * quarter)
    sin = np.sin(ang).astype(np.float32).reshape(n_tok, 2 * quarter)
    return np.ascontiguousarray(np.stack([cos, sin], axis=1))


def _build_cst(inp, b):
    c = np.zeros((128, NCST), np.float32)

    def put(name, arr, parts=128):
        o, n = CST_OFF[name]
        c[0:parts, o:o + n] = np.asarray(arr, np.float32).reshape(parts, n)

    s = np.arange(128)[:, None]
    t = np.arange(128)[None, :]
    v = np.float32(-1.0 / 16.0)
    put("ident", np.eye(128))
    put("trim0", (s <= t) * v)
    put("trim1", (s >= t) * v)
    put("tris0", (s > t) * v)
    put("tris1", (s < t) * v)
    put("mask0", (s <= t) * 1.0)
    put("mask1", (s >= t) * 1.0)
    put("rm", np.array([[1, 0], [0, 1], [1, 1]], np.float32), parts=3)
    cond = np.stack([inp["c_ctx"].reshape(8, 128).T, inp["c"][b].reshape(8, 128).T], axis=-1)
    put("cond", cond)
    put("gmix", inp["norm_mix_g"].reshape(4, 8, 128).transpose(2, 0, 1))
    put("gffn", inp["norm_ffn_g"].reshape(4, 8, 128).transpose(2, 0, 1))
    for i in range(2):
        put("gq%d" % i, np.broadcast_to(inp["gqa_qn_g"][i][None, :], (128, 64)))
        put("gk%d" % i, np.broadcast_to(inp["gqa_kn_g"][i][None, :], (128, 64)))
        put("gout%d" % i, inp["gla_out_g"][i].reshape(128, 1))
        put("gqn%d" % i, np.broadcast_to(inp["mla_q_norm_g"][i][None, :], (128, 384)))
        put("gkvn%d" % i, np.broadcast_to(inp["mla_kv_norm_g"][i][None, :], (128, 256)))
        put("gq96%d" % i, np.broadcast_to(inp["mla_qn_g"][i][None, :], (128, 96)))
        put("gk96%d" % i, np.broadcast_to(inp["mla_kn_g"][i][None, :], (128, 96)))
    return c


_NC_CACHE = {}


def _get_nc(debug=(), stages=None):
    key = (tuple(sorted(debug)), None if stages is None else tuple(sorted(stages)))
    if key not in _NC_CACHE:
        kb = KB(debug, stages)
        nc = kb.build()
        _NC_CACHE[key] = (nc, kb)
    return _NC_CACHE[key]


def make_in_maps(inp):
    inp = {k: np.ascontiguousarray(np.asarray(v)) for k, v in inp.items()}
    ropeg = _rope_tables(1024, 64)
    ropem = _rope_tables(1024, 32)
    shared = dict(
        ada_w=inp["ada_w"], ada_b=inp["ada_b"], ffn_w_in=inp["ffn_w_in"], ffn_w_out=inp["ffn_w_out"],
        ab_w_in=inp["ab_w_in"], ab_w_out=inp["ab_w_out"], a_w2=inp["gla_a_w2"], a_b=inp["gla_a_b"],
        w_qb=inp["mla_w_qb"], w_kvb=inp["mla_w_kvb"], gqa_w_in=inp["gqa_w_in"], gqa_w_out=inp["gqa_w_out"],
        ropeg=ropeg, ropem=ropem)
    in_maps = []
    for core in range(8):
        b = core // 4
        m = dict(shared)
        m["cst"] = _build_cst(inp, b)
        m["xp"] = np.ascontiguousarray(inp["x_prompt"][core * 4:(core + 1) * 4].reshape(1024, D))
        m["xs"] = np.ascontiguousarray(inp["x_sample"][b])
        m["c_ckv"] = np.ascontiguousarray(inp["cache_mla_ckv"][b])
        m["c_kpe"] = np.ascontiguousarray(inp["cache_mla_kpe"][b])
        m["c_gla"] = np.ascontiguousarray(inp["state_gla"][b].reshape(2, 2, 256, 128))
        m["c_gk"] = np.ascontiguousarray(inp["cache_gqa_k"][b].reshape(2, 512, 256))
        m["c_gv"] = np.ascontiguousarray(inp["cache_gqa_v"][b].reshape(2, 512, 256))
        in_maps.append(m)
    return in_maps


def kernel(**inputs):
    nc, kb = _get_nc()
    in_maps = make_in_maps(inputs)
    res = run_bass_kernel_spmd(nc, in_maps, core_ids=list(range(8)))
    R = res.results
    y_prompt = np.concatenate([R[c]["yp"].reshape(4, 256, D) for c in range(8)], axis=0)
    y_sample = np.stack([R[0]["ys"], R[4]["ys"]], axis=0)
    new_ckv = np.concatenate([R[c]["o_ckv"] for c in range(8)], axis=0)
    new_kpe = np.concatenate([R[c]["o_kpe"] for c in range(8)], axis=0)
    new_gla = np.concatenate([R[c]["o_gla"].reshape(4, 2, 2, 4, 64, 128) for c in range(8)], axis=0)
    new_k = np.concatenate([R[c]["o_gk"].reshape(4, 2, 256, 4, 64) for c in range(8)], axis=0)
    new_v = np.concatenate([R[c]["o_gv"].reshape(4, 2, 256, 4, 64) for c in range(8)], axis=0)
    outs = (y_prompt, y_sample, new_ckv, new_kpe, new_gla, new_k, new_v)
    return tuple(np.ascontiguousarray(o, dtype=np.float32) for o in outs)
```

```python
import bisect
import contextlib
import numpy as np
import concourse.bass as bass
import concourse.mybir as mybir
from concourse.bass_utils import run_bass_kernel_spmd

F32 = mybir.dt.float32
BF16 = mybir.dt.bfloat16
AF = mybir.ActivationFunctionType
ALU = mybir.AluOpType
AX = mybir.AxisListType

ENGS = ("pe", "act", "dve", "pool", "sp")
RAW, WAR, WAW = 1, 2, 4
EPS = 1e-6
D = 1024
FH = 2816
ARENA_BYTES = 152 * 1024
NTOK = 1280
NTILE = 10


class Prog:
    def __init__(self):
        self.ops = []
        self.last_w = {}
        self.readers = {}

    def op(self, eng, fn, r=(), w=(), dq=None, inc=16):
        i = len(self.ops)
        deps = {}
        psr = [k for k in r if isinstance(k, tuple) and k and k[0] == "ps"]
        if psr:
            r = [k for k in r if not (isinstance(k, tuple) and k and k[0] == "ps")]
            w = list(w) + [k for k in psr if k not in w]
            for k in psr:
                lw = self.last_w.get(k)
                if lw is not None:
                    deps[lw] = deps.get(lw, 0) | RAW
        for k in r:
            lw = self.last_w.get(k)
            if lw is not None:
                deps[lw] = deps.get(lw, 0) | RAW
        for k in w:
            lw = self.last_w.get(k)
            if lw is not None:
                deps[lw] = deps.get(lw, 0) | WAW
            for rd in self.readers.get(k, ()):
                if rd != i:
                    deps[rd] = deps.get(rd, 0) | WAR
        for k in r:
            self.readers.setdefault(k, []).append(i)
        for k in w:
            self.last_w[k] = i
            self.readers[k] = []
        deps.pop(i, None)
        self.ops.append(dict(eng=eng, fn=fn, deps=deps, dq=dq, bar=None, inc=inc))
        return i

    def barrier(self):
        first = len(self.ops)
        for e in ENGS:
            self.ops.append(dict(eng=e, fn="drain", deps={}, dq=None, bar=("sig", first)))
        sig_ids = list(range(first, first + len(ENGS)))
        for e in ENGS:
            self.ops.append(dict(eng=e, fn="nop", deps={s: RAW for s in sig_ids}, dq=None, bar=("wait", first)))
        iscc = lambda k: isinstance(k, str) and k.startswith("cc")
        self.last_w = {k: v for k, v in self.last_w.items() if iscc(k)}
        self.readers = {k: v for k, v in self.readers.items() if iscc(k)}

    def emit(self, nc, es):
        ops = self.ops
        n = len(ops)
        needed = [False] * n
        for i, o in enumerate(ops):
            kept = []
            for d, kind in o["deps"].items():
                od = ops[d]
                if od["dq"] is None and o["dq"] is None and od["eng"] == o["eng"] and o["bar"] is None:
                    if o["eng"] == "pe":
                        continue
                kept.append(d)
                needed[d] = True
            o["kdeps"] = kept
        eng_sem = {e: es.enter_context(nc.semaphore("sem_" + e)) for e in ENGS}
        dq_keys = []
        seen = set()
        for o in ops:
            if o["dq"] is not None and o["dq"] not in seen:
                seen.add(o["dq"])
                dq_keys.append(o["dq"])
        dq_sem = {k: es.enter_context(nc.semaphore("dq_%d" % j)) for j, k in enumerate(dq_keys)}
        dq_idx = {k: [] for k in dq_keys}
        dq_cum = {k: [0] for k in dq_keys}
        eng_cnt = {e: 0 for e in ENGS}

        def dq_before(k, i):
            return dq_cum[k][bisect.bisect_left(dq_idx[k], i)]

        for i, o in enumerate(ops):
            if o["dq"] is not None:
                dq_idx[o["dq"]].append(i)
                dq_cum[o["dq"]].append(dq_cum[o["dq"]][-1] + o["inc"])
                o["sig"] = ("dq", o["dq"])
            elif needed[i] or (o["bar"] is not None and o["bar"][0] == "sig"):
                eng_cnt[o["eng"]] += 1
                o["sig"] = ("eng", o["eng"], eng_cnt[o["eng"]])
            else:
                o["sig"] = None
        per_eng = {e: [] for e in ENGS}
        for i, o in enumerate(ops):
            per_eng[o["eng"]].append(i)
        self.n_sems = len(ENGS) + len(dq_keys)
        self.counts = {e: len(per_eng[e]) for e in ENGS}

        def run(e, h):
            waited = {}
            for i in per_eng[e]:
                o = ops[i]
                waits = {}
                for d in o["kdeps"]:
                    od = ops[d]
                    if od["dq"] is not None:
                        k = od["dq"]
                        cnt = dq_before(k, i)
                        key = ("dq", k)
                        waits[key] = max(waits.get(key, 0), cnt)
                    else:
                        key = ("eng", od["eng"])
                        waits[key] = max(waits.get(key, 0), od["sig"][2])
                if o["bar"] is not None and o["bar"][0] == "sig" and e == "sp":
                    for k in dq_keys:
                        if isinstance(k, str) and k.startswith("cc"):
                            continue
                        cnt = dq_before(k, i)
                        if cnt:
                            waits[("dq", k)] = cnt
                for key, v in waits.items():
                    if waited.get(key, 0) >= v:
                        continue
                    waited[key] = v
                    sem = dq_sem[key[1]] if key[0] == "dq" else eng_sem[key[1]]
                    h.wait_ge(sem, v)
                if o["fn"] == "drain":
                    inst = h.nop() if e == "sp" else h.drain()
                elif o["fn"] == "nop":
                    inst = None
                else:
                    inst = o["fn"](h)
                s = o["sig"]
                if s is not None:
                    if s[0] == "dq":
                        inst.then_inc(dq_sem[s[1]], o["inc"])
                    else:
                        inst.then_inc(eng_sem[s[1]], 1)
            if e == "sp":
                for k in dq_keys:
                    h.wait_ge(dq_sem[k], dq_cum[k][-1])

        with nc.Block() as block:
            @block.tensor
            def _(h):
                run("pe", h)

            @block.scalar
            def _(h):
                run("act", h)

            @block.vector
            def _(h):
                run("dve", h)

            @block.gpsimd
            def _(h):
                run("pool", h)

            @block.sync
            def _(h):
                run("sp", h)


class Arena:
    def __init__(self, t, nbytes):
        self.t = t
        self.nbytes = nbytes
        self.off = 0
        self.peak = 0

    def reset(self, off=0):
        self.off = off

    def alloc(self, free_shape, dtype, parts=128):
        n = int(np.prod(free_shape))
        esz = 4 if dtype == F32 else 2
        sz = (n * esz + 31) // 32 * 32
        o = self.off
        assert o + sz <= self.nbytes, ("arena overflow", o, sz, self.nbytes)
        self.off = o + sz
        self.peak = max(self.peak, self.off)
        ap = self.t[0:parts, o // 2:(o + n * esz) // 2]
        if dtype == F32:
            ap = ap.bitcast(F32)
        fs = list(free_shape)
        if len(fs) == 2:
            ap = ap.rearrange("p (a b) -> p a b", a=fs[0], b=fs[1])
        elif len(fs) == 3:
            ap = ap.rearrange("p (a b c) -> p a b c", a=fs[0], b=fs[1], c=fs[2])
        elif len(fs) == 4:
            ap = ap.rearrange("p (a b c d) -> p a b c d", a=fs[0], b=fs[1], c=fs[2], d=fs[3])
        return ap


def _cst_layout():
    off = {}
    o = 0

    def add(name, n):
        nonlocal o
        off[name] = (o, n)
        o += n

    add("ident", 128)
    add("trim0", 128)
    add("trim1", 128)
    add("tris0", 128)
    add("tris1", 128)
    add("mask0", 128)
    add("mask1", 128)
    add("rm", 2)
    add("cond", 16)
    add("sel", 8)
    add("gmix", 32)
    add("gffn", 32)
    for i in range(2):
        add("gq%d" % i, 64)
        add("gk%d" % i, 64)
        add("gout%d" % i, 1)
        add("gqn%d" % i, 384)
        add("gkvn%d" % i, 256)
        add("gq96%d" % i, 96)
        add("gk96%d" % i, 96)
    return off, o


CST_OFF, NCST = _cst_layout()


class KB:
    def __init__(self, debug=(), stages=None):
        self.stages = set(stages) if stages is not None else {"adaln", "ffn", "mixc", "mixab", "P", "S"}
        self.debug = set(debug)
        self.dbg_outs = {}

    def mm(self, out, lhsT, rhs, start, stop, r, w, **kw):
        self.P.op("pe", lambda h: h.matmul(out, lhsT, rhs, start=start, stop=stop, **kw), r=r, w=w)

    def tr(self, out, in_, ident, r, w):
        self.P.op("pe", lambda h: h.transpose(out, in_, ident), r=r, w=w)

    def act(self, out, in_, func, r, w, **kw):
        self.P.op("act", lambda h: h.activation(out, in_, func, **kw), r=r, w=w)

    def tt(self, eng, out, a, b, op, r, w):
        self.P.op(eng, lambda h: h.tensor_tensor(out, a, b, op), r=r, w=w)

    def stt(self, out, in0, scalar, in1, op0, op1, r, w):
        self.P.op("dve", lambda h: h.scalar_tensor_tensor(out, in0, scalar, in1, op0, op1), r=r, w=w)

    def cp(self, eng, out, in_, r, w):
        if eng == "act":
            self.P.op("act", lambda h: h.copy(out, in_), r=r, w=w)
        else:
            self.P.op(eng, lambda h: h.tensor_copy(out, in_), r=r, w=w)

    def recip(self, out, in_, r, w):
        self.P.op("dve", lambda h: h.reciprocal(out, in_), r=r, w=w)

    def red(self, out, in_, r, w):
        self.P.op("dve", lambda h: h.tensor_reduce(out, in_, AX.X, ALU.add), r=r, w=w)

    def memset(self, eng, ap, val, w):
        self.P.op(eng, lambda h: h.memset(ap, val), w=w)

    def dma(self, q, out, in_, r, w, dq, **kw):
        self.P.op(q, lambda h: h.dma_start(out=out, in_=in_, **kw), r=r, w=w, dq=dq)

    def allgather(self, key, r, w):
        src, dst = self.CC[key]
        name = "cc_%s_%s" % key
        self.P.op("pool", lambda h: h.collective_compute("AllGather", ALU.bypass, replica_groups=[[0, 1, 2, 3], [4, 5, 6, 7]],
                                                         ins=[src.ap().opt()], outs=[dst.ap().opt()]),
                  r=r, w=w, dq=name, inc=1)

    def bank(self, pool):
        lst, idx = self.pools[pool]
        b = lst[idx % len(lst)]
        self.pools[pool][1] = idx + 1
        return b

    def rstd(self, out, ss, n, r, w):
        self.act(out, ss, AF.Sqrt, r=r, w=w, bias=EPS, scale=1.0 / n)
        self.recip(out, out, r=w, w=w)

    def cst(self, name, parts=128):
        o, n = CST_OFF[name]
        return self.cst_t[0:parts, o:o + n]

    def dbg(self, name, ap, r, shape):
        if name not in self.debug:
            return
        t = self.nc.dram_tensor("dbg_" + name, list(shape), ap.dtype if hasattr(ap, "dtype") else F32, kind="ExternalOutput").ap()
        self.dbg_outs[name] = shape
        self.dma("sp", t, ap, r=r, w=[], dq=("dbg", name))

    def build(self):
        nc = bass.Bass("TRN2", target_bir_lowering=False)
        self.nc = nc
        self.P = Prog()

        def din(name, shape):
            return nc.dram_tensor(name, list(shape), F32, kind="ExternalInput").ap()

        def dout(name, shape):
            return nc.dram_tensor(name, list(shape), F32, kind="ExternalOutput").ap()

        I = {}
        I["cst"] = din("cst", [128, NCST])
        I["xp"] = din("xp", [1024, D])
        I["xs"] = din("xs", [256, D])
        I["ada_w"] = din("ada_w", [4, D, 6 * D])
        I["ada_b"] = din("ada_b", [4, 6 * D])
        I["ffn_w_in"] = din("ffn_w_in", [4, D, 2 * FH])
        I["ffn_w_out"] = din("ffn_w_out", [4, FH, D])
        I["ab_w_in"] = din("ab_w_in", [2, D, 2240])
        I["ab_w_out"] = din("ab_w_out", [2, D, D])
        I["a_w2"] = din("a_w2", [2, 2, 16, 256])
        I["a_b"] = din("a_b", [2, 2, 256])
        I["w_qb"] = din("w_qb", [2, 384, 768])
        I["w_kvb"] = din("w_kvb", [2, 256, 1024])
        I["gqa_w_in"] = din("gqa_w_in", [2, D, 1536])
        I["gqa_w_out"] = din("gqa_w_out", [2, D, D])
        I["c_ckv"] = din("c_ckv", [2, 512, 256])
        I["c_kpe"] = din("c_kpe", [2, 512, 32])
        I["c_gla"] = din("c_gla", [2, 2, 256, 128])
        I["c_gk"] = din("c_gk", [2, 512, 256])
        I["c_gv"] = din("c_gv", [2, 512, 256])
        I["ropeg"] = din("ropeg", [256, 2, 32])
        I["ropem"] = din("ropem", [256, 2, 16])
        I["ropem_all"] = din("ropem_all", [1024, 2, 16])
        O = {}
        O["yp"] = dout("yp", [1024, D])
        O["ys"] = dout("ys", [256, D])
        O["o_ckv"] = dout("o_ckv", [4, 2, 256, 256])
        O["o_kpe"] = dout("o_kpe", [4, 2, 256, 32])
        O["o_gla"] = dout("o_gla", [4, 2, 2, 256, 128])
        O["o_gk"] = dout("o_gk", [4, 2, 256, 256])
        O["o_gv"] = dout("o_gv", [4, 2, 256, 256])
        self.I, self.O = I, O
        self.CC = {}
        for l in range(4):
            if l % 2 == 0:
                self.CC[(l, "g")] = (nc.dram_tensor("ccs_g%d" % l, [512, 129], F32), nc.dram_tensor("ccd_g%d" % l, [2048, 129], F32))
                self.CC[(l, "m")] = (nc.dram_tensor("ccs_m%d" % l, [256, 288], F32), nc.dram_tensor("ccd_m%d" % l, [1024, 288], F32))
            else:
                self.CC[(l, "c")] = (nc.dram_tensor("ccs_c%d" % l, [256, 512], F32), nc.dram_tensor("ccd_c%d" % l, [1024, 512], F32))

        with contextlib.ExitStack() as es:
            self.xT = es.enter_context(nc.sbuf_tensor("xT", [128, 8, NTOK], F32))
            self.cst_t = es.enter_context(nc.sbuf_tensor("cst_sb", [128, NCST], F32))
            small = es.enter_context(nc.sbuf_tensor("small", [128, 4 * 48 * 2 + 96 + 8], F32))
            cbf = es.enter_context(nc.sbuf_tensor("cbf", [128, 256 + 16], BF16))
            arena_t = es.enter_context(nc.sbuf_tensor("arena", [128, ARENA_BYTES // 2], BF16))
            self.A = Arena(arena_t, ARENA_BYTES)
            self.ps = [es.enter_context(nc.psum_tensor("ps%d" % i, [128, 512], F32)) for i in range(8)]
            self.psb = [p.bitcast(BF16) for p in self.ps]
            self.pools = {"mm": [[0, 1, 2, 3], 0], "acc": [[4, 5], 0], "tr": [[6, 7], 0]}
            self.modt = small[:, 0:384].rearrange("p (l c k) -> p l c k", l=4, c=48, k=2)
            self.lsc = small[:, 384:480].rearrange("p (g a b) -> p g a b", g=2, a=6, b=8)
            self.eps_c = small[:, 480:481]
            self.one_c = small[:, 481:482]
            self.ident_b = cbf[:, 0:128]
            self.ones_b = cbf[:, 128:256]
            self.sc_b = cbf[:, 256:272].rearrange("p (k c) -> p k c", k=8, c=2)
            self.ident_f = self.cst("ident")

            self.prologue()
            if "adaln" in self.stages:
                self.adaln_all()
            self.run_all()
            self.P.emit(nc, es)
        return nc

    def prologue(self):
        self.dma("sp", self.cst_t[:, :], self.I["cst"], r=[], w=["cst"], dq="cst")
        self.memset("dve", self.eps_c, EPS, w=["small_c"])
        self.memset("dve", self.one_c, 1.0, w=["small_c"])
        self.memset("dve", self.ones_b, 1.0, w=["ones"])
        self.cp("dve", self.ident_b, self.ident_f, r=["cst"], w=["identb"])
        cond = self.cst("cond").rearrange("p (k c) -> p k c", k=8, c=2)
        self.act(self.sc_b, cond, AF.Silu, r=["cst"], w=["scb"])

    def adaln_all(self):
        A = self.A
        A.reset()
        slots = [A.alloc([8, 512], BF16) for _ in range(3)]
        mtok = A.alloc([6144], F32)
        rm = self.cst("rm", parts=3)
        for l in range(4):
            self.dma("sp", mtok[2:3, :], self.I["ada_b"][l:l + 1, :], r=[], w=[("mtok", "b")], dq="mtokb")
            for j in range(12):
                s = (l * 12 + j) % 3
                src = self.I["ada_w"][l, :, j * 512:(j + 1) * 512].rearrange("(k p) n -> p k n", p=128)
                self.dma("pool", slots[s], src, r=[], w=[("adw", s)], dq=("adw", s))
                b = self.bank("mm")
                for kc in range(8):
                    self.mm(self.ps[b][0:2, :], self.sc_b[:, kc, :], slots[s][:, kc, :], kc == 0, kc == 7,
                            r=[("adw", s), "scb"], w=[("ps", b)])
                self.cp("act", mtok[0:2, j * 512:(j + 1) * 512], self.ps[b][0:2, :], r=[("ps", b)], w=[("mtok", j)])
            b = self.bank("mm")
            for c in range(48):
                self.mm(self.ps[b][:, 2 * c:2 * c + 2], mtok[0:3, c * 128:(c + 1) * 128], rm, True, True,
                        r=[("mtok", c // 4), ("mtok", "b"), "cst"], w=[("ps", b)])
            self.cp("dve", self.modt[:, l, :, :], self.ps[b][:, 0:96].rearrange("p (c k) -> p c k", c=48, k=2),
                    r=[("ps", b)], w=["modt"])
        self.P.barrier()

    def layer_scalars(self, l):
        gmix = self.cst("gmix").rearrange("p (l c) -> p l c", l=4, c=8)[:, l, :]
        gffn = self.cst("gffn").rearrange("p (l c) -> p l c", l=4, c=8)[:, l, :]
        for col in range(2):
            mv = self.modt[:, l, :, col]
            L = self.lsc[:, col, :, :]
            self.stt(L[:, 0, :], mv[:, 8:16], 1.0, gmix, ALU.add, ALU.mult, r=["modt", "cst"], w=["lsc"])
            self.cp("dve", L[:, 1, :], mv[:, 0:8], r=["modt"], w=["lsc"])
            self.cp("dve", L[:, 2, :], mv[:, 16:24], r=["modt"], w=["lsc"])
            self.stt(L[:, 3, :], mv[:, 32:40], 1.0, gffn, ALU.add, ALU.mult, r=["modt", "cst"], w=["lsc"])
            self.cp("dve", L[:, 4, :], mv[:, 24:32], r=["modt"], w=["lsc"])
            self.cp("dve", L[:, 5, :], mv[:, 40:48], r=["modt"], w=["lsc"])

    def alloc_norm_tmp(self):
        A = self.A
        self.nm_sq = A.alloc([8, 128], BF16)
        self.nm_rstd = A.alloc([128], F32)
        self.nm_tmp = A.alloc([8, 128], F32)

    def normmod(self, t, which, dst, dstkey):
        xv = self.xT[:, :, t * 128:(t + 1) * 128]
        xk = [("xT", t, c) for c in range(8)]
        grp = 0 if t < 8 else 1
        G = self.lsc[:, grp, 3 * which, :]
        SH = self.lsc[:, grp, 3 * which + 1, :]
        self.act(self.nm_sq, xv, AF.Square, r=xk, w=["nm_sq"])
        b = self.bank("tr")
        for c in range(8):
            self.mm(self.ps[b][:, 0:128], self.ones_b, self.nm_sq[:, c, :], c == 0, c == 7, r=["nm_sq", "ones"], w=[("ps", b)])
        self.rstd(self.nm_rstd, self.ps[b][:, 0:128], D, r=[("ps", b), "small_c"], w=["nm_rstd"])
        self.tt("dve", self.nm_tmp, xv, self.nm_rstd.unsqueeze(1).broadcast_to([128, 8, 128]), ALU.mult,
                r=xk + ["nm_rstd"], w=["nm_tmp"])
        self.tt("pool", self.nm_tmp, self.nm_tmp, G.unsqueeze(2).broadcast_to([128, 8, 128]), ALU.mult,
                r=["nm_tmp", "lsc"], w=["nm_tmp"])
        self.tt("dve", dst, self.nm_tmp, SH.unsqueeze(2).broadcast_to([128, 8, 128]), ALU.add,
                r=["nm_tmp", "lsc"], w=[dstkey])

    def run_all(self):
        self.seqs = [dict(tiles=[2 * s, 2 * s + 1], ctx=False, rope=False, bidx=s) for s in range(4)]
        self.sseg = dict(tiles=[8, 9], ctx=True, rope=True, bidx=None)
        A = self.A
        A.reset()
        xin = [A.alloc([1024], F32) for _ in range(2)]
        for t in range(NTILE):
            s = t % 2
            src = self.I["xp"][t * 128:(t + 1) * 128, :] if t < 8 else self.I["xs"][(t - 8) * 128:(t - 7) * 128, :]
            self.dma("sp", xin[s], src, r=[], w=[("xin", s)], dq=("xin", s))
            for hb in range(2):
                b = self.bank("tr")
                for cc in range(4):
                    c = hb * 4 + cc
                    self.tr(self.ps[b][:, cc * 128:(cc + 1) * 128], xin[s][:, c * 128:(c + 1) * 128], self.ident_f,
                            r=[("xin", s), "cst"], w=[("ps", b)])
                self.cp("dve" if hb else "act", self.xT[:, hb * 4:hb * 4 + 4, t * 128:(t + 1) * 128],
                        self.ps[b][:, :].rearrange("p (a b) -> p a b", a=4),
                        r=[("ps", b)], w=[("xT", t, hb * 4 + cc) for cc in range(4)])
        self.P.barrier()
        for l in range(4):
            self.layer_scalars(l)
            if l % 2 == 0:
                if "mixab" in self.stages:
                    self.mixer_ab(l, l // 2)
            else:
                if "mixc" in self.stages:
                    self.mixer_c(l, l // 2)
            self.P.barrier()
            if "ffn" in self.stages:
                self.ffn(l)
            self.P.barrier()
        A.reset()
        yo = [A.alloc([1024], F32) for _ in range(2)]
        for t in range(NTILE):
            s = t % 2
            for hb in range(2):
                b = self.bank("tr")
                for cc in range(4):
                    c = hb * 4 + cc
                    self.tr(self.ps[b][:, cc * 128:(cc + 1) * 128], self.xT[:, c, t * 128:(t + 1) * 128], self.ident_f,
                            r=[("xT", t, c), "cst"], w=[("ps", b)])
                self.cp("dve" if hb else "act", yo[s][:, hb * 512:(hb + 1) * 512], self.ps[b][:, :],
                        r=[("ps", b)], w=[("yo", s, hb)])
            dst = self.O["yp"][t * 128:(t + 1) * 128, :] if t < 8 else self.O["ys"][(t - 8) * 128:(t - 7) * 128, :]
            self.dma("sp", dst, yo[s], r=[("yo", s, 0), ("yo", s, 1)], w=[], dq=("yo", s))

    def ffn(self, l):
        A = self.A
        A.reset()
        hT = A.alloc([8, NTOK], BF16)
        actT = A.alloc([22, NTOK], BF16)
        wi = [A.alloc([8, 2, 256], BF16) for _ in range(2)]
        wo = [A.alloc([11, 1024], BF16) for _ in range(2)]
        sg = [A.alloc([512], F32) for _ in range(2)]
        self.alloc_norm_tmp()
        W1 = self.I["ffn_w_in"]
        W2 = self.I["ffn_w_out"]
        TB = [(0, 512, 0, [0, 1, 2, 3]), (512, 512, 0, [4, 5, 6, 7]), (1024, 256, 1, [8, 9])]
        NS = len(wi)

        def load_wi(j2):
            s = j2 % NS
            for gu in range(2):
                c0 = gu * FH + j2 * 256
                src = W1[l, :, c0:c0 + 256].rearrange("(k p) n -> p k n", p=128)
                self.dma("pool", wi[s][:, :, gu, :], src, r=[], w=[("wi", s, gu)], dq=("wi", s))

        def load_wo(hf):
            src = W2[l, hf * 1408:(hf + 1) * 1408, :].rearrange("(j p) n -> p j n", p=128)
            self.dma("pool", wo[hf], src, r=[], w=[("wo", hf)], dq=("wo", hf))

        for j2 in range(NS):
            load_wi(j2)
        for t in range(NTILE):
            self.normmod(t, 1, hT[:, :, t * 128:(t + 1) * 128], ("hT", t))
        load_wo(0)
        load_wo(1)
        k = 0
        for j2 in range(11):
            s = j2 % NS
            for (t0, tn, grp, tl) in TB:
                hk = [("hT", q) for q in tl]
                for hf in range(2):
                    j = j2 * 2 + hf
                    bg = self.bank("mm")
                    for kc in range(8):
                        self.mm(self.ps[bg][:, 0:tn], wi[s][:, kc, 0, hf * 128:(hf + 1) * 128], hT[:, kc, t0:t0 + tn],
                                kc == 0, kc == 7, r=[("wi", s, 0)] + hk, w=[("ps", bg)])
                    bu = self.bank("mm")
                    for kc in range(8):
                        self.mm(self.ps[bu][:, 0:tn], wi[s][:, kc, 1, hf * 128:(hf + 1) * 128], hT[:, kc, t0:t0 + tn],
                                kc == 0, kc == 7, r=[("wi", s, 1)] + hk, w=[("ps", bu)])
                    sgi = k % 2
                    k += 1
                    self.act(sg[sgi][:, 0:tn], self.ps[bg][:, 0:tn], AF.Silu, r=[("ps", bg)], w=[("sg", sgi)])
                    self.tt("dve", actT[:, j, t0:t0 + tn], sg[sgi][:, 0:tn], self.ps[bu][:, 0:tn], ALU.mult,
                            r=[("sg", sgi), ("ps", bu)], w=[("act", j, t0)])
            if j2 + NS < 11:
                load_wi(j2 + NS)
        for hf in range(2):
            for c in range(8):
                for (t0, tn, grp, tl) in TB:
                    gate = self.lsc[:, grp, 5, :]
                    b = self.bank("mm")
                    for jj in range(11):
                        self.mm(self.ps[b][:, 0:tn], wo[hf][:, jj, c * 128:(c + 1) * 128], actT[:, hf * 11 + jj, t0:t0 + tn],
                                jj == 0, jj == 10, r=[("wo", hf), ("act", hf * 11 + jj, t0)], w=[("ps", b)])
                    xv = self.xT[:, c, t0:t0 + tn]
                    xk = [("xT", q, c) for q in tl]
                    self.stt(xv, self.ps[b][:, 0:tn], gate[:, c:c + 1], xv, ALU.mult, ALU.add, r=[("ps", b), "lsc"] + xk, w=xk)

    def mixer_residual(self, t, banks):
        grp = 0 if t < 8 else 1
        gate = self.lsc[:, grp, 2, :]
        tmp = self.nm_tmp
        for hb in range(2):
            b = banks[hb]
            pv = self.ps[b][:, :].rearrange("p (a b) -> p a b", a=4)
            tv = tmp[:, hb * 4:hb * 4 + 4, :]
            self.tt("dve", tv, pv, gate[:, hb * 4:hb * 4 + 4].unsqueeze(2).broadcast_to([128, 4, 128]), ALU.mult,
                    r=[("ps", b), "lsc"], w=["nm_tmp"])
            xv = self.xT[:, hb * 4:hb * 4 + 4, t * 128:(t + 1) * 128]
            xk = [("xT", t, hb * 4 + q) for q in range(4)]
            self.tt("pool", xv, xv, tv, ALU.add, r=["nm_tmp"] + xk, w=xk)

    def rope(self, xv, H, Q, cs, tmps, r, w, keyp):
        x1 = xv[:, :, :, 0, :]
        x2 = xv[:, :, :, 1, :]
        c = cs[:, 0, :].rearrange("p (a q) -> p a q", a=2, q=Q).unsqueeze(1).broadcast_to([128, H, 2, Q])
        s = cs[:, 1, :].rearrange("p (a q) -> p a q", a=2, q=Q).unsqueeze(1).broadcast_to([128, H, 2, Q])
        t1, t2, t3, t4 = tmps
        k1, k2, k3, k4 = [(keyp, i) for i in range(4)]
        self.tt("dve", t1, x1, c, ALU.mult, r=r, w=[k1])
        self.tt("pool", t2, x2, s, ALU.mult, r=r, w=[k2])
        self.tt("dve", t3, x1, s, ALU.mult, r=r, w=[k3])
        self.tt("pool", t4, x2, c, ALU.mult, r=r, w=[k4])
        self.tt("dve", x1, t1, t2, ALU.subtract, r=[k1, k2, k3], w=w)
        self.tt("dve", x2, t3, t4, ALU.add, r=[k3, k4], w=w)

    def attention(self, KT, kparts, kt_list, QTv, nq, VA_of, scale, Pt, out_of, kr, qr, vr, ow):
        ob = self.bank("acc")
        first, last = kt_list[0], kt_list[-1]
        npt = len(Pt)

        def pv(st, pi):
            for j in range(nq):
                self.mm(self.ps[ob][:, j * 65:(j + 1) * 65], Pt[pi][:, j, :], VA_of(st), (st == first and j == 0), st == last,
                        r=[("Pt", pi)] + vr, w=[("ps", ob)], skip_group_check=True)

        pend = None
        for st in kt_list:
            sb = self.bank("mm")
            self.mm(self.ps[sb][:, 0:nq * 128], KT(st), QTv, True, True, r=kr + qr, w=[("ps", sb)])
            pi = self.pt_i % npt
            self.pt_i += 1
            self.act(Pt[pi][:, 0:nq, :], self.ps[sb][:, 0:nq * 128].rearrange("p (a b) -> p a b", a=nq), AF.Exp,
                     r=[("ps", sb)], w=[("Pt", pi)], scale=scale)
            if pend is not None:
                pv(*pend)
            pend = (st, pi)
        pv(*pend)
        ov = self.ps[ob][:, 0:nq * 65].rearrange("p (a b) -> p a b", a=nq)
        rd = self.at_rden
        self.recip(rd[:, 0:nq], ov[:, :, 64], r=[("ps", ob)], w=["at_rden"])
        for j in range(nq):
            dst, dk = out_of(j)
            self.P.op("dve", lambda h, j=j, dst=dst, rd=rd, ov=ov: h.tensor_scalar(dst, ov[:, j, 0:64], rd[:, j:j + 1], None, ALU.mult),
                      r=[("ps", ob), "at_rden"], w=[dk])

    def mixer_c(self, l, i):
        A = self.A
        A.reset()
        I, O = self.I, self.O
        w_in = A.alloc([8, 1536], BF16)
        w_out = A.alloc([8, 1024], BF16)
        hT = A.alloc([8, 256], BF16)
        KT = A.alloc([4, 1536], BF16)
        VA = A.alloc([12, 4, 65], BF16)
        self.alloc_norm_tmp()
        kvf = A.alloc([512], F32)
        sq = A.alloc([1024], F32)
        qn = A.alloc([1024], F32)
        st16 = A.alloc([16], F32)
        knb = A.alloc([256], BF16)
        qnb = A.alloc([1024], BF16)
        QT = A.alloc([16, 128], BF16)
        Pt = [A.alloc([4, 128], BF16) for _ in range(3)]
        ob = A.alloc([1024], BF16)
        OT = A.alloc([8, 128], BF16)
        self.at_rden = A.alloc([4], F32)
        rtm = [A.alloc([16, 2, 16], F32) for _ in range(4)]
        ropeg = A.alloc([2, 2, 32], F32)
        ck = A.alloc([4, 256], F32)
        cv = A.alloc([4, 256], F32)
        ckb = A.alloc([4, 256], BF16)
        kg = [A.alloc([512], F32) for _ in range(2)]
        self.pt_i = 0
        gq = self.cst("gq%d" % i)
        gk = self.cst("gk%d" % i)
        for kh in range(2):
            self.dma("pool", w_in[:, kh * 4:(kh + 1) * 4, :],
                     I["gqa_w_in"][i, kh * 512:(kh + 1) * 512, :].rearrange("(k p) n -> p k n", p=128), r=[], w=[("w_in", kh)], dq="w_in")
        self.dma("pool", w_out, I["gqa_w_out"][i].rearrange("(k p) n -> p k n", p=128), r=[], w=["w_out"], dq="w_out")
        self.memset("pool", VA[:, :, :, 64:65], 1.0, w=["VA1"])
        wk = [("w_in", 0), ("w_in", 1)]
        self.dma("sp", ropeg, I["ropeg"].rearrange("(t p) c q -> p t c q", p=128), r=[], w=["ropeg"], dq="ropeg")
        cc_src, cc_dst = self.CC[(l, "c")]

        def put_keys(knb_ap, kt, rk):
            b = self.bank("tr")
            for g in range(4):
                self.tr(self.psb[b][0:64, g * 128:(g + 1) * 128], knb_ap[:, g * 64:(g + 1) * 64], self.ident_b,
                        r=rk + ["identb"], w=[("ps", b)])
            self.cp("dve", KT[0:64, :, kt * 128:(kt + 1) * 128], self.psb[b][0:64, 0:512].rearrange("p (a b) -> p a b", a=4),
                    r=[("ps", b)], w=[("KT", kt)])

        def kv_side(t, n, rope_n):
            self.normmod(t, 0, hT[:, :, n * 128:(n + 1) * 128], ("hT", n))
            b = self.bank("mm")
            for kc in range(8):
                self.mm(self.ps[b][:, :], hT[:, kc, n * 128:(n + 1) * 128], w_in[:, kc, 1024:1536], kc == 0, kc == 7,
                        r=[("hT", n)] + wk, w=[("ps", b)])
            self.cp("act", kvf, self.ps[b][:, :], r=[("ps", b)], w=["kvf"])
            self.act(sq[:, 0:256], self.ps[b][:, 0:256], AF.Square, r=[("ps", b)], w=["sq"])
            self.red(st16[:, 0:4], sq[:, 0:256].rearrange("p (g d) -> p g d", g=4), r=["sq"], w=["st16"])
            self.rstd(st16[:, 0:4], st16[:, 0:4], 64, r=["st16"], w=["st16"])
            kv3 = kvf[:, 0:256].rearrange("p (g d) -> p g d", g=4)
            self.tt("dve", kv3, kv3, st16[:, 0:4].unsqueeze(2).broadcast_to([128, 4, 64]), ALU.mult, r=["kvf", "st16"], w=["kvf"])
            self.tt("dve", kv3, kv3, gk.unsqueeze(1).broadcast_to([128, 4, 64]), ALU.mult, r=["kvf", "cst"], w=["kvf"])
            if rope_n is not None:
                self.rope(kvf[:, 0:256].rearrange("p (h a b q) -> p h a b q", h=4, a=2, b=2, q=16), 4, 16, ropeg[:, rope_n, :, :],
                          [x[:, 0:4, :, :] for x in rtm], r=["kvf", "ropeg"], w=["kvf"], keyp="rtm")

        def q_attn_out(t, n, rope_n, kt_list):
            kkeys = [("KT", kt) for kt in kt_list]
            vkeys = [("VA", kt) for kt in kt_list] + ["VA1"]
            qb = []
            for bk in range(2):
                b = self.bank("mm")
                qb.append(b)
                for kc in range(8):
                    self.mm(self.ps[b][:, :], hT[:, kc, n * 128:(n + 1) * 128], w_in[:, kc, bk * 512:(bk + 1) * 512], kc == 0, kc == 7,
                            r=[("hT", n)] + wk, w=[("ps", b)])
                self.act(sq[:, bk * 512:(bk + 1) * 512], self.ps[b][:, :], AF.Square, r=[("ps", b)], w=["sq"])
            self.red(st16, sq.rearrange("p (g d) -> p g d", g=16), r=["sq"], w=["st16"])
            self.rstd(st16, st16, 64, r=["st16"], w=["st16"])
            for bk in range(2):
                self.tt("dve", qn[:, bk * 512:(bk + 1) * 512].rearrange("p (g d) -> p g d", g=8),
                        self.ps[qb[bk]][:, :].rearrange("p (g d) -> p g d", g=8),
                        st16[:, bk * 8:(bk + 1) * 8].unsqueeze(2).broadcast_to([128, 8, 64]), ALU.mult,
                        r=[("ps", qb[bk]), "st16"], w=["qn"])
            qn3 = qn.rearrange("p (g d) -> p g d", g=16)
            self.tt("pool", qn3, qn3, gq.unsqueeze(1).broadcast_to([128, 16, 64]), ALU.mult, r=["qn", "cst"], w=["qn"])
            if rope_n is not None:
                self.rope(qn.rearrange("p (h a b q) -> p h a b q", h=16, a=2, b=2, q=16), 16, 16, ropeg[:, rope_n, :, :], rtm,
                          r=["qn", "ropeg"], w=["qn"], keyp="rtm")
            self.cp("act", qnb, qn, r=["qn"], w=["qnb"])
            for hb in range(2):
                b = self.bank("tr")
                for hh in range(8):
                    h_ = hb * 8 + hh
                    self.tr(self.psb[b][0:64, hh * 128:(hh + 1) * 128], qnb[:, h_ * 64:(h_ + 1) * 64], self.ident_b,
                            r=["qnb", "identb"], w=[("ps", b)])
                self.cp("dve" if hb else "act", QT[0:64, hb * 8:(hb + 1) * 8, :],
                        self.psb[b][0:64, :].rearrange("p (a b) -> p a b", a=8), r=[("ps", b)], w=[("QT", hb)])
            for g in range(4):
                self.attention(
                    KT=lambda st, g=g: KT[0:64, g, st * 128:(st + 1) * 128], kparts=64, kt_list=kt_list,
                    QTv=QT[0:64, 4 * g:4 * g + 4, :], nq=4,
                    VA_of=lambda st, g=g: VA[:, st, g, :], scale=0.125, Pt=Pt,
                    out_of=lambda j, g=g: (ob[:, (4 * g + j) * 64:(4 * g + j + 1) * 64], ("ob", g)),
                    kr=kkeys, qr=[("QT", g // 2)], vr=vkeys, ow=None)
            b = self.bank("tr")
            for c in range(8):
                self.tr(self.psb[b][:, c * 128:(c + 1) * 128], ob[:, c * 128:(c + 1) * 128], self.ident_b,
                        r=[("ob", c // 2), "identb"], w=[("ps", b)])
            self.cp("act", OT, self.psb[b][:, :].rearrange("p (a b) -> p a b", a=8), r=[("ps", b)], w=["OT"])
            banks = []
            for hb in range(2):
                b = self.bank("mm")
                banks.append(b)
                for cc in range(4):
                    c = hb * 4 + cc
                    for kc in range(8):
                        self.mm(self.ps[b][:, cc * 128:(cc + 1) * 128], w_out[:, kc, c * 128:(c + 1) * 128], OT[:, kc, :], kc == 0, kc == 7,
                                r=["w_out", "OT"], w=[("ps", b)])
            self.mixer_residual(t, banks)

        cck = "cc_src_c%d" % l
        ccd = "cc_dst_c%d" % l
        for n, t in enumerate(self.sseg["tiles"]):
            kv_side(t, n, n)
            self.dma("sp", cc_src[n * 128:(n + 1) * 128, :], kvf, r=["kvf"], w=[cck], dq="ccb_c")
        self.allgather((l, "c"), r=[cck], w=[ccd])
        for sq_ in self.seqs:
            tiles = sq_["tiles"]
            bi = sq_["bidx"]
            for n, t in enumerate(tiles):
                kv_side(t, n, None)
                self.dma("sp", O["o_gk"][bi, i, n * 128:(n + 1) * 128, :], kvf[:, 0:256], r=["kvf"], w=[], dq="kvf_o")
                self.dma("sp", O["o_gv"][bi, i, n * 128:(n + 1) * 128, :], kvf[:, 256:512], r=["kvf"], w=[], dq="kvf_o")
                self.cp("act", knb, kvf[:, 0:256], r=["kvf"], w=["knb"])
                put_keys(knb, n, ["knb"])
                self.cp("pool", VA[:, n, :, 0:64], kvf[:, 256:512].rearrange("p (g d) -> p g d", g=4), r=["kvf"], w=[("VA", n)])
            for n, t in enumerate(tiles):
                q_attn_out(t, n, None, [0, 1])
        self.dma("sp", ck, I["c_gk"][i].rearrange("(t p) n -> p t n", p=128), r=[], w=["ck"], dq="ck")
        self.dma("sp", cv, I["c_gv"][i].rearrange("(t p) n -> p t n", p=128), r=[], w=["cv"], dq="cv")
        self.cp("act", ckb, ck, r=["ck"], w=["ckb"])
        for kt in range(4):
            put_keys(ckb[:, kt, :], kt, ["ckb"])
            self.cp("pool", VA[:, kt, :, 0:64], cv[:, kt, :].rearrange("p (g d) -> p g d", g=4), r=["cv"], w=[("VA", kt)])
        for k in range(8):
            kgk = kg[k % 2]
            kk = ("kg", k % 2)
            self.dma("sp", kgk, cc_dst[k * 128:(k + 1) * 128, :], r=[ccd], w=[kk], dq=kk)
            self.cp("act", knb, kgk[:, 0:256], r=[kk], w=["knb"])
            put_keys(knb, 4 + k, ["knb"])
            self.cp("pool", VA[:, 4 + k, :, 0:64], kgk[:, 256:512].rearrange("p (g d) -> p g d", g=4), r=[kk], w=[("VA", 4 + k)])
        for n, t in enumerate(self.sseg["tiles"]):
            self.normmod(t, 0, hT[:, :, n * 128:(n + 1) * 128], ("hT", n))
            q_attn_out(t, n, n, list(range(12)))

    def mixer_ab(self, l, i):
        A = self.A
        A.reset()
        I, O = self.I, self.O
        w_in = A.alloc([8, 2240], BF16)
        w_out = A.alloc([8, 1024], BF16)
        OG = A.alloc([4, NTOK], BF16)
        hTm = A.alloc([8, 128], BF16)
        hk = "hTm"
        self.alloc_norm_tmp()
        self.at_rden = A.alloc([4], F32)
        Pt = [A.alloc([4, 128], BF16) for _ in range(3)]
        self.pt_i = 0
        SP = dict(qlT=[A.alloc([2, 256], BF16) for _ in range(2)], oT=A.alloc([4, 256], F32), rsT=A.alloc([4, 256], BF16),
                  etot=A.alloc([4, 2], F32), cqb=A.alloc([2, 384], BF16), aseg=A.alloc([4], F32))
        base_off = A.off
        for kh in range(2):
            for ch in range(2):
                self.dma("pool", w_in[:, kh * 4:(kh + 1) * 4, ch * 1120:(ch + 1) * 1120],
                         I["ab_w_in"][i, kh * 512:(kh + 1) * 512, ch * 1120:(ch + 1) * 1120].rearrange("(k p) n -> p k n", p=128),
                         r=[], w=[("w_in", kh, ch)], dq="w_in")
        self.dma("pool", w_out, I["ab_w_out"][i].rearrange("(k p) n -> p k n", p=128), r=[], w=["w_out"], dq="w_out")
        wk = [("w_in", 0, 0), ("w_in", 0, 1), ("w_in", 1, 0), ("w_in", 1, 1)]
        gout = self.cst("gout%d" % i)
        gqn = self.cst("gqn%d" % i)
        gkvn = self.cst("gkvn%d" % i)
        gq96 = self.cst("gq96%d" % i)
        gk96 = self.cst("gk96%d" % i)
        sel = self.cst("sel")
        trim = [self.cst("trim0"), self.cst("trim1")]
        tris = [self.cst("tris0"), self.cst("tris1")]
        mask = [self.cst("mask0"), self.cst("mask1")]
        ccg_src, ccg_dst = self.CC[(l, "g")]
        ccm_src, ccm_dst = self.CC[(l, "m")]

        def g_alloc(NT, persist=None):
            T = NT * 128
            B = {}
            if persist is None:
                B["qlT"] = [A.alloc([2, T], BF16) for _ in range(2)]
                B["oT"] = A.alloc([4, T], F32)
                B["rsT"] = A.alloc([4, T], BF16)
                B["etot"] = A.alloc([4, NT], F32)
            else:
                for k in ("qlT", "oT", "rsT", "etot"):
                    B[k] = persist[k]
            B["klT"] = [A.alloc([2, T], BF16) for _ in range(2)]
            B["kst"] = [A.alloc([NT, 256], BF16) for _ in range(2)]
            B["vtk"] = A.alloc([NT, 512], BF16)
            B["Sst"] = [[A.alloc([128], F32) for _ in range(2)] for _ in range(2)]
            B["Sbf"] = [[A.alloc([128], BF16) for _ in range(2)] for _ in range(2)]
            B["alT"] = A.alloc([128], F32)
            B["lsp"] = A.alloc([512], F32)
            B["Eb"] = A.alloc([4, 128], F32)
            B["Enb"] = A.alloc([4, 128], F32)
            B["Ed2"] = A.alloc([512], F32)
            B["ATm"] = [A.alloc([128], BF16) for _ in range(2)]
            B["aw2"] = A.alloc([512], F32)
            self.memset("dve", B["alT"][32:33, :], 1.0, w=["alT1"])
            aw2 = B["aw2"]
            self.memset("dve", aw2[0:33, :], 0.0, w=["aw2"])
            for z in range(2):
                self.dma("sp", aw2[16 * z:16 * z + 16, z * 256:(z + 1) * 256], I["a_w2"][i, z], r=[], w=["aw2"], dq="aw2")
            self.dma("sp", aw2[32:33, :], I["a_b"][i:i + 1].rearrange("o z n -> o (z n)"), r=[], w=["aw2"], dq="aw2")
            return B

        def g_prep(tiles, B):
            qlT, klT, kst, vtk, rsT, etot = B["qlT"], B["klT"], B["kst"], B["vtk"], B["rsT"], B["etot"]
            alT, lsp, Eb, Enb, Ed2, aw2 = B["alT"], B["lsp"], B["Eb"], B["Enb"], B["Ed2"], B["aw2"]
            for n, t in enumerate(tiles):
                nc_ = slice(n * 128, (n + 1) * 128)
                h = hTm
                self.normmod(t, 0, h, hk)
                bqk = self.bank("mm")
                for ch in range(4):
                    for kc in range(8):
                        self.mm(self.ps[bqk][:, ch * 128:(ch + 1) * 128], w_in[:, kc, ch * 128:(ch + 1) * 128], h[:, kc, :], kc == 0, kc == 7,
                                r=[hk] + wk, w=[("ps", bqk)])
                br = self.bank("mm")
                for ch in range(4):
                    for kc in range(8):
                        self.mm(self.ps[br][:, ch * 128:(ch + 1) * 128], w_in[:, kc, 1024 + ch * 128:1024 + (ch + 1) * 128], h[:, kc, :], kc == 0, kc == 7,
                                r=[hk] + wk, w=[("ps", br)])
                self.act(rsT[:, :, nc_], self.ps[br][:, :].rearrange("p (a b) -> p a b", a=4), AF.Silu, r=[("ps", br)], w=[("rsT", n)])
                ba = self.bank("tr")
                for kc in range(8):
                    self.mm(self.ps[ba][0:32, 0:128], w_in[:, kc, 1536:1568], h[:, kc, :], kc == 0, kc == 7, r=[hk] + wk, w=[("ps", ba)])
                self.cp("act", alT[0:32, :], self.ps[ba][0:32, 0:128], r=[("ps", ba)], w=["alT"])
                bkv = self.bank("mm")
                for kc in range(8):
                    self.mm(self.ps[bkv][:, :], h[:, kc, :], w_in[:, kc, 256:768], kc == 0, kc == 7, r=[hk] + wk, w=[("ps", bkv)])
                bv2 = self.bank("mm")
                for kc in range(8):
                    self.mm(self.ps[bv2][:, 0:256], h[:, kc, :], w_in[:, kc, 768:1024], kc == 0, kc == 7, r=[hk] + wk, w=[("ps", bv2)])
                self.cp("act", vtk[:, n, 0:256], self.ps[bkv][:, 256:512], r=[("ps", bkv)], w=[("vtk", n, 0)])
                self.cp("act", vtk[:, n, 256:512], self.ps[bv2][:, 0:256], r=[("ps", bv2)], w=[("vtk", n, 1)])
                bl = self.bank("tr")
                self.mm(self.ps[bl][:, :], alT[0:33, :], aw2[0:33, :], True, True, r=["alT", "alT1", "aw2"], w=[("ps", bl)])
                self.act(lsp, self.ps[bl][:, :], AF.Exp, r=[("ps", bl)], w=["lsp"], scale=-1.0)
                self.act(lsp, lsp, AF.Ln, r=["lsp"], w=["lsp"], bias=1.0)
                bb = self.bank("tr")
                for z in range(2):
                    for fc in range(2):
                        zf = z * 2 + fc
                        self.mm(self.ps[bb][:, zf * 128:(zf + 1) * 128], lsp[:, zf * 128:(zf + 1) * 128], trim[z], True, True,
                                r=["lsp", "cst"], w=[("ps", bb)])
                bd = self.bank("tr")
                for z in range(2):
                    self.mm(self.ps[bd][:, z * 256:(z + 1) * 256], tris[z], lsp[:, z * 256:(z + 1) * 256], True, True,
                            r=["lsp", "cst"], w=[("ps", bd)])
                pbb = self.ps[bb][:, :].rearrange("p (a b) -> p a b", a=4)
                self.act(Eb, pbb, AF.Exp, r=[("ps", bb)], w=["Eb"])
                self.act(Enb, pbb, AF.Exp, r=[("ps", bb)], w=["Enb"], scale=-1.0)
                self.act(Ed2, self.ps[bd][:, :], AF.Exp, r=[("ps", bd)], w=["Ed2"])
                self.cp("pool", etot[:, 0:2, n], Eb[:, 0:2, 127], r=["Eb"], w=[("etot", n)])
                self.cp("pool", etot[:, 2:4, n], Eb[:, 2:4, 0], r=["Eb"], w=[("etot", n)])
                pqk = self.ps[bqk][:, :].rearrange("p (a b) -> p a b", a=4)
                for z in range(2):
                    self.stt(qlT[z][:, :, nc_], pqk[:, 0:2, :], 0.125, Eb[:, 2 * z:2 * z + 2, :], ALU.mult, ALU.mult,
                             r=[("ps", bqk), "Eb"], w=[("qlT", z, n)])
                    self.tt("dve", klT[z][:, :, nc_], pqk[:, 2:4, :], Enb[:, 2 * z:2 * z + 2, :], ALU.mult,
                            r=[("ps", bqk), "Enb"], w=[("klT", z, n)])
                    self.tt("dve", kst[z][:, n, :], self.ps[bkv][:, 0:256], Ed2[:, z * 256:(z + 1) * 256], ALU.mult,
                            r=[("ps", bkv), "Ed2"], w=[("kst", z, n)])

        def g_scan(NT, B, bidx):
            qlT, klT, kst, vtk, oT, etot, Sst, Sbf, ATm = (B[k] for k in ("qlT", "klT", "kst", "vtk", "oT", "etot", "Sst", "Sbf", "ATm"))
            for z in range(2):
                for fc in range(2):
                    self.memset("pool", Sst[z][fc], 0.0, w=[("S", z, fc)])
                    self.cp("act", Sbf[z][fc], Sst[z][fc], r=[("S", z, fc)], w=[("Sbf", z, fc)])
                order = list(range(NT)) if z == 0 else list(range(NT - 1, -1, -1))
                ai = 0
                for n in order:
                    nc_ = slice(n * 128, (n + 1) * 128)
                    bo = self.bank("acc")
                    for hh in range(4):
                        fc, hp = hh // 2, hh % 2
                        pr = slice(hp * 64, (hp + 1) * 64)
                        bat = self.bank("mm")
                        self.mm(self.ps[bat][:, 0:128], klT[z][pr, fc, nc_], qlT[z][pr, fc, nc_], True, True,
                                r=[("klT", z, n), ("qlT", z, n)], w=[("ps", bat)])
                        am = ATm[ai % 2]
                        amk = ("ATm", ai % 2)
                        ai += 1
                        self.tt("dve", am, self.ps[bat][:, 0:128], mask[z], ALU.mult, r=[("ps", bat), "cst"], w=[amk])
                        self.mm(self.ps[bo][:, hh * 128:(hh + 1) * 128], vtk[:, n, hh * 128:(hh + 1) * 128], am, True, False,
                                r=[("vtk", n, 0), ("vtk", n, 1), amk], w=[("ps", bo)])
                        self.mm(self.ps[bo][:, hh * 128:(hh + 1) * 128], Sbf[z][fc][pr, :], qlT[z][pr, fc, nc_], False, True,
                                r=[("Sbf", z, fc), ("qlT", z, n)], w=[("ps", bo)])
                    pbo = self.ps[bo][:, :].rearrange("p (a b) -> p a b", a=4)
                    if z == 0:
                        self.cp("act", oT[:, :, nc_], pbo, r=[("ps", bo)], w=[("oT", n)])
                    else:
                        self.tt("dve", oT[:, :, nc_], oT[:, :, nc_], pbo, ALU.add, r=[("ps", bo), ("oT", n)], w=[("oT", n)])
                    for fc in range(2):
                        bu = self.bank("mm")
                        for hp in range(2):
                            hh = fc * 2 + hp
                            self.mm(self.ps[bu][hp * 64:(hp + 1) * 64, 0:128], kst[z][:, n, hh * 64:(hh + 1) * 64],
                                    vtk[:, n, hh * 128:(hh + 1) * 128], True, True,
                                    r=[("kst", z, n), ("vtk", n, 0), ("vtk", n, 1)], w=[("ps", bu)])
                        self.stt(Sst[z][fc], Sst[z][fc], etot[:, z * 2 + fc, n:n + 1], self.ps[bu][:, 0:128], ALU.mult, ALU.add,
                                 r=[("S", z, fc), ("etot", n), ("ps", bu)], w=[("S", z, fc)])
                        self.cp("act", Sbf[z][fc], Sst[z][fc], r=[("S", z, fc)], w=[("Sbf", z, fc)])
                if bidx is not None:
                    for fc in range(2):
                        self.dma("sp", O["o_gla"][bidx, i, z, fc * 128:(fc + 1) * 128, :], Sst[z][fc],
                                 r=[("S", z, fc)], w=[], dq=("S", z, fc))

        def g_out(tiles, oT, rsT):
            osq = A.alloc([4, 128], BF16)
            orst = A.alloc([4, 128], F32)
            otmp = A.alloc([4, 128], F32)
            for n, t in enumerate(tiles):
                nc_ = slice(n * 128, (n + 1) * 128)
                self.act(osq, oT[:, :, nc_], AF.Square, r=[("oT", n)], w=["osq"])
                b = self.bank("tr")
                self.mm(self.ps[b][:, :], self.ones_b, osq.rearrange("p a b -> p (a b)"), True, True, r=["osq", "ones"], w=[("ps", b)])
                self.rstd(orst, self.ps[b][:, :].rearrange("p (a b) -> p a b", a=4), 128, r=[("ps", b)], w=["orst"])
                self.tt("dve", otmp, oT[:, :, nc_], orst, ALU.mult, r=[("oT", n), "orst"], w=["otmp"])
                self.stt(OG[:, :, t * 128:(t + 1) * 128], otmp, gout, rsT[:, :, nc_], ALU.mult, ALU.mult,
                         r=["otmp", "cst", ("rsT", n)], w=[("OG", t)])

        def m_alloc(NK):
            M = {}
            M["w_qb"] = A.alloc([3, 768], BF16)
            M["w_kvb"] = A.alloc([2, 1024], BF16)
            self.dma("pool", M["w_qb"], I["w_qb"][i].rearrange("(k p) n -> p k n", p=128), r=[], w=["w_qb"], dq="w_qb")
            self.dma("pool", M["w_kvb"], I["w_kvb"][i].rearrange("(k p) n -> p k n", p=128), r=[], w=["w_kvb"], dq="w_kvb")
            M["KTm"] = A.alloc([8, NK * 128], BF16)
            M["VAm"] = A.alloc([NK, 8, 65], BF16)
            M["QTm"] = A.alloc([8, 256], BF16)
            M["omb"] = A.alloc([2, 512], BF16)
            M["OM"] = A.alloc([4, 256], BF16)
            M["cb"] = A.alloc([256], BF16)
            M["cT"] = A.alloc([2, 128], BF16)
            M["kc96"] = A.alloc([8, 96], F32)
            M["knb"] = A.alloc([8, 96], BF16)
            M["st8"] = A.alloc([8], F32)
            M["cqT"] = A.alloc([3, 128], BF16)
            M["rtm"] = [A.alloc([8, 2, 8], F32) for _ in range(4)]
            self.memset("pool", M["VAm"][:, :, :, 64:65], 1.0, w=["VA1"])
            return M

        def own_alloc():
            W = {}
            W["sq96"] = A.alloc([8, 96], F32)
            W["ckvf"] = A.alloc([256], F32)
            W["ckvn"] = A.alloc([256], F32)
            W["kpe"] = A.alloc([32], F32)
            W["st1"] = A.alloc([2], F32)
            return W

        def norm96(M, sq96, gain, rope_cs):
            kc96, knb, st8, rtm = M["kc96"], M["knb"], M["st8"], M["rtm"]
            self.act(sq96, kc96, AF.Square, r=["kc96"], w=["sq96"])
            self.red(st8, sq96, r=["sq96"], w=["st8"])
            self.rstd(st8, st8, 96, r=["st8"], w=["st8"])
            self.tt("dve", kc96, kc96, st8.unsqueeze(2).broadcast_to([128, 8, 96]), ALU.mult, r=["kc96", "st8"], w=["kc96"])
            self.tt("pool", kc96, kc96, gain.unsqueeze(1).broadcast_to([128, 8, 96]), ALU.mult, r=["kc96", "cst"], w=["kc96"])
            if rope_cs is not None:
                cs, csk = rope_cs
                self.rope(kc96[:, :, 64:96].rearrange("p h (a b q) -> p h a b q", a=2, b=2, q=8), 8, 8, cs, rtm,
                          r=["kc96", csk], w=["kc96"], keyp="rtm")
            self.cp("act", knb, kc96, r=["kc96"], w=["knb"])

        def kside(M, sq96, ckvn_ap, ckvn_k, kpe_ap, kpe_k, kt, rope_cs):
            cb, cT, kc96, knb, KTm, VAm, w_kvb = M["cb"], M["cT"], M["kc96"], M["knb"], M["KTm"], M["VAm"], M["w_kvb"]
            self.cp("act", cb, ckvn_ap, r=[ckvn_k], w=["cb"])
            b = self.bank("tr")
            for kc in range(2):
                self.tr(self.psb[b][:, kc * 128:(kc + 1) * 128], cb[:, kc * 128:(kc + 1) * 128], self.ident_b, r=["cb", "identb"], w=[("ps", b)])
            self.cp("dve", cT, self.psb[b][:, 0:256].rearrange("p (a b) -> p a b", a=2), r=[("ps", b)], w=["cT"])
            for bk in range(2):
                b = self.bank("mm")
                for kc in range(2):
                    self.mm(self.ps[b][:, :], cT[:, kc, :], w_kvb[:, kc, bk * 512:(bk + 1) * 512], kc == 0, kc == 1,
                            r=["cT", "w_kvb"], w=[("ps", b)])
                pv = self.ps[b][:, :].rearrange("p (h d) -> p h d", h=4)
                self.cp("act", kc96[:, bk * 4:(bk + 1) * 4, 0:64], pv[:, :, 0:64], r=[("ps", b)], w=["kc96"])
                self.cp("dve", VAm[:, kt, bk * 4:(bk + 1) * 4, 0:64], pv[:, :, 64:128], r=[("ps", b)], w=[("VAm", kt)])
            self.cp("pool", kc96[:, :, 64:96], kpe_ap.unsqueeze(1).broadcast_to([128, 8, 32]), r=[kpe_k], w=["kc96"])
            norm96(M, sq96, gk96, rope_cs)
            b = self.bank("tr")
            for hh in range(8):
                self.tr(self.psb[b][0:96, hh * 128:(hh + 1) * 128], knb[:, hh, :], self.ident_b, r=["knb", "identb"], w=[("ps", b)])
            self.cp("dve", KTm[0:96, :, kt * 128:(kt + 1) * 128], self.psb[b][0:96, :].rearrange("p (a b) -> p a b", a=8),
                    r=[("ps", b)], w=[("KTm", kt)])

        def m_own(t, n, W, cqb):
            sq96, ckvf, ckvn, kpe, st1 = W["sq96"], W["ckvf"], W["ckvn"], W["kpe"], W["st1"]
            h = hTm
            self.normmod(t, 0, h, hk)
            b1 = self.bank("mm")
            for kc in range(8):
                self.mm(self.ps[b1][:, :], h[:, kc, :], w_in[:, kc, 1568:2080], kc == 0, kc == 7, r=[hk] + wk, w=[("ps", b1)])
            b2 = self.bank("mm")
            for kc in range(8):
                self.mm(self.ps[b2][:, 0:160], h[:, kc, :], w_in[:, kc, 2080:2240], kc == 0, kc == 7, r=[hk] + wk, w=[("ps", b2)])
            sqf = sq96.rearrange("p a b -> p (a b)")
            self.act(sqf[:, 0:384], self.ps[b1][:, 0:384], AF.Square, r=[("ps", b1)], w=["sq96", "st1"], accum_out=st1[:, 0:1])
            self.rstd(st1[:, 0:1], st1[:, 0:1], 384, r=["st1"], w=["st1"])
            self.stt(cqb[:, n, :], self.ps[b1][:, 0:384], st1[:, 0:1], gqn, ALU.mult, ALU.mult, r=[("ps", b1), "st1", "cst"], w=[("cqb", n)])
            self.cp("act", ckvf[:, 0:128], self.ps[b1][:, 384:512], r=[("ps", b1)], w=["ckvf"])
            self.cp("act", ckvf[:, 128:256], self.ps[b2][:, 0:128], r=[("ps", b2)], w=["ckvf"])
            self.cp("dve", kpe, self.ps[b2][:, 128:160], r=[("ps", b2)], w=["kpe"])
            self.act(sqf[:, 0:256], ckvf, AF.Square, r=["ckvf"], w=["sq96", "st1b"], accum_out=st1[:, 1:2])
            self.rstd(st1[:, 1:2], st1[:, 1:2], 256, r=["st1b"], w=["st1b"])
            self.stt(ckvn, ckvf, st1[:, 1:2], gkvn, ALU.mult, ALU.mult, r=["ckvf", "st1b", "cst"], w=["ckvn"])

        def m_attn(M, sq96, tiles, cqb, rope_q, NK):
            QTm, omb, OM, KTm, VAm, cqT, kc96, knb, w_qb = (M[k] for k in ("QTm", "omb", "OM", "KTm", "VAm", "cqT", "kc96", "knb", "w_qb"))
            kt_list = list(range(NK))
            kkeys = [("KTm", kt) for kt in kt_list]
            vkeys = [("VAm", kt) for kt in kt_list] + ["VA1"]
            nq = len(tiles)
            for n, t in enumerate(tiles):
                b = self.bank("tr")
                for kc in range(3):
                    self.tr(self.psb[b][:, kc * 128:(kc + 1) * 128], cqb[:, n, kc * 128:(kc + 1) * 128], self.ident_b,
                            r=[("cqb", n), "identb"], w=[("ps", b)])
                self.cp("act", cqT, self.psb[b][:, 0:384].rearrange("p (a b) -> p a b", a=3), r=[("ps", b)], w=["cqT"])
                for bk in range(2):
                    b = self.bank("mm")
                    for kc in range(3):
                        self.mm(self.ps[b][:, 0:384], cqT[:, kc, :], w_qb[:, kc, bk * 384:(bk + 1) * 384], kc == 0, kc == 2,
                                r=["cqT", "w_qb"], w=[("ps", b)])
                    self.cp("act", kc96[:, bk * 4:(bk + 1) * 4, :], self.ps[b][:, 0:384].rearrange("p (h d) -> p h d", h=4),
                            r=[("ps", b)], w=["kc96"])
                norm96(M, sq96, gq96, None if rope_q is None else (rope_q[0][:, n, :, :], rope_q[1]))
                b = self.bank("tr")
                for hh in range(8):
                    self.tr(self.psb[b][0:96, hh * 128:(hh + 1) * 128], knb[:, hh, :], self.ident_b, r=["knb", "identb"], w=[("ps", b)])
                self.cp("dve", QTm[0:96, :, n * 128:(n + 1) * 128], self.psb[b][0:96, :].rearrange("p (a b) -> p a b", a=8),
                        r=[("ps", b)], w=[("QTm", n)])
            for hh in range(8):
                self.attention(
                    KT=lambda st, hh=hh: KTm[0:96, hh, st * 128:(st + 1) * 128], kparts=96, kt_list=kt_list,
                    QTv=QTm[0:96, hh, 0:nq * 128], nq=nq,
                    VA_of=lambda st, hh=hh: VAm[:, st, hh, :], scale=float(96 ** -0.5), Pt=Pt,
                    out_of=lambda j, hh=hh: (omb[:, j, hh * 64:(hh + 1) * 64], ("omb", j)),
                    kr=kkeys, qr=[("QTm", j) for j in range(nq)], vr=vkeys, ow=None)
            for n, t in enumerate(tiles):
                b = self.bank("tr")
                for c in range(4):
                    self.tr(self.psb[b][:, c * 128:(c + 1) * 128], omb[:, n, c * 128:(c + 1) * 128], self.ident_b,
                            r=[("omb", n), "identb"], w=[("ps", b)])
                self.cp("act", OM[:, :, n * 128:(n + 1) * 128], self.psb[b][:, 0:512].rearrange("p (a b) -> p a b", a=4),
                        r=[("ps", b)], w=[("OM", n)])
                banks = []
                for hb in range(2):
                    b = self.bank("mm")
                    banks.append(b)
                    for cc in range(4):
                        c = hb * 4 + cc
                        for kc in range(8):
                            rhs = OG[:, kc, t * 128:(t + 1) * 128] if kc < 4 else OM[:, kc - 4, n * 128:(n + 1) * 128]
                            self.mm(self.ps[b][:, cc * 128:(cc + 1) * 128], w_out[:, kc, c * 128:(c + 1) * 128],
                                    rhs, kc == 0, kc == 7,
                                    r=["w_out", ("OG", t), ("OM", n)], w=[("ps", b)])
                self.mixer_residual(t, banks)

        st_ = self.sseg["tiles"]
        A.reset(base_off)
        B = g_alloc(2, persist=SP)
        g_prep(st_, B)
        g_scan(2, B, None)
        ccgs, ccgd = "cc_src_g%d" % l, "cc_dst_g%d" % l
        gsrc = A.alloc([4, 129], F32)
        self.tt("dve", gsrc[:, :, 128], SP["etot"][:, :, 0], SP["etot"][:, :, 1], ALU.mult, r=[("etot", 0), ("etot", 1)], w=["gsrc_a"])
        for z in range(2):
            for fc in range(2):
                zf = z * 2 + fc
                self.cp("act", gsrc[:, zf, 0:128], B["Sst"][z][fc], r=[("S", z, fc)], w=[("gsrc", zf)])
        self.dma("sp", ccg_src.ap().rearrange("(zf p) c -> p zf c", p=128), gsrc,
                 r=["gsrc_a"] + [("gsrc", zf) for zf in range(4)], w=[ccgs], dq="bnc_g")
        self.allgather((l, "g"), r=[ccgs], w=[ccgd])
        self.P.barrier()
        if "x1" in self.stages:
            return
        A.reset(base_off)
        W = own_alloc()
        ccms, ccmd = "cc_src_m%d" % l, "cc_dst_m%d" % l
        for n, t in enumerate(st_):
            m_own(t, n, W, SP["cqb"])
            self.dma("sp", ccm_src[n * 128:(n + 1) * 128, 0:256], W["ckvn"], r=["ckvn"], w=[ccms], dq="bnc_m")
            self.dma("sp", ccm_src[n * 128:(n + 1) * 128, 256:288], W["kpe"], r=["kpe"], w=[ccms], dq="bnc_m")
        self.allgather((l, "m"), r=[ccms], w=[ccmd])
        self.P.barrier()
        if "x2" in self.stages:
            return

        for sq_ in self.seqs:
            tiles = sq_["tiles"]
            bi = sq_["bidx"]
            A.reset(base_off)
            B = g_alloc(2)
            g_prep(tiles, B)
            g_scan(2, B, bi)
            g_out(tiles, B["oT"], B["rsT"])
            self.P.barrier()
            A.reset(base_off)
            M = m_alloc(2)
            W = own_alloc()
            cqb = A.alloc([2, 384], BF16)
            for n, t in enumerate(tiles):
                m_own(t, n, W, cqb)
                self.dma("sp", O["o_ckv"][bi, i, n * 128:(n + 1) * 128, :], W["ckvn"], r=["ckvn"], w=[], dq="ckvn_o")
                self.dma("sp", O["o_kpe"][bi, i, n * 128:(n + 1) * 128, :], W["kpe"], r=["kpe"], w=[], dq="kpe_o")
                kside(M, W["sq96"], W["ckvn"], "ckvn", W["kpe"], "kpe", n, None)
            m_attn(M, W["sq96"], tiles, cqb, None, 2)
            self.P.barrier()

        if "x3" in self.stages:
            return
        A.reset(base_off)
        gU = A.alloc([16, 129], F32)
        self.dma("sp", gU, ccg_dst.ap().rearrange("(rz p) c -> p rz c", p=128), r=[ccgd], w=["gU"], dq="gU")
        Sin = [[A.alloc([128], F32) for _ in range(2)] for _ in range(2)]
        Sib = [[A.alloc([128], BF16) for _ in range(2)] for _ in range(2)]
        tS = A.alloc([128], F32)
        for z in range(2):
            for fc in range(2):
                zf = z * 2 + fc
                S_ = Sin[z][fc]
                sk = ("Sin", z, fc)
                self.dma("sp", S_, I["c_gla"][i, z, fc * 128:(fc + 1) * 128, :], r=[], w=[sk], dq=sk)
                ranks = [0, 1, 2, 3] if z == 0 else [3, 2, 1, 0]
                for k in ranks:
                    gi = k * 4 + zf
                    self.stt(tS, S_, gU[:, gi, 128:129], gU[:, gi, 0:128], ALU.mult, ALU.add, r=[sk, "gU"], w=["tS"])
                    self.tt("dve", tS, tS, S_, ALU.subtract, r=["tS", sk], w=["tS"])
                    self.stt(S_, tS, sel[:, z * 4 + k:z * 4 + k + 1], S_, ALU.mult, ALU.add, r=["tS", sk, "cst"], w=[sk])
                self.cp("act", Sib[z][fc], S_, r=[sk], w=[("Sib", z, fc)])
        if "x5" in self.stages:
            self.P.barrier()
            return
        for z in range(2):
            order = [0, 1] if z == 0 else [1, 0]
            for oi, n in enumerate(order):
                nc_ = slice(n * 128, (n + 1) * 128)
                bh = [self.bank("acc"), self.bank("acc")]
                for hp in range(2):
                    pr = slice(hp * 64, (hp + 1) * 64)
                    for fc in range(2):
                        self.mm(self.ps[bh[hp]][:, fc * 128:(fc + 1) * 128], Sib[z][fc][pr, :], SP["qlT"][z][pr, fc, nc_], True, True,
                                r=[("Sib", z, fc), ("qlT", z, n)], w=[("ps", bh[hp])])
                oview = SP["oT"][:, :, nc_].rearrange("p (f h) t -> p f h t", f=2, h=2)
                for hp in range(2):
                    pb = self.ps[bh[hp]][:, 0:256].rearrange("p (a b) -> p a b", a=2)
                    self.tt("dve", oview[:, :, hp, :], oview[:, :, hp, :], pb, ALU.add, r=[("ps", bh[hp]), ("oT", n)], w=[("oT", n)])
                if oi == 0:
                    for fc in range(2):
                        zf = z * 2 + fc
                        sk = ("Sin", z, fc)
                        self.P.op("dve", lambda h, S_=Sin[z][fc], e=SP["etot"][:, zf, n:n + 1]: h.tensor_scalar(S_, S_, e, None, ALU.mult),
                                  r=[sk, ("etot", n)], w=[sk])
                        self.cp("act", Sib[z][fc], Sin[z][fc], r=[sk], w=[("Sib", z, fc)])
        if "x6" in self.stages:
            self.P.barrier()
            return
        g_out(st_, SP["oT"], SP["rsT"])
        self.P.barrier()
        if "x4" in self.stages:
            return
        A.reset(base_off)
        M = m_alloc(12)
        sq96 = A.alloc([8, 96], F32)
        ropem_all = A.alloc([8, 2, 16], F32)
        ropem_own = A.alloc([2, 2, 16], F32)
        ckp = A.alloc([4, 32], F32)
        cck = A.alloc([256], F32)
        kgm = [A.alloc([288], F32) for _ in range(2)]
        self.dma("sp", ropem_all, I["ropem_all"].rearrange("(t p) c q -> p t c q", p=128), r=[], w=["ropem_all"], dq="ropem")
        self.dma("sp", ropem_own, I["ropem"].rearrange("(t p) c q -> p t c q", p=128), r=[], w=["ropem_own"], dq="ropem")
        self.dma("sp", ckp, I["c_kpe"][i].rearrange("(t p) n -> p t n", p=128), r=[], w=["ckp"], dq="ckp")
        for kt in range(4):
            self.dma("sp", cck, I["c_ckv"][i, kt * 128:(kt + 1) * 128, :], r=[], w=["cck"], dq="cck")
            kside(M, sq96, cck, "cck", ckp[:, kt, :], "ckp", kt, None)
        for k in range(8):
            kg_ = kgm[k % 2]
            kk = ("kgm", k % 2)
            self.dma("sp", kg_, ccm_dst[k * 128:(k + 1) * 128, :], r=[ccmd], w=[kk], dq=kk)
            kside(M, sq96, kg_[:, 0:256], kk, kg_[:, 256:288], kk, 4 + k, (ropem_all[:, k, :, :], "ropem_all"))
        m_attn(M, sq96, st_, SP["cqb"], (ropem_own, "ropem_own"), 12)


def _rope_tables(n_tok, d_rot):
    t = np.arange(n_tok, dtype=np.int32)
    pos = np.stack([t // 64, t % 64], axis=-1).astype(np.float32)
    quarter = d_rot // 4
    inv = np.power(np.float32(10000.0), -np.arange(quarter, dtype=np.float32) / np.float32(quarter)).astype(np.float32)
    ang = pos[:, :, None] * inv
    cos = np.cos(ang).astype(np.float32).reshape(n_tok, 2 * quarter)
    sin = np.sin(ang).astype(np.float32).reshape(n_tok, 2 * quarter)
    return np.ascontiguousarray(np.stack([cos, sin], axis=1))


def _build_cst(inp, b, j):
    c = np.zeros((128, NCST), np.float32)

    def put(name, arr, parts=128):
        o, n = CST_OFF[name]
        c[0:parts, o:o + n] = np.asarray(arr, np.float32).reshape(parts, n)

    s = np.arange(128)[:, None]
    t = np.arange(128)[None, :]
    v = np.float32(-1.0 / 16.0)
    put("ident", np.eye(128))
    put("trim0", (s <= t) * v)
    put("trim1", (s >= t) * v)
    put("tris0", (s > t) * v)
    put("tris1", (s < t) * v)
    put("mask0", (s <= t) * 1.0)
    put("mask1", (s >= t) * 1.0)
    put("rm", np.array([[1, 0], [0, 1], [1, 1]], np.float32), parts=3)
    cond = np.stack([inp["c_ctx"].reshape(8, 128).T, inp["c"][b].reshape(8, 128).T], axis=-1)
    put("cond", cond)
    selv = np.array([1.0 if k < j else 0.0 for k in range(4)] + [1.0 if k > j else 0.0 for k in range(4)], np.float32)
    put("sel", np.broadcast_to(selv[None, :], (128, 8)))
    put("gmix", inp["norm_mix_g"].reshape(4, 8, 128).transpose(2, 0, 1))
    put("gffn", inp["norm_ffn_g"].reshape(4, 8, 128).transpose(2, 0, 1))
    for i in range(2):
        put("gq%d" % i, np.broadcast_to(inp["gqa_qn_g"][i][None, :], (128, 64)))
        put("gk%d" % i, np.broadcast_to(inp["gqa_kn_g"][i][None, :], (128, 64)))
        put("gout%d" % i, inp["gla_out_g"][i].reshape(128, 1))
        put("gqn%d" % i, np.broadcast_to(inp["mla_q_norm_g"][i][None, :], (128, 384)))
        put("gkvn%d" % i, np.broadcast_to(inp["mla_kv_norm_g"][i][None, :], (128, 256)))
        put("gq96%d" % i, np.broadcast_to(inp["mla_qn_g"][i][None, :], (128, 96)))
        put("gk96%d" % i, np.broadcast_to(inp["mla_kn_g"][i][None, :], (128, 96)))
    return c


_NC_CACHE = {}


def _get_nc(debug=(), stages=None):
    key = (tuple(sorted(debug)), None if stages is None else tuple(sorted(stages)))
    if key not in _NC_CACHE:
        kb = KB(debug, stages)
        nc = kb.build()
        _NC_CACHE[key] = (nc, kb)
    return _NC_CACHE[key]


def make_in_maps(inp):
    inp = {k: np.ascontiguousarray(np.asarray(v)) for k, v in inp.items()}
    ropeg = _rope_tables(1024, 64)
    ropem = _rope_tables(1024, 32)
    shared = dict(
        ada_w=inp["ada_w"], ada_b=inp["ada_b"], ffn_w_in=inp["ffn_w_in"], ffn_w_out=inp["ffn_w_out"],
        ab_w_in=inp["ab_w_in"], ab_w_out=inp["ab_w_out"], a_w2=inp["gla_a_w2"], a_b=inp["gla_a_b"],
        w_qb=inp["mla_w_qb"], w_kvb=inp["mla_w_kvb"], gqa_w_in=inp["gqa_w_in"], gqa_w_out=inp["gqa_w_out"])
    in_maps = []
    for core in range(8):
        b, j = core // 4, core % 4
        m = dict(shared)
        m["ropem_all"] = ropem
        m["ropeg"] = np.ascontiguousarray(ropeg[j * 256:(j + 1) * 256])
        m["ropem"] = np.ascontiguousarray(ropem[j * 256:(j + 1) * 256])
        m["cst"] = _build_cst(inp, b, j)
        m["xp"] = np.ascontiguousarray(inp["x_prompt"][core * 4:(core + 1) * 4].reshape(1024, D))
        m["xs"] = np.ascontiguousarray(inp["x_sample"][b, j * 256:(j + 1) * 256])
        m["c_ckv"] = np.ascontiguousarray(inp["cache_mla_ckv"][b])
        m["c_kpe"] = np.ascontiguousarray(inp["cache_mla_kpe"][b])
        m["c_gla"] = np.ascontiguousarray(inp["state_gla"][b].reshape(2, 2, 256, 128))
        m["c_gk"] = np.ascontiguousarray(inp["cache_gqa_k"][b].reshape(2, 512, 256))
        m["c_gv"] = np.ascontiguousarray(inp["cache_gqa_v"][b].reshape(2, 512, 256))
        in_maps.append(m)
    return in_maps


def kernel(**inputs):
    nc, kb = _get_nc()
    in_maps = make_in_maps(inputs)
    res = run_bass_kernel_spmd(nc, in_maps, core_ids=list(range(8)))
    R = res.results
    y_prompt = np.concatenate([R[c]["yp"].reshape(4, 256, D) for c in range(8)], axis=0)
    y_sample = np.stack([np.concatenate([R[4 * b + j]["ys"] for j in range(4)], axis=0) for b in range(2)], axis=0)
    new_ckv = np.concatenate([R[c]["o_ckv"] for c in range(8)], axis=0)
    new_kpe = np.concatenate([R[c]["o_kpe"] for c in range(8)], axis=0)
    new_gla = np.concatenate([R[c]["o_gla"].reshape(4, 2, 2, 4, 64, 128) for c in range(8)], axis=0)
    new_k = np.concatenate([R[c]["o_gk"].reshape(4, 2, 256, 4, 64) for c in range(8)], axis=0)
    new_v = np.concatenate([R[c]["o_gv"].reshape(4, 2, 256, 4, 64) for c in range(8)], axis=0)
    outs = (y_prompt, y_sample, new_ckv, new_kpe, new_gla, new_k, new_v)
    return tuple(np.ascontiguousarray(o, dtype=np.float32) for o in outs)
```

```python
import bisect
import contextlib
import numpy as np
import concourse.bass as bass
import concourse.mybir as mybir
from concourse.bass_utils import run_bass_kernel_spmd

F32 = mybir.dt.float32
BF16 = mybir.dt.bfloat16
AF = mybir.ActivationFunctionType
ALU = mybir.AluOpType
AX = mybir.AxisListType

ENGS = ("pe", "act", "dve", "pool", "sp")
RAW, WAR, WAW = 1, 2, 4
EPS = 1e-6
D = 1024
FH = 2816
ARENA_BYTES = 152 * 1024
NTOK = 1280
NTILE = 10


class Prog:
    def __init__(self):
        self.ops = []
        self.last_w = {}
        self.readers = {}

    def op(self, eng, fn, r=(), w=(), dq=None, inc=16):
        i = len(self.ops)
        deps = {}
        psr = [k for k in r if isinstance(k, tuple) and k and k[0] == "ps"]
        if psr:
            r = [k for k in r if not (isinstance(k, tuple) and k and k[0] == "ps")]
            w = list(w) + [k for k in psr if k not in w]
            for k in psr:
                lw = self.last_w.get(k)
                if lw is not None:
                    deps[lw] = deps.get(lw, 0) | RAW
        for k in r:
            lw = self.last_w.get(k)
            if lw is not None:
                deps[lw] = deps.get(lw, 0) | RAW
        for k in w:
            lw = self.last_w.get(k)
            if lw is not None:
                deps[lw] = deps.get(lw, 0) | WAW
            for rd in self.readers.get(k, ()):
                if rd != i:
                    deps[rd] = deps.get(rd, 0) | WAR
        for k in r:
            self.readers.setdefault(k, []).append(i)
        for k in w:
            self.last_w[k] = i
            self.readers[k] = []
        deps.pop(i, None)
        self.ops.append(dict(eng=eng, fn=fn, deps=deps, dq=dq, bar=None, inc=inc))
        return i

    def barrier(self):
        first = len(self.ops)
        for e in ENGS:
            self.ops.append(dict(eng=e, fn="drain", deps={}, dq=None, bar=("sig", first)))
        sig_ids = list(range(first, first + len(ENGS)))
        for e in ENGS:
            self.ops.append(dict(eng=e, fn="nop", deps={s: RAW for s in sig_ids}, dq=None, bar=("wait", first)))
        iscc = lambda k: isinstance(k, str) and k.startswith("cc")
        self.last_w = {k: v for k, v in self.last_w.items() if iscc(k)}
        self.readers = {k: v for k, v in self.readers.items() if iscc(k)}

    def emit(self, nc, es):
        ops = self.ops
        n = len(ops)
        needed = [False] * n
        for i, o in enumerate(ops):
            kept = []
            for d, kind in o["deps"].items():
                od = ops[d]
                if od["dq"] is None and o["dq"] is None and od["eng"] == o["eng"] and o["bar"] is None:
                    if o["eng"] == "pe":
                        continue
                kept.append(d)
                needed[d] = True
            o["kdeps"] = kept
        eng_sem = {e: es.enter_context(nc.semaphore("sem_" + e)) for e in ENGS}
        dq_keys = []
        seen = set()
        for o in ops:
            if o["dq"] is not None and o["dq"] not in seen:
                seen.add(o["dq"])
                dq_keys.append(o["dq"])
        dq_sem = {k: es.enter_context(nc.semaphore("dq_%d" % j)) for j, k in enumerate(dq_keys)}
        dq_idx = {k: [] for k in dq_keys}
        dq_cum = {k: [0] for k in dq_keys}
        eng_cnt = {e: 0 for e in ENGS}

        def dq_before(k, i):
            return dq_cum[k][bisect.bisect_left(dq_idx[k], i)]

        for i, o in enumerate(ops):
            if o["dq"] is not None:
                dq_idx[o["dq"]].append(i)
                dq_cum[o["dq"]].append(dq_cum[o["dq"]][-1] + o["inc"])
                o["sig"] = ("dq", o["dq"])
            elif needed[i] or (o["bar"] is not None and o["bar"][0] == "sig"):
                eng_cnt[o["eng"]] += 1
                o["sig"] = ("eng", o["eng"], eng_cnt[o["eng"]])
            else:
                o["sig"] = None
        per_eng = {e: [] for e in ENGS}
        for i, o in enumerate(ops):
            per_eng[o["eng"]].append(i)
        self.n_sems = len(ENGS) + len(dq_keys)
        self.counts = {e: len(per_eng[e]) for e in ENGS}

        def run(e, h):
            waited = {}
            for i in per_eng[e]:
                o = ops[i]
                waits = {}
                for d in o["kdeps"]:
                    od = ops[d]
                    if od["dq"] is not None:
                        k = od["dq"]
                        cnt = dq_before(k, i)
                        key = ("dq", k)
                        waits[key] = max(waits.get(key, 0), cnt)
                    else:
                        key = ("eng", od["eng"])
                        waits[key] = max(waits.get(key, 0), od["sig"][2])
                if o["bar"] is not None and o["bar"][0] == "sig" and e == "sp":
                    for k in dq_keys:
                        if isinstance(k, str) and k.startswith("cc"):
                            continue
                        cnt = dq_before(k, i)
                        if cnt:
                            waits[("dq", k)] = cnt
                for key, v in waits.items():
                    if waited.get(key, 0) >= v:
                        continue
                    waited[key] = v
                    sem = dq_sem[key[1]] if key[0] == "dq" else eng_sem[key[1]]
                    h.wait_ge(sem, v)
                if o["fn"] == "drain":
                    inst = h.nop() if e == "sp" else h.drain()
                elif o["fn"] == "nop":
                    inst = None
                else:
                    inst = o["fn"](h)
                s = o["sig"]
                if s is not None:
                    if s[0] == "dq":
                        inst.then_inc(dq_sem[s[1]], o["inc"])
                    else:
                        inst.then_inc(eng_sem[s[1]], 1)
            if e == "sp":
                for k in dq_keys:
                    h.wait_ge(dq_sem[k], dq_cum[k][-1])

        with nc.Block() as block:
            @block.tensor
            def _(h):
                run("pe", h)

            @block.scalar
            def _(h):
                run("act", h)

            @block.vector
            def _(h):
                run("dve", h)

            @block.gpsimd
            def _(h):
                run("pool", h)

            @block.sync
            def _(h):
                run("sp", h)


class Arena:
    def __init__(self, t, nbytes):
        self.t = t
        self.nbytes = nbytes
        self.off = 0
        self.peak = 0

    def reset(self, off=0):
        self.off = off

    def alloc(self, free_shape, dtype, parts=128):
        n = int(np.prod(free_shape))
        esz = 4 if dtype == F32 else 2
        sz = (n * esz + 31) // 32 * 32
        o = self.off
        assert o + sz <= self.nbytes, ("arena overflow", o, sz, self.nbytes)
        self.off = o + sz
        self.peak = max(self.peak, self.off)
        ap = self.t[0:parts, o // 2:(o + n * esz) // 2]
        if dtype == F32:
            ap = ap.bitcast(F32)
        fs = list(free_shape)
        if len(fs) == 2:
            ap = ap.rearrange("p (a b) -> p a b", a=fs[0], b=fs[1])
        elif len(fs) == 3:
            ap = ap.rearrange("p (a b c) -> p a b c", a=fs[0], b=fs[1], c=fs[2])
        elif len(fs) == 4:
            ap = ap.rearrange("p (a b c d) -> p a b c d", a=fs[0], b=fs[1], c=fs[2], d=fs[3])
        return ap


def _cst_layout():
    off = {}
    o = 0

    def add(name, n):
        nonlocal o
        off[name] = (o, n)
        o += n

    add("ident", 128)
    add("trim0", 128)
    add("trim1", 128)
    add("tris0", 128)
    add("tris1", 128)
    add("mask0", 128)
    add("mask1", 128)
    add("rm", 2)
    add("cond", 16)
    add("sel", 8)
    add("gmix", 32)
    add("gffn", 32)
    for i in range(2):
        add("gq%d" % i, 64)
        add("gk%d" % i, 64)
        add("gout%d" % i, 1)
        add("gqn%d" % i, 384)
        add("gkvn%d" % i, 256)
        add("gq96%d" % i, 96)
        add("gk96%d" % i, 96)
    return off, o


CST_OFF, NCST = _cst_layout()


class KB:
    def __init__(self, debug=(), stages=None):
        self.stages = set(stages) if stages is not None else {"adaln", "ffn", "mixc", "mixab", "P", "S"}
        self.debug = set(debug)
        self.dbg_outs = {}

    def mm(self, out, lhsT, rhs, start, stop, r, w, **kw):
        self.P.op("pe", lambda h: h.matmul(out, lhsT, rhs, start=start, stop=stop, **kw), r=r, w=w)

    def tr(self, out, in_, ident, r, w):
        self.P.op("pe", lambda h: h.transpose(out, in_, ident), r=r, w=w)

    def act(self, out, in_, func, r, w, **kw):
        self.P.op("act", lambda h: h.activation(out, in_, func, **kw), r=r, w=w)

    def tt(self, eng, out, a, b, op, r, w):
        self.P.op(eng, lambda h: h.tensor_tensor(out, a, b, op), r=r, w=w)

    def stt(self, out, in0, scalar, in1, op0, op1, r, w):
        self.P.op("dve", lambda h: h.scalar_tensor_tensor(out, in0, scalar, in1, op0, op1), r=r, w=w)

    def cp(self, eng, out, in_, r, w):
        if eng == "act":
            self.P.op("act", lambda h: h.copy(out, in_), r=r, w=w)
        else:
            self.P.op(eng, lambda h: h.tensor_copy(out, in_), r=r, w=w)

    def recip(self, out, in_, r, w):
        self.P.op("dve", lambda h: h.reciprocal(out, in_), r=r, w=w)

    def red(self, out, in_, r, w):
        self.P.op("dve", lambda h: h.tensor_reduce(out, in_, AX.X, ALU.add), r=r, w=w)

    def memset(self, eng, ap, val, w):
        self.P.op(eng, lambda h: h.memset(ap, val), w=w)

    def dma(self, q, out, in_, r, w, dq, **kw):
        self.P.op(q, lambda h: h.dma_start(out=out, in_=in_, **kw), r=r, w=w, dq=dq)

    def allgather(self, key, r, w):
        src, dst = self.CC[key]
        name = "cc_%s_%s" % key
        self.P.op("pool", lambda h: h.collective_compute("AllGather", ALU.bypass, replica_groups=[[0, 1, 2, 3], [4, 5, 6, 7]],
                                                         ins=[src.ap().opt()], outs=[dst.ap().opt()]),
                  r=r, w=w, dq=name, inc=1)

    def bank(self, pool):
        lst, idx = self.pools[pool]
        b = lst[idx % len(lst)]
        self.pools[pool][1] = idx + 1
        return b

    def rstd(self, out, ss, n, r, w):
        self.act(out, ss, AF.Sqrt, r=r, w=w, bias=EPS, scale=1.0 / n)
        self.recip(out, out, r=w, w=w)

    def cst(self, name, parts=128):
        o, n = CST_OFF[name]
        return self.cst_t[0:parts, o:o + n]

    def dbg(self, name, ap, r, shape):
        if name not in self.debug:
            return
        t = self.nc.dram_tensor("dbg_" + name, list(shape), ap.dtype if hasattr(ap, "dtype") else F32, kind="ExternalOutput").ap()
        self.dbg_outs[name] = shape
        self.dma("sp", t, ap, r=r, w=[], dq=("dbg", name))

    def build(self):
        nc = bass.Bass("TRN2", target_bir_lowering=False)
        self.nc = nc
        self.P = Prog()

        def din(name, shape):
            return nc.dram_tensor(name, list(shape), F32, kind="ExternalInput").ap()

        def dout(name, shape):
            return nc.dram_tensor(name, list(shape), F32, kind="ExternalOutput").ap()

        I = {}
        I["cst"] = din("cst", [128, NCST])
        I["xp"] = din("xp", [1024, D])
        I["xs"] = din("xs", [256, D])
        I["ada_w"] = din("ada_w", [4, D, 1536])
        I["ada_b"] = din("ada_b", [4, 1536])
        I["ffn_w_in"] = din("ffn_w_in", [4, D, 2 * FH])
        I["ffn_w_out"] = din("ffn_w_out", [4, FH, D])
        I["ab_w_in"] = din("ab_w_in", [2, D, 2240])
        I["ab_w_out"] = din("ab_w_out", [2, D, D])
        I["a_w2"] = din("a_w2", [2, 2, 16, 256])
        I["a_b"] = din("a_b", [2, 2, 256])
        I["w_qb"] = din("w_qb", [2, 384, 768])
        I["w_kvb"] = din("w_kvb", [2, 256, 1024])
        I["gqa_w_in"] = din("gqa_w_in", [2, D, 1536])
        I["gqa_w_out"] = din("gqa_w_out", [2, D, D])
        I["c_ckv"] = din("c_ckv", [2, 512, 256])
        I["c_kpe"] = din("c_kpe", [2, 512, 32])
        I["c_gla"] = din("c_gla", [2, 2, 256, 128])
        I["c_gk"] = din("c_gk", [2, 512, 256])
        I["c_gv"] = din("c_gv", [2, 512, 256])
        I["ropeg"] = din("ropeg", [256, 2, 32])
        I["ropem"] = din("ropem", [256, 2, 16])
        I["ropem_all"] = din("ropem_all", [1024, 2, 16])
        O = {}
        O["yp"] = dout("yp", [1024, D])
        O["ys"] = dout("ys", [256, D])
        O["o_ckv"] = dout("o_ckv", [4, 2, 256, 256])
        O["o_kpe"] = dout("o_kpe", [4, 2, 256, 32])
        O["o_gla"] = dout("o_gla", [4, 2, 2, 256, 128])
        O["o_gk"] = dout("o_gk", [4, 2, 256, 256])
        O["o_gv"] = dout("o_gv", [4, 2, 256, 256])
        self.I, self.O = I, O
        self.CC = {}
        self.CC[("a", "a")] = (nc.dram_tensor("ccs_a", [3, 6144], F32), nc.dram_tensor("ccd_a", [12, 6144], F32))
        for l in range(4):
            if l % 2 == 0:
                self.CC[(l, "g")] = (nc.dram_tensor("ccs_g%d" % l, [512, 129], F32), nc.dram_tensor("ccd_g%d" % l, [2048, 129], F32))
                self.CC[(l, "m")] = (nc.dram_tensor("ccs_m%d" % l, [256, 288], F32), nc.dram_tensor("ccd_m%d" % l, [1024, 288], F32))
            else:
                self.CC[(l, "c")] = (nc.dram_tensor("ccs_c%d" % l, [256, 512], F32), nc.dram_tensor("ccd_c%d" % l, [1024, 512], F32))

        with contextlib.ExitStack() as es:
            self.xT = es.enter_context(nc.sbuf_tensor("xT", [128, 8, NTOK], F32))
            self.cst_t = es.enter_context(nc.sbuf_tensor("cst_sb", [128, NCST], F32))
            small = es.enter_context(nc.sbuf_tensor("small", [128, 4 * 48 * 2 + 96 + 8], F32))
            cbf = es.enter_context(nc.sbuf_tensor("cbf", [128, 256 + 16], BF16))
            arena_t = es.enter_context(nc.sbuf_tensor("arena", [128, ARENA_BYTES // 2], BF16))
            self.A = Arena(arena_t, ARENA_BYTES)
            self.ps = [es.enter_context(nc.psum_tensor("ps%d" % i, [128, 512], F32)) for i in range(8)]
            self.psb = [p.bitcast(BF16) for p in self.ps]
            self.pools = {"mm": [[0, 1, 2, 3], 0], "acc": [[4, 5], 0], "tr": [[6, 7], 0]}
            self.modt = small[:, 0:384].rearrange("p (l c k) -> p l c k", l=4, c=48, k=2)
            self.lsc = small[:, 384:480].rearrange("p (g a b) -> p g a b", g=2, a=6, b=8)
            self.eps_c = small[:, 480:481]
            self.one_c = small[:, 481:482]
            self.ident_b = cbf[:, 0:128]
            self.ones_b = cbf[:, 128:256]
            self.sc_b = cbf[:, 256:272].rearrange("p (k c) -> p k c", k=8, c=2)
            self.ident_f = self.cst("ident")

            self.prologue()
            if "adaln" in self.stages:
                self.adaln_all()
            self.run_all()
            self.P.emit(nc, es)
        return nc

    def prologue(self):
        self.dma("sp", self.cst_t[:, :], self.I["cst"], r=[], w=["cst"], dq="cst")
        self.memset("dve", self.eps_c, EPS, w=["small_c"])
        self.memset("dve", self.one_c, 1.0, w=["small_c"])
        self.memset("dve", self.ones_b, 1.0, w=["ones"])
        self.cp("dve", self.ident_b, self.ident_f, r=["cst"], w=["identb"])
        cond = self.cst("cond").rearrange("p (k c) -> p k c", k=8, c=2)
        self.act(self.sc_b, cond, AF.Silu, r=["cst"], w=["scb"])

    def adaln_all(self):
        A = self.A
        A.reset()
        slots = [A.alloc([8, 512], BF16) for _ in range(3)]
        mq = A.alloc([6144], F32)
        mtok = A.alloc([6144], F32)
        rm = self.cst("rm", parts=3)
        cc_src, cc_dst = self.CC[("a", "a")]
        k = 0
        for l in range(4):
            self.dma("sp", mq[2:3, l * 1536:(l + 1) * 1536], self.I["ada_b"][l:l + 1, :], r=[], w=[("mq", "b")], dq="mtokb")
            for j in range(3):
                s = k % 3
                k += 1
                src = self.I["ada_w"][l, :, j * 512:(j + 1) * 512].rearrange("(k p) n -> p k n", p=128)
                self.dma("pool", slots[s], src, r=[], w=[("adw", s)], dq=("adw", s))
                b = self.bank("mm")
                for kc in range(8):
                    self.mm(self.ps[b][0:2, :], self.sc_b[:, kc, :], slots[s][:, kc, :], kc == 0, kc == 7,
                            r=[("adw", s), "scb"], w=[("ps", b)])
                c0 = l * 1536 + j * 512
                self.cp("act", mq[0:2, c0:c0 + 512], self.ps[b][0:2, :], r=[("ps", b)], w=[("mq", l, j)])
        self.dma("sp", cc_src.ap(), mq[0:3, :], r=[("mq", "b")] + [("mq", l, j) for l in range(4) for j in range(3)],
                 w=["cc_src_a"], dq="bnc_a")
        self.allgather(("a", "a"), r=["cc_src_a"], w=["cc_dst_a"])
        dview = cc_dst.ap().rearrange("(j r) (l c) -> r l j c", r=3, l=4)
        for l in range(4):
            self.dma("sp", mtok[0:3, :].rearrange("r (j c) -> r j c", j=4), dview[:, l, :, :], r=["cc_dst_a"], w=["mtok"], dq="mtok")
            b = self.bank("mm")
            for c in range(48):
                self.mm(self.ps[b][:, 2 * c:2 * c + 2], mtok[0:3, c * 128:(c + 1) * 128], rm, True, True,
                        r=["mtok", "cst"], w=[("ps", b)])
            self.cp("dve", self.modt[:, l, :, :], self.ps[b][:, 0:96].rearrange("p (c k) -> p c k", c=48, k=2),
                    r=[("ps", b)], w=["modt"])
        self.P.barrier()

    def layer_scalars(self, l):
        gmix = self.cst("gmix").rearrange("p (l c) -> p l c", l=4, c=8)[:, l, :]
        gffn = self.cst("gffn").rearrange("p (l c) -> p l c", l=4, c=8)[:, l, :]
        for col in range(2):
            mv = self.modt[:, l, :, col]
            L = self.lsc[:, col, :, :]
            self.stt(L[:, 0, :], mv[:, 8:16], 1.0, gmix, ALU.add, ALU.mult, r=["modt", "cst"], w=["lsc"])
            self.cp("dve", L[:, 1, :], mv[:, 0:8], r=["modt"], w=["lsc"])
            self.cp("dve", L[:, 2, :], mv[:, 16:24], r=["modt"], w=["lsc"])
            self.stt(L[:, 3, :], mv[:, 32:40], 1.0, gffn, ALU.add, ALU.mult, r=["modt", "cst"], w=["lsc"])
            self.cp("dve", L[:, 4, :], mv[:, 24:32], r=["modt"], w=["lsc"])
            self.cp("dve", L[:, 5, :], mv[:, 40:48], r=["modt"], w=["lsc"])

    def alloc_norm_tmp(self):
        A = self.A
        self.nm_sq = A.alloc([8, 128], BF16)
        self.nm_rstd = A.alloc([128], F32)
        self.nm_tmp = A.alloc([8, 128], F32)

    def normmod(self, t, which, dst, dstkey):
        xv = self.xT[:, :, t * 128:(t + 1) * 128]
        xk = [("xT", t, c) for c in range(8)]
        grp = 0 if t < 8 else 1
        G = self.lsc[:, grp, 3 * which, :]
        SH = self.lsc[:, grp, 3 * which + 1, :]
        self.act(self.nm_sq, xv, AF.Square, r=xk, w=["nm_sq"])
        b = self.bank("tr")
        for c in range(8):
            self.mm(self.ps[b][:, 0:128], self.ones_b, self.nm_sq[:, c, :], c == 0, c == 7, r=["nm_sq", "ones"], w=[("ps", b)])
        self.rstd(self.nm_rstd, self.ps[b][:, 0:128], D, r=[("ps", b), "small_c"], w=["nm_rstd"])
        self.tt("dve", self.nm_tmp, xv, self.nm_rstd.unsqueeze(1).broadcast_to([128, 8, 128]), ALU.mult,
                r=xk + ["nm_rstd"], w=["nm_tmp"])
        self.tt("pool", self.nm_tmp, self.nm_tmp, G.unsqueeze(2).broadcast_to([128, 8, 128]), ALU.mult,
                r=["nm_tmp", "lsc"], w=["nm_tmp"])
        self.tt("dve", dst, self.nm_tmp, SH.unsqueeze(2).broadcast_to([128, 8, 128]), ALU.add,
                r=["nm_tmp", "lsc"], w=[dstkey])

    def run_all(self):
        self.seqs = [dict(tiles=[2 * s, 2 * s + 1], ctx=False, rope=False, bidx=s) for s in range(4)]
        self.sseg = dict(tiles=[8, 9], ctx=True, rope=True, bidx=None)
        A = self.A
        A.reset()
        xin = [A.alloc([1024], F32) for _ in range(2)]
        for t in range(NTILE):
            s = t % 2
            src = self.I["xp"][t * 128:(t + 1) * 128, :] if t < 8 else self.I["xs"][(t - 8) * 128:(t - 7) * 128, :]
            self.dma("sp", xin[s], src, r=[], w=[("xin", s)], dq=("xin", s))
            for hb in range(2):
                b = self.bank("tr")
                for cc in range(4):
                    c = hb * 4 + cc
                    self.tr(self.ps[b][:, cc * 128:(cc + 1) * 128], xin[s][:, c * 128:(c + 1) * 128], self.ident_f,
                            r=[("xin", s), "cst"], w=[("ps", b)])
                self.cp("dve" if hb else "act", self.xT[:, hb * 4:hb * 4 + 4, t * 128:(t + 1) * 128],
                        self.ps[b][:, :].rearrange("p (a b) -> p a b", a=4),
                        r=[("ps", b)], w=[("xT", t, hb * 4 + cc) for cc in range(4)])
        self.P.barrier()
        for l in range(4):
            self.layer_scalars(l)
            if l % 2 == 0:
                if "mixab" in self.stages:
                    self.mixer_ab(l, l // 2)
            else:
                if "mixc" in self.stages:
                    self.mixer_c(l, l // 2)
            self.P.barrier()
            if "ffn" in self.stages:
                self.ffn(l)
            self.P.barrier()
        A.reset()
        yo = [A.alloc([1024], F32) for _ in range(2)]
        for t in range(NTILE):
            s = t % 2
            for hb in range(2):
                b = self.bank("tr")
                for cc in range(4):
                    c = hb * 4 + cc
                    self.tr(self.ps[b][:, cc * 128:(cc + 1) * 128], self.xT[:, c, t * 128:(t + 1) * 128], self.ident_f,
                            r=[("xT", t, c), "cst"], w=[("ps", b)])
                self.cp("dve" if hb else "act", yo[s][:, hb * 512:(hb + 1) * 512], self.ps[b][:, :],
                        r=[("ps", b)], w=[("yo", s, hb)])
            dst = self.O["yp"][t * 128:(t + 1) * 128, :] if t < 8 else self.O["ys"][(t - 8) * 128:(t - 7) * 128, :]
            self.dma("sp", dst, yo[s], r=[("yo", s, 0), ("yo", s, 1)], w=[], dq=("yo", s))

    def ffn(self, l):
        A = self.A
        A.reset()
        hT = A.alloc([8, NTOK], BF16)
        actT = A.alloc([22, NTOK], BF16)
        wi = [A.alloc([8, 2, 256], BF16) for _ in range(2)]
        wo = [A.alloc([11, 1024], BF16) for _ in range(2)]
        sg = [A.alloc([512], F32) for _ in range(2)]
        self.alloc_norm_tmp()
        W1 = self.I["ffn_w_in"]
        W2 = self.I["ffn_w_out"]
        TB = [(0, 512, 0, [0, 1, 2, 3]), (512, 512, 0, [4, 5, 6, 7]), (1024, 256, 1, [8, 9])]
        NS = len(wi)

        def load_wi(j2):
            s = j2 % NS
            for gu in range(2):
                c0 = gu * FH + j2 * 256
                src = W1[l, :, c0:c0 + 256].rearrange("(k p) n -> p k n", p=128)
                self.dma("pool", wi[s][:, :, gu, :], src, r=[], w=[("wi", s, gu)], dq=("wi", s))

        def load_wo(hf):
            src = W2[l, hf * 1408:(hf + 1) * 1408, :].rearrange("(j p) n -> p j n", p=128)
            self.dma("pool", wo[hf], src, r=[], w=[("wo", hf)], dq=("wo", hf))

        for j2 in range(NS):
            load_wi(j2)
        for t in range(NTILE):
            self.normmod(t, 1, hT[:, :, t * 128:(t + 1) * 128], ("hT", t))
        load_wo(0)
        load_wo(1)
        k = 0
        for j2 in range(11):
            s = j2 % NS
            for (t0, tn, grp, tl) in TB:
                hk = [("hT", q) for q in tl]
                for hf in range(2):
                    j = j2 * 2 + hf
                    bg = self.bank("mm")
                    for kc in range(8):
                        self.mm(self.ps[bg][:, 0:tn], wi[s][:, kc, 0, hf * 128:(hf + 1) * 128], hT[:, kc, t0:t0 + tn],
                                kc == 0, kc == 7, r=[("wi", s, 0)] + hk, w=[("ps", bg)])
                    bu = self.bank("mm")
                    for kc in range(8):
                        self.mm(self.ps[bu][:, 0:tn], wi[s][:, kc, 1, hf * 128:(hf + 1) * 128], hT[:, kc, t0:t0 + tn],
                                kc == 0, kc == 7, r=[("wi", s, 1)] + hk, w=[("ps", bu)])
                    sgi = k % 2
                    k += 1
                    self.act(sg[sgi][:, 0:tn], self.ps[bg][:, 0:tn], AF.Silu, r=[("ps", bg)], w=[("sg", sgi)])
                    self.tt("dve", actT[:, j, t0:t0 + tn], sg[sgi][:, 0:tn], self.ps[bu][:, 0:tn], ALU.mult,
                            r=[("sg", sgi), ("ps", bu)], w=[("act", j, t0)])
            if j2 + NS < 11:
                load_wi(j2 + NS)
        for hf in range(2):
            for c in range(8):
                for (t0, tn, grp, tl) in TB:
                    gate = self.lsc[:, grp, 5, :]
                    b = self.bank("mm")
                    for jj in range(11):
                        self.mm(self.ps[b][:, 0:tn], wo[hf][:, jj, c * 128:(c + 1) * 128], actT[:, hf * 11 + jj, t0:t0 + tn],
                                jj == 0, jj == 10, r=[("wo", hf), ("act", hf * 11 + jj, t0)], w=[("ps", b)])
                    xv = self.xT[:, c, t0:t0 + tn]
                    xk = [("xT", q, c) for q in tl]
                    self.stt(xv, self.ps[b][:, 0:tn], gate[:, c:c + 1], xv, ALU.mult, ALU.add, r=[("ps", b), "lsc"] + xk, w=xk)

    def mixer_residual(self, t, banks):
        grp = 0 if t < 8 else 1
        gate = self.lsc[:, grp, 2, :]
        tmp = self.nm_tmp
        for hb in range(2):
            b = banks[hb]
            pv = self.ps[b][:, :].rearrange("p (a b) -> p a b", a=4)
            tv = tmp[:, hb * 4:hb * 4 + 4, :]
            self.tt("dve", tv, pv, gate[:, hb * 4:hb * 4 + 4].unsqueeze(2).broadcast_to([128, 4, 128]), ALU.mult,
                    r=[("ps", b), "lsc"], w=["nm_tmp"])
            xv = self.xT[:, hb * 4:hb * 4 + 4, t * 128:(t + 1) * 128]
            xk = [("xT", t, hb * 4 + q) for q in range(4)]
            self.tt("pool", xv, xv, tv, ALU.add, r=["nm_tmp"] + xk, w=xk)

    def rope(self, xv, H, Q, cs, tmps, r, w, keyp):
        x1 = xv[:, :, :, 0, :]
        x2 = xv[:, :, :, 1, :]
        c = cs[:, 0, :].rearrange("p (a q) -> p a q", a=2, q=Q).unsqueeze(1).broadcast_to([128, H, 2, Q])
        s = cs[:, 1, :].rearrange("p (a q) -> p a q", a=2, q=Q).unsqueeze(1).broadcast_to([128, H, 2, Q])
        t1, t2, t3, t4 = tmps
        k1, k2, k3, k4 = [(keyp, i) for i in range(4)]
        self.tt("dve", t1, x1, c, ALU.mult, r=r, w=[k1])
        self.tt("pool", t2, x2, s, ALU.mult, r=r, w=[k2])
        self.tt("dve", t3, x1, s, ALU.mult, r=r, w=[k3])
        self.tt("pool", t4, x2, c, ALU.mult, r=r, w=[k4])
        self.tt("dve", x1, t1, t2, ALU.subtract, r=[k1, k2, k3], w=w)
        self.tt("dve", x2, t3, t4, ALU.add, r=[k3, k4], w=w)

    def attention(self, KT, kparts, kt_list, QTv, nq, VA_of, scale, Pt, out_of, kr, qr, vr, ow):
        ob = self.bank("acc")
        first, last = kt_list[0], kt_list[-1]
        npt = len(Pt)

        def pv(st, pi):
            for j in range(nq):
                self.mm(self.ps[ob][:, j * 65:(j + 1) * 65], Pt[pi][:, j, :], VA_of(st), (st == first and j == 0), st == last,
                        r=[("Pt", pi)] + vr, w=[("ps", ob)], skip_group_check=True)

        pend = None
        for st in kt_list:
            sb = self.bank("mm")
            self.mm(self.ps[sb][:, 0:nq * 128], KT(st), QTv, True, True, r=kr + qr, w=[("ps", sb)])
            pi = self.pt_i % npt
            self.pt_i += 1
            self.act(Pt[pi][:, 0:nq, :], self.ps[sb][:, 0:nq * 128].rearrange("p (a b) -> p a b", a=nq), AF.Exp,
                     r=[("ps", sb)], w=[("Pt", pi)], scale=scale)
            if pend is not None:
                pv(*pend)
            pend = (st, pi)
        pv(*pend)
        ov = self.ps[ob][:, 0:nq * 65].rearrange("p (a b) -> p a b", a=nq)
        rd = self.at_rden
        self.recip(rd[:, 0:nq], ov[:, :, 64], r=[("ps", ob)], w=["at_rden"])
        for j in range(nq):
            dst, dk = out_of(j)
            self.P.op("dve", lambda h, j=j, dst=dst, rd=rd, ov=ov: h.tensor_scalar(dst, ov[:, j, 0:64], rd[:, j:j + 1], None, ALU.mult),
                      r=[("ps", ob), "at_rden"], w=[dk])

    def mixer_c(self, l, i):
        A = self.A
        A.reset()
        I, O = self.I, self.O
        w_in = A.alloc([8, 1536], BF16)
        w_out = A.alloc([8, 1024], BF16)
        hT = A.alloc([8, 256], BF16)
        KT = A.alloc([4, 1536], BF16)
        VA = A.alloc([12, 4, 65], BF16)
        self.alloc_norm_tmp()
        kvf = A.alloc([512], F32)
        sq = A.alloc([1024], F32)
        qn = A.alloc([1024], F32)
        st16 = A.alloc([16], F32)
        knb = A.alloc([256], BF16)
        qnb = A.alloc([1024], BF16)
        QT = A.alloc([16, 128], BF16)
        Pt = [A.alloc([4, 128], BF16) for _ in range(3)]
        ob = A.alloc([1024], BF16)
        OT = A.alloc([8, 128], BF16)
        self.at_rden = A.alloc([4], F32)
        rtm = [A.alloc([16, 2, 16], F32) for _ in range(4)]
        ropeg = A.alloc([2, 2, 32], F32)
        ck = A.alloc([4, 256], F32)
        cv = A.alloc([4, 256], F32)
        ckb = A.alloc([4, 256], BF16)
        kg = [A.alloc([512], F32) for _ in range(2)]
        self.pt_i = 0
        gq = self.cst("gq%d" % i)
        gk = self.cst("gk%d" % i)
        for kh in range(2):
            self.dma("pool", w_in[:, kh * 4:(kh + 1) * 4, :],
                     I["gqa_w_in"][i, kh * 512:(kh + 1) * 512, :].rearrange("(k p) n -> p k n", p=128), r=[], w=[("w_in", kh)], dq="w_in")
        self.dma("pool", w_out, I["gqa_w_out"][i].rearrange("(k p) n -> p k n", p=128), r=[], w=["w_out"], dq="w_out")
        self.memset("pool", VA[:, :, :, 64:65], 1.0, w=["VA1"])
        wk = [("w_in", 0), ("w_in", 1)]
        self.dma("sp", ropeg, I["ropeg"].rearrange("(t p) c q -> p t c q", p=128), r=[], w=["ropeg"], dq="ropeg")
        cc_src, cc_dst = self.CC[(l, "c")]

        def put_keys(knb_ap, kt, rk):
            b = self.bank("tr")
            for g in range(4):
                self.tr(self.psb[b][0:64, g * 128:(g + 1) * 128], knb_ap[:, g * 64:(g + 1) * 64], self.ident_b,
                        r=rk + ["identb"], w=[("ps", b)])
            self.cp("dve", KT[0:64, :, kt * 128:(kt + 1) * 128], self.psb[b][0:64, 0:512].rearrange("p (a b) -> p a b", a=4),
                    r=[("ps", b)], w=[("KT", kt)])

        def kv_side(t, n, rope_n):
            self.normmod(t, 0, hT[:, :, n * 128:(n + 1) * 128], ("hT", n))
            b = self.bank("mm")
            for kc in range(8):
                self.mm(self.ps[b][:, :], hT[:, kc, n * 128:(n + 1) * 128], w_in[:, kc, 1024:1536], kc == 0, kc == 7,
                        r=[("hT", n)] + wk, w=[("ps", b)])
            self.cp("act", kvf, self.ps[b][:, :], r=[("ps", b)], w=["kvf"])
            self.act(sq[:, 0:256], self.ps[b][:, 0:256], AF.Square, r=[("ps", b)], w=["sq"])
            self.red(st16[:, 0:4], sq[:, 0:256].rearrange("p (g d) -> p g d", g=4), r=["sq"], w=["st16"])
            self.rstd(st16[:, 0:4], st16[:, 0:4], 64, r=["st16"], w=["st16"])
            kv3 = kvf[:, 0:256].rearrange("p (g d) -> p g d", g=4)
            self.tt("dve", kv3, kv3, st16[:, 0:4].unsqueeze(2).broadcast_to([128, 4, 64]), ALU.mult, r=["kvf", "st16"], w=["kvf"])
            self.tt("dve", kv3, kv3, gk.unsqueeze(1).broadcast_to([128, 4, 64]), ALU.mult, r=["kvf", "cst"], w=["kvf"])
            if rope_n is not None:
                self.rope(kvf[:, 0:256].rearrange("p (h a b q) -> p h a b q", h=4, a=2, b=2, q=16), 4, 16, ropeg[:, rope_n, :, :],
                          [x[:, 0:4, :, :] for x in rtm], r=["kvf", "ropeg"], w=["kvf"], keyp="rtm")

        def q_attn_out(t, n, rope_n, kt_list):
            kkeys = [("KT", kt) for kt in kt_list]
            vkeys = [("VA", kt) for kt in kt_list] + ["VA1"]
            qb = []
            for bk in range(2):
                b = self.bank("mm")
                qb.append(b)
                for kc in range(8):
                    self.mm(self.ps[b][:, :], hT[:, kc, n * 128:(n + 1) * 128], w_in[:, kc, bk * 512:(bk + 1) * 512], kc == 0, kc == 7,
                            r=[("hT", n)] + wk, w=[("ps", b)])
                self.act(sq[:, bk * 512:(bk + 1) * 512], self.ps[b][:, :], AF.Square, r=[("ps", b)], w=["sq"])
            self.red(st16, sq.rearrange("p (g d) -> p g d", g=16), r=["sq"], w=["st16"])
            self.rstd(st16, st16, 64, r=["st16"], w=["st16"])
            for bk in range(2):
                self.tt("dve", qn[:, bk * 512:(bk + 1) * 512].rearrange("p (g d) -> p g d", g=8),
                        self.ps[qb[bk]][:, :].rearrange("p (g d) -> p g d", g=8),
                        st16[:, bk * 8:(bk + 1) * 8].unsqueeze(2).broadcast_to([128, 8, 64]), ALU.mult,
                        r=[("ps", qb[bk]), "st16"], w=["qn"])
            qn3 = qn.rearrange("p (g d) -> p g d", g=16)
            self.tt("pool", qn3, qn3, gq.unsqueeze(1).broadcast_to([128, 16, 64]), ALU.mult, r=["qn", "cst"], w=["qn"])
            if rope_n is not None:
                self.rope(qn.rearrange("p (h a b q) -> p h a b q", h=16, a=2, b=2, q=16), 16, 16, ropeg[:, rope_n, :, :], rtm,
                          r=["qn", "ropeg"], w=["qn"], keyp="rtm")
            self.cp("act", qnb, qn, r=["qn"], w=["qnb"])
            for hb in range(2):
                b = self.bank("tr")
                for hh in range(8):
                    h_ = hb * 8 + hh
                    self.tr(self.psb[b][0:64, hh * 128:(hh + 1) * 128], qnb[:, h_ * 64:(h_ + 1) * 64], self.ident_b,
                            r=["qnb", "identb"], w=[("ps", b)])
                self.cp("dve" if hb else "act", QT[0:64, hb * 8:(hb + 1) * 8, :],
                        self.psb[b][0:64, :].rearrange("p (a b) -> p a b", a=8), r=[("ps", b)], w=[("QT", hb)])
            for g in range(4):
                self.attention(
                    KT=lambda st, g=g: KT[0:64, g, st * 128:(st + 1) * 128], kparts=64, kt_list=kt_list,
                    QTv=QT[0:64, 4 * g:4 * g + 4, :], nq=4,
                    VA_of=lambda st, g=g: VA[:, st, g, :], scale=0.125, Pt=Pt,
                    out_of=lambda j, g=g: (ob[:, (4 * g + j) * 64:(4 * g + j + 1) * 64], ("ob", g)),
                    kr=kkeys, qr=[("QT", g // 2)], vr=vkeys, ow=None)
            b = self.bank("tr")
            for c in range(8):
                self.tr(self.psb[b][:, c * 128:(c + 1) * 128], ob[:, c * 128:(c + 1) * 128], self.ident_b,
                        r=[("ob", c // 2), "identb"], w=[("ps", b)])
            self.cp("act", OT, self.psb[b][:, :].rearrange("p (a b) -> p a b", a=8), r=[("ps", b)], w=["OT"])
            banks = []
            for hb in range(2):
                b = self.bank("mm")
                banks.append(b)
                for cc in range(4):
                    c = hb * 4 + cc
                    for kc in range(8):
                        self.mm(self.ps[b][:, cc * 128:(cc + 1) * 128], w_out[:, kc, c * 128:(c + 1) * 128], OT[:, kc, :], kc == 0, kc == 7,
                                r=["w_out", "OT"], w=[("ps", b)])
            self.mixer_residual(t, banks)

        cck = "cc_src_c%d" % l
        ccd = "cc_dst_c%d" % l
        for n, t in enumerate(self.sseg["tiles"]):
            kv_side(t, n, n)
            self.dma("sp", cc_src[n * 128:(n + 1) * 128, :], kvf, r=["kvf"], w=[cck], dq="ccb_c")
        self.allgather((l, "c"), r=[cck], w=[ccd])
        for sq_ in self.seqs:
            tiles = sq_["tiles"]
            bi = sq_["bidx"]
            for n, t in enumerate(tiles):
                kv_side(t, n, None)
                self.dma("sp", O["o_gk"][bi, i, n * 128:(n + 1) * 128, :], kvf[:, 0:256], r=["kvf"], w=[], dq="kvf_o")
                self.dma("sp", O["o_gv"][bi, i, n * 128:(n + 1) * 128, :], kvf[:, 256:512], r=["kvf"], w=[], dq="kvf_o")
                self.cp("act", knb, kvf[:, 0:256], r=["kvf"], w=["knb"])
                put_keys(knb, n, ["knb"])
                self.cp("pool", VA[:, n, :, 0:64], kvf[:, 256:512].rearrange("p (g d) -> p g d", g=4), r=["kvf"], w=[("VA", n)])
            for n, t in enumerate(tiles):
                q_attn_out(t, n, None, [0, 1])
        self.dma("sp", ck, I["c_gk"][i].rearrange("(t p) n -> p t n", p=128), r=[], w=["ck"], dq="ck")
        self.dma("sp", cv, I["c_gv"][i].rearrange("(t p) n -> p t n", p=128), r=[], w=["cv"], dq="cv")
        self.cp("act", ckb, ck, r=["ck"], w=["ckb"])
        for kt in range(4):
            put_keys(ckb[:, kt, :], kt, ["ckb"])
            self.cp("pool", VA[:, kt, :, 0:64], cv[:, kt, :].rearrange("p (g d) -> p g d", g=4), r=["cv"], w=[("VA", kt)])
        for k in range(8):
            kgk = kg[k % 2]
            kk = ("kg", k % 2)
            self.dma("sp", kgk, cc_dst[k * 128:(k + 1) * 128, :], r=[ccd], w=[kk], dq=kk)
            self.cp("act", knb, kgk[:, 0:256], r=[kk], w=["knb"])
            put_keys(knb, 4 + k, ["knb"])
            self.cp("pool", VA[:, 4 + k, :, 0:64], kgk[:, 256:512].rearrange("p (g d) -> p g d", g=4), r=[kk], w=[("VA", 4 + k)])
        for n, t in enumerate(self.sseg["tiles"]):
            self.normmod(t, 0, hT[:, :, n * 128:(n + 1) * 128], ("hT", n))
            q_attn_out(t, n, n, list(range(12)))

    def mixer_ab(self, l, i):
        A = self.A
        A.reset()
        I, O = self.I, self.O
        w_in = A.alloc([8, 2240], BF16)
        w_out = A.alloc([8, 1024], BF16)
        OG = A.alloc([4, NTOK], BF16)
        hTm = A.alloc([8, 128], BF16)
        hk = "hTm"
        self.alloc_norm_tmp()
        self.at_rden = A.alloc([4], F32)
        Pt = [A.alloc([4, 128], BF16) for _ in range(3)]
        self.pt_i = 0
        SP = dict(qlT=[A.alloc([2, 256], BF16) for _ in range(2)], oT=A.alloc([4, 256], F32), rsT=A.alloc([4, 256], BF16),
                  etot=A.alloc([4, 2], F32), cqb=A.alloc([2, 384], BF16), aseg=A.alloc([4], F32))
        base_off = A.off
        for kh in range(2):
            for ch in range(2):
                self.dma("pool", w_in[:, kh * 4:(kh + 1) * 4, ch * 1120:(ch + 1) * 1120],
                         I["ab_w_in"][i, kh * 512:(kh + 1) * 512, ch * 1120:(ch + 1) * 1120].rearrange("(k p) n -> p k n", p=128),
                         r=[], w=[("w_in", kh, ch)], dq="w_in")
        self.dma("pool", w_out, I["ab_w_out"][i].rearrange("(k p) n -> p k n", p=128), r=[], w=["w_out"], dq="w_out")
        wk = [("w_in", 0, 0), ("w_in", 0, 1), ("w_in", 1, 0), ("w_in", 1, 1)]
        gout = self.cst("gout%d" % i)
        gqn = self.cst("gqn%d" % i)
        gkvn = self.cst("gkvn%d" % i)
        gq96 = self.cst("gq96%d" % i)
        gk96 = self.cst("gk96%d" % i)
        sel = self.cst("sel")
        trim = [self.cst("trim0"), self.cst("trim1")]
        tris = [self.cst("tris0"), self.cst("tris1")]
        mask = [self.cst("mask0"), self.cst("mask1")]
        ccg_src, ccg_dst = self.CC[(l, "g")]
        ccm_src, ccm_dst = self.CC[(l, "m")]

        def g_alloc(NT, persist=None):
            T = NT * 128
            B = {}
            if persist is None:
                B["qlT"] = [A.alloc([2, T], BF16) for _ in range(2)]
                B["oT"] = A.alloc([4, T], F32)
                B["rsT"] = A.alloc([4, T], BF16)
                B["etot"] = A.alloc([4, NT], F32)
            else:
                for k in ("qlT", "oT", "rsT", "etot"):
                    B[k] = persist[k]
            B["klT"] = [A.alloc([2, T], BF16) for _ in range(2)]
            B["kst"] = [A.alloc([NT, 256], BF16) for _ in range(2)]
            B["vtk"] = A.alloc([NT, 512], BF16)
            B["Sst"] = [[A.alloc([128], F32) for _ in range(2)] for _ in range(2)]
            B["Sbf"] = [[A.alloc([128], BF16) for _ in range(2)] for _ in range(2)]
            B["alT"] = A.alloc([128], F32)
            B["lsp"] = A.alloc([512], F32)
            B["Eb"] = A.alloc([4, 128], F32)
            B["Enb"] = A.alloc([4, 128], F32)
            B["Ed2"] = A.alloc([512], F32)
            B["ATm"] = [A.alloc([128], BF16) for _ in range(2)]
            B["aw2"] = A.alloc([512], F32)
            self.memset("dve", B["alT"][32:33, :], 1.0, w=["alT1"])
            aw2 = B["aw2"]
            self.memset("dve", aw2[0:33, :], 0.0, w=["aw2"])
            for z in range(2):
                self.dma("sp", aw2[16 * z:16 * z + 16, z * 256:(z + 1) * 256], I["a_w2"][i, z], r=[], w=["aw2"], dq="aw2")
            self.dma("sp", aw2[32:33, :], I["a_b"][i:i + 1].rearrange("o z n -> o (z n)"), r=[], w=["aw2"], dq="aw2")
            return B

        def g_prep(tiles, B):
            qlT, klT, kst, vtk, rsT, etot = B["qlT"], B["klT"], B["kst"], B["vtk"], B["rsT"], B["etot"]
            alT, lsp, Eb, Enb, Ed2, aw2 = B["alT"], B["lsp"], B["Eb"], B["Enb"], B["Ed2"], B["aw2"]
            for n, t in enumerate(tiles):
                nc_ = slice(n * 128, (n + 1) * 128)
                h = hTm
                self.normmod(t, 0, h, hk)
                bqk = self.bank("mm")
                for ch in range(4):
                    for kc in range(8):
                        self.mm(self.ps[bqk][:, ch * 128:(ch + 1) * 128], w_in[:, kc, ch * 128:(ch + 1) * 128], h[:, kc, :], kc == 0, kc == 7,
                                r=[hk] + wk, w=[("ps", bqk)])
                br = self.bank("mm")
                for ch in range(4):
                    for kc in range(8):
                        self.mm(self.ps[br][:, ch * 128:(ch + 1) * 128], w_in[:, kc, 1024 + ch * 128:1024 + (ch + 1) * 128], h[:, kc, :], kc == 0, kc == 7,
                                r=[hk] + wk, w=[("ps", br)])
                self.act(rsT[:, :, nc_], self.ps[br][:, :].rearrange("p (a b) -> p a b", a=4), AF.Silu, r=[("ps", br)], w=[("rsT", n)])
                ba = self.bank("tr")
                for kc in range(8):
                    self.mm(self.ps[ba][0:32, 0:128], w_in[:, kc, 1536:1568], h[:, kc, :], kc == 0, kc == 7, r=[hk] + wk, w=[("ps", ba)])
                self.cp("act", alT[0:32, :], self.ps[ba][0:32, 0:128], r=[("ps", ba)], w=["alT"])
                bkv = self.bank("mm")
                for kc in range(8):
                    self.mm(self.ps[bkv][:, :], h[:, kc, :], w_in[:, kc, 256:768], kc == 0, kc == 7, r=[hk] + wk, w=[("ps", bkv)])
                bv2 = self.bank("mm")
                for kc in range(8):
                    self.mm(self.ps[bv2][:, 0:256], h[:, kc, :], w_in[:, kc, 768:1024], kc == 0, kc == 7, r=[hk] + wk, w=[("ps", bv2)])
                self.cp("act", vtk[:, n, 0:256], self.ps[bkv][:, 256:512], r=[("ps", bkv)], w=[("vtk", n, 0)])
                self.cp("act", vtk[:, n, 256:512], self.ps[bv2][:, 0:256], r=[("ps", bv2)], w=[("vtk", n, 1)])
                bl = self.bank("tr")
                self.mm(self.ps[bl][:, :], alT[0:33, :], aw2[0:33, :], True, True, r=["alT", "alT1", "aw2"], w=[("ps", bl)])
                self.act(lsp, self.ps[bl][:, :], AF.Exp, r=[("ps", bl)], w=["lsp"], scale=-1.0)
                self.act(lsp, lsp, AF.Ln, r=["lsp"], w=["lsp"], bias=1.0)
                bb = self.bank("tr")
                for z in range(2):
                    for fc in range(2):
                        zf = z * 2 + fc
                        self.mm(self.ps[bb][:, zf * 128:(zf + 1) * 128], lsp[:, zf * 128:(zf + 1) * 128], trim[z], True, True,
                                r=["lsp", "cst"], w=[("ps", bb)])
                bd = self.bank("tr")
                for z in range(2):
                    self.mm(self.ps[bd][:, z * 256:(z + 1) * 256], tris[z], lsp[:, z * 256:(z + 1) * 256], True, True,
                            r=["lsp", "cst"], w=[("ps", bd)])
                pbb = self.ps[bb][:, :].rearrange("p (a b) -> p a b", a=4)
                self.act(Eb, pbb, AF.Exp, r=[("ps", bb)], w=["Eb"])
                self.act(Enb, pbb, AF.Exp, r=[("ps", bb)], w=["Enb"], scale=-1.0)
                self.act(Ed2, self.ps[bd][:, :], AF.Exp, r=[("ps", bd)], w=["Ed2"])
                self.cp("pool", etot[:, 0:2, n], Eb[:, 0:2, 127], r=["Eb"], w=[("etot", n)])
                self.cp("pool", etot[:, 2:4, n], Eb[:, 2:4, 0], r=["Eb"], w=[("etot", n)])
                pqk = self.ps[bqk][:, :].rearrange("p (a b) -> p a b", a=4)
                for z in range(2):
                    self.stt(qlT[z][:, :, nc_], pqk[:, 0:2, :], 0.125, Eb[:, 2 * z:2 * z + 2, :], ALU.mult, ALU.mult,
                             r=[("ps", bqk), "Eb"], w=[("qlT", z, n)])
                    self.tt("dve", klT[z][:, :, nc_], pqk[:, 2:4, :], Enb[:, 2 * z:2 * z + 2, :], ALU.mult,
                            r=[("ps", bqk), "Enb"], w=[("klT", z, n)])
                    self.tt("dve", kst[z][:, n, :], self.ps[bkv][:, 0:256], Ed2[:, z * 256:(z + 1) * 256], ALU.mult,
                            r=[("ps", bkv), "Ed2"], w=[("kst", z, n)])

        def g_scan(NT, B, bidx):
            qlT, klT, kst, vtk, oT, etot, Sst, Sbf, ATm = (B[k] for k in ("qlT", "klT", "kst", "vtk", "oT", "etot", "Sst", "Sbf", "ATm"))
            for z in range(2):
                for fc in range(2):
                    self.memset("pool", Sst[z][fc], 0.0, w=[("S", z, fc)])
                    self.cp("act", Sbf[z][fc], Sst[z][fc], r=[("S", z, fc)], w=[("Sbf", z, fc)])
                order = list(range(NT)) if z == 0 else list(range(NT - 1, -1, -1))
                ai = 0
                for n in order:
                    nc_ = slice(n * 128, (n + 1) * 128)
                    bo = self.bank("acc")
                    for hh in range(4):
                        fc, hp = hh // 2, hh % 2
                        pr = slice(hp * 64, (hp + 1) * 64)
                        bat = self.bank("mm")
                        self.mm(self.ps[bat][:, 0:128], klT[z][pr, fc, nc_], qlT[z][pr, fc, nc_], True, True,
                                r=[("klT", z, n), ("qlT", z, n)], w=[("ps", bat)])
                        am = ATm[ai % 2]
                        amk = ("ATm", ai % 2)
                        ai += 1
                        self.tt("dve", am, self.ps[bat][:, 0:128], mask[z], ALU.mult, r=[("ps", bat), "cst"], w=[amk])
                        self.mm(self.ps[bo][:, hh * 128:(hh + 1) * 128], vtk[:, n, hh * 128:(hh + 1) * 128], am, True, False,
                                r=[("vtk", n, 0), ("vtk", n, 1), amk], w=[("ps", bo)])
                        self.mm(self.ps[bo][:, hh * 128:(hh + 1) * 128], Sbf[z][fc][pr, :], qlT[z][pr, fc, nc_], False, True,
                                r=[("Sbf", z, fc), ("qlT", z, n)], w=[("ps", bo)])
                    pbo = self.ps[bo][:, :].rearrange("p (a b) -> p a b", a=4)
                    if z == 0:
                        self.cp("act", oT[:, :, nc_], pbo, r=[("ps", bo)], w=[("oT", n)])
                    else:
                        self.tt("dve", oT[:, :, nc_], oT[:, :, nc_], pbo, ALU.add, r=[("ps", bo), ("oT", n)], w=[("oT", n)])
                    for fc in range(2):
                        bu = self.bank("mm")
                        for hp in range(2):
                            hh = fc * 2 + hp
                            self.mm(self.ps[bu][hp * 64:(hp + 1) * 64, 0:128], kst[z][:, n, hh * 64:(hh + 1) * 64],
                                    vtk[:, n, hh * 128:(hh + 1) * 128], True, True,
                                    r=[("kst", z, n), ("vtk", n, 0), ("vtk", n, 1)], w=[("ps", bu)])
                        self.stt(Sst[z][fc], Sst[z][fc], etot[:, z * 2 + fc, n:n + 1], self.ps[bu][:, 0:128], ALU.mult, ALU.add,
                                 r=[("S", z, fc), ("etot", n), ("ps", bu)], w=[("S", z, fc)])
                        self.cp("act", Sbf[z][fc], Sst[z][fc], r=[("S", z, fc)], w=[("Sbf", z, fc)])
                if bidx is not None:
                    for fc in range(2):
                        self.dma("sp", O["o_gla"][bidx, i, z, fc * 128:(fc + 1) * 128, :], Sst[z][fc],
                                 r=[("S", z, fc)], w=[], dq=("S", z, fc))

        def g_out(tiles, oT, rsT):
            osq = A.alloc([4, 128], BF16)
            orst = A.alloc([4, 128], F32)
            otmp = A.alloc([4, 128], F32)
            for n, t in enumerate(tiles):
                nc_ = slice(n * 128, (n + 1) * 128)
                self.act(osq, oT[:, :, nc_], AF.Square, r=[("oT", n)], w=["osq"])
                b = self.bank("tr")
                self.mm(self.ps[b][:, :], self.ones_b, osq.rearrange("p a b -> p (a b)"), True, True, r=["osq", "ones"], w=[("ps", b)])
                self.rstd(orst, self.ps[b][:, :].rearrange("p (a b) -> p a b", a=4), 128, r=[("ps", b)], w=["orst"])
                self.tt("dve", otmp, oT[:, :, nc_], orst, ALU.mult, r=[("oT", n), "orst"], w=["otmp"])
                self.stt(OG[:, :, t * 128:(t + 1) * 128], otmp, gout, rsT[:, :, nc_], ALU.mult, ALU.mult,
                         r=["otmp", "cst", ("rsT", n)], w=[("OG", t)])

        def m_alloc(NK):
            M = {}
            M["w_qb"] = A.alloc([3, 768], BF16)
            M["w_kvb"] = A.alloc([2, 1024], BF16)
            self.dma("pool", M["w_qb"], I["w_qb"][i].rearrange("(k p) n -> p k n", p=128), r=[], w=["w_qb"], dq="w_qb")
            self.dma("pool", M["w_kvb"], I["w_kvb"][i].rearrange("(k p) n -> p k n", p=128), r=[], w=["w_kvb"], dq="w_kvb")
            M["KTm"] = A.alloc([8, NK * 128], BF16)
            M["VAm"] = A.alloc([NK, 8, 65], BF16)
            M["QTm"] = A.alloc([8, 256], BF16)
            M["omb"] = A.alloc([2, 512], BF16)
            M["OM"] = A.alloc([4, 256], BF16)
            M["cb"] = A.alloc([256], BF16)
            M["cT"] = A.alloc([2, 128], BF16)
            M["kc96"] = A.alloc([8, 96], F32)
            M["knb"] = A.alloc([8, 96], BF16)
            M["st8"] = A.alloc([8], F32)
            M["cqT"] = A.alloc([3, 128], BF16)
            M["rtm"] = [A.alloc([8, 2, 8], F32) for _ in range(4)]
            self.memset("pool", M["VAm"][:, :, :, 64:65], 1.0, w=["VA1"])
            return M

        def own_alloc():
            W = {}
            W["sq96"] = A.alloc([8, 96], F32)
            W["ckvf"] = A.alloc([256], F32)
            W["ckvn"] = A.alloc([256], F32)
            W["kpe"] = A.alloc([32], F32)
            W["st1"] = A.alloc([2], F32)
            return W

        def norm96(M, sq96, gain, rope_cs):
            kc96, knb, st8, rtm = M["kc96"], M["knb"], M["st8"], M["rtm"]
            self.act(sq96, kc96, AF.Square, r=["kc96"], w=["sq96"])
            self.red(st8, sq96, r=["sq96"], w=["st8"])
            self.rstd(st8, st8, 96, r=["st8"], w=["st8"])
            self.tt("dve", kc96, kc96, st8.unsqueeze(2).broadcast_to([128, 8, 96]), ALU.mult, r=["kc96", "st8"], w=["kc96"])
            self.tt("pool", kc96, kc96, gain.unsqueeze(1).broadcast_to([128, 8, 96]), ALU.mult, r=["kc96", "cst"], w=["kc96"])
            if rope_cs is not None:
                cs, csk = rope_cs
                self.rope(kc96[:, :, 64:96].rearrange("p h (a b q) -> p h a b q", a=2, b=2, q=8), 8, 8, cs, rtm,
                          r=["kc96", csk], w=["kc96"], keyp="rtm")
            self.cp("act", knb, kc96, r=["kc96"], w=["knb"])

        def kside(M, sq96, ckvn_ap, ckvn_k, kpe_ap, kpe_k, kt, rope_cs):
            cb, cT, kc96, knb, KTm, VAm, w_kvb = M["cb"], M["cT"], M["kc96"], M["knb"], M["KTm"], M["VAm"], M["w_kvb"]
            self.cp("act", cb, ckvn_ap, r=[ckvn_k], w=["cb"])
            b = self.bank("tr")
            for kc in range(2):
                self.tr(self.psb[b][:, kc * 128:(kc + 1) * 128], cb[:, kc * 128:(kc + 1) * 128], self.ident_b, r=["cb", "identb"], w=[("ps", b)])
            self.cp("dve", cT, self.psb[b][:, 0:256].rearrange("p (a b) -> p a b", a=2), r=[("ps", b)], w=["cT"])
            for bk in range(2):
                b = self.bank("mm")
                for kc in range(2):
                    self.mm(self.ps[b][:, :], cT[:, kc, :], w_kvb[:, kc, bk * 512:(bk + 1) * 512], kc == 0, kc == 1,
                            r=["cT", "w_kvb"], w=[("ps", b)])
                pv = self.ps[b][:, :].rearrange("p (h d) -> p h d", h=4)
                self.cp("act", kc96[:, bk * 4:(bk + 1) * 4, 0:64], pv[:, :, 0:64], r=[("ps", b)], w=["kc96"])
                self.cp("dve", VAm[:, kt, bk * 4:(bk + 1) * 4, 0:64], pv[:, :, 64:128], r=[("ps", b)], w=[("VAm", kt)])
            self.cp("pool", kc96[:, :, 64:96], kpe_ap.unsqueeze(1).broadcast_to([128, 8, 32]), r=[kpe_k], w=["kc96"])
            norm96(M, sq96, gk96, rope_cs)
            b = self.bank("tr")
            for hh in range(8):
                self.tr(self.psb[b][0:96, hh * 128:(hh + 1) * 128], knb[:, hh, :], self.ident_b, r=["knb", "identb"], w=[("ps", b)])
            self.cp("dve", KTm[0:96, :, kt * 128:(kt + 1) * 128], self.psb[b][0:96, :].rearrange("p (a b) -> p a b", a=8),
                    r=[("ps", b)], w=[("KTm", kt)])

        def m_own(t, n, W, cqb):
            sq96, ckvf, ckvn, kpe, st1 = W["sq96"], W["ckvf"], W["ckvn"], W["kpe"], W["st1"]
            h = hTm
            self.normmod(t, 0, h, hk)
            b1 = self.bank("mm")
            for kc in range(8):
                self.mm(self.ps[b1][:, :], h[:, kc, :], w_in[:, kc, 1568:2080], kc == 0, kc == 7, r=[hk] + wk, w=[("ps", b1)])
            b2 = self.bank("mm")
            for kc in range(8):
                self.mm(self.ps[b2][:, 0:160], h[:, kc, :], w_in[:, kc, 2080:2240], kc == 0, kc == 7, r=[hk] + wk, w=[("ps", b2)])
            sqf = sq96.rearrange("p a b -> p (a b)")
            self.act(sqf[:, 0:384], self.ps[b1][:, 0:384], AF.Square, r=[("ps", b1)], w=["sq96", "st1"], accum_out=st1[:, 0:1])
            self.rstd(st1[:, 0:1], st1[:, 0:1], 384, r=["st1"], w=["st1"])
            self.stt(cqb[:, n, :], self.ps[b1][:, 0:384], st1[:, 0:1], gqn, ALU.mult, ALU.mult, r=[("ps", b1), "st1", "cst"], w=[("cqb", n)])
            self.cp("act", ckvf[:, 0:128], self.ps[b1][:, 384:512], r=[("ps", b1)], w=["ckvf"])
            self.cp("act", ckvf[:, 128:256], self.ps[b2][:, 0:128], r=[("ps", b2)], w=["ckvf"])
            self.cp("dve", kpe, self.ps[b2][:, 128:160], r=[("ps", b2)], w=["kpe"])
            self.act(sqf[:, 0:256], ckvf, AF.Square, r=["ckvf"], w=["sq96", "st1b"], accum_out=st1[:, 1:2])
            self.rstd(st1[:, 1:2], st1[:, 1:2], 256, r=["st1b"], w=["st1b"])
            self.stt(ckvn, ckvf, st1[:, 1:2], gkvn, ALU.mult, ALU.mult, r=["ckvf", "st1b", "cst"], w=["ckvn"])

        def m_attn(M, sq96, tiles, cqb, rope_q, NK):
            QTm, omb, OM, KTm, VAm, cqT, kc96, knb, w_qb = (M[k] for k in ("QTm", "omb", "OM", "KTm", "VAm", "cqT", "kc96", "knb", "w_qb"))
            kt_list = list(range(NK))
            kkeys = [("KTm", kt) for kt in kt_list]
            vkeys = [("VAm", kt) for kt in kt_list] + ["VA1"]
            nq = len(tiles)
            for n, t in enumerate(tiles):
                b = self.bank("tr")
                for kc in range(3):
                    self.tr(self.psb[b][:, kc * 128:(kc + 1) * 128], cqb[:, n, kc * 128:(kc + 1) * 128], self.ident_b,
                            r=[("cqb", n), "identb"], w=[("ps", b)])
                self.cp("act", cqT, self.psb[b][:, 0:384].rearrange("p (a b) -> p a b", a=3), r=[("ps", b)], w=["cqT"])
                for bk in range(2):
                    b = self.bank("mm")
                    for kc in range(3):
                        self.mm(self.ps[b][:, 0:384], cqT[:, kc, :], w_qb[:, kc, bk * 384:(bk + 1) * 384], kc == 0, kc == 2,
                                r=["cqT", "w_qb"], w=[("ps", b)])
                    self.cp("act", kc96[:, bk * 4:(bk + 1) * 4, :], self.ps[b][:, 0:384].rearrange("p (h d) -> p h d", h=4),
                            r=[("ps", b)], w=["kc96"])
                norm96(M, sq96, gq96, None if rope_q is None else (rope_q[0][:, n, :, :], rope_q[1]))
                b = self.bank("tr")
                for hh in range(8):
                    self.tr(self.psb[b][0:96, hh * 128:(hh + 1) * 128], knb[:, hh, :], self.ident_b, r=["knb", "identb"], w=[("ps", b)])
                self.cp("dve", QTm[0:96, :, n * 128:(n + 1) * 128], self.psb[b][0:96, :].rearrange("p (a b) -> p a b", a=8),
                        r=[("ps", b)], w=[("QTm", n)])
            for hh in range(8):
                self.attention(
                    KT=lambda st, hh=hh: KTm[0:96, hh, st * 128:(st + 1) * 128], kparts=96, kt_list=kt_list,
                    QTv=QTm[0:96, hh, 0:nq * 128], nq=nq,
                    VA_of=lambda st, hh=hh: VAm[:, st, hh, :], scale=float(96 ** -0.5), Pt=Pt,
                    out_of=lambda j, hh=hh: (omb[:, j, hh * 64:(hh + 1) * 64], ("omb", j)),
                    kr=kkeys, qr=[("QTm", j) for j in range(nq)], vr=vkeys, ow=None)
            for n, t in enumerate(tiles):
                b = self.bank("tr")
                for c in range(4):
                    self.tr(self.psb[b][:, c * 128:(c + 1) * 128], omb[:, n, c * 128:(c + 1) * 128], self.ident_b,
                            r=[("omb", n), "identb"], w=[("ps", b)])
                self.cp("act", OM[:, :, n * 128:(n + 1) * 128], self.psb[b][:, 0:512].rearrange("p (a b) -> p a b", a=4),
                        r=[("ps", b)], w=[("OM", n)])
                banks = []
                for hb in range(2):
                    b = self.bank("mm")
                    banks.append(b)
                    for cc in range(4):
                        c = hb * 4 + cc
                        for kc in range(8):
                            rhs = OG[:, kc, t * 128:(t + 1) * 128] if kc < 4 else OM[:, kc - 4, n * 128:(n + 1) * 128]
                            self.mm(self.ps[b][:, cc * 128:(cc + 1) * 128], w_out[:, kc, c * 128:(c + 1) * 128],
                                    rhs, kc == 0, kc == 7,
                                    r=["w_out", ("OG", t), ("OM", n)], w=[("ps", b)])
                self.mixer_residual(t, banks)

        st_ = self.sseg["tiles"]
        A.reset(base_off)
        B = g_alloc(2, persist=SP)
        g_prep(st_, B)
        g_scan(2, B, None)
        ccgs, ccgd = "cc_src_g%d" % l, "cc_dst_g%d" % l
        gsrc = A.alloc([4, 129], F32)
        self.tt("dve", gsrc[:, :, 128], SP["etot"][:, :, 0], SP["etot"][:, :, 1], ALU.mult, r=[("etot", 0), ("etot", 1)], w=["gsrc_a"])
        for z in range(2):
            for fc in range(2):
                zf = z * 2 + fc
                self.cp("act", gsrc[:, zf, 0:128], B["Sst"][z][fc], r=[("S", z, fc)], w=[("gsrc", zf)])
        self.dma("sp", ccg_src.ap().rearrange("(zf p) c -> p zf c", p=128), gsrc,
                 r=["gsrc_a"] + [("gsrc", zf) for zf in range(4)], w=[ccgs], dq="bnc_g")
        self.allgather((l, "g"), r=[ccgs], w=[ccgd])
        self.P.barrier()
        if "x1" in self.stages:
            return
        A.reset(base_off)
        W = own_alloc()
        ccms, ccmd = "cc_src_m%d" % l, "cc_dst_m%d" % l
        for n, t in enumerate(st_):
            m_own(t, n, W, SP["cqb"])
            self.dma("sp", ccm_src[n * 128:(n + 1) * 128, 0:256], W["ckvn"], r=["ckvn"], w=[ccms], dq="bnc_m")
            self.dma("sp", ccm_src[n * 128:(n + 1) * 128, 256:288], W["kpe"], r=["kpe"], w=[ccms], dq="bnc_m")
        self.allgather((l, "m"), r=[ccms], w=[ccmd])
        self.P.barrier()
        if "x2" in self.stages:
            return

        for sq_ in self.seqs:
            tiles = sq_["tiles"]
            bi = sq_["bidx"]
            A.reset(base_off)
            B = g_alloc(2)
            g_prep(tiles, B)
            g_scan(2, B, bi)
            g_out(tiles, B["oT"], B["rsT"])
            self.P.barrier()
            A.reset(base_off)
            M = m_alloc(2)
            W = own_alloc()
            cqb = A.alloc([2, 384], BF16)
            for n, t in enumerate(tiles):
                m_own(t, n, W, cqb)
                self.dma("sp", O["o_ckv"][bi, i, n * 128:(n + 1) * 128, :], W["ckvn"], r=["ckvn"], w=[], dq="ckvn_o")
                self.dma("sp", O["o_kpe"][bi, i, n * 128:(n + 1) * 128, :], W["kpe"], r=["kpe"], w=[], dq="kpe_o")
                kside(M, W["sq96"], W["ckvn"], "ckvn", W["kpe"], "kpe", n, None)
            m_attn(M, W["sq96"], tiles, cqb, None, 2)
            self.P.barrier()

        if "x3" in self.stages:
            return
        A.reset(base_off)
        gU = A.alloc([16, 129], F32)
        self.dma("sp", gU, ccg_dst.ap().rearrange("(rz p) c -> p rz c", p=128), r=[ccgd], w=["gU"], dq="gU")
        Sin = [[A.alloc([128], F32) for _ in range(2)] for _ in range(2)]
        Sib = [[A.alloc([128], BF16) for _ in range(2)] for _ in range(2)]
        tS = A.alloc([128], F32)
        for z in range(2):
            for fc in range(2):
                zf = z * 2 + fc
                S_ = Sin[z][fc]
                sk = ("Sin", z, fc)
                self.dma("sp", S_, I["c_gla"][i, z, fc * 128:(fc + 1) * 128, :], r=[], w=[sk], dq=sk)
                ranks = [0, 1, 2, 3] if z == 0 else [3, 2, 1, 0]
                for k in ranks:
                    gi = k * 4 + zf
                    self.stt(tS, S_, gU[:, gi, 128:129], gU[:, gi, 0:128], ALU.mult, ALU.add, r=[sk, "gU"], w=["tS"])
                    self.tt("dve", tS, tS, S_, ALU.subtract, r=["tS", sk], w=["tS"])
                    self.stt(S_, tS, sel[:, z * 4 + k:z * 4 + k + 1], S_, ALU.mult, ALU.add, r=["tS", sk, "cst"], w=[sk])
                self.cp("act", Sib[z][fc], S_, r=[sk], w=[("Sib", z, fc)])
        if "x5" in self.stages:
            self.P.barrier()
            return
        for z in range(2):
            order = [0, 1] if z == 0 else [1, 0]
            for oi, n in enumerate(order):
                nc_ = slice(n * 128, (n + 1) * 128)
                bh = [self.bank("acc"), self.bank("acc")]
                for hp in range(2):
                    pr = slice(hp * 64, (hp + 1) * 64)
                    for fc in range(2):
                        self.mm(self.ps[bh[hp]][:, fc * 128:(fc + 1) * 128], Sib[z][fc][pr, :], SP["qlT"][z][pr, fc, nc_], True, True,
                                r=[("Sib", z, fc), ("qlT", z, n)], w=[("ps", bh[hp])])
                oview = SP["oT"][:, :, nc_].rearrange("p (f h) t -> p f h t", f=2, h=2)
                for hp in range(2):
                    pb = self.ps[bh[hp]][:, 0:256].rearrange("p (a b) -> p a b", a=2)
                    self.tt("dve", oview[:, :, hp, :], oview[:, :, hp, :], pb, ALU.add, r=[("ps", bh[hp]), ("oT", n)], w=[("oT", n)])
                if oi == 0:
                    for fc in range(2):
                        zf = z * 2 + fc
                        sk = ("Sin", z, fc)
                        self.P.op("dve", lambda h, S_=Sin[z][fc], e=SP["etot"][:, zf, n:n + 1]: h.tensor_scalar(S_, S_, e, None, ALU.mult),
                                  r=[sk, ("etot", n)], w=[sk])
                        self.cp("act", Sib[z][fc], Sin[z][fc], r=[sk], w=[("Sib", z, fc)])
        if "x6" in self.stages:
            self.P.barrier()
            return
        g_out(st_, SP["oT"], SP["rsT"])
        self.P.barrier()
        if "x4" in self.stages:
            return
        A.reset(base_off)
        M = m_alloc(12)
        sq96 = A.alloc([8, 96], F32)
        ropem_all = A.alloc([8, 2, 16], F32)
        ropem_own = A.alloc([2, 2, 16], F32)
        ckp = A.alloc([4, 32], F32)
        cck = A.alloc([256], F32)
        kgm = [A.alloc([288], F32) for _ in range(2)]
        self.dma("sp", ropem_all, I["ropem_all"].rearrange("(t p) c q -> p t c q", p=128), r=[], w=["ropem_all"], dq="ropem")
        self.dma("sp", ropem_own, I["ropem"].rearrange("(t p) c q -> p t c q", p=128), r=[], w=["ropem_own"], dq="ropem")
        self.dma("sp", ckp, I["c_kpe"][i].rearrange("(t p) n -> p t n", p=128), r=[], w=["ckp"], dq="ckp")
        for kt in range(4):
            self.dma("sp", cck, I["c_ckv"][i, kt * 128:(kt + 1) * 128, :], r=[], w=["cck"], dq="cck")
            kside(M, sq96, cck, "cck", ckp[:, kt, :], "ckp", kt, None)
        for k in range(8):
            kg_ = kgm[k % 2]
            kk = ("kgm", k % 2)
            self.dma("sp", kg_, ccm_dst[k * 128:(k + 1) * 128, :], r=[ccmd], w=[kk], dq=kk)
            kside(M, sq96, kg_[:, 0:256], kk, kg_[:, 256:288], kk, 4 + k, (ropem_all[:, k, :, :], "ropem_all"))
        m_attn(M, sq96, st_, SP["cqb"], (ropem_own, "ropem_own"), 12)


def _rope_tables(n_tok, d_rot):
    t = np.arange(n_tok, dtype=np.int32)
    pos = np.stack([t // 64, t % 64], axis=-1).astype(np.float32)
    quarter = d_rot // 4
    inv = np.power(np.float32(10000.0), -np.arange(quarter, dtype=np.float32) / np.float32(quarter)).astype(np.float32)
    ang = pos[:, :, None] * inv
    cos = np.cos(ang).astype(np.float32).reshape(n_tok, 2 * quarter)
    sin = np.sin(ang).astype(np.float32).reshape(n_tok, 2 * quarter)
    return np.ascontiguousarray(np.stack([cos, sin], axis=1))


def _build_cst(inp, b, j):
    c = np.zeros((128, NCST), np.float32)

    def put(name, arr, parts=128):
        o, n = CST_OFF[name]
        c[0:parts, o:o + n] = np.asarray(arr, np.float32).reshape(parts, n)

    s = np.arange(128)[:, None]
    t = np.arange(128)[None, :]
    v = np.float32(-1.0 / 16.0)
    put("ident", np.eye(128))
    put("trim0", (s <= t) * v)
    put("trim1", (s >= t) * v)
    put("tris0", (s > t) * v)
    put("tris1", (s < t) * v)
    put("mask0", (s <= t) * 1.0)
    put("mask1", (s >= t) * 1.0)
    put("rm", np.array([[1, 0], [0, 1], [1, 1]], np.float32), parts=3)
    cond = np.stack([inp["c_ctx"].reshape(8, 128).T, inp["c"][b].reshape(8, 128).T], axis=-1)
    put("cond", cond)
    selv = np.array([1.0 if k < j else 0.0 for k in range(4)] + [1.0 if k > j else 0.0 for k in range(4)], np.float32)
    put("sel", np.broadcast_to(selv[None, :], (128, 8)))
    put("gmix", inp["norm_mix_g"].reshape(4, 8, 128).transpose(2, 0, 1))
    put("gffn", inp["norm_ffn_g"].reshape(4, 8, 128).transpose(2, 0, 1))
    for i in range(2):
        put("gq%d" % i, np.broadcast_to(inp["gqa_qn_g"][i][None, :], (128, 64)))
        put("gk%d" % i, np.broadcast_to(inp["gqa_kn_g"][i][None, :], (128, 64)))
        put("gout%d" % i, inp["gla_out_g"][i].reshape(128, 1))
        put("gqn%d" % i, np.broadcast_to(inp["mla_q_norm_g"][i][None, :], (128, 384)))
        put("gkvn%d" % i, np.broadcast_to(inp["mla_kv_norm_g"][i][None, :], (128, 256)))
        put("gq96%d" % i, np.broadcast_to(inp["mla_qn_g"][i][None, :], (128, 96)))
        put("gk96%d" % i, np.broadcast_to(inp["mla_kn_g"][i][None, :], (128, 96)))
    return c


_NC_CACHE = {}


def _get_nc(debug=(), stages=None):
    key = (tuple(sorted(debug)), None if stages is None else tuple(sorted(stages)))
    if key not in _NC_CACHE:
        kb = KB(debug, stages)
        nc = kb.build()
        _NC_CACHE[key] = (nc, kb)
    return _NC_CACHE[key]


def make_in_maps(inp):
    inp = {k: np.ascontiguousarray(np.asarray(v)) for k, v in inp.items()}
    ropeg = _rope_tables(1024, 64)
    ropem = _rope_tables(1024, 32)
    shared = dict(
        ffn_w_in=inp["ffn_w_in"], ffn_w_out=inp["ffn_w_out"],
        ab_w_in=inp["ab_w_in"], ab_w_out=inp["ab_w_out"], a_w2=inp["gla_a_w2"], a_b=inp["gla_a_b"],
        w_qb=inp["mla_w_qb"], w_kvb=inp["mla_w_kvb"], gqa_w_in=inp["gqa_w_in"], gqa_w_out=inp["gqa_w_out"])
    in_maps = []
    for core in range(8):
        b, j = core // 4, core % 4
        m = dict(shared)
        m["ropem_all"] = ropem
        m["ada_w"] = np.ascontiguousarray(inp["ada_w"][:, :, j * 1536:(j + 1) * 1536])
        m["ada_b"] = np.ascontiguousarray(inp["ada_b"][:, j * 1536:(j + 1) * 1536])
        m["ropeg"] = np.ascontiguousarray(ropeg[j * 256:(j + 1) * 256])
        m["ropem"] = np.ascontiguousarray(ropem[j * 256:(j + 1) * 256])
        m["cst"] = _build_cst(inp, b, j)
        m["xp"] = np.ascontiguousarray(inp["x_prompt"][core * 4:(core + 1) * 4].reshape(1024, D))
        m["xs"] = np.ascontiguousarray(inp["x_sample"][b, j * 256:(j + 1) * 256])
        m["c_ckv"] = np.ascontiguousarray(inp["cache_mla_ckv"][b])
        m["c_kpe"] = np.ascontiguousarray(inp["cache_mla_kpe"][b])
        m["c_gla"] = np.ascontiguousarray(inp["state_gla"][b].reshape(2, 2, 256, 128))
        m["c_gk"] = np.ascontiguousarray(inp["cache_gqa_k"][b].reshape(2, 512, 256))
        m["c_gv"] = np.ascontiguousarray(inp["cache_gqa_v"][b].reshape(2, 512, 256))
        in_maps.append(m)
    return in_maps


def kernel(**inputs):
    nc, kb = _get_nc()
    in_maps = make_in_maps(inputs)
    res = run_bass_kernel_spmd(nc, in_maps, core_ids=list(range(8)))
    R = res.results
    y_prompt = np.concatenate([R[c]["yp"].reshape(4, 256, D) for c in range(8)], axis=0)
    y_sample = np.stack([np.concatenate([R[4 * b + j]["ys"] for j in range(4)], axis=0) for b in range(2)], axis=0)
    new_ckv = np.concatenate([R[c]["o_ckv"] for c in range(8)], axis=0)
    new_kpe = np.concatenate([R[c]["o_kpe"] for c in range(8)], axis=0)
    new_gla = np.concatenate([R[c]["o_gla"].reshape(4, 2, 2, 4, 64, 128) for c in range(8)], axis=0)
    new_k = np.concatenate([R[c]["o_gk"].reshape(4, 2, 256, 4, 64) for c in range(8)], axis=0)
    new_v = np.concatenate([R[c]["o_gv"].reshape(4, 2, 256, 4, 64) for c in range(8)], axis=0)
    outs = (y_prompt, y_sample, new_ckv, new_kpe, new_gla, new_k, new_v)
    return tuple(np.ascontiguousarray(o, dtype=np.float32) for o in outs)
```

```python
import bisect
import contextlib
import numpy as np
import concourse.bass as bass
import concourse.mybir as mybir
from concourse.bass_utils import run_bass_kernel_spmd

F32 = mybir.dt.float32
BF16 = mybir.dt.bfloat16
AF = mybir.ActivationFunctionType
ALU = mybir.AluOpType
AX = mybir.AxisListType

ENGS = ("pe", "act", "dve", "pool", "sp")
RAW, WAR, WAW = 1, 2, 4
EPS = 1e-6
D = 1024
FH = 2816
ARENA_BYTES = 152 * 1024
NTOK = 1280
NTILE = 10


class Prog:
    def __init__(self):
        self.ops = []
        self.last_w = {}
        self.readers = {}

    def op(self, eng, fn, r=(), w=(), dq=None, inc=16):
        i = len(self.ops)
        deps = {}
        psr = [k for k in r if isinstance(k, tuple) and k and k[0] == "ps"]
        if psr:
            r = [k for k in r if not (isinstance(k, tuple) and k and k[0] == "ps")]
            w = list(w) + [k for k in psr if k not in w]
            for k in psr:
                lw = self.last_w.get(k)
                if lw is not None:
                    deps[lw] = deps.get(lw, 0) | RAW
        for k in r:
            lw = self.last_w.get(k)
            if lw is not None:
                deps[lw] = deps.get(lw, 0) | RAW
        for k in w:
            lw = self.last_w.get(k)
            if lw is not None:
                deps[lw] = deps.get(lw, 0) | WAW
            for rd in self.readers.get(k, ()):
                if rd != i:
                    deps[rd] = deps.get(rd, 0) | WAR
        for k in r:
            self.readers.setdefault(k, []).append(i)
        for k in w:
            self.last_w[k] = i
            self.readers[k] = []
        deps.pop(i, None)
        self.ops.append(dict(eng=eng, fn=fn, deps=deps, dq=dq, bar=None, inc=inc))
        return i

    def barrier(self):
        first = len(self.ops)
        for e in ENGS:
            self.ops.append(dict(eng=e, fn="drain", deps={}, dq=None, bar=("sig", first)))
        sig_ids = list(range(first, first + len(ENGS)))
        for e in ENGS:
            self.ops.append(dict(eng=e, fn="nop", deps={s: RAW for s in sig_ids}, dq=None, bar=("wait", first)))
        iscc = lambda k: isinstance(k, str) and k.startswith("cc")
        self.last_w = {k: v for k, v in self.last_w.items() if iscc(k)}
        self.readers = {k: v for k, v in self.readers.items() if iscc(k)}

    def emit(self, nc, es):
        ops = self.ops
        n = len(ops)
        needed = [False] * n
        for i, o in enumerate(ops):
            kept = []
            for d, kind in o["deps"].items():
                od = ops[d]
                if od["dq"] is None and o["dq"] is None and od["eng"] == o["eng"] and o["bar"] is None:
                    if o["eng"] == "pe":
                        continue
                kept.append(d)
                needed[d] = True
            o["kdeps"] = kept
        eng_sem = {e: es.enter_context(nc.semaphore("sem_" + e)) for e in ENGS}
        dq_keys = []
        seen = set()
        for o in ops:
            if o["dq"] is not None and o["dq"] not in seen:
                seen.add(o["dq"])
                dq_keys.append(o["dq"])
        dq_sem = {k: es.enter_context(nc.semaphore("dq_%d" % j)) for j, k in enumerate(dq_keys)}
        dq_idx = {k: [] for k in dq_keys}
        dq_cum = {k: [0] for k in dq_keys}
        eng_cnt = {e: 0 for e in ENGS}

        def dq_before(k, i):
            return dq_cum[k][bisect.bisect_left(dq_idx[k], i)]

        for i, o in enumerate(ops):
            if o["dq"] is not None:
                dq_idx[o["dq"]].append(i)
                dq_cum[o["dq"]].append(dq_cum[o["dq"]][-1] + o["inc"])
                o["sig"] = ("dq", o["dq"])
            elif needed[i] or (o["bar"] is not None and o["bar"][0] == "sig"):
                eng_cnt[o["eng"]] += 1
                o["sig"] = ("eng", o["eng"], eng_cnt[o["eng"]])
            else:
                o["sig"] = None
        per_eng = {e: [] for e in ENGS}
        for i, o in enumerate(ops):
            per_eng[o["eng"]].append(i)
        self.n_sems = len(ENGS) + len(dq_keys)
        self.counts = {e: len(per_eng[e]) for e in ENGS}

        def run(e, h):
            waited = {}
            for i in per_eng[e]:
                o = ops[i]
                waits = {}
                for d in o["kdeps"]:
                    od = ops[d]
                    if od["dq"] is not None:
                        k = od["dq"]
                        cnt = dq_before(k, i)
                        key = ("dq", k)
                        waits[key] = max(waits.get(key, 0), cnt)
                    else:
                        key = ("eng", od["eng"])
                        waits[key] = max(waits.get(key, 0), od["sig"][2])
                if o["bar"] is not None and o["bar"][0] == "sig" and e == "sp":
                    for k in dq_keys:
                        if isinstance(k, str) and k.startswith("cc"):
                            continue
                        cnt = dq_before(k, i)
                        if cnt:
                            waits[("dq", k)] = cnt
                for key, v in waits.items():
                    if waited.get(key, 0) >= v:
                        continue
                    waited[key] = v
                    sem = dq_sem[key[1]] if key[0] == "dq" else eng_sem[key[1]]
                    h.wait_ge(sem, v)
                if o["fn"] == "drain":
                    inst = h.nop() if e == "sp" else h.drain()
                elif o["fn"] == "nop":
                    inst = None
                else:
                    inst = o["fn"](h)
                s = o["sig"]
                if s is not None:
                    if s[0] == "dq":
                        inst.then_inc(dq_sem[s[1]], o["inc"])
                    else:
                        inst.then_inc(eng_sem[s[1]], 1)
            if e == "sp":
                for k in dq_keys:
                    h.wait_ge(dq_sem[k], dq_cum[k][-1])

        with nc.Block() as block:
            @block.tensor
            def _(h):
                run("pe", h)

            @block.scalar
            def _(h):
                run("act", h)

            @block.vector
            def _(h):
                run("dve", h)

            @block.gpsimd
            def _(h):
                run("pool", h)

            @block.sync
            def _(h):
                run("sp", h)


class Arena:
    def __init__(self, t, nbytes):
        self.t = t
        self.nbytes = nbytes
        self.off = 0
        self.peak = 0

    def reset(self, off=0):
        self.off = off

    def alloc(self, free_shape, dtype, parts=128):
        n = int(np.prod(free_shape))
        esz = 4 if dtype == F32 else 2
        sz = (n * esz + 31) // 32 * 32
        o = self.off
        assert o + sz <= self.nbytes, ("arena overflow", o, sz, self.nbytes)
        self.off = o + sz
        self.peak = max(self.peak, self.off)
        ap = self.t[0:parts, o // 2:(o + n * esz) // 2]
        if dtype == F32:
            ap = ap.bitcast(F32)
        fs = list(free_shape)
        if len(fs) == 2:
            ap = ap.rearrange("p (a b) -> p a b", a=fs[0], b=fs[1])
        elif len(fs) == 3:
            ap = ap.rearrange("p (a b c) -> p a b c", a=fs[0], b=fs[1], c=fs[2])
        elif len(fs) == 4:
            ap = ap.rearrange("p (a b c d) -> p a b c d", a=fs[0], b=fs[1], c=fs[2], d=fs[3])
        return ap


def _cst_layout():
    off = {}
    o = 0

    def add(name, n):
        nonlocal o
        off[name] = (o, n)
        o += n

    add("ident", 128)
    add("trim0", 128)
    add("trim1", 128)
    add("tris0", 128)
    add("tris1", 128)
    add("mask0", 128)
    add("mask1", 128)
    add("rm", 2)
    add("cond", 16)
    add("sel", 8)
    add("gmix", 32)
    add("gffn", 32)
    for i in range(2):
        add("gq%d" % i, 64)
        add("gk%d" % i, 64)
        add("gout%d" % i, 1)
        add("gqn%d" % i, 384)
        add("gkvn%d" % i, 256)
        add("gq96%d" % i, 96)
        add("gk96%d" % i, 96)
    return off, o


CST_OFF, NCST = _cst_layout()


class KB:
    def __init__(self, debug=(), stages=None):
        self.stages = set(stages) if stages is not None else {"adaln", "ffn", "mixc", "mixab", "P", "S"}
        self.debug = set(debug)
        self.dbg_outs = {}

    def mm(self, out, lhsT, rhs, start, stop, r, w, **kw):
        self.P.op("pe", lambda h: h.matmul(out, lhsT, rhs, start=start, stop=stop, **kw), r=r, w=w)

    def tr(self, out, in_, ident, r, w):
        self.P.op("pe", lambda h: h.transpose(out, in_, ident), r=r, w=w)

    def act(self, out, in_, func, r, w, **kw):
        self.P.op("act", lambda h: h.activation(out, in_, func, **kw), r=r, w=w)

    def tt(self, eng, out, a, b, op, r, w):
        self.P.op(eng, lambda h: h.tensor_tensor(out, a, b, op), r=r, w=w)

    def stt(self, out, in0, scalar, in1, op0, op1, r, w):
        self.P.op("dve", lambda h: h.scalar_tensor_tensor(out, in0, scalar, in1, op0, op1), r=r, w=w)

    def cp(self, eng, out, in_, r, w):
        if eng == "act":
            self.P.op("act", lambda h: h.copy(out, in_), r=r, w=w)
        else:
            self.P.op(eng, lambda h: h.tensor_copy(out, in_), r=r, w=w)

    def recip(self, out, in_, r, w):
        self.P.op("dve", lambda h: h.reciprocal(out, in_), r=r, w=w)

    def red(self, out, in_, r, w):
        self.P.op("dve", lambda h: h.tensor_reduce(out, in_, AX.X, ALU.add), r=r, w=w)

    def memset(self, eng, ap, val, w):
        self.P.op(eng, lambda h: h.memset(ap, val), w=w)

    def dma(self, q, out, in_, r, w, dq, **kw):
        self.P.op(q, lambda h: h.dma_start(out=out, in_=in_, **kw), r=r, w=w, dq=dq)

    def allgather(self, key, r, w):
        src, dst = self.CC[key]
        name = "cc_%s_%s" % key
        self.P.op("pool", lambda h: h.collective_compute("AllGather", ALU.bypass, replica_groups=[[0, 1, 2, 3], [4, 5, 6, 7]],
                                                         ins=[src.ap().opt()], outs=[dst.ap().opt()]),
                  r=r, w=w, dq=name, inc=1)

    def bank(self, pool):
        lst, idx = self.pools[pool]
        b = lst[idx % len(lst)]
        self.pools[pool][1] = idx + 1
        return b

    def rstd(self, out, ss, n, r, w):
        self.act(out, ss, AF.Ln, r=r, w=w, bias=EPS, scale=1.0 / n)
        self.act(out, out, AF.Exp, r=w, w=w, scale=-0.5)

    def cst(self, name, parts=128):
        o, n = CST_OFF[name]
        return self.cst_t[0:parts, o:o + n]

    def dbg(self, name, ap, r, shape):
        if name not in self.debug:
            return
        t = self.nc.dram_tensor("dbg_" + name, list(shape), ap.dtype if hasattr(ap, "dtype") else F32, kind="ExternalOutput").ap()
        self.dbg_outs[name] = shape
        self.dma("sp", t, ap, r=r, w=[], dq=("dbg", name))

    def build(self):
        nc = bass.Bass("TRN2", target_bir_lowering=False)
        self.nc = nc
        self.P = Prog()

        def din(name, shape):
            return nc.dram_tensor(name, list(shape), F32, kind="ExternalInput").ap()

        def dout(name, shape):
            return nc.dram_tensor(name, list(shape), F32, kind="ExternalOutput").ap()

        I = {}
        I["cst"] = din("cst", [128, NCST])
        I["xp"] = din("xp", [1024, D])
        I["xs"] = din("xs", [256, D])
        I["ada_w"] = din("ada_w", [4, D, 1536])
        I["ada_b"] = din("ada_b", [4, 1536])
        I["ffn_w_in"] = din("ffn_w_in", [4, D, 2 * FH])
        I["ffn_w_out"] = din("ffn_w_out", [4, FH, D])
        I["ab_w_in"] = din("ab_w_in", [2, D, 2240])
        I["ab_w_out"] = din("ab_w_out", [2, D, D])
        I["a_w2"] = din("a_w2", [2, 2, 16, 256])
        I["a_b"] = din("a_b", [2, 2, 256])
        I["w_qb"] = din("w_qb", [2, 384, 768])
        I["w_kvb"] = din("w_kvb", [2, 256, 1024])
        I["gqa_w_in"] = din("gqa_w_in", [2, D, 1536])
        I["gqa_w_out"] = din("gqa_w_out", [2, D, D])
        I["c_ckv"] = din("c_ckv", [2, 512, 256])
        I["c_kpe"] = din("c_kpe", [2, 512, 32])
        I["c_gla"] = din("c_gla", [2, 2, 256, 128])
        I["c_gk"] = din("c_gk", [2, 512, 256])
        I["c_gv"] = din("c_gv", [2, 512, 256])
        I["ropeg"] = din("ropeg", [256, 2, 32])
        I["ropem"] = din("ropem", [256, 2, 16])
        I["ropem_all"] = din("ropem_all", [1024, 2, 16])
        O = {}
        O["yp"] = dout("yp", [1024, D])
        O["ys"] = dout("ys", [256, D])
        O["o_ckv"] = dout("o_ckv", [4, 2, 256, 256])
        O["o_kpe"] = dout("o_kpe", [4, 2, 256, 32])
        O["o_gla"] = dout("o_gla", [4, 2, 2, 256, 128])
        O["o_gk"] = dout("o_gk", [4, 2, 256, 256])
        O["o_gv"] = dout("o_gv", [4, 2, 256, 256])
        self.I, self.O = I, O
        self.CC = {}
        self.CC[("a", "a")] = (nc.dram_tensor("ccs_a", [3, 6144], F32), nc.dram_tensor("ccd_a", [12, 6144], F32))
        for l in range(4):
            if l % 2 == 0:
                self.CC[(l, "g")] = (nc.dram_tensor("ccs_g%d" % l, [512, 129], F32), nc.dram_tensor("ccd_g%d" % l, [2048, 129], F32))
                self.CC[(l, "m")] = (nc.dram_tensor("ccs_m%d" % l, [256, 288], F32), nc.dram_tensor("ccd_m%d" % l, [1024, 288], F32))
            else:
                self.CC[(l, "c")] = (nc.dram_tensor("ccs_c%d" % l, [256, 512], F32), nc.dram_tensor("ccd_c%d" % l, [1024, 512], F32))

        with contextlib.ExitStack() as es:
            self.xT = es.enter_context(nc.sbuf_tensor("xT", [128, 8, NTOK], F32))
            self.cst_t = es.enter_context(nc.sbuf_tensor("cst_sb", [128, NCST], F32))
            small = es.enter_context(nc.sbuf_tensor("small", [128, 4 * 48 * 2 + 96 + 8], F32))
            cbf = es.enter_context(nc.sbuf_tensor("cbf", [128, 256 + 16], BF16))
            arena_t = es.enter_context(nc.sbuf_tensor("arena", [128, ARENA_BYTES // 2], BF16))
            self.A = Arena(arena_t, ARENA_BYTES)
            self.ps = [es.enter_context(nc.psum_tensor("ps%d" % i, [128, 512], F32)) for i in range(8)]
            self.psb = [p.bitcast(BF16) for p in self.ps]
            self.pools = {"mm": [[0, 1, 2, 3], 0], "acc": [[4, 5], 0], "tr": [[6, 7], 0]}
            self.modt = small[:, 0:384].rearrange("p (l c k) -> p l c k", l=4, c=48, k=2)
            self.lsc = small[:, 384:480].rearrange("p (g a b) -> p g a b", g=2, a=6, b=8)
            self.eps_c = small[:, 480:481]
            self.one_c = small[:, 481:482]
            self.ident_b = cbf[:, 0:128]
            self.ones_b = cbf[:, 128:256]
            self.sc_b = cbf[:, 256:272].rearrange("p (k c) -> p k c", k=8, c=2)
            self.ident_f = self.cst("ident")

            self.prologue()
            if "adaln" in self.stages:
                self.adaln_all()
            self.run_all()
            self.P.emit(nc, es)
        return nc

    def prologue(self):
        self.dma("sp", self.cst_t[:, :], self.I["cst"], r=[], w=["cst"], dq="cst")
        self.memset("dve", self.eps_c, EPS, w=["small_c"])
        self.memset("dve", self.one_c, 1.0, w=["small_c"])
        self.memset("dve", self.ones_b, 1.0, w=["ones"])
        self.cp("dve", self.ident_b, self.ident_f, r=["cst"], w=["identb"])
        cond = self.cst("cond").rearrange("p (k c) -> p k c", k=8, c=2)
        self.act(self.sc_b, cond, AF.Silu, r=["cst"], w=["scb"])

    def adaln_all(self):
        A = self.A
        A.reset()
        slots = [A.alloc([8, 512], BF16) for _ in range(3)]
        mq = A.alloc([6144], F32)
        mtok = A.alloc([6144], F32)
        rm = self.cst("rm", parts=3)
        cc_src, cc_dst = self.CC[("a", "a")]
        k = 0
        for l in range(4):
            self.dma("sp", mq[2:3, l * 1536:(l + 1) * 1536], self.I["ada_b"][l:l + 1, :], r=[], w=[("mq", "b")], dq="mtokb")
            for j in range(3):
                s = k % 3
                k += 1
                src = self.I["ada_w"][l, :, j * 512:(j + 1) * 512].rearrange("(k p) n -> p k n", p=128)
                self.dma("pool", slots[s], src, r=[], w=[("adw", s)], dq=("adw", s))
                b = self.bank("mm")
                for kc in range(8):
                    self.mm(self.ps[b][0:2, :], self.sc_b[:, kc, :], slots[s][:, kc, :], kc == 0, kc == 7,
                            r=[("adw", s), "scb"], w=[("ps", b)])
                c0 = l * 1536 + j * 512
                self.cp("act", mq[0:2, c0:c0 + 512], self.ps[b][0:2, :], r=[("ps", b)], w=[("mq", l, j)])
        self.dma("sp", cc_src.ap(), mq[0:3, :], r=[("mq", "b")] + [("mq", l, j) for l in range(4) for j in range(3)],
                 w=["cc_src_a"], dq="bnc_a")
        self.allgather(("a", "a"), r=["cc_src_a"], w=["cc_dst_a"])
        dview = cc_dst.ap().rearrange("(j r) (l c) -> r l j c", r=3, l=4)
        for l in range(4):
            self.dma("sp", mtok[0:3, :].rearrange("r (j c) -> r j c", j=4), dview[:, l, :, :], r=["cc_dst_a"], w=["mtok"], dq="mtok")
            b = self.bank("mm")
            for c in range(48):
                self.mm(self.ps[b][:, 2 * c:2 * c + 2], mtok[0:3, c * 128:(c + 1) * 128], rm, True, True,
                        r=["mtok", "cst"], w=[("ps", b)])
            self.cp("dve", self.modt[:, l, :, :], self.ps[b][:, 0:96].rearrange("p (c k) -> p c k", c=48, k=2),
                    r=[("ps", b)], w=["modt"])
        self.P.barrier()

    def layer_scalars(self, l):
        gmix = self.cst("gmix").rearrange("p (l c) -> p l c", l=4, c=8)[:, l, :]
        gffn = self.cst("gffn").rearrange("p (l c) -> p l c", l=4, c=8)[:, l, :]
        for col in range(2):
            mv = self.modt[:, l, :, col]
            L = self.lsc[:, col, :, :]
            self.stt(L[:, 0, :], mv[:, 8:16], 1.0, gmix, ALU.add, ALU.mult, r=["modt", "cst"], w=["lsc"])
            self.cp("dve", L[:, 1, :], mv[:, 0:8], r=["modt"], w=["lsc"])
            self.cp("dve", L[:, 2, :], mv[:, 16:24], r=["modt"], w=["lsc"])
            self.stt(L[:, 3, :], mv[:, 32:40], 1.0, gffn, ALU.add, ALU.mult, r=["modt", "cst"], w=["lsc"])
            self.cp("dve", L[:, 4, :], mv[:, 24:32], r=["modt"], w=["lsc"])
            self.cp("dve", L[:, 5, :], mv[:, 40:48], r=["modt"], w=["lsc"])

    def alloc_norm_tmp(self):
        A = self.A
        self.nm_sq = A.alloc([8, 128], BF16)
        self.nm_rstd = A.alloc([128], F32)
        self.nm_tmp = A.alloc([8, 128], F32)

    def normmod(self, t, which, dst, dstkey):
        xv = self.xT[:, :, t * 128:(t + 1) * 128]
        xk = [("xT", t, c) for c in range(8)]
        grp = 0 if t < 8 else 1
        G = self.lsc[:, grp, 3 * which, :]
        SH = self.lsc[:, grp, 3 * which + 1, :]
        self.tt("pool", self.nm_tmp, xv, G.unsqueeze(2).broadcast_to([128, 8, 128]), ALU.mult, r=xk + ["lsc"], w=["nm_tmp"])
        self.act(self.nm_sq, xv, AF.Square, r=xk, w=["nm_sq"])
        b = self.bank("tr")
        for c in range(8):
            self.mm(self.ps[b][:, 0:128], self.ones_b, self.nm_sq[:, c, :], c == 0, c == 7, r=["nm_sq", "ones"], w=[("ps", b)])
        self.rstd(self.nm_rstd, self.ps[b][:, 0:128], D, r=[("ps", b)], w=["nm_rstd"])
        self.tt("dve", self.nm_tmp, self.nm_tmp, self.nm_rstd.unsqueeze(1).broadcast_to([128, 8, 128]), ALU.mult,
                r=["nm_tmp", "nm_rstd"], w=["nm_tmp"])
        self.tt("dve", dst, self.nm_tmp, SH.unsqueeze(2).broadcast_to([128, 8, 128]), ALU.add,
                r=["nm_tmp", "lsc"], w=[dstkey])

    def run_all(self):
        self.seqs = [dict(tiles=[2 * s, 2 * s + 1], ctx=False, rope=False, bidx=s) for s in range(4)]
        self.sseg = dict(tiles=[8, 9], ctx=True, rope=True, bidx=None)
        A = self.A
        A.reset()
        xin = [A.alloc([1024], F32) for _ in range(2)]
        for t in range(NTILE):
            s = t % 2
            src = self.I["xp"][t * 128:(t + 1) * 128, :] if t < 8 else self.I["xs"][(t - 8) * 128:(t - 7) * 128, :]
            self.dma("sp", xin[s], src, r=[], w=[("xin", s)], dq=("xin", s))
            for hb in range(2):
                b = self.bank("tr")
                for cc in range(4):
                    c = hb * 4 + cc
                    self.tr(self.ps[b][:, cc * 128:(cc + 1) * 128], xin[s][:, c * 128:(c + 1) * 128], self.ident_f,
                            r=[("xin", s), "cst"], w=[("ps", b)])
                self.cp("dve" if hb else "act", self.xT[:, hb * 4:hb * 4 + 4, t * 128:(t + 1) * 128],
                        self.ps[b][:, :].rearrange("p (a b) -> p a b", a=4),
                        r=[("ps", b)], w=[("xT", t, hb * 4 + cc) for cc in range(4)])
        self.P.barrier()
        for l in range(4):
            self.layer_scalars(l)
            if l % 2 == 0:
                if "mixab" in self.stages:
                    self.mixer_ab(l, l // 2)
            else:
                if "mixc" in self.stages:
                    self.mixer_c(l, l // 2)
            self.P.barrier()
            if "ffn" in self.stages:
                self.ffn(l)
            self.P.barrier()
        A.reset()
        yo = [A.alloc([1024], F32) for _ in range(2)]
        for t in range(NTILE):
            s = t % 2
            for hb in range(2):
                b = self.bank("tr")
                for cc in range(4):
                    c = hb * 4 + cc
                    self.tr(self.ps[b][:, cc * 128:(cc + 1) * 128], self.xT[:, c, t * 128:(t + 1) * 128], self.ident_f,
                            r=[("xT", t, c), "cst"], w=[("ps", b)])
                self.cp("dve" if hb else "act", yo[s][:, hb * 512:(hb + 1) * 512], self.ps[b][:, :],
                        r=[("ps", b)], w=[("yo", s, hb)])
            dst = self.O["yp"][t * 128:(t + 1) * 128, :] if t < 8 else self.O["ys"][(t - 8) * 128:(t - 7) * 128, :]
            self.dma("sp", dst, yo[s], r=[("yo", s, 0), ("yo", s, 1)], w=[], dq=("yo", s))

    def ffn(self, l):
        A = self.A
        A.reset()
        hT = A.alloc([8, NTOK], BF16)
        actT = A.alloc([22, NTOK], BF16)
        wi = [A.alloc([8, 2, 256], BF16) for _ in range(2)]
        wo = [A.alloc([11, 1024], BF16) for _ in range(2)]
        sg = [A.alloc([512], F32) for _ in range(2)]
        self.alloc_norm_tmp()
        W1 = self.I["ffn_w_in"]
        W2 = self.I["ffn_w_out"]
        TB = [(0, 512, 0, [0, 1, 2, 3]), (512, 512, 0, [4, 5, 6, 7]), (1024, 256, 1, [8, 9])]
        NS = len(wi)

        def load_wi(j2):
            s = j2 % NS
            for gu in range(2):
                c0 = gu * FH + j2 * 256
                src = W1[l, :, c0:c0 + 256].rearrange("(k p) n -> p k n", p=128)
                self.dma("pool", wi[s][:, :, gu, :], src, r=[], w=[("wi", s, gu)], dq=("wi", s))

        def load_wo(hf):
            src = W2[l, hf * 1408:(hf + 1) * 1408, :].rearrange("(j p) n -> p j n", p=128)
            self.dma("pool", wo[hf], src, r=[], w=[("wo", hf)], dq=("wo", hf))

        for j2 in range(NS):
            load_wi(j2)
        for t in range(NTILE):
            self.normmod(t, 1, hT[:, :, t * 128:(t + 1) * 128], ("hT", t))
        load_wo(0)
        load_wo(1)
        k = 0
        for j2 in range(11):
            s = j2 % NS
            for (t0, tn, grp, tl) in TB:
                hk = [("hT", q) for q in tl]
                for hf in range(2):
                    j = j2 * 2 + hf
                    bg = self.bank("mm")
                    for kc in range(8):
                        self.mm(self.ps[bg][:, 0:tn], wi[s][:, kc, 0, hf * 128:(hf + 1) * 128], hT[:, kc, t0:t0 + tn],
                                kc == 0, kc == 7, r=[("wi", s, 0)] + hk, w=[("ps", bg)])
                    bu = self.bank("mm")
                    for kc in range(8):
                        self.mm(self.ps[bu][:, 0:tn], wi[s][:, kc, 1, hf * 128:(hf + 1) * 128], hT[:, kc, t0:t0 + tn],
                                kc == 0, kc == 7, r=[("wi", s, 1)] + hk, w=[("ps", bu)])
                    sgi = k % 2
                    k += 1
                    self.act(sg[sgi][:, 0:tn], self.ps[bg][:, 0:tn], AF.Silu, r=[("ps", bg)], w=[("sg", sgi)])
                    self.tt("dve", actT[:, j, t0:t0 + tn], sg[sgi][:, 0:tn], self.ps[bu][:, 0:tn], ALU.mult,
                            r=[("sg", sgi), ("ps", bu)], w=[("act", j, t0)])
            if j2 + NS < 11:
                load_wi(j2 + NS)
        for hf in range(2):
            for c in range(8):
                for (t0, tn, grp, tl) in TB:
                    gate = self.lsc[:, grp, 5, :]
                    b = self.bank("mm")
                    for jj in range(11):
                        self.mm(self.ps[b][:, 0:tn], wo[hf][:, jj, c * 128:(c + 1) * 128], actT[:, hf * 11 + jj, t0:t0 + tn],
                                jj == 0, jj == 10, r=[("wo", hf), ("act", hf * 11 + jj, t0)], w=[("ps", b)])
                    xv = self.xT[:, c, t0:t0 + tn]
                    xk = [("xT", q, c) for q in tl]
                    self.stt(xv, self.ps[b][:, 0:tn], gate[:, c:c + 1], xv, ALU.mult, ALU.add, r=[("ps", b), "lsc"] + xk, w=xk)

    def mixer_residual(self, t, banks):
        grp = 0 if t < 8 else 1
        gate = self.lsc[:, grp, 2, :]
        tmp = self.nm_tmp
        for hb in range(2):
            b = banks[hb]
            pv = self.ps[b][:, :].rearrange("p (a b) -> p a b", a=4)
            tv = tmp[:, hb * 4:hb * 4 + 4, :]
            self.tt("dve", tv, pv, gate[:, hb * 4:hb * 4 + 4].unsqueeze(2).broadcast_to([128, 4, 128]), ALU.mult,
                    r=[("ps", b), "lsc"], w=["nm_tmp"])
            xv = self.xT[:, hb * 4:hb * 4 + 4, t * 128:(t + 1) * 128]
            xk = [("xT", t, hb * 4 + q) for q in range(4)]
            self.tt("pool", xv, xv, tv, ALU.add, r=["nm_tmp"] + xk, w=xk)

    def rope(self, xv, H, Q, cs, tmps, r, w, keyp):
        x1 = xv[:, :, :, 0, :]
        x2 = xv[:, :, :, 1, :]
        c = cs[:, 0, :].rearrange("p (a q) -> p a q", a=2, q=Q).unsqueeze(1).broadcast_to([128, H, 2, Q])
        s = cs[:, 1, :].rearrange("p (a q) -> p a q", a=2, q=Q).unsqueeze(1).broadcast_to([128, H, 2, Q])
        t1, t2, t3, t4 = tmps
        k1, k2, k3, k4 = [(keyp, i) for i in range(4)]
        self.tt("dve", t1, x1, c, ALU.mult, r=r, w=[k1])
        self.tt("pool", t2, x2, s, ALU.mult, r=r, w=[k2])
        self.tt("dve", t3, x1, s, ALU.mult, r=r, w=[k3])
        self.tt("pool", t4, x2, c, ALU.mult, r=r, w=[k4])
        self.tt("dve", x1, t1, t2, ALU.subtract, r=[k1, k2, k3], w=w)
        self.tt("dve", x2, t3, t4, ALU.add, r=[k3, k4], w=w)

    def attention(self, KT, kparts, kt_list, QTv, nq, VA_of, scale, Pt, out_of, kr, qr, vr, ow):
        ob = self.bank("acc")
        first, last = kt_list[0], kt_list[-1]
        npt = len(Pt)

        def pv(st, pi):
            for j in range(nq):
                self.mm(self.ps[ob][:, j * 65:(j + 1) * 65], Pt[pi][:, j, :], VA_of(st), (st == first and j == 0), st == last,
                        r=[("Pt", pi)] + vr, w=[("ps", ob)], skip_group_check=True)

        pend = None
        for st in kt_list:
            sb = self.bank("mm")
            self.mm(self.ps[sb][:, 0:nq * 128], KT(st), QTv, True, True, r=kr + qr, w=[("ps", sb)])
            pi = self.pt_i % npt
            self.pt_i += 1
            self.act(Pt[pi][:, 0:nq, :], self.ps[sb][:, 0:nq * 128].rearrange("p (a b) -> p a b", a=nq), AF.Exp,
                     r=[("ps", sb)], w=[("Pt", pi)], scale=scale)
            if pend is not None:
                pv(*pend)
            pend = (st, pi)
        pv(*pend)
        ov = self.ps[ob][:, 0:nq * 65].rearrange("p (a b) -> p a b", a=nq)
        rd = self.at_rden
        self.recip(rd[:, 0:nq], ov[:, :, 64], r=[("ps", ob)], w=["at_rden"])
        for j in range(nq):
            dst, dk = out_of(j)
            self.P.op("dve", lambda h, j=j, dst=dst, rd=rd, ov=ov: h.tensor_scalar(dst, ov[:, j, 0:64], rd[:, j:j + 1], None, ALU.mult),
                      r=[("ps", ob), "at_rden"], w=[dk])

    def mixer_c(self, l, i):
        A = self.A
        A.reset()
        I, O = self.I, self.O
        w_in = A.alloc([8, 1536], BF16)
        w_out = A.alloc([8, 1024], BF16)
        hT = A.alloc([8, 256], BF16)
        KT = A.alloc([4, 1536], BF16)
        VA = A.alloc([12, 4, 65], BF16)
        self.alloc_norm_tmp()
        kvf = A.alloc([512], F32)
        sq = A.alloc([1024], F32)
        qn = A.alloc([1024], F32)
        st16 = A.alloc([16], F32)
        knb = A.alloc([256], BF16)
        qnb = A.alloc([1024], BF16)
        QT = A.alloc([16, 128], BF16)
        Pt = [A.alloc([4, 128], BF16) for _ in range(3)]
        ob = A.alloc([1024], BF16)
        OT = A.alloc([8, 128], BF16)
        self.at_rden = A.alloc([4], F32)
        rtm = [A.alloc([16, 2, 16], F32) for _ in range(4)]
        ropeg = A.alloc([2, 2, 32], F32)
        ck = A.alloc([4, 256], F32)
        cv = A.alloc([4, 256], F32)
        ckb = A.alloc([4, 256], BF16)
        kg = [A.alloc([512], F32) for _ in range(2)]
        self.pt_i = 0
        gq = self.cst("gq%d" % i)
        gk = self.cst("gk%d" % i)
        for kh in range(2):
            self.dma("pool", w_in[:, kh * 4:(kh + 1) * 4, :],
                     I["gqa_w_in"][i, kh * 512:(kh + 1) * 512, :].rearrange("(k p) n -> p k n", p=128), r=[], w=[("w_in", kh)], dq="w_in")
        self.dma("pool", w_out, I["gqa_w_out"][i].rearrange("(k p) n -> p k n", p=128), r=[], w=["w_out"], dq="w_out")
        self.memset("pool", VA[:, :, :, 64:65], 1.0, w=["VA1"])
        wk = [("w_in", 0), ("w_in", 1)]
        self.dma("sp", ropeg, I["ropeg"].rearrange("(t p) c q -> p t c q", p=128), r=[], w=["ropeg"], dq="ropeg")
        cc_src, cc_dst = self.CC[(l, "c")]

        def put_keys(knb_ap, kt, rk):
            b = self.bank("tr")
            for g in range(4):
                self.tr(self.psb[b][0:64, g * 128:(g + 1) * 128], knb_ap[:, g * 64:(g + 1) * 64], self.ident_b,
                        r=rk + ["identb"], w=[("ps", b)])
            self.cp("dve", KT[0:64, :, kt * 128:(kt + 1) * 128], self.psb[b][0:64, 0:512].rearrange("p (a b) -> p a b", a=4),
                    r=[("ps", b)], w=[("KT", kt)])

        def kv_side(t, n, rope_n):
            self.normmod(t, 0, hT[:, :, n * 128:(n + 1) * 128], ("hT", n))
            b = self.bank("mm")
            for kc in range(8):
                self.mm(self.ps[b][:, :], hT[:, kc, n * 128:(n + 1) * 128], w_in[:, kc, 1024:1536], kc == 0, kc == 7,
                        r=[("hT", n)] + wk, w=[("ps", b)])
            self.cp("act", kvf, self.ps[b][:, :], r=[("ps", b)], w=["kvf"])
            self.act(sq[:, 0:256], self.ps[b][:, 0:256], AF.Square, r=[("ps", b)], w=["sq"])
            self.red(st16[:, 0:4], sq[:, 0:256].rearrange("p (g d) -> p g d", g=4), r=["sq"], w=["st16"])
            self.rstd(st16[:, 0:4], st16[:, 0:4], 64, r=["st16"], w=["st16"])
            kv3 = kvf[:, 0:256].rearrange("p (g d) -> p g d", g=4)
            self.tt("dve", kv3, kv3, st16[:, 0:4].unsqueeze(2).broadcast_to([128, 4, 64]), ALU.mult, r=["kvf", "st16"], w=["kvf"])
            self.tt("dve", kv3, kv3, gk.unsqueeze(1).broadcast_to([128, 4, 64]), ALU.mult, r=["kvf", "cst"], w=["kvf"])
            if rope_n is not None:
                self.rope(kvf[:, 0:256].rearrange("p (h a b q) -> p h a b q", h=4, a=2, b=2, q=16), 4, 16, ropeg[:, rope_n, :, :],
                          [x[:, 0:4, :, :] for x in rtm], r=["kvf", "ropeg"], w=["kvf"], keyp="rtm")

        def q_attn_out(t, n, rope_n, kt_list):
            kkeys = [("KT", kt) for kt in kt_list]
            vkeys = [("VA", kt) for kt in kt_list] + ["VA1"]
            qb = []
            for bk in range(2):
                b = self.bank("mm")
                qb.append(b)
                for kc in range(8):
                    self.mm(self.ps[b][:, :], hT[:, kc, n * 128:(n + 1) * 128], w_in[:, kc, bk * 512:(bk + 1) * 512], kc == 0, kc == 7,
                            r=[("hT", n)] + wk, w=[("ps", b)])
                self.act(sq[:, bk * 512:(bk + 1) * 512], self.ps[b][:, :], AF.Square, r=[("ps", b)], w=["sq"])
            self.red(st16, sq.rearrange("p (g d) -> p g d", g=16), r=["sq"], w=["st16"])
            self.rstd(st16, st16, 64, r=["st16"], w=["st16"])
            for bk in range(2):
                self.tt("dve", qn[:, bk * 512:(bk + 1) * 512].rearrange("p (g d) -> p g d", g=8),
                        self.ps[qb[bk]][:, :].rearrange("p (g d) -> p g d", g=8),
                        st16[:, bk * 8:(bk + 1) * 8].unsqueeze(2).broadcast_to([128, 8, 64]), ALU.mult,
                        r=[("ps", qb[bk]), "st16"], w=["qn"])
            qn3 = qn.rearrange("p (g d) -> p g d", g=16)
            self.tt("pool", qn3, qn3, gq.unsqueeze(1).broadcast_to([128, 16, 64]), ALU.mult, r=["qn", "cst"], w=["qn"])
            if rope_n is not None:
                self.rope(qn.rearrange("p (h a b q) -> p h a b q", h=16, a=2, b=2, q=16), 16, 16, ropeg[:, rope_n, :, :], rtm,
                          r=["qn", "ropeg"], w=["qn"], keyp="rtm")
            self.cp("act", qnb, qn, r=["qn"], w=["qnb"])
            for hb in range(2):
                b = self.bank("tr")
                for hh in range(8):
                    h_ = hb * 8 + hh
                    self.tr(self.psb[b][0:64, hh * 128:(hh + 1) * 128], qnb[:, h_ * 64:(h_ + 1) * 64], self.ident_b,
                            r=["qnb", "identb"], w=[("ps", b)])
                self.cp("dve" if hb else "act", QT[0:64, hb * 8:(hb + 1) * 8, :],
                        self.psb[b][0:64, :].rearrange("p (a b) -> p a b", a=8), r=[("ps", b)], w=[("QT", hb)])
            for g in range(4):
                self.attention(
                    KT=lambda st, g=g: KT[0:64, g, st * 128:(st + 1) * 128], kparts=64, kt_list=kt_list,
                    QTv=QT[0:64, 4 * g:4 * g + 4, :], nq=4,
                    VA_of=lambda st, g=g: VA[:, st, g, :], scale=0.125, Pt=Pt,
                    out_of=lambda j, g=g: (ob[:, (4 * g + j) * 64:(4 * g + j + 1) * 64], ("ob", g)),
                    kr=kkeys, qr=[("QT", g // 2)], vr=vkeys, ow=None)
            b = self.bank("tr")
            for c in range(8):
                self.tr(self.psb[b][:, c * 128:(c + 1) * 128], ob[:, c * 128:(c + 1) * 128], self.ident_b,
                        r=[("ob", c // 2), "identb"], w=[("ps", b)])
            self.cp("act", OT, self.psb[b][:, :].rearrange("p (a b) -> p a b", a=8), r=[("ps", b)], w=["OT"])
            banks = []
            for hb in range(2):
                b = self.bank("mm")
                banks.append(b)
                for cc in range(4):
                    c = hb * 4 + cc
                    for kc in range(8):
                        self.mm(self.ps[b][:, cc * 128:(cc + 1) * 128], w_out[:, kc, c * 128:(c + 1) * 128], OT[:, kc, :], kc == 0, kc == 7,
                                r=["w_out", "OT"], w=[("ps", b)])
            self.mixer_residual(t, banks)

        cck = "cc_src_c%d" % l
        ccd = "cc_dst_c%d" % l
        for n, t in enumerate(self.sseg["tiles"]):
            kv_side(t, n, n)
            self.dma("sp", cc_src[n * 128:(n + 1) * 128, :], kvf, r=["kvf"], w=[cck], dq="ccb_c")
        self.allgather((l, "c"), r=[cck], w=[ccd])
        for sq_ in self.seqs:
            tiles = sq_["tiles"]
            bi = sq_["bidx"]
            for n, t in enumerate(tiles):
                kv_side(t, n, None)
                self.dma("sp", O["o_gk"][bi, i, n * 128:(n + 1) * 128, :], kvf[:, 0:256], r=["kvf"], w=[], dq="kvf_o")
                self.dma("sp", O["o_gv"][bi, i, n * 128:(n + 1) * 128, :], kvf[:, 256:512], r=["kvf"], w=[], dq="kvf_o")
                self.cp("act", knb, kvf[:, 0:256], r=["kvf"], w=["knb"])
                put_keys(knb, n, ["knb"])
                self.cp("pool", VA[:, n, :, 0:64], kvf[:, 256:512].rearrange("p (g d) -> p g d", g=4), r=["kvf"], w=[("VA", n)])
            for n, t in enumerate(tiles):
                q_attn_out(t, n, None, [0, 1])
        self.dma("sp", ck, I["c_gk"][i].rearrange("(t p) n -> p t n", p=128), r=[], w=["ck"], dq="ck")
        self.dma("sp", cv, I["c_gv"][i].rearrange("(t p) n -> p t n", p=128), r=[], w=["cv"], dq="cv")
        self.cp("act", ckb, ck, r=["ck"], w=["ckb"])
        for kt in range(4):
            put_keys(ckb[:, kt, :], kt, ["ckb"])
            self.cp("pool", VA[:, kt, :, 0:64], cv[:, kt, :].rearrange("p (g d) -> p g d", g=4), r=["cv"], w=[("VA", kt)])
        for k in range(8):
            kgk = kg[k % 2]
            kk = ("kg", k % 2)
            self.dma("sp", kgk, cc_dst[k * 128:(k + 1) * 128, :], r=[ccd], w=[kk], dq=kk)
            self.cp("act", knb, kgk[:, 0:256], r=[kk], w=["knb"])
            put_keys(knb, 4 + k, ["knb"])
            self.cp("pool", VA[:, 4 + k, :, 0:64], kgk[:, 256:512].rearrange("p (g d) -> p g d", g=4), r=[kk], w=[("VA", 4 + k)])
        for n, t in enumerate(self.sseg["tiles"]):
            self.normmod(t, 0, hT[:, :, n * 128:(n + 1) * 128], ("hT", n))
            q_attn_out(t, n, n, list(range(12)))

    def mixer_ab(self, l, i):
        A = self.A
        A.reset()
        I, O = self.I, self.O
        w_in = A.alloc([8, 2240], BF16)
        w_out = A.alloc([8, 1024], BF16)
        OG = A.alloc([4, NTOK], BF16)
        hTm = A.alloc([8, 128], BF16)
        hk = "hTm"
        self.alloc_norm_tmp()
        self.at_rden = A.alloc([4], F32)
        Pt = [A.alloc([4, 128], BF16) for _ in range(3)]
        self.pt_i = 0
        SP = dict(qlT=[A.alloc([2, 256], BF16) for _ in range(2)], oT=A.alloc([4, 256], F32), rsT=A.alloc([4, 256], BF16),
                  etot=A.alloc([4, 2], F32), cqb=A.alloc([2, 384], BF16), aseg=A.alloc([4], F32))
        base_off = A.off
        for kh in range(2):
            for ch in range(2):
                self.dma("pool", w_in[:, kh * 4:(kh + 1) * 4, ch * 1120:(ch + 1) * 1120],
                         I["ab_w_in"][i, kh * 512:(kh + 1) * 512, ch * 1120:(ch + 1) * 1120].rearrange("(k p) n -> p k n", p=128),
                         r=[], w=[("w_in", kh, ch)], dq="w_in")
        self.dma("pool", w_out, I["ab_w_out"][i].rearrange("(k p) n -> p k n", p=128), r=[], w=["w_out"], dq="w_out")
        wk = [("w_in", 0, 0), ("w_in", 0, 1), ("w_in", 1, 0), ("w_in", 1, 1)]
        gout = self.cst("gout%d" % i)
        gqn = self.cst("gqn%d" % i)
        gkvn = self.cst("gkvn%d" % i)
        gq96 = self.cst("gq96%d" % i)
        gk96 = self.cst("gk96%d" % i)
        sel = self.cst("sel")
        trim = [self.cst("trim0"), self.cst("trim1")]
        tris = [self.cst("tris0"), self.cst("tris1")]
        mask = [self.cst("mask0"), self.cst("mask1")]
        ccg_src, ccg_dst = self.CC[(l, "g")]
        ccm_src, ccm_dst = self.CC[(l, "m")]

        def g_alloc(NT, persist=None):
            T = NT * 128
            B = {}
            if persist is None:
                B["qlT"] = [A.alloc([2, T], BF16) for _ in range(2)]
                B["oT"] = A.alloc([4, T], F32)
                B["rsT"] = A.alloc([4, T], BF16)
                B["etot"] = A.alloc([4, NT], F32)
            else:
                for k in ("qlT", "oT", "rsT", "etot"):
                    B[k] = persist[k]
            B["klT"] = [A.alloc([2, T], BF16) for _ in range(2)]
            B["kst"] = [A.alloc([NT, 256], BF16) for _ in range(2)]
            B["vtk"] = A.alloc([NT, 512], BF16)
            B["Sst"] = [[A.alloc([128], F32) for _ in range(2)] for _ in range(2)]
            B["Sbf"] = [[A.alloc([128], BF16) for _ in range(2)] for _ in range(2)]
            B["alT"] = A.alloc([128], F32)
            B["lsp"] = A.alloc([512], F32)
            B["Eb"] = A.alloc([4, 128], F32)
            B["Enb"] = A.alloc([4, 128], F32)
            B["Ed2"] = A.alloc([512], F32)
            B["ATm"] = [A.alloc([128], BF16) for _ in range(4)]
            B["aw2"] = A.alloc([512], F32)
            self.memset("dve", B["alT"][32:33, :], 1.0, w=["alT1"])
            aw2 = B["aw2"]
            self.memset("dve", aw2[0:33, :], 0.0, w=["aw2"])
            for z in range(2):
                self.dma("sp", aw2[16 * z:16 * z + 16, z * 256:(z + 1) * 256], I["a_w2"][i, z], r=[], w=["aw2"], dq="aw2")
            self.dma("sp", aw2[32:33, :], I["a_b"][i:i + 1].rearrange("o z n -> o (z n)"), r=[], w=["aw2"], dq="aw2")
            return B

        def g_prep(tiles, B):
            qlT, klT, kst, vtk, rsT, etot = B["qlT"], B["klT"], B["kst"], B["vtk"], B["rsT"], B["etot"]
            alT, lsp, Eb, Enb, Ed2, aw2 = B["alT"], B["lsp"], B["Eb"], B["Enb"], B["Ed2"], B["aw2"]
            for n, t in enumerate(tiles):
                nc_ = slice(n * 128, (n + 1) * 128)
                h = hTm
                self.normmod(t, 0, h, hk)
                bqk = self.bank("mm")
                for ch in range(4):
                    for kc in range(8):
                        self.mm(self.ps[bqk][:, ch * 128:(ch + 1) * 128], w_in[:, kc, ch * 128:(ch + 1) * 128], h[:, kc, :], kc == 0, kc == 7,
                                r=[hk] + wk, w=[("ps", bqk)])
                br = self.bank("mm")
                for ch in range(4):
                    for kc in range(8):
                        self.mm(self.ps[br][:, ch * 128:(ch + 1) * 128], w_in[:, kc, 1024 + ch * 128:1024 + (ch + 1) * 128], h[:, kc, :], kc == 0, kc == 7,
                                r=[hk] + wk, w=[("ps", br)])
                self.act(rsT[:, :, nc_], self.ps[br][:, :].rearrange("p (a b) -> p a b", a=4), AF.Silu, r=[("ps", br)], w=[("rsT", n)])
                ba = self.bank("tr")
                for kc in range(8):
                    self.mm(self.ps[ba][0:32, 0:128], w_in[:, kc, 1536:1568], h[:, kc, :], kc == 0, kc == 7, r=[hk] + wk, w=[("ps", ba)])
                self.cp("act", alT[0:32, :], self.ps[ba][0:32, 0:128], r=[("ps", ba)], w=["alT"])
                bkv = self.bank("mm")
                for kc in range(8):
                    self.mm(self.ps[bkv][:, :], h[:, kc, :], w_in[:, kc, 256:768], kc == 0, kc == 7, r=[hk] + wk, w=[("ps", bkv)])
                bv2 = self.bank("mm")
                for kc in range(8):
                    self.mm(self.ps[bv2][:, 0:256], h[:, kc, :], w_in[:, kc, 768:1024], kc == 0, kc == 7, r=[hk] + wk, w=[("ps", bv2)])
                self.cp("act", vtk[:, n, 0:256], self.ps[bkv][:, 256:512], r=[("ps", bkv)], w=[("vtk", n, 0)])
                self.cp("act", vtk[:, n, 256:512], self.ps[bv2][:, 0:256], r=[("ps", bv2)], w=[("vtk", n, 1)])
                bl = self.bank("tr")
                self.mm(self.ps[bl][:, :], alT[0:33, :], aw2[0:33, :], True, True, r=["alT", "alT1", "aw2"], w=[("ps", bl)])
                self.act(lsp, self.ps[bl][:, :], AF.Exp, r=[("ps", bl)], w=["lsp"], scale=-1.0)
                self.act(lsp, lsp, AF.Ln, r=["lsp"], w=["lsp"], bias=1.0)
                bb = self.bank("tr")
                for z in range(2):
                    for fc in range(2):
                        zf = z * 2 + fc
                        self.mm(self.ps[bb][:, zf * 128:(zf + 1) * 128], lsp[:, zf * 128:(zf + 1) * 128], trim[z], True, True,
                                r=["lsp", "cst"], w=[("ps", bb)])
                bd = self.bank("tr")
                for z in range(2):
                    self.mm(self.ps[bd][:, z * 256:(z + 1) * 256], tris[z], lsp[:, z * 256:(z + 1) * 256], True, True,
                            r=["lsp", "cst"], w=[("ps", bd)])
                pbb = self.ps[bb][:, :].rearrange("p (a b) -> p a b", a=4)
                self.act(Eb, pbb, AF.Exp, r=[("ps", bb)], w=["Eb"])
                self.act(Enb, pbb, AF.Exp, r=[("ps", bb)], w=["Enb"], scale=-1.0)
                self.act(Ed2, self.ps[bd][:, :], AF.Exp, r=[("ps", bd)], w=["Ed2"])
                self.cp("pool", etot[:, 0:2, n], Eb[:, 0:2, 127], r=["Eb"], w=[("etot", n)])
                self.cp("pool", etot[:, 2:4, n], Eb[:, 2:4, 0], r=["Eb"], w=[("etot", n)])
                pqk = self.ps[bqk][:, :].rearrange("p (a b) -> p a b", a=4)
                for z in range(2):
                    self.stt(qlT[z][:, :, nc_], pqk[:, 0:2, :], 0.125, Eb[:, 2 * z:2 * z + 2, :], ALU.mult, ALU.mult,
                             r=[("ps", bqk), "Eb"], w=[("qlT", z, n)])
                    self.tt("dve", klT[z][:, :, nc_], pqk[:, 2:4, :], Enb[:, 2 * z:2 * z + 2, :], ALU.mult,
                            r=[("ps", bqk), "Enb"], w=[("klT", z, n)])
                    self.tt("dve", kst[z][:, n, :], self.ps[bkv][:, 0:256], Ed2[:, z * 256:(z + 1) * 256], ALU.mult,
                            r=[("ps", bkv), "Ed2"], w=[("kst", z, n)])

        def g_scan(NT, B, bidx):
            qlT, klT, kst, vtk, oT, etot, Sst, Sbf, ATm = (B[k] for k in ("qlT", "klT", "kst", "vtk", "oT", "etot", "Sst", "Sbf", "ATm"))
            for z in range(2):
                for fc in range(2):
                    self.memset("pool", Sst[z][fc], 0.0, w=[("S", z, fc)])
                    self.cp("act", Sbf[z][fc], Sst[z][fc], r=[("S", z, fc)], w=[("Sbf", z, fc)])
                order = list(range(NT)) if z == 0 else list(range(NT - 1, -1, -1))
                for n in order:
                    nc_ = slice(n * 128, (n + 1) * 128)
                    bo = self.bank("acc")
                    bats = []
                    for hh in range(4):
                        fc, hp = hh // 2, hh % 2
                        pr = slice(hp * 64, (hp + 1) * 64)
                        bat = self.bank("mm")
                        bats.append(bat)
                        self.mm(self.ps[bat][:, 0:128], klT[z][pr, fc, nc_], qlT[z][pr, fc, nc_], True, True,
                                r=[("klT", z, n), ("qlT", z, n)], w=[("ps", bat)])
                    for hh in range(4):
                        self.tt("dve", ATm[hh], self.ps[bats[hh]][:, 0:128], mask[z], ALU.mult, r=[("ps", bats[hh]), "cst"], w=[("ATm", hh)])
                    for hh in range(4):
                        fc, hp = hh // 2, hh % 2
                        pr = slice(hp * 64, (hp + 1) * 64)
                        self.mm(self.ps[bo][:, hh * 128:(hh + 1) * 128], vtk[:, n, hh * 128:(hh + 1) * 128], ATm[hh], True, False,
                                r=[("vtk", n, 0), ("vtk", n, 1), ("ATm", hh)], w=[("ps", bo)])
                        self.mm(self.ps[bo][:, hh * 128:(hh + 1) * 128], Sbf[z][fc][pr, :], qlT[z][pr, fc, nc_], False, True,
                                r=[("Sbf", z, fc), ("qlT", z, n)], w=[("ps", bo)])
                    pbo = self.ps[bo][:, :].rearrange("p (a b) -> p a b", a=4)
                    if z == 0:
                        self.cp("act", oT[:, :, nc_], pbo, r=[("ps", bo)], w=[("oT", n)])
                    else:
                        self.tt("dve", oT[:, :, nc_], oT[:, :, nc_], pbo, ALU.add, r=[("ps", bo), ("oT", n)], w=[("oT", n)])
                    bus = []
                    for fc in range(2):
                        bu = self.bank("mm")
                        bus.append(bu)
                        for hp in range(2):
                            hh = fc * 2 + hp
                            self.mm(self.ps[bu][hp * 64:(hp + 1) * 64, 0:128], kst[z][:, n, hh * 64:(hh + 1) * 64],
                                    vtk[:, n, hh * 128:(hh + 1) * 128], True, True,
                                    r=[("kst", z, n), ("vtk", n, 0), ("vtk", n, 1)], w=[("ps", bu)])
                    for fc in range(2):
                        self.stt(Sst[z][fc], Sst[z][fc], etot[:, z * 2 + fc, n:n + 1], self.ps[bus[fc]][:, 0:128], ALU.mult, ALU.add,
                                 r=[("S", z, fc), ("etot", n), ("ps", bus[fc])], w=[("S", z, fc)])
                    for fc in range(2):
                        self.cp("act", Sbf[z][fc], Sst[z][fc], r=[("S", z, fc)], w=[("Sbf", z, fc)])
                if bidx is not None:
                    for fc in range(2):
                        self.dma("sp", O["o_gla"][bidx, i, z, fc * 128:(fc + 1) * 128, :], Sst[z][fc],
                                 r=[("S", z, fc)], w=[], dq=("S", z, fc))

        def g_out(tiles, oT, rsT):
            osq = A.alloc([4, 128], BF16)
            orst = A.alloc([4, 128], F32)
            otmp = A.alloc([4, 128], F32)
            for n, t in enumerate(tiles):
                nc_ = slice(n * 128, (n + 1) * 128)
                self.act(osq, oT[:, :, nc_], AF.Square, r=[("oT", n)], w=["osq"])
                b = self.bank("tr")
                self.mm(self.ps[b][:, :], self.ones_b, osq.rearrange("p a b -> p (a b)"), True, True, r=["osq", "ones"], w=[("ps", b)])
                self.rstd(orst, self.ps[b][:, :].rearrange("p (a b) -> p a b", a=4), 128, r=[("ps", b)], w=["orst"])
                self.tt("dve", otmp, oT[:, :, nc_], orst, ALU.mult, r=[("oT", n), "orst"], w=["otmp"])
                self.stt(OG[:, :, t * 128:(t + 1) * 128], otmp, gout, rsT[:, :, nc_], ALU.mult, ALU.mult,
                         r=["otmp", "cst", ("rsT", n)], w=[("OG", t)])

        def m_alloc(NK):
            M = {}
            M["w_qb"] = A.alloc([3, 768], BF16)
            M["w_kvb"] = A.alloc([2, 1024], BF16)
            self.dma("pool", M["w_qb"], I["w_qb"][i].rearrange("(k p) n -> p k n", p=128), r=[], w=["w_qb"], dq="w_qb")
            self.dma("pool", M["w_kvb"], I["w_kvb"][i].rearrange("(k p) n -> p k n", p=128), r=[], w=["w_kvb"], dq="w_kvb")
            M["KTm"] = A.alloc([8, NK * 128], BF16)
            M["VAm"] = A.alloc([NK, 8, 65], BF16)
            M["QTm"] = A.alloc([8, 256], BF16)
            M["omb"] = A.alloc([2, 512], BF16)
            M["OM"] = A.alloc([4, 256], BF16)
            M["cb"] = A.alloc([256], BF16)
            M["cT"] = A.alloc([2, 128], BF16)
            M["kc96"] = A.alloc([8, 96], F32)
            M["knb"] = A.alloc([8, 96], BF16)
            M["st8"] = A.alloc([8], F32)
            M["cqT"] = A.alloc([3, 128], BF16)
            M["rtm"] = [A.alloc([8, 2, 8], F32) for _ in range(4)]
            self.memset("pool", M["VAm"][:, :, :, 64:65], 1.0, w=["VA1"])
            return M

        def own_alloc():
            W = {}
            W["sq96"] = A.alloc([8, 96], F32)
            W["ckvf"] = A.alloc([256], F32)
            W["ckvn"] = A.alloc([256], F32)
            W["kpe"] = A.alloc([32], F32)
            W["st1"] = A.alloc([2], F32)
            return W

        def norm96(M, sq96, gain, rope_cs):
            kc96, knb, st8, rtm = M["kc96"], M["knb"], M["st8"], M["rtm"]
            self.act(sq96, kc96, AF.Square, r=["kc96"], w=["sq96"])
            self.red(st8, sq96, r=["sq96"], w=["st8"])
            self.rstd(st8, st8, 96, r=["st8"], w=["st8"])
            self.tt("dve", kc96, kc96, st8.unsqueeze(2).broadcast_to([128, 8, 96]), ALU.mult, r=["kc96", "st8"], w=["kc96"])
            self.tt("pool", kc96, kc96, gain.unsqueeze(1).broadcast_to([128, 8, 96]), ALU.mult, r=["kc96", "cst"], w=["kc96"])
            if rope_cs is not None:
                cs, csk = rope_cs
                self.rope(kc96[:, :, 64:96].rearrange("p h (a b q) -> p h a b q", a=2, b=2, q=8), 8, 8, cs, rtm,
                          r=["kc96", csk], w=["kc96"], keyp="rtm")
            self.cp("act", knb, kc96, r=["kc96"], w=["knb"])

        def kside(M, sq96, ckvn_ap, ckvn_k, kpe_ap, kpe_k, kt, rope_cs):
            cb, cT, kc96, knb, KTm, VAm, w_kvb = M["cb"], M["cT"], M["kc96"], M["knb"], M["KTm"], M["VAm"], M["w_kvb"]
            self.cp("act", cb, ckvn_ap, r=[ckvn_k], w=["cb"])
            b = self.bank("tr")
            for kc in range(2):
                self.tr(self.psb[b][:, kc * 128:(kc + 1) * 128], cb[:, kc * 128:(kc + 1) * 128], self.ident_b, r=["cb", "identb"], w=[("ps", b)])
            self.cp("dve", cT, self.psb[b][:, 0:256].rearrange("p (a b) -> p a b", a=2), r=[("ps", b)], w=["cT"])
            for bk in range(2):
                b = self.bank("mm")
                for kc in range(2):
                    self.mm(self.ps[b][:, :], cT[:, kc, :], w_kvb[:, kc, bk * 512:(bk + 1) * 512], kc == 0, kc == 1,
                            r=["cT", "w_kvb"], w=[("ps", b)])
                pv = self.ps[b][:, :].rearrange("p (h d) -> p h d", h=4)
                self.cp("act", kc96[:, bk * 4:(bk + 1) * 4, 0:64], pv[:, :, 0:64], r=[("ps", b)], w=["kc96"])
                self.cp("dve", VAm[:, kt, bk * 4:(bk + 1) * 4, 0:64], pv[:, :, 64:128], r=[("ps", b)], w=[("VAm", kt)])
            self.cp("pool", kc96[:, :, 64:96], kpe_ap.unsqueeze(1).broadcast_to([128, 8, 32]), r=[kpe_k], w=["kc96"])
            norm96(M, sq96, gk96, rope_cs)
            b = self.bank("tr")
            for hh in range(8):
                self.tr(self.psb[b][0:96, hh * 128:(hh + 1) * 128], knb[:, hh, :], self.ident_b, r=["knb", "identb"], w=[("ps", b)])
            self.cp("dve", KTm[0:96, :, kt * 128:(kt + 1) * 128], self.psb[b][0:96, :].rearrange("p (a b) -> p a b", a=8),
                    r=[("ps", b)], w=[("KTm", kt)])

        def m_own(t, n, W, cqb):
            sq96, ckvf, ckvn, kpe, st1 = W["sq96"], W["ckvf"], W["ckvn"], W["kpe"], W["st1"]
            h = hTm
            self.normmod(t, 0, h, hk)
            b1 = self.bank("mm")
            for kc in range(8):
                self.mm(self.ps[b1][:, :], h[:, kc, :], w_in[:, kc, 1568:2080], kc == 0, kc == 7, r=[hk] + wk, w=[("ps", b1)])
            b2 = self.bank("mm")
            for kc in range(8):
                self.mm(self.ps[b2][:, 0:160], h[:, kc, :], w_in[:, kc, 2080:2240], kc == 0, kc == 7, r=[hk] + wk, w=[("ps", b2)])
            sqf = sq96.rearrange("p a b -> p (a b)")
            self.act(sqf[:, 0:384], self.ps[b1][:, 0:384], AF.Square, r=[("ps", b1)], w=["sq96", "st1"], accum_out=st1[:, 0:1])
            self.rstd(st1[:, 0:1], st1[:, 0:1], 384, r=["st1"], w=["st1"])
            self.stt(cqb[:, n, :], self.ps[b1][:, 0:384], st1[:, 0:1], gqn, ALU.mult, ALU.mult, r=[("ps", b1), "st1", "cst"], w=[("cqb", n)])
            self.cp("act", ckvf[:, 0:128], self.ps[b1][:, 384:512], r=[("ps", b1)], w=["ckvf"])
            self.cp("act", ckvf[:, 128:256], self.ps[b2][:, 0:128], r=[("ps", b2)], w=["ckvf"])
            self.cp("dve", kpe, self.ps[b2][:, 128:160], r=[("ps", b2)], w=["kpe"])
            self.act(sqf[:, 0:256], ckvf, AF.Square, r=["ckvf"], w=["sq96", "st1b"], accum_out=st1[:, 1:2])
            self.rstd(st1[:, 1:2], st1[:, 1:2], 256, r=["st1b"], w=["st1b"])
            self.stt(ckvn, ckvf, st1[:, 1:2], gkvn, ALU.mult, ALU.mult, r=["ckvf", "st1b", "cst"], w=["ckvn"])

        def m_attn(M, sq96, tiles, cqb, rope_q, NK):
            QTm, omb, OM, KTm, VAm, cqT, kc96, knb, w_qb = (M[k] for k in ("QTm", "omb", "OM", "KTm", "VAm", "cqT", "kc96", "knb", "w_qb"))
            kt_list = list(range(NK))
            kkeys = [("KTm", kt) for kt in kt_list]
            vkeys = [("VAm", kt) for kt in kt_list] + ["VA1"]
            nq = len(tiles)
            for n, t in enumerate(tiles):
                b = self.bank("tr")
                for kc in range(3):
                    self.tr(self.psb[b][:, kc * 128:(kc + 1) * 128], cqb[:, n, kc * 128:(kc + 1) * 128], self.ident_b,
                            r=[("cqb", n), "identb"], w=[("ps", b)])
                self.cp("act", cqT, self.psb[b][:, 0:384].rearrange("p (a b) -> p a b", a=3), r=[("ps", b)], w=["cqT"])
                for bk in range(2):
                    b = self.bank("mm")
                    for kc in range(3):
                        self.mm(self.ps[b][:, 0:384], cqT[:, kc, :], w_qb[:, kc, bk * 384:(bk + 1) * 384], kc == 0, kc == 2,
                                r=["cqT", "w_qb"], w=[("ps", b)])
                    self.cp("act", kc96[:, bk * 4:(bk + 1) * 4, :], self.ps[b][:, 0:384].rearrange("p (h d) -> p h d", h=4),
                            r=[("ps", b)], w=["kc96"])
                norm96(M, sq96, gq96, None if rope_q is None else (rope_q[0][:, n, :, :], rope_q[1]))
                b = self.bank("tr")
                for hh in range(8):
                    self.tr(self.psb[b][0:96, hh * 128:(hh + 1) * 128], knb[:, hh, :], self.ident_b, r=["knb", "identb"], w=[("ps", b)])
                self.cp("dve", QTm[0:96, :, n * 128:(n + 1) * 128], self.psb[b][0:96, :].rearrange("p (a b) -> p a b", a=8),
                        r=[("ps", b)], w=[("QTm", n)])
            for hh in range(8):
                self.attention(
                    KT=lambda st, hh=hh: KTm[0:96, hh, st * 128:(st + 1) * 128], kparts=96, kt_list=kt_list,
                    QTv=QTm[0:96, hh, 0:nq * 128], nq=nq,
                    VA_of=lambda st, hh=hh: VAm[:, st, hh, :], scale=float(96 ** -0.5), Pt=Pt,
                    out_of=lambda j, hh=hh: (omb[:, j, hh * 64:(hh + 1) * 64], ("omb", j)),
                    kr=kkeys, qr=[("QTm", j) for j in range(nq)], vr=vkeys, ow=None)
            for n, t in enumerate(tiles):
                b = self.bank("tr")
                for c in range(4):
                    self.tr(self.psb[b][:, c * 128:(c + 1) * 128], omb[:, n, c * 128:(c + 1) * 128], self.ident_b,
                            r=[("omb", n), "identb"], w=[("ps", b)])
                self.cp("act", OM[:, :, n * 128:(n + 1) * 128], self.psb[b][:, 0:512].rearrange("p (a b) -> p a b", a=4),
                        r=[("ps", b)], w=[("OM", n)])
                banks = []
                for hb in range(2):
                    b = self.bank("mm")
                    banks.append(b)
                    for cc in range(4):
                        c = hb * 4 + cc
                        for kc in range(8):
                            rhs = OG[:, kc, t * 128:(t + 1) * 128] if kc < 4 else OM[:, kc - 4, n * 128:(n + 1) * 128]
                            self.mm(self.ps[b][:, cc * 128:(cc + 1) * 128], w_out[:, kc, c * 128:(c + 1) * 128],
                                    rhs, kc == 0, kc == 7,
                                    r=["w_out", ("OG", t), ("OM", n)], w=[("ps", b)])
                self.mixer_residual(t, banks)

        st_ = self.sseg["tiles"]
        A.reset(base_off)
        B = g_alloc(2, persist=SP)
        g_prep(st_, B)
        g_scan(2, B, None)
        ccgs, ccgd = "cc_src_g%d" % l, "cc_dst_g%d" % l
        gsrc = A.alloc([4, 129], F32)
        self.tt("dve", gsrc[:, :, 128], SP["etot"][:, :, 0], SP["etot"][:, :, 1], ALU.mult, r=[("etot", 0), ("etot", 1)], w=["gsrc_a"])
        for z in range(2):
            for fc in range(2):
                zf = z * 2 + fc
                self.cp("act", gsrc[:, zf, 0:128], B["Sst"][z][fc], r=[("S", z, fc)], w=[("gsrc", zf)])
        self.dma("sp", ccg_src.ap().rearrange("(zf p) c -> p zf c", p=128), gsrc,
                 r=["gsrc_a"] + [("gsrc", zf) for zf in range(4)], w=[ccgs], dq="bnc_g")
        self.allgather((l, "g"), r=[ccgs], w=[ccgd])
        self.P.barrier()
        if "x1" in self.stages:
            return
        A.reset(base_off)
        W = own_alloc()
        ccms, ccmd = "cc_src_m%d" % l, "cc_dst_m%d" % l
        for n, t in enumerate(st_):
            m_own(t, n, W, SP["cqb"])
            self.dma("sp", ccm_src[n * 128:(n + 1) * 128, 0:256], W["ckvn"], r=["ckvn"], w=[ccms], dq="bnc_m")
            self.dma("sp", ccm_src[n * 128:(n + 1) * 128, 256:288], W["kpe"], r=["kpe"], w=[ccms], dq="bnc_m")
        self.allgather((l, "m"), r=[ccms], w=[ccmd])
        self.P.barrier()
        if "x2" in self.stages:
            return

        for sq_ in self.seqs:
            tiles = sq_["tiles"]
            bi = sq_["bidx"]
            A.reset(base_off)
            B = g_alloc(2)
            g_prep(tiles, B)
            g_scan(2, B, bi)
            g_out(tiles, B["oT"], B["rsT"])
            self.P.barrier()
            A.reset(base_off)
            M = m_alloc(2)
            W = own_alloc()
            cqb = A.alloc([2, 384], BF16)
            for n, t in enumerate(tiles):
                m_own(t, n, W, cqb)
                self.dma("sp", O["o_ckv"][bi, i, n * 128:(n + 1) * 128, :], W["ckvn"], r=["ckvn"], w=[], dq="ckvn_o")
                self.dma("sp", O["o_kpe"][bi, i, n * 128:(n + 1) * 128, :], W["kpe"], r=["kpe"], w=[], dq="kpe_o")
                kside(M, W["sq96"], W["ckvn"], "ckvn", W["kpe"], "kpe", n, None)
            m_attn(M, W["sq96"], tiles, cqb, None, 2)
            self.P.barrier()

        if "x3" in self.stages:
            return
        A.reset(base_off)
        gU = A.alloc([16, 129], F32)
        self.dma("sp", gU, ccg_dst.ap().rearrange("(rz p) c -> p rz c", p=128), r=[ccgd], w=["gU"], dq="gU")
        Sin = [[A.alloc([128], F32) for _ in range(2)] for _ in range(2)]
        Sib = [[A.alloc([128], BF16) for _ in range(2)] for _ in range(2)]
        tS = A.alloc([128], F32)
        for z in range(2):
            for fc in range(2):
                zf = z * 2 + fc
                S_ = Sin[z][fc]
                sk = ("Sin", z, fc)
                self.dma("sp", S_, I["c_gla"][i, z, fc * 128:(fc + 1) * 128, :], r=[], w=[sk], dq=sk)
                ranks = [0, 1, 2, 3] if z == 0 else [3, 2, 1, 0]
                for k in ranks:
                    gi = k * 4 + zf
                    self.stt(tS, S_, gU[:, gi, 128:129], gU[:, gi, 0:128], ALU.mult, ALU.add, r=[sk, "gU"], w=["tS"])
                    self.tt("dve", tS, tS, S_, ALU.subtract, r=["tS", sk], w=["tS"])
                    self.stt(S_, tS, sel[:, z * 4 + k:z * 4 + k + 1], S_, ALU.mult, ALU.add, r=["tS", sk, "cst"], w=[sk])
                self.cp("act", Sib[z][fc], S_, r=[sk], w=[("Sib", z, fc)])
        if "x5" in self.stages:
            self.P.barrier()
            return
        for z in range(2):
            order = [0, 1] if z == 0 else [1, 0]
            for oi, n in enumerate(order):
                nc_ = slice(n * 128, (n + 1) * 128)
                bh = [self.bank("acc"), self.bank("acc")]
                for hp in range(2):
                    pr = slice(hp * 64, (hp + 1) * 64)
                    for fc in range(2):
                        self.mm(self.ps[bh[hp]][:, fc * 128:(fc + 1) * 128], Sib[z][fc][pr, :], SP["qlT"][z][pr, fc, nc_], True, True,
                                r=[("Sib", z, fc), ("qlT", z, n)], w=[("ps", bh[hp])])
                oview = SP["oT"][:, :, nc_].rearrange("p (f h) t -> p f h t", f=2, h=2)
                for hp in range(2):
                    pb = self.ps[bh[hp]][:, 0:256].rearrange("p (a b) -> p a b", a=2)
                    self.tt("dve", oview[:, :, hp, :], oview[:, :, hp, :], pb, ALU.add, r=[("ps", bh[hp]), ("oT", n)], w=[("oT", n)])
                if oi == 0:
                    for fc in range(2):
                        zf = z * 2 + fc
                        sk = ("Sin", z, fc)
                        self.P.op("dve", lambda h, S_=Sin[z][fc], e=SP["etot"][:, zf, n:n + 1]: h.tensor_scalar(S_, S_, e, None, ALU.mult),
                                  r=[sk, ("etot", n)], w=[sk])
                        self.cp("act", Sib[z][fc], Sin[z][fc], r=[sk], w=[("Sib", z, fc)])
        if "x6" in self.stages:
            self.P.barrier()
            return
        g_out(st_, SP["oT"], SP["rsT"])
        self.P.barrier()
        if "x4" in self.stages:
            return
        A.reset(base_off)
        M = m_alloc(12)
        sq96 = A.alloc([8, 96], F32)
        ropem_all = A.alloc([8, 2, 16], F32)
        ropem_own = A.alloc([2, 2, 16], F32)
        ckp = A.alloc([4, 32], F32)
        cck = A.alloc([256], F32)
        kgm = [A.alloc([288], F32) for _ in range(2)]
        self.dma("sp", ropem_all, I["ropem_all"].rearrange("(t p) c q -> p t c q", p=128), r=[], w=["ropem_all"], dq="ropem")
        self.dma("sp", ropem_own, I["ropem"].rearrange("(t p) c q -> p t c q", p=128), r=[], w=["ropem_own"], dq="ropem")
        self.dma("sp", ckp, I["c_kpe"][i].rearrange("(t p) n -> p t n", p=128), r=[], w=["ckp"], dq="ckp")
        for kt in range(4):
            self.dma("sp", cck, I["c_ckv"][i, kt * 128:(kt + 1) * 128, :], r=[], w=["cck"], dq="cck")
            kside(M, sq96, cck, "cck", ckp[:, kt, :], "ckp", kt, None)
        for k in range(8):
            kg_ = kgm[k % 2]
            kk = ("kgm", k % 2)
            self.dma("sp", kg_, ccm_dst[k * 128:(k + 1) * 128, :], r=[ccmd], w=[kk], dq=kk)
            kside(M, sq96, kg_[:, 0:256], kk, kg_[:, 256:288], kk, 4 + k, (ropem_all[:, k, :, :], "ropem_all"))
        m_attn(M, sq96, st_, SP["cqb"], (ropem_own, "ropem_own"), 12)


def _rope_tables(n_tok, d_rot):
    t = np.arange(n_tok, dtype=np.int32)
    pos = np.stack([t // 64, t % 64], axis=-1).astype(np.float32)
    quarter = d_rot // 4
    inv = np.power(np.float32(10000.0), -np.arange(quarter, dtype=np.float32) / np.float32(quarter)).astype(np.float32)
    ang = pos[:, :, None] * inv
    cos = np.cos(ang).astype(np.float32).reshape(n_tok, 2 * quarter)
    sin = np.sin(ang).astype(np.float32).reshape(n_tok, 2 * quarter)
    return np.ascontiguousarray(np.stack([cos, sin], axis=1))


def _build_cst(inp, b, j):
    c = np.zeros((128, NCST), np.float32)

    def put(name, arr, parts=128):
        o, n = CST_OFF[name]
        c[0:parts, o:o + n] = np.asarray(arr, np.float32).reshape(parts, n)

    s = np.arange(128)[:, None]
    t = np.arange(128)[None, :]
    v = np.float32(-1.0 / 16.0)
    put("ident", np.eye(128))
    put("trim0", (s <= t) * v)
    put("trim1", (s >= t) * v)
    put("tris0", (s > t) * v)
    put("tris1", (s < t) * v)
    put("mask0", (s <= t) * 1.0)
    put("mask1", (s >= t) * 1.0)
    put("rm", np.array([[1, 0], [0, 1], [1, 1]], np.float32), parts=3)
    cond = np.stack([inp["c_ctx"].reshape(8, 128).T, inp["c"][b].reshape(8, 128).T], axis=-1)
    put("cond", cond)
    selv = np.array([1.0 if k < j else 0.0 for k in range(4)] + [1.0 if k > j else 0.0 for k in range(4)], np.float32)
    put("sel", np.broadcast_to(selv[None, :], (128, 8)))
    put("gmix", inp["norm_mix_g"].reshape(4, 8, 128).transpose(2, 0, 1))
    put("gffn", inp["norm_ffn_g"].reshape(4, 8, 128).transpose(2, 0, 1))
    for i in range(2):
        put("gq%d" % i, np.broadcast_to(inp["gqa_qn_g"][i][None, :], (128, 64)))
        put("gk%d" % i, np.broadcast_to(inp["gqa_kn_g"][i][None, :], (128, 64)))
        put("gout%d" % i, inp["gla_out_g"][i].reshape(128, 1))
        put("gqn%d" % i, np.broadcast_to(inp["mla_q_norm_g"][i][None, :], (128, 384)))
        put("gkvn%d" % i, np.broadcast_to(inp["mla_kv_norm_g"][i][None, :], (128, 256)))
        put("gq96%d" % i, np.broadcast_to(inp["mla_qn_g"][i][None, :], (128, 96)))
        put("gk96%d" % i, np.broadcast_to(inp["mla_kn_g"][i][None, :], (128, 96)))
    return c


_NC_CACHE = {}


def _get_nc(debug=(), stages=None):
    key = (tuple(sorted(debug)), None if stages is None else tuple(sorted(stages)))
    if key not in _NC_CACHE:
        kb = KB(debug, stages)
        nc = kb.build()
        _NC_CACHE[key] = (nc, kb)
    return _NC_CACHE[key]


def make_in_maps(inp):
    inp = {k: np.ascontiguousarray(np.asarray(v)) for k, v in inp.items()}
    ropeg = _rope_tables(1024, 64)
    ropem = _rope_tables(1024, 32)
    shared = dict(
        ffn_w_in=inp["ffn_w_in"], ffn_w_out=inp["ffn_w_out"],
        ab_w_in=inp["ab_w_in"], ab_w_out=inp["ab_w_out"], a_w2=inp["gla_a_w2"], a_b=inp["gla_a_b"],
        w_qb=inp["mla_w_qb"], w_kvb=inp["mla_w_kvb"], gqa_w_in=inp["gqa_w_in"], gqa_w_out=inp["gqa_w_out"])
    in_maps = []
    for core in range(8):
        b, j = core // 4, core % 4
        m = dict(shared)
        m["ropem_all"] = ropem
        m["ada_w"] = np.ascontiguousarray(inp["ada_w"][:, :, j * 1536:(j + 1) * 1536])
        m["ada_b"] = np.ascontiguousarray(inp["ada_b"][:, j * 1536:(j + 1) * 1536])
        m["ropeg"] = np.ascontiguousarray(ropeg[j * 256:(j + 1) * 256])
        m["ropem"] = np.ascontiguousarray(ropem[j * 256:(j + 1) * 256])
        m["cst"] = _build_cst(inp, b, j)
        m["xp"] = np.ascontiguousarray(inp["x_prompt"][core * 4:(core + 1) * 4].reshape(1024, D))
        m["xs"] = np.ascontiguousarray(inp["x_sample"][b, j * 256:(j + 1) * 256])
        m["c_ckv"] = np.ascontiguousarray(inp["cache_mla_ckv"][b])
        m["c_kpe"] = np.ascontiguousarray(inp["cache_mla_kpe"][b])
        m["c_gla"] = np.ascontiguousarray(inp["state_gla"][b].reshape(2, 2, 256, 128))
        m["c_gk"] = np.ascontiguousarray(inp["cache_gqa_k"][b].reshape(2, 512, 256))
        m["c_gv"] = np.ascontiguousarray(inp["cache_gqa_v"][b].reshape(2, 512, 256))
        in_maps.append(m)
    return in_maps


def kernel(**inputs):
    nc, kb = _get_nc()
    in_maps = make_in_maps(inputs)
    res = run_bass_kernel_spmd(nc, in_maps, core_ids=list(range(8)))
    R = res.results
    y_prompt = np.concatenate([R[c]["yp"].reshape(4, 256, D) for c in range(8)], axis=0)
    y_sample = np.stack([np.concatenate([R[4 * b + j]["ys"] for j in range(4)], axis=0) for b in range(2)], axis=0)
    new_ckv = np.concatenate([R[c]["o_ckv"] for c in range(8)], axis=0)
    new_kpe = np.concatenate([R[c]["o_kpe"] for c in range(8)], axis=0)
    new_gla = np.concatenate([R[c]["o_gla"].reshape(4, 2, 2, 4, 64, 128) for c in range(8)], axis=0)
    new_k = np.concatenate([R[c]["o_gk"].reshape(4, 2, 256, 4, 64) for c in range(8)], axis=0)
    new_v = np.concatenate([R[c]["o_gv"].reshape(4, 2, 256, 4, 64) for c in range(8)], axis=0)
    outs = (y_prompt, y_sample, new_ckv, new_kpe, new_gla, new_k, new_v)
    return tuple(np.ascontiguousarray(o, dtype=np.float32) for o in outs)
```

```python
import bisect
import contextlib
import numpy as np
import concourse.bass as bass
import concourse.mybir as mybir
from concourse.bass_utils import run_bass_kernel_spmd

F32 = mybir.dt.float32
BF16 = mybir.dt.bfloat16
AF = mybir.ActivationFunctionType
ALU = mybir.AluOpType
AX = mybir.AxisListType

ENGS = ("pe", "act", "dve", "pool", "sp")
RAW, WAR, WAW = 1, 2, 4
EPS = 1e-6
D = 1024
FH = 2816
ARENA_BYTES = 152 * 1024
NTOK = 1280
NTILE = 10


class Prog:
    def __init__(self):
        self.ops = []
        self.last_w = {}
        self.readers = {}

    def op(self, eng, fn, r=(), w=(), dq=None, inc=16):
        i = len(self.ops)
        deps = {}
        psr = [k for k in r if isinstance(k, tuple) and k and k[0] == "ps"]
        if psr:
            r = [k for k in r if not (isinstance(k, tuple) and k and k[0] == "ps")]
            w = list(w) + [k for k in psr if k not in w]
            for k in psr:
                lw = self.last_w.get(k)
                if lw is not None:
                    deps[lw] = deps.get(lw, 0) | RAW
        for k in r:
            lw = self.last_w.get(k)
            if lw is not None:
                deps[lw] = deps.get(lw, 0) | RAW
        for k in w:
            lw = self.last_w.get(k)
            if lw is not None:
                deps[lw] = deps.get(lw, 0) | WAW
            for rd in self.readers.get(k, ()):
                if rd != i:
                    deps[rd] = deps.get(rd, 0) | WAR
        for k in r:
            self.readers.setdefault(k, []).append(i)
        for k in w:
            self.last_w[k] = i
            self.readers[k] = []
        deps.pop(i, None)
        self.ops.append(dict(eng=eng, fn=fn, deps=deps, dq=dq, bar=None, inc=inc))
        return i

    def barrier(self):
        first = len(self.ops)
        for e in ENGS:
            self.ops.append(dict(eng=e, fn="drain", deps={}, dq=None, bar=("sig", first)))
        sig_ids = list(range(first, first + len(ENGS)))
        for e in ENGS:
            self.ops.append(dict(eng=e, fn="nop", deps={s: RAW for s in sig_ids}, dq=None, bar=("wait", first)))
        iscc = lambda k: isinstance(k, str) and k.startswith("cc")
        self.last_w = {k: v for k, v in self.last_w.items() if iscc(k)}
        self.readers = {k: v for k, v in self.readers.items() if iscc(k)}

    def emit(self, nc, es):
        ops = self.ops
        n = len(ops)
        needed = [False] * n
        for i, o in enumerate(ops):
            kept = []
            for d, kind in o["deps"].items():
                od = ops[d]
                if od["dq"] is None and o["dq"] is None and od["eng"] == o["eng"] and o["bar"] is None:
                    if o["eng"] == "pe":
                        continue
                kept.append(d)
                needed[d] = True
            o["kdeps"] = kept
        eng_sem = {e: es.enter_context(nc.semaphore("sem_" + e)) for e in ENGS}
        dq_keys = []
        seen = set()
        for o in ops:
            if o["dq"] is not None and o["dq"] not in seen:
                seen.add(o["dq"])
                dq_keys.append(o["dq"])
        dq_sem = {k: es.enter_context(nc.semaphore("dq_%d" % j)) for j, k in enumerate(dq_keys)}
        dq_idx = {k: [] for k in dq_keys}
        dq_cum = {k: [0] for k in dq_keys}
        eng_cnt = {e: 0 for e in ENGS}

        def dq_before(k, i):
            return dq_cum[k][bisect.bisect_left(dq_idx[k], i)]

        for i, o in enumerate(ops):
            if o["dq"] is not None:
                dq_idx[o["dq"]].append(i)
                dq_cum[o["dq"]].append(dq_cum[o["dq"]][-1] + o["inc"])
                o["sig"] = ("dq", o["dq"])
            elif needed[i] or (o["bar"] is not None and o["bar"][0] == "sig"):
                eng_cnt[o["eng"]] += 1
                o["sig"] = ("eng", o["eng"], eng_cnt[o["eng"]])
            else:
                o["sig"] = None
        per_eng = {e: [] for e in ENGS}
        for i, o in enumerate(ops):
            per_eng[o["eng"]].append(i)
        self.n_sems = len(ENGS) + len(dq_keys)
        self.counts = {e: len(per_eng[e]) for e in ENGS}

        def run(e, h):
            waited = {}
            for i in per_eng[e]:
                o = ops[i]
                waits = {}
                for d in o["kdeps"]:
                    od = ops[d]
                    if od["dq"] is not None:
                        k = od["dq"]
                        cnt = dq_before(k, i)
                        key = ("dq", k)
                        waits[key] = max(waits.get(key, 0), cnt)
                    else:
                        key = ("eng", od["eng"])
                        waits[key] = max(waits.get(key, 0), od["sig"][2])
                if o["bar"] is not None and o["bar"][0] == "sig" and e == "sp":
                    for k in dq_keys:
                        if isinstance(k, str) and k.startswith("cc"):
                            continue
                        cnt = dq_before(k, i)
                        if cnt:
                            waits[("dq", k)] = cnt
                for key, v in waits.items():
                    if waited.get(key, 0) >= v:
                        continue
                    waited[key] = v
                    sem = dq_sem[key[1]] if key[0] == "dq" else eng_sem[key[1]]
                    h.wait_ge(sem, v)
                if o["fn"] == "drain":
                    inst = h.nop() if e == "sp" else h.drain()
                elif o["fn"] == "nop":
                    inst = None
                else:
                    inst = o["fn"](h)
                s = o["sig"]
                if s is not None:
                    if s[0] == "dq":
                        inst.then_inc(dq_sem[s[1]], o["inc"])
                    else:
                        inst.then_inc(eng_sem[s[1]], 1)
            if e == "sp":
                for k in dq_keys:
                    h.wait_ge(dq_sem[k], dq_cum[k][-1])

        with nc.Block() as block:
            @block.tensor
            def _(h):
                run("pe", h)

            @block.scalar
            def _(h):
                run("act", h)

            @block.vector
            def _(h):
                run("dve", h)

            @block.gpsimd
            def _(h):
                run("pool", h)

            @block.sync
            def _(h):
                run("sp", h)


class Arena:
    def __init__(self, t, nbytes):
        self.t = t
        self.nbytes = nbytes
        self.off = 0
        self.peak = 0

    def reset(self, off=0):
        self.off = off

    def alloc(self, free_shape, dtype, parts=128):
        n = int(np.prod(free_shape))
        esz = 4 if dtype == F32 else 2
        sz = (n * esz + 31) // 32 * 32
        o = self.off
        assert o + sz <= self.nbytes, ("arena overflow", o, sz, self.nbytes)
        self.off = o + sz
        self.peak = max(self.peak, self.off)
        ap = self.t[0:parts, o // 2:(o + n * esz) // 2]
        if dtype == F32:
            ap = ap.bitcast(F32)
        fs = list(free_shape)
        if len(fs) == 2:
            ap = ap.rearrange("p (a b) -> p a b", a=fs[0], b=fs[1])
        elif len(fs) == 3:
            ap = ap.rearrange("p (a b c) -> p a b c", a=fs[0], b=fs[1], c=fs[2])
        elif len(fs) == 4:
            ap = ap.rearrange("p (a b c d) -> p a b c d", a=fs[0], b=fs[1], c=fs[2], d=fs[3])
        return ap


def _cst_layout():
    off = {}
    o = 0

    def add(name, n):
        nonlocal o
        off[name] = (o, n)
        o += n

    add("ident", 128)
    add("trim0", 128)
    add("trim1", 128)
    add("tris0", 128)
    add("tris1", 128)
    add("mask0", 128)
    add("mask1", 128)
    add("rm", 2)
    add("cond", 16)
    add("sel", 8)
    add("gmix", 32)
    add("gffn", 32)
    for i in range(2):
        add("gq%d" % i, 64)
        add("gk%d" % i, 64)
        add("gout%d" % i, 1)
        add("gqn%d" % i, 384)
        add("gkvn%d" % i, 256)
        add("gq96%d" % i, 96)
        add("gk96%d" % i, 96)
    return off, o


CST_OFF, NCST = _cst_layout()


class KB:
    def __init__(self, debug=(), stages=None):
        self.stages = set(stages) if stages is not None else {"adaln", "ffn", "mixc", "mixab", "P", "S"}
        self.debug = set(debug)
        self.dbg_outs = {}

    def mm(self, out, lhsT, rhs, start, stop, r, w, **kw):
        self.P.op("pe", lambda h: h.matmul(out, lhsT, rhs, start=start, stop=stop, **kw), r=r, w=w)

    def tr(self, out, in_, ident, r, w):
        self.P.op("pe", lambda h: h.transpose(out, in_, ident), r=r, w=w)

    def act(self, out, in_, func, r, w, **kw):
        self.P.op("act", lambda h: h.activation(out, in_, func, **kw), r=r, w=w)

    def tt(self, eng, out, a, b, op, r, w):
        self.P.op(eng, lambda h: h.tensor_tensor(out, a, b, op), r=r, w=w)

    def stt(self, out, in0, scalar, in1, op0, op1, r, w):
        self.P.op("dve", lambda h: h.scalar_tensor_tensor(out, in0, scalar, in1, op0, op1), r=r, w=w)

    def cp(self, eng, out, in_, r, w):
        if eng == "act":
            self.P.op("act", lambda h: h.copy(out, in_), r=r, w=w)
        else:
            self.P.op(eng, lambda h: h.tensor_copy(out, in_), r=r, w=w)

    def recip(self, out, in_, r, w):
        self.P.op("dve", lambda h: h.reciprocal(out, in_), r=r, w=w)

    def red(self, out, in_, r, w):
        self.P.op("dve", lambda h: h.tensor_reduce(out, in_, AX.X, ALU.add), r=r, w=w)

    def memset(self, eng, ap, val, w):
        self.P.op(eng, lambda h: h.memset(ap, val), w=w)

    def dma(self, q, out, in_, r, w, dq, **kw):
        self.P.op(q, lambda h: h.dma_start(out=out, in_=in_, **kw), r=r, w=w, dq=dq)

    def allgather(self, key, r, w):
        src, dst = self.CC[key]
        name = "cc_%s_%s" % key
        self.P.op("pool", lambda h: h.collective_compute("AllGather", ALU.bypass, replica_groups=[[0, 1, 2, 3], [4, 5, 6, 7]],
                                                         ins=[src.ap().opt()], outs=[dst.ap().opt()]),
                  r=r, w=w, dq=name, inc=1)

    def bank(self, pool):
        lst, idx = self.pools[pool]
        b = lst[idx % len(lst)]
        self.pools[pool][1] = idx + 1
        return b

    def rstd(self, out, ss, n, r, w):
        self.act(out, ss, AF.Ln, r=r, w=w, bias=EPS, scale=1.0 / n)
        self.act(out, out, AF.Exp, r=w, w=w, scale=-0.5)

    def cst(self, name, parts=128):
        o, n = CST_OFF[name]
        return self.cst_t[0:parts, o:o + n]

    def dbg(self, name, ap, r, shape):
        if name not in self.debug:
            return
        t = self.nc.dram_tensor("dbg_" + name, list(shape), ap.dtype if hasattr(ap, "dtype") else F32, kind="ExternalOutput").ap()
        self.dbg_outs[name] = shape
        self.dma("sp", t, ap, r=r, w=[], dq=("dbg", name))

    def build(self):
        nc = bass.Bass("TRN2", target_bir_lowering=False)
        self.nc = nc
        self.P = Prog()

        def din(name, shape):
            return nc.dram_tensor(name, list(shape), F32, kind="ExternalInput").ap()

        def dout(name, shape):
            return nc.dram_tensor(name, list(shape), F32, kind="ExternalOutput").ap()

        I = {}
        I["cst"] = din("cst", [128, NCST])
        I["xp"] = din("xp", [1024, D])
        I["xs"] = din("xs", [256, D])
        I["ada_w"] = din("ada_w", [4, D, 1536])
        I["ada_b"] = din("ada_b", [4, 1536])
        I["ffn_w_in"] = din("ffn_w_in", [4, D, 2 * FH])
        I["ffn_w_out"] = din("ffn_w_out", [4, FH, D])
        I["ab_w_in"] = din("ab_w_in", [2, D, 2240])
        I["ab_w_out"] = din("ab_w_out", [2, D, D])
        I["a_w2"] = din("a_w2", [2, 2, 16, 256])
        I["a_b"] = din("a_b", [2, 2, 256])
        I["w_qb"] = din("w_qb", [2, 384, 768])
        I["w_kvb"] = din("w_kvb", [2, 256, 1024])
        I["gqa_w_in"] = din("gqa_w_in", [2, D, 1536])
        I["gqa_w_out"] = din("gqa_w_out", [2, D, D])
        I["c_ckv"] = din("c_ckv", [2, 512, 256])
        I["c_kpe"] = din("c_kpe", [2, 512, 32])
        I["c_gla"] = din("c_gla", [2, 2, 256, 128])
        I["c_gk"] = din("c_gk", [2, 512, 256])
        I["c_gv"] = din("c_gv", [2, 512, 256])
        I["ropeg"] = din("ropeg", [256, 2, 32])
        I["ropem"] = din("ropem", [256, 2, 16])
        I["ropem_all"] = din("ropem_all", [1024, 2, 16])
        O = {}
        O["yp"] = dout("yp", [1024, D])
        O["ys"] = dout("ys", [256, D])
        O["o_ckv"] = dout("o_ckv", [4, 2, 256, 256])
        O["o_kpe"] = dout("o_kpe", [4, 2, 256, 32])
        O["o_gla"] = dout("o_gla", [4, 2, 2, 256, 128])
        O["o_gk"] = dout("o_gk", [4, 2, 256, 256])
        O["o_gv"] = dout("o_gv", [4, 2, 256, 256])
        self.I, self.O = I, O
        self.CC = {}
        self.CC[("a", "a")] = (nc.dram_tensor("ccs_a", [3, 6144], F32), nc.dram_tensor("ccd_a", [12, 6144], F32))
        for l in range(4):
            if l % 2 == 0:
                self.CC[(l, "g")] = (nc.dram_tensor("ccs_g%d" % l, [512, 129], F32), nc.dram_tensor("ccd_g%d" % l, [2048, 129], F32))
                self.CC[(l, "m")] = (nc.dram_tensor("ccs_m%d" % l, [256, 288], F32), nc.dram_tensor("ccd_m%d" % l, [1024, 288], F32))
            else:
                self.CC[(l, "c")] = (nc.dram_tensor("ccs_c%d" % l, [256, 512], F32), nc.dram_tensor("ccd_c%d" % l, [1024, 512], F32))

        with contextlib.ExitStack() as es:
            self.xT = es.enter_context(nc.sbuf_tensor("xT", [128, 8, NTOK], F32))
            self.cst_t = es.enter_context(nc.sbuf_tensor("cst_sb", [128, NCST], F32))
            small = es.enter_context(nc.sbuf_tensor("small", [128, 4 * 48 * 2 + 96 + 8], F32))
            cbf = es.enter_context(nc.sbuf_tensor("cbf", [128, 256 + 16], BF16))
            arena_t = es.enter_context(nc.sbuf_tensor("arena", [128, ARENA_BYTES // 2], BF16))
            self.A = Arena(arena_t, ARENA_BYTES)
            self.ps = [es.enter_context(nc.psum_tensor("ps%d" % i, [128, 512], F32)) for i in range(8)]
            self.psb = [p.bitcast(BF16) for p in self.ps]
            self.pools = {"mm": [[0, 1, 2, 3], 0], "acc": [[4, 5], 0], "tr": [[6, 7], 0]}
            self.modt = small[:, 0:384].rearrange("p (l c k) -> p l c k", l=4, c=48, k=2)
            self.lsc = small[:, 384:480].rearrange("p (g a b) -> p g a b", g=2, a=6, b=8)
            self.eps_c = small[:, 480:481]
            self.one_c = small[:, 481:482]
            self.ident_b = cbf[:, 0:128]
            self.ones_b = cbf[:, 128:256]
            self.sc_b = cbf[:, 256:272].rearrange("p (k c) -> p k c", k=8, c=2)
            self.ident_f = self.cst("ident")

            self.prologue()
            if "adaln" in self.stages:
                self.adaln_all()
            self.run_all()
            self.P.emit(nc, es)
        return nc

    def prologue(self):
        self.dma("sp", self.cst_t[:, :], self.I["cst"], r=[], w=["cst"], dq="cst")
        self.memset("dve", self.eps_c, EPS, w=["small_c"])
        self.memset("dve", self.one_c, 1.0, w=["small_c"])
        self.memset("dve", self.ones_b, 1.0, w=["ones"])
        self.cp("dve", self.ident_b, self.ident_f, r=["cst"], w=["identb"])
        cond = self.cst("cond").rearrange("p (k c) -> p k c", k=8, c=2)
        self.act(self.sc_b, cond, AF.Silu, r=["cst"], w=["scb"])

    def adaln_all(self):
        A = self.A
        A.reset()
        slots = [A.alloc([8, 512], BF16) for _ in range(3)]
        mq = A.alloc([6144], F32)
        mtok = A.alloc([6144], F32)
        rm = self.cst("rm", parts=3)
        cc_src, cc_dst = self.CC[("a", "a")]
        k = 0
        for l in range(4):
            self.dma("sp", mq[2:3, l * 1536:(l + 1) * 1536], self.I["ada_b"][l:l + 1, :], r=[], w=[("mq", "b")], dq="mtokb")
            for j in range(3):
                s = k % 3
                k += 1
                src = self.I["ada_w"][l, :, j * 512:(j + 1) * 512].rearrange("(k p) n -> p k n", p=128)
                self.dma("pool", slots[s], src, r=[], w=[("adw", s)], dq=("adw", s))
                b = self.bank("mm")
                for kc in range(8):
                    self.mm(self.ps[b][0:2, :], self.sc_b[:, kc, :], slots[s][:, kc, :], kc == 0, kc == 7,
                            r=[("adw", s), "scb"], w=[("ps", b)])
                c0 = l * 1536 + j * 512
                self.cp("act", mq[0:2, c0:c0 + 512], self.ps[b][0:2, :], r=[("ps", b)], w=[("mq", l, j)])
        self.dma("sp", cc_src.ap(), mq[0:3, :], r=[("mq", "b")] + [("mq", l, j) for l in range(4) for j in range(3)],
                 w=["cc_src_a"], dq="bnc_a")
        self.allgather(("a", "a"), r=["cc_src_a"], w=["cc_dst_a"])
        dview = cc_dst.ap().rearrange("(j r) (l c) -> r l j c", r=3, l=4)
        for l in range(4):
            self.dma("sp", mtok[0:3, :].rearrange("r (j c) -> r j c", j=4), dview[:, l, :, :], r=["cc_dst_a"], w=["mtok"], dq="mtok")
            b = self.bank("mm")
            for c in range(48):
                self.mm(self.ps[b][:, 2 * c:2 * c + 2], mtok[0:3, c * 128:(c + 1) * 128], rm, True, True,
                        r=["mtok", "cst"], w=[("ps", b)])
            self.cp("dve", self.modt[:, l, :, :], self.ps[b][:, 0:96].rearrange("p (c k) -> p c k", c=48, k=2),
                    r=[("ps", b)], w=["modt"])
        self.P.barrier()

    def layer_scalars(self, l):
        gmix = self.cst("gmix").rearrange("p (l c) -> p l c", l=4, c=8)[:, l, :]
        gffn = self.cst("gffn").rearrange("p (l c) -> p l c", l=4, c=8)[:, l, :]
        for col in range(2):
            mv = self.modt[:, l, :, col]
            L = self.lsc[:, col, :, :]
            self.stt(L[:, 0, :], mv[:, 8:16], 1.0, gmix, ALU.add, ALU.mult, r=["modt", "cst"], w=["lsc"])
            self.cp("dve", L[:, 1, :], mv[:, 0:8], r=["modt"], w=["lsc"])
            self.cp("dve", L[:, 2, :], mv[:, 16:24], r=["modt"], w=["lsc"])
            self.stt(L[:, 3, :], mv[:, 32:40], 1.0, gffn, ALU.add, ALU.mult, r=["modt", "cst"], w=["lsc"])
            self.cp("dve", L[:, 4, :], mv[:, 24:32], r=["modt"], w=["lsc"])
            self.cp("dve", L[:, 5, :], mv[:, 40:48], r=["modt"], w=["lsc"])

    def alloc_norm_tmp(self):
        A = self.A
        self.nm_sq = A.alloc([8, 128], BF16)
        self.nm_rstd = A.alloc([128], F32)
        self.nm_tmp = A.alloc([8, 128], F32)

    def normmod(self, t, which, dst, dstkey):
        xv = self.xT[:, :, t * 128:(t + 1) * 128]
        xk = [("xT", t, c) for c in range(8)]
        grp = 0 if t < 8 else 1
        G = self.lsc[:, grp, 3 * which, :]
        SH = self.lsc[:, grp, 3 * which + 1, :]
        self.tt("pool", self.nm_tmp, xv, G.unsqueeze(2).broadcast_to([128, 8, 128]), ALU.mult, r=xk + ["lsc"], w=["nm_tmp"])
        self.act(self.nm_sq, xv, AF.Square, r=xk, w=["nm_sq"])
        b = self.bank("tr")
        for c in range(8):
            self.mm(self.ps[b][:, 0:128], self.ones_b, self.nm_sq[:, c, :], c == 0, c == 7, r=["nm_sq", "ones"], w=[("ps", b)])
        self.rstd(self.nm_rstd, self.ps[b][:, 0:128], D, r=[("ps", b)], w=["nm_rstd"])
        self.tt("dve", self.nm_tmp, self.nm_tmp, self.nm_rstd.unsqueeze(1).broadcast_to([128, 8, 128]), ALU.mult,
                r=["nm_tmp", "nm_rstd"], w=["nm_tmp"])
        self.tt("dve", dst, self.nm_tmp, SH.unsqueeze(2).broadcast_to([128, 8, 128]), ALU.add,
                r=["nm_tmp", "lsc"], w=[dstkey])

    def run_all(self):
        self.seqs = [dict(tiles=[2 * s, 2 * s + 1], ctx=False, rope=False, bidx=s) for s in range(4)]
        self.sseg = dict(tiles=[8, 9], ctx=True, rope=True, bidx=None)
        A = self.A
        A.reset()
        xin = [A.alloc([1024], F32) for _ in range(2)]
        for t in range(NTILE):
            s = t % 2
            src = self.I["xp"][t * 128:(t + 1) * 128, :] if t < 8 else self.I["xs"][(t - 8) * 128:(t - 7) * 128, :]
            self.dma("sp", xin[s], src, r=[], w=[("xin", s)], dq=("xin", s))
            for hb in range(2):
                b = self.bank("tr")
                for cc in range(4):
                    c = hb * 4 + cc
                    self.tr(self.ps[b][:, cc * 128:(cc + 1) * 128], xin[s][:, c * 128:(c + 1) * 128], self.ident_f,
                            r=[("xin", s), "cst"], w=[("ps", b)])
                self.cp("dve" if hb else "act", self.xT[:, hb * 4:hb * 4 + 4, t * 128:(t + 1) * 128],
                        self.ps[b][:, :].rearrange("p (a b) -> p a b", a=4),
                        r=[("ps", b)], w=[("xT", t, hb * 4 + cc) for cc in range(4)])
        self.P.barrier()
        for l in range(4):
            self.layer_scalars(l)
            if l % 2 == 0:
                if "mixab" in self.stages:
                    self.mixer_ab(l, l // 2)
            else:
                if "mixc" in self.stages:
                    self.mixer_c(l, l // 2)
            self.P.barrier()
            if "ffn" in self.stages:
                self.ffn(l)
            self.P.barrier()
        A.reset()
        yo = [A.alloc([1024], F32) for _ in range(2)]
        for t in range(NTILE):
            s = t % 2
            for hb in range(2):
                b = self.bank("tr")
                for cc in range(4):
                    c = hb * 4 + cc
                    self.tr(self.ps[b][:, cc * 128:(cc + 1) * 128], self.xT[:, c, t * 128:(t + 1) * 128], self.ident_f,
                            r=[("xT", t, c), "cst"], w=[("ps", b)])
                self.cp("dve" if hb else "act", yo[s][:, hb * 512:(hb + 1) * 512], self.ps[b][:, :],
                        r=[("ps", b)], w=[("yo", s, hb)])
            dst = self.O["yp"][t * 128:(t + 1) * 128, :] if t < 8 else self.O["ys"][(t - 8) * 128:(t - 7) * 128, :]
            self.dma("sp", dst, yo[s], r=[("yo", s, 0), ("yo", s, 1)], w=[], dq=("yo", s))

    def ffn(self, l):
        A = self.A
        A.reset()
        hT = A.alloc([8, NTOK], BF16)
        actT = A.alloc([22, NTOK], BF16)
        wi = [A.alloc([8, 2, 256], BF16) for _ in range(2)]
        wo = [A.alloc([11, 1024], BF16) for _ in range(2)]
        sg = [A.alloc([512], F32) for _ in range(2)]
        self.alloc_norm_tmp()
        W1 = self.I["ffn_w_in"]
        W2 = self.I["ffn_w_out"]
        TB = [(0, 512, 0, [0, 1, 2, 3]), (512, 512, 0, [4, 5, 6, 7]), (1024, 256, 1, [8, 9])]
        NS = len(wi)

        def load_wi(j2):
            s = j2 % NS
            for gu in range(2):
                c0 = gu * FH + j2 * 256
                src = W1[l, :, c0:c0 + 256].rearrange("(k p) n -> p k n", p=128)
                self.dma("pool", wi[s][:, :, gu, :], src, r=[], w=[("wi", s, gu)], dq=("wi", s))

        def load_wo(hf):
            src = W2[l, hf * 1408:(hf + 1) * 1408, :].rearrange("(j p) n -> p j n", p=128)
            self.dma("pool", wo[hf], src, r=[], w=[("wo", hf)], dq=("wo", hf))

        for j2 in range(NS):
            load_wi(j2)
        for t in range(NTILE):
            self.normmod(t, 1, hT[:, :, t * 128:(t + 1) * 128], ("hT", t))
        load_wo(0)
        load_wo(1)
        k = 0
        for j2 in range(11):
            s = j2 % NS
            for (t0, tn, grp, tl) in TB:
                hk = [("hT", q) for q in tl]
                for hf in range(2):
                    j = j2 * 2 + hf
                    bg = self.bank("mm")
                    for kc in range(8):
                        self.mm(self.ps[bg][:, 0:tn], wi[s][:, kc, 0, hf * 128:(hf + 1) * 128], hT[:, kc, t0:t0 + tn],
                                kc == 0, kc == 7, r=[("wi", s, 0)] + hk, w=[("ps", bg)])
                    bu = self.bank("mm")
                    for kc in range(8):
                        self.mm(self.ps[bu][:, 0:tn], wi[s][:, kc, 1, hf * 128:(hf + 1) * 128], hT[:, kc, t0:t0 + tn],
                                kc == 0, kc == 7, r=[("wi", s, 1)] + hk, w=[("ps", bu)])
                    sgi = k % 2
                    k += 1
                    self.act(sg[sgi][:, 0:tn], self.ps[bg][:, 0:tn], AF.Silu, r=[("ps", bg)], w=[("sg", sgi)])
                    self.tt("dve", actT[:, j, t0:t0 + tn], sg[sgi][:, 0:tn], self.ps[bu][:, 0:tn], ALU.mult,
                            r=[("sg", sgi), ("ps", bu)], w=[("act", j, t0)])
            if j2 + NS < 11:
                load_wi(j2 + NS)
        for hf in range(2):
            for c in range(8):
                for (t0, tn, grp, tl) in TB:
                    gate = self.lsc[:, grp, 5, :]
                    b = self.bank("mm")
                    for jj in range(11):
                        self.mm(self.ps[b][:, 0:tn], wo[hf][:, jj, c * 128:(c + 1) * 128], actT[:, hf * 11 + jj, t0:t0 + tn],
                                jj == 0, jj == 10, r=[("wo", hf), ("act", hf * 11 + jj, t0)], w=[("ps", b)])
                    xv = self.xT[:, c, t0:t0 + tn]
                    xk = [("xT", q, c) for q in tl]
                    self.stt(xv, self.ps[b][:, 0:tn], gate[:, c:c + 1], xv, ALU.mult, ALU.add, r=[("ps", b), "lsc"] + xk, w=xk)

    def mixer_residual(self, t, banks):
        grp = 0 if t < 8 else 1
        gate = self.lsc[:, grp, 2, :]
        tmp = self.nm_tmp
        for hb in range(2):
            b = banks[hb]
            pv = self.ps[b][:, :].rearrange("p (a b) -> p a b", a=4)
            tv = tmp[:, hb * 4:hb * 4 + 4, :]
            self.tt("dve", tv, pv, gate[:, hb * 4:hb * 4 + 4].unsqueeze(2).broadcast_to([128, 4, 128]), ALU.mult,
                    r=[("ps", b), "lsc"], w=["nm_tmp"])
            xv = self.xT[:, hb * 4:hb * 4 + 4, t * 128:(t + 1) * 128]
            xk = [("xT", t, hb * 4 + q) for q in range(4)]
            self.tt("pool", xv, xv, tv, ALU.add, r=["nm_tmp"] + xk, w=xk)

    def out_proj_pair(self, t0, w_out, rhs_of, rkeys):
        grp = 0 if t0 < 8 else 1
        gate = self.lsc[:, grp, 2, :]
        banks = []
        for bi in range(4):
            b = self.bank("mm")
            banks.append(b)
            for cc in range(2):
                c = 2 * bi + cc
                for kc in range(8):
                    self.mm(self.ps[b][:, cc * 256:(cc + 1) * 256], w_out[:, kc, c * 128:(c + 1) * 128], rhs_of(kc), kc == 0, kc == 7,
                            r=["w_out"] + rkeys, w=[("ps", b)])
        tmpv = self.nm_tmp.rearrange("p c t -> p (c t)")
        for bi in range(4):
            b = banks[bi]
            pv = self.ps[b][:, :].rearrange("p (a b) -> p a b", a=2)
            tv = tmpv[:, (bi % 2) * 512:(bi % 2 + 1) * 512].rearrange("p (a b) -> p a b", a=2)
            self.tt("dve", tv, pv, gate[:, 2 * bi:2 * bi + 2].unsqueeze(2).broadcast_to([128, 2, 256]), ALU.mult,
                    r=[("ps", b), "lsc"], w=["nm_tmp"])
            xv = self.xT[:, 2 * bi:2 * bi + 2, t0 * 128:(t0 + 2) * 128]
            xk = [("xT", t0 + q, 2 * bi + cc) for q in range(2) for cc in range(2)]
            self.tt("pool", xv, xv, tv, ALU.add, r=["nm_tmp"] + xk, w=xk)

    def rope(self, xv, H, Q, cs, tmps, r, w, keyp):
        x1 = xv[:, :, :, 0, :]
        x2 = xv[:, :, :, 1, :]
        c = cs[:, 0, :].rearrange("p (a q) -> p a q", a=2, q=Q).unsqueeze(1).broadcast_to([128, H, 2, Q])
        s = cs[:, 1, :].rearrange("p (a q) -> p a q", a=2, q=Q).unsqueeze(1).broadcast_to([128, H, 2, Q])
        t1, t2, t3, t4 = tmps
        k1, k2, k3, k4 = [(keyp, i) for i in range(4)]
        self.tt("dve", t1, x1, c, ALU.mult, r=r, w=[k1])
        self.tt("pool", t2, x2, s, ALU.mult, r=r, w=[k2])
        self.tt("dve", t3, x1, s, ALU.mult, r=r, w=[k3])
        self.tt("pool", t4, x2, c, ALU.mult, r=r, w=[k4])
        self.tt("dve", x1, t1, t2, ALU.subtract, r=[k1, k2, k3], w=w)
        self.tt("dve", x2, t3, t4, ALU.add, r=[k3, k4], w=w)

    def attention(self, KT, kparts, kt_list, QTv, nq, VA_of, scale, Pt, out_of, kr, qr, vr, ow):
        ob = self.bank("acc")
        first, last = kt_list[0], kt_list[-1]
        npt = len(Pt)

        def pv(st, pi):
            for j in range(nq):
                self.mm(self.ps[ob][:, j * 65:(j + 1) * 65], Pt[pi][:, j, :], VA_of(st), (st == first and j == 0), st == last,
                        r=[("Pt", pi)] + vr, w=[("ps", ob)], skip_group_check=True)

        pend = None
        for st in kt_list:
            sb = self.bank("mm")
            self.mm(self.ps[sb][:, 0:nq * 128], KT(st), QTv, True, True, r=kr + qr, w=[("ps", sb)])
            pi = self.pt_i % npt
            self.pt_i += 1
            self.act(Pt[pi][:, 0:nq, :], self.ps[sb][:, 0:nq * 128].rearrange("p (a b) -> p a b", a=nq), AF.Exp,
                     r=[("ps", sb)], w=[("Pt", pi)], scale=scale)
            if pend is not None:
                pv(*pend)
            pend = (st, pi)
        pv(*pend)
        ov = self.ps[ob][:, 0:nq * 65].rearrange("p (a b) -> p a b", a=nq)
        rd = self.at_rden
        self.recip(rd[:, 0:nq], ov[:, :, 64], r=[("ps", ob)], w=["at_rden"])
        for j in range(nq):
            dst, dk = out_of(j)
            self.P.op("dve", lambda h, j=j, dst=dst, rd=rd, ov=ov: h.tensor_scalar(dst, ov[:, j, 0:64], rd[:, j:j + 1], None, ALU.mult),
                      r=[("ps", ob), "at_rden"], w=[dk])

    def mixer_c(self, l, i):
        A = self.A
        A.reset()
        I, O = self.I, self.O
        w_in = A.alloc([8, 1536], BF16)
        w_out = A.alloc([8, 1024], BF16)
        hT = A.alloc([8, 256], BF16)
        KT = A.alloc([4, 1536], BF16)
        VA = A.alloc([12, 4, 65], BF16)
        self.alloc_norm_tmp()
        kvf = A.alloc([512], F32)
        sq = A.alloc([1024], F32)
        qn = A.alloc([1024], F32)
        st16 = A.alloc([16], F32)
        knb = A.alloc([256], BF16)
        qnb = A.alloc([1024], BF16)
        QT = A.alloc([16, 128], BF16)
        Pt = [A.alloc([4, 128], BF16) for _ in range(3)]
        ob = A.alloc([1024], BF16)
        OT = A.alloc([8, 256], BF16)
        self.at_rden = A.alloc([4], F32)
        rtm = [A.alloc([16, 2, 16], F32) for _ in range(4)]
        ropeg = A.alloc([2, 2, 32], F32)
        ck = A.alloc([4, 256], F32)
        cv = A.alloc([4, 256], F32)
        ckb = A.alloc([4, 256], BF16)
        kg = [A.alloc([512], F32) for _ in range(2)]
        self.pt_i = 0
        gq = self.cst("gq%d" % i)
        gk = self.cst("gk%d" % i)
        for kh in range(2):
            self.dma("pool", w_in[:, kh * 4:(kh + 1) * 4, :],
                     I["gqa_w_in"][i, kh * 512:(kh + 1) * 512, :].rearrange("(k p) n -> p k n", p=128), r=[], w=[("w_in", kh)], dq="w_in")
        self.dma("pool", w_out, I["gqa_w_out"][i].rearrange("(k p) n -> p k n", p=128), r=[], w=["w_out"], dq="w_out")
        self.memset("pool", VA[:, :, :, 64:65], 1.0, w=["VA1"])
        wk = [("w_in", 0), ("w_in", 1)]
        self.dma("sp", ropeg, I["ropeg"].rearrange("(t p) c q -> p t c q", p=128), r=[], w=["ropeg"], dq="ropeg")
        cc_src, cc_dst = self.CC[(l, "c")]

        def put_keys(knb_ap, kt, rk):
            b = self.bank("tr")
            for g in range(4):
                self.tr(self.psb[b][0:64, g * 128:(g + 1) * 128], knb_ap[:, g * 64:(g + 1) * 64], self.ident_b,
                        r=rk + ["identb"], w=[("ps", b)])
            self.cp("dve", KT[0:64, :, kt * 128:(kt + 1) * 128], self.psb[b][0:64, 0:512].rearrange("p (a b) -> p a b", a=4),
                    r=[("ps", b)], w=[("KT", kt)])

        def kv_side(t, n, rope_n):
            self.normmod(t, 0, hT[:, :, n * 128:(n + 1) * 128], ("hT", n))
            b = self.bank("mm")
            for kc in range(8):
                self.mm(self.ps[b][:, :], hT[:, kc, n * 128:(n + 1) * 128], w_in[:, kc, 1024:1536], kc == 0, kc == 7,
                        r=[("hT", n)] + wk, w=[("ps", b)])
            self.cp("act", kvf, self.ps[b][:, :], r=[("ps", b)], w=["kvf"])
            self.act(sq[:, 0:256], self.ps[b][:, 0:256], AF.Square, r=[("ps", b)], w=["sq"])
            self.red(st16[:, 0:4], sq[:, 0:256].rearrange("p (g d) -> p g d", g=4), r=["sq"], w=["st16"])
            self.rstd(st16[:, 0:4], st16[:, 0:4], 64, r=["st16"], w=["st16"])
            kv3 = kvf[:, 0:256].rearrange("p (g d) -> p g d", g=4)
            self.tt("dve", kv3, kv3, st16[:, 0:4].unsqueeze(2).broadcast_to([128, 4, 64]), ALU.mult, r=["kvf", "st16"], w=["kvf"])
            self.tt("dve", kv3, kv3, gk.unsqueeze(1).broadcast_to([128, 4, 64]), ALU.mult, r=["kvf", "cst"], w=["kvf"])
            if rope_n is not None:
                self.rope(kvf[:, 0:256].rearrange("p (h a b q) -> p h a b q", h=4, a=2, b=2, q=16), 4, 16, ropeg[:, rope_n, :, :],
                          [x[:, 0:4, :, :] for x in rtm], r=["kvf", "ropeg"], w=["kvf"], keyp="rtm")

        def q_attn_out(t, n, rope_n, kt_list):
            kkeys = [("KT", kt) for kt in kt_list]
            vkeys = [("VA", kt) for kt in kt_list] + ["VA1"]
            qb = []
            for bk in range(2):
                b = self.bank("mm")
                qb.append(b)
                for kc in range(8):
                    self.mm(self.ps[b][:, :], hT[:, kc, n * 128:(n + 1) * 128], w_in[:, kc, bk * 512:(bk + 1) * 512], kc == 0, kc == 7,
                            r=[("hT", n)] + wk, w=[("ps", b)])
                self.act(sq[:, bk * 512:(bk + 1) * 512], self.ps[b][:, :], AF.Square, r=[("ps", b)], w=["sq"])
            self.red(st16, sq.rearrange("p (g d) -> p g d", g=16), r=["sq"], w=["st16"])
            self.rstd(st16, st16, 64, r=["st16"], w=["st16"])
            for bk in range(2):
                self.tt("dve", qn[:, bk * 512:(bk + 1) * 512].rearrange("p (g d) -> p g d", g=8),
                        self.ps[qb[bk]][:, :].rearrange("p (g d) -> p g d", g=8),
                        st16[:, bk * 8:(bk + 1) * 8].unsqueeze(2).broadcast_to([128, 8, 64]), ALU.mult,
                        r=[("ps", qb[bk]), "st16"], w=["qn"])
            qn3 = qn.rearrange("p (g d) -> p g d", g=16)
            self.tt("pool", qn3, qn3, gq.unsqueeze(1).broadcast_to([128, 16, 64]), ALU.mult, r=["qn", "cst"], w=["qn"])
            if rope_n is not None:
                self.rope(qn.rearrange("p (h a b q) -> p h a b q", h=16, a=2, b=2, q=16), 16, 16, ropeg[:, rope_n, :, :], rtm,
                          r=["qn", "ropeg"], w=["qn"], keyp="rtm")
            self.cp("act", qnb, qn, r=["qn"], w=["qnb"])
            for hb in range(2):
                b = self.bank("tr")
                for hh in range(8):
                    h_ = hb * 8 + hh
                    self.tr(self.psb[b][0:64, hh * 128:(hh + 1) * 128], qnb[:, h_ * 64:(h_ + 1) * 64], self.ident_b,
                            r=["qnb", "identb"], w=[("ps", b)])
                self.cp("dve" if hb else "act", QT[0:64, hb * 8:(hb + 1) * 8, :],
                        self.psb[b][0:64, :].rearrange("p (a b) -> p a b", a=8), r=[("ps", b)], w=[("QT", hb)])
            for g in range(4):
                self.attention(
                    KT=lambda st, g=g: KT[0:64, g, st * 128:(st + 1) * 128], kparts=64, kt_list=kt_list,
                    QTv=QT[0:64, 4 * g:4 * g + 4, :], nq=4,
                    VA_of=lambda st, g=g: VA[:, st, g, :], scale=0.125, Pt=Pt,
                    out_of=lambda j, g=g: (ob[:, (4 * g + j) * 64:(4 * g + j + 1) * 64], ("ob", g)),
                    kr=kkeys, qr=[("QT", g // 2)], vr=vkeys, ow=None)
            b = self.bank("tr")
            for c in range(8):
                self.tr(self.psb[b][:, c * 128:(c + 1) * 128], ob[:, c * 128:(c + 1) * 128], self.ident_b,
                        r=[("ob", c // 2), "identb"], w=[("ps", b)])
            self.cp("act", OT[:, :, n * 128:(n + 1) * 128], self.psb[b][:, :].rearrange("p (a b) -> p a b", a=8), r=[("ps", b)], w=[("OT", n)])
            if n == 1:
                self.out_proj_pair(t - 1, w_out, lambda kc: OT[:, kc, :], [("OT", 0), ("OT", 1)])

        cck = "cc_src_c%d" % l
        ccd = "cc_dst_c%d" % l
        for n, t in enumerate(self.sseg["tiles"]):
            kv_side(t, n, n)
            self.dma("sp", cc_src[n * 128:(n + 1) * 128, :], kvf, r=["kvf"], w=[cck], dq="ccb_c")
        self.allgather((l, "c"), r=[cck], w=[ccd])
        for sq_ in self.seqs:
            tiles = sq_["tiles"]
            bi = sq_["bidx"]
            for n, t in enumerate(tiles):
                kv_side(t, n, None)
                self.dma("sp", O["o_gk"][bi, i, n * 128:(n + 1) * 128, :], kvf[:, 0:256], r=["kvf"], w=[], dq="kvf_o")
                self.dma("sp", O["o_gv"][bi, i, n * 128:(n + 1) * 128, :], kvf[:, 256:512], r=["kvf"], w=[], dq="kvf_o")
                self.cp("act", knb, kvf[:, 0:256], r=["kvf"], w=["knb"])
                put_keys(knb, n, ["knb"])
                self.cp("pool", VA[:, n, :, 0:64], kvf[:, 256:512].rearrange("p (g d) -> p g d", g=4), r=["kvf"], w=[("VA", n)])
            for n, t in enumerate(tiles):
                q_attn_out(t, n, None, [0, 1])
        self.dma("sp", ck, I["c_gk"][i].rearrange("(t p) n -> p t n", p=128), r=[], w=["ck"], dq="ck")
        self.dma("sp", cv, I["c_gv"][i].rearrange("(t p) n -> p t n", p=128), r=[], w=["cv"], dq="cv")
        self.cp("act", ckb, ck, r=["ck"], w=["ckb"])
        for kt in range(4):
            put_keys(ckb[:, kt, :], kt, ["ckb"])
            self.cp("pool", VA[:, kt, :, 0:64], cv[:, kt, :].rearrange("p (g d) -> p g d", g=4), r=["cv"], w=[("VA", kt)])
        for k in range(8):
            kgk = kg[k % 2]
            kk = ("kg", k % 2)
            self.dma("sp", kgk, cc_dst[k * 128:(k + 1) * 128, :], r=[ccd], w=[kk], dq=kk)
            self.cp("act", knb, kgk[:, 0:256], r=[kk], w=["knb"])
            put_keys(knb, 4 + k, ["knb"])
            self.cp("pool", VA[:, 4 + k, :, 0:64], kgk[:, 256:512].rearrange("p (g d) -> p g d", g=4), r=[kk], w=[("VA", 4 + k)])
        for n, t in enumerate(self.sseg["tiles"]):
            self.normmod(t, 0, hT[:, :, n * 128:(n + 1) * 128], ("hT", n))
            q_attn_out(t, n, n, list(range(12)))

    def mixer_ab(self, l, i):
        A = self.A
        A.reset()
        I, O = self.I, self.O
        w_in = A.alloc([8, 2240], BF16)
        w_out = A.alloc([8, 1024], BF16)
        OG = A.alloc([4, NTOK], BF16)
        hTm = A.alloc([8, 128], BF16)
        hk = "hTm"
        self.alloc_norm_tmp()
        self.at_rden = A.alloc([4], F32)
        Pt = [A.alloc([4, 128], BF16) for _ in range(3)]
        self.pt_i = 0
        SP = dict(qlT=[A.alloc([2, 256], BF16) for _ in range(2)], oT=A.alloc([4, 256], F32), rsT=A.alloc([4, 256], BF16),
                  etot=A.alloc([4, 2], F32), cqb=A.alloc([2, 384], BF16), aseg=A.alloc([4], F32))
        base_off = A.off
        for kh in range(2):
            for ch in range(2):
                self.dma("pool", w_in[:, kh * 4:(kh + 1) * 4, ch * 1120:(ch + 1) * 1120],
                         I["ab_w_in"][i, kh * 512:(kh + 1) * 512, ch * 1120:(ch + 1) * 1120].rearrange("(k p) n -> p k n", p=128),
                         r=[], w=[("w_in", kh, ch)], dq="w_in")
        self.dma("pool", w_out, I["ab_w_out"][i].rearrange("(k p) n -> p k n", p=128), r=[], w=["w_out"], dq="w_out")
        wk = [("w_in", 0, 0), ("w_in", 0, 1), ("w_in", 1, 0), ("w_in", 1, 1)]
        gout = self.cst("gout%d" % i)
        gqn = self.cst("gqn%d" % i)
        gkvn = self.cst("gkvn%d" % i)
        gq96 = self.cst("gq96%d" % i)
        gk96 = self.cst("gk96%d" % i)
        sel = self.cst("sel")
        trim = [self.cst("trim0"), self.cst("trim1")]
        tris = [self.cst("tris0"), self.cst("tris1")]
        mask = [self.cst("mask0"), self.cst("mask1")]
        ccg_src, ccg_dst = self.CC[(l, "g")]
        ccm_src, ccm_dst = self.CC[(l, "m")]

        def g_alloc(NT, persist=None):
            T = NT * 128
            B = {}
            if persist is None:
                B["qlT"] = [A.alloc([2, T], BF16) for _ in range(2)]
                B["oT"] = A.alloc([4, T], F32)
                B["rsT"] = A.alloc([4, T], BF16)
                B["etot"] = A.alloc([4, NT], F32)
            else:
                for k in ("qlT", "oT", "rsT", "etot"):
                    B[k] = persist[k]
            B["klT"] = [A.alloc([2, T], BF16) for _ in range(2)]
            B["kst"] = [A.alloc([NT, 256], BF16) for _ in range(2)]
            B["vtk"] = A.alloc([NT, 512], BF16)
            B["Sst"] = [[A.alloc([128], F32) for _ in range(2)] for _ in range(2)]
            B["Sbf"] = [[A.alloc([128], BF16) for _ in range(2)] for _ in range(2)]
            B["alT"] = A.alloc([128], F32)
            B["lsp"] = A.alloc([512], F32)
            B["Eb"] = A.alloc([4, 128], F32)
            B["Enb"] = A.alloc([4, 128], F32)
            B["Ed2"] = A.alloc([512], F32)
            B["ATm"] = [A.alloc([128], BF16) for _ in range(4)]
            B["aw2"] = A.alloc([512], F32)
            self.memset("dve", B["alT"][32:33, :], 1.0, w=["alT1"])
            aw2 = B["aw2"]
            self.memset("dve", aw2[0:33, :], 0.0, w=["aw2"])
            for z in range(2):
                self.dma("sp", aw2[16 * z:16 * z + 16, z * 256:(z + 1) * 256], I["a_w2"][i, z], r=[], w=["aw2"], dq="aw2")
            self.dma("sp", aw2[32:33, :], I["a_b"][i:i + 1].rearrange("o z n -> o (z n)"), r=[], w=["aw2"], dq="aw2")
            return B

        def g_prep(tiles, B):
            qlT, klT, kst, vtk, rsT, etot = B["qlT"], B["klT"], B["kst"], B["vtk"], B["rsT"], B["etot"]
            alT, lsp, Eb, Enb, Ed2, aw2 = B["alT"], B["lsp"], B["Eb"], B["Enb"], B["Ed2"], B["aw2"]
            for n, t in enumerate(tiles):
                nc_ = slice(n * 128, (n + 1) * 128)
                h = hTm
                self.normmod(t, 0, h, hk)
                bqk = self.bank("mm")
                for ch in range(4):
                    for kc in range(8):
                        self.mm(self.ps[bqk][:, ch * 128:(ch + 1) * 128], w_in[:, kc, ch * 128:(ch + 1) * 128], h[:, kc, :], kc == 0, kc == 7,
                                r=[hk] + wk, w=[("ps", bqk)])
                br = self.bank("mm")
                for ch in range(4):
                    for kc in range(8):
                        self.mm(self.ps[br][:, ch * 128:(ch + 1) * 128], w_in[:, kc, 1024 + ch * 128:1024 + (ch + 1) * 128], h[:, kc, :], kc == 0, kc == 7,
                                r=[hk] + wk, w=[("ps", br)])
                self.act(rsT[:, :, nc_], self.ps[br][:, :].rearrange("p (a b) -> p a b", a=4), AF.Silu, r=[("ps", br)], w=[("rsT", n)])
                ba = self.bank("tr")
                for kc in range(8):
                    self.mm(self.ps[ba][0:32, 0:128], w_in[:, kc, 1536:1568], h[:, kc, :], kc == 0, kc == 7, r=[hk] + wk, w=[("ps", ba)])
                self.cp("act", alT[0:32, :], self.ps[ba][0:32, 0:128], r=[("ps", ba)], w=["alT"])
                bkv = self.bank("mm")
                for kc in range(8):
                    self.mm(self.ps[bkv][:, :], h[:, kc, :], w_in[:, kc, 256:768], kc == 0, kc == 7, r=[hk] + wk, w=[("ps", bkv)])
                bv2 = self.bank("mm")
                for kc in range(8):
                    self.mm(self.ps[bv2][:, 0:256], h[:, kc, :], w_in[:, kc, 768:1024], kc == 0, kc == 7, r=[hk] + wk, w=[("ps", bv2)])
                self.cp("act", vtk[:, n, 0:256], self.ps[bkv][:, 256:512], r=[("ps", bkv)], w=[("vtk", n, 0)])
                self.cp("act", vtk[:, n, 256:512], self.ps[bv2][:, 0:256], r=[("ps", bv2)], w=[("vtk", n, 1)])
                bl = self.bank("tr")
                self.mm(self.ps[bl][:, :], alT[0:33, :], aw2[0:33, :], True, True, r=["alT", "alT1", "aw2"], w=[("ps", bl)])
                self.act(lsp, self.ps[bl][:, :], AF.Exp, r=[("ps", bl)], w=["lsp"], scale=-1.0)
                self.act(lsp, lsp, AF.Ln, r=["lsp"], w=["lsp"], bias=1.0)
                bb = self.bank("tr")
                for z in range(2):
                    for fc in range(2):
                        zf = z * 2 + fc
                        self.mm(self.ps[bb][:, zf * 128:(zf + 1) * 128], lsp[:, zf * 128:(zf + 1) * 128], trim[z], True, True,
                                r=["lsp", "cst"], w=[("ps", bb)])
                bd = self.bank("tr")
                for z in range(2):
                    self.mm(self.ps[bd][:, z * 256:(z + 1) * 256], tris[z], lsp[:, z * 256:(z + 1) * 256], True, True,
                            r=["lsp", "cst"], w=[("ps", bd)])
                pbb = self.ps[bb][:, :].rearrange("p (a b) -> p a b", a=4)
                self.act(Eb, pbb, AF.Exp, r=[("ps", bb)], w=["Eb"])
                self.act(Enb, pbb, AF.Exp, r=[("ps", bb)], w=["Enb"], scale=-1.0)
                self.act(Ed2, self.ps[bd][:, :], AF.Exp, r=[("ps", bd)], w=["Ed2"])
                self.cp("pool", etot[:, 0:2, n], Eb[:, 0:2, 127], r=["Eb"], w=[("etot", n)])
                self.cp("pool", etot[:, 2:4, n], Eb[:, 2:4, 0], r=["Eb"], w=[("etot", n)])
                pqk = self.ps[bqk][:, :].rearrange("p (a b) -> p a b", a=4)
                for z in range(2):
                    self.stt(qlT[z][:, :, nc_], pqk[:, 0:2, :], 0.125, Eb[:, 2 * z:2 * z + 2, :], ALU.mult, ALU.mult,
                             r=[("ps", bqk), "Eb"], w=[("qlT", z, n)])
                    self.tt("dve", klT[z][:, :, nc_], pqk[:, 2:4, :], Enb[:, 2 * z:2 * z + 2, :], ALU.mult,
                            r=[("ps", bqk), "Enb"], w=[("klT", z, n)])
                    self.tt("dve", kst[z][:, n, :], self.ps[bkv][:, 0:256], Ed2[:, z * 256:(z + 1) * 256], ALU.mult,
                            r=[("ps", bkv), "Ed2"], w=[("kst", z, n)])

        def g_scan(NT, B, bidx):
            qlT, klT, kst, vtk, oT, etot, Sst, Sbf, ATm = (B[k] for k in ("qlT", "klT", "kst", "vtk", "oT", "etot", "Sst", "Sbf", "ATm"))
            for z in range(2):
                for fc in range(2):
                    self.memset("pool", Sst[z][fc], 0.0, w=[("S", z, fc)])
                    self.cp("act", Sbf[z][fc], Sst[z][fc], r=[("S", z, fc)], w=[("Sbf", z, fc)])
                order = list(range(NT)) if z == 0 else list(range(NT - 1, -1, -1))
                for n in order:
                    nc_ = slice(n * 128, (n + 1) * 128)
                    bo = self.bank("acc")
                    bats = []
                    for hh in range(4):
                        fc, hp = hh // 2, hh % 2
                        pr = slice(hp * 64, (hp + 1) * 64)
                        bat = self.bank("mm")
                        bats.append(bat)
                        self.mm(self.ps[bat][:, 0:128], klT[z][pr, fc, nc_], qlT[z][pr, fc, nc_], True, True,
                                r=[("klT", z, n), ("qlT", z, n)], w=[("ps", bat)])
                    for hh in range(4):
                        self.tt("dve", ATm[hh], self.ps[bats[hh]][:, 0:128], mask[z], ALU.mult, r=[("ps", bats[hh]), "cst"], w=[("ATm", hh)])
                    for hh in range(4):
                        fc, hp = hh // 2, hh % 2
                        pr = slice(hp * 64, (hp + 1) * 64)
                        self.mm(self.ps[bo][:, hh * 128:(hh + 1) * 128], vtk[:, n, hh * 128:(hh + 1) * 128], ATm[hh], True, False,
                                r=[("vtk", n, 0), ("vtk", n, 1), ("ATm", hh)], w=[("ps", bo)])
                        self.mm(self.ps[bo][:, hh * 128:(hh + 1) * 128], Sbf[z][fc][pr, :], qlT[z][pr, fc, nc_], False, True,
                                r=[("Sbf", z, fc), ("qlT", z, n)], w=[("ps", bo)])
                    pbo = self.ps[bo][:, :].rearrange("p (a b) -> p a b", a=4)
                    if z == 0:
                        self.cp("act", oT[:, :, nc_], pbo, r=[("ps", bo)], w=[("oT", n)])
                    else:
                        self.tt("dve", oT[:, :, nc_], oT[:, :, nc_], pbo, ALU.add, r=[("ps", bo), ("oT", n)], w=[("oT", n)])
                    bus = []
                    for fc in range(2):
                        bu = self.bank("mm")
                        bus.append(bu)
                        for hp in range(2):
                            hh = fc * 2 + hp
                            self.mm(self.ps[bu][hp * 64:(hp + 1) * 64, 0:128], kst[z][:, n, hh * 64:(hh + 1) * 64],
                                    vtk[:, n, hh * 128:(hh + 1) * 128], True, True,
                                    r=[("kst", z, n), ("vtk", n, 0), ("vtk", n, 1)], w=[("ps", bu)])
                    for fc in range(2):
                        self.stt(Sst[z][fc], Sst[z][fc], etot[:, z * 2 + fc, n:n + 1], self.ps[bus[fc]][:, 0:128], ALU.mult, ALU.add,
                                 r=[("S", z, fc), ("etot", n), ("ps", bus[fc])], w=[("S", z, fc)])
                    for fc in range(2):
                        self.cp("act", Sbf[z][fc], Sst[z][fc], r=[("S", z, fc)], w=[("Sbf", z, fc)])
                if bidx is not None:
                    for fc in range(2):
                        self.dma("sp", O["o_gla"][bidx, i, z, fc * 128:(fc + 1) * 128, :], Sst[z][fc],
                                 r=[("S", z, fc)], w=[], dq=("S", z, fc))

        def g_out(tiles, oT, rsT):
            osq = A.alloc([4, 128], BF16)
            orst = A.alloc([4, 128], F32)
            otmp = A.alloc([4, 128], F32)
            for n, t in enumerate(tiles):
                nc_ = slice(n * 128, (n + 1) * 128)
                self.act(osq, oT[:, :, nc_], AF.Square, r=[("oT", n)], w=["osq"])
                b = self.bank("tr")
                self.mm(self.ps[b][:, :], self.ones_b, osq.rearrange("p a b -> p (a b)"), True, True, r=["osq", "ones"], w=[("ps", b)])
                self.rstd(orst, self.ps[b][:, :].rearrange("p (a b) -> p a b", a=4), 128, r=[("ps", b)], w=["orst"])
                self.tt("dve", otmp, oT[:, :, nc_], orst, ALU.mult, r=[("oT", n), "orst"], w=["otmp"])
                self.stt(OG[:, :, t * 128:(t + 1) * 128], otmp, gout, rsT[:, :, nc_], ALU.mult, ALU.mult,
                         r=["otmp", "cst", ("rsT", n)], w=[("OG", t)])

        def m_alloc(NK):
            M = {}
            M["w_qb"] = A.alloc([3, 768], BF16)
            M["w_kvb"] = A.alloc([2, 1024], BF16)
            self.dma("pool", M["w_qb"], I["w_qb"][i].rearrange("(k p) n -> p k n", p=128), r=[], w=["w_qb"], dq="w_qb")
            self.dma("pool", M["w_kvb"], I["w_kvb"][i].rearrange("(k p) n -> p k n", p=128), r=[], w=["w_kvb"], dq="w_kvb")
            M["KTm"] = A.alloc([8, NK * 128], BF16)
            M["VAm"] = A.alloc([NK, 8, 65], BF16)
            M["QTm"] = A.alloc([8, 256], BF16)
            M["omb"] = A.alloc([2, 512], BF16)
            M["OM"] = A.alloc([4, 256], BF16)
            M["cb"] = A.alloc([256], BF16)
            M["cT"] = A.alloc([2, 128], BF16)
            M["kc96"] = A.alloc([8, 96], F32)
            M["knb"] = A.alloc([8, 96], BF16)
            M["st8"] = A.alloc([8], F32)
            M["cqT"] = A.alloc([3, 128], BF16)
            M["rtm"] = [A.alloc([8, 2, 8], F32) for _ in range(4)]
            self.memset("pool", M["VAm"][:, :, :, 64:65], 1.0, w=["VA1"])
            return M

        def own_alloc():
            W = {}
            W["sq96"] = A.alloc([8, 96], F32)
            W["ckvf"] = A.alloc([256], F32)
            W["ckvn"] = A.alloc([256], F32)
            W["kpe"] = A.alloc([32], F32)
            W["st1"] = A.alloc([2], F32)
            return W

        def norm96(M, sq96, gain, rope_cs):
            kc96, knb, st8, rtm = M["kc96"], M["knb"], M["st8"], M["rtm"]
            self.act(sq96, kc96, AF.Square, r=["kc96"], w=["sq96"])
            self.red(st8, sq96, r=["sq96"], w=["st8"])
            self.rstd(st8, st8, 96, r=["st8"], w=["st8"])
            self.tt("dve", kc96, kc96, st8.unsqueeze(2).broadcast_to([128, 8, 96]), ALU.mult, r=["kc96", "st8"], w=["kc96"])
            self.tt("pool", kc96, kc96, gain.unsqueeze(1).broadcast_to([128, 8, 96]), ALU.mult, r=["kc96", "cst"], w=["kc96"])
            if rope_cs is not None:
                cs, csk = rope_cs
                self.rope(kc96[:, :, 64:96].rearrange("p h (a b q) -> p h a b q", a=2, b=2, q=8), 8, 8, cs, rtm,
                          r=["kc96", csk], w=["kc96"], keyp="rtm")
            self.cp("act", knb, kc96, r=["kc96"], w=["knb"])

        def kside(M, sq96, ckvn_ap, ckvn_k, kpe_ap, kpe_k, kt, rope_cs):
            cb, cT, kc96, knb, KTm, VAm, w_kvb = M["cb"], M["cT"], M["kc96"], M["knb"], M["KTm"], M["VAm"], M["w_kvb"]
            self.cp("act", cb, ckvn_ap, r=[ckvn_k], w=["cb"])
            b = self.bank("tr")
            for kc in range(2):
                self.tr(self.psb[b][:, kc * 128:(kc + 1) * 128], cb[:, kc * 128:(kc + 1) * 128], self.ident_b, r=["cb", "identb"], w=[("ps", b)])
            self.cp("dve", cT, self.psb[b][:, 0:256].rearrange("p (a b) -> p a b", a=2), r=[("ps", b)], w=["cT"])
            for bk in range(2):
                b = self.bank("mm")
                for kc in range(2):
                    self.mm(self.ps[b][:, :], cT[:, kc, :], w_kvb[:, kc, bk * 512:(bk + 1) * 512], kc == 0, kc == 1,
                            r=["cT", "w_kvb"], w=[("ps", b)])
                pv = self.ps[b][:, :].rearrange("p (h d) -> p h d", h=4)
                self.cp("act", kc96[:, bk * 4:(bk + 1) * 4, 0:64], pv[:, :, 0:64], r=[("ps", b)], w=["kc96"])
                self.cp("dve", VAm[:, kt, bk * 4:(bk + 1) * 4, 0:64], pv[:, :, 64:128], r=[("ps", b)], w=[("VAm", kt)])
            self.cp("pool", kc96[:, :, 64:96], kpe_ap.unsqueeze(1).broadcast_to([128, 8, 32]), r=[kpe_k], w=["kc96"])
            norm96(M, sq96, gk96, rope_cs)
            b = self.bank("tr")
            for hh in range(8):
                self.tr(self.psb[b][0:96, hh * 128:(hh + 1) * 128], knb[:, hh, :], self.ident_b, r=["knb", "identb"], w=[("ps", b)])
            self.cp("dve", KTm[0:96, :, kt * 128:(kt + 1) * 128], self.psb[b][0:96, :].rearrange("p (a b) -> p a b", a=8),
                    r=[("ps", b)], w=[("KTm", kt)])

        def m_own(t, n, W, cqb):
            sq96, ckvf, ckvn, kpe, st1 = W["sq96"], W["ckvf"], W["ckvn"], W["kpe"], W["st1"]
            h = hTm
            self.normmod(t, 0, h, hk)
            b1 = self.bank("mm")
            for kc in range(8):
                self.mm(self.ps[b1][:, :], h[:, kc, :], w_in[:, kc, 1568:2080], kc == 0, kc == 7, r=[hk] + wk, w=[("ps", b1)])
            b2 = self.bank("mm")
            for kc in range(8):
                self.mm(self.ps[b2][:, 0:160], h[:, kc, :], w_in[:, kc, 2080:2240], kc == 0, kc == 7, r=[hk] + wk, w=[("ps", b2)])
            sqf = sq96.rearrange("p a b -> p (a b)")
            self.act(sqf[:, 0:384], self.ps[b1][:, 0:384], AF.Square, r=[("ps", b1)], w=["sq96", "st1"], accum_out=st1[:, 0:1])
            self.rstd(st1[:, 0:1], st1[:, 0:1], 384, r=["st1"], w=["st1"])
            self.stt(cqb[:, n, :], self.ps[b1][:, 0:384], st1[:, 0:1], gqn, ALU.mult, ALU.mult, r=[("ps", b1), "st1", "cst"], w=[("cqb", n)])
            self.cp("act", ckvf[:, 0:128], self.ps[b1][:, 384:512], r=[("ps", b1)], w=["ckvf"])
            self.cp("act", ckvf[:, 128:256], self.ps[b2][:, 0:128], r=[("ps", b2)], w=["ckvf"])
            self.cp("dve", kpe, self.ps[b2][:, 128:160], r=[("ps", b2)], w=["kpe"])
            self.act(sqf[:, 0:256], ckvf, AF.Square, r=["ckvf"], w=["sq96", "st1b"], accum_out=st1[:, 1:2])
            self.rstd(st1[:, 1:2], st1[:, 1:2], 256, r=["st1b"], w=["st1b"])
            self.stt(ckvn, ckvf, st1[:, 1:2], gkvn, ALU.mult, ALU.mult, r=["ckvf", "st1b", "cst"], w=["ckvn"])

        def m_attn(M, sq96, tiles, cqb, rope_q, NK):
            QTm, omb, OM, KTm, VAm, cqT, kc96, knb, w_qb = (M[k] for k in ("QTm", "omb", "OM", "KTm", "VAm", "cqT", "kc96", "knb", "w_qb"))
            kt_list = list(range(NK))
            kkeys = [("KTm", kt) for kt in kt_list]
            vkeys = [("VAm", kt) for kt in kt_list] + ["VA1"]
            nq = len(tiles)
            for n, t in enumerate(tiles):
                b = self.bank("tr")
                for kc in range(3):
                    self.tr(self.psb[b][:, kc * 128:(kc + 1) * 128], cqb[:, n, kc * 128:(kc + 1) * 128], self.ident_b,
                            r=[("cqb", n), "identb"], w=[("ps", b)])
                self.cp("act", cqT, self.psb[b][:, 0:384].rearrange("p (a b) -> p a b", a=3), r=[("ps", b)], w=["cqT"])
                for bk in range(2):
                    b = self.bank("mm")
                    for kc in range(3):
                        self.mm(self.ps[b][:, 0:384], cqT[:, kc, :], w_qb[:, kc, bk * 384:(bk + 1) * 384], kc == 0, kc == 2,
                                r=["cqT", "w_qb"], w=[("ps", b)])
                    self.cp("act", kc96[:, bk * 4:(bk + 1) * 4, :], self.ps[b][:, 0:384].rearrange("p (h d) -> p h d", h=4),
                            r=[("ps", b)], w=["kc96"])
                norm96(M, sq96, gq96, None if rope_q is None else (rope_q[0][:, n, :, :], rope_q[1]))
                b = self.bank("tr")
                for hh in range(8):
                    self.tr(self.psb[b][0:96, hh * 128:(hh + 1) * 128], knb[:, hh, :], self.ident_b, r=["knb", "identb"], w=[("ps", b)])
                self.cp("dve", QTm[0:96, :, n * 128:(n + 1) * 128], self.psb[b][0:96, :].rearrange("p (a b) -> p a b", a=8),
                        r=[("ps", b)], w=[("QTm", n)])
            for hh in range(8):
                self.attention(
                    KT=lambda st, hh=hh: KTm[0:96, hh, st * 128:(st + 1) * 128], kparts=96, kt_list=kt_list,
                    QTv=QTm[0:96, hh, 0:nq * 128], nq=nq,
                    VA_of=lambda st, hh=hh: VAm[:, st, hh, :], scale=float(96 ** -0.5), Pt=Pt,
                    out_of=lambda j, hh=hh: (omb[:, j, hh * 64:(hh + 1) * 64], ("omb", j)),
                    kr=kkeys, qr=[("QTm", j) for j in range(nq)], vr=vkeys, ow=None)
            for n, t in enumerate(tiles):
                b = self.bank("tr")
                for c in range(4):
                    self.tr(self.psb[b][:, c * 128:(c + 1) * 128], omb[:, n, c * 128:(c + 1) * 128], self.ident_b,
                            r=[("omb", n), "identb"], w=[("ps", b)])
                self.cp("act", OM[:, :, n * 128:(n + 1) * 128], self.psb[b][:, 0:512].rearrange("p (a b) -> p a b", a=4),
                        r=[("ps", b)], w=[("OM", n)])
            t0 = tiles[0]
            self.out_proj_pair(t0, w_out,
                               lambda kc: OG[:, kc, t0 * 128:(t0 + 2) * 128] if kc < 4 else OM[:, kc - 4, 0:256],
                               [("OG", t0), ("OG", t0 + 1), ("OM", 0), ("OM", 1)])

        st_ = self.sseg["tiles"]
        A.reset(base_off)
        B = g_alloc(2, persist=SP)
        g_prep(st_, B)
        g_scan(2, B, None)
        ccgs, ccgd = "cc_src_g%d" % l, "cc_dst_g%d" % l
        gsrc = A.alloc([4, 129], F32)
        self.tt("dve", gsrc[:, :, 128], SP["etot"][:, :, 0], SP["etot"][:, :, 1], ALU.mult, r=[("etot", 0), ("etot", 1)], w=["gsrc_a"])
        for z in range(2):
            for fc in range(2):
                zf = z * 2 + fc
                self.cp("act", gsrc[:, zf, 0:128], B["Sst"][z][fc], r=[("S", z, fc)], w=[("gsrc", zf)])
        self.dma("sp", ccg_src.ap().rearrange("(zf p) c -> p zf c", p=128), gsrc,
                 r=["gsrc_a"] + [("gsrc", zf) for zf in range(4)], w=[ccgs], dq="bnc_g")
        self.allgather((l, "g"), r=[ccgs], w=[ccgd])
        self.P.barrier()
        if "x1" in self.stages:
            return
        A.reset(base_off)
        W = own_alloc()
        ccms, ccmd = "cc_src_m%d" % l, "cc_dst_m%d" % l
        for n, t in enumerate(st_):
            m_own(t, n, W, SP["cqb"])
            self.dma("sp", ccm_src[n * 128:(n + 1) * 128, 0:256], W["ckvn"], r=["ckvn"], w=[ccms], dq="bnc_m")
            self.dma("sp", ccm_src[n * 128:(n + 1) * 128, 256:288], W["kpe"], r=["kpe"], w=[ccms], dq="bnc_m")
        self.allgather((l, "m"), r=[ccms], w=[ccmd])
        self.P.barrier()
        if "x2" in self.stages:
            return

        for sq_ in self.seqs:
            tiles = sq_["tiles"]
            bi = sq_["bidx"]
            A.reset(base_off)
            B = g_alloc(2)
            g_prep(tiles, B)
            g_scan(2, B, bi)
            g_out(tiles, B["oT"], B["rsT"])
            self.P.barrier()
            A.reset(base_off)
            M = m_alloc(2)
            W = own_alloc()
            cqb = A.alloc([2, 384], BF16)
            for n, t in enumerate(tiles):
                m_own(t, n, W, cqb)
                self.dma("sp", O["o_ckv"][bi, i, n * 128:(n + 1) * 128, :], W["ckvn"], r=["ckvn"], w=[], dq="ckvn_o")
                self.dma("sp", O["o_kpe"][bi, i, n * 128:(n + 1) * 128, :], W["kpe"], r=["kpe"], w=[], dq="kpe_o")
                kside(M, W["sq96"], W["ckvn"], "ckvn", W["kpe"], "kpe", n, None)
            m_attn(M, W["sq96"], tiles, cqb, None, 2)
            self.P.barrier()

        if "x3" in self.stages:
            return
        A.reset(base_off)
        gU = A.alloc([16, 129], F32)
        self.dma("sp", gU, ccg_dst.ap().rearrange("(rz p) c -> p rz c", p=128), r=[ccgd], w=["gU"], dq="gU")
        Sin = [[A.alloc([128], F32) for _ in range(2)] for _ in range(2)]
        Sib = [[A.alloc([128], BF16) for _ in range(2)] for _ in range(2)]
        tS = A.alloc([128], F32)
        for z in range(2):
            for fc in range(2):
                zf = z * 2 + fc
                S_ = Sin[z][fc]
                sk = ("Sin", z, fc)
                self.dma("sp", S_, I["c_gla"][i, z, fc * 128:(fc + 1) * 128, :], r=[], w=[sk], dq=sk)
                ranks = [0, 1, 2, 3] if z == 0 else [3, 2, 1, 0]
                for k in ranks:
                    gi = k * 4 + zf
                    self.stt(tS, S_, gU[:, gi, 128:129], gU[:, gi, 0:128], ALU.mult, ALU.add, r=[sk, "gU"], w=["tS"])
                    self.tt("dve", tS, tS, S_, ALU.subtract, r=["tS", sk], w=["tS"])
                    self.stt(S_, tS, sel[:, z * 4 + k:z * 4 + k + 1], S_, ALU.mult, ALU.add, r=["tS", sk, "cst"], w=[sk])
                self.cp("act", Sib[z][fc], S_, r=[sk], w=[("Sib", z, fc)])
        if "x5" in self.stages:
            self.P.barrier()
            return
        for z in range(2):
            order = [0, 1] if z == 0 else [1, 0]
            for oi, n in enumerate(order):
                nc_ = slice(n * 128, (n + 1) * 128)
                bh = [self.bank("acc"), self.bank("acc")]
                for hp in range(2):
                    pr = slice(hp * 64, (hp + 1) * 64)
                    for fc in range(2):
                        self.mm(self.ps[bh[hp]][:, fc * 128:(fc + 1) * 128], Sib[z][fc][pr, :], SP["qlT"][z][pr, fc, nc_], True, True,
                                r=[("Sib", z, fc), ("qlT", z, n)], w=[("ps", bh[hp])])
                oview = SP["oT"][:, :, nc_].rearrange("p (f h) t -> p f h t", f=2, h=2)
                for hp in range(2):
                    pb = self.ps[bh[hp]][:, 0:256].rearrange("p (a b) -> p a b", a=2)
                    self.tt("dve", oview[:, :, hp, :], oview[:, :, hp, :], pb, ALU.add, r=[("ps", bh[hp]), ("oT", n)], w=[("oT", n)])
                if oi == 0:
                    for fc in range(2):
                        zf = z * 2 + fc
                        sk = ("Sin", z, fc)
                        self.P.op("dve", lambda h, S_=Sin[z][fc], e=SP["etot"][:, zf, n:n + 1]: h.tensor_scalar(S_, S_, e, None, ALU.mult),
                                  r=[sk, ("etot", n)], w=[sk])
                        self.cp("act", Sib[z][fc], Sin[z][fc], r=[sk], w=[("Sib", z, fc)])
        if "x6" in self.stages:
            self.P.barrier()
            return
        g_out(st_, SP["oT"], SP["rsT"])
        self.P.barrier()
        if "x4" in self.stages:
            return
        A.reset(base_off)
        M = m_alloc(12)
        sq96 = A.alloc([8, 96], F32)
        ropem_all = A.alloc([8, 2, 16], F32)
        ropem_own = A.alloc([2, 2, 16], F32)
        ckp = A.alloc([4, 32], F32)
        cck = A.alloc([256], F32)
        kgm = [A.alloc([288], F32) for _ in range(2)]
        self.dma("sp", ropem_all, I["ropem_all"].rearrange("(t p) c q -> p t c q", p=128), r=[], w=["ropem_all"], dq="ropem")
        self.dma("sp", ropem_own, I["ropem"].rearrange("(t p) c q -> p t c q", p=128), r=[], w=["ropem_own"], dq="ropem")
        self.dma("sp", ckp, I["c_kpe"][i].rearrange("(t p) n -> p t n", p=128), r=[], w=["ckp"], dq="ckp")
        for kt in range(4):
            self.dma("sp", cck, I["c_ckv"][i, kt * 128:(kt + 1) * 128, :], r=[], w=["cck"], dq="cck")
            kside(M, sq96, cck, "cck", ckp[:, kt, :], "ckp", kt, None)
        for k in range(8):
            kg_ = kgm[k % 2]
            kk = ("kgm", k % 2)
            self.dma("sp", kg_, ccm_dst[k * 128:(k + 1) * 128, :], r=[ccmd], w=[kk], dq=kk)
            kside(M, sq96, kg_[:, 0:256], kk, kg_[:, 256:288], kk, 4 + k, (ropem_all[:, k, :, :], "ropem_all"))
        m_attn(M, sq96, st_, SP["cqb"], (ropem_own, "ropem_own"), 12)


def _rope_tables(n_tok, d_rot):
    t = np.arange(n_tok, dtype=np.int32)
    pos = np.stack([t // 64, t % 64], axis=-1).astype(np.float32)
    quarter = d_rot // 4
    inv = np.power(np.float32(10000.0), -np.arange(quarter, dtype=np.float32) / np.float32(quarter)).astype(np.float32)
    ang = pos[:, :, None] * inv
    cos = np.cos(ang).astype(np.float32).reshape(n_tok, 2 * quarter)
    sin = np.sin(ang).astype(np.float32).reshape(n_tok, 2 * quarter)
    return np.ascontiguousarray(np.stack([cos, sin], axis=1))


def _build_cst(inp, b, j):
    c = np.zeros((128, NCST), np.float32)

    def put(name, arr, parts=128):
        o, n = CST_OFF[name]
        c[0:parts, o:o + n] = np.asarray(arr, np.float32).reshape(parts, n)

    s = np.arange(128)[:, None]
    t = np.arange(128)[None, :]
    v = np.float32(-1.0 / 16.0)
    put("ident", np.eye(128))
    put("trim0", (s <= t) * v)
    put("trim1", (s >= t) * v)
    put("tris0", (s > t) * v)
    put("tris1", (s < t) * v)
    put("mask0", (s <= t) * 1.0)
    put("mask1", (s >= t) * 1.0)
    put("rm", np.array([[1, 0], [0, 1], [1, 1]], np.float32), parts=3)
    cond = np.stack([inp["c_ctx"].reshape(8, 128).T, inp["c"][b].reshape(8, 128).T], axis=-1)
    put("cond", cond)
    selv = np.array([1.0 if k < j else 0.0 for k in range(4)] + [1.0 if k > j else 0.0 for k in range(4)], np.float32)
    put("sel", np.broadcast_to(selv[None, :], (128, 8)))
    put("gmix", inp["norm_mix_g"].reshape(4, 8, 128).transpose(2, 0, 1))
    put("gffn", inp["norm_ffn_g"].reshape(4, 8, 128).transpose(2, 0, 1))
    for i in range(2):
        put("gq%d" % i, np.broadcast_to(inp["gqa_qn_g"][i][None, :], (128, 64)))
        put("gk%d" % i, np.broadcast_to(inp["gqa_kn_g"][i][None, :], (128, 64)))
        put("gout%d" % i, inp["gla_out_g"][i].reshape(128, 1))
        put("gqn%d" % i, np.broadcast_to(inp["mla_q_norm_g"][i][None, :], (128, 384)))
        put("gkvn%d" % i, np.broadcast_to(inp["mla_kv_norm_g"][i][None, :], (128, 256)))
        put("gq96%d" % i, np.broadcast_to(inp["mla_qn_g"][i][None, :], (128, 96)))
        put("gk96%d" % i, np.broadcast_to(inp["mla_kn_g"][i][None, :], (128, 96)))
    return c


_NC_CACHE = {}


def _get_nc(debug=(), stages=None):
    key = (tuple(sorted(debug)), None if stages is None else tuple(sorted(stages)))
    if key not in _NC_CACHE:
        kb = KB(debug, stages)
        nc = kb.build()
        _NC_CACHE[key] = (nc, kb)
    return _NC_CACHE[key]


def make_in_maps(inp):
    inp = {k: np.ascontiguousarray(np.asarray(v)) for k, v in inp.items()}
    ropeg = _rope_tables(1024, 64)
    ropem = _rope_tables(1024, 32)
    shared = dict(
        ffn_w_in=inp["ffn_w_in"], ffn_w_out=inp["ffn_w_out"],
        ab_w_in=inp["ab_w_in"], ab_w_out=inp["ab_w_out"], a_w2=inp["gla_a_w2"], a_b=inp["gla_a_b"],
        w_qb=inp["mla_w_qb"], w_kvb=inp["mla_w_kvb"], gqa_w_in=inp["gqa_w_in"], gqa_w_out=inp["gqa_w_out"])
    in_maps = []
    for core in range(8):
        b, j = core // 4, core % 4
        m = dict(shared)
        m["ropem_all"] = ropem
        m["ada_w"] = np.ascontiguousarray(inp["ada_w"][:, :, j * 1536:(j + 1) * 1536])
        m["ada_b"] = np.ascontiguousarray(inp["ada_b"][:, j * 1536:(j + 1) * 1536])
        m["ropeg"] = np.ascontiguousarray(ropeg[j * 256:(j + 1) * 256])
        m["ropem"] = np.ascontiguousarray(ropem[j * 256:(j + 1) * 256])
        m["cst"] = _build_cst(inp, b, j)
        m["xp"] = np.ascontiguousarray(inp["x_prompt"][core * 4:(core + 1) * 4].reshape(1024, D))
        m["xs"] = np.ascontiguousarray(inp["x_sample"][b, j * 256:(j + 1) * 256])
        m["c_ckv"] = np.ascontiguousarray(inp["cache_mla_ckv"][b])
        m["c_kpe"] = np.ascontiguousarray(inp["cache_mla_kpe"][b])
        m["c_gla"] = np.ascontiguousarray(inp["state_gla"][b].reshape(2, 2, 256, 128))
        m["c_gk"] = np.ascontiguousarray(inp["cache_gqa_k"][b].reshape(2, 512, 256))
        m["c_gv"] = np.ascontiguousarray(inp["cache_gqa_v"][b].reshape(2, 512, 256))
        in_maps.append(m)
    return in_maps


def kernel(**inputs):
    nc, kb = _get_nc()
    in_maps = make_in_maps(inputs)
    res = run_bass_kernel_spmd(nc, in_maps, core_ids=list(range(8)))
    R = res.results
    y_prompt = np.concatenate([R[c]["yp"].reshape(4, 256, D) for c in range(8)], axis=0)
    y_sample = np.stack([np.concatenate([R[4 * b + j]["ys"] for j in range(4)], axis=0) for b in range(2)], axis=0)
    new_ckv = np.concatenate([R[c]["o_ckv"] for c in range(8)], axis=0)
    new_kpe = np.concatenate([R[c]["o_kpe"] for c in range(8)], axis=0)
    new_gla = np.concatenate([R[c]["o_gla"].reshape(4, 2, 2, 4, 64, 128) for c in range(8)], axis=0)
    new_k = np.concatenate([R[c]["o_gk"].reshape(4, 2, 256, 4, 64) for c in range(8)], axis=0)
    new_v = np.concatenate([R[c]["o_gv"].reshape(4, 2, 256, 4, 64) for c in range(8)], axis=0)
    outs = (y_prompt, y_sample, new_ckv, new_kpe, new_gla, new_k, new_v)
    return tuple(np.ascontiguousarray(o, dtype=np.float32) for o in outs)
```

```python
import bisect
import contextlib
import numpy as np
import concourse.bass as bass
import concourse.mybir as mybir
from concourse.bass_utils import run_bass_kernel_spmd

F32 = mybir.dt.float32
BF16 = mybir.dt.bfloat16
AF = mybir.ActivationFunctionType
ALU = mybir.AluOpType
AX = mybir.AxisListType

ENGS = ("pe", "act", "dve", "pool", "sp")
RAW, WAR, WAW = 1, 2, 4
EPS = 1e-6
D = 1024
FH = 2816
ARENA_BYTES = 152 * 1024
NTOK = 1280
NTILE = 10


class Prog:
    def __init__(self):
        self.ops = []
        self.last_w = {}
        self.readers = {}

    def op(self, eng, fn, r=(), w=(), dq=None, inc=16):
        i = len(self.ops)
        deps = {}
        psr = [k for k in r if isinstance(k, tuple) and k and k[0] == "ps"]
        if psr:
            r = [k for k in r if not (isinstance(k, tuple) and k and k[0] == "ps")]
            w = list(w) + [k for k in psr if k not in w]
            for k in psr:
                lw = self.last_w.get(k)
                if lw is not None:
                    deps[lw] = deps.get(lw, 0) | RAW
        for k in r:
            lw = self.last_w.get(k)
            if lw is not None:
                deps[lw] = deps.get(lw, 0) | RAW
        for k in w:
            lw = self.last_w.get(k)
            if lw is not None:
                deps[lw] = deps.get(lw, 0) | WAW
            for rd in self.readers.get(k, ()):
                if rd != i:
                    deps[rd] = deps.get(rd, 0) | WAR
        for k in r:
            self.readers.setdefault(k, []).append(i)
        for k in w:
            self.last_w[k] = i
            self.readers[k] = []
        deps.pop(i, None)
        self.ops.append(dict(eng=eng, fn=fn, deps=deps, dq=dq, bar=None, inc=inc))
        return i

    def barrier(self):
        first = len(self.ops)
        for e in ENGS:
            self.ops.append(dict(eng=e, fn="drain", deps={}, dq=None, bar=("sig", first)))
        sig_ids = list(range(first, first + len(ENGS)))
        for e in ENGS:
            self.ops.append(dict(eng=e, fn="nop", deps={s: RAW for s in sig_ids}, dq=None, bar=("wait", first)))
        iscc = lambda k: isinstance(k, str) and k.startswith("cc")
        self.last_w = {k: v for k, v in self.last_w.items() if iscc(k)}
        self.readers = {k: v for k, v in self.readers.items() if iscc(k)}

    def emit(self, nc, es):
        ops = self.ops
        n = len(ops)
        needed = [False] * n
        for i, o in enumerate(ops):
            kept = []
            for d, kind in o["deps"].items():
                od = ops[d]
                if od["dq"] is None and o["dq"] is None and od["eng"] == o["eng"] and o["bar"] is None:
                    if o["eng"] == "pe":
                        continue
                kept.append(d)
                needed[d] = True
            o["kdeps"] = kept
        eng_sem = {e: es.enter_context(nc.semaphore("sem_" + e)) for e in ENGS}
        dq_keys = []
        seen = set()
        for o in ops:
            if o["dq"] is not None and o["dq"] not in seen:
                seen.add(o["dq"])
                dq_keys.append(o["dq"])
        dq_sem = {k: es.enter_context(nc.semaphore("dq_%d" % j)) for j, k in enumerate(dq_keys)}
        dq_idx = {k: [] for k in dq_keys}
        dq_cum = {k: [0] for k in dq_keys}
        eng_cnt = {e: 0 for e in ENGS}

        def dq_before(k, i):
            return dq_cum[k][bisect.bisect_left(dq_idx[k], i)]

        for i, o in enumerate(ops):
            if o["dq"] is not None:
                dq_idx[o["dq"]].append(i)
                dq_cum[o["dq"]].append(dq_cum[o["dq"]][-1] + o["inc"])
                o["sig"] = ("dq", o["dq"])
            elif needed[i] or (o["bar"] is not None and o["bar"][0] == "sig"):
                eng_cnt[o["eng"]] += 1
                o["sig"] = ("eng", o["eng"], eng_cnt[o["eng"]])
            else:
                o["sig"] = None
        per_eng = {e: [] for e in ENGS}
        for i, o in enumerate(ops):
            per_eng[o["eng"]].append(i)
        self.n_sems = len(ENGS) + len(dq_keys)
        self.counts = {e: len(per_eng[e]) for e in ENGS}

        def run(e, h):
            waited = {}
            for i in per_eng[e]:
                o = ops[i]
                waits = {}
                for d in o["kdeps"]:
                    od = ops[d]
                    if od["dq"] is not None:
                        k = od["dq"]
                        cnt = dq_before(k, i)
                        key = ("dq", k)
                        waits[key] = max(waits.get(key, 0), cnt)
                    else:
                        key = ("eng", od["eng"])
                        waits[key] = max(waits.get(key, 0), od["sig"][2])
                if o["bar"] is not None and o["bar"][0] == "sig" and e == "sp":
                    for k in dq_keys:
                        if isinstance(k, str) and k.startswith("cc"):
                            continue
                        cnt = dq_before(k, i)
                        if cnt:
                            waits[("dq", k)] = cnt
                for key, v in waits.items():
                    if waited.get(key, 0) >= v:
                        continue
                    waited[key] = v
                    sem = dq_sem[key[1]] if key[0] == "dq" else eng_sem[key[1]]
                    h.wait_ge(sem, v)
                if o["fn"] == "drain":
                    inst = h.nop() if e == "sp" else h.drain()
                elif o["fn"] == "nop":
                    inst = None
                else:
                    inst = o["fn"](h)
                s = o["sig"]
                if s is not None:
                    if s[0] == "dq":
                        inst.then_inc(dq_sem[s[1]], o["inc"])
                    else:
                        inst.then_inc(eng_sem[s[1]], 1)
            if e == "sp":
                for k in dq_keys:
                    h.wait_ge(dq_sem[k], dq_cum[k][-1])

        with nc.Block() as block:
            @block.tensor
            def _(h):
                run("pe", h)

            @block.scalar
            def _(h):
                run("act", h)

            @block.vector
            def _(h):
                run("dve", h)

            @block.gpsimd
            def _(h):
                run("pool", h)

            @block.sync
            def _(h):
                run("sp", h)


class Arena:
    def __init__(self, t, nbytes):
        self.t = t
        self.nbytes = nbytes
        self.off = 0
        self.peak = 0

    def reset(self, off=0):
        self.off = off

    def alloc(self, free_shape, dtype, parts=128):
        n = int(np.prod(free_shape))
        esz = 4 if dtype == F32 else 2
        sz = (n * esz + 31) // 32 * 32
        o = self.off
        assert o + sz <= self.nbytes, ("arena overflow", o, sz, self.nbytes)
        self.off = o + sz
        self.peak = max(self.peak, self.off)
        ap = self.t[0:parts, o // 2:(o + n * esz) // 2]
        if dtype == F32:
            ap = ap.bitcast(F32)
        fs = list(free_shape)
        if len(fs) == 2:
            ap = ap.rearrange("p (a b) -> p a b", a=fs[0], b=fs[1])
        elif len(fs) == 3:
            ap = ap.rearrange("p (a b c) -> p a b c", a=fs[0], b=fs[1], c=fs[2])
        elif len(fs) == 4:
            ap = ap.rearrange("p (a b c d) -> p a b c d", a=fs[0], b=fs[1], c=fs[2], d=fs[3])
        return ap


def _cst_layout():
    off = {}
    o = 0

    def add(name, n):
        nonlocal o
        off[name] = (o, n)
        o += n

    add("ident", 128)
    add("trim0", 128)
    add("trim1", 128)
    add("tris0", 128)
    add("tris1", 128)
    add("mask0", 128)
    add("mask1", 128)
    add("rm", 2)
    add("cond", 16)
    add("sel", 8)
    add("gmix", 32)
    add("gffn", 32)
    for i in range(2):
        add("gq%d" % i, 64)
        add("gk%d" % i, 64)
        add("gout%d" % i, 1)
        add("gqn%d" % i, 384)
        add("gkvn%d" % i, 256)
        add("gq96%d" % i, 96)
        add("gk96%d" % i, 96)
    return off, o


CST_OFF, NCST = _cst_layout()


class KB:
    def __init__(self, debug=(), stages=None):
        self.stages = set(stages) if stages is not None else {"adaln", "ffn", "mixc", "mixab", "P", "S"}
        self.debug = set(debug)
        self.dbg_outs = {}

    def mm(self, out, lhsT, rhs, start, stop, r, w, **kw):
        self.P.op("pe", lambda h: h.matmul(out, lhsT, rhs, start=start, stop=stop, **kw), r=r, w=w)

    def tr(self, out, in_, ident, r, w):
        self.P.op("pe", lambda h: h.transpose(out, in_, ident), r=r, w=w)

    def act(self, out, in_, func, r, w, **kw):
        self.P.op("act", lambda h: h.activation(out, in_, func, **kw), r=r, w=w)

    def tt(self, eng, out, a, b, op, r, w):
        self.P.op(eng, lambda h: h.tensor_tensor(out, a, b, op), r=r, w=w)

    def stt(self, out, in0, scalar, in1, op0, op1, r, w):
        self.P.op("dve", lambda h: h.scalar_tensor_tensor(out, in0, scalar, in1, op0, op1), r=r, w=w)

    def cp(self, eng, out, in_, r, w):
        if eng == "act":
            self.P.op("act", lambda h: h.copy(out, in_), r=r, w=w)
        else:
            self.P.op(eng, lambda h: h.tensor_copy(out, in_), r=r, w=w)

    def recip(self, out, in_, r, w):
        self.P.op("dve", lambda h: h.reciprocal(out, in_), r=r, w=w)

    def red(self, out, in_, r, w):
        self.P.op("dve", lambda h: h.tensor_reduce(out, in_, AX.X, ALU.add), r=r, w=w)

    def memset(self, eng, ap, val, w):
        self.P.op(eng, lambda h: h.memset(ap, val), w=w)

    def dma(self, q, out, in_, r, w, dq, **kw):
        self.P.op(q, lambda h: h.dma_start(out=out, in_=in_, **kw), r=r, w=w, dq=dq)

    def allgather(self, key, r, w):
        src, dst = self.CC[key]
        name = "cc_%s_%s" % key
        self.P.op("pool", lambda h: h.collective_compute("AllGather", ALU.bypass, replica_groups=[[0, 1, 2, 3], [4, 5, 6, 7]],
                                                         ins=[src.ap().opt()], outs=[dst.ap().opt()]),
                  r=r, w=w, dq=name, inc=1)

    def bank(self, pool):
        lst, idx = self.pools[pool]
        b = lst[idx % len(lst)]
        self.pools[pool][1] = idx + 1
        return b

    def rstd(self, out, ss, n, r, w):
        self.act(out, ss, AF.Ln, r=r, w=w, bias=EPS, scale=1.0 / n)
        self.act(out, out, AF.Exp, r=w, w=w, scale=-0.5)

    def cst(self, name, parts=128):
        o, n = CST_OFF[name]
        return self.cst_t[0:parts, o:o + n]

    def dbg(self, name, ap, r, shape):
        if name not in self.debug:
            return
        t = self.nc.dram_tensor("dbg_" + name, list(shape), ap.dtype if hasattr(ap, "dtype") else F32, kind="ExternalOutput").ap()
        self.dbg_outs[name] = shape
        self.dma("sp", t, ap, r=r, w=[], dq=("dbg", name))

    def build(self):
        nc = bass.Bass("TRN2", target_bir_lowering=False)
        self.nc = nc
        self.P = Prog()

        def din(name, shape):
            return nc.dram_tensor(name, list(shape), F32, kind="ExternalInput").ap()

        def dout(name, shape):
            return nc.dram_tensor(name, list(shape), F32, kind="ExternalOutput").ap()

        I = {}
        I["cst"] = din("cst", [128, NCST])
        I["xp"] = din("xp", [1024, D])
        I["xs"] = din("xs", [256, D])
        I["ada_w"] = din("ada_w", [4, D, 1536])
        I["ada_b"] = din("ada_b", [4, 1536])
        I["ffn_w_in"] = din("ffn_w_in", [4, D, 2 * FH])
        I["ffn_w_out"] = din("ffn_w_out", [4, FH, D])
        I["ab_w_in"] = din("ab_w_in", [2, D, 2240])
        I["ab_w_out"] = din("ab_w_out", [2, D, D])
        I["a_w2"] = din("a_w2", [2, 2, 16, 256])
        I["a_b"] = din("a_b", [2, 2, 256])
        I["w_qb"] = din("w_qb", [2, 384, 768])
        I["w_kvb"] = din("w_kvb", [2, 256, 1024])
        I["gqa_w_in"] = din("gqa_w_in", [2, D, 1536])
        I["gqa_w_out"] = din("gqa_w_out", [2, D, D])
        I["c_ckv"] = din("c_ckv", [2, 512, 256])
        I["c_kpe"] = din("c_kpe", [2, 512, 32])
        I["c_gla"] = din("c_gla", [2, 2, 256, 128])
        I["c_gk"] = din("c_gk", [2, 512, 256])
        I["c_gv"] = din("c_gv", [2, 512, 256])
        I["ropeg"] = din("ropeg", [256, 2, 32])
        I["ropem"] = din("ropem", [256, 2, 16])
        I["ropem_all"] = din("ropem_all", [1024, 2, 16])
        O = {}
        O["yp"] = dout("yp", [1024, D])
        O["ys"] = dout("ys", [256, D])
        O["o_ckv"] = dout("o_ckv", [4, 2, 256, 256])
        O["o_kpe"] = dout("o_kpe", [4, 2, 256, 32])
        O["o_gla"] = dout("o_gla", [4, 2, 2, 256, 128])
        O["o_gk"] = dout("o_gk", [4, 2, 256, 256])
        O["o_gv"] = dout("o_gv", [4, 2, 256, 256])
        self.I, self.O = I, O
        self.CC = {}
        self.CC[("a", "a")] = (nc.dram_tensor("ccs_a", [3, 6144], F32), nc.dram_tensor("ccd_a", [12, 6144], F32))
        for l in range(4):
            if l % 2 == 0:
                self.CC[(l, "g")] = (nc.dram_tensor("ccs_g%d" % l, [512, 129], F32), nc.dram_tensor("ccd_g%d" % l, [2048, 129], F32))
                self.CC[(l, "m")] = (nc.dram_tensor("ccs_m%d" % l, [256, 288], F32), nc.dram_tensor("ccd_m%d" % l, [1024, 288], F32))
            else:
                self.CC[(l, "c")] = (nc.dram_tensor("ccs_c%d" % l, [256, 512], F32), nc.dram_tensor("ccd_c%d" % l, [1024, 512], F32))

        with contextlib.ExitStack() as es:
            self.xT = es.enter_context(nc.sbuf_tensor("xT", [128, 8, NTOK], F32))
            self.cst_t = es.enter_context(nc.sbuf_tensor("cst_sb", [128, NCST], F32))
            small = es.enter_context(nc.sbuf_tensor("small", [128, 4 * 48 * 2 + 96 + 8], F32))
            cbf = es.enter_context(nc.sbuf_tensor("cbf", [128, 256 + 16], BF16))
            arena_t = es.enter_context(nc.sbuf_tensor("arena", [128, ARENA_BYTES // 2], BF16))
            self.A = Arena(arena_t, ARENA_BYTES)
            self.ps = [es.enter_context(nc.psum_tensor("ps%d" % i, [128, 512], F32)) for i in range(8)]
            self.psb = [p.bitcast(BF16) for p in self.ps]
            self.pools = {"mm": [[0, 1, 2, 3], 0], "acc": [[4, 5], 0], "tr": [[6, 7], 0]}
            self.modt = small[:, 0:384].rearrange("p (l c k) -> p l c k", l=4, c=48, k=2)
            self.lsc = small[:, 384:480].rearrange("p (g a b) -> p g a b", g=2, a=6, b=8)
            self.eps_c = small[:, 480:481]
            self.one_c = small[:, 481:482]
            self.ident_b = cbf[:, 0:128]
            self.ones_b = cbf[:, 128:256]
            self.sc_b = cbf[:, 256:272].rearrange("p (k c) -> p k c", k=8, c=2)
            self.ident_f = self.cst("ident")

            self.prologue()
            if "adaln" in self.stages:
                self.adaln_all()
            self.run_all()
            self.P.emit(nc, es)
        return nc

    def prologue(self):
        self.dma("sp", self.cst_t[:, :], self.I["cst"], r=[], w=["cst"], dq="cst")
        self.memset("dve", self.eps_c, EPS, w=["small_c"])
        self.memset("dve", self.one_c, 1.0, w=["small_c"])
        self.memset("dve", self.ones_b, 1.0, w=["ones"])
        self.cp("dve", self.ident_b, self.ident_f, r=["cst"], w=["identb"])
        cond = self.cst("cond").rearrange("p (k c) -> p k c", k=8, c=2)
        self.act(self.sc_b, cond, AF.Silu, r=["cst"], w=["scb"])

    def adaln_all(self):
        A = self.A
        A.reset()
        slots = [A.alloc([8, 512], BF16) for _ in range(3)]
        mq = A.alloc([6144], F32)
        mtok = A.alloc([6144], F32)
        rm = self.cst("rm", parts=3)
        cc_src, cc_dst = self.CC[("a", "a")]
        k = 0
        for l in range(4):
            self.dma("sp", mq[2:3, l * 1536:(l + 1) * 1536], self.I["ada_b"][l:l + 1, :], r=[], w=[("mq", "b")], dq="mtokb")
            for j in range(3):
                s = k % 3
                k += 1
                src = self.I["ada_w"][l, :, j * 512:(j + 1) * 512].rearrange("(k p) n -> p k n", p=128)
                self.dma("pool", slots[s], src, r=[], w=[("adw", s)], dq=("adw", s))
                b = self.bank("mm")
                for kc in range(8):
                    self.mm(self.ps[b][0:2, :], self.sc_b[:, kc, :], slots[s][:, kc, :], kc == 0, kc == 7,
                            r=[("adw", s), "scb"], w=[("ps", b)])
                c0 = l * 1536 + j * 512
                self.cp("act", mq[0:2, c0:c0 + 512], self.ps[b][0:2, :], r=[("ps", b)], w=[("mq", l, j)])
        self.dma("sp", cc_src.ap(), mq[0:3, :], r=[("mq", "b")] + [("mq", l, j) for l in range(4) for j in range(3)],
                 w=["cc_src_a"], dq="bnc_a")
        self.allgather(("a", "a"), r=["cc_src_a"], w=["cc_dst_a"])
        dview = cc_dst.ap().rearrange("(j r) (l c) -> r l j c", r=3, l=4)
        for l in range(4):
            self.dma("sp", mtok[0:3, :].rearrange("r (j c) -> r j c", j=4), dview[:, l, :, :], r=["cc_dst_a"], w=["mtok"], dq="mtok")
            b = self.bank("mm")
            for c in range(48):
                self.mm(self.ps[b][:, 2 * c:2 * c + 2], mtok[0:3, c * 128:(c + 1) * 128], rm, True, True,
                        r=["mtok", "cst"], w=[("ps", b)])
            self.cp("dve", self.modt[:, l, :, :], self.ps[b][:, 0:96].rearrange("p (c k) -> p c k", c=48, k=2),
                    r=[("ps", b)], w=["modt"])
        self.P.barrier()

    def layer_scalars(self, l):
        gmix = self.cst("gmix").rearrange("p (l c) -> p l c", l=4, c=8)[:, l, :]
        gffn = self.cst("gffn").rearrange("p (l c) -> p l c", l=4, c=8)[:, l, :]
        for col in range(2):
            mv = self.modt[:, l, :, col]
            L = self.lsc[:, col, :, :]
            self.stt(L[:, 0, :], mv[:, 8:16], 1.0, gmix, ALU.add, ALU.mult, r=["modt", "cst"], w=["lsc"])
            self.cp("dve", L[:, 1, :], mv[:, 0:8], r=["modt"], w=["lsc"])
            self.cp("dve", L[:, 2, :], mv[:, 16:24], r=["modt"], w=["lsc"])
            self.stt(L[:, 3, :], mv[:, 32:40], 1.0, gffn, ALU.add, ALU.mult, r=["modt", "cst"], w=["lsc"])
            self.cp("dve", L[:, 4, :], mv[:, 24:32], r=["modt"], w=["lsc"])
            self.cp("dve", L[:, 5, :], mv[:, 40:48], r=["modt"], w=["lsc"])

    def alloc_norm_tmp(self):
        A = self.A
        self.nm_sq = A.alloc([8, 128], BF16)
        self.nm_rstd = A.alloc([128], F32)
        self.nm_tmp = A.alloc([8, 128], F32)

    def normmod(self, t, which, dst, dstkey):
        xv = self.xT[:, :, t * 128:(t + 1) * 128]
        xk = [("xT", t, c) for c in range(8)]
        grp = 0 if t < 8 else 1
        G = self.lsc[:, grp, 3 * which, :]
        SH = self.lsc[:, grp, 3 * which + 1, :]
        self.tt("pool", self.nm_tmp, xv, G.unsqueeze(2).broadcast_to([128, 8, 128]), ALU.mult, r=xk + ["lsc"], w=["nm_tmp"])
        self.act(self.nm_sq, xv, AF.Square, r=xk, w=["nm_sq"])
        b = self.bank("tr")
        for c in range(8):
            self.mm(self.ps[b][:, 0:128], self.ones_b, self.nm_sq[:, c, :], c == 0, c == 7, r=["nm_sq", "ones"], w=[("ps", b)])
        self.rstd(self.nm_rstd, self.ps[b][:, 0:128], D, r=[("ps", b)], w=["nm_rstd"])
        self.tt("dve", self.nm_tmp, self.nm_tmp, self.nm_rstd.unsqueeze(1).broadcast_to([128, 8, 128]), ALU.mult,
                r=["nm_tmp", "nm_rstd"], w=["nm_tmp"])
        self.tt("dve", dst, self.nm_tmp, SH.unsqueeze(2).broadcast_to([128, 8, 128]), ALU.add,
                r=["nm_tmp", "lsc"], w=[dstkey])

    def run_all(self):
        self.seqs = [dict(tiles=[2 * s, 2 * s + 1], ctx=False, rope=False, bidx=s) for s in range(4)]
        self.sseg = dict(tiles=[8, 9], ctx=True, rope=True, bidx=None)
        A = self.A
        A.reset()
        xin = [A.alloc([1024], F32) for _ in range(2)]
        for t in range(NTILE):
            s = t % 2
            src = self.I["xp"][t * 128:(t + 1) * 128, :] if t < 8 else self.I["xs"][(t - 8) * 128:(t - 7) * 128, :]
            self.dma("sp", xin[s], src, r=[], w=[("xin", s)], dq=("xin", s))
            for hb in range(2):
                b = self.bank("tr")
                for cc in range(4):
                    c = hb * 4 + cc
                    self.tr(self.ps[b][:, cc * 128:(cc + 1) * 128], xin[s][:, c * 128:(c + 1) * 128], self.ident_f,
                            r=[("xin", s), "cst"], w=[("ps", b)])
                self.cp("dve" if hb else "act", self.xT[:, hb * 4:hb * 4 + 4, t * 128:(t + 1) * 128],
                        self.ps[b][:, :].rearrange("p (a b) -> p a b", a=4),
                        r=[("ps", b)], w=[("xT", t, hb * 4 + cc) for cc in range(4)])
        self.P.barrier()
        for l in range(4):
            self.layer_scalars(l)
            if l % 2 == 0:
                if "mixab" in self.stages:
                    self.mixer_ab(l, l // 2)
            else:
                if "mixc" in self.stages:
                    self.mixer_c(l, l // 2)
            self.P.barrier()
            if "ffn" in self.stages:
                self.ffn(l)
            self.P.barrier()
        A.reset()
        yo = [A.alloc([1024], F32) for _ in range(2)]
        for t in range(NTILE):
            s = t % 2
            for hb in range(2):
                b = self.bank("tr")
                for cc in range(4):
                    c = hb * 4 + cc
                    self.tr(self.ps[b][:, cc * 128:(cc + 1) * 128], self.xT[:, c, t * 128:(t + 1) * 128], self.ident_f,
                            r=[("xT", t, c), "cst"], w=[("ps", b)])
                self.cp("dve" if hb else "act", yo[s][:, hb * 512:(hb + 1) * 512], self.ps[b][:, :],
                        r=[("ps", b)], w=[("yo", s, hb)])
            dst = self.O["yp"][t * 128:(t + 1) * 128, :] if t < 8 else self.O["ys"][(t - 8) * 128:(t - 7) * 128, :]
            self.dma("sp", dst, yo[s], r=[("yo", s, 0), ("yo", s, 1)], w=[], dq=("yo", s))

    def ffn(self, l):
        A = self.A
        A.reset()
        hT = A.alloc([8, NTOK], BF16)
        actT = A.alloc([22, NTOK], BF16)
        wi = [A.alloc([8, 2, 256], BF16) for _ in range(2)]
        wo = [A.alloc([11, 1024], BF16) for _ in range(2)]
        sg = [A.alloc([512], F32) for _ in range(2)]
        self.alloc_norm_tmp()
        W1 = self.I["ffn_w_in"]
        W2 = self.I["ffn_w_out"]
        TB = [(0, 512, 0, [0, 1, 2, 3]), (512, 512, 0, [4, 5, 6, 7]), (1024, 256, 1, [8, 9])]
        NS = len(wi)

        def load_wi(j2):
            s = j2 % NS
            for gu in range(2):
                c0 = gu * FH + j2 * 256
                src = W1[l, :, c0:c0 + 256].rearrange("(k p) n -> p k n", p=128)
                self.dma("pool", wi[s][:, :, gu, :], src, r=[], w=[("wi", s, gu)], dq=("wi", s))

        def load_wo(hf):
            src = W2[l, hf * 1408:(hf + 1) * 1408, :].rearrange("(j p) n -> p j n", p=128)
            self.dma("pool", wo[hf], src, r=[], w=[("wo", hf)], dq=("wo", hf))

        for j2 in range(NS):
            load_wi(j2)
        for t in range(NTILE):
            self.normmod(t, 1, hT[:, :, t * 128:(t + 1) * 128], ("hT", t))
        load_wo(0)
        load_wo(1)
        k = 0
        for j2 in range(11):
            s = j2 % NS
            for (t0, tn, grp, tl) in TB:
                hk = [("hT", q) for q in tl]
                for hf in range(2):
                    j = j2 * 2 + hf
                    bg = self.bank("mm")
                    for kc in range(8):
                        self.mm(self.ps[bg][:, 0:tn], wi[s][:, kc, 0, hf * 128:(hf + 1) * 128], hT[:, kc, t0:t0 + tn],
                                kc == 0, kc == 7, r=[("wi", s, 0)] + hk, w=[("ps", bg)])
                    bu = self.bank("mm")
                    for kc in range(8):
                        self.mm(self.ps[bu][:, 0:tn], wi[s][:, kc, 1, hf * 128:(hf + 1) * 128], hT[:, kc, t0:t0 + tn],
                                kc == 0, kc == 7, r=[("wi", s, 1)] + hk, w=[("ps", bu)])
                    sgi = k % 2
                    k += 1
                    self.act(sg[sgi][:, 0:tn], self.ps[bg][:, 0:tn], AF.Silu, r=[("ps", bg)], w=[("sg", sgi)])
                    self.tt("dve", actT[:, j, t0:t0 + tn], sg[sgi][:, 0:tn], self.ps[bu][:, 0:tn], ALU.mult,
                            r=[("sg", sgi), ("ps", bu)], w=[("act", j, t0)])
            if j2 + NS < 11:
                load_wi(j2 + NS)
        for hf in range(2):
            for c in range(8):
                for (t0, tn, grp, tl) in TB:
                    gate = self.lsc[:, grp, 5, :]
                    b = self.bank("mm")
                    for jj in range(11):
                        self.mm(self.ps[b][:, 0:tn], wo[hf][:, jj, c * 128:(c + 1) * 128], actT[:, hf * 11 + jj, t0:t0 + tn],
                                jj == 0, jj == 10, r=[("wo", hf), ("act", hf * 11 + jj, t0)], w=[("ps", b)])
                    xv = self.xT[:, c, t0:t0 + tn]
                    xk = [("xT", q, c) for q in tl]
                    self.stt(xv, self.ps[b][:, 0:tn], gate[:, c:c + 1], xv, ALU.mult, ALU.add, r=[("ps", b), "lsc"] + xk, w=xk)

    def mixer_residual(self, t, banks):
        grp = 0 if t < 8 else 1
        gate = self.lsc[:, grp, 2, :]
        tmp = self.nm_tmp
        for hb in range(2):
            b = banks[hb]
            pv = self.ps[b][:, :].rearrange("p (a b) -> p a b", a=4)
            tv = tmp[:, hb * 4:hb * 4 + 4, :]
            self.tt("dve", tv, pv, gate[:, hb * 4:hb * 4 + 4].unsqueeze(2).broadcast_to([128, 4, 128]), ALU.mult,
                    r=[("ps", b), "lsc"], w=["nm_tmp"])
            xv = self.xT[:, hb * 4:hb * 4 + 4, t * 128:(t + 1) * 128]
            xk = [("xT", t, hb * 4 + q) for q in range(4)]
            self.tt("pool", xv, xv, tv, ALU.add, r=["nm_tmp"] + xk, w=xk)

    def out_proj_pair(self, t0, w_out, rhs_of, rkeys):
        grp = 0 if t0 < 8 else 1
        gate = self.lsc[:, grp, 2, :]
        banks = []
        for bi in range(4):
            b = self.bank("mm")
            banks.append(b)
            for cc in range(2):
                c = 2 * bi + cc
                for kc in range(8):
                    self.mm(self.ps[b][:, cc * 256:(cc + 1) * 256], w_out[:, kc, c * 128:(c + 1) * 128], rhs_of(kc), kc == 0, kc == 7,
                            r=["w_out"] + rkeys, w=[("ps", b)])
        tmpv = self.nm_tmp.rearrange("p c t -> p (c t)")
        for bi in range(4):
            b = banks[bi]
            pv = self.ps[b][:, :].rearrange("p (a b) -> p a b", a=2)
            tv = tmpv[:, (bi % 2) * 512:(bi % 2 + 1) * 512].rearrange("p (a b) -> p a b", a=2)
            self.tt("dve", tv, pv, gate[:, 2 * bi:2 * bi + 2].unsqueeze(2).broadcast_to([128, 2, 256]), ALU.mult,
                    r=[("ps", b), "lsc"], w=["nm_tmp"])
            xv = self.xT[:, 2 * bi:2 * bi + 2, t0 * 128:(t0 + 2) * 128]
            xk = [("xT", t0 + q, 2 * bi + cc) for q in range(2) for cc in range(2)]
            self.tt("pool", xv, xv, tv, ALU.add, r=["nm_tmp"] + xk, w=xk)

    def rope(self, xv, H, Q, cs, tmps, r, w, keyp):
        x1 = xv[:, :, :, 0, :]
        x2 = xv[:, :, :, 1, :]
        c = cs[:, 0, :].rearrange("p (a q) -> p a q", a=2, q=Q).unsqueeze(1).broadcast_to([128, H, 2, Q])
        s = cs[:, 1, :].rearrange("p (a q) -> p a q", a=2, q=Q).unsqueeze(1).broadcast_to([128, H, 2, Q])
        t1, t2, t3, t4 = tmps
        k1, k2, k3, k4 = [(keyp, i) for i in range(4)]
        self.tt("dve", t1, x1, c, ALU.mult, r=r, w=[k1])
        self.tt("pool", t2, x2, s, ALU.mult, r=r, w=[k2])
        self.tt("dve", t3, x1, s, ALU.mult, r=r, w=[k3])
        self.tt("pool", t4, x2, c, ALU.mult, r=r, w=[k4])
        self.tt("dve", x1, t1, t2, ALU.subtract, r=[k1, k2, k3], w=w)
        self.tt("dve", x2, t3, t4, ALU.add, r=[k3, k4], w=w)

    def attention(self, KT, kparts, kt_list, QTv, nq, VA_of, scale, Pt, out_of, kr, qr, vr, ow):
        ob = self.bank("acc")
        first, last = kt_list[0], kt_list[-1]
        npt = len(Pt)

        def pv(st, pi):
            for j in range(nq):
                self.mm(self.ps[ob][:, j * 65:(j + 1) * 65], Pt[pi][:, j, :], VA_of(st), (st == first and j == 0), st == last,
                        r=[("Pt", pi)] + vr, w=[("ps", ob)], skip_group_check=True)

        pend = None
        for st in kt_list:
            sb = self.bank("mm")
            self.mm(self.ps[sb][:, 0:nq * 128], KT(st), QTv, True, True, r=kr + qr, w=[("ps", sb)])
            pi = self.pt_i % npt
            self.pt_i += 1
            self.act(Pt[pi][:, 0:nq, :], self.ps[sb][:, 0:nq * 128].rearrange("p (a b) -> p a b", a=nq), AF.Exp,
                     r=[("ps", sb)], w=[("Pt", pi)], scale=scale)
            if pend is not None:
                pv(*pend)
            pend = (st, pi)
        pv(*pend)
        ov = self.ps[ob][:, 0:nq * 65].rearrange("p (a b) -> p a b", a=nq)
        rd = self.at_rden
        self.recip(rd[:, 0:nq], ov[:, :, 64], r=[("ps", ob)], w=["at_rden"])
        for j in range(nq):
            dst, dk = out_of(j)
            self.P.op("dve", lambda h, j=j, dst=dst, rd=rd, ov=ov: h.tensor_scalar(dst, ov[:, j, 0:64], rd[:, j:j + 1], None, ALU.mult),
                      r=[("ps", ob), "at_rden"], w=[dk])

    def mixer_c(self, l, i):
        A = self.A
        A.reset()
        I, O = self.I, self.O
        w_in = A.alloc([8, 1536], BF16)
        w_out = A.alloc([8, 1024], BF16)
        hT = A.alloc([8, 256], BF16)
        KT = A.alloc([4, 1536], BF16)
        VA = A.alloc([12, 4, 65], BF16)
        self.alloc_norm_tmp()
        kvf = A.alloc([512], F32)
        sq = A.alloc([1024], F32)
        qn = A.alloc([1024], F32)
        st16 = A.alloc([16], F32)
        knb = A.alloc([256], BF16)
        qnb = A.alloc([1024], BF16)
        QT = A.alloc([16, 128], BF16)
        Pt = [A.alloc([4, 128], BF16) for _ in range(3)]
        ob = A.alloc([1024], BF16)
        OT = A.alloc([8, 256], BF16)
        self.at_rden = A.alloc([4], F32)
        rtm = [A.alloc([16, 2, 16], F32) for _ in range(4)]
        ropeg = A.alloc([2, 2, 32], F32)
        ck = A.alloc([4, 256], F32)
        cv = A.alloc([4, 256], F32)
        ckb = A.alloc([4, 256], BF16)
        kg = [A.alloc([512], F32) for _ in range(2)]
        self.pt_i = 0
        gq = self.cst("gq%d" % i)
        gk = self.cst("gk%d" % i)
        for kh in range(2):
            self.dma("pool", w_in[:, kh * 4:(kh + 1) * 4, :],
                     I["gqa_w_in"][i, kh * 512:(kh + 1) * 512, :].rearrange("(k p) n -> p k n", p=128), r=[], w=[("w_in", kh)], dq="w_in")
        self.dma("pool", w_out, I["gqa_w_out"][i].rearrange("(k p) n -> p k n", p=128), r=[], w=["w_out"], dq="w_out")
        self.memset("pool", VA[:, :, :, 64:65], 1.0, w=["VA1"])
        wk = [("w_in", 0), ("w_in", 1)]
        self.dma("sp", ropeg, I["ropeg"].rearrange("(t p) c q -> p t c q", p=128), r=[], w=["ropeg"], dq="ropeg")
        cc_src, cc_dst = self.CC[(l, "c")]

        def put_keys(knb_ap, kt, rk):
            b = self.bank("tr")
            for g in range(4):
                self.tr(self.psb[b][0:64, g * 128:(g + 1) * 128], knb_ap[:, g * 64:(g + 1) * 64], self.ident_b,
                        r=rk + ["identb"], w=[("ps", b)])
            self.cp("dve", KT[0:64, :, kt * 128:(kt + 1) * 128], self.psb[b][0:64, 0:512].rearrange("p (a b) -> p a b", a=4),
                    r=[("ps", b)], w=[("KT", kt)])

        def kv_side(t, n, rope_n):
            self.normmod(t, 0, hT[:, :, n * 128:(n + 1) * 128], ("hT", n))
            b = self.bank("mm")
            for kc in range(8):
                self.mm(self.ps[b][:, :], hT[:, kc, n * 128:(n + 1) * 128], w_in[:, kc, 1024:1536], kc == 0, kc == 7,
                        r=[("hT", n)] + wk, w=[("ps", b)])
            self.cp("act", kvf, self.ps[b][:, :], r=[("ps", b)], w=["kvf"])
            self.act(sq[:, 0:256], self.ps[b][:, 0:256], AF.Square, r=[("ps", b)], w=["sq"])
            self.red(st16[:, 0:4], sq[:, 0:256].rearrange("p (g d) -> p g d", g=4), r=["sq"], w=["st16"])
            self.rstd(st16[:, 0:4], st16[:, 0:4], 64, r=["st16"], w=["st16"])
            kv3 = kvf[:, 0:256].rearrange("p (g d) -> p g d", g=4)
            self.tt("dve", kv3, kv3, st16[:, 0:4].unsqueeze(2).broadcast_to([128, 4, 64]), ALU.mult, r=["kvf", "st16"], w=["kvf"])
            self.tt("dve", kv3, kv3, gk.unsqueeze(1).broadcast_to([128, 4, 64]), ALU.mult, r=["kvf", "cst"], w=["kvf"])
            if rope_n is not None:
                self.rope(kvf[:, 0:256].rearrange("p (h a b q) -> p h a b q", h=4, a=2, b=2, q=16), 4, 16, ropeg[:, rope_n, :, :],
                          [x[:, 0:4, :, :] for x in rtm], r=["kvf", "ropeg"], w=["kvf"], keyp="rtm")

        def q_attn_out(t, n, rope_n, kt_list):
            kkeys = [("KT", kt) for kt in kt_list]
            vkeys = [("VA", kt) for kt in kt_list] + ["VA1"]
            qb = []
            for bk in range(2):
                b = self.bank("mm")
                qb.append(b)
                for kc in range(8):
                    self.mm(self.ps[b][:, :], hT[:, kc, n * 128:(n + 1) * 128], w_in[:, kc, bk * 512:(bk + 1) * 512], kc == 0, kc == 7,
                            r=[("hT", n)] + wk, w=[("ps", b)])
                self.act(sq[:, bk * 512:(bk + 1) * 512], self.ps[b][:, :], AF.Square, r=[("ps", b)], w=["sq"])
            self.red(st16, sq.rearrange("p (g d) -> p g d", g=16), r=["sq"], w=["st16"])
            self.rstd(st16, st16, 64, r=["st16"], w=["st16"])
            for bk in range(2):
                self.tt("dve", qn[:, bk * 512:(bk + 1) * 512].rearrange("p (g d) -> p g d", g=8),
                        self.ps[qb[bk]][:, :].rearrange("p (g d) -> p g d", g=8),
                        st16[:, bk * 8:(bk + 1) * 8].unsqueeze(2).broadcast_to([128, 8, 64]), ALU.mult,
                        r=[("ps", qb[bk]), "st16"], w=["qn"])
            qn3 = qn.rearrange("p (g d) -> p g d", g=16)
            self.tt("pool", qn3, qn3, gq.unsqueeze(1).broadcast_to([128, 16, 64]), ALU.mult, r=["qn", "cst"], w=["qn"])
            if rope_n is not None:
                self.rope(qn.rearrange("p (h a b q) -> p h a b q", h=16, a=2, b=2, q=16), 16, 16, ropeg[:, rope_n, :, :], rtm,
                          r=["qn", "ropeg"], w=["qn"], keyp="rtm")
            self.cp("act", qnb, qn, r=["qn"], w=["qnb"])
            for hb in range(2):
                b = self.bank("tr")
                for hh in range(8):
                    h_ = hb * 8 + hh
                    self.tr(self.psb[b][0:64, hh * 128:(hh + 1) * 128], qnb[:, h_ * 64:(h_ + 1) * 64], self.ident_b,
                            r=["qnb", "identb"], w=[("ps", b)])
                self.cp("dve" if hb else "act", QT[0:64, hb * 8:(hb + 1) * 8, :],
                        self.psb[b][0:64, :].rearrange("p (a b) -> p a b", a=8), r=[("ps", b)], w=[("QT", hb)])
            for g in range(4):
                self.attention(
                    KT=lambda st, g=g: KT[0:64, g, st * 128:(st + 1) * 128], kparts=64, kt_list=kt_list,
                    QTv=QT[0:64, 4 * g:4 * g + 4, :], nq=4,
                    VA_of=lambda st, g=g: VA[:, st, g, :], scale=0.125, Pt=Pt,
                    out_of=lambda j, g=g: (ob[:, (4 * g + j) * 64:(4 * g + j + 1) * 64], ("ob", g)),
                    kr=kkeys, qr=[("QT", g // 2)], vr=vkeys, ow=None)
            b = self.bank("tr")
            for c in range(8):
                self.tr(self.psb[b][:, c * 128:(c + 1) * 128], ob[:, c * 128:(c + 1) * 128], self.ident_b,
                        r=[("ob", c // 2), "identb"], w=[("ps", b)])
            self.cp("act", OT[:, :, n * 128:(n + 1) * 128], self.psb[b][:, :].rearrange("p (a b) -> p a b", a=8), r=[("ps", b)], w=[("OT", n)])
            if n == 1:
                self.out_proj_pair(t - 1, w_out, lambda kc: OT[:, kc, :], [("OT", 0), ("OT", 1)])

        cck = "cc_src_c%d" % l
        ccd = "cc_dst_c%d" % l
        for n, t in enumerate(self.sseg["tiles"]):
            kv_side(t, n, n)
            self.dma("sp", cc_src[n * 128:(n + 1) * 128, :], kvf, r=["kvf"], w=[cck], dq="ccb_c")
        self.allgather((l, "c"), r=[cck], w=[ccd])
        for sq_ in self.seqs:
            tiles = sq_["tiles"]
            bi = sq_["bidx"]
            for n, t in enumerate(tiles):
                kv_side(t, n, None)
                self.dma("sp", O["o_gk"][bi, i, n * 128:(n + 1) * 128, :], kvf[:, 0:256], r=["kvf"], w=[], dq="kvf_o")
                self.dma("sp", O["o_gv"][bi, i, n * 128:(n + 1) * 128, :], kvf[:, 256:512], r=["kvf"], w=[], dq="kvf_o")
                self.cp("act", knb, kvf[:, 0:256], r=["kvf"], w=["knb"])
                put_keys(knb, n, ["knb"])
                self.cp("pool", VA[:, n, :, 0:64], kvf[:, 256:512].rearrange("p (g d) -> p g d", g=4), r=["kvf"], w=[("VA", n)])
            for n, t in enumerate(tiles):
                q_attn_out(t, n, None, [0, 1])
        self.dma("sp", ck, I["c_gk"][i].rearrange("(t p) n -> p t n", p=128), r=[], w=["ck"], dq="ck")
        self.dma("sp", cv, I["c_gv"][i].rearrange("(t p) n -> p t n", p=128), r=[], w=["cv"], dq="cv")
        self.cp("act", ckb, ck, r=["ck"], w=["ckb"])
        for kt in range(4):
            put_keys(ckb[:, kt, :], kt, ["ckb"])
            self.cp("pool", VA[:, kt, :, 0:64], cv[:, kt, :].rearrange("p (g d) -> p g d", g=4), r=["cv"], w=[("VA", kt)])
        for k in range(8):
            kgk = kg[k % 2]
            kk = ("kg", k % 2)
            self.dma("sp", kgk, cc_dst[k * 128:(k + 1) * 128, :], r=[ccd], w=[kk], dq=kk)
            self.cp("act", knb, kgk[:, 0:256], r=[kk], w=["knb"])
            put_keys(knb, 4 + k, ["knb"])
            self.cp("pool", VA[:, 4 + k, :, 0:64], kgk[:, 256:512].rearrange("p (g d) -> p g d", g=4), r=[kk], w=[("VA", 4 + k)])
        for n, t in enumerate(self.sseg["tiles"]):
            self.normmod(t, 0, hT[:, :, n * 128:(n + 1) * 128], ("hT", n))
            q_attn_out(t, n, n, list(range(12)))

    def mixer_ab(self, l, i):
        A = self.A
        A.reset()
        I, O = self.I, self.O
        w_in = A.alloc([8, 2240], BF16)
        w_out = A.alloc([8, 1024], BF16)
        OG = A.alloc([4, 256], BF16)
        hTm = A.alloc([8, 128], BF16)
        hk = "hTm"
        self.alloc_norm_tmp()
        self.at_rden = A.alloc([4], F32)
        Pt = [A.alloc([4, 128], BF16) for _ in range(3)]
        self.pt_i = 0
        SP = dict(qlT=[A.alloc([2, 256], BF16) for _ in range(2)], oT=A.alloc([4, 256], F32), rsT=A.alloc([4, 256], BF16),
                  etot=A.alloc([4, 2], F32), cqb=A.alloc([2, 384], BF16), aseg=A.alloc([4], F32))
        base_off = A.off
        for kh in range(2):
            for ch in range(2):
                self.dma("pool", w_in[:, kh * 4:(kh + 1) * 4, ch * 1120:(ch + 1) * 1120],
                         I["ab_w_in"][i, kh * 512:(kh + 1) * 512, ch * 1120:(ch + 1) * 1120].rearrange("(k p) n -> p k n", p=128),
                         r=[], w=[("w_in", kh, ch)], dq="w_in")
        self.dma("pool", w_out, I["ab_w_out"][i].rearrange("(k p) n -> p k n", p=128), r=[], w=["w_out"], dq="w_out")
        wk = [("w_in", 0, 0), ("w_in", 0, 1), ("w_in", 1, 0), ("w_in", 1, 1)]
        gout = self.cst("gout%d" % i)
        gqn = self.cst("gqn%d" % i)
        gkvn = self.cst("gkvn%d" % i)
        gq96 = self.cst("gq96%d" % i)
        gk96 = self.cst("gk96%d" % i)
        sel = self.cst("sel")
        trim = [self.cst("trim0"), self.cst("trim1")]
        tris = [self.cst("tris0"), self.cst("tris1")]
        mask = [self.cst("mask0"), self.cst("mask1")]
        ccg_src, ccg_dst = self.CC[(l, "g")]
        ccm_src, ccm_dst = self.CC[(l, "m")]

        def g_alloc(NT, persist=None):
            T = NT * 128
            B = {}
            if persist is None:
                B["qlT"] = [A.alloc([2, T], BF16) for _ in range(2)]
                B["oT"] = A.alloc([4, T], F32)
                B["rsT"] = A.alloc([4, T], BF16)
                B["etot"] = A.alloc([4, NT], F32)
            else:
                for k in ("qlT", "oT", "rsT", "etot"):
                    B[k] = persist[k]
            B["klT"] = [A.alloc([2, T], BF16) for _ in range(2)]
            B["kst"] = [A.alloc([NT, 256], BF16) for _ in range(2)]
            B["vtk"] = A.alloc([NT, 512], BF16)
            B["Sst"] = [[A.alloc([128], F32) for _ in range(2)] for _ in range(2)]
            B["Sbf"] = [[A.alloc([128], BF16) for _ in range(2)] for _ in range(2)]
            B["alT"] = A.alloc([128], F32)
            B["lsp"] = A.alloc([512], F32)
            B["Eb"] = A.alloc([4, 128], F32)
            B["Enb"] = A.alloc([4, 128], F32)
            B["Ed2"] = A.alloc([512], F32)
            B["ATm"] = [A.alloc([128], BF16) for _ in range(4)]
            B["aw2"] = A.alloc([512], F32)
            self.memset("dve", B["alT"][32:33, :], 1.0, w=["alT1"])
            aw2 = B["aw2"]
            self.memset("dve", aw2[0:33, :], 0.0, w=["aw2"])
            for z in range(2):
                self.dma("sp", aw2[16 * z:16 * z + 16, z * 256:(z + 1) * 256], I["a_w2"][i, z], r=[], w=["aw2"], dq="aw2")
            self.dma("sp", aw2[32:33, :], I["a_b"][i:i + 1].rearrange("o z n -> o (z n)"), r=[], w=["aw2"], dq="aw2")
            return B

        def g_prep(tiles, B):
            qlT, klT, kst, vtk, rsT, etot = B["qlT"], B["klT"], B["kst"], B["vtk"], B["rsT"], B["etot"]
            alT, lsp, Eb, Enb, Ed2, aw2 = B["alT"], B["lsp"], B["Eb"], B["Enb"], B["Ed2"], B["aw2"]
            for n, t in enumerate(tiles):
                nc_ = slice(n * 128, (n + 1) * 128)
                h = hTm
                self.normmod(t, 0, h, hk)
                bqk = self.bank("mm")
                for ch in range(4):
                    for kc in range(8):
                        self.mm(self.ps[bqk][:, ch * 128:(ch + 1) * 128], w_in[:, kc, ch * 128:(ch + 1) * 128], h[:, kc, :], kc == 0, kc == 7,
                                r=[hk] + wk, w=[("ps", bqk)])
                br = self.bank("mm")
                for ch in range(4):
                    for kc in range(8):
                        self.mm(self.ps[br][:, ch * 128:(ch + 1) * 128], w_in[:, kc, 1024 + ch * 128:1024 + (ch + 1) * 128], h[:, kc, :], kc == 0, kc == 7,
                                r=[hk] + wk, w=[("ps", br)])
                self.act(rsT[:, :, nc_], self.ps[br][:, :].rearrange("p (a b) -> p a b", a=4), AF.Silu, r=[("ps", br)], w=[("rsT", n)])
                ba = self.bank("tr")
                for kc in range(8):
                    self.mm(self.ps[ba][0:32, 0:128], w_in[:, kc, 1536:1568], h[:, kc, :], kc == 0, kc == 7, r=[hk] + wk, w=[("ps", ba)])
                self.cp("act", alT[0:32, :], self.ps[ba][0:32, 0:128], r=[("ps", ba)], w=["alT"])
                bkv = self.bank("mm")
                for kc in range(8):
                    self.mm(self.ps[bkv][:, :], h[:, kc, :], w_in[:, kc, 256:768], kc == 0, kc == 7, r=[hk] + wk, w=[("ps", bkv)])
                bv2 = self.bank("mm")
                for kc in range(8):
                    self.mm(self.ps[bv2][:, 0:256], h[:, kc, :], w_in[:, kc, 768:1024], kc == 0, kc == 7, r=[hk] + wk, w=[("ps", bv2)])
                self.cp("act", vtk[:, n, 0:256], self.ps[bkv][:, 256:512], r=[("ps", bkv)], w=[("vtk", n, 0)])
                self.cp("act", vtk[:, n, 256:512], self.ps[bv2][:, 0:256], r=[("ps", bv2)], w=[("vtk", n, 1)])
                bl = self.bank("tr")
                self.mm(self.ps[bl][:, :], alT[0:33, :], aw2[0:33, :], True, True, r=["alT", "alT1", "aw2"], w=[("ps", bl)])
                self.act(lsp, self.ps[bl][:, :], AF.Exp, r=[("ps", bl)], w=["lsp"], scale=-1.0)
                self.act(lsp, lsp, AF.Ln, r=["lsp"], w=["lsp"], bias=1.0)
                bb = self.bank("tr")
                for z in range(2):
                    for fc in range(2):
                        zf = z * 2 + fc
                        self.mm(self.ps[bb][:, zf * 128:(zf + 1) * 128], lsp[:, zf * 128:(zf + 1) * 128], trim[z], True, True,
                                r=["lsp", "cst"], w=[("ps", bb)])
                bd = self.bank("tr")
                for z in range(2):
                    self.mm(self.ps[bd][:, z * 256:(z + 1) * 256], tris[z], lsp[:, z * 256:(z + 1) * 256], True, True,
                            r=["lsp", "cst"], w=[("ps", bd)])
                pbb = self.ps[bb][:, :].rearrange("p (a b) -> p a b", a=4)
                self.act(Eb, pbb, AF.Exp, r=[("ps", bb)], w=["Eb"])
                self.act(Enb, pbb, AF.Exp, r=[("ps", bb)], w=["Enb"], scale=-1.0)
                self.act(Ed2, self.ps[bd][:, :], AF.Exp, r=[("ps", bd)], w=["Ed2"])
                self.cp("pool", etot[:, 0:2, n], Eb[:, 0:2, 127], r=["Eb"], w=[("etot", n)])
                self.cp("pool", etot[:, 2:4, n], Eb[:, 2:4, 0], r=["Eb"], w=[("etot", n)])
                pqk = self.ps[bqk][:, :].rearrange("p (a b) -> p a b", a=4)
                for z in range(2):
                    self.stt(qlT[z][:, :, nc_], pqk[:, 0:2, :], 0.125, Eb[:, 2 * z:2 * z + 2, :], ALU.mult, ALU.mult,
                             r=[("ps", bqk), "Eb"], w=[("qlT", z, n)])
                    self.tt("dve", klT[z][:, :, nc_], pqk[:, 2:4, :], Enb[:, 2 * z:2 * z + 2, :], ALU.mult,
                            r=[("ps", bqk), "Enb"], w=[("klT", z, n)])
                    self.tt("dve", kst[z][:, n, :], self.ps[bkv][:, 0:256], Ed2[:, z * 256:(z + 1) * 256], ALU.mult,
                            r=[("ps", bkv), "Ed2"], w=[("kst", z, n)])

        def g_scan(NT, B, bidx):
            qlT, klT, kst, vtk, oT, etot, Sst, Sbf, ATm = (B[k] for k in ("qlT", "klT", "kst", "vtk", "oT", "etot", "Sst", "Sbf", "ATm"))
            for z in range(2):
                for fc in range(2):
                    self.memset("pool", Sst[z][fc], 0.0, w=[("S", z, fc)])
                    self.cp("act", Sbf[z][fc], Sst[z][fc], r=[("S", z, fc)], w=[("Sbf", z, fc)])
                order = list(range(NT)) if z == 0 else list(range(NT - 1, -1, -1))
                for n in order:
                    nc_ = slice(n * 128, (n + 1) * 128)
                    bo = self.bank("acc")
                    bats = []
                    for hh in range(4):
                        fc, hp = hh // 2, hh % 2
                        pr = slice(hp * 64, (hp + 1) * 64)
                        bat = self.bank("mm")
                        bats.append(bat)
                        self.mm(self.ps[bat][:, 0:128], klT[z][pr, fc, nc_], qlT[z][pr, fc, nc_], True, True,
                                r=[("klT", z, n), ("qlT", z, n)], w=[("ps", bat)])
                    for hh in range(4):
                        self.tt("dve", ATm[hh], self.ps[bats[hh]][:, 0:128], mask[z], ALU.mult, r=[("ps", bats[hh]), "cst"], w=[("ATm", hh)])
                    for hh in range(4):
                        fc, hp = hh // 2, hh % 2
                        pr = slice(hp * 64, (hp + 1) * 64)
                        self.mm(self.ps[bo][:, hh * 128:(hh + 1) * 128], vtk[:, n, hh * 128:(hh + 1) * 128], ATm[hh], True, False,
                                r=[("vtk", n, 0), ("vtk", n, 1), ("ATm", hh)], w=[("ps", bo)])
                        self.mm(self.ps[bo][:, hh * 128:(hh + 1) * 128], Sbf[z][fc][pr, :], qlT[z][pr, fc, nc_], False, True,
                                r=[("Sbf", z, fc), ("qlT", z, n)], w=[("ps", bo)])
                    pbo = self.ps[bo][:, :].rearrange("p (a b) -> p a b", a=4)
                    if z == 0:
                        self.cp("act", oT[:, :, nc_], pbo, r=[("ps", bo)], w=[("oT", n)])
                    else:
                        self.tt("dve", oT[:, :, nc_], oT[:, :, nc_], pbo, ALU.add, r=[("ps", bo), ("oT", n)], w=[("oT", n)])
                    bus = []
                    for fc in range(2):
                        bu = self.bank("mm")
                        bus.append(bu)
                        for hp in range(2):
                            hh = fc * 2 + hp
                            self.mm(self.ps[bu][hp * 64:(hp + 1) * 64, 0:128], kst[z][:, n, hh * 64:(hh + 1) * 64],
                                    vtk[:, n, hh * 128:(hh + 1) * 128], True, True,
                                    r=[("kst", z, n), ("vtk", n, 0), ("vtk", n, 1)], w=[("ps", bu)])
                    for fc in range(2):
                        self.stt(Sst[z][fc], Sst[z][fc], etot[:, z * 2 + fc, n:n + 1], self.ps[bus[fc]][:, 0:128], ALU.mult, ALU.add,
                                 r=[("S", z, fc), ("etot", n), ("ps", bus[fc])], w=[("S", z, fc)])
                    for fc in range(2):
                        self.cp("act", Sbf[z][fc], Sst[z][fc], r=[("S", z, fc)], w=[("Sbf", z, fc)])
                if bidx is not None:
                    for fc in range(2):
                        self.dma("sp", O["o_gla"][bidx, i, z, fc * 128:(fc + 1) * 128, :], Sst[z][fc],
                                 r=[("S", z, fc)], w=[], dq=("S", z, fc))

        def g_out_alloc():
            return (A.alloc([4, 128], BF16), A.alloc([4, 128], F32), A.alloc([4, 128], F32))

        def g_out(tiles, oT, rsT, tmps=None):
            osq, orst, otmp = tmps if tmps is not None else g_out_alloc()
            for n, t in enumerate(tiles):
                nc_ = slice(n * 128, (n + 1) * 128)
                self.act(osq, oT[:, :, nc_], AF.Square, r=[("oT", n)], w=["osq"])
                b = self.bank("tr")
                self.mm(self.ps[b][:, :], self.ones_b, osq.rearrange("p a b -> p (a b)"), True, True, r=["osq", "ones"], w=[("ps", b)])
                self.rstd(orst, self.ps[b][:, :].rearrange("p (a b) -> p a b", a=4), 128, r=[("ps", b)], w=["orst"])
                self.tt("dve", otmp, oT[:, :, nc_], orst, ALU.mult, r=[("oT", n), "orst"], w=["otmp"])
                self.stt(OG[:, :, nc_], otmp, gout, rsT[:, :, nc_], ALU.mult, ALU.mult,
                         r=["otmp", "cst", ("rsT", n)], w=[("OG", n)])

        def m_alloc(NK, rope=True):
            M = {}
            M["w_qb"] = A.alloc([3, 768], BF16)
            M["w_kvb"] = A.alloc([2, 1024], BF16)
            self.dma("pool", M["w_qb"], I["w_qb"][i].rearrange("(k p) n -> p k n", p=128), r=[], w=["w_qb"], dq="w_qb")
            self.dma("pool", M["w_kvb"], I["w_kvb"][i].rearrange("(k p) n -> p k n", p=128), r=[], w=["w_kvb"], dq="w_kvb")
            M["KTm"] = A.alloc([8, NK * 128], BF16)
            M["VAm"] = A.alloc([NK, 8, 65], BF16)
            M["QTm"] = A.alloc([8, 256], BF16)
            M["omb"] = A.alloc([2, 512], BF16)
            M["OM"] = A.alloc([4, 256], BF16)
            M["cb"] = A.alloc([256], BF16)
            M["cT"] = A.alloc([2, 128], BF16)
            M["kc96"] = A.alloc([8, 96], F32)
            M["knb"] = A.alloc([8, 96], BF16)
            M["st8"] = A.alloc([8], F32)
            M["cqT"] = A.alloc([3, 128], BF16)
            M["rtm"] = [A.alloc([8, 2, 8], F32) for _ in range(4)] if rope else None
            self.memset("pool", M["VAm"][:, :, :, 64:65], 1.0, w=["VA1"])
            return M

        def own_alloc():
            W = {}
            W["sq96"] = A.alloc([8, 96], F32)
            W["ckvf"] = A.alloc([256], F32)
            W["ckvn"] = A.alloc([256], F32)
            W["kpe"] = A.alloc([32], F32)
            W["st1"] = A.alloc([2], F32)
            return W

        def norm96(M, sq96, gain, rope_cs):
            kc96, knb, st8, rtm = M["kc96"], M["knb"], M["st8"], M["rtm"]
            self.act(sq96, kc96, AF.Square, r=["kc96"], w=["sq96"])
            self.red(st8, sq96, r=["sq96"], w=["st8"])
            self.rstd(st8, st8, 96, r=["st8"], w=["st8"])
            self.tt("dve", kc96, kc96, st8.unsqueeze(2).broadcast_to([128, 8, 96]), ALU.mult, r=["kc96", "st8"], w=["kc96"])
            self.tt("pool", kc96, kc96, gain.unsqueeze(1).broadcast_to([128, 8, 96]), ALU.mult, r=["kc96", "cst"], w=["kc96"])
            if rope_cs is not None:
                cs, csk = rope_cs
                self.rope(kc96[:, :, 64:96].rearrange("p h (a b q) -> p h a b q", a=2, b=2, q=8), 8, 8, cs, rtm,
                          r=["kc96", csk], w=["kc96"], keyp="rtm")
            self.cp("act", knb, kc96, r=["kc96"], w=["knb"])

        def kside(M, sq96, ckvn_ap, ckvn_k, kpe_ap, kpe_k, kt, rope_cs):
            cb, cT, kc96, knb, KTm, VAm, w_kvb = M["cb"], M["cT"], M["kc96"], M["knb"], M["KTm"], M["VAm"], M["w_kvb"]
            self.cp("act", cb, ckvn_ap, r=[ckvn_k], w=["cb"])
            b = self.bank("tr")
            for kc in range(2):
                self.tr(self.psb[b][:, kc * 128:(kc + 1) * 128], cb[:, kc * 128:(kc + 1) * 128], self.ident_b, r=["cb", "identb"], w=[("ps", b)])
            self.cp("dve", cT, self.psb[b][:, 0:256].rearrange("p (a b) -> p a b", a=2), r=[("ps", b)], w=["cT"])
            for bk in range(2):
                b = self.bank("mm")
                for kc in range(2):
                    self.mm(self.ps[b][:, :], cT[:, kc, :], w_kvb[:, kc, bk * 512:(bk + 1) * 512], kc == 0, kc == 1,
                            r=["cT", "w_kvb"], w=[("ps", b)])
                pv = self.ps[b][:, :].rearrange("p (h d) -> p h d", h=4)
                self.cp("act", kc96[:, bk * 4:(bk + 1) * 4, 0:64], pv[:, :, 0:64], r=[("ps", b)], w=["kc96"])
                self.cp("dve", VAm[:, kt, bk * 4:(bk + 1) * 4, 0:64], pv[:, :, 64:128], r=[("ps", b)], w=[("VAm", kt)])
            self.cp("pool", kc96[:, :, 64:96], kpe_ap.unsqueeze(1).broadcast_to([128, 8, 32]), r=[kpe_k], w=["kc96"])
            norm96(M, sq96, gk96, rope_cs)
            b = self.bank("tr")
            for hh in range(8):
                self.tr(self.psb[b][0:96, hh * 128:(hh + 1) * 128], knb[:, hh, :], self.ident_b, r=["knb", "identb"], w=[("ps", b)])
            self.cp("dve", KTm[0:96, :, kt * 128:(kt + 1) * 128], self.psb[b][0:96, :].rearrange("p (a b) -> p a b", a=8),
                    r=[("ps", b)], w=[("KTm", kt)])

        def m_own(t, n, W, cqb):
            sq96, ckvf, ckvn, kpe, st1 = W["sq96"], W["ckvf"], W["ckvn"], W["kpe"], W["st1"]
            h = hTm
            self.normmod(t, 0, h, hk)
            b1 = self.bank("mm")
            for kc in range(8):
                self.mm(self.ps[b1][:, :], h[:, kc, :], w_in[:, kc, 1568:2080], kc == 0, kc == 7, r=[hk] + wk, w=[("ps", b1)])
            b2 = self.bank("mm")
            for kc in range(8):
                self.mm(self.ps[b2][:, 0:160], h[:, kc, :], w_in[:, kc, 2080:2240], kc == 0, kc == 7, r=[hk] + wk, w=[("ps", b2)])
            sqf = sq96.rearrange("p a b -> p (a b)")
            self.act(sqf[:, 0:384], self.ps[b1][:, 0:384], AF.Square, r=[("ps", b1)], w=["sq96", "st1"], accum_out=st1[:, 0:1])
            self.rstd(st1[:, 0:1], st1[:, 0:1], 384, r=["st1"], w=["st1"])
            self.stt(cqb[:, n, :], self.ps[b1][:, 0:384], st1[:, 0:1], gqn, ALU.mult, ALU.mult, r=[("ps", b1), "st1", "cst"], w=[("cqb", n)])
            self.cp("act", ckvf[:, 0:128], self.ps[b1][:, 384:512], r=[("ps", b1)], w=["ckvf"])
            self.cp("act", ckvf[:, 128:256], self.ps[b2][:, 0:128], r=[("ps", b2)], w=["ckvf"])
            self.cp("dve", kpe, self.ps[b2][:, 128:160], r=[("ps", b2)], w=["kpe"])
            self.act(sqf[:, 0:256], ckvf, AF.Square, r=["ckvf"], w=["sq96", "st1b"], accum_out=st1[:, 1:2])
            self.rstd(st1[:, 1:2], st1[:, 1:2], 256, r=["st1b"], w=["st1b"])
            self.stt(ckvn, ckvf, st1[:, 1:2], gkvn, ALU.mult, ALU.mult, r=["ckvf", "st1b", "cst"], w=["ckvn"])

        def m_attn(M, sq96, tiles, cqb, rope_q, NK):
            QTm, omb, OM, KTm, VAm, cqT, kc96, knb, w_qb = (M[k] for k in ("QTm", "omb", "OM", "KTm", "VAm", "cqT", "kc96", "knb", "w_qb"))
            kt_list = list(range(NK))
            kkeys = [("KTm", kt) for kt in kt_list]
            vkeys = [("VAm", kt) for kt in kt_list] + ["VA1"]
            nq = len(tiles)
            for n, t in enumerate(tiles):
                b = self.bank("tr")
                for kc in range(3):
                    self.tr(self.psb[b][:, kc * 128:(kc + 1) * 128], cqb[:, n, kc * 128:(kc + 1) * 128], self.ident_b,
                            r=[("cqb", n), "identb"], w=[("ps", b)])
                self.cp("act", cqT, self.psb[b][:, 0:384].rearrange("p (a b) -> p a b", a=3), r=[("ps", b)], w=["cqT"])
                for bk in range(2):
                    b = self.bank("mm")
                    for kc in range(3):
                        self.mm(self.ps[b][:, 0:384], cqT[:, kc, :], w_qb[:, kc, bk * 384:(bk + 1) * 384], kc == 0, kc == 2,
                                r=["cqT", "w_qb"], w=[("ps", b)])
                    self.cp("act", kc96[:, bk * 4:(bk + 1) * 4, :], self.ps[b][:, 0:384].rearrange("p (h d) -> p h d", h=4),
                            r=[("ps", b)], w=["kc96"])
                norm96(M, sq96, gq96, None if rope_q is None else (rope_q[0][:, n, :, :], rope_q[1]))
                b = self.bank("tr")
                for hh in range(8):
                    self.tr(self.psb[b][0:96, hh * 128:(hh + 1) * 128], knb[:, hh, :], self.ident_b, r=["knb", "identb"], w=[("ps", b)])
                self.cp("dve", QTm[0:96, :, n * 128:(n + 1) * 128], self.psb[b][0:96, :].rearrange("p (a b) -> p a b", a=8),
                        r=[("ps", b)], w=[("QTm", n)])
            for hh in range(8):
                self.attention(
                    KT=lambda st, hh=hh: KTm[0:96, hh, st * 128:(st + 1) * 128], kparts=96, kt_list=kt_list,
                    QTv=QTm[0:96, hh, 0:nq * 128], nq=nq,
                    VA_of=lambda st, hh=hh: VAm[:, st, hh, :], scale=float(96 ** -0.5), Pt=Pt,
                    out_of=lambda j, hh=hh: (omb[:, j, hh * 64:(hh + 1) * 64], ("omb", j)),
                    kr=kkeys, qr=[("QTm", j) for j in range(nq)], vr=vkeys, ow=None)
            for n, t in enumerate(tiles):
                b = self.bank("tr")
                for c in range(4):
                    self.tr(self.psb[b][:, c * 128:(c + 1) * 128], omb[:, n, c * 128:(c + 1) * 128], self.ident_b,
                            r=[("omb", n), "identb"], w=[("ps", b)])
                self.cp("act", OM[:, :, n * 128:(n + 1) * 128], self.psb[b][:, 0:512].rearrange("p (a b) -> p a b", a=4),
                        r=[("ps", b)], w=[("OM", n)])
            t0 = tiles[0]
            self.out_proj_pair(t0, w_out,
                               lambda kc: OG[:, kc, :] if kc < 4 else OM[:, kc - 4, 0:256],
                               [("OG", 0), ("OG", 1), ("OM", 0), ("OM", 1)])

        st_ = self.sseg["tiles"]
        A.reset(base_off)
        B = g_alloc(2, persist=SP)
        g_prep(st_, B)
        g_scan(2, B, None)
        ccgs, ccgd = "cc_src_g%d" % l, "cc_dst_g%d" % l
        gsrc = A.alloc([4, 129], F32)
        self.tt("dve", gsrc[:, :, 128], SP["etot"][:, :, 0], SP["etot"][:, :, 1], ALU.mult, r=[("etot", 0), ("etot", 1)], w=["gsrc_a"])
        for z in range(2):
            for fc in range(2):
                zf = z * 2 + fc
                self.cp("act", gsrc[:, zf, 0:128], B["Sst"][z][fc], r=[("S", z, fc)], w=[("gsrc", zf)])
        self.dma("sp", ccg_src.ap().rearrange("(zf p) c -> p zf c", p=128), gsrc,
                 r=["gsrc_a"] + [("gsrc", zf) for zf in range(4)], w=[ccgs], dq="bnc_g")
        self.allgather((l, "g"), r=[ccgs], w=[ccgd])
        self.P.barrier()
        if "x1" in self.stages:
            return
        A.reset(base_off)
        W = own_alloc()
        ccms, ccmd = "cc_src_m%d" % l, "cc_dst_m%d" % l
        for n, t in enumerate(st_):
            m_own(t, n, W, SP["cqb"])
            self.dma("sp", ccm_src[n * 128:(n + 1) * 128, 0:256], W["ckvn"], r=["ckvn"], w=[ccms], dq="bnc_m")
            self.dma("sp", ccm_src[n * 128:(n + 1) * 128, 256:288], W["kpe"], r=["kpe"], w=[ccms], dq="bnc_m")
        self.allgather((l, "m"), r=[ccms], w=[ccmd])
        self.P.barrier()
        if "x2" in self.stages:
            return

        A.reset(base_off)
        B = g_alloc(2)
        GT = g_out_alloc()
        M = m_alloc(2, rope=False)
        W = own_alloc()
        cqb = A.alloc([2, 384], BF16)
        for sq_ in self.seqs:
            tiles = sq_["tiles"]
            bi = sq_["bidx"]
            g_prep(tiles, B)
            g_scan(2, B, bi)
            g_out(tiles, B["oT"], B["rsT"], GT)
            for n, t in enumerate(tiles):
                m_own(t, n, W, cqb)
                self.dma("sp", O["o_ckv"][bi, i, n * 128:(n + 1) * 128, :], W["ckvn"], r=["ckvn"], w=[], dq="ckvn_o")
                self.dma("sp", O["o_kpe"][bi, i, n * 128:(n + 1) * 128, :], W["kpe"], r=["kpe"], w=[], dq="kpe_o")
                kside(M, W["sq96"], W["ckvn"], "ckvn", W["kpe"], "kpe", n, None)
            m_attn(M, W["sq96"], tiles, cqb, None, 2)
        self.P.barrier()

        if "x3" in self.stages:
            return
        A.reset(base_off)
        gU = A.alloc([16, 129], F32)
        self.dma("sp", gU, ccg_dst.ap().rearrange("(rz p) c -> p rz c", p=128), r=[ccgd], w=["gU"], dq="gU")
        Sin = [[A.alloc([128], F32) for _ in range(2)] for _ in range(2)]
        Sib = [[A.alloc([128], BF16) for _ in range(2)] for _ in range(2)]
        tS = A.alloc([128], F32)
        for z in range(2):
            for fc in range(2):
                zf = z * 2 + fc
                S_ = Sin[z][fc]
                sk = ("Sin", z, fc)
                self.dma("sp", S_, I["c_gla"][i, z, fc * 128:(fc + 1) * 128, :], r=[], w=[sk], dq=sk)
                ranks = [0, 1, 2, 3] if z == 0 else [3, 2, 1, 0]
                for k in ranks:
                    gi = k * 4 + zf
                    self.stt(tS, S_, gU[:, gi, 128:129], gU[:, gi, 0:128], ALU.mult, ALU.add, r=[sk, "gU"], w=["tS"])
                    self.tt("dve", tS, tS, S_, ALU.subtract, r=["tS", sk], w=["tS"])
                    self.stt(S_, tS, sel[:, z * 4 + k:z * 4 + k + 1], S_, ALU.mult, ALU.add, r=["tS", sk, "cst"], w=[sk])
                self.cp("act", Sib[z][fc], S_, r=[sk], w=[("Sib", z, fc)])
        if "x5" in self.stages:
            self.P.barrier()
            return
        for z in range(2):
            order = [0, 1] if z == 0 else [1, 0]
            for oi, n in enumerate(order):
                nc_ = slice(n * 128, (n + 1) * 128)
                bh = [self.bank("acc"), self.bank("acc")]
                for hp in range(2):
                    pr = slice(hp * 64, (hp + 1) * 64)
                    for fc in range(2):
                        self.mm(self.ps[bh[hp]][:, fc * 128:(fc + 1) * 128], Sib[z][fc][pr, :], SP["qlT"][z][pr, fc, nc_], True, True,
                                r=[("Sib", z, fc), ("qlT", z, n)], w=[("ps", bh[hp])])
                oview = SP["oT"][:, :, nc_].rearrange("p (f h) t -> p f h t", f=2, h=2)
                for hp in range(2):
                    pb = self.ps[bh[hp]][:, 0:256].rearrange("p (a b) -> p a b", a=2)
                    self.tt("dve", oview[:, :, hp, :], oview[:, :, hp, :], pb, ALU.add, r=[("ps", bh[hp]), ("oT", n)], w=[("oT", n)])
                if oi == 0:
                    for fc in range(2):
                        zf = z * 2 + fc
                        sk = ("Sin", z, fc)
                        self.P.op("dve", lambda h, S_=Sin[z][fc], e=SP["etot"][:, zf, n:n + 1]: h.tensor_scalar(S_, S_, e, None, ALU.mult),
                                  r=[sk, ("etot", n)], w=[sk])
                        self.cp("act", Sib[z][fc], Sin[z][fc], r=[sk], w=[("Sib", z, fc)])
        if "x6" in self.stages:
            self.P.barrier()
            return
        g_out(st_, SP["oT"], SP["rsT"])
        self.P.barrier()
        if "x4" in self.stages:
            return
        A.reset(base_off)
        M = m_alloc(12)
        sq96 = A.alloc([8, 96], F32)
        ropem_all = A.alloc([8, 2, 16], F32)
        ropem_own = A.alloc([2, 2, 16], F32)
        ckp = A.alloc([4, 32], F32)
        cck = A.alloc([256], F32)
        kgm = [A.alloc([288], F32) for _ in range(2)]
        self.dma("sp", ropem_all, I["ropem_all"].rearrange("(t p) c q -> p t c q", p=128), r=[], w=["ropem_all"], dq="ropem")
        self.dma("sp", ropem_own, I["ropem"].rearrange("(t p) c q -> p t c q", p=128), r=[], w=["ropem_own"], dq="ropem")
        self.dma("sp", ckp, I["c_kpe"][i].rearrange("(t p) n -> p t n", p=128), r=[], w=["ckp"], dq="ckp")
        for kt in range(4):
            self.dma("sp", cck, I["c_ckv"][i, kt * 128:(kt + 1) * 128, :], r=[], w=["cck"], dq="cck")
            kside(M, sq96, cck, "cck", ckp[:, kt, :], "ckp", kt, None)
        for k in range(8):
            kg_ = kgm[k % 2]
            kk = ("kgm", k % 2)
            self.dma("sp", kg_, ccm_dst[k * 128:(k + 1) * 128, :], r=[ccmd], w=[kk], dq=kk)
            kside(M, sq96, kg_[:, 0:256], kk, kg_[:, 256:288], kk, 4 + k, (ropem_all[:, k, :, :], "ropem_all"))
        m_attn(M, sq96, st_, SP["cqb"], (ropem_own, "ropem_own"), 12)


def _rope_tables(n_tok, d_rot):
    t = np.arange(n_tok, dtype=np.int32)
    pos = np.stack([t // 64, t % 64], axis=-1).astype(np.float32)
    quarter = d_rot // 4
    inv = np.power(np.float32(10000.0), -np.arange(quarter, dtype=np.float32) / np.float32(quarter)).astype(np.float32)
    ang = pos[:, :, None] * inv
    cos = np.cos(ang).astype(np.float32).reshape(n_tok, 2 * quarter)
    sin = np.sin(ang).astype(np.float32).reshape(n_tok, 2 * quarter)
    return np.ascontiguousarray(np.stack([cos, sin], axis=1))


def _build_cst(inp, b, j):
    c = np.zeros((128, NCST), np.float32)

    def put(name, arr, parts=128):
        o, n = CST_OFF[name]
        c[0:parts, o:o + n] = np.asarray(arr, np.float32).reshape(parts, n)

    s = np.arange(128)[:, None]
    t = np.arange(128)[None, :]
    v = np.float32(-1.0 / 16.0)
    put("ident", np.eye(128))
    put("trim0", (s <= t) * v)
    put("trim1", (s >= t) * v)
    put("tris0", (s > t) * v)
    put("tris1", (s < t) * v)
    put("mask0", (s <= t) * 1.0)
    put("mask1", (s >= t) * 1.0)
    put("rm", np.array([[1, 0], [0, 1], [1, 1]], np.float32), parts=3)
    cond = np.stack([inp["c_ctx"].reshape(8, 128).T, inp["c"][b].reshape(8, 128).T], axis=-1)
    put("cond", cond)
    selv = np.array([1.0 if k < j else 0.0 for k in range(4)] + [1.0 if k > j else 0.0 for k in range(4)], np.float32)
    put("sel", np.broadcast_to(selv[None, :], (128, 8)))
    put("gmix", inp["norm_mix_g"].reshape(4, 8, 128).transpose(2, 0, 1))
    put("gffn", inp["norm_ffn_g"].reshape(4, 8, 128).transpose(2, 0, 1))
    for i in range(2):
        put("gq%d" % i, np.broadcast_to(inp["gqa_qn_g"][i][None, :], (128, 64)))
        put("gk%d" % i, np.broadcast_to(inp["gqa_kn_g"][i][None, :], (128, 64)))
        put("gout%d" % i, inp["gla_out_g"][i].reshape(128, 1))
        put("gqn%d" % i, np.broadcast_to(inp["mla_q_norm_g"][i][None, :], (128, 384)))
        put("gkvn%d" % i, np.broadcast_to(inp["mla_kv_norm_g"][i][None, :], (128, 256)))
        put("gq96%d" % i, np.broadcast_to(inp["mla_qn_g"][i][None, :], (128, 96)))
        put("gk96%d" % i, np.broadcast_to(inp["mla_kn_g"][i][None, :], (128, 96)))
    return c


_NC_CACHE = {}


def _get_nc(debug=(), stages=None):
    key = (tuple(sorted(debug)), None if stages is None else tuple(sorted(stages)))
    if key not in _NC_CACHE:
        kb = KB(debug, stages)
        nc = kb.build()
        _NC_CACHE[key] = (nc, kb)
    return _NC_CACHE[key]


def make_in_maps(inp):
    inp = {k: np.ascontiguousarray(np.asarray(v)) for k, v in inp.items()}
    ropeg = _rope_tables(1024, 64)
    ropem = _rope_tables(1024, 32)
    shared = dict(
        ffn_w_in=inp["ffn_w_in"], ffn_w_out=inp["ffn_w_out"],
        ab_w_in=inp["ab_w_in"], ab_w_out=inp["ab_w_out"], a_w2=inp["gla_a_w2"], a_b=inp["gla_a_b"],
        w_qb=inp["mla_w_qb"], w_kvb=inp["mla_w_kvb"], gqa_w_in=inp["gqa_w_in"], gqa_w_out=inp["gqa_w_out"])
    in_maps = []
    for core in range(8):
        b, j = core // 4, core % 4
        m = dict(shared)
        m["ropem_all"] = ropem
        m["ada_w"] = np.ascontiguousarray(inp["ada_w"][:, :, j * 1536:(j + 1) * 1536])
        m["ada_b"] = np.ascontiguousarray(inp["ada_b"][:, j * 1536:(j + 1) * 1536])
        m["ropeg"] = np.ascontiguousarray(ropeg[j * 256:(j + 1) * 256])
        m["ropem"] = np.ascontiguousarray(ropem[j * 256:(j + 1) * 256])
        m["cst"] = _build_cst(inp, b, j)
        m["xp"] = np.ascontiguousarray(inp["x_prompt"][core * 4:(core + 1) * 4].reshape(1024, D))
        m["xs"] = np.ascontiguousarray(inp["x_sample"][b, j * 256:(j + 1) * 256])
        m["c_ckv"] = np.ascontiguousarray(inp["cache_mla_ckv"][b])
        m["c_kpe"] = np.ascontiguousarray(inp["cache_mla_kpe"][b])
        m["c_gla"] = np.ascontiguousarray(inp["state_gla"][b].reshape(2, 2, 256, 128))
        m["c_gk"] = np.ascontiguousarray(inp["cache_gqa_k"][b].reshape(2, 512, 256))
        m["c_gv"] = np.ascontiguousarray(inp["cache_gqa_v"][b].reshape(2, 512, 256))
        in_maps.append(m)
    return in_maps


def kernel(**inputs):
    nc, kb = _get_nc()
    in_maps = make_in_maps(inputs)
    res = run_bass_kernel_spmd(nc, in_maps, core_ids=list(range(8)))
    R = res.results
    y_prompt = np.concatenate([R[c]["yp"].reshape(4, 256, D) for c in range(8)], axis=0)
    y_sample = np.stack([np.concatenate([R[4 * b + j]["ys"] for j in range(4)], axis=0) for b in range(2)], axis=0)
    new_ckv = np.concatenate([R[c]["o_ckv"] for c in range(8)], axis=0)
    new_kpe = np.concatenate([R[c]["o_kpe"] for c in range(8)], axis=0)
    new_gla = np.concatenate([R[c]["o_gla"].reshape(4, 2, 2, 4, 64, 128) for c in range(8)], axis=0)
    new_k = np.concatenate([R[c]["o_gk"].reshape(4, 2, 256, 4, 64) for c in range(8)], axis=0)
    new_v = np.concatenate([R[c]["o_gv"].reshape(4, 2, 256, 4, 64) for c in range(8)], axis=0)
    outs = (y_prompt, y_sample, new_ckv, new_kpe, new_gla, new_k, new_v)
    return tuple(np.ascontiguousarray(o, dtype=np.float32) for o in outs)
```

```python
import bisect
import contextlib
import numpy as np
import concourse.bass as bass
import concourse.mybir as mybir
from concourse.bass_utils import run_bass_kernel_spmd

F32 = mybir.dt.float32
BF16 = mybir.dt.bfloat16
AF = mybir.ActivationFunctionType
ALU = mybir.AluOpType
AX = mybir.AxisListType

ENGS = ("pe", "act", "dve", "pool", "sp")
RAW, WAR, WAW = 1, 2, 4
EPS = 1e-6
D = 1024
FH = 2816
ARENA_BYTES = 152 * 1024
NTOK = 1280
NTILE = 10


class Prog:
    def __init__(self):
        self.ops = []
        self.last_w = {}
        self.readers = {}

    def op(self, eng, fn, r=(), w=(), dq=None, inc=16):
        i = len(self.ops)
        deps = {}
        psr = [k for k in r if isinstance(k, tuple) and k and k[0] == "ps"]
        if psr:
            r = [k for k in r if not (isinstance(k, tuple) and k and k[0] == "ps")]
            w = list(w) + [k for k in psr if k not in w]
            for k in psr:
                lw = self.last_w.get(k)
                if lw is not None:
                    deps[lw] = deps.get(lw, 0) | RAW
        for k in r:
            lw = self.last_w.get(k)
            if lw is not None:
                deps[lw] = deps.get(lw, 0) | RAW
        for k in w:
            lw = self.last_w.get(k)
            if lw is not None:
                deps[lw] = deps.get(lw, 0) | WAW
            for rd in self.readers.get(k, ()):
                if rd != i:
                    deps[rd] = deps.get(rd, 0) | WAR
        for k in r:
            self.readers.setdefault(k, []).append(i)
        for k in w:
            self.last_w[k] = i
            self.readers[k] = []
        deps.pop(i, None)
        self.ops.append(dict(eng=eng, fn=fn, deps=deps, dq=dq, bar=None, inc=inc))
        return i

    def barrier(self):
        first = len(self.ops)
        for e in ENGS:
            self.ops.append(dict(eng=e, fn="drain", deps={}, dq=None, bar=("sig", first)))
        sig_ids = list(range(first, first + len(ENGS)))
        for e in ENGS:
            self.ops.append(dict(eng=e, fn="nop", deps={s: RAW for s in sig_ids}, dq=None, bar=("wait", first)))
        iscc = lambda k: isinstance(k, str) and k.startswith("cc")
        self.last_w = {k: v for k, v in self.last_w.items() if iscc(k)}
        self.readers = {k: v for k, v in self.readers.items() if iscc(k)}

    def emit(self, nc, es):
        ops = self.ops
        n = len(ops)
        needed = [False] * n
        for i, o in enumerate(ops):
            kept = []
            for d, kind in o["deps"].items():
                od = ops[d]
                if od["dq"] is None and o["dq"] is None and od["eng"] == o["eng"] and o["bar"] is None:
                    if o["eng"] == "pe":
                        continue
                kept.append(d)
                needed[d] = True
            o["kdeps"] = kept
        eng_sem = {e: es.enter_context(nc.semaphore("sem_" + e)) for e in ENGS}
        dq_keys = []
        seen = set()
        for o in ops:
            if o["dq"] is not None and o["dq"] not in seen:
                seen.add(o["dq"])
                dq_keys.append(o["dq"])
        dq_sem = {k: es.enter_context(nc.semaphore("dq_%d" % j)) for j, k in enumerate(dq_keys)}
        dq_idx = {k: [] for k in dq_keys}
        dq_cum = {k: [0] for k in dq_keys}
        eng_cnt = {e: 0 for e in ENGS}

        def dq_before(k, i):
            return dq_cum[k][bisect.bisect_left(dq_idx[k], i)]

        for i, o in enumerate(ops):
            if o["dq"] is not None:
                dq_idx[o["dq"]].append(i)
                dq_cum[o["dq"]].append(dq_cum[o["dq"]][-1] + o["inc"])
                o["sig"] = ("dq", o["dq"])
            elif needed[i] or (o["bar"] is not None and o["bar"][0] == "sig"):
                eng_cnt[o["eng"]] += 1
                o["sig"] = ("eng", o["eng"], eng_cnt[o["eng"]])
            else:
                o["sig"] = None
        per_eng = {e: [] for e in ENGS}
        for i, o in enumerate(ops):
            per_eng[o["eng"]].append(i)
        self.n_sems = len(ENGS) + len(dq_keys)
        self.counts = {e: len(per_eng[e]) for e in ENGS}

        def run(e, h):
            waited = {}
            for i in per_eng[e]:
                o = ops[i]
                waits = {}
                for d in o["kdeps"]:
                    od = ops[d]
                    if od["dq"] is not None:
                        k = od["dq"]
                        cnt = dq_before(k, i)
                        key = ("dq", k)
                        waits[key] = max(waits.get(key, 0), cnt)
                    else:
                        key = ("eng", od["eng"])
                        waits[key] = max(waits.get(key, 0), od["sig"][2])
                if o["bar"] is not None and o["bar"][0] == "sig" and e == "sp":
                    for k in dq_keys:
                        if isinstance(k, str) and k.startswith("cc"):
                            continue
                        cnt = dq_before(k, i)
                        if cnt:
                            waits[("dq", k)] = cnt
                for key, v in waits.items():
                    if waited.get(key, 0) >= v:
                        continue
                    waited[key] = v
                    sem = dq_sem[key[1]] if key[0] == "dq" else eng_sem[key[1]]
                    h.wait_ge(sem, v)
                if o["fn"] == "drain":
                    inst = h.nop() if e == "sp" else h.drain()
                elif o["fn"] == "nop":
                    inst = None
                else:
                    inst = o["fn"](h)
                s = o["sig"]
                if s is not None:
                    if s[0] == "dq":
                        inst.then_inc(dq_sem[s[1]], o["inc"])
                    else:
                        inst.then_inc(eng_sem[s[1]], 1)
            if e == "sp":
                for k in dq_keys:
                    h.wait_ge(dq_sem[k], dq_cum[k][-1])

        with nc.Block() as block:
            @block.tensor
            def _(h):
                run("pe", h)

            @block.scalar
            def _(h):
                run("act", h)

            @block.vector
            def _(h):
                run("dve", h)

            @block.gpsimd
            def _(h):
                run("pool", h)

            @block.sync
            def _(h):
                run("sp", h)


class Arena:
    def __init__(self, t, nbytes):
        self.t = t
        self.nbytes = nbytes
        self.off = 0
        self.peak = 0

    def reset(self, off=0):
        self.off = off

    def alloc(self, free_shape, dtype, parts=128):
        n = int(np.prod(free_shape))
        esz = 4 if dtype == F32 else 2
        sz = (n * esz + 31) // 32 * 32
        o = self.off
        assert o + sz <= self.nbytes, ("arena overflow", o, sz, self.nbytes)
        self.off = o + sz
        self.peak = max(self.peak, self.off)
        ap = self.t[0:parts, o // 2:(o + n * esz) // 2]
        if dtype == F32:
            ap = ap.bitcast(F32)
        fs = list(free_shape)
        if len(fs) == 2:
            ap = ap.rearrange("p (a b) -> p a b", a=fs[0], b=fs[1])
        elif len(fs) == 3:
            ap = ap.rearrange("p (a b c) -> p a b c", a=fs[0], b=fs[1], c=fs[2])
        elif len(fs) == 4:
            ap = ap.rearrange("p (a b c d) -> p a b c d", a=fs[0], b=fs[1], c=fs[2], d=fs[3])
        return ap


def _cst_layout():
    off = {}
    o = 0

    def add(name, n):
        nonlocal o
        off[name] = (o, n)
        o += n

    add("ident", 128)
    add("trim0", 128)
    add("trim1", 128)
    add("tris0", 128)
    add("tris1", 128)
    add("mask0", 128)
    add("mask1", 128)
    add("rm", 2)
    add("cond", 16)
    add("sel", 8)
    add("gmix", 32)
    add("gffn", 32)
    for i in range(2):
        add("gq%d" % i, 64)
        add("gk%d" % i, 64)
        add("gout%d" % i, 1)
        add("gqn%d" % i, 384)
        add("gkvn%d" % i, 256)
        add("gq96%d" % i, 96)
        add("gk96%d" % i, 96)
    return off, o


CST_OFF, NCST = _cst_layout()


class KB:
    def __init__(self, debug=(), stages=None):
        self.stages = set(stages) if stages is not None else {"adaln", "ffn", "mixc", "mixab", "P", "S"}
        self.debug = set(debug)
        self.dbg_outs = {}

    def mm(self, out, lhsT, rhs, start, stop, r, w, **kw):
        self.P.op("pe", lambda h: h.matmul(out, lhsT, rhs, start=start, stop=stop, **kw), r=r, w=w)

    def tr(self, out, in_, ident, r, w):
        self.P.op("pe", lambda h: h.transpose(out, in_, ident), r=r, w=w)

    def act(self, out, in_, func, r, w, **kw):
        self.P.op("act", lambda h: h.activation(out, in_, func, **kw), r=r, w=w)

    def tt(self, eng, out, a, b, op, r, w):
        self.P.op(eng, lambda h: h.tensor_tensor(out, a, b, op), r=r, w=w)

    def stt(self, out, in0, scalar, in1, op0, op1, r, w):
        self.P.op("dve", lambda h: h.scalar_tensor_tensor(out, in0, scalar, in1, op0, op1), r=r, w=w)

    def cp(self, eng, out, in_, r, w):
        if eng == "act":
            self.P.op("act", lambda h: h.copy(out, in_), r=r, w=w)
        else:
            self.P.op(eng, lambda h: h.tensor_copy(out, in_), r=r, w=w)

    def recip(self, out, in_, r, w):
        self.P.op("dve", lambda h: h.reciprocal(out, in_), r=r, w=w)

    def red(self, out, in_, r, w):
        self.P.op("dve", lambda h: h.tensor_reduce(out, in_, AX.X, ALU.add), r=r, w=w)

    def memset(self, eng, ap, val, w):
        self.P.op(eng, lambda h: h.memset(ap, val), w=w)

    def dma(self, q, out, in_, r, w, dq, **kw):
        self.P.op(q, lambda h: h.dma_start(out=out, in_=in_, **kw), r=r, w=w, dq=dq)

    def allgather(self, key, r, w):
        src, dst = self.CC[key]
        name = "cc_%s_%s" % key
        self.P.op("pool", lambda h: h.collective_compute("AllGather", ALU.bypass, replica_groups=[[0, 1, 2, 3], [4, 5, 6, 7]],
                                                         ins=[src.ap().opt()], outs=[dst.ap().opt()]),
                  r=r, w=w, dq=name, inc=1)

    def bank(self, pool):
        lst, idx = self.pools[pool]
        b = lst[idx % len(lst)]
        self.pools[pool][1] = idx + 1
        return b

    def rstd(self, out, ss, n, r, w):
        self.act(out, ss, AF.Ln, r=r, w=w, bias=EPS, scale=1.0 / n)
        self.act(out, out, AF.Exp, r=w, w=w, scale=-0.5)

    def cst(self, name, parts=128):
        o, n = CST_OFF[name]
        return self.cst_t[0:parts, o:o + n]

    def dbg(self, name, ap, r, shape):
        if name not in self.debug:
            return
        t = self.nc.dram_tensor("dbg_" + name, list(shape), ap.dtype if hasattr(ap, "dtype") else F32, kind="ExternalOutput").ap()
        self.dbg_outs[name] = shape
        self.dma("sp", t, ap, r=r, w=[], dq=("dbg", name))

    def build(self):
        nc = bass.Bass("TRN2", target_bir_lowering=False)
        self.nc = nc
        self.P = Prog()

        def din(name, shape):
            return nc.dram_tensor(name, list(shape), F32, kind="ExternalInput").ap()

        def dout(name, shape):
            return nc.dram_tensor(name, list(shape), F32, kind="ExternalOutput").ap()

        I = {}
        I["cst"] = din("cst", [128, NCST])
        I["xp"] = din("xp", [1024, D])
        I["xs"] = din("xs", [256, D])
        I["ada_w"] = din("ada_w", [4, D, 1536])
        I["ada_b"] = din("ada_b", [4, 1536])
        I["ffn_w_in"] = din("ffn_w_in", [4, D, 2 * FH])
        I["ffn_w_out"] = din("ffn_w_out", [4, FH, D])
        I["ab_w_in"] = din("ab_w_in", [2, D, 2240])
        I["ab_w_out"] = din("ab_w_out", [2, D, D])
        I["a_w2"] = din("a_w2", [2, 2, 16, 256])
        I["a_b"] = din("a_b", [2, 2, 256])
        I["w_qb"] = din("w_qb", [2, 384, 768])
        I["w_kvb"] = din("w_kvb", [2, 256, 1024])
        I["gqa_w_in"] = din("gqa_w_in", [2, D, 1536])
        I["gqa_w_out"] = din("gqa_w_out", [2, D, D])
        I["c_ckv"] = din("c_ckv", [2, 512, 256])
        I["c_kpe"] = din("c_kpe", [2, 512, 32])
        I["c_gla"] = din("c_gla", [2, 2, 256, 128])
        I["c_gk"] = din("c_gk", [2, 512, 256])
        I["c_gv"] = din("c_gv", [2, 512, 256])
        I["ropeg"] = din("ropeg", [256, 2, 32])
        I["ropem"] = din("ropem", [256, 2, 16])
        I["ropem_all"] = din("ropem_all", [1024, 2, 16])
        O = {}
        O["yp"] = dout("yp", [1024, D])
        O["ys"] = dout("ys", [256, D])
        O["o_ckv"] = dout("o_ckv", [4, 2, 256, 256])
        O["o_kpe"] = dout("o_kpe", [4, 2, 256, 32])
        O["o_gla"] = dout("o_gla", [4, 2, 2, 256, 128])
        O["o_gk"] = dout("o_gk", [4, 2, 256, 256])
        O["o_gv"] = dout("o_gv", [4, 2, 256, 256])
        self.I, self.O = I, O
        self.CC = {}
        self.CC[("a", "a")] = (nc.dram_tensor("ccs_a", [3, 6144], F32), nc.dram_tensor("ccd_a", [12, 6144], F32))
        for l in range(4):
            if l % 2 == 0:
                self.CC[(l, "g")] = (nc.dram_tensor("ccs_g%d" % l, [512, 129], F32), nc.dram_tensor("ccd_g%d" % l, [2048, 129], F32))
                self.CC[(l, "m")] = (nc.dram_tensor("ccs_m%d" % l, [256, 288], F32), nc.dram_tensor("ccd_m%d" % l, [1024, 288], F32))
            else:
                self.CC[(l, "c")] = (nc.dram_tensor("ccs_c%d" % l, [256, 512], F32), nc.dram_tensor("ccd_c%d" % l, [1024, 512], F32))

        with contextlib.ExitStack() as es:
            self.xT = es.enter_context(nc.sbuf_tensor("xT", [128, 8, NTOK], F32))
            self.cst_t = es.enter_context(nc.sbuf_tensor("cst_sb", [128, NCST], F32))
            small = es.enter_context(nc.sbuf_tensor("small", [128, 4 * 48 * 2 + 96 + 8], F32))
            cbf = es.enter_context(nc.sbuf_tensor("cbf", [128, 256 + 16], BF16))
            arena_t = es.enter_context(nc.sbuf_tensor("arena", [128, ARENA_BYTES // 2], BF16))
            self.A = Arena(arena_t, ARENA_BYTES)
            self.ps = [es.enter_context(nc.psum_tensor("ps%d" % i, [128, 512], F32)) for i in range(8)]
            self.psb = [p.bitcast(BF16) for p in self.ps]
            self.pools = {"mm": [[0, 1, 2, 3], 0], "acc": [[4, 5], 0], "tr": [[6, 7], 0]}
            self.modt = small[:, 0:384].rearrange("p (l c k) -> p l c k", l=4, c=48, k=2)
            self.lsc = small[:, 384:480].rearrange("p (g a b) -> p g a b", g=2, a=6, b=8)
            self.eps_c = small[:, 480:481]
            self.one_c = small[:, 481:482]
            self.ident_b = cbf[:, 0:128]
            self.ones_b = cbf[:, 128:256]
            self.sc_b = cbf[:, 256:272].rearrange("p (k c) -> p k c", k=8, c=2)
            self.ident_f = self.cst("ident")

            self.prologue()
            if "adaln" in self.stages:
                self.adaln_all()
            self.run_all()
            self.P.emit(nc, es)
        return nc

    def prologue(self):
        self.dma("sp", self.cst_t[:, :], self.I["cst"], r=[], w=["cst"], dq="cst")
        self.memset("dve", self.eps_c, EPS, w=["small_c"])
        self.memset("dve", self.one_c, 1.0, w=["small_c"])
        self.memset("dve", self.ones_b, 1.0, w=["ones"])
        self.cp("dve", self.ident_b, self.ident_f, r=["cst"], w=["identb"])
        cond = self.cst("cond").rearrange("p (k c) -> p k c", k=8, c=2)
        self.act(self.sc_b, cond, AF.Silu, r=["cst"], w=["scb"])

    def adaln_all(self):
        A = self.A
        A.reset()
        slots = [A.alloc([8, 512], BF16) for _ in range(3)]
        mq = A.alloc([6144], F32)
        mtok = A.alloc([6144], F32)
        rm = self.cst("rm", parts=3)
        cc_src, cc_dst = self.CC[("a", "a")]
        k = 0
        for l in range(4):
            self.dma("sp", mq[2:3, l * 1536:(l + 1) * 1536], self.I["ada_b"][l:l + 1, :], r=[], w=[("mq", "b")], dq="mtokb")
            for j in range(3):
                s = k % 3
                k += 1
                src = self.I["ada_w"][l, :, j * 512:(j + 1) * 512].rearrange("(k p) n -> p k n", p=128)
                self.dma("pool", slots[s], src, r=[], w=[("adw", s)], dq=("adw", s))
                b = self.bank("mm")
                for kc in range(8):
                    self.mm(self.ps[b][0:2, :], self.sc_b[:, kc, :], slots[s][:, kc, :], kc == 0, kc == 7,
                            r=[("adw", s), "scb"], w=[("ps", b)])
                c0 = l * 1536 + j * 512
                self.cp("act", mq[0:2, c0:c0 + 512], self.ps[b][0:2, :], r=[("ps", b)], w=[("mq", l, j)])
        self.dma("sp", cc_src.ap(), mq[0:3, :], r=[("mq", "b")] + [("mq", l, j) for l in range(4) for j in range(3)],
                 w=["cc_src_a"], dq="bnc_a")
        self.allgather(("a", "a"), r=["cc_src_a"], w=["cc_dst_a"])
        dview = cc_dst.ap().rearrange("(j r) (l c) -> r l j c", r=3, l=4)
        for l in range(4):
            self.dma("sp", mtok[0:3, :].rearrange("r (j c) -> r j c", j=4), dview[:, l, :, :], r=["cc_dst_a"], w=["mtok"], dq="mtok")
            b = self.bank("mm")
            for c in range(48):
                self.mm(self.ps[b][:, 2 * c:2 * c + 2], mtok[0:3, c * 128:(c + 1) * 128], rm, True, True,
                        r=["mtok", "cst"], w=[("ps", b)])
            self.cp("dve", self.modt[:, l, :, :], self.ps[b][:, 0:96].rearrange("p (c k) -> p c k", c=48, k=2),
                    r=[("ps", b)], w=["modt"])
        self.P.barrier()

    def layer_scalars(self, l):
        gmix = self.cst("gmix").rearrange("p (l c) -> p l c", l=4, c=8)[:, l, :]
        gffn = self.cst("gffn").rearrange("p (l c) -> p l c", l=4, c=8)[:, l, :]
        for col in range(2):
            mv = self.modt[:, l, :, col]
            L = self.lsc[:, col, :, :]
            self.stt(L[:, 0, :], mv[:, 8:16], 1.0, gmix, ALU.add, ALU.mult, r=["modt", "cst"], w=["lsc"])
            self.cp("dve", L[:, 1, :], mv[:, 0:8], r=["modt"], w=["lsc"])
            self.cp("dve", L[:, 2, :], mv[:, 16:24], r=["modt"], w=["lsc"])
            self.stt(L[:, 3, :], mv[:, 32:40], 1.0, gffn, ALU.add, ALU.mult, r=["modt", "cst"], w=["lsc"])
            self.cp("dve", L[:, 4, :], mv[:, 24:32], r=["modt"], w=["lsc"])
            self.cp("dve", L[:, 5, :], mv[:, 40:48], r=["modt"], w=["lsc"])

    def alloc_norm_tmp(self):
        A = self.A
        self.nm_sq = A.alloc([8, 128], BF16)
        self.nm_rstd = A.alloc([128], F32)
        self.nm_tmp = A.alloc([8, 128], F32)

    def normmod(self, t, which, dst, dstkey):
        xv = self.xT[:, :, t * 128:(t + 1) * 128]
        xk = [("xT", t, c) for c in range(8)]
        grp = 0 if t < 8 else 1
        G = self.lsc[:, grp, 3 * which, :]
        SH = self.lsc[:, grp, 3 * which + 1, :]
        self.tt("pool", self.nm_tmp, xv, G.unsqueeze(2).broadcast_to([128, 8, 128]), ALU.mult, r=xk + ["lsc"], w=["nm_tmp"])
        self.act(self.nm_sq, xv, AF.Square, r=xk, w=["nm_sq"])
        b = self.bank("tr")
        for c in range(8):
            self.mm(self.ps[b][:, 0:128], self.ones_b, self.nm_sq[:, c, :], c == 0, c == 7, r=["nm_sq", "ones"], w=[("ps", b)])
        self.rstd(self.nm_rstd, self.ps[b][:, 0:128], D, r=[("ps", b)], w=["nm_rstd"])
        self.tt("dve", self.nm_tmp, self.nm_tmp, self.nm_rstd.unsqueeze(1).broadcast_to([128, 8, 128]), ALU.mult,
                r=["nm_tmp", "nm_rstd"], w=["nm_tmp"])
        self.tt("dve", dst, self.nm_tmp, SH.unsqueeze(2).broadcast_to([128, 8, 128]), ALU.add,
                r=["nm_tmp", "lsc"], w=[dstkey])

    def run_all(self):
        self.seqs = [dict(tiles=[2 * s, 2 * s + 1], ctx=False, rope=False, bidx=s) for s in range(4)]
        self.sseg = dict(tiles=[8, 9], ctx=True, rope=True, bidx=None)
        A = self.A
        A.reset()
        xin = [A.alloc([1024], F32) for _ in range(2)]
        for t in range(NTILE):
            s = t % 2
            src = self.I["xp"][t * 128:(t + 1) * 128, :] if t < 8 else self.I["xs"][(t - 8) * 128:(t - 7) * 128, :]
            self.dma("sp", xin[s], src, r=[], w=[("xin", s)], dq=("xin", s))
            for hb in range(2):
                b = self.bank("tr")
                for cc in range(4):
                    c = hb * 4 + cc
                    self.tr(self.ps[b][:, cc * 128:(cc + 1) * 128], xin[s][:, c * 128:(c + 1) * 128], self.ident_f,
                            r=[("xin", s), "cst"], w=[("ps", b)])
                self.cp("dve" if hb else "act", self.xT[:, hb * 4:hb * 4 + 4, t * 128:(t + 1) * 128],
                        self.ps[b][:, :].rearrange("p (a b) -> p a b", a=4),
                        r=[("ps", b)], w=[("xT", t, hb * 4 + cc) for cc in range(4)])
        self.P.barrier()
        for l in range(4):
            self.layer_scalars(l)
            if l % 2 == 0:
                if "mixab" in self.stages:
                    self.mixer_ab(l, l // 2)
            else:
                if "mixc" in self.stages:
                    self.mixer_c(l, l // 2)
            self.P.barrier()
            if "ffn" in self.stages:
                self.ffn(l)
            self.P.barrier()
        A.reset()
        yo = [A.alloc([1024], F32) for _ in range(2)]
        for t in range(NTILE):
            s = t % 2
            for hb in range(2):
                b = self.bank("tr")
                for cc in range(4):
                    c = hb * 4 + cc
                    self.tr(self.ps[b][:, cc * 128:(cc + 1) * 128], self.xT[:, c, t * 128:(t + 1) * 128], self.ident_f,
                            r=[("xT", t, c), "cst"], w=[("ps", b)])
                self.cp("dve" if hb else "act", yo[s][:, hb * 512:(hb + 1) * 512], self.ps[b][:, :],
                        r=[("ps", b)], w=[("yo", s, hb)])
            dst = self.O["yp"][t * 128:(t + 1) * 128, :] if t < 8 else self.O["ys"][(t - 8) * 128:(t - 7) * 128, :]
            self.dma("sp", dst, yo[s], r=[("yo", s, 0), ("yo", s, 1)], w=[], dq=("yo", s))

    def ffn(self, l):
        A = self.A
        A.reset()
        hT = A.alloc([8, NTOK], BF16)
        actT = A.alloc([22, NTOK], BF16)
        wi = [A.alloc([8, 2, 256], BF16) for _ in range(2)]
        wo = [A.alloc([11, 1024], BF16) for _ in range(2)]
        sg = [A.alloc([512], F32) for _ in range(2)]
        self.alloc_norm_tmp()
        W1 = self.I["ffn_w_in"]
        W2 = self.I["ffn_w_out"]
        TB = [(0, 512, 0, [0, 1, 2, 3]), (512, 512, 0, [4, 5, 6, 7]), (1024, 256, 1, [8, 9])]
        NS = len(wi)

        def load_wi(j2):
            s = j2 % NS
            for gu in range(2):
                c0 = gu * FH + j2 * 256
                src = W1[l, :, c0:c0 + 256].rearrange("(k p) n -> p k n", p=128)
                self.dma("pool", wi[s][:, :, gu, :], src, r=[], w=[("wi", s, gu)], dq=("wi", s))

        def load_wo(hf):
            src = W2[l, hf * 1408:(hf + 1) * 1408, :].rearrange("(j p) n -> p j n", p=128)
            self.dma("pool", wo[hf], src, r=[], w=[("wo", hf)], dq=("wo", hf))

        for j2 in range(NS):
            load_wi(j2)
        for t in range(NTILE):
            self.normmod(t, 1, hT[:, :, t * 128:(t + 1) * 128], ("hT", t))
        load_wo(0)
        load_wo(1)
        k = 0
        for j2 in range(11):
            s = j2 % NS
            for (t0, tn, grp, tl) in TB:
                hk = [("hT", q) for q in tl]
                for hf in range(2):
                    j = j2 * 2 + hf
                    bg = self.bank("mm")
                    for kc in range(8):
                        self.mm(self.ps[bg][:, 0:tn], wi[s][:, kc, 0, hf * 128:(hf + 1) * 128], hT[:, kc, t0:t0 + tn],
                                kc == 0, kc == 7, r=[("wi", s, 0)] + hk, w=[("ps", bg)])
                    bu = self.bank("mm")
                    for kc in range(8):
                        self.mm(self.ps[bu][:, 0:tn], wi[s][:, kc, 1, hf * 128:(hf + 1) * 128], hT[:, kc, t0:t0 + tn],
                                kc == 0, kc == 7, r=[("wi", s, 1)] + hk, w=[("ps", bu)])
                    sgi = k % 2
                    k += 1
                    self.act(sg[sgi][:, 0:tn], self.ps[bg][:, 0:tn], AF.Silu, r=[("ps", bg)], w=[("sg", sgi)])
                    self.tt("dve", actT[:, j, t0:t0 + tn], sg[sgi][:, 0:tn], self.ps[bu][:, 0:tn], ALU.mult,
                            r=[("sg", sgi), ("ps", bu)], w=[("act", j, t0)])
            if j2 + NS < 11:
                load_wi(j2 + NS)
        for hf in range(2):
            for c in range(8):
                for (t0, tn, grp, tl) in TB:
                    gate = self.lsc[:, grp, 5, :]
                    b = self.bank("mm")
                    for jj in range(11):
                        self.mm(self.ps[b][:, 0:tn], wo[hf][:, jj, c * 128:(c + 1) * 128], actT[:, hf * 11 + jj, t0:t0 + tn],
                                jj == 0, jj == 10, r=[("wo", hf), ("act", hf * 11 + jj, t0)], w=[("ps", b)])
                    xv = self.xT[:, c, t0:t0 + tn]
                    xk = [("xT", q, c) for q in tl]
                    self.stt(xv, self.ps[b][:, 0:tn], gate[:, c:c + 1], xv, ALU.mult, ALU.add, r=[("ps", b), "lsc"] + xk, w=xk)

    def mixer_residual(self, t, banks):
        grp = 0 if t < 8 else 1
        gate = self.lsc[:, grp, 2, :]
        tmp = self.nm_tmp
        for hb in range(2):
            b = banks[hb]
            pv = self.ps[b][:, :].rearrange("p (a b) -> p a b", a=4)
            tv = tmp[:, hb * 4:hb * 4 + 4, :]
            self.tt("dve", tv, pv, gate[:, hb * 4:hb * 4 + 4].unsqueeze(2).broadcast_to([128, 4, 128]), ALU.mult,
                    r=[("ps", b), "lsc"], w=["nm_tmp"])
            xv = self.xT[:, hb * 4:hb * 4 + 4, t * 128:(t + 1) * 128]
            xk = [("xT", t, hb * 4 + q) for q in range(4)]
            self.tt("pool", xv, xv, tv, ALU.add, r=["nm_tmp"] + xk, w=xk)

    def out_proj_pair(self, t0, w_out, rhs_of, rkeys):
        grp = 0 if t0 < 8 else 1
        gate = self.lsc[:, grp, 2, :]
        banks = []
        for bi in range(4):
            b = self.bank("mm")
            banks.append(b)
            for cc in range(2):
                c = 2 * bi + cc
                for kc in range(8):
                    self.mm(self.ps[b][:, cc * 256:(cc + 1) * 256], w_out[:, kc, c * 128:(c + 1) * 128], rhs_of(kc), kc == 0, kc == 7,
                            r=["w_out"] + rkeys, w=[("ps", b)])
        tmpv = self.nm_tmp.rearrange("p c t -> p (c t)")
        for bi in range(4):
            b = banks[bi]
            pv = self.ps[b][:, :].rearrange("p (a b) -> p a b", a=2)
            tv = tmpv[:, (bi % 2) * 512:(bi % 2 + 1) * 512].rearrange("p (a b) -> p a b", a=2)
            self.tt("dve", tv, pv, gate[:, 2 * bi:2 * bi + 2].unsqueeze(2).broadcast_to([128, 2, 256]), ALU.mult,
                    r=[("ps", b), "lsc"], w=["nm_tmp"])
            xv = self.xT[:, 2 * bi:2 * bi + 2, t0 * 128:(t0 + 2) * 128]
            xk = [("xT", t0 + q, 2 * bi + cc) for q in range(2) for cc in range(2)]
            self.tt("pool", xv, xv, tv, ALU.add, r=["nm_tmp"] + xk, w=xk)

    def rope(self, xv, H, Q, cs, tmps, r, w, keyp):
        x1 = xv[:, :, :, 0, :]
        x2 = xv[:, :, :, 1, :]
        c = cs[:, 0, :].rearrange("p (a q) -> p a q", a=2, q=Q).unsqueeze(1).broadcast_to([128, H, 2, Q])
        s = cs[:, 1, :].rearrange("p (a q) -> p a q", a=2, q=Q).unsqueeze(1).broadcast_to([128, H, 2, Q])
        t1, t2, t3, t4 = tmps
        k1, k2, k3, k4 = [(keyp, i) for i in range(4)]
        self.tt("dve", t1, x1, c, ALU.mult, r=r, w=[k1])
        self.tt("pool", t2, x2, s, ALU.mult, r=r, w=[k2])
        self.tt("dve", t3, x1, s, ALU.mult, r=r, w=[k3])
        self.tt("pool", t4, x2, c, ALU.mult, r=r, w=[k4])
        self.tt("dve", x1, t1, t2, ALU.subtract, r=[k1, k2, k3], w=w)
        self.tt("dve", x2, t3, t4, ALU.add, r=[k3, k4], w=w)

    def attention(self, KT, kparts, kt_list, QTv, nq, VA_of, scale, Pt, out_of, kr, qr, vr, ow):
        ob = self.bank("acc")
        first, last = kt_list[0], kt_list[-1]
        npt = len(Pt)

        def pv(st, pi):
            for j in range(nq):
                self.mm(self.ps[ob][:, j * 65:(j + 1) * 65], Pt[pi][:, j, :], VA_of(st), (st == first and j == 0), st == last,
                        r=[("Pt", pi)] + vr, w=[("ps", ob)], skip_group_check=True)

        pend = None
        for st in kt_list:
            sb = self.bank("mm")
            self.mm(self.ps[sb][:, 0:nq * 128], KT(st), QTv, True, True, r=kr + qr, w=[("ps", sb)])
            pi = self.pt_i % npt
            self.pt_i += 1
            self.act(Pt[pi][:, 0:nq, :], self.ps[sb][:, 0:nq * 128].rearrange("p (a b) -> p a b", a=nq), AF.Exp,
                     r=[("ps", sb)], w=[("Pt", pi)], scale=scale)
            if pend is not None:
                pv(*pend)
            pend = (st, pi)
        pv(*pend)
        ov = self.ps[ob][:, 0:nq * 65].rearrange("p (a b) -> p a b", a=nq)
        rd = self.at_rden
        self.recip(rd[:, 0:nq], ov[:, :, 64], r=[("ps", ob)], w=["at_rden"])
        for j in range(nq):
            dst, dk = out_of(j)
            self.P.op("dve", lambda h, j=j, dst=dst, rd=rd, ov=ov: h.tensor_scalar(dst, ov[:, j, 0:64], rd[:, j:j + 1], None, ALU.mult),
                      r=[("ps", ob), "at_rden"], w=[dk])

    def mixer_c(self, l, i):
        A = self.A
        A.reset()
        I, O = self.I, self.O
        w_in = A.alloc([8, 1536], BF16)
        w_out = A.alloc([8, 1024], BF16)
        hT = A.alloc([8, 256], BF16)
        KT = A.alloc([4, 1536], BF16)
        VA = A.alloc([12, 4, 65], BF16)
        self.alloc_norm_tmp()
        kvf = A.alloc([512], F32)
        sq = A.alloc([1024], F32)
        qn = A.alloc([1024], F32)
        st16 = A.alloc([16], F32)
        knb = A.alloc([256], BF16)
        qnb = A.alloc([1024], BF16)
        QT = A.alloc([16, 128], BF16)
        Pt = [A.alloc([4, 128], BF16) for _ in range(3)]
        ob = A.alloc([1024], BF16)
        OT = A.alloc([8, 256], BF16)
        self.at_rden = A.alloc([4], F32)
        rtm = [A.alloc([16, 2, 16], F32) for _ in range(4)]
        ropeg = A.alloc([2, 2, 32], F32)
        ck = A.alloc([4, 256], F32)
        cv = A.alloc([4, 256], F32)
        ckb = A.alloc([4, 256], BF16)
        kg = [A.alloc([512], F32) for _ in range(2)]
        self.pt_i = 0
        gq = self.cst("gq%d" % i)
        gk = self.cst("gk%d" % i)
        for kh in range(2):
            self.dma("pool", w_in[:, kh * 4:(kh + 1) * 4, :],
                     I["gqa_w_in"][i, kh * 512:(kh + 1) * 512, :].rearrange("(k p) n -> p k n", p=128), r=[], w=[("w_in", kh)], dq="w_in")
        self.dma("pool", w_out, I["gqa_w_out"][i].rearrange("(k p) n -> p k n", p=128), r=[], w=["w_out"], dq="w_out")
        self.memset("pool", VA[:, :, :, 64:65], 1.0, w=["VA1"])
        wk = [("w_in", 0), ("w_in", 1)]
        self.dma("sp", ropeg, I["ropeg"].rearrange("(t p) c q -> p t c q", p=128), r=[], w=["ropeg"], dq="ropeg")
        cc_src, cc_dst = self.CC[(l, "c")]

        def put_keys(knb_ap, kt, rk):
            b = self.bank("tr")
            for g in range(4):
                self.tr(self.psb[b][0:64, g * 128:(g + 1) * 128], knb_ap[:, g * 64:(g + 1) * 64], self.ident_b,
                        r=rk + ["identb"], w=[("ps", b)])
            self.cp("dve", KT[0:64, :, kt * 128:(kt + 1) * 128], self.psb[b][0:64, 0:512].rearrange("p (a b) -> p a b", a=4),
                    r=[("ps", b)], w=[("KT", kt)])

        def kv_side(t, n, rope_n):
            self.normmod(t, 0, hT[:, :, n * 128:(n + 1) * 128], ("hT", n))
            b = self.bank("mm")
            for kc in range(8):
                self.mm(self.ps[b][:, :], hT[:, kc, n * 128:(n + 1) * 128], w_in[:, kc, 1024:1536], kc == 0, kc == 7,
                        r=[("hT", n)] + wk, w=[("ps", b)])
            self.cp("act", kvf, self.ps[b][:, :], r=[("ps", b)], w=["kvf"])
            self.act(sq[:, 0:256], self.ps[b][:, 0:256], AF.Square, r=[("ps", b)], w=["sq"])
            self.red(st16[:, 0:4], sq[:, 0:256].rearrange("p (g d) -> p g d", g=4), r=["sq"], w=["st16"])
            self.rstd(st16[:, 0:4], st16[:, 0:4], 64, r=["st16"], w=["st16"])
            kv3 = kvf[:, 0:256].rearrange("p (g d) -> p g d", g=4)
            self.tt("dve", kv3, kv3, st16[:, 0:4].unsqueeze(2).broadcast_to([128, 4, 64]), ALU.mult, r=["kvf", "st16"], w=["kvf"])
            self.tt("dve", kv3, kv3, gk.unsqueeze(1).broadcast_to([128, 4, 64]), ALU.mult, r=["kvf", "cst"], w=["kvf"])
            if rope_n is not None:
                self.rope(kvf[:, 0:256].rearrange("p (h a b q) -> p h a b q", h=4, a=2, b=2, q=16), 4, 16, ropeg[:, rope_n, :, :],
                          [x[:, 0:4, :, :] for x in rtm], r=["kvf", "ropeg"], w=["kvf"], keyp="rtm")

        def q_attn_out(t, n, rope_n, kt_list):
            kkeys = [("KT", kt) for kt in kt_list]
            vkeys = [("VA", kt) for kt in kt_list] + ["VA1"]
            qb = []
            for bk in range(2):
                b = self.bank("mm")
                qb.append(b)
                for kc in range(8):
                    self.mm(self.ps[b][:, :], hT[:, kc, n * 128:(n + 1) * 128], w_in[:, kc, bk * 512:(bk + 1) * 512], kc == 0, kc == 7,
                            r=[("hT", n)] + wk, w=[("ps", b)])
                self.act(sq[:, bk * 512:(bk + 1) * 512], self.ps[b][:, :], AF.Square, r=[("ps", b)], w=["sq"])
            self.red(st16, sq.rearrange("p (g d) -> p g d", g=16), r=["sq"], w=["st16"])
            self.rstd(st16, st16, 64, r=["st16"], w=["st16"])
            for bk in range(2):
                self.tt("dve", qn[:, bk * 512:(bk + 1) * 512].rearrange("p (g d) -> p g d", g=8),
                        self.ps[qb[bk]][:, :].rearrange("p (g d) -> p g d", g=8),
                        st16[:, bk * 8:(bk + 1) * 8].unsqueeze(2).broadcast_to([128, 8, 64]), ALU.mult,
                        r=[("ps", qb[bk]), "st16"], w=["qn"])
            qn3 = qn.rearrange("p (g d) -> p g d", g=16)
            self.tt("dve", qn3, qn3, gq.unsqueeze(1).broadcast_to([128, 16, 64]), ALU.mult, r=["qn", "cst"], w=["qn"])
            if rope_n is not None:
                self.rope(qn.rearrange("p (h a b q) -> p h a b q", h=16, a=2, b=2, q=16), 16, 16, ropeg[:, rope_n, :, :], rtm,
                          r=["qn", "ropeg"], w=["qn"], keyp="rtm")
            self.cp("act", qnb, qn, r=["qn"], w=["qnb"])
            for hb in range(2):
                b = self.bank("tr")
                for hh in range(8):
                    h_ = hb * 8 + hh
                    self.tr(self.psb[b][0:64, hh * 128:(hh + 1) * 128], qnb[:, h_ * 64:(h_ + 1) * 64], self.ident_b,
                            r=["qnb", "identb"], w=[("ps", b)])
                self.cp("dve" if hb else "act", QT[0:64, hb * 8:(hb + 1) * 8, :],
                        self.psb[b][0:64, :].rearrange("p (a b) -> p a b", a=8), r=[("ps", b)], w=[("QT", hb)])
            for g in range(4):
                self.attention(
                    KT=lambda st, g=g: KT[0:64, g, st * 128:(st + 1) * 128], kparts=64, kt_list=kt_list,
                    QTv=QT[0:64, 4 * g:4 * g + 4, :], nq=4,
                    VA_of=lambda st, g=g: VA[:, st, g, :], scale=0.125, Pt=Pt,
                    out_of=lambda j, g=g: (ob[:, (4 * g + j) * 64:(4 * g + j + 1) * 64], ("ob", g)),
                    kr=kkeys, qr=[("QT", g // 2)], vr=vkeys, ow=None)
            b = self.bank("tr")
            for c in range(8):
                self.tr(self.psb[b][:, c * 128:(c + 1) * 128], ob[:, c * 128:(c + 1) * 128], self.ident_b,
                        r=[("ob", c // 2), "identb"], w=[("ps", b)])
            self.cp("act", OT[:, :, n * 128:(n + 1) * 128], self.psb[b][:, :].rearrange("p (a b) -> p a b", a=8), r=[("ps", b)], w=[("OT", n)])
            if n == 1:
                self.out_proj_pair(t - 1, w_out, lambda kc: OT[:, kc, :], [("OT", 0), ("OT", 1)])

        cck = "cc_src_c%d" % l
        ccd = "cc_dst_c%d" % l
        for n, t in enumerate(self.sseg["tiles"]):
            kv_side(t, n, n)
            self.dma("sp", cc_src[n * 128:(n + 1) * 128, :], kvf, r=["kvf"], w=[cck], dq="ccb_c")
        self.allgather((l, "c"), r=[cck], w=[ccd])
        for sq_ in self.seqs:
            tiles = sq_["tiles"]
            bi = sq_["bidx"]
            for n, t in enumerate(tiles):
                kv_side(t, n, None)
                self.dma("sp", O["o_gk"][bi, i, n * 128:(n + 1) * 128, :], kvf[:, 0:256], r=["kvf"], w=[], dq="kvf_o")
                self.dma("sp", O["o_gv"][bi, i, n * 128:(n + 1) * 128, :], kvf[:, 256:512], r=["kvf"], w=[], dq="kvf_o")
                self.cp("act", knb, kvf[:, 0:256], r=["kvf"], w=["knb"])
                put_keys(knb, n, ["knb"])
                self.cp("pool", VA[:, n, :, 0:64], kvf[:, 256:512].rearrange("p (g d) -> p g d", g=4), r=["kvf"], w=[("VA", n)])
            for n, t in enumerate(tiles):
                q_attn_out(t, n, None, [0, 1])
        self.dma("sp", ck, I["c_gk"][i].rearrange("(t p) n -> p t n", p=128), r=[], w=["ck"], dq="ck")
        self.dma("sp", cv, I["c_gv"][i].rearrange("(t p) n -> p t n", p=128), r=[], w=["cv"], dq="cv")
        self.cp("act", ckb, ck, r=["ck"], w=["ckb"])
        for kt in range(4):
            put_keys(ckb[:, kt, :], kt, ["ckb"])
            self.cp("pool", VA[:, kt, :, 0:64], cv[:, kt, :].rearrange("p (g d) -> p g d", g=4), r=["cv"], w=[("VA", kt)])
        for k in range(8):
            kgk = kg[k % 2]
            kk = ("kg", k % 2)
            self.dma("sp", kgk, cc_dst[k * 128:(k + 1) * 128, :], r=[ccd], w=[kk], dq=kk)
            self.cp("act", knb, kgk[:, 0:256], r=[kk], w=["knb"])
            put_keys(knb, 4 + k, ["knb"])
            self.cp("pool", VA[:, 4 + k, :, 0:64], kgk[:, 256:512].rearrange("p (g d) -> p g d", g=4), r=[kk], w=[("VA", 4 + k)])
        for n, t in enumerate(self.sseg["tiles"]):
            self.normmod(t, 0, hT[:, :, n * 128:(n + 1) * 128], ("hT", n))
            q_attn_out(t, n, n, list(range(12)))

    def mixer_ab(self, l, i):
        A = self.A
        A.reset()
        I, O = self.I, self.O
        w_in = A.alloc([8, 2240], BF16)
        w_out = A.alloc([8, 1024], BF16)
        OG = A.alloc([4, 256], BF16)
        hTm = A.alloc([8, 128], BF16)
        hk = "hTm"
        self.alloc_norm_tmp()
        self.at_rden = A.alloc([4], F32)
        Pt = [A.alloc([4, 128], BF16) for _ in range(3)]
        self.pt_i = 0
        SP = dict(qlT=[A.alloc([2, 256], BF16) for _ in range(2)], oT=A.alloc([4, 256], F32), rsT=A.alloc([4, 256], BF16),
                  etot=A.alloc([4, 2], F32), cqb=A.alloc([2, 384], BF16), aseg=A.alloc([4], F32))
        base_off = A.off
        for kh in range(2):
            for ch in range(2):
                self.dma("pool", w_in[:, kh * 4:(kh + 1) * 4, ch * 1120:(ch + 1) * 1120],
                         I["ab_w_in"][i, kh * 512:(kh + 1) * 512, ch * 1120:(ch + 1) * 1120].rearrange("(k p) n -> p k n", p=128),
                         r=[], w=[("w_in", kh, ch)], dq="w_in")
        self.dma("pool", w_out, I["ab_w_out"][i].rearrange("(k p) n -> p k n", p=128), r=[], w=["w_out"], dq="w_out")
        wk = [("w_in", 0, 0), ("w_in", 0, 1), ("w_in", 1, 0), ("w_in", 1, 1)]
        gout = self.cst("gout%d" % i)
        gqn = self.cst("gqn%d" % i)
        gkvn = self.cst("gkvn%d" % i)
        gq96 = self.cst("gq96%d" % i)
        gk96 = self.cst("gk96%d" % i)
        sel = self.cst("sel")
        trim = [self.cst("trim0"), self.cst("trim1")]
        tris = [self.cst("tris0"), self.cst("tris1")]
        mask = [self.cst("mask0"), self.cst("mask1")]
        ccg_src, ccg_dst = self.CC[(l, "g")]
        ccm_src, ccm_dst = self.CC[(l, "m")]

        def g_alloc(NT, persist=None):
            T = NT * 128
            B = {}
            if persist is None:
                B["qlT"] = [A.alloc([2, T], BF16) for _ in range(2)]
                B["oT"] = A.alloc([4, T], F32)
                B["rsT"] = A.alloc([4, T], BF16)
                B["etot"] = A.alloc([4, NT], F32)
            else:
                for k in ("qlT", "oT", "rsT", "etot"):
                    B[k] = persist[k]
            B["klT"] = [A.alloc([2, T], BF16) for _ in range(2)]
            B["kst"] = [A.alloc([NT, 256], BF16) for _ in range(2)]
            B["vtk"] = A.alloc([NT, 512], BF16)
            B["Sst"] = [[A.alloc([128], F32) for _ in range(2)] for _ in range(2)]
            B["Sbf"] = [[A.alloc([128], BF16) for _ in range(2)] for _ in range(2)]
            B["alT"] = A.alloc([128], F32)
            B["lsp"] = A.alloc([512], F32)
            B["Eb"] = A.alloc([4, 128], F32)
            B["Enb"] = A.alloc([4, 128], F32)
            B["Ed2"] = A.alloc([512], F32)
            B["ATm"] = [A.alloc([128], BF16) for _ in range(4)]
            B["aw2"] = A.alloc([512], F32)
            self.memset("dve", B["alT"][32:33, :], 1.0, w=["alT1"])
            aw2 = B["aw2"]
            self.memset("dve", aw2[0:33, :], 0.0, w=["aw2"])
            for z in range(2):
                self.dma("sp", aw2[16 * z:16 * z + 16, z * 256:(z + 1) * 256], I["a_w2"][i, z], r=[], w=["aw2"], dq="aw2")
            self.dma("sp", aw2[32:33, :], I["a_b"][i:i + 1].rearrange("o z n -> o (z n)"), r=[], w=["aw2"], dq="aw2")
            return B

        def g_prep(tiles, B):
            qlT, klT, kst, vtk, rsT, etot = B["qlT"], B["klT"], B["kst"], B["vtk"], B["rsT"], B["etot"]
            alT, lsp, Eb, Enb, Ed2, aw2 = B["alT"], B["lsp"], B["Eb"], B["Enb"], B["Ed2"], B["aw2"]
            for n, t in enumerate(tiles):
                nc_ = slice(n * 128, (n + 1) * 128)
                h = hTm
                self.normmod(t, 0, h, hk)
                bqk = self.bank("mm")
                for ch in range(4):
                    for kc in range(8):
                        self.mm(self.ps[bqk][:, ch * 128:(ch + 1) * 128], w_in[:, kc, ch * 128:(ch + 1) * 128], h[:, kc, :], kc == 0, kc == 7,
                                r=[hk] + wk, w=[("ps", bqk)])
                br = self.bank("mm")
                for ch in range(4):
                    for kc in range(8):
                        self.mm(self.ps[br][:, ch * 128:(ch + 1) * 128], w_in[:, kc, 1024 + ch * 128:1024 + (ch + 1) * 128], h[:, kc, :], kc == 0, kc == 7,
                                r=[hk] + wk, w=[("ps", br)])
                self.act(rsT[:, :, nc_], self.ps[br][:, :].rearrange("p (a b) -> p a b", a=4), AF.Silu, r=[("ps", br)], w=[("rsT", n)])
                ba = self.bank("tr")
                for kc in range(8):
                    self.mm(self.ps[ba][0:32, 0:128], w_in[:, kc, 1536:1568], h[:, kc, :], kc == 0, kc == 7, r=[hk] + wk, w=[("ps", ba)])
                self.cp("act", alT[0:32, :], self.ps[ba][0:32, 0:128], r=[("ps", ba)], w=["alT"])
                bkv = self.bank("mm")
                for kc in range(8):
                    self.mm(self.ps[bkv][:, :], h[:, kc, :], w_in[:, kc, 256:768], kc == 0, kc == 7, r=[hk] + wk, w=[("ps", bkv)])
                bv2 = self.bank("mm")
                for kc in range(8):
                    self.mm(self.ps[bv2][:, 0:256], h[:, kc, :], w_in[:, kc, 768:1024], kc == 0, kc == 7, r=[hk] + wk, w=[("ps", bv2)])
                self.cp("act", vtk[:, n, 0:256], self.ps[bkv][:, 256:512], r=[("ps", bkv)], w=[("vtk", n, 0)])
                self.cp("act", vtk[:, n, 256:512], self.ps[bv2][:, 0:256], r=[("ps", bv2)], w=[("vtk", n, 1)])
                bl = self.bank("tr")
                self.mm(self.ps[bl][:, :], alT[0:33, :], aw2[0:33, :], True, True, r=["alT", "alT1", "aw2"], w=[("ps", bl)])
                self.act(lsp, self.ps[bl][:, :], AF.Exp, r=[("ps", bl)], w=["lsp"], scale=-1.0)
                self.act(lsp, lsp, AF.Ln, r=["lsp"], w=["lsp"], bias=1.0)
                bb = self.bank("tr")
                for z in range(2):
                    for fc in range(2):
                        zf = z * 2 + fc
                        self.mm(self.ps[bb][:, zf * 128:(zf + 1) * 128], lsp[:, zf * 128:(zf + 1) * 128], trim[z], True, True,
                                r=["lsp", "cst"], w=[("ps", bb)])
                bd = self.bank("tr")
                for z in range(2):
                    self.mm(self.ps[bd][:, z * 256:(z + 1) * 256], tris[z], lsp[:, z * 256:(z + 1) * 256], True, True,
                            r=["lsp", "cst"], w=[("ps", bd)])
                pbb = self.ps[bb][:, :].rearrange("p (a b) -> p a b", a=4)
                self.act(Eb, pbb, AF.Exp, r=[("ps", bb)], w=["Eb"])
                self.act(Enb, pbb, AF.Exp, r=[("ps", bb)], w=["Enb"], scale=-1.0)
                self.act(Ed2, self.ps[bd][:, :], AF.Exp, r=[("ps", bd)], w=["Ed2"])
                self.cp("pool", etot[:, 0:2, n], Eb[:, 0:2, 127], r=["Eb"], w=[("etot", n)])
                self.cp("pool", etot[:, 2:4, n], Eb[:, 2:4, 0], r=["Eb"], w=[("etot", n)])
                pqk = self.ps[bqk][:, :].rearrange("p (a b) -> p a b", a=4)
                for z in range(2):
                    self.stt(qlT[z][:, :, nc_], pqk[:, 0:2, :], 0.125, Eb[:, 2 * z:2 * z + 2, :], ALU.mult, ALU.mult,
                             r=[("ps", bqk), "Eb"], w=[("qlT", z, n)])
                    self.tt("dve", klT[z][:, :, nc_], pqk[:, 2:4, :], Enb[:, 2 * z:2 * z + 2, :], ALU.mult,
                            r=[("ps", bqk), "Enb"], w=[("klT", z, n)])
                    self.tt("dve", kst[z][:, n, :], self.ps[bkv][:, 0:256], Ed2[:, z * 256:(z + 1) * 256], ALU.mult,
                            r=[("ps", bkv), "Ed2"], w=[("kst", z, n)])

        def g_scan(NT, B, bidx):
            qlT, klT, kst, vtk, oT, etot, Sst, Sbf, ATm = (B[k] for k in ("qlT", "klT", "kst", "vtk", "oT", "etot", "Sst", "Sbf", "ATm"))
            for z in range(2):
                for fc in range(2):
                    self.memset("pool", Sst[z][fc], 0.0, w=[("S", z, fc)])
                    self.cp("act", Sbf[z][fc], Sst[z][fc], r=[("S", z, fc)], w=[("Sbf", z, fc)])
                order = list(range(NT)) if z == 0 else list(range(NT - 1, -1, -1))
                for n in order:
                    nc_ = slice(n * 128, (n + 1) * 128)
                    bo = self.bank("acc")
                    bats = []
                    for hh in range(4):
                        fc, hp = hh // 2, hh % 2
                        pr = slice(hp * 64, (hp + 1) * 64)
                        bat = self.bank("mm")
                        bats.append(bat)
                        self.mm(self.ps[bat][:, 0:128], klT[z][pr, fc, nc_], qlT[z][pr, fc, nc_], True, True,
                                r=[("klT", z, n), ("qlT", z, n)], w=[("ps", bat)])
                    for hh in range(4):
                        self.tt("dve", ATm[hh], self.ps[bats[hh]][:, 0:128], mask[z], ALU.mult, r=[("ps", bats[hh]), "cst"], w=[("ATm", hh)])
                    for hh in range(4):
                        fc, hp = hh // 2, hh % 2
                        pr = slice(hp * 64, (hp + 1) * 64)
                        self.mm(self.ps[bo][:, hh * 128:(hh + 1) * 128], vtk[:, n, hh * 128:(hh + 1) * 128], ATm[hh], True, False,
                                r=[("vtk", n, 0), ("vtk", n, 1), ("ATm", hh)], w=[("ps", bo)])
                        self.mm(self.ps[bo][:, hh * 128:(hh + 1) * 128], Sbf[z][fc][pr, :], qlT[z][pr, fc, nc_], False, True,
                                r=[("Sbf", z, fc), ("qlT", z, n)], w=[("ps", bo)])
                    pbo = self.ps[bo][:, :].rearrange("p (a b) -> p a b", a=4)
                    if z == 0:
                        self.cp("act", oT[:, :, nc_], pbo, r=[("ps", bo)], w=[("oT", n)])
                    else:
                        self.tt("dve", oT[:, :, nc_], oT[:, :, nc_], pbo, ALU.add, r=[("ps", bo), ("oT", n)], w=[("oT", n)])
                    bus = []
                    for fc in range(2):
                        bu = self.bank("mm")
                        bus.append(bu)
                        for hp in range(2):
                            hh = fc * 2 + hp
                            self.mm(self.ps[bu][hp * 64:(hp + 1) * 64, 0:128], kst[z][:, n, hh * 64:(hh + 1) * 64],
                                    vtk[:, n, hh * 128:(hh + 1) * 128], True, True,
                                    r=[("kst", z, n), ("vtk", n, 0), ("vtk", n, 1)], w=[("ps", bu)])
                    for fc in range(2):
                        self.stt(Sst[z][fc], Sst[z][fc], etot[:, z * 2 + fc, n:n + 1], self.ps[bus[fc]][:, 0:128], ALU.mult, ALU.add,
                                 r=[("S", z, fc), ("etot", n), ("ps", bus[fc])], w=[("S", z, fc)])
                    for fc in range(2):
                        self.cp("act", Sbf[z][fc], Sst[z][fc], r=[("S", z, fc)], w=[("Sbf", z, fc)])
                if bidx is not None:
                    for fc in range(2):
                        self.dma("sp", O["o_gla"][bidx, i, z, fc * 128:(fc + 1) * 128, :], Sst[z][fc],
                                 r=[("S", z, fc)], w=[], dq=("S", z, fc))

        def g_out_alloc():
            return (A.alloc([4, 128], BF16), A.alloc([4, 128], F32), A.alloc([4, 128], F32))

        def g_out(tiles, oT, rsT, tmps=None):
            osq, orst, otmp = tmps if tmps is not None else g_out_alloc()
            for n, t in enumerate(tiles):
                nc_ = slice(n * 128, (n + 1) * 128)
                self.act(osq, oT[:, :, nc_], AF.Square, r=[("oT", n)], w=["osq"])
                b = self.bank("tr")
                self.mm(self.ps[b][:, :], self.ones_b, osq.rearrange("p a b -> p (a b)"), True, True, r=["osq", "ones"], w=[("ps", b)])
                self.rstd(orst, self.ps[b][:, :].rearrange("p (a b) -> p a b", a=4), 128, r=[("ps", b)], w=["orst"])
                self.tt("dve", otmp, oT[:, :, nc_], orst, ALU.mult, r=[("oT", n), "orst"], w=["otmp"])
                self.stt(OG[:, :, nc_], otmp, gout, rsT[:, :, nc_], ALU.mult, ALU.mult,
                         r=["otmp", "cst", ("rsT", n)], w=[("OG", n)])

        def m_alloc(NK, rope=True):
            M = {}
            M["w_qb"] = A.alloc([3, 768], BF16)
            M["w_kvb"] = A.alloc([2, 1024], BF16)
            self.dma("pool", M["w_qb"], I["w_qb"][i].rearrange("(k p) n -> p k n", p=128), r=[], w=["w_qb"], dq="w_qb")
            self.dma("pool", M["w_kvb"], I["w_kvb"][i].rearrange("(k p) n -> p k n", p=128), r=[], w=["w_kvb"], dq="w_kvb")
            M["KTm"] = A.alloc([8, NK * 128], BF16)
            M["VAm"] = A.alloc([NK, 8, 65], BF16)
            M["QTm"] = A.alloc([8, 256], BF16)
            M["omb"] = A.alloc([2, 512], BF16)
            M["OM"] = A.alloc([4, 256], BF16)
            M["cb"] = A.alloc([256], BF16)
            M["cT"] = A.alloc([2, 128], BF16)
            M["kc96"] = A.alloc([8, 96], F32)
            M["knb"] = A.alloc([8, 96], BF16)
            M["st8"] = A.alloc([8], F32)
            M["cqT"] = A.alloc([3, 128], BF16)
            M["rtm"] = [A.alloc([8, 2, 8], F32) for _ in range(4)] if rope else None
            self.memset("pool", M["VAm"][:, :, :, 64:65], 1.0, w=["VA1"])
            return M

        def own_alloc():
            W = {}
            W["sq96"] = A.alloc([8, 96], F32)
            W["ckvf"] = A.alloc([256], F32)
            W["ckvn"] = A.alloc([256], F32)
            W["kpe"] = A.alloc([32], F32)
            W["st1"] = A.alloc([2], F32)
            return W

        def norm96(M, sq96, gain, rope_cs):
            kc96, knb, st8, rtm = M["kc96"], M["knb"], M["st8"], M["rtm"]
            self.act(sq96, kc96, AF.Square, r=["kc96"], w=["sq96"])
            self.red(st8, sq96, r=["sq96"], w=["st8"])
            self.rstd(st8, st8, 96, r=["st8"], w=["st8"])
            self.tt("dve", kc96, kc96, st8.unsqueeze(2).broadcast_to([128, 8, 96]), ALU.mult, r=["kc96", "st8"], w=["kc96"])
            self.tt("dve", kc96, kc96, gain.unsqueeze(1).broadcast_to([128, 8, 96]), ALU.mult, r=["kc96", "cst"], w=["kc96"])
            if rope_cs is not None:
                cs, csk = rope_cs
                self.rope(kc96[:, :, 64:96].rearrange("p h (a b q) -> p h a b q", a=2, b=2, q=8), 8, 8, cs, rtm,
                          r=["kc96", csk], w=["kc96"], keyp="rtm")
            self.cp("act", knb, kc96, r=["kc96"], w=["knb"])

        def kside(M, sq96, ckvn_ap, ckvn_k, kpe_ap, kpe_k, kt, rope_cs):
            cb, cT, kc96, knb, KTm, VAm, w_kvb = M["cb"], M["cT"], M["kc96"], M["knb"], M["KTm"], M["VAm"], M["w_kvb"]
            self.cp("act", cb, ckvn_ap, r=[ckvn_k], w=["cb"])
            b = self.bank("tr")
            for kc in range(2):
                self.tr(self.psb[b][:, kc * 128:(kc + 1) * 128], cb[:, kc * 128:(kc + 1) * 128], self.ident_b, r=["cb", "identb"], w=[("ps", b)])
            self.cp("dve", cT, self.psb[b][:, 0:256].rearrange("p (a b) -> p a b", a=2), r=[("ps", b)], w=["cT"])
            for bk in range(2):
                b = self.bank("mm")
                for kc in range(2):
                    self.mm(self.ps[b][:, :], cT[:, kc, :], w_kvb[:, kc, bk * 512:(bk + 1) * 512], kc == 0, kc == 1,
                            r=["cT", "w_kvb"], w=[("ps", b)])
                pv = self.ps[b][:, :].rearrange("p (h d) -> p h d", h=4)
                self.cp("act", kc96[:, bk * 4:(bk + 1) * 4, 0:64], pv[:, :, 0:64], r=[("ps", b)], w=["kc96"])
                self.cp("dve", VAm[:, kt, bk * 4:(bk + 1) * 4, 0:64], pv[:, :, 64:128], r=[("ps", b)], w=[("VAm", kt)])
            self.cp("pool", kc96[:, :, 64:96], kpe_ap.unsqueeze(1).broadcast_to([128, 8, 32]), r=[kpe_k], w=["kc96"])
            norm96(M, sq96, gk96, rope_cs)
            b = self.bank("tr")
            for hh in range(8):
                self.tr(self.psb[b][0:96, hh * 128:(hh + 1) * 128], knb[:, hh, :], self.ident_b, r=["knb", "identb"], w=[("ps", b)])
            self.cp("dve", KTm[0:96, :, kt * 128:(kt + 1) * 128], self.psb[b][0:96, :].rearrange("p (a b) -> p a b", a=8),
                    r=[("ps", b)], w=[("KTm", kt)])

        def m_own(t, n, W, cqb):
            sq96, ckvf, ckvn, kpe, st1 = W["sq96"], W["ckvf"], W["ckvn"], W["kpe"], W["st1"]
            h = hTm
            self.normmod(t, 0, h, hk)
            b1 = self.bank("mm")
            for kc in range(8):
                self.mm(self.ps[b1][:, :], h[:, kc, :], w_in[:, kc, 1568:2080], kc == 0, kc == 7, r=[hk] + wk, w=[("ps", b1)])
            b2 = self.bank("mm")
            for kc in range(8):
                self.mm(self.ps[b2][:, 0:160], h[:, kc, :], w_in[:, kc, 2080:2240], kc == 0, kc == 7, r=[hk] + wk, w=[("ps", b2)])
            sqf = sq96.rearrange("p a b -> p (a b)")
            self.act(sqf[:, 0:384], self.ps[b1][:, 0:384], AF.Square, r=[("ps", b1)], w=["sq96", "st1"], accum_out=st1[:, 0:1])
            self.rstd(st1[:, 0:1], st1[:, 0:1], 384, r=["st1"], w=["st1"])
            self.stt(cqb[:, n, :], self.ps[b1][:, 0:384], st1[:, 0:1], gqn, ALU.mult, ALU.mult, r=[("ps", b1), "st1", "cst"], w=[("cqb", n)])
            self.cp("act", ckvf[:, 0:128], self.ps[b1][:, 384:512], r=[("ps", b1)], w=["ckvf"])
            self.cp("act", ckvf[:, 128:256], self.ps[b2][:, 0:128], r=[("ps", b2)], w=["ckvf"])
            self.cp("dve", kpe, self.ps[b2][:, 128:160], r=[("ps", b2)], w=["kpe"])
            self.act(sqf[:, 0:256], ckvf, AF.Square, r=["ckvf"], w=["sq96", "st1b"], accum_out=st1[:, 1:2])
            self.rstd(st1[:, 1:2], st1[:, 1:2], 256, r=["st1b"], w=["st1b"])
            self.stt(ckvn, ckvf, st1[:, 1:2], gkvn, ALU.mult, ALU.mult, r=["ckvf", "st1b", "cst"], w=["ckvn"])

        def m_attn(M, sq96, tiles, cqb, rope_q, NK):
            QTm, omb, OM, KTm, VAm, cqT, kc96, knb, w_qb = (M[k] for k in ("QTm", "omb", "OM", "KTm", "VAm", "cqT", "kc96", "knb", "w_qb"))
            kt_list = list(range(NK))
            kkeys = [("KTm", kt) for kt in kt_list]
            vkeys = [("VAm", kt) for kt in kt_list] + ["VA1"]
            nq = len(tiles)
            for n, t in enumerate(tiles):
                b = self.bank("tr")
                for kc in range(3):
                    self.tr(self.psb[b][:, kc * 128:(kc + 1) * 128], cqb[:, n, kc * 128:(kc + 1) * 128], self.ident_b,
                            r=[("cqb", n), "identb"], w=[("ps", b)])
                self.cp("act", cqT, self.psb[b][:, 0:384].rearrange("p (a b) -> p a b", a=3), r=[("ps", b)], w=["cqT"])
                for bk in range(2):
                    b = self.bank("mm")
                    for kc in range(3):
                        self.mm(self.ps[b][:, 0:384], cqT[:, kc, :], w_qb[:, kc, bk * 384:(bk + 1) * 384], kc == 0, kc == 2,
                                r=["cqT", "w_qb"], w=[("ps", b)])
                    self.cp("act", kc96[:, bk * 4:(bk + 1) * 4, :], self.ps[b][:, 0:384].rearrange("p (h d) -> p h d", h=4),
                            r=[("ps", b)], w=["kc96"])
                norm96(M, sq96, gq96, None if rope_q is None else (rope_q[0][:, n, :, :], rope_q[1]))
                b = self.bank("tr")
                for hh in range(8):
                    self.tr(self.psb[b][0:96, hh * 128:(hh + 1) * 128], knb[:, hh, :], self.ident_b, r=["knb", "identb"], w=[("ps", b)])
                self.cp("dve", QTm[0:96, :, n * 128:(n + 1) * 128], self.psb[b][0:96, :].rearrange("p (a b) -> p a b", a=8),
                        r=[("ps", b)], w=[("QTm", n)])
            for hh in range(8):
                self.attention(
                    KT=lambda st, hh=hh: KTm[0:96, hh, st * 128:(st + 1) * 128], kparts=96, kt_list=kt_list,
                    QTv=QTm[0:96, hh, 0:nq * 128], nq=nq,
                    VA_of=lambda st, hh=hh: VAm[:, st, hh, :], scale=float(96 ** -0.5), Pt=Pt,
                    out_of=lambda j, hh=hh: (omb[:, j, hh * 64:(hh + 1) * 64], ("omb", j)),
                    kr=kkeys, qr=[("QTm", j) for j in range(nq)], vr=vkeys, ow=None)
            for n, t in enumerate(tiles):
                b = self.bank("tr")
                for c in range(4):
                    self.tr(self.psb[b][:, c * 128:(c + 1) * 128], omb[:, n, c * 128:(c + 1) * 128], self.ident_b,
                            r=[("omb", n), "identb"], w=[("ps", b)])
                self.cp("act", OM[:, :, n * 128:(n + 1) * 128], self.psb[b][:, 0:512].rearrange("p (a b) -> p a b", a=4),
                        r=[("ps", b)], w=[("OM", n)])
            t0 = tiles[0]
            self.out_proj_pair(t0, w_out,
                               lambda kc: OG[:, kc, :] if kc < 4 else OM[:, kc - 4, 0:256],
                               [("OG", 0), ("OG", 1), ("OM", 0), ("OM", 1)])

        st_ = self.sseg["tiles"]
        A.reset(base_off)
        B = g_alloc(2, persist=SP)
        g_prep(st_, B)
        g_scan(2, B, None)
        ccgs, ccgd = "cc_src_g%d" % l, "cc_dst_g%d" % l
        gsrc = A.alloc([4, 129], F32)
        self.tt("dve", gsrc[:, :, 128], SP["etot"][:, :, 0], SP["etot"][:, :, 1], ALU.mult, r=[("etot", 0), ("etot", 1)], w=["gsrc_a"])
        for z in range(2):
            for fc in range(2):
                zf = z * 2 + fc
                self.cp("act", gsrc[:, zf, 0:128], B["Sst"][z][fc], r=[("S", z, fc)], w=[("gsrc", zf)])
        self.dma("sp", ccg_src.ap().rearrange("(zf p) c -> p zf c", p=128), gsrc,
                 r=["gsrc_a"] + [("gsrc", zf) for zf in range(4)], w=[ccgs], dq="bnc_g")
        self.allgather((l, "g"), r=[ccgs], w=[ccgd])
        self.P.barrier()
        if "x1" in self.stages:
            return
        A.reset(base_off)
        W = own_alloc()
        ccms, ccmd = "cc_src_m%d" % l, "cc_dst_m%d" % l
        for n, t in enumerate(st_):
            m_own(t, n, W, SP["cqb"])
            self.dma("sp", ccm_src[n * 128:(n + 1) * 128, 0:256], W["ckvn"], r=["ckvn"], w=[ccms], dq="bnc_m")
            self.dma("sp", ccm_src[n * 128:(n + 1) * 128, 256:288], W["kpe"], r=["kpe"], w=[ccms], dq="bnc_m")
        self.allgather((l, "m"), r=[ccms], w=[ccmd])
        self.P.barrier()
        if "x2" in self.stages:
            return

        A.reset(base_off)
        B = g_alloc(2)
        GT = g_out_alloc()
        M = m_alloc(2, rope=False)
        W = own_alloc()
        cqb = A.alloc([2, 384], BF16)
        for sq_ in self.seqs:
            tiles = sq_["tiles"]
            bi = sq_["bidx"]
            g_prep(tiles, B)
            g_scan(2, B, bi)
            g_out(tiles, B["oT"], B["rsT"], GT)
            for n, t in enumerate(tiles):
                m_own(t, n, W, cqb)
                self.dma("sp", O["o_ckv"][bi, i, n * 128:(n + 1) * 128, :], W["ckvn"], r=["ckvn"], w=[], dq="ckvn_o")
                self.dma("sp", O["o_kpe"][bi, i, n * 128:(n + 1) * 128, :], W["kpe"], r=["kpe"], w=[], dq="kpe_o")
                kside(M, W["sq96"], W["ckvn"], "ckvn", W["kpe"], "kpe", n, None)
            m_attn(M, W["sq96"], tiles, cqb, None, 2)
        self.P.barrier()

        if "x3" in self.stages:
            return
        A.reset(base_off)
        gU = A.alloc([16, 129], F32)
        self.dma("sp", gU, ccg_dst.ap().rearrange("(rz p) c -> p rz c", p=128), r=[ccgd], w=["gU"], dq="gU")
        Sin = [[A.alloc([128], F32) for _ in range(2)] for _ in range(2)]
        Sib = [[A.alloc([128], BF16) for _ in range(2)] for _ in range(2)]
        tS = A.alloc([128], F32)
        for z in range(2):
            for fc in range(2):
                zf = z * 2 + fc
                S_ = Sin[z][fc]
                sk = ("Sin", z, fc)
                self.dma("sp", S_, I["c_gla"][i, z, fc * 128:(fc + 1) * 128, :], r=[], w=[sk], dq=sk)
                ranks = [0, 1, 2, 3] if z == 0 else [3, 2, 1, 0]
                for k in ranks:
                    gi = k * 4 + zf
                    self.stt(tS, S_, gU[:, gi, 128:129], gU[:, gi, 0:128], ALU.mult, ALU.add, r=[sk, "gU"], w=["tS"])
                    self.tt("dve", tS, tS, S_, ALU.subtract, r=["tS", sk], w=["tS"])
                    self.stt(S_, tS, sel[:, z * 4 + k:z * 4 + k + 1], S_, ALU.mult, ALU.add, r=["tS", sk, "cst"], w=[sk])
                self.cp("act", Sib[z][fc], S_, r=[sk], w=[("Sib", z, fc)])
        if "x5" in self.stages:
            self.P.barrier()
            return
        for z in range(2):
            order = [0, 1] if z == 0 else [1, 0]
            for oi, n in enumerate(order):
                nc_ = slice(n * 128, (n + 1) * 128)
                bh = [self.bank("acc"), self.bank("acc")]
                for hp in range(2):
                    pr = slice(hp * 64, (hp + 1) * 64)
                    for fc in range(2):
                        self.mm(self.ps[bh[hp]][:, fc * 128:(fc + 1) * 128], Sib[z][fc][pr, :], SP["qlT"][z][pr, fc, nc_], True, True,
                                r=[("Sib", z, fc), ("qlT", z, n)], w=[("ps", bh[hp])])
                oview = SP["oT"][:, :, nc_].rearrange("p (f h) t -> p f h t", f=2, h=2)
                for hp in range(2):
                    pb = self.ps[bh[hp]][:, 0:256].rearrange("p (a b) -> p a b", a=2)
                    self.tt("dve", oview[:, :, hp, :], oview[:, :, hp, :], pb, ALU.add, r=[("ps", bh[hp]), ("oT", n)], w=[("oT", n)])
                if oi == 0:
                    for fc in range(2):
                        zf = z * 2 + fc
                        sk = ("Sin", z, fc)
                        self.P.op("dve", lambda h, S_=Sin[z][fc], e=SP["etot"][:, zf, n:n + 1]: h.tensor_scalar(S_, S_, e, None, ALU.mult),
                                  r=[sk, ("etot", n)], w=[sk])
                        self.cp("act", Sib[z][fc], Sin[z][fc], r=[sk], w=[("Sib", z, fc)])
        if "x6" in self.stages:
            self.P.barrier()
            return
        g_out(st_, SP["oT"], SP["rsT"])
        self.P.barrier()
        if "x4" in self.stages:
            return
        A.reset(base_off)
        M = m_alloc(12)
        sq96 = A.alloc([8, 96], F32)
        ropem_all = A.alloc([8, 2, 16], F32)
        ropem_own = A.alloc([2, 2, 16], F32)
        ckp = A.alloc([4, 32], F32)
        cck = A.alloc([256], F32)
        kgm = [A.alloc([288], F32) for _ in range(2)]
        self.dma("sp", ropem_all, I["ropem_all"].rearrange("(t p) c q -> p t c q", p=128), r=[], w=["ropem_all"], dq="ropem")
        self.dma("sp", ropem_own, I["ropem"].rearrange("(t p) c q -> p t c q", p=128), r=[], w=["ropem_own"], dq="ropem")
        self.dma("sp", ckp, I["c_kpe"][i].rearrange("(t p) n -> p t n", p=128), r=[], w=["ckp"], dq="ckp")
        for kt in range(4):
            self.dma("sp", cck, I["c_ckv"][i, kt * 128:(kt + 1) * 128, :], r=[], w=["cck"], dq="cck")
            kside(M, sq96, cck, "cck", ckp[:, kt, :], "ckp", kt, None)
        for k in range(8):
            kg_ = kgm[k % 2]
            kk = ("kgm", k % 2)
            self.dma("sp", kg_, ccm_dst[k * 128:(k + 1) * 128, :], r=[ccmd], w=[kk], dq=kk)
            kside(M, sq96, kg_[:, 0:256], kk, kg_[:, 256:288], kk, 4 + k, (ropem_all[:, k, :, :], "ropem_all"))
        m_attn(M, sq96, st_, SP["cqb"], (ropem_own, "ropem_own"), 12)


def _rope_tables(n_tok, d_rot):
    t = np.arange(n_tok, dtype=np.int32)
    pos = np.stack([t // 64, t % 64], axis=-1).astype(np.float32)
    quarter = d_rot // 4
    inv = np.power(np.float32(10000.0), -np.arange(quarter, dtype=np.float32) / np.float32(quarter)).astype(np.float32)
    ang = pos[:, :, None] * inv
    cos = np.cos(ang).astype(np.float32).reshape(n_tok, 2 * quarter)
    sin = np.sin(ang).astype(np.float32).reshape(n_tok, 2 * quarter)
    return np.ascontiguousarray(np.stack([cos, sin], axis=1))


def _build_cst(inp, b, j):
    c = np.zeros((128, NCST), np.float32)

    def put(name, arr, parts=128):
        o, n = CST_OFF[name]
        c[0:parts, o:o + n] = np.asarray(arr, np.float32).reshape(parts, n)

    s = np.arange(128)[:, None]
    t = np.arange(128)[None, :]
    v = np.float32(-1.0 / 16.0)
    put("ident", np.eye(128))
    put("trim0", (s <= t) * v)
    put("trim1", (s >= t) * v)
    put("tris0", (s > t) * v)
    put("tris1", (s < t) * v)
    put("mask0", (s <= t) * 1.0)
    put("mask1", (s >= t) * 1.0)
    put("rm", np.array([[1, 0], [0, 1], [1, 1]], np.float32), parts=3)
    cond = np.stack([inp["c_ctx"].reshape(8, 128).T, inp["c"][b].reshape(8, 128).T], axis=-1)
    put("cond", cond)
    selv = np.array([1.0 if k < j else 0.0 for k in range(4)] + [1.0 if k > j else 0.0 for k in range(4)], np.float32)
    put("sel", np.broadcast_to(selv[None, :], (128, 8)))
    put("gmix", inp["norm_mix_g"].reshape(4, 8, 128).transpose(2, 0, 1))
    put("gffn", inp["norm_ffn_g"].reshape(4, 8, 128).transpose(2, 0, 1))
    for i in range(2):
        put("gq%d" % i, np.broadcast_to(inp["gqa_qn_g"][i][None, :], (128, 64)))
        put("gk%d" % i, np.broadcast_to(inp["gqa_kn_g"][i][None, :], (128, 64)))
        put("gout%d" % i, inp["gla_out_g"][i].reshape(128, 1))
        put("gqn%d" % i, np.broadcast_to(inp["mla_q_norm_g"][i][None, :], (128, 384)))
        put("gkvn%d" % i, np.broadcast_to(inp["mla_kv_norm_g"][i][None, :], (128, 256)))
        put("gq96%d" % i, np.broadcast_to(inp["mla_qn_g"][i][None, :], (128, 96)))
        put("gk96%d" % i, np.broadcast_to(inp["mla_kn_g"][i][None, :], (128, 96)))
    return c


_NC_CACHE = {}


def _get_nc(debug=(), stages=None):
    key = (tuple(sorted(debug)), None if stages is None else tuple(sorted(stages)))
    if key not in _NC_CACHE:
        kb = KB(debug, stages)
        nc = kb.build()
        _NC_CACHE[key] = (nc, kb)
    return _NC_CACHE[key]


def make_in_maps(inp):
    inp = {k: np.ascontiguousarray(np.asarray(v)) for k, v in inp.items()}
    ropeg = _rope_tables(1024, 64)
    ropem = _rope_tables(1024, 32)
    shared = dict(
        ffn_w_in=inp["ffn_w_in"], ffn_w_out=inp["ffn_w_out"],
        ab_w_in=inp["ab_w_in"], ab_w_out=inp["ab_w_out"], a_w2=inp["gla_a_w2"], a_b=inp["gla_a_b"],
        w_qb=inp["mla_w_qb"], w_kvb=inp["mla_w_kvb"], gqa_w_in=inp["gqa_w_in"], gqa_w_out=inp["gqa_w_out"])
    in_maps = []
    for core in range(8):
        b, j = core // 4, core % 4
        m = dict(shared)
        m["ropem_all"] = ropem
        m["ada_w"] = np.ascontiguousarray(inp["ada_w"][:, :, j * 1536:(j + 1) * 1536])
        m["ada_b"] = np.ascontiguousarray(inp["ada_b"][:, j * 1536:(j + 1) * 1536])
        m["ropeg"] = np.ascontiguousarray(ropeg[j * 256:(j + 1) * 256])
        m["ropem"] = np.ascontiguousarray(ropem[j * 256:(j + 1) * 256])
        m["cst"] = _build_cst(inp, b, j)
        m["xp"] = np.ascontiguousarray(inp["x_prompt"][core * 4:(core + 1) * 4].reshape(1024, D))
        m["xs"] = np.ascontiguousarray(inp["x_sample"][b, j * 256:(j + 1) * 256])
        m["c_ckv"] = np.ascontiguousarray(inp["cache_mla_ckv"][b])
        m["c_kpe"] = np.ascontiguousarray(inp["cache_mla_kpe"][b])
        m["c_gla"] = np.ascontiguousarray(inp["state_gla"][b].reshape(2, 2, 256, 128))
        m["c_gk"] = np.ascontiguousarray(inp["cache_gqa_k"][b].reshape(2, 512, 256))
        m["c_gv"] = np.ascontiguousarray(inp["cache_gqa_v"][b].reshape(2, 512, 256))
        in_maps.append(m)
    return in_maps


def kernel(**inputs):
    nc, kb = _get_nc()
    in_maps = make_in_maps(inputs)
    res = run_bass_kernel_spmd(nc, in_maps, core_ids=list(range(8)))
    R = res.results
    y_prompt = np.concatenate([R[c]["yp"].reshape(4, 256, D) for c in range(8)], axis=0)
    y_sample = np.stack([np.concatenate([R[4 * b + j]["ys"] for j in range(4)], axis=0) for b in range(2)], axis=0)
    new_ckv = np.concatenate([R[c]["o_ckv"] for c in range(8)], axis=0)
    new_kpe = np.concatenate([R[c]["o_kpe"] for c in range(8)], axis=0)
    new_gla = np.concatenate([R[c]["o_gla"].reshape(4, 2, 2, 4, 64, 128) for c in range(8)], axis=0)
    new_k = np.concatenate([R[c]["o_gk"].reshape(4, 2, 256, 4, 64) for c in range(8)], axis=0)
    new_v = np.concatenate([R[c]["o_gv"].reshape(4, 2, 256, 4, 64) for c in range(8)], axis=0)
    outs = (y_prompt, y_sample, new_ckv, new_kpe, new_gla, new_k, new_v)
    return tuple(np.ascontiguousarray(o, dtype=np.float32) for o in outs)
```

```python
import bisect
import contextlib
import numpy as np
import concourse.bass as bass
import concourse.mybir as mybir
from concourse.bass_utils import run_bass_kernel_spmd

F32 = mybir.dt.float32
BF16 = mybir.dt.bfloat16
AF = mybir.ActivationFunctionType
ALU = mybir.AluOpType
AX = mybir.AxisListType

ENGS = ("pe", "act", "dve", "pool", "sp")
RAW, WAR, WAW = 1, 2, 4
EPS = 1e-6
D = 1024
FH = 2816
ARENA_BYTES = 152 * 1024
NTOK = 1280
NTILE = 10


class Prog:
    def __init__(self):
        self.ops = []
        self.last_w = {}
        self.readers = {}

    def op(self, eng, fn, r=(), w=(), dq=None, inc=16):
        i = len(self.ops)
        deps = {}
        psr = [k for k in r if isinstance(k, tuple) and k and k[0] == "ps"]
        if psr:
            r = [k for k in r if not (isinstance(k, tuple) and k and k[0] == "ps")]
            w = list(w) + [k for k in psr if k not in w]
            for k in psr:
                lw = self.last_w.get(k)
                if lw is not None:
                    deps[lw] = deps.get(lw, 0) | RAW
        for k in r:
            lw = self.last_w.get(k)
            if lw is not None:
                deps[lw] = deps.get(lw, 0) | RAW
        for k in w:
            lw = self.last_w.get(k)
            if lw is not None:
                deps[lw] = deps.get(lw, 0) | WAW
            for rd in self.readers.get(k, ()):
                if rd != i:
                    deps[rd] = deps.get(rd, 0) | WAR
        for k in r:
            self.readers.setdefault(k, []).append(i)
        for k in w:
            self.last_w[k] = i
            self.readers[k] = []
        deps.pop(i, None)
        self.ops.append(dict(eng=eng, fn=fn, deps=deps, dq=dq, bar=None, inc=inc))
        return i

    def barrier(self):
        first = len(self.ops)
        for e in ENGS:
            self.ops.append(dict(eng=e, fn="drain", deps={}, dq=None, bar=("sig", first)))
        sig_ids = list(range(first, first + len(ENGS)))
        for e in ENGS:
            self.ops.append(dict(eng=e, fn="nop", deps={s: RAW for s in sig_ids}, dq=None, bar=("wait", first)))
        iscc = lambda k: isinstance(k, str) and k.startswith("cc")
        self.last_w = {k: v for k, v in self.last_w.items() if iscc(k)}
        self.readers = {k: v for k, v in self.readers.items() if iscc(k)}

    def emit(self, nc, es):
        ops = self.ops
        n = len(ops)
        needed = [False] * n
        for i, o in enumerate(ops):
            kept = []
            for d, kind in o["deps"].items():
                od = ops[d]
                if od["dq"] is None and o["dq"] is None and od["eng"] == o["eng"] and o["bar"] is None:
                    if o["eng"] == "pe":
                        continue
                kept.append(d)
                needed[d] = True
            o["kdeps"] = kept
        eng_sem = {e: es.enter_context(nc.semaphore("sem_" + e)) for e in ENGS}
        dq_keys = []
        seen = set()
        for o in ops:
            if o["dq"] is not None and o["dq"] not in seen:
                seen.add(o["dq"])
                dq_keys.append(o["dq"])
        dq_sem = {k: es.enter_context(nc.semaphore("dq_%d" % j)) for j, k in enumerate(dq_keys)}
        dq_idx = {k: [] for k in dq_keys}
        dq_cum = {k: [0] for k in dq_keys}
        eng_cnt = {e: 0 for e in ENGS}

        def dq_before(k, i):
            return dq_cum[k][bisect.bisect_left(dq_idx[k], i)]

        for i, o in enumerate(ops):
            if o["dq"] is not None:
                dq_idx[o["dq"]].append(i)
                dq_cum[o["dq"]].append(dq_cum[o["dq"]][-1] + o["inc"])
                o["sig"] = ("dq", o["dq"])
            elif needed[i] or (o["bar"] is not None and o["bar"][0] == "sig"):
                eng_cnt[o["eng"]] += 1
                o["sig"] = ("eng", o["eng"], eng_cnt[o["eng"]])
            else:
                o["sig"] = None
        per_eng = {e: [] for e in ENGS}
        for i, o in enumerate(ops):
            per_eng[o["eng"]].append(i)
        self.n_sems = len(ENGS) + len(dq_keys)
        self.counts = {e: len(per_eng[e]) for e in ENGS}

        def run(e, h):
            waited = {}
            for i in per_eng[e]:
                o = ops[i]
                waits = {}
                for d in o["kdeps"]:
                    od = ops[d]
                    if od["dq"] is not None:
                        k = od["dq"]
                        cnt = dq_before(k, i)
                        key = ("dq", k)
                        waits[key] = max(waits.get(key, 0), cnt)
                    else:
                        key = ("eng", od["eng"])
                        waits[key] = max(waits.get(key, 0), od["sig"][2])
                if o["bar"] is not None and o["bar"][0] == "sig" and e == "sp":
                    for k in dq_keys:
                        if isinstance(k, str) and k.startswith("cc"):
                            continue
                        cnt = dq_before(k, i)
                        if cnt:
                            waits[("dq", k)] = cnt
                for key, v in waits.items():
                    if waited.get(key, 0) >= v:
                        continue
                    waited[key] = v
                    sem = dq_sem[key[1]] if key[0] == "dq" else eng_sem[key[1]]
                    h.wait_ge(sem, v)
                if o["fn"] == "drain":
                    inst = h.nop() if e == "sp" else h.drain()
                elif o["fn"] == "nop":
                    inst = None
                else:
                    inst = o["fn"](h)
                s = o["sig"]
                if s is not None:
                    if s[0] == "dq":
                        inst.then_inc(dq_sem[s[1]], o["inc"])
                    else:
                        inst.then_inc(eng_sem[s[1]], 1)
            if e == "sp":
                for k in dq_keys:
                    h.wait_ge(dq_sem[k], dq_cum[k][-1])

        with nc.Block() as block:
            @block.tensor
            def _(h):
                run("pe", h)

            @block.scalar
            def _(h):
                run("act", h)

            @block.vector
            def _(h):
                run("dve", h)

            @block.gpsimd
            def _(h):
                run("pool", h)

            @block.sync
            def _(h):
                run("sp", h)


class Arena:
    def __init__(self, t, nbytes):
        self.t = t
        self.nbytes = nbytes
        self.off = 0
        self.peak = 0

    def reset(self, off=0):
        self.off = off

    def alloc(self, free_shape, dtype, parts=128):
        n = int(np.prod(free_shape))
        esz = 4 if dtype == F32 else 2
        sz = (n * esz + 31) // 32 * 32
        o = self.off
        assert o + sz <= self.nbytes, ("arena overflow", o, sz, self.nbytes)
        self.off = o + sz
        self.peak = max(self.peak, self.off)
        ap = self.t[0:parts, o // 2:(o + n * esz) // 2]
        if dtype == F32:
            ap = ap.bitcast(F32)
        fs = list(free_shape)
        if len(fs) == 2:
            ap = ap.rearrange("p (a b) -> p a b", a=fs[0], b=fs[1])
        elif len(fs) == 3:
            ap = ap.rearrange("p (a b c) -> p a b c", a=fs[0], b=fs[1], c=fs[2])
        elif len(fs) == 4:
            ap = ap.rearrange("p (a b c d) -> p a b c d", a=fs[0], b=fs[1], c=fs[2], d=fs[3])
        return ap


def _cst_layout():
    off = {}
    o = 0

    def add(name, n):
        nonlocal o
        off[name] = (o, n)
        o += n

    add("ident", 128)
    add("trim0", 128)
    add("trim1", 128)
    add("tris0", 128)
    add("tris1", 128)
    add("mask0", 128)
    add("mask1", 128)
    add("rm", 2)
    add("cond", 16)
    add("sel", 8)
    add("gmix", 32)
    add("gffn", 32)
    for i in range(2):
        add("gq%d" % i, 64)
        add("gk%d" % i, 64)
        add("gout%d" % i, 1)
        add("gqn%d" % i, 384)
        add("gkvn%d" % i, 256)
        add("gq96%d" % i, 96)
        add("gk96%d" % i, 96)
    return off, o


CST_OFF, NCST = _cst_layout()


class KB:
    def __init__(self, debug=(), stages=None):
        self.stages = set(stages) if stages is not None else {"adaln", "ffn", "mixc", "mixab", "P", "S"}
        self.debug = set(debug)
        self.dbg_outs = {}

    def mm(self, out, lhsT, rhs, start, stop, r, w, **kw):
        self.P.op("pe", lambda h: h.matmul(out, lhsT, rhs, start=start, stop=stop, **kw), r=r, w=w)

    def tr(self, out, in_, ident, r, w):
        self.P.op("pe", lambda h: h.transpose(out, in_, ident), r=r, w=w)

    def act(self, out, in_, func, r, w, **kw):
        self.P.op("act", lambda h: h.activation(out, in_, func, **kw), r=r, w=w)

    def tt(self, eng, out, a, b, op, r, w):
        self.P.op(eng, lambda h: h.tensor_tensor(out, a, b, op), r=r, w=w)

    def stt(self, out, in0, scalar, in1, op0, op1, r, w):
        self.P.op("dve", lambda h: h.scalar_tensor_tensor(out, in0, scalar, in1, op0, op1), r=r, w=w)

    def cp(self, eng, out, in_, r, w):
        if eng == "act":
            self.P.op("act", lambda h: h.copy(out, in_), r=r, w=w)
        else:
            self.P.op(eng, lambda h: h.tensor_copy(out, in_), r=r, w=w)

    def recip(self, out, in_, r, w):
        self.P.op("dve", lambda h: h.reciprocal(out, in_), r=r, w=w)

    def red(self, out, in_, r, w):
        self.P.op("dve", lambda h: h.tensor_reduce(out, in_, AX.X, ALU.add), r=r, w=w)

    def memset(self, eng, ap, val, w):
        self.P.op(eng, lambda h: h.memset(ap, val), w=w)

    def dma(self, q, out, in_, r, w, dq, **kw):
        self.P.op(q, lambda h: h.dma_start(out=out, in_=in_, **kw), r=r, w=w, dq=dq)

    def allgather(self, key, r, w):
        src, dst = self.CC[key]
        name = "cc_%s_%s" % key
        self.P.op("pool", lambda h: h.collective_compute("AllGather", ALU.bypass, replica_groups=[[0, 1, 2, 3], [4, 5, 6, 7]],
                                                         ins=[src.ap().opt()], outs=[dst.ap().opt()]),
                  r=r, w=w, dq=name, inc=1)

    def bank(self, pool):
        lst, idx = self.pools[pool]
        b = lst[idx % len(lst)]
        self.pools[pool][1] = idx + 1
        return b

    def rstd(self, out, ss, n, r, w):
        self.act(out, ss, AF.Ln, r=r, w=w, bias=EPS, scale=1.0 / n)
        self.act(out, out, AF.Exp, r=w, w=w, scale=-0.5)

    def cst(self, name, parts=128):
        o, n = CST_OFF[name]
        return self.cst_t[0:parts, o:o + n]

    def dbg(self, name, ap, r, shape):
        if name not in self.debug:
            return
        t = self.nc.dram_tensor("dbg_" + name, list(shape), ap.dtype if hasattr(ap, "dtype") else F32, kind="ExternalOutput").ap()
        self.dbg_outs[name] = shape
        self.dma("sp", t, ap, r=r, w=[], dq=("dbg", name))

    def build(self):
        nc = bass.Bass("TRN2", target_bir_lowering=False)
        self.nc = nc
        self.P = Prog()

        def din(name, shape):
            return nc.dram_tensor(name, list(shape), F32, kind="ExternalInput").ap()

        def dout(name, shape):
            return nc.dram_tensor(name, list(shape), F32, kind="ExternalOutput").ap()

        I = {}
        I["cst"] = din("cst", [128, NCST])
        I["xp"] = din("xp", [1024, D])
        I["xs"] = din("xs", [256, D])
        I["ada_w"] = din("ada_w", [4, D, 1536])
        I["ada_b"] = din("ada_b", [4, 1536])
        I["ffn_w_in"] = din("ffn_w_in", [4, D, 2 * FH])
        I["ffn_w_out"] = din("ffn_w_out", [4, FH, D])
        I["ab_w_in"] = din("ab_w_in", [2, D, 2240])
        I["ab_w_out"] = din("ab_w_out", [2, D, D])
        I["a_w2"] = din("a_w2", [2, 2, 16, 256])
        I["a_b"] = din("a_b", [2, 2, 256])
        I["w_qb"] = din("w_qb", [2, 384, 768])
        I["w_kvb"] = din("w_kvb", [2, 256, 1024])
        I["gqa_w_in"] = din("gqa_w_in", [2, D, 1536])
        I["gqa_w_out"] = din("gqa_w_out", [2, D, D])
        I["c_ckv"] = din("c_ckv", [2, 512, 256])
        I["c_kpe"] = din("c_kpe", [2, 512, 32])
        I["c_gla"] = din("c_gla", [2, 2, 256, 128])
        I["c_gk"] = din("c_gk", [2, 512, 256])
        I["c_gv"] = din("c_gv", [2, 512, 256])
        I["ropeg"] = din("ropeg", [256, 2, 32])
        I["ropem"] = din("ropem", [256, 2, 16])
        I["ropem_all"] = din("ropem_all", [1024, 2, 16])
        O = {}
        O["yp"] = dout("yp", [1024, D])
        O["ys"] = dout("ys", [256, D])
        O["o_ckv"] = dout("o_ckv", [4, 2, 256, 256])
        O["o_kpe"] = dout("o_kpe", [4, 2, 256, 32])
        O["o_gla"] = dout("o_gla", [4, 2, 2, 256, 128])
        O["o_gk"] = dout("o_gk", [4, 2, 256, 256])
        O["o_gv"] = dout("o_gv", [4, 2, 256, 256])
        self.I, self.O = I, O
        self.CC = {}
        self.CC[("a", "a")] = (nc.dram_tensor("ccs_a", [3, 6144], F32), nc.dram_tensor("ccd_a", [12, 6144], F32))
        for l in range(4):
            if l % 2 == 0:
                self.CC[(l, "g")] = (nc.dram_tensor("ccs_g%d" % l, [512, 129], F32), nc.dram_tensor("ccd_g%d" % l, [2048, 129], F32))
                self.CC[(l, "m")] = (nc.dram_tensor("ccs_m%d" % l, [256, 288], F32), nc.dram_tensor("ccd_m%d" % l, [1024, 288], F32))
            else:
                self.CC[(l, "c")] = (nc.dram_tensor("ccs_c%d" % l, [256, 512], F32), nc.dram_tensor("ccd_c%d" % l, [1024, 512], F32))

        with contextlib.ExitStack() as es:
            self.xT = es.enter_context(nc.sbuf_tensor("xT", [128, 8, NTOK], F32))
            self.cst_t = es.enter_context(nc.sbuf_tensor("cst_sb", [128, NCST], F32))
            small = es.enter_context(nc.sbuf_tensor("small", [128, 4 * 48 * 2 + 96 + 8], F32))
            cbf = es.enter_context(nc.sbuf_tensor("cbf", [128, 256 + 16], BF16))
            arena_t = es.enter_context(nc.sbuf_tensor("arena", [128, ARENA_BYTES // 2], BF16))
            self.A = Arena(arena_t, ARENA_BYTES)
            self.ps = [es.enter_context(nc.psum_tensor("ps%d" % i, [128, 512], F32)) for i in range(8)]
            self.psb = [p.bitcast(BF16) for p in self.ps]
            self.pools = {"mm": [[0, 1, 2, 3], 0], "acc": [[4, 5], 0], "tr": [[6, 7], 0]}
            self.modt = small[:, 0:384].rearrange("p (l c k) -> p l c k", l=4, c=48, k=2)
            self.lsc = small[:, 384:480].rearrange("p (g a b) -> p g a b", g=2, a=6, b=8)
            self.eps_c = small[:, 480:481]
            self.one_c = small[:, 481:482]
            self.ident_b = cbf[:, 0:128]
            self.ones_b = cbf[:, 128:256]
            self.sc_b = cbf[:, 256:272].rearrange("p (k c) -> p k c", k=8, c=2)
            self.ident_f = self.cst("ident")

            self.prologue()
            if "adaln" in self.stages:
                self.adaln_all()
            self.run_all()
            self.P.emit(nc, es)
        return nc

    def prologue(self):
        self.dma("sp", self.cst_t[:, :], self.I["cst"], r=[], w=["cst"], dq="cst")
        self.memset("dve", self.eps_c, EPS, w=["small_c"])
        self.memset("dve", self.one_c, 1.0, w=["small_c"])
        self.memset("dve", self.ones_b, 1.0, w=["ones"])
        self.cp("dve", self.ident_b, self.ident_f, r=["cst"], w=["identb"])
        cond = self.cst("cond").rearrange("p (k c) -> p k c", k=8, c=2)
        self.act(self.sc_b, cond, AF.Silu, r=["cst"], w=["scb"])

    def adaln_all(self):
        A = self.A
        A.reset()
        slots = [A.alloc([8, 512], BF16) for _ in range(3)]
        mq = A.alloc([6144], F32)
        mtok = A.alloc([6144], F32)
        rm = self.cst("rm", parts=3)
        cc_src, cc_dst = self.CC[("a", "a")]
        k = 0
        for l in range(4):
            self.dma("sp", mq[2:3, l * 1536:(l + 1) * 1536], self.I["ada_b"][l:l + 1, :], r=[], w=[("mq", "b")], dq="mtokb")
            for j in range(3):
                s = k % 3
                k += 1
                src = self.I["ada_w"][l, :, j * 512:(j + 1) * 512].rearrange("(k p) n -> p k n", p=128)
                self.dma("pool", slots[s], src, r=[], w=[("adw", s)], dq=("adw", s))
                b = self.bank("mm")
                for kc in range(8):
                    self.mm(self.ps[b][0:2, :], self.sc_b[:, kc, :], slots[s][:, kc, :], kc == 0, kc == 7,
                            r=[("adw", s), "scb"], w=[("ps", b)])
                c0 = l * 1536 + j * 512
                self.cp("act", mq[0:2, c0:c0 + 512], self.ps[b][0:2, :], r=[("ps", b)], w=[("mq", l, j)])
        self.dma("sp", cc_src.ap(), mq[0:3, :], r=[("mq", "b")] + [("mq", l, j) for l in range(4) for j in range(3)],
                 w=["cc_src_a"], dq="bnc_a")
        self.allgather(("a", "a"), r=["cc_src_a"], w=["cc_dst_a"])
        dview = cc_dst.ap().rearrange("(j r) (l c) -> r l j c", r=3, l=4)
        for l in range(4):
            self.dma("sp", mtok[0:3, :].rearrange("r (j c) -> r j c", j=4), dview[:, l, :, :], r=["cc_dst_a"], w=["mtok"], dq="mtok")
            b = self.bank("mm")
            for c in range(48):
                self.mm(self.ps[b][:, 2 * c:2 * c + 2], mtok[0:3, c * 128:(c + 1) * 128], rm, True, True,
                        r=["mtok", "cst"], w=[("ps", b)])
            self.cp("dve", self.modt[:, l, :, :], self.ps[b][:, 0:96].rearrange("p (c k) -> p c k", c=48, k=2),
                    r=[("ps", b)], w=["modt"])
        self.P.barrier()

    def layer_scalars(self, l):
        gmix = self.cst("gmix").rearrange("p (l c) -> p l c", l=4, c=8)[:, l, :]
        gffn = self.cst("gffn").rearrange("p (l c) -> p l c", l=4, c=8)[:, l, :]
        for col in range(2):
            mv = self.modt[:, l, :, col]
            L = self.lsc[:, col, :, :]
            self.stt(L[:, 0, :], mv[:, 8:16], 1.0, gmix, ALU.add, ALU.mult, r=["modt", "cst"], w=["lsc"])
            self.cp("dve", L[:, 1, :], mv[:, 0:8], r=["modt"], w=["lsc"])
            self.cp("dve", L[:, 2, :], mv[:, 16:24], r=["modt"], w=["lsc"])
            self.stt(L[:, 3, :], mv[:, 32:40], 1.0, gffn, ALU.add, ALU.mult, r=["modt", "cst"], w=["lsc"])
            self.cp("dve", L[:, 4, :], mv[:, 24:32], r=["modt"], w=["lsc"])
            self.cp("dve", L[:, 5, :], mv[:, 40:48], r=["modt"], w=["lsc"])

    def alloc_norm_tmp(self):
        A = self.A
        self.nm_sq = A.alloc([8, 128], BF16)
        self.nm_rstd = A.alloc([128], F32)
        self.nm_tmp = A.alloc([8, 128], F32)

    def normmod(self, t, which, dst, dstkey):
        xv = self.xT[:, :, t * 128:(t + 1) * 128]
        xk = [("xT", t, c) for c in range(8)]
        grp = 0 if t < 8 else 1
        G = self.lsc[:, grp, 3 * which, :]
        SH = self.lsc[:, grp, 3 * which + 1, :]
        self.tt("pool", self.nm_tmp, xv, G.unsqueeze(2).broadcast_to([128, 8, 128]), ALU.mult, r=xk + ["lsc"], w=["nm_tmp"])
        self.act(self.nm_sq, xv, AF.Square, r=xk, w=["nm_sq"])
        b = self.bank("tr")
        for c in range(8):
            self.mm(self.ps[b][:, 0:128], self.ones_b, self.nm_sq[:, c, :], c == 0, c == 7, r=["nm_sq", "ones"], w=[("ps", b)])
        self.rstd(self.nm_rstd, self.ps[b][:, 0:128], D, r=[("ps", b)], w=["nm_rstd"])
        self.tt("dve", self.nm_tmp, self.nm_tmp, self.nm_rstd.unsqueeze(1).broadcast_to([128, 8, 128]), ALU.mult,
                r=["nm_tmp", "nm_rstd"], w=["nm_tmp"])
        self.tt("dve", dst, self.nm_tmp, SH.unsqueeze(2).broadcast_to([128, 8, 128]), ALU.add,
                r=["nm_tmp", "lsc"], w=[dstkey])

    def run_all(self):
        self.seqs = [dict(tiles=[2 * s, 2 * s + 1], ctx=False, rope=False, bidx=s) for s in range(4)]
        self.sseg = dict(tiles=[8, 9], ctx=True, rope=True, bidx=None)
        A = self.A
        A.reset()
        xin = [A.alloc([1024], F32) for _ in range(2)]
        for t in range(NTILE):
            s = t % 2
            src = self.I["xp"][t * 128:(t + 1) * 128, :] if t < 8 else self.I["xs"][(t - 8) * 128:(t - 7) * 128, :]
            self.dma("sp", xin[s], src, r=[], w=[("xin", s)], dq=("xin", s))
            for hb in range(2):
                b = self.bank("tr")
                for cc in range(4):
                    c = hb * 4 + cc
                    self.tr(self.ps[b][:, cc * 128:(cc + 1) * 128], xin[s][:, c * 128:(c + 1) * 128], self.ident_f,
                            r=[("xin", s), "cst"], w=[("ps", b)])
                self.cp("dve" if hb else "act", self.xT[:, hb * 4:hb * 4 + 4, t * 128:(t + 1) * 128],
                        self.ps[b][:, :].rearrange("p (a b) -> p a b", a=4),
                        r=[("ps", b)], w=[("xT", t, hb * 4 + cc) for cc in range(4)])
        self.P.barrier()
        for l in range(4):
            self.layer_scalars(l)
            if l % 2 == 0:
                if "mixab" in self.stages:
                    self.mixer_ab(l, l // 2)
            else:
                if "mixc" in self.stages:
                    self.mixer_c(l, l // 2)
            self.P.barrier()
            if "ffn" in self.stages:
                self.ffn(l)
            self.P.barrier()
        A.reset()
        yo = [A.alloc([1024], F32) for _ in range(2)]
        for t in range(NTILE):
            s = t % 2
            for hb in range(2):
                b = self.bank("tr")
                for cc in range(4):
                    c = hb * 4 + cc
                    self.tr(self.ps[b][:, cc * 128:(cc + 1) * 128], self.xT[:, c, t * 128:(t + 1) * 128], self.ident_f,
                            r=[("xT", t, c), "cst"], w=[("ps", b)])
                self.cp("dve" if hb else "act", yo[s][:, hb * 512:(hb + 1) * 512], self.ps[b][:, :],
                        r=[("ps", b)], w=[("yo", s, hb)])
            dst = self.O["yp"][t * 128:(t + 1) * 128, :] if t < 8 else self.O["ys"][(t - 8) * 128:(t - 7) * 128, :]
            self.dma("sp", dst, yo[s], r=[("yo", s, 0), ("yo", s, 1)], w=[], dq=("yo", s))

    def ffn(self, l):
        A = self.A
        A.reset()
        hT = A.alloc([8, NTOK], BF16)
        actT = A.alloc([22, NTOK], BF16)
        wi = [A.alloc([8, 2, 256], BF16) for _ in range(2)]
        wo = [A.alloc([11, 1024], BF16) for _ in range(2)]
        sg = [A.alloc([512], F32) for _ in range(2)]
        self.alloc_norm_tmp()
        W1 = self.I["ffn_w_in"]
        W2 = self.I["ffn_w_out"]
        TB = [(0, 512, 0, [0, 1, 2, 3]), (512, 512, 0, [4, 5, 6, 7]), (1024, 256, 1, [8, 9])]
        NS = len(wi)

        def load_wi(j2):
            s = j2 % NS
            for gu in range(2):
                c0 = gu * FH + j2 * 256
                src = W1[l, :, c0:c0 + 256].rearrange("(k p) n -> p k n", p=128)
                self.dma("pool", wi[s][:, :, gu, :], src, r=[], w=[("wi", s, gu)], dq=("wi", s))

        def load_wo(hf):
            src = W2[l, hf * 1408:(hf + 1) * 1408, :].rearrange("(j p) n -> p j n", p=128)
            self.dma("pool", wo[hf], src, r=[], w=[("wo", hf)], dq=("wo", hf))

        for j2 in range(NS):
            load_wi(j2)
        for t in range(NTILE):
            self.normmod(t, 1, hT[:, :, t * 128:(t + 1) * 128], ("hT", t))
        load_wo(0)
        load_wo(1)
        k = 0
        for j2 in range(11):
            s = j2 % NS
            for (t0, tn, grp, tl) in TB:
                hk = [("hT", q) for q in tl]
                for hf in range(2):
                    j = j2 * 2 + hf
                    bg = self.bank("mm")
                    for kc in range(8):
                        self.mm(self.ps[bg][:, 0:tn], wi[s][:, kc, 0, hf * 128:(hf + 1) * 128], hT[:, kc, t0:t0 + tn],
                                kc == 0, kc == 7, r=[("wi", s, 0)] + hk, w=[("ps", bg)])
                    bu = self.bank("mm")
                    for kc in range(8):
                        self.mm(self.ps[bu][:, 0:tn], wi[s][:, kc, 1, hf * 128:(hf + 1) * 128], hT[:, kc, t0:t0 + tn],
                                kc == 0, kc == 7, r=[("wi", s, 1)] + hk, w=[("ps", bu)])
                    sgi = k % 2
                    k += 1
                    self.act(sg[sgi][:, 0:tn], self.ps[bg][:, 0:tn], AF.Silu, r=[("ps", bg)], w=[("sg", sgi)])
                    self.tt("dve", actT[:, j, t0:t0 + tn], sg[sgi][:, 0:tn], self.ps[bu][:, 0:tn], ALU.mult,
                            r=[("sg", sgi), ("ps", bu)], w=[("act", j, t0)])
            if j2 + NS < 11:
                load_wi(j2 + NS)
        for hf in range(2):
            for c in range(8):
                for (t0, tn, grp, tl) in TB:
                    gate = self.lsc[:, grp, 5, :]
                    b = self.bank("mm")
                    for jj in range(11):
                        self.mm(self.ps[b][:, 0:tn], wo[hf][:, jj, c * 128:(c + 1) * 128], actT[:, hf * 11 + jj, t0:t0 + tn],
                                jj == 0, jj == 10, r=[("wo", hf), ("act", hf * 11 + jj, t0)], w=[("ps", b)])
                    xv = self.xT[:, c, t0:t0 + tn]
                    xk = [("xT", q, c) for q in tl]
                    self.stt(xv, self.ps[b][:, 0:tn], gate[:, c:c + 1], xv, ALU.mult, ALU.add, r=[("ps", b), "lsc"] + xk, w=xk)

    def mixer_residual(self, t, banks):
        grp = 0 if t < 8 else 1
        gate = self.lsc[:, grp, 2, :]
        tmp = self.nm_tmp
        for hb in range(2):
            b = banks[hb]
            pv = self.ps[b][:, :].rearrange("p (a b) -> p a b", a=4)
            tv = tmp[:, hb * 4:hb * 4 + 4, :]
            self.tt("dve", tv, pv, gate[:, hb * 4:hb * 4 + 4].unsqueeze(2).broadcast_to([128, 4, 128]), ALU.mult,
                    r=[("ps", b), "lsc"], w=["nm_tmp"])
            xv = self.xT[:, hb * 4:hb * 4 + 4, t * 128:(t + 1) * 128]
            xk = [("xT", t, hb * 4 + q) for q in range(4)]
            self.tt("pool", xv, xv, tv, ALU.add, r=["nm_tmp"] + xk, w=xk)

    def out_proj_pair(self, t0, w_out, rhs_of, rkeys):
        grp = 0 if t0 < 8 else 1
        gate = self.lsc[:, grp, 2, :]
        banks = []
        for bi in range(4):
            b = self.bank("mm")
            banks.append(b)
            for cc in range(2):
                c = 2 * bi + cc
                for kc in range(8):
                    self.mm(self.ps[b][:, cc * 256:(cc + 1) * 256], w_out[:, kc, c * 128:(c + 1) * 128], rhs_of(kc), kc == 0, kc == 7,
                            r=["w_out"] + rkeys, w=[("ps", b)])
        tmpv = self.nm_tmp.rearrange("p c t -> p (c t)")
        for bi in range(4):
            b = banks[bi]
            pv = self.ps[b][:, :].rearrange("p (a b) -> p a b", a=2)
            tv = tmpv[:, (bi % 2) * 512:(bi % 2 + 1) * 512].rearrange("p (a b) -> p a b", a=2)
            self.tt("dve", tv, pv, gate[:, 2 * bi:2 * bi + 2].unsqueeze(2).broadcast_to([128, 2, 256]), ALU.mult,
                    r=[("ps", b), "lsc"], w=["nm_tmp"])
            xv = self.xT[:, 2 * bi:2 * bi + 2, t0 * 128:(t0 + 2) * 128]
            xk = [("xT", t0 + q, 2 * bi + cc) for q in range(2) for cc in range(2)]
            self.tt("pool", xv, xv, tv, ALU.add, r=["nm_tmp"] + xk, w=xk)

    def rope(self, xv, H, Q, cs, tmps, r, w, keyp):
        x1 = xv[:, :, :, 0, :]
        x2 = xv[:, :, :, 1, :]
        c = cs[:, 0, :].rearrange("p (a q) -> p a q", a=2, q=Q).unsqueeze(1).broadcast_to([128, H, 2, Q])
        s = cs[:, 1, :].rearrange("p (a q) -> p a q", a=2, q=Q).unsqueeze(1).broadcast_to([128, H, 2, Q])
        t1, t2, t3, t4 = tmps
        k1, k2, k3, k4 = [(keyp, i) for i in range(4)]
        self.tt("dve", t1, x1, c, ALU.mult, r=r, w=[k1])
        self.tt("pool", t2, x2, s, ALU.mult, r=r, w=[k2])
        self.tt("dve", t3, x1, s, ALU.mult, r=r, w=[k3])
        self.tt("pool", t4, x2, c, ALU.mult, r=r, w=[k4])
        self.tt("dve", x1, t1, t2, ALU.subtract, r=[k1, k2, k3], w=w)
        self.tt("dve", x2, t3, t4, ALU.add, r=[k3, k4], w=w)

    def attention(self, KT, kparts, kt_list, QTv, nq, VA_of, scale, Pt, out_of, kr, qr, vr, ow):
        ob = self.bank("acc")
        first, last = kt_list[0], kt_list[-1]
        npt = len(Pt)

        def pv(st, pi):
            for j in range(nq):
                self.mm(self.ps[ob][:, j * 65:(j + 1) * 65], Pt[pi][:, j, :], VA_of(st), (st == first and j == 0), st == last,
                        r=[("Pt", pi)] + vr, w=[("ps", ob)], skip_group_check=True)

        LOOK = 2
        pend = []
        for st in kt_list:
            sb = self.bank("mm")
            self.mm(self.ps[sb][:, 0:nq * 128], KT(st), QTv, True, True, r=kr + qr, w=[("ps", sb)])
            pi = self.pt_i % npt
            self.pt_i += 1
            self.act(Pt[pi][:, 0:nq, :], self.ps[sb][:, 0:nq * 128].rearrange("p (a b) -> p a b", a=nq), AF.Exp,
                     r=[("ps", sb)], w=[("Pt", pi)], scale=scale)
            pend.append((st, pi))
            if len(pend) > LOOK:
                pv(*pend.pop(0))
        while pend:
            pv(*pend.pop(0))
        ov = self.ps[ob][:, 0:nq * 65].rearrange("p (a b) -> p a b", a=nq)
        rd = self.at_rden
        self.recip(rd[:, 0:nq], ov[:, :, 64], r=[("ps", ob)], w=["at_rden"])
        for j in range(nq):
            dst, dk = out_of(j)
            self.P.op("dve", lambda h, j=j, dst=dst, rd=rd, ov=ov: h.tensor_scalar(dst, ov[:, j, 0:64], rd[:, j:j + 1], None, ALU.mult),
                      r=[("ps", ob), "at_rden"], w=[dk])

    def mixer_c(self, l, i):
        A = self.A
        A.reset()
        I, O = self.I, self.O
        w_in = A.alloc([8, 1536], BF16)
        w_out = A.alloc([8, 1024], BF16)
        hT = A.alloc([8, 256], BF16)
        KT = A.alloc([4, 1536], BF16)
        VA = A.alloc([12, 4, 65], BF16)
        self.alloc_norm_tmp()
        kvf = A.alloc([512], F32)
        sq = A.alloc([1024], F32)
        qn = A.alloc([1024], F32)
        st16 = A.alloc([16], F32)
        knb = A.alloc([256], BF16)
        qnb = A.alloc([1024], BF16)
        QT = A.alloc([16, 128], BF16)
        Pt = [A.alloc([4, 128], BF16) for _ in range(3)]
        ob = A.alloc([1024], BF16)
        OT = A.alloc([8, 256], BF16)
        self.at_rden = A.alloc([4], F32)
        rtm = [A.alloc([16, 2, 16], F32) for _ in range(4)]
        ropeg = A.alloc([2, 2, 32], F32)
        ck = A.alloc([4, 256], F32)
        cv = A.alloc([4, 256], F32)
        ckb = A.alloc([4, 256], BF16)
        kg = [A.alloc([512], F32) for _ in range(2)]
        self.pt_i = 0
        gq = self.cst("gq%d" % i)
        gk = self.cst("gk%d" % i)
        for kh in range(2):
            self.dma("pool", w_in[:, kh * 4:(kh + 1) * 4, :],
                     I["gqa_w_in"][i, kh * 512:(kh + 1) * 512, :].rearrange("(k p) n -> p k n", p=128), r=[], w=[("w_in", kh)], dq="w_in")
        self.dma("pool", w_out, I["gqa_w_out"][i].rearrange("(k p) n -> p k n", p=128), r=[], w=["w_out"], dq="w_out")
        self.memset("pool", VA[:, :, :, 64:65], 1.0, w=["VA1"])
        wk = [("w_in", 0), ("w_in", 1)]
        self.dma("sp", ropeg, I["ropeg"].rearrange("(t p) c q -> p t c q", p=128), r=[], w=["ropeg"], dq="ropeg")
        cc_src, cc_dst = self.CC[(l, "c")]

        def put_keys(knb_ap, kt, rk):
            b = self.bank("tr")
            for g in range(4):
                self.tr(self.psb[b][0:64, g * 128:(g + 1) * 128], knb_ap[:, g * 64:(g + 1) * 64], self.ident_b,
                        r=rk + ["identb"], w=[("ps", b)])
            self.cp("dve", KT[0:64, :, kt * 128:(kt + 1) * 128], self.psb[b][0:64, 0:512].rearrange("p (a b) -> p a b", a=4),
                    r=[("ps", b)], w=[("KT", kt)])

        def kv_side(t, n, rope_n):
            self.normmod(t, 0, hT[:, :, n * 128:(n + 1) * 128], ("hT", n))
            b = self.bank("mm")
            for kc in range(8):
                self.mm(self.ps[b][:, :], hT[:, kc, n * 128:(n + 1) * 128], w_in[:, kc, 1024:1536], kc == 0, kc == 7,
                        r=[("hT", n)] + wk, w=[("ps", b)])
            self.cp("act", kvf, self.ps[b][:, :], r=[("ps", b)], w=["kvf"])
            self.act(sq[:, 0:256], self.ps[b][:, 0:256], AF.Square, r=[("ps", b)], w=["sq"])
            self.red(st16[:, 0:4], sq[:, 0:256].rearrange("p (g d) -> p g d", g=4), r=["sq"], w=["st16"])
            self.rstd(st16[:, 0:4], st16[:, 0:4], 64, r=["st16"], w=["st16"])
            kv3 = kvf[:, 0:256].rearrange("p (g d) -> p g d", g=4)
            self.tt("dve", kv3, kv3, st16[:, 0:4].unsqueeze(2).broadcast_to([128, 4, 64]), ALU.mult, r=["kvf", "st16"], w=["kvf"])
            self.tt("dve", kv3, kv3, gk.unsqueeze(1).broadcast_to([128, 4, 64]), ALU.mult, r=["kvf", "cst"], w=["kvf"])
            if rope_n is not None:
                self.rope(kvf[:, 0:256].rearrange("p (h a b q) -> p h a b q", h=4, a=2, b=2, q=16), 4, 16, ropeg[:, rope_n, :, :],
                          [x[:, 0:4, :, :] for x in rtm], r=["kvf", "ropeg"], w=["kvf"], keyp="rtm")

        def q_attn_out(t, n, rope_n, kt_list):
            kkeys = [("KT", kt) for kt in kt_list]
            vkeys = [("VA", kt) for kt in kt_list] + ["VA1"]
            qb = []
            for bk in range(2):
                b = self.bank("mm")
                qb.append(b)
                for kc in range(8):
                    self.mm(self.ps[b][:, :], hT[:, kc, n * 128:(n + 1) * 128], w_in[:, kc, bk * 512:(bk + 1) * 512], kc == 0, kc == 7,
                            r=[("hT", n)] + wk, w=[("ps", b)])
                self.act(sq[:, bk * 512:(bk + 1) * 512], self.ps[b][:, :], AF.Square, r=[("ps", b)], w=["sq"])
            self.red(st16, sq.rearrange("p (g d) -> p g d", g=16), r=["sq"], w=["st16"])
            self.rstd(st16, st16, 64, r=["st16"], w=["st16"])
            for bk in range(2):
                self.tt("dve", qn[:, bk * 512:(bk + 1) * 512].rearrange("p (g d) -> p g d", g=8),
                        self.ps[qb[bk]][:, :].rearrange("p (g d) -> p g d", g=8),
                        st16[:, bk * 8:(bk + 1) * 8].unsqueeze(2).broadcast_to([128, 8, 64]), ALU.mult,
                        r=[("ps", qb[bk]), "st16"], w=["qn"])
            qn3 = qn.rearrange("p (g d) -> p g d", g=16)
            self.tt("dve", qn3, qn3, gq.unsqueeze(1).broadcast_to([128, 16, 64]), ALU.mult, r=["qn", "cst"], w=["qn"])
            if rope_n is not None:
                self.rope(qn.rearrange("p (h a b q) -> p h a b q", h=16, a=2, b=2, q=16), 16, 16, ropeg[:, rope_n, :, :], rtm,
                          r=["qn", "ropeg"], w=["qn"], keyp="rtm")
            self.cp("act", qnb, qn, r=["qn"], w=["qnb"])
            for hb in range(2):
                b = self.bank("tr")
                for hh in range(8):
                    h_ = hb * 8 + hh
                    self.tr(self.psb[b][0:64, hh * 128:(hh + 1) * 128], qnb[:, h_ * 64:(h_ + 1) * 64], self.ident_b,
                            r=["qnb", "identb"], w=[("ps", b)])
                self.cp("dve" if hb else "act", QT[0:64, hb * 8:(hb + 1) * 8, :],
                        self.psb[b][0:64, :].rearrange("p (a b) -> p a b", a=8), r=[("ps", b)], w=[("QT", hb)])
            for g in range(4):
                self.attention(
                    KT=lambda st, g=g: KT[0:64, g, st * 128:(st + 1) * 128], kparts=64, kt_list=kt_list,
                    QTv=QT[0:64, 4 * g:4 * g + 4, :], nq=4,
                    VA_of=lambda st, g=g: VA[:, st, g, :], scale=0.125, Pt=Pt,
                    out_of=lambda j, g=g: (ob[:, (4 * g + j) * 64:(4 * g + j + 1) * 64], ("ob", g)),
                    kr=kkeys, qr=[("QT", g // 2)], vr=vkeys, ow=None)
            b = self.bank("tr")
            for c in range(8):
                self.tr(self.psb[b][:, c * 128:(c + 1) * 128], ob[:, c * 128:(c + 1) * 128], self.ident_b,
                        r=[("ob", c // 2), "identb"], w=[("ps", b)])
            self.cp("act", OT[:, :, n * 128:(n + 1) * 128], self.psb[b][:, :].rearrange("p (a b) -> p a b", a=8), r=[("ps", b)], w=[("OT", n)])
            if n == 1:
                self.out_proj_pair(t - 1, w_out, lambda kc: OT[:, kc, :], [("OT", 0), ("OT", 1)])

        cck = "cc_src_c%d" % l
        ccd = "cc_dst_c%d" % l
        for n, t in enumerate(self.sseg["tiles"]):
            kv_side(t, n, n)
            self.dma("sp", cc_src[n * 128:(n + 1) * 128, :], kvf, r=["kvf"], w=[cck], dq="ccb_c")
        self.allgather((l, "c"), r=[cck], w=[ccd])
        for sq_ in self.seqs:
            tiles = sq_["tiles"]
            bi = sq_["bidx"]
            for n, t in enumerate(tiles):
                kv_side(t, n, None)
                self.dma("sp", O["o_gk"][bi, i, n * 128:(n + 1) * 128, :], kvf[:, 0:256], r=["kvf"], w=[], dq="kvf_o")
                self.dma("sp", O["o_gv"][bi, i, n * 128:(n + 1) * 128, :], kvf[:, 256:512], r=["kvf"], w=[], dq="kvf_o")
                self.cp("act", knb, kvf[:, 0:256], r=["kvf"], w=["knb"])
                put_keys(knb, n, ["knb"])
                self.cp("pool", VA[:, n, :, 0:64], kvf[:, 256:512].rearrange("p (g d) -> p g d", g=4), r=["kvf"], w=[("VA", n)])
            for n, t in enumerate(tiles):
                q_attn_out(t, n, None, [0, 1])
        self.dma("sp", ck, I["c_gk"][i].rearrange("(t p) n -> p t n", p=128), r=[], w=["ck"], dq="ck")
        self.dma("sp", cv, I["c_gv"][i].rearrange("(t p) n -> p t n", p=128), r=[], w=["cv"], dq="cv")
        self.cp("act", ckb, ck, r=["ck"], w=["ckb"])
        for kt in range(4):
            put_keys(ckb[:, kt, :], kt, ["ckb"])
            self.cp("pool", VA[:, kt, :, 0:64], cv[:, kt, :].rearrange("p (g d) -> p g d", g=4), r=["cv"], w=[("VA", kt)])
        for k in range(8):
            kgk = kg[k % 2]
            kk = ("kg", k % 2)
            self.dma("sp", kgk, cc_dst[k * 128:(k + 1) * 128, :], r=[ccd], w=[kk], dq=kk)
            self.cp("act", knb, kgk[:, 0:256], r=[kk], w=["knb"])
            put_keys(knb, 4 + k, ["knb"])
            self.cp("pool", VA[:, 4 + k, :, 0:64], kgk[:, 256:512].rearrange("p (g d) -> p g d", g=4), r=[kk], w=[("VA", 4 + k)])
        for n, t in enumerate(self.sseg["tiles"]):
            self.normmod(t, 0, hT[:, :, n * 128:(n + 1) * 128], ("hT", n))
            q_attn_out(t, n, n, list(range(12)))

    def mixer_ab(self, l, i):
        A = self.A
        A.reset()
        I, O = self.I, self.O
        w_in = A.alloc([8, 2240], BF16)
        w_out = A.alloc([8, 1024], BF16)
        OG = A.alloc([4, 256], BF16)
        hTm = A.alloc([8, 128], BF16)
        hk = "hTm"
        self.alloc_norm_tmp()
        self.at_rden = A.alloc([4], F32)
        Pt = [A.alloc([4, 128], BF16) for _ in range(3)]
        self.pt_i = 0
        SP = dict(qlT=[A.alloc([2, 256], BF16) for _ in range(2)], oT=A.alloc([4, 256], F32), rsT=A.alloc([4, 256], BF16),
                  etot=A.alloc([4, 2], F32), cqb=A.alloc([2, 384], BF16), aseg=A.alloc([4], F32))
        base_off = A.off
        for kh in range(2):
            for ch in range(2):
                self.dma("pool", w_in[:, kh * 4:(kh + 1) * 4, ch * 1120:(ch + 1) * 1120],
                         I["ab_w_in"][i, kh * 512:(kh + 1) * 512, ch * 1120:(ch + 1) * 1120].rearrange("(k p) n -> p k n", p=128),
                         r=[], w=[("w_in", kh, ch)], dq="w_in")
        self.dma("pool", w_out, I["ab_w_out"][i].rearrange("(k p) n -> p k n", p=128), r=[], w=["w_out"], dq="w_out")
        wk = [("w_in", 0, 0), ("w_in", 0, 1), ("w_in", 1, 0), ("w_in", 1, 1)]
        gout = self.cst("gout%d" % i)
        gqn = self.cst("gqn%d" % i)
        gkvn = self.cst("gkvn%d" % i)
        gq96 = self.cst("gq96%d" % i)
        gk96 = self.cst("gk96%d" % i)
        sel = self.cst("sel")
        trim = [self.cst("trim0"), self.cst("trim1")]
        tris = [self.cst("tris0"), self.cst("tris1")]
        mask = [self.cst("mask0"), self.cst("mask1")]
        ccg_src, ccg_dst = self.CC[(l, "g")]
        ccm_src, ccm_dst = self.CC[(l, "m")]

        def g_alloc(NT, persist=None):
            T = NT * 128
            B = {}
            if persist is None:
                B["qlT"] = [A.alloc([2, T], BF16) for _ in range(2)]
                B["oT"] = A.alloc([4, T], F32)
                B["rsT"] = A.alloc([4, T], BF16)
                B["etot"] = A.alloc([4, NT], F32)
            else:
                for k in ("qlT", "oT", "rsT", "etot"):
                    B[k] = persist[k]
            B["klT"] = [A.alloc([2, T], BF16) for _ in range(2)]
            B["kst"] = [A.alloc([NT, 256], BF16) for _ in range(2)]
            B["vtk"] = A.alloc([NT, 512], BF16)
            B["Sst"] = [[A.alloc([128], F32) for _ in range(2)] for _ in range(2)]
            B["Sbf"] = [[A.alloc([128], BF16) for _ in range(2)] for _ in range(2)]
            B["alT"] = A.alloc([128], F32)
            B["lsp"] = A.alloc([512], F32)
            B["Eb"] = A.alloc([4, 128], F32)
            B["Enb"] = A.alloc([4, 128], F32)
            B["Ed2"] = A.alloc([512], F32)
            B["ATm"] = [A.alloc([128], BF16) for _ in range(4)]
            B["aw2"] = A.alloc([512], F32)
            self.memset("dve", B["alT"][32:33, :], 1.0, w=["alT1"])
            aw2 = B["aw2"]
            self.memset("dve", aw2[0:33, :], 0.0, w=["aw2"])
            for z in range(2):
                self.dma("sp", aw2[16 * z:16 * z + 16, z * 256:(z + 1) * 256], I["a_w2"][i, z], r=[], w=["aw2"], dq="aw2")
            self.dma("sp", aw2[32:33, :], I["a_b"][i:i + 1].rearrange("o z n -> o (z n)"), r=[], w=["aw2"], dq="aw2")
            return B

        def g_prep(tiles, B):
            qlT, klT, kst, vtk, rsT, etot = B["qlT"], B["klT"], B["kst"], B["vtk"], B["rsT"], B["etot"]
            alT, lsp, Eb, Enb, Ed2, aw2 = B["alT"], B["lsp"], B["Eb"], B["Enb"], B["Ed2"], B["aw2"]
            for n, t in enumerate(tiles):
                nc_ = slice(n * 128, (n + 1) * 128)
                h = hTm
                self.normmod(t, 0, h, hk)
                bqk = self.bank("mm")
                for ch in range(4):
                    for kc in range(8):
                        self.mm(self.ps[bqk][:, ch * 128:(ch + 1) * 128], w_in[:, kc, ch * 128:(ch + 1) * 128], h[:, kc, :], kc == 0, kc == 7,
                                r=[hk] + wk, w=[("ps", bqk)])
                br = self.bank("mm")
                for ch in range(4):
                    for kc in range(8):
                        self.mm(self.ps[br][:, ch * 128:(ch + 1) * 128], w_in[:, kc, 1024 + ch * 128:1024 + (ch + 1) * 128], h[:, kc, :], kc == 0, kc == 7,
                                r=[hk] + wk, w=[("ps", br)])
                self.act(rsT[:, :, nc_], self.ps[br][:, :].rearrange("p (a b) -> p a b", a=4), AF.Silu, r=[("ps", br)], w=[("rsT", n)])
                ba = self.bank("tr")
                for kc in range(8):
                    self.mm(self.ps[ba][0:32, 0:128], w_in[:, kc, 1536:1568], h[:, kc, :], kc == 0, kc == 7, r=[hk] + wk, w=[("ps", ba)])
                self.cp("act", alT[0:32, :], self.ps[ba][0:32, 0:128], r=[("ps", ba)], w=["alT"])
                bkv = self.bank("mm")
                for kc in range(8):
                    self.mm(self.ps[bkv][:, :], h[:, kc, :], w_in[:, kc, 256:768], kc == 0, kc == 7, r=[hk] + wk, w=[("ps", bkv)])
                bv2 = self.bank("mm")
                for kc in range(8):
                    self.mm(self.ps[bv2][:, 0:256], h[:, kc, :], w_in[:, kc, 768:1024], kc == 0, kc == 7, r=[hk] + wk, w=[("ps", bv2)])
                self.cp("act", vtk[:, n, 0:256], self.ps[bkv][:, 256:512], r=[("ps", bkv)], w=[("vtk", n, 0)])
                self.cp("act", vtk[:, n, 256:512], self.ps[bv2][:, 0:256], r=[("ps", bv2)], w=[("vtk", n, 1)])
                bl = self.bank("tr")
                self.mm(self.ps[bl][:, :], alT[0:33, :], aw2[0:33, :], True, True, r=["alT", "alT1", "aw2"], w=[("ps", bl)])
                self.act(lsp, self.ps[bl][:, :], AF.Exp, r=[("ps", bl)], w=["lsp"], scale=-1.0)
                self.act(lsp, lsp, AF.Ln, r=["lsp"], w=["lsp"], bias=1.0)
                bb = self.bank("tr")
                for z in range(2):
                    for fc in range(2):
                        zf = z * 2 + fc
                        self.mm(self.ps[bb][:, zf * 128:(zf + 1) * 128], lsp[:, zf * 128:(zf + 1) * 128], trim[z], True, True,
                                r=["lsp", "cst"], w=[("ps", bb)])
                bd = self.bank("tr")
                for z in range(2):
                    self.mm(self.ps[bd][:, z * 256:(z + 1) * 256], tris[z], lsp[:, z * 256:(z + 1) * 256], True, True,
                            r=["lsp", "cst"], w=[("ps", bd)])
                pbb = self.ps[bb][:, :].rearrange("p (a b) -> p a b", a=4)
                self.act(Eb, pbb, AF.Exp, r=[("ps", bb)], w=["Eb"])
                self.act(Enb, pbb, AF.Exp, r=[("ps", bb)], w=["Enb"], scale=-1.0)
                self.act(Ed2, self.ps[bd][:, :], AF.Exp, r=[("ps", bd)], w=["Ed2"])
                self.cp("pool", etot[:, 0:2, n], Eb[:, 0:2, 127], r=["Eb"], w=[("etot", n)])
                self.cp("pool", etot[:, 2:4, n], Eb[:, 2:4, 0], r=["Eb"], w=[("etot", n)])
                pqk = self.ps[bqk][:, :].rearrange("p (a b) -> p a b", a=4)
                for z in range(2):
                    self.stt(qlT[z][:, :, nc_], pqk[:, 0:2, :], 0.125, Eb[:, 2 * z:2 * z + 2, :], ALU.mult, ALU.mult,
                             r=[("ps", bqk), "Eb"], w=[("qlT", z, n)])
                    self.tt("dve", klT[z][:, :, nc_], pqk[:, 2:4, :], Enb[:, 2 * z:2 * z + 2, :], ALU.mult,
                            r=[("ps", bqk), "Enb"], w=[("klT", z, n)])
                    self.tt("dve", kst[z][:, n, :], self.ps[bkv][:, 0:256], Ed2[:, z * 256:(z + 1) * 256], ALU.mult,
                            r=[("ps", bkv), "Ed2"], w=[("kst", z, n)])

        def g_scan(NT, B, bidx):
            qlT, klT, kst, vtk, oT, etot, Sst, Sbf, ATm = (B[k] for k in ("qlT", "klT", "kst", "vtk", "oT", "etot", "Sst", "Sbf", "ATm"))
            for z in range(2):
                for fc in range(2):
                    self.memset("pool", Sst[z][fc], 0.0, w=[("S", z, fc)])
                    self.cp("act", Sbf[z][fc], Sst[z][fc], r=[("S", z, fc)], w=[("Sbf", z, fc)])
                order = list(range(NT)) if z == 0 else list(range(NT - 1, -1, -1))
                for n in order:
                    nc_ = slice(n * 128, (n + 1) * 128)
                    bo = self.bank("acc")
                    bats = []
                    for hh in range(4):
                        fc, hp = hh // 2, hh % 2
                        pr = slice(hp * 64, (hp + 1) * 64)
                        bat = self.bank("mm")
                        bats.append(bat)
                        self.mm(self.ps[bat][:, 0:128], klT[z][pr, fc, nc_], qlT[z][pr, fc, nc_], True, True,
                                r=[("klT", z, n), ("qlT", z, n)], w=[("ps", bat)])
                    for hh in range(4):
                        self.tt("dve", ATm[hh], self.ps[bats[hh]][:, 0:128], mask[z], ALU.mult, r=[("ps", bats[hh]), "cst"], w=[("ATm", hh)])
                    for hh in range(4):
                        fc, hp = hh // 2, hh % 2
                        pr = slice(hp * 64, (hp + 1) * 64)
                        self.mm(self.ps[bo][:, hh * 128:(hh + 1) * 128], vtk[:, n, hh * 128:(hh + 1) * 128], ATm[hh], True, False,
                                r=[("vtk", n, 0), ("vtk", n, 1), ("ATm", hh)], w=[("ps", bo)])
                        self.mm(self.ps[bo][:, hh * 128:(hh + 1) * 128], Sbf[z][fc][pr, :], qlT[z][pr, fc, nc_], False, True,
                                r=[("Sbf", z, fc), ("qlT", z, n)], w=[("ps", bo)])
                    pbo = self.ps[bo][:, :].rearrange("p (a b) -> p a b", a=4)
                    if z == 0:
                        self.cp("act", oT[:, :, nc_], pbo, r=[("ps", bo)], w=[("oT", n)])
                    else:
                        self.tt("dve", oT[:, :, nc_], oT[:, :, nc_], pbo, ALU.add, r=[("ps", bo), ("oT", n)], w=[("oT", n)])
                    bus = []
                    for fc in range(2):
                        bu = self.bank("mm")
                        bus.append(bu)
                        for hp in range(2):
                            hh = fc * 2 + hp
                            self.mm(self.ps[bu][hp * 64:(hp + 1) * 64, 0:128], kst[z][:, n, hh * 64:(hh + 1) * 64],
                                    vtk[:, n, hh * 128:(hh + 1) * 128], True, True,
                                    r=[("kst", z, n), ("vtk", n, 0), ("vtk", n, 1)], w=[("ps", bu)])
                    for fc in range(2):
                        self.stt(Sst[z][fc], Sst[z][fc], etot[:, z * 2 + fc, n:n + 1], self.ps[bus[fc]][:, 0:128], ALU.mult, ALU.add,
                                 r=[("S", z, fc), ("etot", n), ("ps", bus[fc])], w=[("S", z, fc)])
                    for fc in range(2):
                        self.cp("act", Sbf[z][fc], Sst[z][fc], r=[("S", z, fc)], w=[("Sbf", z, fc)])
                if bidx is not None:
                    for fc in range(2):
                        self.dma("sp", O["o_gla"][bidx, i, z, fc * 128:(fc + 1) * 128, :], Sst[z][fc],
                                 r=[("S", z, fc)], w=[], dq=("S", z, fc))

        def g_out_alloc():
            return (A.alloc([4, 128], BF16), A.alloc([4, 128], F32), A.alloc([4, 128], F32))

        def g_out(tiles, oT, rsT, tmps=None):
            osq, orst, otmp = tmps if tmps is not None else g_out_alloc()
            for n, t in enumerate(tiles):
                nc_ = slice(n * 128, (n + 1) * 128)
                self.act(osq, oT[:, :, nc_], AF.Square, r=[("oT", n)], w=["osq"])
                b = self.bank("tr")
                self.mm(self.ps[b][:, :], self.ones_b, osq.rearrange("p a b -> p (a b)"), True, True, r=["osq", "ones"], w=[("ps", b)])
                self.rstd(orst, self.ps[b][:, :].rearrange("p (a b) -> p a b", a=4), 128, r=[("ps", b)], w=["orst"])
                self.tt("dve", otmp, oT[:, :, nc_], orst, ALU.mult, r=[("oT", n), "orst"], w=["otmp"])
                self.stt(OG[:, :, nc_], otmp, gout, rsT[:, :, nc_], ALU.mult, ALU.mult,
                         r=["otmp", "cst", ("rsT", n)], w=[("OG", n)])

        def m_alloc(NK, rope=True):
            M = {}
            M["w_qb"] = A.alloc([3, 768], BF16)
            M["w_kvb"] = A.alloc([2, 1024], BF16)
            self.dma("pool", M["w_qb"], I["w_qb"][i].rearrange("(k p) n -> p k n", p=128), r=[], w=["w_qb"], dq="w_qb")
            self.dma("pool", M["w_kvb"], I["w_kvb"][i].rearrange("(k p) n -> p k n", p=128), r=[], w=["w_kvb"], dq="w_kvb")
            M["KTm"] = A.alloc([8, NK * 128], BF16)
            M["VAm"] = A.alloc([NK, 8, 65], BF16)
            M["QTm"] = A.alloc([8, 256], BF16)
            M["omb"] = A.alloc([2, 512], BF16)
            M["OM"] = A.alloc([4, 256], BF16)
            M["cb"] = A.alloc([256], BF16)
            M["cT"] = A.alloc([2, 128], BF16)
            M["kc96"] = A.alloc([8, 96], F32)
            M["knb"] = A.alloc([8, 96], BF16)
            M["st8"] = A.alloc([8], F32)
            M["cqT"] = A.alloc([3, 128], BF16)
            M["rtm"] = [A.alloc([8, 2, 8], F32) for _ in range(4)] if rope else None
            self.memset("pool", M["VAm"][:, :, :, 64:65], 1.0, w=["VA1"])
            return M

        def own_alloc():
            W = {}
            W["sq96"] = A.alloc([8, 96], F32)
            W["ckvf"] = A.alloc([256], F32)
            W["ckvn"] = A.alloc([256], F32)
            W["kpe"] = A.alloc([32], F32)
            W["st1"] = A.alloc([2], F32)
            return W

        def norm96(M, sq96, gain, rope_cs):
            kc96, knb, st8, rtm = M["kc96"], M["knb"], M["st8"], M["rtm"]
            self.act(sq96, kc96, AF.Square, r=["kc96"], w=["sq96"])
            self.red(st8, sq96, r=["sq96"], w=["st8"])
            self.rstd(st8, st8, 96, r=["st8"], w=["st8"])
            self.tt("dve", kc96, kc96, st8.unsqueeze(2).broadcast_to([128, 8, 96]), ALU.mult, r=["kc96", "st8"], w=["kc96"])
            self.tt("dve", kc96, kc96, gain.unsqueeze(1).broadcast_to([128, 8, 96]), ALU.mult, r=["kc96", "cst"], w=["kc96"])
            if rope_cs is not None:
                cs, csk = rope_cs
                self.rope(kc96[:, :, 64:96].rearrange("p h (a b q) -> p h a b q", a=2, b=2, q=8), 8, 8, cs, rtm,
                          r=["kc96", csk], w=["kc96"], keyp="rtm")
            self.cp("act", knb, kc96, r=["kc96"], w=["knb"])

        def kside(M, sq96, ckvn_ap, ckvn_k, kpe_ap, kpe_k, kt, rope_cs):
            cb, cT, kc96, knb, KTm, VAm, w_kvb = M["cb"], M["cT"], M["kc96"], M["knb"], M["KTm"], M["VAm"], M["w_kvb"]
            self.cp("act", cb, ckvn_ap, r=[ckvn_k], w=["cb"])
            b = self.bank("tr")
            for kc in range(2):
                self.tr(self.psb[b][:, kc * 128:(kc + 1) * 128], cb[:, kc * 128:(kc + 1) * 128], self.ident_b, r=["cb", "identb"], w=[("ps", b)])
            self.cp("dve", cT, self.psb[b][:, 0:256].rearrange("p (a b) -> p a b", a=2), r=[("ps", b)], w=["cT"])
            for bk in range(2):
                b = self.bank("mm")
                for kc in range(2):
                    self.mm(self.ps[b][:, :], cT[:, kc, :], w_kvb[:, kc, bk * 512:(bk + 1) * 512], kc == 0, kc == 1,
                            r=["cT", "w_kvb"], w=[("ps", b)])
                pv = self.ps[b][:, :].rearrange("p (h d) -> p h d", h=4)
                self.cp("act", kc96[:, bk * 4:(bk + 1) * 4, 0:64], pv[:, :, 0:64], r=[("ps", b)], w=["kc96"])
                self.cp("dve", VAm[:, kt, bk * 4:(bk + 1) * 4, 0:64], pv[:, :, 64:128], r=[("ps", b)], w=[("VAm", kt)])
            self.cp("pool", kc96[:, :, 64:96], kpe_ap.unsqueeze(1).broadcast_to([128, 8, 32]), r=[kpe_k], w=["kc96"])
            norm96(M, sq96, gk96, rope_cs)
            b = self.bank("tr")
            for hh in range(8):
                self.tr(self.psb[b][0:96, hh * 128:(hh + 1) * 128], knb[:, hh, :], self.ident_b, r=["knb", "identb"], w=[("ps", b)])
            self.cp("dve", KTm[0:96, :, kt * 128:(kt + 1) * 128], self.psb[b][0:96, :].rearrange("p (a b) -> p a b", a=8),
                    r=[("ps", b)], w=[("KTm", kt)])

        def m_own(t, n, W, cqb):
            sq96, ckvf, ckvn, kpe, st1 = W["sq96"], W["ckvf"], W["ckvn"], W["kpe"], W["st1"]
            h = hTm
            self.normmod(t, 0, h, hk)
            b1 = self.bank("mm")
            for kc in range(8):
                self.mm(self.ps[b1][:, :], h[:, kc, :], w_in[:, kc, 1568:2080], kc == 0, kc == 7, r=[hk] + wk, w=[("ps", b1)])
            b2 = self.bank("mm")
            for kc in range(8):
                self.mm(self.ps[b2][:, 0:160], h[:, kc, :], w_in[:, kc, 2080:2240], kc == 0, kc == 7, r=[hk] + wk, w=[("ps", b2)])
            sqf = sq96.rearrange("p a b -> p (a b)")
            self.act(sqf[:, 0:384], self.ps[b1][:, 0:384], AF.Square, r=[("ps", b1)], w=["sq96", "st1"], accum_out=st1[:, 0:1])
            self.rstd(st1[:, 0:1], st1[:, 0:1], 384, r=["st1"], w=["st1"])
            self.stt(cqb[:, n, :], self.ps[b1][:, 0:384], st1[:, 0:1], gqn, ALU.mult, ALU.mult, r=[("ps", b1), "st1", "cst"], w=[("cqb", n)])
            self.cp("act", ckvf[:, 0:128], self.ps[b1][:, 384:512], r=[("ps", b1)], w=["ckvf"])
            self.cp("act", ckvf[:, 128:256], self.ps[b2][:, 0:128], r=[("ps", b2)], w=["ckvf"])
            self.cp("dve", kpe, self.ps[b2][:, 128:160], r=[("ps", b2)], w=["kpe"])
            self.act(sqf[:, 0:256], ckvf, AF.Square, r=["ckvf"], w=["sq96", "st1b"], accum_out=st1[:, 1:2])
            self.rstd(st1[:, 1:2], st1[:, 1:2], 256, r=["st1b"], w=["st1b"])
            self.stt(ckvn, ckvf, st1[:, 1:2], gkvn, ALU.mult, ALU.mult, r=["ckvf", "st1b", "cst"], w=["ckvn"])

        def m_attn(M, sq96, tiles, cqb, rope_q, NK):
            QTm, omb, OM, KTm, VAm, cqT, kc96, knb, w_qb = (M[k] for k in ("QTm", "omb", "OM", "KTm", "VAm", "cqT", "kc96", "knb", "w_qb"))
            kt_list = list(range(NK))
            kkeys = [("KTm", kt) for kt in kt_list]
            vkeys = [("VAm", kt) for kt in kt_list] + ["VA1"]
            nq = len(tiles)
            for n, t in enumerate(tiles):
                b = self.bank("tr")
                for kc in range(3):
                    self.tr(self.psb[b][:, kc * 128:(kc + 1) * 128], cqb[:, n, kc * 128:(kc + 1) * 128], self.ident_b,
                            r=[("cqb", n), "identb"], w=[("ps", b)])
                self.cp("act", cqT, self.psb[b][:, 0:384].rearrange("p (a b) -> p a b", a=3), r=[("ps", b)], w=["cqT"])
                for bk in range(2):
                    b = self.bank("mm")
                    for kc in range(3):
                        self.mm(self.ps[b][:, 0:384], cqT[:, kc, :], w_qb[:, kc, bk * 384:(bk + 1) * 384], kc == 0, kc == 2,
                                r=["cqT", "w_qb"], w=[("ps", b)])
                    self.cp("act", kc96[:, bk * 4:(bk + 1) * 4, :], self.ps[b][:, 0:384].rearrange("p (h d) -> p h d", h=4),
                            r=[("ps", b)], w=["kc96"])
                norm96(M, sq96, gq96, None if rope_q is None else (rope_q[0][:, n, :, :], rope_q[1]))
                b = self.bank("tr")
                for hh in range(8):
                    self.tr(self.psb[b][0:96, hh * 128:(hh + 1) * 128], knb[:, hh, :], self.ident_b, r=["knb", "identb"], w=[("ps", b)])
                self.cp("dve", QTm[0:96, :, n * 128:(n + 1) * 128], self.psb[b][0:96, :].rearrange("p (a b) -> p a b", a=8),
                        r=[("ps", b)], w=[("QTm", n)])
            for hh in range(8):
                self.attention(
                    KT=lambda st, hh=hh: KTm[0:96, hh, st * 128:(st + 1) * 128], kparts=96, kt_list=kt_list,
                    QTv=QTm[0:96, hh, 0:nq * 128], nq=nq,
                    VA_of=lambda st, hh=hh: VAm[:, st, hh, :], scale=float(96 ** -0.5), Pt=Pt,
                    out_of=lambda j, hh=hh: (omb[:, j, hh * 64:(hh + 1) * 64], ("omb", j)),
                    kr=kkeys, qr=[("QTm", j) for j in range(nq)], vr=vkeys, ow=None)
            for n, t in enumerate(tiles):
                b = self.bank("tr")
                for c in range(4):
                    self.tr(self.psb[b][:, c * 128:(c + 1) * 128], omb[:, n, c * 128:(c + 1) * 128], self.ident_b,
                            r=[("omb", n), "identb"], w=[("ps", b)])
                self.cp("act", OM[:, :, n * 128:(n + 1) * 128], self.psb[b][:, 0:512].rearrange("p (a b) -> p a b", a=4),
                        r=[("ps", b)], w=[("OM", n)])
            t0 = tiles[0]
            self.out_proj_pair(t0, w_out,
                               lambda kc: OG[:, kc, :] if kc < 4 else OM[:, kc - 4, 0:256],
                               [("OG", 0), ("OG", 1), ("OM", 0), ("OM", 1)])

        st_ = self.sseg["tiles"]
        A.reset(base_off)
        B = g_alloc(2, persist=SP)
        g_prep(st_, B)
        g_scan(2, B, None)
        ccgs, ccgd = "cc_src_g%d" % l, "cc_dst_g%d" % l
        gsrc = A.alloc([4, 129], F32)
        self.tt("dve", gsrc[:, :, 128], SP["etot"][:, :, 0], SP["etot"][:, :, 1], ALU.mult, r=[("etot", 0), ("etot", 1)], w=["gsrc_a"])
        for z in range(2):
            for fc in range(2):
                zf = z * 2 + fc
                self.cp("act", gsrc[:, zf, 0:128], B["Sst"][z][fc], r=[("S", z, fc)], w=[("gsrc", zf)])
        self.dma("sp", ccg_src.ap().rearrange("(zf p) c -> p zf c", p=128), gsrc,
                 r=["gsrc_a"] + [("gsrc", zf) for zf in range(4)], w=[ccgs], dq="bnc_g")
        self.allgather((l, "g"), r=[ccgs], w=[ccgd])
        self.P.barrier()
        if "x1" in self.stages:
            return
        A.reset(base_off)
        W = own_alloc()
        ccms, ccmd = "cc_src_m%d" % l, "cc_dst_m%d" % l
        for n, t in enumerate(st_):
            m_own(t, n, W, SP["cqb"])
            self.dma("sp", ccm_src[n * 128:(n + 1) * 128, 0:256], W["ckvn"], r=["ckvn"], w=[ccms], dq="bnc_m")
            self.dma("sp", ccm_src[n * 128:(n + 1) * 128, 256:288], W["kpe"], r=["kpe"], w=[ccms], dq="bnc_m")
        self.allgather((l, "m"), r=[ccms], w=[ccmd])
        self.P.barrier()
        if "x2" in self.stages:
            return

        A.reset(base_off)
        B = g_alloc(2)
        GT = g_out_alloc()
        M = m_alloc(2, rope=False)
        W = own_alloc()
        cqb = A.alloc([2, 384], BF16)
        for sq_ in self.seqs:
            tiles = sq_["tiles"]
            bi = sq_["bidx"]
            g_prep(tiles, B)
            g_scan(2, B, bi)
            g_out(tiles, B["oT"], B["rsT"], GT)
            for n, t in enumerate(tiles):
                m_own(t, n, W, cqb)
                self.dma("sp", O["o_ckv"][bi, i, n * 128:(n + 1) * 128, :], W["ckvn"], r=["ckvn"], w=[], dq="ckvn_o")
                self.dma("sp", O["o_kpe"][bi, i, n * 128:(n + 1) * 128, :], W["kpe"], r=["kpe"], w=[], dq="kpe_o")
                kside(M, W["sq96"], W["ckvn"], "ckvn", W["kpe"], "kpe", n, None)
            m_attn(M, W["sq96"], tiles, cqb, None, 2)
        self.P.barrier()

        if "x3" in self.stages:
            return
        A.reset(base_off)
        gU = A.alloc([16, 129], F32)
        self.dma("sp", gU, ccg_dst.ap().rearrange("(rz p) c -> p rz c", p=128), r=[ccgd], w=["gU"], dq="gU")
        Sin = [[A.alloc([128], F32) for _ in range(2)] for _ in range(2)]
        Sib = [[A.alloc([128], BF16) for _ in range(2)] for _ in range(2)]
        tS = A.alloc([128], F32)
        for z in range(2):
            for fc in range(2):
                zf = z * 2 + fc
                S_ = Sin[z][fc]
                sk = ("Sin", z, fc)
                self.dma("sp", S_, I["c_gla"][i, z, fc * 128:(fc + 1) * 128, :], r=[], w=[sk], dq=sk)
                ranks = [0, 1, 2, 3] if z == 0 else [3, 2, 1, 0]
                for k in ranks:
                    gi = k * 4 + zf
                    self.stt(tS, S_, gU[:, gi, 128:129], gU[:, gi, 0:128], ALU.mult, ALU.add, r=[sk, "gU"], w=["tS"])
                    self.tt("dve", tS, tS, S_, ALU.subtract, r=["tS", sk], w=["tS"])
                    self.stt(S_, tS, sel[:, z * 4 + k:z * 4 + k + 1], S_, ALU.mult, ALU.add, r=["tS", sk, "cst"], w=[sk])
                self.cp("act", Sib[z][fc], S_, r=[sk], w=[("Sib", z, fc)])
        if "x5" in self.stages:
            self.P.barrier()
            return
        for z in range(2):
            order = [0, 1] if z == 0 else [1, 0]
            for oi, n in enumerate(order):
                nc_ = slice(n * 128, (n + 1) * 128)
                bh = [self.bank("acc"), self.bank("acc")]
                for hp in range(2):
                    pr = slice(hp * 64, (hp + 1) * 64)
                    for fc in range(2):
                        self.mm(self.ps[bh[hp]][:, fc * 128:(fc + 1) * 128], Sib[z][fc][pr, :], SP["qlT"][z][pr, fc, nc_], True, True,
                                r=[("Sib", z, fc), ("qlT", z, n)], w=[("ps", bh[hp])])
                oview = SP["oT"][:, :, nc_].rearrange("p (f h) t -> p f h t", f=2, h=2)
                for hp in range(2):
                    pb = self.ps[bh[hp]][:, 0:256].rearrange("p (a b) -> p a b", a=2)
                    self.tt("dve", oview[:, :, hp, :], oview[:, :, hp, :], pb, ALU.add, r=[("ps", bh[hp]), ("oT", n)], w=[("oT", n)])
                if oi == 0:
                    for fc in range(2):
                        zf = z * 2 + fc
                        sk = ("Sin", z, fc)
                        self.P.op("dve", lambda h, S_=Sin[z][fc], e=SP["etot"][:, zf, n:n + 1]: h.tensor_scalar(S_, S_, e, None, ALU.mult),
                                  r=[sk, ("etot", n)], w=[sk])
                        self.cp("act", Sib[z][fc], Sin[z][fc], r=[sk], w=[("Sib", z, fc)])
        if "x6" in self.stages:
            self.P.barrier()
            return
        g_out(st_, SP["oT"], SP["rsT"])
        self.P.barrier()
        if "x4" in self.stages:
            return
        A.reset(base_off)
        M = m_alloc(12)
        sq96 = A.alloc([8, 96], F32)
        ropem_all = A.alloc([8, 2, 16], F32)
        ropem_own = A.alloc([2, 2, 16], F32)
        ckp = A.alloc([4, 32], F32)
        cck = A.alloc([256], F32)
        kgm = [A.alloc([288], F32) for _ in range(2)]
        self.dma("sp", ropem_all, I["ropem_all"].rearrange("(t p) c q -> p t c q", p=128), r=[], w=["ropem_all"], dq="ropem")
        self.dma("sp", ropem_own, I["ropem"].rearrange("(t p) c q -> p t c q", p=128), r=[], w=["ropem_own"], dq="ropem")
        self.dma("sp", ckp, I["c_kpe"][i].rearrange("(t p) n -> p t n", p=128), r=[], w=["ckp"], dq="ckp")
        for kt in range(4):
            self.dma("sp", cck, I["c_ckv"][i, kt * 128:(kt + 1) * 128, :], r=[], w=["cck"], dq="cck")
            kside(M, sq96, cck, "cck", ckp[:, kt, :], "ckp", kt, None)
        for k in range(8):
            kg_ = kgm[k % 2]
            kk = ("kgm", k % 2)
            self.dma("sp", kg_, ccm_dst[k * 128:(k + 1) * 128, :], r=[ccmd], w=[kk], dq=kk)
            kside(M, sq96, kg_[:, 0:256], kk, kg_[:, 256:288], kk, 4 + k, (ropem_all[:, k, :, :], "ropem_all"))
        m_attn(M, sq96, st_, SP["cqb"], (ropem_own, "ropem_own"), 12)


def _rope_tables(n_tok, d_rot):
    t = np.arange(n_tok, dtype=np.int32)
    pos = np.stack([t // 64, t % 64], axis=-1).astype(np.float32)
    quarter = d_rot // 4
    inv = np.power(np.float32(10000.0), -np.arange(quarter, dtype=np.float32) / np.float32(quarter)).astype(np.float32)
    ang = pos[:, :, None] * inv
    cos = np.cos(ang).astype(np.float32).reshape(n_tok, 2 * quarter)
    sin = np.sin(ang).astype(np.float32).reshape(n_tok, 2 * quarter)
    return np.ascontiguousarray(np.stack([cos, sin], axis=1))


def _build_cst(inp, b, j):
    c = np.zeros((128, NCST), np.float32)

    def put(name, arr, parts=128):
        o, n = CST_OFF[name]
        c[0:parts, o:o + n] = np.asarray(arr, np.float32).reshape(parts, n)

    s = np.arange(128)[:, None]
    t = np.arange(128)[None, :]
    v = np.float32(-1.0 / 16.0)
    put("ident", np.eye(128))
    put("trim0", (s <= t) * v)
    put("trim1", (s >= t) * v)
    put("tris0", (s > t) * v)
    put("tris1", (s < t) * v)
    put("mask0", (s <= t) * 1.0)
    put("mask1", (s >= t) * 1.0)
    put("rm", np.array([[1, 0], [0, 1], [1, 1]], np.float32), parts=3)
    cond = np.stack([inp["c_ctx"].reshape(8, 128).T, inp["c"][b].reshape(8, 128).T], axis=-1)
    put("cond", cond)
    selv = np.array([1.0 if k < j else 0.0 for k in range(4)] + [1.0 if k > j else 0.0 for k in range(4)], np.float32)
    put("sel", np.broadcast_to(selv[None, :], (128, 8)))
    put("gmix", inp["norm_mix_g"].reshape(4, 8, 128).transpose(2, 0, 1))
    put("gffn", inp["norm_ffn_g"].reshape(4, 8, 128).transpose(2, 0, 1))
    for i in range(2):
        put("gq%d" % i, np.broadcast_to(inp["gqa_qn_g"][i][None, :], (128, 64)))
        put("gk%d" % i, np.broadcast_to(inp["gqa_kn_g"][i][None, :], (128, 64)))
        put("gout%d" % i, inp["gla_out_g"][i].reshape(128, 1))
        put("gqn%d" % i, np.broadcast_to(inp["mla_q_norm_g"][i][None, :], (128, 384)))
        put("gkvn%d" % i, np.broadcast_to(inp["mla_kv_norm_g"][i][None, :], (128, 256)))
        put("gq96%d" % i, np.broadcast_to(inp["mla_qn_g"][i][None, :], (128, 96)))
        put("gk96%d" % i, np.broadcast_to(inp["mla_kn_g"][i][None, :], (128, 96)))
    return c


_NC_CACHE = {}


def _get_nc(debug=(), stages=None):
    key = (tuple(sorted(debug)), None if stages is None else tuple(sorted(stages)))
    if key not in _NC_CACHE:
        kb = KB(debug, stages)
        nc = kb.build()
        _NC_CACHE[key] = (nc, kb)
    return _NC_CACHE[key]


def make_in_maps(inp):
    inp = {k: np.ascontiguousarray(np.asarray(v)) for k, v in inp.items()}
    ropeg = _rope_tables(1024, 64)
    ropem = _rope_tables(1024, 32)
    shared = dict(
        ffn_w_in=inp["ffn_w_in"], ffn_w_out=inp["ffn_w_out"],
        ab_w_in=inp["ab_w_in"], ab_w_out=inp["ab_w_out"], a_w2=inp["gla_a_w2"], a_b=inp["gla_a_b"],
        w_qb=inp["mla_w_qb"], w_kvb=inp["mla_w_kvb"], gqa_w_in=inp["gqa_w_in"], gqa_w_out=inp["gqa_w_out"])
    in_maps = []
    for core in range(8):
        b, j = core // 4, core % 4
        m = dict(shared)
        m["ropem_all"] = ropem
        m["ada_w"] = np.ascontiguousarray(inp["ada_w"][:, :, j * 1536:(j + 1) * 1536])
        m["ada_b"] = np.ascontiguousarray(inp["ada_b"][:, j * 1536:(j + 1) * 1536])
        m["ropeg"] = np.ascontiguousarray(ropeg[j * 256:(j + 1) * 256])
        m["ropem"] = np.ascontiguousarray(ropem[j * 256:(j + 1) * 256])
        m["cst"] = _build_cst(inp, b, j)
        m["xp"] = np.ascontiguousarray(inp["x_prompt"][core * 4:(core + 1) * 4].reshape(1024, D))
        m["xs"] = np.ascontiguousarray(inp["x_sample"][b, j * 256:(j + 1) * 256])
        m["c_ckv"] = np.ascontiguousarray(inp["cache_mla_ckv"][b])
        m["c_kpe"] = np.ascontiguousarray(inp["cache_mla_kpe"][b])
        m["c_gla"] = np.ascontiguousarray(inp["state_gla"][b].reshape(2, 2, 256, 128))
        m["c_gk"] = np.ascontiguousarray(inp["cache_gqa_k"][b].reshape(2, 512, 256))
        m["c_gv"] = np.ascontiguousarray(inp["cache_gqa_v"][b].reshape(2, 512, 256))
        in_maps.append(m)
    return in_maps


def kernel(**inputs):
    nc, kb = _get_nc()
    in_maps = make_in_maps(inputs)
    res = run_bass_kernel_spmd(nc, in_maps, core_ids=list(range(8)))
    R = res.results
    y_prompt = np.concatenate([R[c]["yp"].reshape(4, 256, D) for c in range(8)], axis=0)
    y_sample = np.stack([np.concatenate([R[4 * b + j]["ys"] for j in range(4)], axis=0) for b in range(2)], axis=0)
    new_ckv = np.concatenate([R[c]["o_ckv"] for c in range(8)], axis=0)
    new_kpe = np.concatenate([R[c]["o_kpe"] for c in range(8)], axis=0)
    new_gla = np.concatenate([R[c]["o_gla"].reshape(4, 2, 2, 4, 64, 128) for c in range(8)], axis=0)
    new_k = np.concatenate([R[c]["o_gk"].reshape(4, 2, 256, 4, 64) for c in range(8)], axis=0)
    new_v = np.concatenate([R[c]["o_gv"].reshape(4, 2, 256, 4, 64) for c in range(8)], axis=0)
    outs = (y_prompt, y_sample, new_ckv, new_kpe, new_gla, new_k, new_v)
    return tuple(np.ascontiguousarray(o, dtype=np.float32) for o in outs)
```

```python
import bisect
import contextlib
import numpy as np
import concourse.bass as bass
import concourse.mybir as mybir
from concourse.bass_utils import run_bass_kernel_spmd

F32 = mybir.dt.float32
BF16 = mybir.dt.bfloat16
AF = mybir.ActivationFunctionType
ALU = mybir.AluOpType
AX = mybir.AxisListType

ENGS = ("pe", "act", "dve", "pool", "sp")
RAW, WAR, WAW = 1, 2, 4
EPS = 1e-6
D = 1024
FH = 2816
ARENA_BYTES = 152 * 1024
NTOK = 1280
NTILE = 10


class Prog:
    def __init__(self):
        self.ops = []
        self.last_w = {}
        self.readers = {}

    def op(self, eng, fn, r=(), w=(), dq=None, inc=16):
        i = len(self.ops)
        deps = {}
        psr = [k for k in r if isinstance(k, tuple) and k and k[0] == "ps"]
        if psr:
            r = [k for k in r if not (isinstance(k, tuple) and k and k[0] == "ps")]
            w = list(w) + [k for k in psr if k not in w]
            for k in psr:
                lw = self.last_w.get(k)
                if lw is not None:
                    deps[lw] = deps.get(lw, 0) | RAW
        for k in r:
            lw = self.last_w.get(k)
            if lw is not None:
                deps[lw] = deps.get(lw, 0) | RAW
        for k in w:
            lw = self.last_w.get(k)
            if lw is not None:
                deps[lw] = deps.get(lw, 0) | WAW
            for rd in self.readers.get(k, ()):
                if rd != i:
                    deps[rd] = deps.get(rd, 0) | WAR
        for k in r:
            self.readers.setdefault(k, []).append(i)
        for k in w:
            self.last_w[k] = i
            self.readers[k] = []
        deps.pop(i, None)
        self.ops.append(dict(eng=eng, fn=fn, deps=deps, dq=dq, bar=None, inc=inc))
        return i

    def barrier(self):
        first = len(self.ops)
        for e in ENGS:
            self.ops.append(dict(eng=e, fn="drain", deps={}, dq=None, bar=("sig", first)))
        sig_ids = list(range(first, first + len(ENGS)))
        for e in ENGS:
            self.ops.append(dict(eng=e, fn="nop", deps={s: RAW for s in sig_ids}, dq=None, bar=("wait", first)))
        iscc = lambda k: isinstance(k, str) and k.startswith("cc")
        self.last_w = {k: v for k, v in self.last_w.items() if iscc(k)}
        self.readers = {k: v for k, v in self.readers.items() if iscc(k)}

    def emit(self, nc, es):
        ops = self.ops
        n = len(ops)
        needed = [False] * n
        for i, o in enumerate(ops):
            kept = []
            for d, kind in o["deps"].items():
                od = ops[d]
                if od["dq"] is None and o["dq"] is None and od["eng"] == o["eng"] and o["bar"] is None:
                    if o["eng"] == "pe":
                        continue
                kept.append(d)
                needed[d] = True
            o["kdeps"] = kept
        eng_sem = {e: es.enter_context(nc.semaphore("sem_" + e)) for e in ENGS}
        dq_keys = []
        seen = set()
        for o in ops:
            if o["dq"] is not None and o["dq"] not in seen:
                seen.add(o["dq"])
                dq_keys.append(o["dq"])
        dq_sem = {k: es.enter_context(nc.semaphore("dq_%d" % j)) for j, k in enumerate(dq_keys)}
        dq_idx = {k: [] for k in dq_keys}
        dq_cum = {k: [0] for k in dq_keys}
        eng_cnt = {e: 0 for e in ENGS}

        def dq_before(k, i):
            return dq_cum[k][bisect.bisect_left(dq_idx[k], i)]

        for i, o in enumerate(ops):
            if o["dq"] is not None:
                dq_idx[o["dq"]].append(i)
                dq_cum[o["dq"]].append(dq_cum[o["dq"]][-1] + o["inc"])
                o["sig"] = ("dq", o["dq"])
            elif needed[i] or (o["bar"] is not None and o["bar"][0] == "sig"):
                eng_cnt[o["eng"]] += 1
                o["sig"] = ("eng", o["eng"], eng_cnt[o["eng"]])
            else:
                o["sig"] = None
        per_eng = {e: [] for e in ENGS}
        for i, o in enumerate(ops):
            per_eng[o["eng"]].append(i)
        self.n_sems = len(ENGS) + len(dq_keys)
        self.counts = {e: len(per_eng[e]) for e in ENGS}

        def run(e, h):
            waited = {}
            for i in per_eng[e]:
                o = ops[i]
                waits = {}
                for d in o["kdeps"]:
                    od = ops[d]
                    if od["dq"] is not None:
                        k = od["dq"]
                        cnt = dq_before(k, i)
                        key = ("dq", k)
                        waits[key] = max(waits.get(key, 0), cnt)
                    else:
                        key = ("eng", od["eng"])
                        waits[key] = max(waits.get(key, 0), od["sig"][2])
                if o["bar"] is not None and o["bar"][0] == "sig" and e == "sp":
                    for k in dq_keys:
                        if isinstance(k, str) and k.startswith("cc"):
                            continue
                        cnt = dq_before(k, i)
                        if cnt:
                            waits[("dq", k)] = cnt
                for key, v in waits.items():
                    if waited.get(key, 0) >= v:
                        continue
                    waited[key] = v
                    sem = dq_sem[key[1]] if key[0] == "dq" else eng_sem[key[1]]
                    h.wait_ge(sem, v)
                if o["fn"] == "drain":
                    inst = h.nop() if e == "sp" else h.drain()
                elif o["fn"] == "nop":
                    inst = None
                else:
                    inst = o["fn"](h)
                s = o["sig"]
                if s is not None:
                    if s[0] == "dq":
                        inst.then_inc(dq_sem[s[1]], o["inc"])
                    else:
                        inst.then_inc(eng_sem[s[1]], 1)
            if e == "sp":
                for k in dq_keys:
                    h.wait_ge(dq_sem[k], dq_cum[k][-1])

        with nc.Block() as block:
            @block.tensor
            def _(h):
                run("pe", h)

            @block.scalar
            def _(h):
                run("act", h)

            @block.vector
            def _(h):
                run("dve", h)

            @block.gpsimd
            def _(h):
                run("pool", h)

            @block.sync
            def _(h):
                run("sp", h)


class Arena:
    def __init__(self, t, nbytes):
        self.t = t
        self.nbytes = nbytes
        self.off = 0
        self.peak = 0

    def reset(self, off=0):
        self.off = off

    def alloc(self, free_shape, dtype, parts=128):
        n = int(np.prod(free_shape))
        esz = 4 if dtype == F32 else 2
        sz = (n * esz + 31) // 32 * 32
        o = self.off
        assert o + sz <= self.nbytes, ("arena overflow", o, sz, self.nbytes)
        self.off = o + sz
        self.peak = max(self.peak, self.off)
        ap = self.t[0:parts, o // 2:(o + n * esz) // 2]
        if dtype == F32:
            ap = ap.bitcast(F32)
        fs = list(free_shape)
        if len(fs) == 2:
            ap = ap.rearrange("p (a b) -> p a b", a=fs[0], b=fs[1])
        elif len(fs) == 3:
            ap = ap.rearrange("p (a b c) -> p a b c", a=fs[0], b=fs[1], c=fs[2])
        elif len(fs) == 4:
            ap = ap.rearrange("p (a b c d) -> p a b c d", a=fs[0], b=fs[1], c=fs[2], d=fs[3])
        return ap


def _cst_layout():
    off = {}
    o = 0

    def add(name, n):
        nonlocal o
        off[name] = (o, n)
        o += n

    add("ident", 128)
    add("trim0", 128)
    add("trim1", 128)
    add("tris0", 128)
    add("tris1", 128)
    add("mask0", 128)
    add("mask1", 128)
    add("rm", 2)
    add("cond", 16)
    add("sel", 8)
    add("gmix", 32)
    add("gffn", 32)
    for i in range(2):
        add("gq%d" % i, 64)
        add("gk%d" % i, 64)
        add("gout%d" % i, 1)
        add("gqn%d" % i, 384)
        add("gkvn%d" % i, 256)
        add("gq96%d" % i, 96)
        add("gk96%d" % i, 96)
    return off, o


CST_OFF, NCST = _cst_layout()


class KB:
    def __init__(self, debug=(), stages=None):
        self.stages = set(stages) if stages is not None else {"adaln", "ffn", "mixc", "mixab", "P", "S"}
        self.debug = set(debug)
        self.dbg_outs = {}

    def mm(self, out, lhsT, rhs, start, stop, r, w, **kw):
        self.P.op("pe", lambda h: h.matmul(out, lhsT, rhs, start=start, stop=stop, **kw), r=r, w=w)

    def tr(self, out, in_, ident, r, w):
        self.P.op("pe", lambda h: h.transpose(out, in_, ident), r=r, w=w)

    def act(self, out, in_, func, r, w, **kw):
        self.P.op("act", lambda h: h.activation(out, in_, func, **kw), r=r, w=w)

    def tt(self, eng, out, a, b, op, r, w):
        self.P.op(eng, lambda h: h.tensor_tensor(out, a, b, op), r=r, w=w)

    def stt(self, out, in0, scalar, in1, op0, op1, r, w):
        self.P.op("dve", lambda h: h.scalar_tensor_tensor(out, in0, scalar, in1, op0, op1), r=r, w=w)

    def cp(self, eng, out, in_, r, w):
        if eng == "act":
            self.P.op("act", lambda h: h.copy(out, in_), r=r, w=w)
        else:
            self.P.op(eng, lambda h: h.tensor_copy(out, in_), r=r, w=w)

    def recip(self, out, in_, r, w):
        self.P.op("dve", lambda h: h.reciprocal(out, in_), r=r, w=w)

    def red(self, out, in_, r, w):
        self.P.op("dve", lambda h: h.tensor_reduce(out, in_, AX.X, ALU.add), r=r, w=w)

    def memset(self, eng, ap, val, w):
        self.P.op(eng, lambda h: h.memset(ap, val), w=w)

    def dma(self, q, out, in_, r, w, dq, **kw):
        self.P.op(q, lambda h: h.dma_start(out=out, in_=in_, **kw), r=r, w=w, dq=dq)

    def allgather(self, key, r, w):
        src, dst = self.CC[key]
        name = "cc_%s_%s" % key
        self.P.op("pool", lambda h: h.collective_compute("AllGather", ALU.bypass, replica_groups=[[0, 1, 2, 3], [4, 5, 6, 7]],
                                                         ins=[src.ap().opt()], outs=[dst.ap().opt()]),
                  r=r, w=w, dq=name, inc=1)

    def bank(self, pool):
        lst, idx = self.pools[pool]
        b = lst[idx % len(lst)]
        self.pools[pool][1] = idx + 1
        return b

    def rstd(self, out, ss, n, r, w):
        self.act(out, ss, AF.Ln, r=r, w=w, bias=EPS, scale=1.0 / n)
        self.act(out, out, AF.Exp, r=w, w=w, scale=-0.5)

    def cst(self, name, parts=128):
        o, n = CST_OFF[name]
        return self.cst_t[0:parts, o:o + n]

    def dbg(self, name, ap, r, shape):
        if name not in self.debug:
            return
        t = self.nc.dram_tensor("dbg_" + name, list(shape), ap.dtype if hasattr(ap, "dtype") else F32, kind="ExternalOutput").ap()
        self.dbg_outs[name] = shape
        self.dma("sp", t, ap, r=r, w=[], dq=("dbg", name))

    def build(self):
        nc = bass.Bass("TRN2", target_bir_lowering=False)
        self.nc = nc
        self.P = Prog()

        def din(name, shape):
            return nc.dram_tensor(name, list(shape), F32, kind="ExternalInput").ap()

        def dout(name, shape):
            return nc.dram_tensor(name, list(shape), F32, kind="ExternalOutput").ap()

        I = {}
        I["cst"] = din("cst", [128, NCST])
        I["xp"] = din("xp", [1024, D])
        I["xs"] = din("xs", [256, D])
        I["ada_w"] = din("ada_w", [4, D, 1536])
        I["ada_b"] = din("ada_b", [4, 1536])
        I["ffn_w_in"] = din("ffn_w_in", [4, D, 2 * FH])
        I["ffn_w_out"] = din("ffn_w_out", [4, FH, D])
        I["ab_w_in"] = din("ab_w_in", [2, D, 2240])
        I["ab_w_out"] = din("ab_w_out", [2, D, D])
        I["a_w2"] = din("a_w2", [2, 2, 16, 256])
        I["a_b"] = din("a_b", [2, 2, 256])
        I["w_qb"] = din("w_qb", [2, 384, 768])
        I["w_kvb"] = din("w_kvb", [2, 256, 1024])
        I["gqa_w_in"] = din("gqa_w_in", [2, D, 1536])
        I["gqa_w_out"] = din("gqa_w_out", [2, D, D])
        I["c_ckv"] = din("c_ckv", [2, 512, 256])
        I["c_kpe"] = din("c_kpe", [2, 512, 32])
        I["c_gla"] = din("c_gla", [2, 2, 256, 128])
        I["c_gk"] = din("c_gk", [2, 512, 256])
        I["c_gv"] = din("c_gv", [2, 512, 256])
        I["ropeg"] = din("ropeg", [256, 2, 32])
        I["ropem"] = din("ropem", [256, 2, 16])
        I["ropem_all"] = din("ropem_all", [1024, 2, 16])
        O = {}
        O["yp"] = dout("yp", [1024, D])
        O["ys"] = dout("ys", [256, D])
        O["o_ckv"] = dout("o_ckv", [4, 2, 256, 256])
        O["o_kpe"] = dout("o_kpe", [4, 2, 256, 32])
        O["o_gla"] = dout("o_gla", [4, 2, 2, 256, 128])
        O["o_gk"] = dout("o_gk", [4, 2, 256, 256])
        O["o_gv"] = dout("o_gv", [4, 2, 256, 256])
        self.I, self.O = I, O
        self.CC = {}
        self.CC[("a", "a")] = (nc.dram_tensor("ccs_a", [3, 6144], F32), nc.dram_tensor("ccd_a", [12, 6144], F32))
        for l in range(4):
            if l % 2 == 0:
                self.CC[(l, "g")] = (nc.dram_tensor("ccs_g%d" % l, [512, 129], F32), nc.dram_tensor("ccd_g%d" % l, [2048, 129], F32))
                self.CC[(l, "m")] = (nc.dram_tensor("ccs_m%d" % l, [256, 288], F32), nc.dram_tensor("ccd_m%d" % l, [1024, 288], F32))
            else:
                self.CC[(l, "c")] = (nc.dram_tensor("ccs_c%d" % l, [256, 512], F32), nc.dram_tensor("ccd_c%d" % l, [1024, 512], F32))

        with contextlib.ExitStack() as es:
            self.xT = es.enter_context(nc.sbuf_tensor("xT", [128, 8, NTOK], F32))
            self.cst_t = es.enter_context(nc.sbuf_tensor("cst_sb", [128, NCST], F32))
            small = es.enter_context(nc.sbuf_tensor("small", [128, 4 * 48 * 2 + 96 + 8], F32))
            cbf = es.enter_context(nc.sbuf_tensor("cbf", [128, 256 + 16], BF16))
            arena_t = es.enter_context(nc.sbuf_tensor("arena", [128, ARENA_BYTES // 2], BF16))
            self.A = Arena(arena_t, ARENA_BYTES)
            self.ps = [es.enter_context(nc.psum_tensor("ps%d" % i, [128, 512], F32)) for i in range(8)]
            self.psb = [p.bitcast(BF16) for p in self.ps]
            self.pools = {"mm": [[0, 1, 2, 3], 0], "acc": [[4, 5], 0], "tr": [[6, 7], 0]}
            self.modt = small[:, 0:384].rearrange("p (l c k) -> p l c k", l=4, c=48, k=2)
            self.lsc = small[:, 384:480].rearrange("p (g a b) -> p g a b", g=2, a=6, b=8)
            self.eps_c = small[:, 480:481]
            self.one_c = small[:, 481:482]
            self.ident_b = cbf[:, 0:128]
            self.ones_b = cbf[:, 128:256]
            self.sc_b = cbf[:, 256:272].rearrange("p (k c) -> p k c", k=8, c=2)
            self.ident_f = self.cst("ident")

            self.prologue()
            if "adaln" in self.stages:
                self.adaln_all()
            self.run_all()
            self.P.emit(nc, es)
        return nc

    def prologue(self):
        self.dma("sp", self.cst_t[:, :], self.I["cst"], r=[], w=["cst"], dq="cst")
        self.memset("dve", self.eps_c, EPS, w=["small_c"])
        self.memset("dve", self.one_c, 1.0, w=["small_c"])
        self.memset("dve", self.ones_b, 1.0, w=["ones"])
        self.cp("dve", self.ident_b, self.ident_f, r=["cst"], w=["identb"])
        cond = self.cst("cond").rearrange("p (k c) -> p k c", k=8, c=2)
        self.act(self.sc_b, cond, AF.Silu, r=["cst"], w=["scb"])

    def adaln_all(self):
        A = self.A
        A.reset()
        slots = [A.alloc([8, 512], BF16) for _ in range(3)]
        mq = A.alloc([6144], F32)
        mtok = A.alloc([6144], F32)
        rm = self.cst("rm", parts=3)
        cc_src, cc_dst = self.CC[("a", "a")]
        k = 0
        for l in range(4):
            self.dma("sp", mq[2:3, l * 1536:(l + 1) * 1536], self.I["ada_b"][l:l + 1, :], r=[], w=[("mq", "b")], dq="mtokb")
            for j in range(3):
                s = k % 3
                k += 1
                src = self.I["ada_w"][l, :, j * 512:(j + 1) * 512].rearrange("(k p) n -> p k n", p=128)
                self.dma("pool", slots[s], src, r=[], w=[("adw", s)], dq=("adw", s))
                b = self.bank("mm")
                for kc in range(8):
                    self.mm(self.ps[b][0:2, :], self.sc_b[:, kc, :], slots[s][:, kc, :], kc == 0, kc == 7,
                            r=[("adw", s), "scb"], w=[("ps", b)])
                c0 = l * 1536 + j * 512
                self.cp("act", mq[0:2, c0:c0 + 512], self.ps[b][0:2, :], r=[("ps", b)], w=[("mq", l, j)])
        self.dma("sp", cc_src.ap(), mq[0:3, :], r=[("mq", "b")] + [("mq", l, j) for l in range(4) for j in range(3)],
                 w=["cc_src_a"], dq="bnc_a")
        self.allgather(("a", "a"), r=["cc_src_a"], w=["cc_dst_a"])
        dview = cc_dst.ap().rearrange("(j r) (l c) -> r l j c", r=3, l=4)
        for l in range(4):
            self.dma("sp", mtok[0:3, :].rearrange("r (j c) -> r j c", j=4), dview[:, l, :, :], r=["cc_dst_a"], w=["mtok"], dq="mtok")
            b = self.bank("mm")
            for c in range(48):
                self.mm(self.ps[b][:, 2 * c:2 * c + 2], mtok[0:3, c * 128:(c + 1) * 128], rm, True, True,
                        r=["mtok", "cst"], w=[("ps", b)])
            self.cp("dve", self.modt[:, l, :, :], self.ps[b][:, 0:96].rearrange("p (c k) -> p c k", c=48, k=2),
                    r=[("ps", b)], w=["modt"])
        self.P.barrier()

    def layer_scalars(self, l):
        gmix = self.cst("gmix").rearrange("p (l c) -> p l c", l=4, c=8)[:, l, :]
        gffn = self.cst("gffn").rearrange("p (l c) -> p l c", l=4, c=8)[:, l, :]
        for col in range(2):
            mv = self.modt[:, l, :, col]
            L = self.lsc[:, col, :, :]
            self.stt(L[:, 0, :], mv[:, 8:16], 1.0, gmix, ALU.add, ALU.mult, r=["modt", "cst"], w=["lsc"])
            self.cp("dve", L[:, 1, :], mv[:, 0:8], r=["modt"], w=["lsc"])
            self.cp("dve", L[:, 2, :], mv[:, 16:24], r=["modt"], w=["lsc"])
            self.stt(L[:, 3, :], mv[:, 32:40], 1.0, gffn, ALU.add, ALU.mult, r=["modt", "cst"], w=["lsc"])
            self.cp("dve", L[:, 4, :], mv[:, 24:32], r=["modt"], w=["lsc"])
            self.cp("dve", L[:, 5, :], mv[:, 40:48], r=["modt"], w=["lsc"])

    def alloc_norm_tmp(self):
        A = self.A
        self.nm_sq = A.alloc([8, 128], BF16)
        self.nm_rstd = A.alloc([128], F32)
        self.nm_tmp = A.alloc([8, 128], F32)

    def normmod(self, t, which, dst, dstkey):
        xv = self.xT[:, :, t * 128:(t + 1) * 128]
        xk = [("xT", t, c) for c in range(8)]
        grp = 0 if t < 8 else 1
        G = self.lsc[:, grp, 3 * which, :]
        SH = self.lsc[:, grp, 3 * which + 1, :]
        self.tt("pool", self.nm_tmp, xv, G.unsqueeze(2).broadcast_to([128, 8, 128]), ALU.mult, r=xk + ["lsc"], w=["nm_tmp"])
        self.act(self.nm_sq, xv, AF.Square, r=xk, w=["nm_sq"])
        b = self.bank("tr")
        for c in range(8):
            self.mm(self.ps[b][:, 0:128], self.ones_b, self.nm_sq[:, c, :], c == 0, c == 7, r=["nm_sq", "ones"], w=[("ps", b)])
        self.rstd(self.nm_rstd, self.ps[b][:, 0:128], D, r=[("ps", b)], w=["nm_rstd"])
        self.tt("dve", self.nm_tmp, self.nm_tmp, self.nm_rstd.unsqueeze(1).broadcast_to([128, 8, 128]), ALU.mult,
                r=["nm_tmp", "nm_rstd"], w=["nm_tmp"])
        self.tt("dve", dst, self.nm_tmp, SH.unsqueeze(2).broadcast_to([128, 8, 128]), ALU.add,
                r=["nm_tmp", "lsc"], w=[dstkey])

    def run_all(self):
        self.seqs = [dict(tiles=[2 * s, 2 * s + 1], ctx=False, rope=False, bidx=s) for s in range(4)]
        self.sseg = dict(tiles=[8, 9], ctx=True, rope=True, bidx=None)
        A = self.A
        A.reset()
        xin = [A.alloc([1024], F32) for _ in range(2)]
        for t in range(NTILE):
            s = t % 2
            src = self.I["xp"][t * 128:(t + 1) * 128, :] if t < 8 else self.I["xs"][(t - 8) * 128:(t - 7) * 128, :]
            self.dma("sp", xin[s], src, r=[], w=[("xin", s)], dq=("xin", s))
            for hb in range(2):
                b = self.bank("tr")
                for cc in range(4):
                    c = hb * 4 + cc
                    self.tr(self.ps[b][:, cc * 128:(cc + 1) * 128], xin[s][:, c * 128:(c + 1) * 128], self.ident_f,
                            r=[("xin", s), "cst"], w=[("ps", b)])
                self.cp("dve" if hb else "act", self.xT[:, hb * 4:hb * 4 + 4, t * 128:(t + 1) * 128],
                        self.ps[b][:, :].rearrange("p (a b) -> p a b", a=4),
                        r=[("ps", b)], w=[("xT", t, hb * 4 + cc) for cc in range(4)])
        self.P.barrier()
        for l in range(4):
            self.layer_scalars(l)
            if l % 2 == 0:
                if "mixab" in self.stages:
                    self.mixer_ab(l, l // 2)
            else:
                if "mixc" in self.stages:
                    self.mixer_c(l, l // 2)
            self.P.barrier()
            if "ffn" in self.stages:
                self.ffn(l)
            self.P.barrier()
        A.reset()
        yo = [A.alloc([1024], F32) for _ in range(2)]
        for t in range(NTILE):
            s = t % 2
            for hb in range(2):
                b = self.bank("tr")
                for cc in range(4):
                    c = hb * 4 + cc
                    self.tr(self.ps[b][:, cc * 128:(cc + 1) * 128], self.xT[:, c, t * 128:(t + 1) * 128], self.ident_f,
                            r=[("xT", t, c), "cst"], w=[("ps", b)])
                self.cp("dve" if hb else "act", yo[s][:, hb * 512:(hb + 1) * 512], self.ps[b][:, :],
                        r=[("ps", b)], w=[("yo", s, hb)])
            dst = self.O["yp"][t * 128:(t + 1) * 128, :] if t < 8 else self.O["ys"][(t - 8) * 128:(t - 7) * 128, :]
            self.dma("sp", dst, yo[s], r=[("yo", s, 0), ("yo", s, 1)], w=[], dq=("yo", s))

    def ffn(self, l):
        A = self.A
        A.reset()
        hT = A.alloc([8, NTOK], BF16)
        actT = A.alloc([22, NTOK], BF16)
        wi = [A.alloc([8, 2, 256], BF16) for _ in range(2)]
        wo = [A.alloc([11, 1024], BF16) for _ in range(2)]
        sg = [A.alloc([512], F32) for _ in range(2)]
        self.alloc_norm_tmp()
        W1 = self.I["ffn_w_in"]
        W2 = self.I["ffn_w_out"]
        TB = [(0, 512, 0, [0, 1, 2, 3]), (512, 512, 0, [4, 5, 6, 7]), (1024, 256, 1, [8, 9])]
        NS = len(wi)

        def load_wi(j2):
            s = j2 % NS
            for gu in range(2):
                c0 = gu * FH + j2 * 256
                src = W1[l, :, c0:c0 + 256].rearrange("(k p) n -> p k n", p=128)
                self.dma("pool", wi[s][:, :, gu, :], src, r=[], w=[("wi", s, gu)], dq=("wi", s))

        def load_wo(hf):
            src = W2[l, hf * 1408:(hf + 1) * 1408, :].rearrange("(j p) n -> p j n", p=128)
            self.dma("pool", wo[hf], src, r=[], w=[("wo", hf)], dq=("wo", hf))

        for j2 in range(NS):
            load_wi(j2)
        for t in range(NTILE):
            self.normmod(t, 1, hT[:, :, t * 128:(t + 1) * 128], ("hT", t))
        load_wo(0)
        load_wo(1)
        k = 0
        for j2 in range(11):
            s = j2 % NS
            for (t0, tn, grp, tl) in TB:
                hk = [("hT", q) for q in tl]
                for hf in range(2):
                    j = j2 * 2 + hf
                    bg = self.bank("mm")
                    for kc in range(8):
                        self.mm(self.ps[bg][:, 0:tn], wi[s][:, kc, 0, hf * 128:(hf + 1) * 128], hT[:, kc, t0:t0 + tn],
                                kc == 0, kc == 7, r=[("wi", s, 0)] + hk, w=[("ps", bg)])
                    bu = self.bank("mm")
                    for kc in range(8):
                        self.mm(self.ps[bu][:, 0:tn], wi[s][:, kc, 1, hf * 128:(hf + 1) * 128], hT[:, kc, t0:t0 + tn],
                                kc == 0, kc == 7, r=[("wi", s, 1)] + hk, w=[("ps", bu)])
                    sgi = k % 2
                    k += 1
                    self.act(sg[sgi][:, 0:tn], self.ps[bg][:, 0:tn], AF.Silu, r=[("ps", bg)], w=[("sg", sgi)])
                    self.tt("dve", actT[:, j, t0:t0 + tn], sg[sgi][:, 0:tn], self.ps[bu][:, 0:tn], ALU.mult,
                            r=[("sg", sgi), ("ps", bu)], w=[("act", j, t0)])
            if j2 + NS < 11:
                load_wi(j2 + NS)
        for hf in range(2):
            for c in range(8):
                for (t0, tn, grp, tl) in TB:
                    gate = self.lsc[:, grp, 5, :]
                    b = self.bank("mm")
                    for jj in range(11):
                        self.mm(self.ps[b][:, 0:tn], wo[hf][:, jj, c * 128:(c + 1) * 128], actT[:, hf * 11 + jj, t0:t0 + tn],
                                jj == 0, jj == 10, r=[("wo", hf), ("act", hf * 11 + jj, t0)], w=[("ps", b)])
                    xv = self.xT[:, c, t0:t0 + tn]
                    xk = [("xT", q, c) for q in tl]
                    self.stt(xv, self.ps[b][:, 0:tn], gate[:, c:c + 1], xv, ALU.mult, ALU.add, r=[("ps", b), "lsc"] + xk, w=xk)

    def mixer_residual(self, t, banks):
        grp = 0 if t < 8 else 1
        gate = self.lsc[:, grp, 2, :]
        tmp = self.nm_tmp
        for hb in range(2):
            b = banks[hb]
            pv = self.ps[b][:, :].rearrange("p (a b) -> p a b", a=4)
            tv = tmp[:, hb * 4:hb * 4 + 4, :]
            self.tt("dve", tv, pv, gate[:, hb * 4:hb * 4 + 4].unsqueeze(2).broadcast_to([128, 4, 128]), ALU.mult,
                    r=[("ps", b), "lsc"], w=["nm_tmp"])
            xv = self.xT[:, hb * 4:hb * 4 + 4, t * 128:(t + 1) * 128]
            xk = [("xT", t, hb * 4 + q) for q in range(4)]
            self.tt("pool", xv, xv, tv, ALU.add, r=["nm_tmp"] + xk, w=xk)

    def out_proj_pair(self, t0, w_out, rhs_of, rkeys):
        grp = 0 if t0 < 8 else 1
        gate = self.lsc[:, grp, 2, :]
        banks = []
        for bi in range(4):
            b = self.bank("mm")
            banks.append(b)
            for cc in range(2):
                c = 2 * bi + cc
                for kc in range(8):
                    self.mm(self.ps[b][:, cc * 256:(cc + 1) * 256], w_out[:, kc, c * 128:(c + 1) * 128], rhs_of(kc), kc == 0, kc == 7,
                            r=["w_out"] + rkeys, w=[("ps", b)])
        tmpv = self.nm_tmp.rearrange("p c t -> p (c t)")
        for bi in range(4):
            b = banks[bi]
            pv = self.ps[b][:, :].rearrange("p (a b) -> p a b", a=2)
            tv = tmpv[:, (bi % 2) * 512:(bi % 2 + 1) * 512].rearrange("p (a b) -> p a b", a=2)
            self.tt("dve", tv, pv, gate[:, 2 * bi:2 * bi + 2].unsqueeze(2).broadcast_to([128, 2, 256]), ALU.mult,
                    r=[("ps", b), "lsc"], w=["nm_tmp"])
            xv = self.xT[:, 2 * bi:2 * bi + 2, t0 * 128:(t0 + 2) * 128]
            xk = [("xT", t0 + q, 2 * bi + cc) for q in range(2) for cc in range(2)]
            self.tt("pool", xv, xv, tv, ALU.add, r=["nm_tmp"] + xk, w=xk)

    def rope(self, xv, H, Q, cs, tmps, r, w, keyp):
        x1 = xv[:, :, :, 0, :]
        x2 = xv[:, :, :, 1, :]
        c = cs[:, 0, :].rearrange("p (a q) -> p a q", a=2, q=Q).unsqueeze(1).broadcast_to([128, H, 2, Q])
        s = cs[:, 1, :].rearrange("p (a q) -> p a q", a=2, q=Q).unsqueeze(1).broadcast_to([128, H, 2, Q])
        t1, t2, t3, t4 = tmps
        k1, k2, k3, k4 = [(keyp, i) for i in range(4)]
        self.tt("dve", t1, x1, c, ALU.mult, r=r, w=[k1])
        self.tt("pool", t2, x2, s, ALU.mult, r=r, w=[k2])
        self.tt("dve", t3, x1, s, ALU.mult, r=r, w=[k3])
        self.tt("pool", t4, x2, c, ALU.mult, r=r, w=[k4])
        self.tt("dve", x1, t1, t2, ALU.subtract, r=[k1, k2, k3], w=w)
        self.tt("dve", x2, t3, t4, ALU.add, r=[k3, k4], w=w)

    def attention(self, KT, kparts, kt_list, QTv, nq, VA_of, scale, Pt, out_of, kr, qr, vr, ow):
        ob = self.bank("acc")
        first, last = kt_list[0], kt_list[-1]
        npt = len(Pt)

        def pv(st, pi):
            for j in range(nq):
                self.mm(self.ps[ob][:, j * 65:(j + 1) * 65], Pt[pi][:, j, :], VA_of(st), (st == first and j == 0), st == last,
                        r=[("Pt", pi)] + vr, w=[("ps", ob)], skip_group_check=True)

        LOOK = 2
        rd = self.at_rden
        ov = self.ps[ob][:, 0:nq * 65].rearrange("p (a b) -> p a b", a=nq)

        def tail(pend):
            for p in pend:
                pv(*p)
            self.recip(rd[:, 0:nq], ov[:, :, 64], r=[("ps", ob)], w=["at_rden"])
            for j in range(nq):
                dst, dk = out_of(j)
                self.P.op("dve", lambda h, j=j, dst=dst, rd=rd, ov=ov: h.tensor_scalar(dst, ov[:, j, 0:64], rd[:, j:j + 1], None, ALU.mult),
                          r=[("ps", ob), "at_rden"], w=[dk])

        prev_tail = getattr(self, "_attn_tail", None)
        self._attn_tail = None
        pend = []
        nfirst = min(LOOK, len(kt_list))
        for idx, st in enumerate(kt_list):
            sb = self.bank("mm")
            self.mm(self.ps[sb][:, 0:nq * 128], KT(st), QTv, True, True, r=kr + qr, w=[("ps", sb)])
            pi = self.pt_i % npt
            self.pt_i += 1
            self.act(Pt[pi][:, 0:nq, :], self.ps[sb][:, 0:nq * 128].rearrange("p (a b) -> p a b", a=nq), AF.Exp,
                     r=[("ps", sb)], w=[("Pt", pi)], scale=scale)
            pend.append((st, pi))
            if idx == nfirst - 1 and prev_tail is not None:
                prev_tail()
                prev_tail = None
            if len(pend) > LOOK:
                pv(*pend.pop(0))
        if prev_tail is not None:
            prev_tail()
        self._attn_tail = lambda pend=pend: tail(pend)

    def attn_flush(self):
        t = getattr(self, "_attn_tail", None)
        if t is not None:
            self._attn_tail = None
            t()

    def mixer_c(self, l, i):
        A = self.A
        A.reset()
        I, O = self.I, self.O
        w_in = A.alloc([8, 1536], BF16)
        w_out = A.alloc([8, 1024], BF16)
        hT = A.alloc([8, 256], BF16)
        KT = A.alloc([4, 1536], BF16)
        VA = A.alloc([12, 4, 65], BF16)
        self.alloc_norm_tmp()
        kvf = A.alloc([512], F32)
        sq = A.alloc([1024], F32)
        qn = A.alloc([1024], F32)
        st16 = A.alloc([16], F32)
        knb = A.alloc([256], BF16)
        qnb = A.alloc([1024], BF16)
        QT = A.alloc([16, 128], BF16)
        Pt = [A.alloc([4, 128], BF16) for _ in range(4)]
        ob = A.alloc([1024], BF16)
        OT = A.alloc([8, 256], BF16)
        self.at_rden = A.alloc([4], F32)
        rtm = [A.alloc([16, 2, 16], F32) for _ in range(4)]
        ropeg = A.alloc([2, 2, 32], F32)
        ck = A.alloc([4, 256], F32)
        cv = A.alloc([4, 256], F32)
        ckb = A.alloc([4, 256], BF16)
        kg = [A.alloc([512], F32) for _ in range(2)]
        self.pt_i = 0
        gq = self.cst("gq%d" % i)
        gk = self.cst("gk%d" % i)
        for kh in range(2):
            self.dma("pool", w_in[:, kh * 4:(kh + 1) * 4, :],
                     I["gqa_w_in"][i, kh * 512:(kh + 1) * 512, :].rearrange("(k p) n -> p k n", p=128), r=[], w=[("w_in", kh)], dq="w_in")
        self.dma("pool", w_out, I["gqa_w_out"][i].rearrange("(k p) n -> p k n", p=128), r=[], w=["w_out"], dq="w_out")
        self.memset("pool", VA[:, :, :, 64:65], 1.0, w=["VA1"])
        wk = [("w_in", 0), ("w_in", 1)]
        self.dma("sp", ropeg, I["ropeg"].rearrange("(t p) c q -> p t c q", p=128), r=[], w=["ropeg"], dq="ropeg")
        cc_src, cc_dst = self.CC[(l, "c")]

        def put_keys(knb_ap, kt, rk):
            b = self.bank("tr")
            for g in range(4):
                self.tr(self.psb[b][0:64, g * 128:(g + 1) * 128], knb_ap[:, g * 64:(g + 1) * 64], self.ident_b,
                        r=rk + ["identb"], w=[("ps", b)])
            self.cp("dve", KT[0:64, :, kt * 128:(kt + 1) * 128], self.psb[b][0:64, 0:512].rearrange("p (a b) -> p a b", a=4),
                    r=[("ps", b)], w=[("KT", kt)])

        def kv_side(t, n, rope_n):
            self.normmod(t, 0, hT[:, :, n * 128:(n + 1) * 128], ("hT", n))
            b = self.bank("mm")
            for kc in range(8):
                self.mm(self.ps[b][:, :], hT[:, kc, n * 128:(n + 1) * 128], w_in[:, kc, 1024:1536], kc == 0, kc == 7,
                        r=[("hT", n)] + wk, w=[("ps", b)])
            self.cp("act", kvf, self.ps[b][:, :], r=[("ps", b)], w=["kvf"])
            self.act(sq[:, 0:256], self.ps[b][:, 0:256], AF.Square, r=[("ps", b)], w=["sq"])
            self.red(st16[:, 0:4], sq[:, 0:256].rearrange("p (g d) -> p g d", g=4), r=["sq"], w=["st16"])
            self.rstd(st16[:, 0:4], st16[:, 0:4], 64, r=["st16"], w=["st16"])
            kv3 = kvf[:, 0:256].rearrange("p (g d) -> p g d", g=4)
            self.tt("dve", kv3, kv3, st16[:, 0:4].unsqueeze(2).broadcast_to([128, 4, 64]), ALU.mult, r=["kvf", "st16"], w=["kvf"])
            self.tt("dve", kv3, kv3, gk.unsqueeze(1).broadcast_to([128, 4, 64]), ALU.mult, r=["kvf", "cst"], w=["kvf"])
            if rope_n is not None:
                self.rope(kvf[:, 0:256].rearrange("p (h a b q) -> p h a b q", h=4, a=2, b=2, q=16), 4, 16, ropeg[:, rope_n, :, :],
                          [x[:, 0:4, :, :] for x in rtm], r=["kvf", "ropeg"], w=["kvf"], keyp="rtm")

        def q_attn_out(t, n, rope_n, kt_list):
            kkeys = [("KT", kt) for kt in kt_list]
            vkeys = [("VA", kt) for kt in kt_list] + ["VA1"]
            qb = []
            for bk in range(2):
                b = self.bank("mm")
                qb.append(b)
                for kc in range(8):
                    self.mm(self.ps[b][:, :], hT[:, kc, n * 128:(n + 1) * 128], w_in[:, kc, bk * 512:(bk + 1) * 512], kc == 0, kc == 7,
                            r=[("hT", n)] + wk, w=[("ps", b)])
                self.act(sq[:, bk * 512:(bk + 1) * 512], self.ps[b][:, :], AF.Square, r=[("ps", b)], w=["sq"])
            self.red(st16, sq.rearrange("p (g d) -> p g d", g=16), r=["sq"], w=["st16"])
            self.rstd(st16, st16, 64, r=["st16"], w=["st16"])
            for bk in range(2):
                self.tt("dve", qn[:, bk * 512:(bk + 1) * 512].rearrange("p (g d) -> p g d", g=8),
                        self.ps[qb[bk]][:, :].rearrange("p (g d) -> p g d", g=8),
                        st16[:, bk * 8:(bk + 1) * 8].unsqueeze(2).broadcast_to([128, 8, 64]), ALU.mult,
                        r=[("ps", qb[bk]), "st16"], w=["qn"])
            qn3 = qn.rearrange("p (g d) -> p g d", g=16)
            self.tt("dve", qn3, qn3, gq.unsqueeze(1).broadcast_to([128, 16, 64]), ALU.mult, r=["qn", "cst"], w=["qn"])
            if rope_n is not None:
                self.rope(qn.rearrange("p (h a b q) -> p h a b q", h=16, a=2, b=2, q=16), 16, 16, ropeg[:, rope_n, :, :], rtm,
                          r=["qn", "ropeg"], w=["qn"], keyp="rtm")
            self.cp("act", qnb, qn, r=["qn"], w=["qnb"])
            for hb in range(2):
                b = self.bank("tr")
                for hh in range(8):
                    h_ = hb * 8 + hh
                    self.tr(self.psb[b][0:64, hh * 128:(hh + 1) * 128], qnb[:, h_ * 64:(h_ + 1) * 64], self.ident_b,
                            r=["qnb", "identb"], w=[("ps", b)])
                self.cp("dve" if hb else "act", QT[0:64, hb * 8:(hb + 1) * 8, :],
                        self.psb[b][0:64, :].rearrange("p (a b) -> p a b", a=8), r=[("ps", b)], w=[("QT", hb)])
            for g in range(4):
                self.attention(
                    KT=lambda st, g=g: KT[0:64, g, st * 128:(st + 1) * 128], kparts=64, kt_list=kt_list,
                    QTv=QT[0:64, 4 * g:4 * g + 4, :], nq=4,
                    VA_of=lambda st, g=g: VA[:, st, g, :], scale=0.125, Pt=Pt,
                    out_of=lambda j, g=g: (ob[:, (4 * g + j) * 64:(4 * g + j + 1) * 64], ("ob", g)),
                    kr=kkeys, qr=[("QT", g // 2)], vr=vkeys, ow=None)
            self.attn_flush()
            b = self.bank("tr")
            for c in range(8):
                self.tr(self.psb[b][:, c * 128:(c + 1) * 128], ob[:, c * 128:(c + 1) * 128], self.ident_b,
                        r=[("ob", c // 2), "identb"], w=[("ps", b)])
            self.cp("act", OT[:, :, n * 128:(n + 1) * 128], self.psb[b][:, :].rearrange("p (a b) -> p a b", a=8), r=[("ps", b)], w=[("OT", n)])
            if n == 1:
                self.out_proj_pair(t - 1, w_out, lambda kc: OT[:, kc, :], [("OT", 0), ("OT", 1)])

        cck = "cc_src_c%d" % l
        ccd = "cc_dst_c%d" % l
        for n, t in enumerate(self.sseg["tiles"]):
            kv_side(t, n, n)
            self.dma("sp", cc_src[n * 128:(n + 1) * 128, :], kvf, r=["kvf"], w=[cck], dq="ccb_c")
        self.allgather((l, "c"), r=[cck], w=[ccd])
        for sq_ in self.seqs:
            tiles = sq_["tiles"]
            bi = sq_["bidx"]
            for n, t in enumerate(tiles):
                kv_side(t, n, None)
                self.dma("sp", O["o_gk"][bi, i, n * 128:(n + 1) * 128, :], kvf[:, 0:256], r=["kvf"], w=[], dq="kvf_o")
                self.dma("sp", O["o_gv"][bi, i, n * 128:(n + 1) * 128, :], kvf[:, 256:512], r=["kvf"], w=[], dq="kvf_o")
                self.cp("act", knb, kvf[:, 0:256], r=["kvf"], w=["knb"])
                put_keys(knb, n, ["knb"])
                self.cp("pool", VA[:, n, :, 0:64], kvf[:, 256:512].rearrange("p (g d) -> p g d", g=4), r=["kvf"], w=[("VA", n)])
            for n, t in enumerate(tiles):
                q_attn_out(t, n, None, [0, 1])
        self.dma("sp", ck, I["c_gk"][i].rearrange("(t p) n -> p t n", p=128), r=[], w=["ck"], dq="ck")
        self.dma("sp", cv, I["c_gv"][i].rearrange("(t p) n -> p t n", p=128), r=[], w=["cv"], dq="cv")
        self.cp("act", ckb, ck, r=["ck"], w=["ckb"])
        for kt in range(4):
            put_keys(ckb[:, kt, :], kt, ["ckb"])
            self.cp("pool", VA[:, kt, :, 0:64], cv[:, kt, :].rearrange("p (g d) -> p g d", g=4), r=["cv"], w=[("VA", kt)])
        for k in range(8):
            kgk = kg[k % 2]
            kk = ("kg", k % 2)
            self.dma("sp", kgk, cc_dst[k * 128:(k + 1) * 128, :], r=[ccd], w=[kk], dq=kk)
            self.cp("act", knb, kgk[:, 0:256], r=[kk], w=["knb"])
            put_keys(knb, 4 + k, ["knb"])
            self.cp("pool", VA[:, 4 + k, :, 0:64], kgk[:, 256:512].rearrange("p (g d) -> p g d", g=4), r=[kk], w=[("VA", 4 + k)])
        for n, t in enumerate(self.sseg["tiles"]):
            self.normmod(t, 0, hT[:, :, n * 128:(n + 1) * 128], ("hT", n))
            q_attn_out(t, n, n, list(range(12)))

    def mixer_ab(self, l, i):
        A = self.A
        A.reset()
        I, O = self.I, self.O
        w_in = A.alloc([8, 2240], BF16)
        w_out = A.alloc([8, 1024], BF16)
        OG = A.alloc([4, 256], BF16)
        hTm = A.alloc([8, 128], BF16)
        hk = "hTm"
        self.alloc_norm_tmp()
        self.at_rden = A.alloc([4], F32)
        Pt = [A.alloc([4, 128], BF16) for _ in range(4)]
        self.pt_i = 0
        SP = dict(qlT=[A.alloc([2, 256], BF16) for _ in range(2)], oT=A.alloc([4, 256], F32), rsT=A.alloc([4, 256], BF16),
                  etot=A.alloc([4, 2], F32), cqb=A.alloc([2, 384], BF16), aseg=A.alloc([4], F32))
        base_off = A.off
        for kh in range(2):
            for ch in range(2):
                self.dma("pool", w_in[:, kh * 4:(kh + 1) * 4, ch * 1120:(ch + 1) * 1120],
                         I["ab_w_in"][i, kh * 512:(kh + 1) * 512, ch * 1120:(ch + 1) * 1120].rearrange("(k p) n -> p k n", p=128),
                         r=[], w=[("w_in", kh, ch)], dq="w_in")
        self.dma("pool", w_out, I["ab_w_out"][i].rearrange("(k p) n -> p k n", p=128), r=[], w=["w_out"], dq="w_out")
        wk = [("w_in", 0, 0), ("w_in", 0, 1), ("w_in", 1, 0), ("w_in", 1, 1)]
        gout = self.cst("gout%d" % i)
        gqn = self.cst("gqn%d" % i)
        gkvn = self.cst("gkvn%d" % i)
        gq96 = self.cst("gq96%d" % i)
        gk96 = self.cst("gk96%d" % i)
        sel = self.cst("sel")
        trim = [self.cst("trim0"), self.cst("trim1")]
        tris = [self.cst("tris0"), self.cst("tris1")]
        mask = [self.cst("mask0"), self.cst("mask1")]
        ccg_src, ccg_dst = self.CC[(l, "g")]
        ccm_src, ccm_dst = self.CC[(l, "m")]

        def g_alloc(NT, persist=None):
            T = NT * 128
            B = {}
            if persist is None:
                B["qlT"] = [A.alloc([2, T], BF16) for _ in range(2)]
                B["oT"] = A.alloc([4, T], F32)
                B["rsT"] = A.alloc([4, T], BF16)
                B["etot"] = A.alloc([4, NT], F32)
            else:
                for k in ("qlT", "oT", "rsT", "etot"):
                    B[k] = persist[k]
            B["klT"] = [A.alloc([2, T], BF16) for _ in range(2)]
            B["kst"] = [A.alloc([NT, 256], BF16) for _ in range(2)]
            B["vtk"] = A.alloc([NT, 512], BF16)
            B["Sst"] = [[A.alloc([128], F32) for _ in range(2)] for _ in range(2)]
            B["Sbf"] = [[A.alloc([128], BF16) for _ in range(2)] for _ in range(2)]
            B["alT"] = A.alloc([128], F32)
            B["lsp"] = A.alloc([512], F32)
            B["Eb"] = A.alloc([4, 128], F32)
            B["Enb"] = A.alloc([4, 128], F32)
            B["Ed2"] = A.alloc([512], F32)
            B["ATm"] = [A.alloc([128], BF16) for _ in range(4)]
            B["aw2"] = A.alloc([512], F32)
            self.memset("dve", B["alT"][32:33, :], 1.0, w=["alT1"])
            aw2 = B["aw2"]
            self.memset("dve", aw2[0:33, :], 0.0, w=["aw2"])
            for z in range(2):
                self.dma("sp", aw2[16 * z:16 * z + 16, z * 256:(z + 1) * 256], I["a_w2"][i, z], r=[], w=["aw2"], dq="aw2")
            self.dma("sp", aw2[32:33, :], I["a_b"][i:i + 1].rearrange("o z n -> o (z n)"), r=[], w=["aw2"], dq="aw2")
            return B

        def g_prep(tiles, B):
            qlT, klT, kst, vtk, rsT, etot = B["qlT"], B["klT"], B["kst"], B["vtk"], B["rsT"], B["etot"]
            alT, lsp, Eb, Enb, Ed2, aw2 = B["alT"], B["lsp"], B["Eb"], B["Enb"], B["Ed2"], B["aw2"]
            for n, t in enumerate(tiles):
                nc_ = slice(n * 128, (n + 1) * 128)
                h = hTm
                self.normmod(t, 0, h, hk)
                bqk = self.bank("mm")
                for ch in range(4):
                    for kc in range(8):
                        self.mm(self.ps[bqk][:, ch * 128:(ch + 1) * 128], w_in[:, kc, ch * 128:(ch + 1) * 128], h[:, kc, :], kc == 0, kc == 7,
                                r=[hk] + wk, w=[("ps", bqk)])
                br = self.bank("mm")
                for ch in range(4):
                    for kc in range(8):
                        self.mm(self.ps[br][:, ch * 128:(ch + 1) * 128], w_in[:, kc, 1024 + ch * 128:1024 + (ch + 1) * 128], h[:, kc, :], kc == 0, kc == 7,
                                r=[hk] + wk, w=[("ps", br)])
                self.act(rsT[:, :, nc_], self.ps[br][:, :].rearrange("p (a b) -> p a b", a=4), AF.Silu, r=[("ps", br)], w=[("rsT", n)])
                ba = self.bank("tr")
                for kc in range(8):
                    self.mm(self.ps[ba][0:32, 0:128], w_in[:, kc, 1536:1568], h[:, kc, :], kc == 0, kc == 7, r=[hk] + wk, w=[("ps", ba)])
                self.cp("act", alT[0:32, :], self.ps[ba][0:32, 0:128], r=[("ps", ba)], w=["alT"])
                bkv = self.bank("mm")
                for kc in range(8):
                    self.mm(self.ps[bkv][:, :], h[:, kc, :], w_in[:, kc, 256:768], kc == 0, kc == 7, r=[hk] + wk, w=[("ps", bkv)])
                bv2 = self.bank("mm")
                for kc in range(8):
                    self.mm(self.ps[bv2][:, 0:256], h[:, kc, :], w_in[:, kc, 768:1024], kc == 0, kc == 7, r=[hk] + wk, w=[("ps", bv2)])
                self.cp("act", vtk[:, n, 0:256], self.ps[bkv][:, 256:512], r=[("ps", bkv)], w=[("vtk", n, 0)])
                self.cp("act", vtk[:, n, 256:512], self.ps[bv2][:, 0:256], r=[("ps", bv2)], w=[("vtk", n, 1)])
                bl = self.bank("tr")
                self.mm(self.ps[bl][:, :], alT[0:33, :], aw2[0:33, :], True, True, r=["alT", "alT1", "aw2"], w=[("ps", bl)])
                self.act(lsp, self.ps[bl][:, :], AF.Exp, r=[("ps", bl)], w=["lsp"], scale=-1.0)
                self.act(lsp, lsp, AF.Ln, r=["lsp"], w=["lsp"], bias=1.0)
                bb = self.bank("tr")
                for z in range(2):
                    for fc in range(2):
                        zf = z * 2 + fc
                        self.mm(self.ps[bb][:, zf * 128:(zf + 1) * 128], lsp[:, zf * 128:(zf + 1) * 128], trim[z], True, True,
                                r=["lsp", "cst"], w=[("ps", bb)])
                bd = self.bank("tr")
                for z in range(2):
                    self.mm(self.ps[bd][:, z * 256:(z + 1) * 256], tris[z], lsp[:, z * 256:(z + 1) * 256], True, True,
                            r=["lsp", "cst"], w=[("ps", bd)])
                pbb = self.ps[bb][:, :].rearrange("p (a b) -> p a b", a=4)
                self.act(Eb, pbb, AF.Exp, r=[("ps", bb)], w=["Eb"])
                self.act(Enb, pbb, AF.Exp, r=[("ps", bb)], w=["Enb"], scale=-1.0)
                self.act(Ed2, self.ps[bd][:, :], AF.Exp, r=[("ps", bd)], w=["Ed2"])
                self.cp("pool", etot[:, 0:2, n], Eb[:, 0:2, 127], r=["Eb"], w=[("etot", n)])
                self.cp("pool", etot[:, 2:4, n], Eb[:, 2:4, 0], r=["Eb"], w=[("etot", n)])
                pqk = self.ps[bqk][:, :].rearrange("p (a b) -> p a b", a=4)
                for z in range(2):
                    self.stt(qlT[z][:, :, nc_], pqk[:, 0:2, :], 0.125, Eb[:, 2 * z:2 * z + 2, :], ALU.mult, ALU.mult,
                             r=[("ps", bqk), "Eb"], w=[("qlT", z, n)])
                    self.tt("dve", klT[z][:, :, nc_], pqk[:, 2:4, :], Enb[:, 2 * z:2 * z + 2, :], ALU.mult,
                            r=[("ps", bqk), "Enb"], w=[("klT", z, n)])
                    self.tt("dve", kst[z][:, n, :], self.ps[bkv][:, 0:256], Ed2[:, z * 256:(z + 1) * 256], ALU.mult,
                            r=[("ps", bkv), "Ed2"], w=[("kst", z, n)])

        def g_scan(NT, B, bidx):
            qlT, klT, kst, vtk, oT, etot, Sst, Sbf, ATm = (B[k] for k in ("qlT", "klT", "kst", "vtk", "oT", "etot", "Sst", "Sbf", "ATm"))
            for z in range(2):
                for fc in range(2):
                    self.memset("pool", Sst[z][fc], 0.0, w=[("S", z, fc)])
                    self.cp("act", Sbf[z][fc], Sst[z][fc], r=[("S", z, fc)], w=[("Sbf", z, fc)])
                order = list(range(NT)) if z == 0 else list(range(NT - 1, -1, -1))
                for n in order:
                    nc_ = slice(n * 128, (n + 1) * 128)
                    bo = self.bank("acc")
                    bats = []
                    for hh in range(4):
                        fc, hp = hh // 2, hh % 2
                        pr = slice(hp * 64, (hp + 1) * 64)
                        bat = self.bank("mm")
                        bats.append(bat)
                        self.mm(self.ps[bat][:, 0:128], klT[z][pr, fc, nc_], qlT[z][pr, fc, nc_], True, True,
                                r=[("klT", z, n), ("qlT", z, n)], w=[("ps", bat)])
                    for hh in range(4):
                        self.tt("dve", ATm[hh], self.ps[bats[hh]][:, 0:128], mask[z], ALU.mult, r=[("ps", bats[hh]), "cst"], w=[("ATm", hh)])
                    for hh in range(4):
                        fc, hp = hh // 2, hh % 2
                        pr = slice(hp * 64, (hp + 1) * 64)
                        self.mm(self.ps[bo][:, hh * 128:(hh + 1) * 128], vtk[:, n, hh * 128:(hh + 1) * 128], ATm[hh], True, False,
                                r=[("vtk", n, 0), ("vtk", n, 1), ("ATm", hh)], w=[("ps", bo)])
                        self.mm(self.ps[bo][:, hh * 128:(hh + 1) * 128], Sbf[z][fc][pr, :], qlT[z][pr, fc, nc_], False, True,
                                r=[("Sbf", z, fc), ("qlT", z, n)], w=[("ps", bo)])
                    pbo = self.ps[bo][:, :].rearrange("p (a b) -> p a b", a=4)
                    if z == 0:
                        self.cp("act", oT[:, :, nc_], pbo, r=[("ps", bo)], w=[("oT", n)])
                    else:
                        self.tt("dve", oT[:, :, nc_], oT[:, :, nc_], pbo, ALU.add, r=[("ps", bo), ("oT", n)], w=[("oT", n)])
                    bus = []
                    for fc in range(2):
                        bu = self.bank("mm")
                        bus.append(bu)
                        for hp in range(2):
                            hh = fc * 2 + hp
                            self.mm(self.ps[bu][hp * 64:(hp + 1) * 64, 0:128], kst[z][:, n, hh * 64:(hh + 1) * 64],
                                    vtk[:, n, hh * 128:(hh + 1) * 128], True, True,
                                    r=[("kst", z, n), ("vtk", n, 0), ("vtk", n, 1)], w=[("ps", bu)])
                    for fc in range(2):
                        self.stt(Sst[z][fc], Sst[z][fc], etot[:, z * 2 + fc, n:n + 1], self.ps[bus[fc]][:, 0:128], ALU.mult, ALU.add,
                                 r=[("S", z, fc), ("etot", n), ("ps", bus[fc])], w=[("S", z, fc)])
                    for fc in range(2):
                        self.cp("act", Sbf[z][fc], Sst[z][fc], r=[("S", z, fc)], w=[("Sbf", z, fc)])
                if bidx is not None:
                    for fc in range(2):
                        self.dma("sp", O["o_gla"][bidx, i, z, fc * 128:(fc + 1) * 128, :], Sst[z][fc],
                                 r=[("S", z, fc)], w=[], dq=("S", z, fc))

        def g_out_alloc():
            return (A.alloc([4, 128], BF16), A.alloc([4, 128], F32), A.alloc([4, 128], F32))

        def g_out(tiles, oT, rsT, tmps=None):
            osq, orst, otmp = tmps if tmps is not None else g_out_alloc()
            for n, t in enumerate(tiles):
                nc_ = slice(n * 128, (n + 1) * 128)
                self.act(osq, oT[:, :, nc_], AF.Square, r=[("oT", n)], w=["osq"])
                b = self.bank("tr")
                self.mm(self.ps[b][:, :], self.ones_b, osq.rearrange("p a b -> p (a b)"), True, True, r=["osq", "ones"], w=[("ps", b)])
                self.rstd(orst, self.ps[b][:, :].rearrange("p (a b) -> p a b", a=4), 128, r=[("ps", b)], w=["orst"])
                self.tt("dve", otmp, oT[:, :, nc_], orst, ALU.mult, r=[("oT", n), "orst"], w=["otmp"])
                self.stt(OG[:, :, nc_], otmp, gout, rsT[:, :, nc_], ALU.mult, ALU.mult,
                         r=["otmp", "cst", ("rsT", n)], w=[("OG", n)])

        def m_alloc(NK, rope=True):
            M = {}
            M["w_qb"] = A.alloc([3, 768], BF16)
            M["w_kvb"] = A.alloc([2, 1024], BF16)
            self.dma("pool", M["w_qb"], I["w_qb"][i].rearrange("(k p) n -> p k n", p=128), r=[], w=["w_qb"], dq="w_qb")
            self.dma("pool", M["w_kvb"], I["w_kvb"][i].rearrange("(k p) n -> p k n", p=128), r=[], w=["w_kvb"], dq="w_kvb")
            M["KTm"] = A.alloc([8, NK * 128], BF16)
            M["VAm"] = A.alloc([NK, 8, 65], BF16)
            M["QTm"] = A.alloc([8, 256], BF16)
            M["omb"] = A.alloc([2, 512], BF16)
            M["OM"] = A.alloc([4, 256], BF16)
            M["cb"] = A.alloc([256], BF16)
            M["cT"] = A.alloc([2, 128], BF16)
            M["kc96"] = A.alloc([8, 96], F32)
            M["knb"] = A.alloc([8, 96], BF16)
            M["st8"] = A.alloc([8], F32)
            M["cqT"] = A.alloc([3, 128], BF16)
            M["rtm"] = [A.alloc([8, 2, 8], F32) for _ in range(4)] if rope else None
            self.memset("pool", M["VAm"][:, :, :, 64:65], 1.0, w=["VA1"])
            return M

        def own_alloc():
            W = {}
            W["sq96"] = A.alloc([8, 96], F32)
            W["ckvf"] = A.alloc([256], F32)
            W["ckvn"] = A.alloc([256], F32)
            W["kpe"] = A.alloc([32], F32)
            W["st1"] = A.alloc([2], F32)
            return W

        def norm96(M, sq96, gain, rope_cs):
            kc96, knb, st8, rtm = M["kc96"], M["knb"], M["st8"], M["rtm"]
            self.act(sq96, kc96, AF.Square, r=["kc96"], w=["sq96"])
            self.red(st8, sq96, r=["sq96"], w=["st8"])
            self.rstd(st8, st8, 96, r=["st8"], w=["st8"])
            self.tt("dve", kc96, kc96, st8.unsqueeze(2).broadcast_to([128, 8, 96]), ALU.mult, r=["kc96", "st8"], w=["kc96"])
            self.tt("dve", kc96, kc96, gain.unsqueeze(1).broadcast_to([128, 8, 96]), ALU.mult, r=["kc96", "cst"], w=["kc96"])
            if rope_cs is not None:
                cs, csk = rope_cs
                self.rope(kc96[:, :, 64:96].rearrange("p h (a b q) -> p h a b q", a=2, b=2, q=8), 8, 8, cs, rtm,
                          r=["kc96", csk], w=["kc96"], keyp="rtm")
            self.cp("act", knb, kc96, r=["kc96"], w=["knb"])

        def kside(M, sq96, ckvn_ap, ckvn_k, kpe_ap, kpe_k, kt, rope_cs):
            cb, cT, kc96, knb, KTm, VAm, w_kvb = M["cb"], M["cT"], M["kc96"], M["knb"], M["KTm"], M["VAm"], M["w_kvb"]
            self.cp("act", cb, ckvn_ap, r=[ckvn_k], w=["cb"])
            b = self.bank("tr")
            for kc in range(2):
                self.tr(self.psb[b][:, kc * 128:(kc + 1) * 128], cb[:, kc * 128:(kc + 1) * 128], self.ident_b, r=["cb", "identb"], w=[("ps", b)])
            self.cp("dve", cT, self.psb[b][:, 0:256].rearrange("p (a b) -> p a b", a=2), r=[("ps", b)], w=["cT"])
            for bk in range(2):
                b = self.bank("mm")
                for kc in range(2):
                    self.mm(self.ps[b][:, :], cT[:, kc, :], w_kvb[:, kc, bk * 512:(bk + 1) * 512], kc == 0, kc == 1,
                            r=["cT", "w_kvb"], w=[("ps", b)])
                pv = self.ps[b][:, :].rearrange("p (h d) -> p h d", h=4)
                self.cp("act", kc96[:, bk * 4:(bk + 1) * 4, 0:64], pv[:, :, 0:64], r=[("ps", b)], w=["kc96"])
                self.cp("dve", VAm[:, kt, bk * 4:(bk + 1) * 4, 0:64], pv[:, :, 64:128], r=[("ps", b)], w=[("VAm", kt)])
            self.cp("pool", kc96[:, :, 64:96], kpe_ap.unsqueeze(1).broadcast_to([128, 8, 32]), r=[kpe_k], w=["kc96"])
            norm96(M, sq96, gk96, rope_cs)
            b = self.bank("tr")
            for hh in range(8):
                self.tr(self.psb[b][0:96, hh * 128:(hh + 1) * 128], knb[:, hh, :], self.ident_b, r=["knb", "identb"], w=[("ps", b)])
            self.cp("dve", KTm[0:96, :, kt * 128:(kt + 1) * 128], self.psb[b][0:96, :].rearrange("p (a b) -> p a b", a=8),
                    r=[("ps", b)], w=[("KTm", kt)])

        def m_own(t, n, W, cqb):
            sq96, ckvf, ckvn, kpe, st1 = W["sq96"], W["ckvf"], W["ckvn"], W["kpe"], W["st1"]
            h = hTm
            self.normmod(t, 0, h, hk)
            b1 = self.bank("mm")
            for kc in range(8):
                self.mm(self.ps[b1][:, :], h[:, kc, :], w_in[:, kc, 1568:2080], kc == 0, kc == 7, r=[hk] + wk, w=[("ps", b1)])
            b2 = self.bank("mm")
            for kc in range(8):
                self.mm(self.ps[b2][:, 0:160], h[:, kc, :], w_in[:, kc, 2080:2240], kc == 0, kc == 7, r=[hk] + wk, w=[("ps", b2)])
            sqf = sq96.rearrange("p a b -> p (a b)")
            self.act(sqf[:, 0:384], self.ps[b1][:, 0:384], AF.Square, r=[("ps", b1)], w=["sq96", "st1"], accum_out=st1[:, 0:1])
            self.rstd(st1[:, 0:1], st1[:, 0:1], 384, r=["st1"], w=["st1"])
            self.stt(cqb[:, n, :], self.ps[b1][:, 0:384], st1[:, 0:1], gqn, ALU.mult, ALU.mult, r=[("ps", b1), "st1", "cst"], w=[("cqb", n)])
            self.cp("act", ckvf[:, 0:128], self.ps[b1][:, 384:512], r=[("ps", b1)], w=["ckvf"])
            self.cp("act", ckvf[:, 128:256], self.ps[b2][:, 0:128], r=[("ps", b2)], w=["ckvf"])
            self.cp("dve", kpe, self.ps[b2][:, 128:160], r=[("ps", b2)], w=["kpe"])
            self.act(sqf[:, 0:256], ckvf, AF.Square, r=["ckvf"], w=["sq96", "st1b"], accum_out=st1[:, 1:2])
            self.rstd(st1[:, 1:2], st1[:, 1:2], 256, r=["st1b"], w=["st1b"])
            self.stt(ckvn, ckvf, st1[:, 1:2], gkvn, ALU.mult, ALU.mult, r=["ckvf", "st1b", "cst"], w=["ckvn"])

        def m_attn(M, sq96, tiles, cqb, rope_q, NK):
            QTm, omb, OM, KTm, VAm, cqT, kc96, knb, w_qb = (M[k] for k in ("QTm", "omb", "OM", "KTm", "VAm", "cqT", "kc96", "knb", "w_qb"))
            kt_list = list(range(NK))
            kkeys = [("KTm", kt) for kt in kt_list]
            vkeys = [("VAm", kt) for kt in kt_list] + ["VA1"]
            nq = len(tiles)
            for n, t in enumerate(tiles):
                b = self.bank("tr")
                for kc in range(3):
                    self.tr(self.psb[b][:, kc * 128:(kc + 1) * 128], cqb[:, n, kc * 128:(kc + 1) * 128], self.ident_b,
                            r=[("cqb", n), "identb"], w=[("ps", b)])
                self.cp("act", cqT, self.psb[b][:, 0:384].rearrange("p (a b) -> p a b", a=3), r=[("ps", b)], w=["cqT"])
                for bk in range(2):
                    b = self.bank("mm")
                    for kc in range(3):
                        self.mm(self.ps[b][:, 0:384], cqT[:, kc, :], w_qb[:, kc, bk * 384:(bk + 1) * 384], kc == 0, kc == 2,
                                r=["cqT", "w_qb"], w=[("ps", b)])
                    self.cp("act", kc96[:, bk * 4:(bk + 1) * 4, :], self.ps[b][:, 0:384].rearrange("p (h d) -> p h d", h=4),
                            r=[("ps", b)], w=["kc96"])
                norm96(M, sq96, gq96, None if rope_q is None else (rope_q[0][:, n, :, :], rope_q[1]))
                b = self.bank("tr")
                for hh in range(8):
                    self.tr(self.psb[b][0:96, hh * 128:(hh + 1) * 128], knb[:, hh, :], self.ident_b, r=["knb", "identb"], w=[("ps", b)])
                self.cp("dve", QTm[0:96, :, n * 128:(n + 1) * 128], self.psb[b][0:96, :].rearrange("p (a b) -> p a b", a=8),
                        r=[("ps", b)], w=[("QTm", n)])
            for hh in range(8):
                self.attention(
                    KT=lambda st, hh=hh: KTm[0:96, hh, st * 128:(st + 1) * 128], kparts=96, kt_list=kt_list,
                    QTv=QTm[0:96, hh, 0:nq * 128], nq=nq,
                    VA_of=lambda st, hh=hh: VAm[:, st, hh, :], scale=float(96 ** -0.5), Pt=Pt,
                    out_of=lambda j, hh=hh: (omb[:, j, hh * 64:(hh + 1) * 64], ("omb", j)),
                    kr=kkeys, qr=[("QTm", j) for j in range(nq)], vr=vkeys, ow=None)
            self.attn_flush()
            for n, t in enumerate(tiles):
                b = self.bank("tr")
                for c in range(4):
                    self.tr(self.psb[b][:, c * 128:(c + 1) * 128], omb[:, n, c * 128:(c + 1) * 128], self.ident_b,
                            r=[("omb", n), "identb"], w=[("ps", b)])
                self.cp("act", OM[:, :, n * 128:(n + 1) * 128], self.psb[b][:, 0:512].rearrange("p (a b) -> p a b", a=4),
                        r=[("ps", b)], w=[("OM", n)])
            t0 = tiles[0]
            self.out_proj_pair(t0, w_out,
                               lambda kc: OG[:, kc, :] if kc < 4 else OM[:, kc - 4, 0:256],
                               [("OG", 0), ("OG", 1), ("OM", 0), ("OM", 1)])

        st_ = self.sseg["tiles"]
        A.reset(base_off)
        B = g_alloc(2, persist=SP)
        g_prep(st_, B)
        g_scan(2, B, None)
        ccgs, ccgd = "cc_src_g%d" % l, "cc_dst_g%d" % l
        gsrc = A.alloc([4, 129], F32)
        self.tt("dve", gsrc[:, :, 128], SP["etot"][:, :, 0], SP["etot"][:, :, 1], ALU.mult, r=[("etot", 0), ("etot", 1)], w=["gsrc_a"])
        for z in range(2):
            for fc in range(2):
                zf = z * 2 + fc
                self.cp("act", gsrc[:, zf, 0:128], B["Sst"][z][fc], r=[("S", z, fc)], w=[("gsrc", zf)])
        self.dma("sp", ccg_src.ap().rearrange("(zf p) c -> p zf c", p=128), gsrc,
                 r=["gsrc_a"] + [("gsrc", zf) for zf in range(4)], w=[ccgs], dq="bnc_g")
        self.allgather((l, "g"), r=[ccgs], w=[ccgd])
        self.P.barrier()
        if "x1" in self.stages:
            return
        A.reset(base_off)
        W = own_alloc()
        ccms, ccmd = "cc_src_m%d" % l, "cc_dst_m%d" % l
        for n, t in enumerate(st_):
            m_own(t, n, W, SP["cqb"])
            self.dma("sp", ccm_src[n * 128:(n + 1) * 128, 0:256], W["ckvn"], r=["ckvn"], w=[ccms], dq="bnc_m")
            self.dma("sp", ccm_src[n * 128:(n + 1) * 128, 256:288], W["kpe"], r=["kpe"], w=[ccms], dq="bnc_m")
        self.allgather((l, "m"), r=[ccms], w=[ccmd])
        self.P.barrier()
        if "x2" in self.stages:
            return

        A.reset(base_off)
        B = g_alloc(2)
        GT = g_out_alloc()
        M = m_alloc(2, rope=False)
        W = own_alloc()
        cqb = A.alloc([2, 384], BF16)
        for sq_ in self.seqs:
            tiles = sq_["tiles"]
            bi = sq_["bidx"]
            g_prep(tiles, B)
            g_scan(2, B, bi)
            g_out(tiles, B["oT"], B["rsT"], GT)
            for n, t in enumerate(tiles):
                m_own(t, n, W, cqb)
                self.dma("sp", O["o_ckv"][bi, i, n * 128:(n + 1) * 128, :], W["ckvn"], r=["ckvn"], w=[], dq="ckvn_o")
                self.dma("sp", O["o_kpe"][bi, i, n * 128:(n + 1) * 128, :], W["kpe"], r=["kpe"], w=[], dq="kpe_o")
                kside(M, W["sq96"], W["ckvn"], "ckvn", W["kpe"], "kpe", n, None)
            m_attn(M, W["sq96"], tiles, cqb, None, 2)
        self.P.barrier()

        if "x3" in self.stages:
            return
        A.reset(base_off)
        gU = A.alloc([16, 129], F32)
        self.dma("sp", gU, ccg_dst.ap().rearrange("(rz p) c -> p rz c", p=128), r=[ccgd], w=["gU"], dq="gU")
        Sin = [[A.alloc([128], F32) for _ in range(2)] for _ in range(2)]
        Sib = [[A.alloc([128], BF16) for _ in range(2)] for _ in range(2)]
        tS = A.alloc([128], F32)
        for z in range(2):
            for fc in range(2):
                zf = z * 2 + fc
                S_ = Sin[z][fc]
                sk = ("Sin", z, fc)
                self.dma("sp", S_, I["c_gla"][i, z, fc * 128:(fc + 1) * 128, :], r=[], w=[sk], dq=sk)
                ranks = [0, 1, 2, 3] if z == 0 else [3, 2, 1, 0]
                for k in ranks:
                    gi = k * 4 + zf
                    self.stt(tS, S_, gU[:, gi, 128:129], gU[:, gi, 0:128], ALU.mult, ALU.add, r=[sk, "gU"], w=["tS"])
                    self.tt("dve", tS, tS, S_, ALU.subtract, r=["tS", sk], w=["tS"])
                    self.stt(S_, tS, sel[:, z * 4 + k:z * 4 + k + 1], S_, ALU.mult, ALU.add, r=["tS", sk, "cst"], w=[sk])
                self.cp("act", Sib[z][fc], S_, r=[sk], w=[("Sib", z, fc)])
        if "x5" in self.stages:
            self.P.barrier()
            return
        for z in range(2):
            order = [0, 1] if z == 0 else [1, 0]
            for oi, n in enumerate(order):
                nc_ = slice(n * 128, (n + 1) * 128)
                bh = [self.bank("acc"), self.bank("acc")]
                for hp in range(2):
                    pr = slice(hp * 64, (hp + 1) * 64)
                    for fc in range(2):
                        self.mm(self.ps[bh[hp]][:, fc * 128:(fc + 1) * 128], Sib[z][fc][pr, :], SP["qlT"][z][pr, fc, nc_], True, True,
                                r=[("Sib", z, fc), ("qlT", z, n)], w=[("ps", bh[hp])])
                oview = SP["oT"][:, :, nc_].rearrange("p (f h) t -> p f h t", f=2, h=2)
                for hp in range(2):
                    pb = self.ps[bh[hp]][:, 0:256].rearrange("p (a b) -> p a b", a=2)
                    self.tt("dve", oview[:, :, hp, :], oview[:, :, hp, :], pb, ALU.add, r=[("ps", bh[hp]), ("oT", n)], w=[("oT", n)])
                if oi == 0:
                    for fc in range(2):
                        zf = z * 2 + fc
                        sk = ("Sin", z, fc)
                        self.P.op("dve", lambda h, S_=Sin[z][fc], e=SP["etot"][:, zf, n:n + 1]: h.tensor_scalar(S_, S_, e, None, ALU.mult),
                                  r=[sk, ("etot", n)], w=[sk])
                        self.cp("act", Sib[z][fc], Sin[z][fc], r=[sk], w=[("Sib", z, fc)])
        if "x6" in self.stages:
            self.P.barrier()
            return
        g_out(st_, SP["oT"], SP["rsT"])
        self.P.barrier()
        if "x4" in self.stages:
            return
        A.reset(base_off)
        M = m_alloc(12)
        sq96 = A.alloc([8, 96], F32)
        ropem_all = A.alloc([8, 2, 16], F32)
        ropem_own = A.alloc([2, 2, 16], F32)
        ckp = A.alloc([4, 32], F32)
        cck = A.alloc([256], F32)
        kgm = [A.alloc([288], F32) for _ in range(2)]
        self.dma("sp", ropem_all, I["ropem_all"].rearrange("(t p) c q -> p t c q", p=128), r=[], w=["ropem_all"], dq="ropem")
        self.dma("sp", ropem_own, I["ropem"].rearrange("(t p) c q -> p t c q", p=128), r=[], w=["ropem_own"], dq="ropem")
        self.dma("sp", ckp, I["c_kpe"][i].rearrange("(t p) n -> p t n", p=128), r=[], w=["ckp"], dq="ckp")
        for kt in range(4):
            self.dma("sp", cck, I["c_ckv"][i, kt * 128:(kt + 1) * 128, :], r=[], w=["cck"], dq="cck")
            kside(M, sq96, cck, "cck", ckp[:, kt, :], "ckp", kt, None)
        for k in range(8):
            kg_ = kgm[k % 2]
            kk = ("kgm", k % 2)
            self.dma("sp", kg_, ccm_dst[k * 128:(k + 1) * 128, :], r=[ccmd], w=[kk], dq=kk)
            kside(M, sq96, kg_[:, 0:256], kk, kg_[:, 256:288], kk, 4 + k, (ropem_all[:, k, :, :], "ropem_all"))
        m_attn(M, sq96, st_, SP["cqb"], (ropem_own, "ropem_own"), 12)


def _rope_tables(n_tok, d_rot):
    t = np.arange(n_tok, dtype=np.int32)
    pos = np.stack([t // 64, t % 64], axis=-1).astype(np.float32)
    quarter = d_rot // 4
    inv = np.power(np.float32(10000.0), -np.arange(quarter, dtype=np.float32) / np.float32(quarter)).astype(np.float32)
    ang = pos[:, :, None] * inv
    cos = np.cos(ang).astype(np.float32).reshape(n_tok, 2 * quarter)
    sin = np.sin(ang).astype(np.float32).reshape(n_tok, 2 * quarter)
    return np.ascontiguousarray(np.stack([cos, sin], axis=1))


def _build_cst(inp, b, j):
    c = np.zeros((128, NCST), np.float32)

    def put(name, arr, parts=128):
        o, n = CST_OFF[name]
        c[0:parts, o:o + n] = np.asarray(arr, np.float32).reshape(parts, n)

    s = np.arange(128)[:, None]
    t = np.arange(128)[None, :]
    v = np.float32(-1.0 / 16.0)
    put("ident", np.eye(128))
    put("trim0", (s <= t) * v)
    put("trim1", (s >= t) * v)
    put("tris0", (s > t) * v)
    put("tris1", (s < t) * v)
    put("mask0", (s <= t) * 1.0)
    put("mask1", (s >= t) * 1.0)
    put("rm", np.array([[1, 0], [0, 1], [1, 1]], np.float32), parts=3)
    cond = np.stack([inp["c_ctx"].reshape(8, 128).T, inp["c"][b].reshape(8, 128).T], axis=-1)
    put("cond", cond)
    selv = np.array([1.0 if k < j else 0.0 for k in range(4)] + [1.0 if k > j else 0.0 for k in range(4)], np.float32)
    put("sel", np.broadcast_to(selv[None, :], (128, 8)))
    put("gmix", inp["norm_mix_g"].reshape(4, 8, 128).transpose(2, 0, 1))
    put("gffn", inp["norm_ffn_g"].reshape(4, 8, 128).transpose(2, 0, 1))
    for i in range(2):
        put("gq%d" % i, np.broadcast_to(inp["gqa_qn_g"][i][None, :], (128, 64)))
        put("gk%d" % i, np.broadcast_to(inp["gqa_kn_g"][i][None, :], (128, 64)))
        put("gout%d" % i, inp["gla_out_g"][i].reshape(128, 1))
        put("gqn%d" % i, np.broadcast_to(inp["mla_q_norm_g"][i][None, :], (128, 384)))
        put("gkvn%d" % i, np.broadcast_to(inp["mla_kv_norm_g"][i][None, :], (128, 256)))
        put("gq96%d" % i, np.broadcast_to(inp["mla_qn_g"][i][None, :], (128, 96)))
        put("gk96%d" % i, np.broadcast_to(inp["mla_kn_g"][i][None, :], (128, 96)))
    return c


_NC_CACHE = {}


def _get_nc(debug=(), stages=None):
    key = (tuple(sorted(debug)), None if stages is None else tuple(sorted(stages)))
    if key not in _NC_CACHE:
        kb = KB(debug, stages)
        nc = kb.build()
        _NC_CACHE[key] = (nc, kb)
    return _NC_CACHE[key]


def make_in_maps(inp):
    inp = {k: np.ascontiguousarray(np.asarray(v)) for k, v in inp.items()}
    ropeg = _rope_tables(1024, 64)
    ropem = _rope_tables(1024, 32)
    shared = dict(
        ffn_w_in=inp["ffn_w_in"], ffn_w_out=inp["ffn_w_out"],
        ab_w_in=inp["ab_w_in"], ab_w_out=inp["ab_w_out"], a_w2=inp["gla_a_w2"], a_b=inp["gla_a_b"],
        w_qb=inp["mla_w_qb"], w_kvb=inp["mla_w_kvb"], gqa_w_in=inp["gqa_w_in"], gqa_w_out=inp["gqa_w_out"])
    in_maps = []
    for core in range(8):
        b, j = core // 4, core % 4
        m = dict(shared)
        m["ropem_all"] = ropem
        m["ada_w"] = np.ascontiguousarray(inp["ada_w"][:, :, j * 1536:(j + 1) * 1536])
        m["ada_b"] = np.ascontiguousarray(inp["ada_b"][:, j * 1536:(j + 1) * 1536])
        m["ropeg"] = np.ascontiguousarray(ropeg[j * 256:(j + 1) * 256])
        m["ropem"] = np.ascontiguousarray(ropem[j * 256:(j + 1) * 256])
        m["cst"] = _build_cst(inp, b, j)
        m["xp"] = np.ascontiguousarray(inp["x_prompt"][core * 4:(core + 1) * 4].reshape(1024, D))
        m["xs"] = np.ascontiguousarray(inp["x_sample"][b, j * 256:(j + 1) * 256])
        m["c_ckv"] = np.ascontiguousarray(inp["cache_mla_ckv"][b])
        m["c_kpe"] = np.ascontiguousarray(inp["cache_mla_kpe"][b])
        m["c_gla"] = np.ascontiguousarray(inp["state_gla"][b].reshape(2, 2, 256, 128))
        m["c_gk"] = np.ascontiguousarray(inp["cache_gqa_k"][b].reshape(2, 512, 256))
        m["c_gv"] = np.ascontiguousarray(inp["cache_gqa_v"][b].reshape(2, 512, 256))
        in_maps.append(m)
    return in_maps


def kernel(**inputs):
    nc, kb = _get_nc()
    in_maps = make_in_maps(inputs)
    res = run_bass_kernel_spmd(nc, in_maps, core_ids=list(range(8)))
    R = res.results
    y_prompt = np.concatenate([R[c]["yp"].reshape(4, 256, D) for c in range(8)], axis=0)
    y_sample = np.stack([np.concatenate([R[4 * b + j]["ys"] for j in range(4)], axis=0) for b in range(2)], axis=0)
    new_ckv = np.concatenate([R[c]["o_ckv"] for c in range(8)], axis=0)
    new_kpe = np.concatenate([R[c]["o_kpe"] for c in range(8)], axis=0)
    new_gla = np.concatenate([R[c]["o_gla"].reshape(4, 2, 2, 4, 64, 128) for c in range(8)], axis=0)
    new_k = np.concatenate([R[c]["o_gk"].reshape(4, 2, 256, 4, 64) for c in range(8)], axis=0)
    new_v = np.concatenate([R[c]["o_gv"].reshape(4, 2, 256, 4, 64) for c in range(8)], axis=0)
    outs = (y_prompt, y_sample, new_ckv, new_kpe, new_gla, new_k, new_v)
    return tuple(np.ascontiguousarray(o, dtype=np.float32) for o in outs)
```
